# Optimizing a Trainium2 kernel written in Bass

```python
import jax
import jax.numpy as jnp
from jax import lax
import numpy as np


D_MODEL = 1024
BATCH = 2
SEQ = 16384
DEPTH = 4

HEAD_DIM = 64
N_HEADS = D_MODEL // HEAD_DIM
A_KV_HEADS = 4
CMP_BLOCK = 32
CMP_STRIDE = 16
CMP_HIDDEN = 256
SEL_BLOCK = 64
N_SELECT = 16
WINDOW_A = 512
QUERY_BLOCK = 128
B_KV_HEADS = 2
WINDOW_B = 128
N_A_LAYERS = max(1, DEPTH // 2)
D_FF = 2816
CONV_WIDTH = 3
ROPE_THETA = 10000.0
EPS = 1e-6
NEG = -1e30
FORCE = 1e6
A_SPLIT = [N_HEADS * HEAD_DIM] + [A_KV_HEADS * HEAD_DIM] * 6 + [3 * N_HEADS]
A_IN_COLS = sum(A_SPLIT)

kernel_name = 'hybrid_nsa_swa_sink_yoco_convffn'


def rms_norm(x, g):
    xf = x.astype(jnp.float32)
    y = xf * lax.rsqrt(jnp.mean(xf * xf, axis=-1, keepdims=True) + EPS)
    return (y * g.astype(jnp.float32)).astype(x.dtype)


def rope(x, pos):
    half = HEAD_DIM // 2
    inv = jnp.float32(ROPE_THETA) ** (-jnp.arange(half, dtype=jnp.float32) / half)
    ang = pos.astype(jnp.float32)[:, None] * inv[None, :]
    cos = jnp.cos(ang)[None, :, None, :]
    sin = jnp.sin(ang)[None, :, None, :]
    xf = x.astype(jnp.float32)
    x1, x2 = xf[..., :half], xf[..., half:]
    return jnp.concatenate([x1 * cos - x2 * sin, x2 * cos + x1 * sin], axis=-1).astype(x.dtype)


def compress(k, pos_emb, w1, w2):
    b, s, g, dh = k.shape
    nseg = s // CMP_STRIDE
    r = CMP_BLOCK // CMP_STRIDE
    nc = nseg - r + 1
    seg = k.reshape(b, nseg, CMP_STRIDE, g, dh)
    blocks = jnp.concatenate([seg[:, i:i + nc] for i in range(r)], axis=2)
    blocks = blocks + pos_emb[None, None, :, None, :]
    flat = blocks.transpose(0, 1, 3, 2, 4).reshape(b, nc, g, CMP_BLOCK * dh)
    return jax.nn.gelu(flat @ w1) @ w2


def _sel_weights():
    r_c = CMP_BLOCK // CMP_STRIDE
    r_s = SEL_BLOCK // CMP_STRIDE
    ws = []
    for o in range(-(r_c - 1), r_s):
        lo = max(o * CMP_STRIDE, 0)
        hi = min(o * CMP_STRIDE + CMP_BLOCK, SEL_BLOCK)
        ws.append(max(hi - lo, 0) // CMP_STRIDE)
    return ws


def block_importance(p, n_sel):
    r_c = CMP_BLOCK // CMP_STRIDE
    r_s = SEL_BLOCK // CMP_STRIDE
    ws = _sel_weights()
    front = r_c - 1
    span = r_s * (n_sel - 1) + 1
    back = max(len(ws) - 1 + span - front - p.shape[-1], 0)
    pp = jnp.pad(p, [(0, 0)] * (p.ndim - 1) + [(front, back)])
    out = ws[0] * pp[..., 0:span:r_s]
    for i in range(1, len(ws)):
        out = out + ws[i] * pp[..., i:i + span:r_s]
    return out


def nsa_mixer(h, w_in, cmp_pos, cmp_w1, cmp_w2, w_out):
    b, s, _ = h.shape
    g_n = A_KV_HEADS
    r_n = N_HEADS // g_n
    dh = HEAD_DIM
    split_idx = np.cumsum(A_SPLIT)[:-1].tolist()
    q, kc, vc, ks, vs, kw, vw, gl = jnp.split(h @ w_in, split_idx, axis=-1)
    pos = jnp.arange(s)
    q = rope(q.reshape(b, s, N_HEADS, dh), pos)
    kc = kc.reshape(b, s, g_n, dh)
    vc = vc.reshape(b, s, g_n, dh)
    ks = rope(ks.reshape(b, s, g_n, dh), pos)
    vs = vs.reshape(b, s, g_n, dh)
    kw = rope(kw.reshape(b, s, g_n, dh), pos)
    vw = vw.reshape(b, s, g_n, dh)
    kc_c = compress(kc, cmp_pos[0], cmp_w1[0], cmp_w2[0])
    vc_c = compress(vc, cmp_pos[1], cmp_w1[1], cmp_w2[1])
    nc = kc_c.shape[1]
    cmp_end = jnp.arange(nc) * CMP_STRIDE + CMP_BLOCK - 1
    kc_c = rope(kc_c, cmp_end)
    gates = jax.nn.sigmoid(gl.astype(jnp.float32)).astype(h.dtype)
    gh = gates.reshape(b, s, g_n, r_n, 3).transpose(0, 2, 3, 1, 4)
    qh = q.reshape(b, s, g_n, r_n, dh).transpose(0, 2, 3, 1, 4)
    kc_h = kc_c.transpose(0, 2, 1, 3)
    vc_h = vc_c.transpose(0, 2, 1, 3)
    n_sel = s // SEL_BLOCK
    ks_b = ks.transpose(0, 2, 1, 3).reshape(b, g_n, n_sel, SEL_BLOCK, dh)
    vs_b = vs.transpose(0, 2, 1, 3).reshape(b, g_n, n_sel, SEL_BLOCK, dh)
    pad_w = ((0, 0), (0, 0), (WINDOW_A, 0), (0, 0))
    kw_p = jnp.pad(kw.transpose(0, 2, 1, 3), pad_w)
    vw_p = jnp.pad(vw.transpose(0, 2, 1, 3), pad_w)
    top = min(N_SELECT, n_sel)
    scale = dh ** -0.5
    bi = jnp.arange(b)[:, None, None, None]
    gi = jnp.arange(g_n)[None, :, None, None]
    sel_j = jnp.arange(n_sel)
    sel_c = jnp.arange(SEL_BLOCK)

    def one_block(qb):
        start = qb * QUERY_BLOCK
        t = start + jnp.arange(QUERY_BLOCK)
        qq = lax.dynamic_slice_in_dim(qh, start, QUERY_BLOCK, axis=3)
        gg = lax.dynamic_slice_in_dim(gh, start, QUERY_BLOCK, axis=3)
        sc = jnp.einsum('bgrqd,bgnd->bgrqn', qq, kc_h).astype(jnp.float32) * scale
        valid = cmp_end[None, :] <= t[:, None]
        p_c = jax.nn.softmax(jnp.where(valid, sc, NEG), axis=-1) * valid
        o_c = jnp.einsum('bgrqn,bgnd->bgrqd', p_c.astype(vc_h.dtype), vc_h)
        imp = block_importance(p_c.sum(axis=2), n_sel)
        cur = t // SEL_BLOCK
        forced = (sel_j[None, :] == 0) | (sel_j[None, :] == cur[:, None]) | (sel_j[None, :] == cur[:, None] - 1)
        causal_blk = sel_j[None, :] * SEL_BLOCK <= t[:, None]
        imp = jnp.where(forced, FORCE, jnp.where(causal_blk, imp, -FORCE))
        _, idx = lax.top_k(imp, top)
        k_sel = ks_b[bi, gi, idx]
        v_sel = vs_b[bi, gi, idx]
        kpos = idx[..., None] * SEL_BLOCK + sel_c
        m_sel = (kpos <= t[:, None, None])[:, :, None]
        ss = jnp.einsum('bgrqd,bgqkcd->bgrqkc', qq, k_sel).astype(jnp.float32) * scale
        ss = jnp.where(m_sel, ss, NEG).reshape(b, g_n, r_n, QUERY_BLOCK, top * SEL_BLOCK)
        p_s = jax.nn.softmax(ss, axis=-1).reshape(b, g_n, r_n, QUERY_BLOCK, top, SEL_BLOCK)
        o_s = jnp.einsum('bgrqkc,bgqkcd->bgrqd', p_s.astype(v_sel.dtype), v_sel)
        kwb = lax.dynamic_slice_in_dim(kw_p, start, QUERY_BLOCK + WINDOW_A, axis=2)
        vwb = lax.dynamic_slice_in_dim(vw_p, start, QUERY_BLOCK + WINDOW_A, axis=2)
        wpos = start - WINDOW_A + jnp.arange(QUERY_BLOCK + WINDOW_A)
        dist = t[:, None] - wpos[None, :]
        m_w = (dist >= 0) & (dist < WINDOW_A) & (wpos[None, :] >= 0)
        sw = jnp.einsum('bgrqd,bgkd->bgrqk', qq, kwb).astype(jnp.float32) * scale
        p_w = jax.nn.softmax(jnp.where(m_w, sw, NEG), axis=-1)
        o_w = jnp.einsum('bgrqk,bgkd->bgrqd', p_w.astype(vwb.dtype), vwb)
        return gg[..., 0:1] * o_c + gg[..., 1:2] * o_s + gg[..., 2:3] * o_w

    out = lax.map(one_block, jnp.arange(s // QUERY_BLOCK))
    out = out.transpose(1, 0, 4, 2, 3, 5).reshape(b, s, N_HEADS * dh)
    return out @ w_out


def shared_kv(h, kv_norm, w_kv):
    b, s, _ = h.shape
    k, v = jnp.split(rms_norm(h, kv_norm) @ w_kv, 2, axis=-1)
    k = rope(k.reshape(b, s, B_KV_HEADS, HEAD_DIM), jnp.arange(s))
    v = v.reshape(b, s, B_KV_HEADS, HEAD_DIM)
    return k, v


def _with_prev_block(xb):
    pad = [(0, 0), (1, 0)] + [(0, 0)] * (xb.ndim - 2)
    return jnp.concatenate([jnp.pad(xb, pad)[:, :-1], xb], axis=2)


def swa_sink_mixer(h, k, v, w_q, sinks, w_out):
    b, s, _ = h.shape
    g_n = B_KV_HEADS
    r_n = N_HEADS // g_n
    w = WINDOW_B
    nb = s // w
    q = rope((h @ w_q).reshape(b, s, N_HEADS, HEAD_DIM), jnp.arange(s))
    qb = q.reshape(b, nb, w, g_n, r_n, HEAD_DIM)
    kk = _with_prev_block(k.reshape(b, nb, w, g_n, HEAD_DIM))
    vv = _with_prev_block(v.reshape(b, nb, w, g_n, HEAD_DIM))
    sc = jnp.einsum('bnqgrd,bnkgd->bgrnqk', qb, kk).astype(jnp.float32) * (HEAD_DIM ** -0.5)
    qi = jnp.arange(w)[:, None]
    ki = jnp.arange(2 * w)[None, :]
    rel = qi + w - ki
    band = (rel >= 0) & (rel < w)
    exists = (jnp.arange(nb)[:, None, None] > 0) | (ki[None] >= w)
    mask = band[None] & exists
    sc = jnp.where(mask, sc, NEG)
    sink = sinks.astype(jnp.float32).reshape(1, g_n, r_n, 1, 1, 1)
    mx = jnp.maximum(sc.max(axis=-1, keepdims=True), sink)
    e = jnp.exp(sc - mx)
    p = e / (e.sum(axis=-1, keepdims=True) + jnp.exp(sink - mx))
    o = jnp.einsum('bgrnqk,bnkgd->bnqgrd', p.astype(vv.dtype), vv)
    return o.reshape(b, s, N_HEADS * HEAD_DIM) @ w_out


def conv_ffn(h, w_in, conv_w, conv_b, w_out):
    s = h.shape[1]
    u = h @ w_in
    up = jnp.pad(u, ((0, 0), (CONV_WIDTH - 1, 0), (0, 0)))
    c = conv_b + conv_w[0] * up[:, 0:s]
    for i in range(1, CONV_WIDTH):
        c = c + conv_w[i] * up[:, i:i + s]
    a, gate_in = jnp.split(c, 2, axis=-1)
    return (jax.nn.silu(a) * gate_in) @ w_out


def setup_inputs(seed: int = 0) -> dict:
    key = jax.random.key(seed)
    ks = jax.random.split(key, 18)
    d = D_MODEL
    n_a = N_A_LAYERS
    n_b = DEPTH - N_A_LAYERS
    f32 = jnp.float32

    def w(k, shape, fan_in):
        return jax.random.normal(k, shape, f32) * (fan_in ** -0.5)

    def gain(k, shape):
        return 1.0 + 0.02 * jax.random.normal(k, shape, f32)

    return {
        'x': jax.random.normal(ks[0], (BATCH, SEQ, d), f32),
        'norm_attn': gain(ks[1], (DEPTH, d)),
        'norm_ffn': gain(ks[2], (DEPTH, d)),
        'a_w_in': w(ks[3], (n_a, d, A_IN_COLS), d),
        'a_cmp_pos': 0.1 * jax.random.normal(ks[4], (n_a, 2, CMP_BLOCK, HEAD_DIM), f32),
        'a_cmp_w1': w(ks[5], (n_a, 2, CMP_BLOCK * HEAD_DIM, CMP_HIDDEN), CMP_BLOCK * HEAD_DIM),
        'a_cmp_w2': w(ks[6], (n_a, 2, CMP_HIDDEN, HEAD_DIM), CMP_HIDDEN),
        'a_w_out': w(ks[7], (n_a, d, d), d),
        'kv_norm': gain(ks[8], (d,)),
        'b_w_kv': w(ks[9], (d, 2 * B_KV_HEADS * HEAD_DIM), d),
        'b_w_q': w(ks[10], (n_b, d, d), d),
        'b_sinks': jax.random.normal(ks[11], (n_b, N_HEADS), f32),
        'b_w_out': w(ks[12], (n_b, d, d), d),
        'ffn_w_in': w(ks[13], (DEPTH, d, 2 * D_FF), d),
        'ffn_conv_w': w(ks[14], (DEPTH, CONV_WIDTH, 2 * D_FF), CONV_WIDTH),
        'ffn_conv_b': 0.02 * jax.random.normal(ks[15], (DEPTH, 2 * D_FF), f32),
        'ffn_w_out': w(ks[16], (DEPTH, D_FF, d), D_FF),
        'final_norm': gain(ks[17], (d,)),
    }


def reference(x, norm_attn, norm_ffn, a_w_in, a_cmp_pos, a_cmp_w1, a_cmp_w2, a_w_out, kv_norm, b_w_kv, b_w_q, b_sinks, b_w_out, ffn_w_in, ffn_conv_w, ffn_conv_b, ffn_w_out, final_norm):
    h = x
    k_sh = None
    v_sh = None
    for layer in range(DEPTH):
        hn = rms_norm(h, norm_attn[layer])
        if layer < N_A_LAYERS:
            h = h + nsa_mixer(hn, a_w_in[layer], a_cmp_pos[layer], a_cmp_w1[layer], a_cmp_w2[layer], a_w_out[layer])
        else:
            if layer == N_A_LAYERS:
                k_sh, v_sh = shared_kv(h, kv_norm, b_w_kv)
                hn = rms_norm(h, norm_attn[layer])
            j = layer - N_A_LAYERS
            h = h + swa_sink_mixer(hn, k_sh, v_sh, b_w_q[j], b_sinks[j], b_w_out[j])
        h = h + conv_ffn(rms_norm(h, norm_ffn[layer]), ffn_w_in[layer], ffn_conv_w[layer], ffn_conv_b[layer], ffn_w_out[layer])
    return rms_norm(h, final_norm)
```

```python
import numpy as np
import ml_dtypes
import concourse.bass as bass
import concourse.mybir as mybir
from concourse.bass_utils import run_bass_kernel_spmd

F32 = mybir.dt.float32
BF16 = mybir.dt.bfloat16
AF = mybir.ActivationFunctionType
ALU = mybir.AluOpType
AX = mybir.AxisListType

NPBF16 = ml_dtypes.bfloat16

D = 1024
DFF = 2816
EPS = 1e-6
MASKV = -240000.0

COMPUTE = ("pe", "act", "dve", "pool")
EPOCH = 30000
NSLOT = 12
SAME_ENG_SYNC = True
ZERO_BIAS = True


class Sched:
    def __init__(self, nc):
        self.nc = nc
        self.streams = {e: [] for e in COMPUTE + ("sp",)}
        self.cnt = {e: 0 for e in COMPUTE}
        self.known = {e: {} for e in self.streams}
        self.known_dma = {e: set() for e in self.streams}
        self.last_w = {}
        self.readers = {}
        self.ndma = {e: 0 for e in self.streams}
        self.sems = {}
        self.nsem = 0
        self.out_dmas = []
        self.ncc = 0

    def _sem(self, name):
        if name not in self.sems:
            self.sems[name] = self.nc.alloc_semaphore(name=name)
        return self.sems[name]

    def _ev_wait_args(self, ev):
        kind = ev[0]
        if kind == "c":
            _, eng, idx = ev
            ep, off = divmod(idx, EPOCH)
            return self._sem(f"s_{eng}_{ep}"), off + 1
        elif kind == "x":
            return self._sem(f"x_{ev[1]}"), ev[2]
        else:
            _, q, j = ev
            slot, use = j % NSLOT, j // NSLOT
            return self._sem(f"d_{q}_{slot}"), 16 * (use + 1)

    def _deps(self, eng, reads, writes):
        deps = set()
        for k in reads:
            w = self.last_w.get(k)
            if w is not None:
                deps.add(w)
        for k in writes:
            w = self.last_w.get(k)
            if w is not None:
                deps.add(w)
            for r in self.readers.get(k, ()):
                deps.add(r)
        waits = []
        best = {}
        for ev in deps:
            if ev[0] == "c":
                _, src, idx = ev
                if src == eng and eng == "pe":
                    continue
                if self.known[eng].get(src, -1) >= idx:
                    continue
                if best.get(src, -1) < idx:
                    best[src] = idx
            else:
                if ev in self.known_dma[eng]:
                    continue
                waits.append(ev)
                self.known_dma[eng].add(ev)
        for src, idx in best.items():
            self.known[eng][src] = idx
            waits.append(("c", src, idx))
        return waits

    def _mark(self, ev, reads, writes):
        for k in reads:
            self.readers.setdefault(k, []).append(ev)
        for k in writes:
            self.last_w[k] = ev
            self.readers[k] = []

    def op(self, eng, fn, reads=(), writes=()):
        assert eng in COMPUTE
        waits = self._deps(eng, reads, writes)
        idx = self.cnt[eng]
        self.cnt[eng] += 1
        ev = ("c", eng, idx)
        if not SAME_ENG_SYNC or eng == "pe":
            self.known[eng][eng] = idx
        self._mark(ev, reads, writes)
        self.streams[eng].append((waits, fn, ev))
        return ev

    def dma(self, q, fn, reads=(), writes=(), is_output=False):
        waits = self._deps(q, reads, writes)
        j = self.ndma[q]
        self.ndma[q] += 1
        if j >= NSLOT:
            prev = ("d", q, j - NSLOT)
            if prev not in self.known_dma[q]:
                waits.append(prev)
                self.known_dma[q].add(prev)
        ev = ("d", q, j)
        self._mark(ev, reads, writes)
        self.streams[q].append((waits, fn, ev))
        if is_output:
            self.out_dmas.append(ev)
        return ev

    def pid(self, e, key="pid", fn=None):
        k = (self.cur_eng, key)
        if k not in self.pid_cache:
            if key == "pid":
                self.pid_cache[k] = e.partition_id()
            else:
                self.pid_cache[k] = e.snap(fn(self.pid(e)))
        return self.pid_cache[k]

    def cc(self, fn, reads=(), writes=(), n=1):
        waits = self._deps("pool", reads, writes)
        ev = ("x", self.ncc, n)
        self.ncc += 1
        self._mark(ev, reads, writes)
        self.streams["pool"].append((waits, fn, ev))
        self.known_dma["pool"].add(ev)
        idx = self.cnt["pool"]
        self.cnt["pool"] += 1
        nev = ("c", "pool", idx)
        self._mark(nev, (), writes)
        self.streams["pool"].append(([ev], lambda e: e.nop(), nev))
        return nev

    def barrier(self):
        evs = []
        for eng in COMPUTE:
            if self.cnt[eng] > 0:
                evs.append(("c", eng, self.cnt[eng] - 1))
        for q, n in self.ndma.items():
            for j in range(max(0, n - NSLOT), n):
                evs.append(("d", q, j))
        for eng in self.streams:
            waits = []
            for ev in evs:
                if ev[0] == "c":
                    if ev[1] == eng:
                        continue
                    if self.known[eng].get(ev[1], -1) >= ev[2]:
                        continue
                    self.known[eng][ev[1]] = ev[2]
                elif ev[0] == "x":
                    continue
                elif ev in self.known_dma[eng]:
                    continue
                else:
                    self.known_dma[eng].add(ev)
                waits.append(ev)
            self.streams[eng].append((waits, None, None))

    def emit(self, final=True):
        nc = self.nc
        final_waits = list(self.out_dmas) if final else []
        self.pid_cache = {}
        with nc.Block() as block:
            def run(engname, e):
                self.cur_eng = engname
                for waits, fn, ev in self.streams[engname]:
                    for w in waits:
                        s, v = self._ev_wait_args(w)
                        e.wait_ge(s, v)
                    if fn is None:
                        continue
                    s, v = self._ev_wait_args(ev)
                    if ev[0] == "x":
                        fn(e, s)
                        continue
                    ins = fn(e)
                    if ev[0] == "c":
                        ins.then_inc(s, 1)
                    else:
                        ins.then_inc(s, 16)
                if engname == "sp":
                    for w in final_waits:
                        s, v = self._ev_wait_args(w)
                        e.wait_ge(s, v)

            @block.tensor
            def _(e):
                run("pe", e)

            @block.scalar
            def _(e):
                run("act", e)

            @block.vector
            def _(e):
                run("dve", e)

            @block.gpsimd
            def _(e):
                run("pool", e)

            @block.sync
            def _(e):
                run("sp", e)
        for k in self.streams:
            self.streams[k] = []


class Prog:
    def __init__(self, nc):
        from contextlib import ExitStack
        self.nc = nc
        self.s = Sched(nc)
        self.es = ExitStack()
        self.ndram = 0
        self.sfx = ""
        self.dins = {}
        self.ext = {}

    def sb(self, name, shape, dt):
        return self.es.enter_context(self.nc.sbuf_tensor("sb_" + name + self.sfx, list(shape), dt))

    def ps(self, name, shape, dt=F32):
        return self.es.enter_context(self.nc.psum_tensor("ps_" + name + self.sfx, list(shape), dt))

    def din(self, name, shape, dt):
        nm = name + self.sfx
        if nm in self.ext:
            return self.ext[nm]
        self.dins[nm] = (tuple(shape), dt)
        return self.nc.dram_tensor(nm, list(shape), dt, kind="ExternalInput").ap()

    def dint(self, name, shape, dt):
        return self.nc.dram_tensor(name, list(shape), dt)

    def phase_end(self):
        from contextlib import ExitStack
        self.s.barrier()
        self.s.emit(final=False)
        self.es.close()
        self.es = ExitStack()

    def dmaf(self, fn, r=(), w=(), q="sp", is_output=False):
        return self.s.dma(q, fn, r, w, is_output)

    def dout(self, name, shape, dt):
        return self.nc.dram_tensor(name, list(shape), dt, kind="ExternalOutput").ap()

    def dma(self, out, in_, r=(), w=(), q="sp", is_output=False):
        return self.s.dma(q, lambda e: e.dma_start(out=out, in_=in_), r, w, is_output)

    def mm(self, out, lhsT, rhs, start, stop, r=(), w=()):
        return self.s.op("pe", lambda e: e.matmul(out, lhsT, rhs, start=start, stop=stop), r, w)

    def tr(self, out, in_, ident, r=(), w=()):
        return self.s.op("pe", lambda e: e.transpose(out, in_, ident), r, w)

    def act(self, out, in_, func, r=(), w=(), bias=None, scale=None, accum_out=None):
        kw = {}
        if bias is not None:
            kw["bias"] = bias
        if scale is not None:
            kw["scale"] = scale
        if accum_out is not None:
            kw["accum_out"] = accum_out
        return self.s.op("act", lambda e: e.activation(out, in_, func, **kw), r, w)

    def tt(self, out, in0, in1, op, r=(), w=(), eng="dve"):
        return self.s.op(eng, lambda e: e.tensor_tensor(out, in0, in1, op), r, w)

    def ts(self, out, in0, s1, s2, op0, op1=None, r=(), w=(), eng="dve", accum_out=None):
        kw = {}
        if accum_out is not None:
            kw["accum_out"] = accum_out
        if op1 is None:
            return self.s.op(eng, lambda e: e.tensor_scalar(out, in0, s1, s2, op0, **kw), r, w)
        return self.s.op(eng, lambda e: e.tensor_scalar(out, in0, s1, s2, op0, op1, **kw), r, w)

    def stt(self, out, in0, scalar, in1, op0, op1, r=(), w=(), eng="dve"):
        return self.s.op(eng, lambda e: e.scalar_tensor_tensor(out, in0, scalar, in1, op0, op1), r, w)

    def cp(self, out, in_, r=(), w=(), eng="dve"):
        if eng == "act":
            return self.s.op("act", lambda e: e.copy(out, in_), r, w)
        return self.s.op(eng, lambda e: e.tensor_copy(out, in_), r, w)

    def recip(self, out, in_, r=(), w=()):
        return self.s.op("dve", lambda e: e.reciprocal(out, in_), r, w)

    def memset(self, ap, val, w=(), eng="dve"):
        return self.s.op(eng, lambda e: e.memset(ap, val), (), w)

    def finish(self):
        self.s.emit()
        self.es.close()
        return self.nc


def load_cast_weight(p, w_dram, dst, nk, ncols, stage, tag, chunk_cols=1024):
    i = 0
    for kc in range(nk):
        for c0 in range(0, ncols, chunk_cols):
            cw = min(chunk_cols, ncols - c0)
            stg = stage[i % 2]
            p.dma(stg[:, 0:cw], w_dram[kc * 128:(kc + 1) * 128, c0:c0 + cw],
                  w=[("stg", i % 2)])
            p.cp(dst[:, kc, c0:c0 + cw], stg[:, 0:cw], r=[("stg", i % 2)],
                 w=[(tag, kc)], eng="pool")
            i += 1


def rmsnorm_tile(p, x_ap, gain_bc, out_ap, scr, keys_r, keys_w, tagk):
    sq, ss, sd, rs = scr
    p.act(sq, x_ap, AF.Square, r=keys_r, w=[("sq", tagk), ("ss", tagk)], accum_out=ss)
    p.act(sd, ss, AF.Sqrt, r=[("ss", tagk)], w=[("sd", tagk)], bias=EPS, scale=1.0 / D)
    p.recip(rs, sd, r=[("sd", tagk)], w=[("rs", tagk)])
    if gain_bc is None:
        p.ts(out_ap, x_ap, rs, None, ALU.mult, r=list(keys_r) + [("rs", tagk)], w=keys_w)
    else:
        p.stt(out_ap, x_ap, rs, gain_bc, ALU.mult, ALU.mult,
              r=list(keys_r) + [("rs", tagk), "gains"], w=keys_w)


class FFNCtx:
    def __init__(self, p, pre, max_nt=4):
        self.p = p
        self.max_nt = max_nt
        nt = max_nt
        self.wo = p.sb(pre + "wo", [64, 16, 1024], BF16)
        self.woch = [p.sb(pre + f"woch{i}", [128, 512], BF16) for i in range(2)]
        self.fstage = [p.sb(pre + f"fstg{i}", [128, 2048], F32) for i in range(2)]
        self.stage = [self.fstage[i][:, 0:1024] for i in range(2)]
        self.gT = p.sb(pre + "gT", [128, 8], F32)
        self.wch = [p.sb(pre + f"wch{i}", [128, 8, 2, 128], BF16) for i in range(2)]
        self.h1 = p.sb(pre + "h1", [128, nt, 1024], F32)
        self.oT = p.sb(pre + "oT", [64, 16, nt * 128], BF16)
        self.hn = p.sb(pre + "hn", [128, 1024], BF16)
        self.hnT = p.sb(pre + "hnT", [128, 8, nt * 128], BF16)
        self.actT = p.sb(pre + "actT", [128, 22, nt * 128], BF16)
        self.usb = [[p.sb(pre + f"usb{i}{a}", [128, 2 + nt * 128], F32) for a in range(2)]
                    for i in range(2)]
        self.carry = p.sb(pre + "carry", [128, 44, 2], F32)
        self.cwb = p.sb(pre + "cwb", [128, 44, 4], F32)
        self.t1 = p.sb(pre + "t1", [128, nt * 128], F32)
        self.t2 = p.sb(pre + "t2", [128, nt * 128], F32)
        self.ca = p.sb(pre + "ca", [128, nt * 128], F32)
        self.cg = p.sb(pre + "cg", [128, nt * 128], F32)
        self.sa = p.sb(pre + "sa", [128, nt * 128], F32)
        self.sq = p.sb(pre + "sq", [128, 1024], BF16)
        self.ss = p.sb(pre + "ss", [128, 1], F32)
        self.sd = p.sb(pre + "sd", [128, 1], F32)
        self.rs = p.sb(pre + "rs", [128, 1], F32)
        self.gainf = p.sb(pre + "gainf", [128, 1024], F32)
        self.ident = p.sb(pre + "ident", [128, 128], BF16)
        self.hfin = p.sb(pre + "hfin", [128, 1024], F32)
        self.psum = p.ps(pre + "psum", [128, 7 * 512])
        self.psT = p.ps(pre + "psT", [128, 8, 128], BF16)
        self.psA = [self.bank(0), self.bank(1)]
        self.psU = [[self.bank(2), self.bank(3)], [self.bank(4), self.bank(5)]]
        self.nA = 0
        self.nfc = 0
        self.nwo = 0

    def bank(self, i, n=1):
        return self.psum[:, i * 512:(i + n) * 512]

    def load_weights(self, w_o, w_in, w_out, cwb, g_ffn, ident, g_final=None, head_order=None):
        p = self.p
        self.w_in = w_in
        p.dma(self.ident[:], ident, w=["ident"])
        p.dma(self.cwb[:], cwb.rearrange("(c p) f -> p c f", p=128), w=["cwb"])
        p.dma(self.gT[:], g_ffn, w=["gT"])
        self.w_out = w_out
        if g_final is not None:
            p.dma(self.gainf[:], g_final.to_broadcast([128, 1024]), w=["gainf"])
        p.memset(self.carry[:], 0.0, w=["carry"])
        if w_o is not None:
            ho = head_order if head_order is not None else list(range(16))
            for i, h in enumerate(ho):
                stg = self.stage[i % 2]
                sk = ("stg", "ffn", i % 2, 0)
                p.dma(stg[0:64, :], w_o[h * 64:(h + 1) * 64, :], w=[sk])
                p.cp(self.wo[:, i, :], stg[0:64, :], r=[sk], w=[("wo", i)], eng="pool")

    def run_supertile(self, nt, h_src, oT_src, h_dst, n_skip_out=0, final_norm=False,
                      h1_preloaded=False, h_fn=None, oT_fn=None, n_flag=0, flag=None,
                      rkeys=(), dkey=None):
        p = self.p
        ntok = nt * 128
        if not h1_preloaded:
            for j in range(nt):
                p.dma(self.h1[:, j, :], h_src[j * 128:(j + 1) * 128, :], r=list(rkeys),
                      w=[("h1", j)])
                if j < n_flag:
                    p.ts(self.h1[:, j, :], self.h1[:, j, :], flag, None, ALU.mult,
                         r=[("h1", j), "flag"], w=[("h1", j)])
        if oT_src is not None or oT_fn is not None:
            if oT_fn is not None:
                for j in range(nt):
                    for g in range(4):
                        p.dma(self.oT[:, g * 4:(g + 1) * 4, j * 128:(j + 1) * 128], oT_fn(j, g),
                              r=list(rkeys), w=["oT"])
            elif not isinstance(oT_src, str):
                p.dma(self.oT[:, :, 0:ntok], oT_src.rearrange("(c p) t -> p c t", p=64), w=["oT"])
            for j in range(nt):
                for half in range(2):
                    ps = self.psA[self.nA % 2]
                    pk = ("bank", self.nA % 2)
                    self.nA += 1
                    for kc in range(16):
                        p.mm(ps, self.oT[:, kc, j * 128:(j + 1) * 128],
                             self.wo[:, kc, half * 512:(half + 1) * 512],
                             start=(kc == 0), stop=(kc == 15),
                             r=["oT", ("wo", kc)], w=[pk])
                    hs = self.h1[:, j, half * 512:(half + 1) * 512]
                    p.tt(hs, hs, ps, ALU.add, r=[pk, ("h1", j)], w=[("h1", j)])
        for j in range(nt):
            rmsnorm_tile(p, self.h1[:, j, :], None, self.hn[:],
                         (self.sq[:], self.ss[:], self.sd[:], self.rs[:]),
                         [("h1", j)], ["hn"], "f")
            for kc in range(8):
                p.tr(self.psT[:, kc, :], self.hn[:, kc * 128:(kc + 1) * 128], self.ident[:],
                     r=["hn", "ident"], w=["psT"])
            p.cp(self.hnT[:, :, j * 128:(j + 1) * 128], self.psT[:], r=["psT"], w=[("hnT", j)],
                 eng="act")
        hnT_keys = [("hnT", j) for j in range(nt)]
        for fc in range(22):
            par = self.nfc % 2
            self.nfc += 1
            stg = self.fstage[par]
            wch = self.wch[par]
            for ag in range(2):
                c0 = ag * DFF + fc * 128
                p.dma(stg[:, ag * 1024:(ag + 1) * 1024].rearrange("p (c f) -> p c f", c=8),
                      self.w_in[:, c0:c0 + 128].rearrange("(c p) f -> p c f", p=128),
                      w=[("stg", "ffn", par, ag)])
                p.tt(wch[:, :, ag, :],
                     stg[:, ag * 1024:(ag + 1) * 1024].rearrange("p (c f) -> p c f", c=8),
                     self.gT[:].unsqueeze(2).to_broadcast([128, 8, 128]), ALU.mult,
                     r=[("stg", "ffn", par, ag), "gT"], w=[("wch", par, ag)], eng="pool")
            cs = []
            for ag in range(2):
                ps = self.psU[par][ag]
                pk = ("bank", 2 + 2 * par + ag)
                for kc in range(8):
                    p.mm(ps[:, 0:ntok], wch[:, kc, ag, :], self.hnT[:, kc, 0:ntok],
                         start=(kc == 0), stop=(kc == 7),
                         r=[("wch", par, ag)] + hnT_keys, w=[pk])
                usb = self.usb[par][ag]
                uk = ("usb", par, ag)
                ch = ag * 22 + fc
                p.cp(usb[:, 0:2], self.carry[:, ch, :], r=["carry%d" % ch, "carry"], w=[uk],
                     eng="pool")
                p.cp(usb[:, 2:2 + ntok], ps[:, 0:ntok], r=[pk], w=[uk], eng="act")
                p.cp(self.carry[:, ch, :], usb[:, ntok:ntok + 2], r=[uk], w=["carry%d" % ch],
                     eng="pool")
                cw = self.cwb
                dst = self.ca if ag == 0 else self.cg
                dk = "ca" if ag == 0 else "cg"
                p.ts(self.t1[:, 0:ntok], usb[:, 2:2 + ntok], cw[:, ch, 2:3], cw[:, ch, 3:4],
                     ALU.mult, ALU.add, r=[uk, "cwb"], w=["t1"])
                p.stt(self.t2[:, 0:ntok], usb[:, 1:1 + ntok], cw[:, ch, 1:2], self.t1[:, 0:ntok],
                      ALU.mult, ALU.add, r=[uk, "cwb", "t1"], w=["t2"])
                p.stt(dst[:, 0:ntok], usb[:, 0:ntok], cw[:, ch, 0:1], self.t2[:, 0:ntok],
                      ALU.mult, ALU.add, r=[uk, "cwb", "t2"], w=[dk])
            p.act(self.sa[:, 0:ntok], self.ca[:, 0:ntok], AF.Silu, r=["ca"], w=["sa"])
            p.tt(self.actT[:, fc, 0:ntok], self.sa[:, 0:ntok], self.cg[:, 0:ntok], ALU.mult,
                 r=["sa", "cg"], w=[("actT", fc)])
        for half in range(2):
            for fc in range(22):
                wp = self.nwo % 2
                self.nwo += 1
                stg = self.fstage[wp]
                sk = ("stg", "ffn", wp, 0)
                p.dma(stg[:, 0:512], self.w_out[fc * 128:(fc + 1) * 128, half * 512:(half + 1) * 512],
                      w=[sk])
                p.cp(self.woch[wp][:], stg[:, 0:512], r=[sk], w=[("woch", wp)], eng="pool")
                for j in range(nt):
                    p.mm(self.bank(j), self.actT[:, fc, j * 128:(j + 1) * 128], self.woch[wp][:],
                         start=(fc == 0), stop=(fc == 21),
                         r=[("actT", fc), ("woch", wp)], w=[("bank", j)])
            for j in range(nt):
                hs = self.h1[:, j, half * 512:(half + 1) * 512]
                p.tt(hs, hs, self.bank(j), ALU.add, r=[("bank", j), ("h1", j)], w=[("h1", j)])
        for j in range(nt):
            if h_dst is not None and j >= n_skip_out:
                jo = j - n_skip_out
                if final_norm:
                    rmsnorm_tile(p, self.h1[:, j, :], self.gainf[:], self.hfin[:],
                                 (self.sq[:], self.ss[:], self.sd[:], self.rs[:]),
                                 [("h1", j), "gainf"], ["hfin"], "f")
                    p.dma(h_dst[jo * 128:(jo + 1) * 128, :], self.hfin[:], r=["hfin"],
                          w=([dkey] if dkey else []), q="pool", is_output=True)
                else:
                    p.dma(h_dst[jo * 128:(jo + 1) * 128, :], self.h1[:, j, :], r=[("h1", j)],
                          w=([dkey] if dkey else []), q="pool", is_output=(dkey is None))


def ident_np():
    return np.eye(128, dtype=np.float32).astype(NPBF16)


def b_head_order():
    return [g * 4 + 2 * hpl + par for g in range(4) for par in range(2) for hpl in range(2)]


def build_B(st_sizes, n_skip_tiles, final_norm=False, p=None, io=None):
    fused = p is not None
    if not fused:
        nc = bass.Bass("TRN2", target_bir_lowering=False)
        p = Prog(nc)
    ntiles = sum(st_sizes)
    ntok = ntiles * 128
    if not fused:
        h_in = p.din("h_in", [ntok, D], F32)
        oT_in = p.din("oT_in", [D, ntok], BF16)
    w_o = p.din("w_o", [D, D], F32)
    w_in = p.din("w_in", [D, 2 * DFF], F32)
    w_out = p.din("w_out", [DFF, D], F32)
    cwb = p.din("cwb", [2 * DFF, 4], F32)
    g_ffn = p.din("g_ffn", [128, 8], F32)
    g_fin = p.din("g_fin", [1, D], F32)
    ident = p.din("ident", [128, 128], BF16)
    if not fused:
        h_out = p.dout("h_out", [(ntiles - n_skip_tiles) * 128, D], F32)
    else:
        h_out = io["h_dst"]
    f = FFNCtx(p, "f_", max_nt=max(st_sizes))
    f.load_weights(w_o, w_in, w_out, cwb, g_ffn, ident, g_fin,
                   head_order=(b_head_order() if fused else None))
    if fused:
        flag_sb = p.sb("flag", [128, 1], F32)
        p.dma(flag_sb[:], io["flag"], w=["flag"])
    t0 = 0
    for nt in st_sizes:
        skip = max(0, min(nt, n_skip_tiles - t0))
        o0 = max(0, t0 - n_skip_tiles)
        dst = h_out[o0 * 128:(o0 + nt - skip) * 128, :] if skip < nt else None
        if fused:
            f.run_supertile(nt, io["h_ap"][t0 * 128:(t0 + nt) * 128, :], None, dst,
                            n_skip_out=skip, final_norm=final_norm,
                            oT_fn=lambda j, g, t0=t0: io["oT_ap"](t0 + j, g),
                            n_flag=skip, flag=flag_sb[:], rkeys=io["rkeys"], dkey=io["dkey"])
        else:
            f.run_supertile(nt, h_in[t0 * 128:(t0 + nt) * 128, :],
                            oT_in[:, t0 * 128:(t0 + nt) * 128], dst, n_skip_out=skip,
                            final_norm=final_norm)
        t0 += nt
    if fused:
        return None
    return p.finish()


def c_head_order():
    return [8 * g + 2 * hpl + par for g in range(2) for par in range(2) for hpl in range(4)]


def build_C(st_sizes, n_skip_tiles, final_norm=False, p=None, io=None):
    fused = p is not None
    if not fused:
        nc = bass.Bass("TRN2", target_bir_lowering=False)
        p = Prog(nc)
    ntiles = sum(st_sizes)
    ntok = ntiles * 128
    mx = max(st_sizes)
    if not fused:
        h_in = p.din("h_in", [ntok, D], F32)
        hkv_in = p.din("hkv_in", [ntok, D], F32)
    w_q = p.din("w_q", [D, D], F32)
    w_kv = p.din("w_kv", [D, 256], F32)
    sinks_b = p.din("sinks_b", [1, 2048], F32)
    g_attn = p.din("g_attn", [128, 8], F32)
    g_kv = p.din("g_kv", [128, 8], F32)
    cos_t = p.din("cos_t", [128, ntok], F32)
    sin_t = p.din("sin_t", [128, ntok], F32)
    masks = p.din("masks", [3, 128, 512], BF16)
    w_o = p.din("w_o", [D, D], F32)
    w_in = p.din("w_in", [D, 2 * DFF], F32)
    w_out = p.din("w_out", [DFF, D], F32)
    cwb = p.din("cwb", [2 * DFF, 4], F32)
    g_ffn = p.din("g_ffn", [128, 8], F32)
    g_fin = p.din("g_fin", [1, D], F32)
    ident = p.din("ident", [128, 128], BF16)
    if not fused:
        h_out = p.dout("h_out", [(ntiles - n_skip_tiles) * 128, D], F32)
    else:
        h_out = io["h_dst"]

    f = FFNCtx(p, "f_", max_nt=mx)
    f.load_weights(w_o, w_in, w_out, cwb, g_ffn, ident, g_fin, head_order=c_head_order())
    if fused:
        flag_sb = p.sb("flag", [128, 1], F32)
        p.dma(flag_sb[:], io["flag"], w=["flag"])

    gTq = p.sb("gTq", [128, 8], F32)
    gTk = p.sb("gTk", [128, 8], F32)
    wk2 = p.sb("wk2", [128, 8, 2, 2, 128], BF16)
    wv = p.sb("wv", [128, 8, 128], BF16)
    hkv = p.sb("hkv", [128, 1024], F32)
    hnqT = f.hnT
    hkvT = p.sb("hkvT", [128, 8, mx * 128], BF16)
    QT2 = p.sb("QT2", [128, 8, mx * 128], BF16)
    KT2 = p.sb("KT2", [128, 2, (mx + 1) * 128], BF16)
    VA = p.sb("VA", [128, mx + 1, 2, 65], BF16)
    PT = [p.sb(f"PT{i}", [128, 1024], BF16) for i in range(2)]
    msk = p.sb("msk", [128, 3, 512], BF16)
    cos_sb = p.sb("cos_sb", [128, mx * 128], F32)
    sin_sb = p.sb("sin_sb", [128, mx * 128], F32)
    sexp = p.sb("sexp", [128, 2048], BF16)
    zr = p.sb("zr", [128, 1024], F32)
    rz = zr
    ones = p.sb("ones", [128, 64], F32)
    osb = f.hfin[0:64, :]
    psS = [f.bank(2, 2), f.bank(4, 2)]
    psSk = [[("bank", 2), ("bank", 3)], [("bank", 4), ("bank", 5)]]
    psO = f.bank(0, 2)
    psOk = [("bank", 0), ("bank", 1)]
    psB = f.bank(6)
    psBk = [("bank", 6)]

    p.dma(gTq[:], g_attn, w=["gTq"])
    p.dma(gTk[:], g_kv, w=["gTk"])
    p.dma(msk[:], masks.rearrange("m p c -> p m c"), w=["msk"])
    p.dma(zr[64:65, :], sinks_b[:, 0:1024], w=["zr"])
    p.act(sexp[64:65, 0:1024], zr[64:65, :], AF.Exp, r=["zr"], w=["sexp"])
    p.dma(zr[64:65, :], sinks_b[:, 1024:2048], r=["sexp"], w=["zr"])
    p.act(sexp[64:65, 1024:2048], zr[64:65, :], AF.Exp, r=["zr"], w=["sexp"])
    p.memset(ones[:], 1.0, w=["ones"])
    p.memset(VA[:], 1.0, w=["VA"] + [("VA", i) for i in range(mx + 1)])
    p.memset(KT2[:], 0.0, w=["KT2", ("KT2", 0), ("KT2", 1)])
    for kc in range(8):
        stg = f.stage[kc % 2]
        sk = ("stg", "ffn", kc % 2, 0)
        gs = gTk[:, kc:kc + 1]
        p.dma(stg[:, 0:256], w_kv[kc * 128:(kc + 1) * 128, :], w=[sk])
        for g in range(2):
            for dup in range(2):
                p.ts(wk2[:, kc, g, 0, dup * 64:(dup + 1) * 64], stg[:, g * 64:(g + 1) * 64],
                     gs, None, ALU.mult, r=[sk, "gTk"], w=["wk2"], eng="pool")
                p.ts(wk2[:, kc, g, 1, dup * 64:dup * 64 + 32], stg[:, g * 64 + 32:g * 64 + 64],
                     gs, None, ALU.mult, r=[sk, "gTk"], w=["wk2"], eng="pool")
                p.ts(wk2[:, kc, g, 1, dup * 64 + 32:dup * 64 + 64], stg[:, g * 64:g * 64 + 32],
                     gs, None, ALU.mult, r=[sk, "gTk"], w=["wk2"], eng="pool")
        p.ts(wv[:, kc, :], stg[:, 128:256], gs, None, ALU.mult, r=[sk, "gTk"], w=["wv"], eng="pool")

    scr = (f.sq[:], f.ss[:], f.sd[:], f.rs[:])
    t0 = 0
    first_real = n_skip_tiles
    for nt in st_sizes:
        n = nt * 128
        for j in range(nt):
            if fused:
                p.dma(f.h1[:, j, :], io["h_ap"][(t0 + j) * 128:(t0 + j + 1) * 128, :],
                      r=list(io["rkeys"]), w=[("h1", j)])
                if t0 + j < n_skip_tiles:
                    p.ts(f.h1[:, j, :], f.h1[:, j, :], flag_sb[:], None, ALU.mult,
                         r=[("h1", j), "flag"], w=[("h1", j)])
            else:
                p.dma(f.h1[:, j, :], h_in[(t0 + j) * 128:(t0 + j + 1) * 128, :], w=[("h1", j)])
        p.dma(cos_sb[:, 0:n], cos_t[:, t0 * 128:t0 * 128 + n], w=["cos"])
        p.dma(sin_sb[:, 0:n], sin_t[:, t0 * 128:t0 * 128 + n], w=["sin"])
        for j in range(nt):
            rmsnorm_tile(p, f.h1[:, j, :], None, f.hn[:], scr, [("h1", j)], ["hn"], "f")
            for kc in range(8):
                p.tr(f.psT[:, kc, :], f.hn[:, kc * 128:(kc + 1) * 128], f.ident[:],
                     r=["hn", "ident"], w=["psT"])
            p.cp(hnqT[:, :, j * 128:(j + 1) * 128], f.psT[:], r=["psT"], w=[("hnT", j)], eng="act")
            if fused:
                p.dma(hkv[:], io["hkv_ap"][(t0 + j) * 128:(t0 + j + 1) * 128, :],
                      r=list(io["rkeys"]), w=["hkv"])
                if t0 + j < n_skip_tiles:
                    p.ts(hkv[:], hkv[:], flag_sb[:], None, ALU.mult, r=["hkv", "flag"], w=["hkv"])
            else:
                p.dma(hkv[:], hkv_in[(t0 + j) * 128:(t0 + j + 1) * 128, :], w=["hkv"])
            rmsnorm_tile(p, hkv[:], None, f.hn[:], scr, ["hkv"], ["hn"], "f")
            for kc in range(8):
                p.tr(f.psT[:, kc, :], f.hn[:, kc * 128:(kc + 1) * 128], f.ident[:],
                     r=["hn", "ident"], w=["psT"])
            p.cp(hkvT[:, :, j * 128:(j + 1) * 128], f.psT[:], r=["psT"], w=[("hkvT", j)], eng="act")
        hq_keys = [("hnT", j) for j in range(nt)]
        hk_keys = [("hkvT", j) for j in range(nt)]

        def rope_out(dst, psn, pss, rk, wk):
            p.tt(f.t1[:, 0:n], psn, cos_sb[:, 0:n], ALU.mult, r=rk[0:1] + ["cos"], w=["t1"])
            p.tt(f.t2[:, 0:n], pss, sin_sb[:, 0:n], ALU.mult, r=rk[1:2] + ["sin"], w=["t2"])
            p.tt(dst, f.t1[:, 0:n], f.t2[:, 0:n], ALU.add, r=["t1", "t2"], w=wk)

        for g in range(2):
            par = f.nfc % 2
            f.nfc += 1
            bk = [("bank", 2 + 2 * par), ("bank", 3 + 2 * par)]
            for v in range(2):
                for kc in range(8):
                    p.mm(f.psU[par][v][:, 0:n], wk2[:, kc, g, v, :], hkvT[:, kc, 0:n],
                         start=(kc == 0), stop=(kc == 7), r=["wk2"] + hk_keys, w=[bk[v]])
            rope_out(KT2[:, g, 128:128 + n], f.psU[par][0][:, 0:n], f.psU[par][1][:, 0:n],
                     bk, [("KT2", g)])
        for j in range(nt):
            ps = f.psA[f.nA % 2]
            pk = ("bank", f.nA % 2)
            f.nA += 1
            for kc in range(8):
                p.mm(ps[:, 0:128], hkvT[:, kc, j * 128:(j + 1) * 128], wv[:, kc, :],
                     start=(kc == 0), stop=(kc == 7), r=[("hkvT", j), "wv"], w=[pk])
            p.cp(VA[:, j + 1, :, 0:64], ps[:, 0:128].rearrange("p (g d) -> p g d", g=2),
                 r=[pk], w=[("VA", j + 1)], eng="act")
        for hp in range(8):
            par = f.nfc % 2
            f.nfc += 1
            stg = f.fstage[par]
            wch = f.wch[par]
            p.dma(stg[:, 0:1024].rearrange("p (c f) -> p c f", c=8),
                  w_q[:, hp * 128:(hp + 1) * 128].rearrange("(c p) f -> p c f", p=128),
                  w=[("stg", "ffn", par, 0)])
            p.tt(wch[:, :, 0, :], stg[:, 0:1024].rearrange("p (c f) -> p c f", c=8),
                 gTq[:].unsqueeze(2).to_broadcast([128, 8, 128]), ALU.mult,
                 r=[("stg", "ffn", par, 0), "gTq"], w=[("wch", par, 0)], eng="pool")
            src = wch[:, :, 0, :].rearrange("p c (h d) -> p c h d", h=2)
            dsw = wch[:, :, 1, :].rearrange("p c (h d) -> p c h d", h=2)
            p.cp(dsw[:, :, :, 0:32], src[:, :, :, 32:64], r=[("wch", par, 0)],
                 w=[("wch", par, 1)], eng="pool")
            p.cp(dsw[:, :, :, 32:64], src[:, :, :, 0:32], r=[("wch", par, 0)],
                 w=[("wch", par, 1)], eng="pool")
            bk = [("bank", 2 + 2 * par), ("bank", 3 + 2 * par)]
            for v in range(2):
                for kc in range(8):
                    p.mm(f.psU[par][v][:, 0:n], wch[:, kc, v, :], hnqT[:, kc, 0:n],
                         start=(kc == 0), stop=(kc == 7),
                         r=[("wch", par, v)] + hq_keys, w=[bk[v]])
            rope_out(QT2[:, hp, 0:n], f.psU[par][0][:, 0:n], f.psU[par][1][:, 0:n],
                     bk, [("QT2", hp)])
        nS = 0
        for j in range(nt):
            gt = t0 + j
            for g in range(2):
                chunks = [(j, 0 if gt == first_real else 1), (j + 1, 2)]
                for ci, (slot, mi) in enumerate(chunks):
                    sp_ = nS % 2
                    nS += 1
                    for par in range(2):
                        pr = slice(par * 64, (par + 1) * 64)
                        p.mm(psS[sp_][:, par * 512:(par + 1) * 512],
                             KT2[pr, g, slot * 128:(slot + 1) * 128],
                             QT2[pr, 4 * g:4 * g + 4, j * 128:(j + 1) * 128],
                             start=True, stop=False,
                             r=[("KT2", g)] + [("QT2", 4 * g + i) for i in range(4)],
                             w=[psSk[sp_][par]])
                        p.mm(psS[sp_][:, par * 512:(par + 1) * 512], f.ident[:], msk[:, mi, :],
                             start=False, stop=True, r=["ident", "msk"], w=[psSk[sp_][par]])
                    p.act(PT[sp_][:], psS[sp_], AF.Exp, r=psSk[sp_], w=[("PT", sp_)], scale=0.125)
                    for par in range(2):
                        p.mm(psO[0:65, par * 512:(par + 1) * 512], VA[:, slot, g, :],
                             PT[sp_][:, par * 512:(par + 1) * 512],
                             start=(ci == 0), stop=(ci == 1),
                             r=[("PT", sp_), ("VA", slot), "VA"], w=[psOk[par]])
                p.tt(zr[64:65, :], psO[64:65, :], sexp[64:65, g * 1024:(g + 1) * 1024], ALU.add,
                     r=psOk + ["sexp"], w=["zr"])
                p.recip(rz[64:65, :], zr[64:65, :], r=["zr"], w=["rz"])
                p.cp(osb, psO[0:64, :], r=psOk, w=["hfin"], eng="act")
                for par in range(2):
                    p.mm(psB[0:64, :], ones[64:65, :], rz[64:65, par * 512:(par + 1) * 512],
                         start=True, stop=True, r=["ones", "rz"], w=psBk)
                    dst = f.oT[:, g * 8 + par * 4:g * 8 + par * 4 + 4, j * 128:(j + 1) * 128]
                    p.tt(dst, osb[:, par * 512:(par + 1) * 512].rearrange("p (h q) -> p h q", h=4),
                         psB[0:64, :].rearrange("p (h q) -> p h q", h=4), ALU.mult,
                         r=["hfin"] + psBk, w=["oT"])
        for g in range(2):
            p.cp(KT2[:, g, 0:128], KT2[:, g, n:n + 128], r=[("KT2", g)], w=[("KT2", g)], eng="pool")
        p.cp(VA[:, 0, :, :], VA[:, nt, :, :], r=[("VA", nt)], w=[("VA", 0)], eng="pool")
        skip = max(0, min(nt, n_skip_tiles - t0))
        o0 = max(0, t0 - n_skip_tiles)
        dst = h_out[o0 * 128:(o0 + nt - skip) * 128, :] if skip < nt else None
        f.run_supertile(nt, None, "resident", dst, n_skip_out=skip, final_norm=final_norm,
                        h1_preloaded=True, dkey=(io["dkey"] if fused else None))
        t0 += nt
    if fused:
        return None
    return p.finish()


def rope_tables(pos):
    half = 32
    inv = (np.float32(10000.0) ** (-np.arange(half, dtype=np.float32) / half)).astype(np.float32)
    ang = pos.astype(np.float32)[None, :] * inv[:, None]
    cos = np.cos(ang).astype(np.float32)
    sin = np.sin(ang).astype(np.float32)
    cos64 = np.concatenate([cos, cos], 0)
    sin64 = np.concatenate([-sin, sin], 0)
    return (np.ascontiguousarray(np.concatenate([cos64, cos64], 0)),
            np.ascontiguousarray(np.concatenate([sin64, sin64], 0)))


def swa_masks(first_exists):
    i = np.arange(128)[:, None]
    q = np.arange(128)[None, :]
    prev = np.where(i > q, 0.0, MASKV).astype(np.float32)
    cur = np.where(i <= q, 0.0, MASKV).astype(np.float32)
    pf = prev if first_exists else np.full((128, 128), MASKV, np.float32)
    m = np.stack([np.tile(pf, (1, 4)), np.tile(prev, (1, 4)), np.tile(cur, (1, 4))], 0)
    return m.astype(NPBF16)


def sinks_row(sinks16):
    ho = c_head_order()
    return np.ascontiguousarray(
        np.repeat(np.asarray(sinks16, np.float32)[ho], 128)[None, :])


def gT_np(g):
    return np.ascontiguousarray(np.asarray(g, np.float32).reshape(8, 128).T)


FORCE = 1.0e6
TINY = 1.0e-30


def build_A(S, dbg=99, p=None, io=None):
    fused = p is not None
    if not fused:
        nc = bass.Bass("TRN2", target_bir_lowering=False)
        p = Prog(nc)
    nc = p.nc
    NST = S // 512
    NQB = S // 128
    NCC = max(1, S // 2048)
    if not fused:
        h_in = p.din("h_in", [S, D], F32)
    g_attn = p.din("g_attn", [128, 8], F32)
    wq_d = p.din("wq", [D, 256], F32)
    wk3_d = p.din("wk3", [D, 192], F32)
    wv3_d = p.din("wv3", [D, 192], F32)
    wg_d = p.din("wg", [D, 12], F32)
    w1_d = p.din("w1", [2, 2048, 256], F32)
    w2_d = p.din("w2", [2, 256, 64], F32)
    posT_d = p.din("posT", [64, 2, 32], F32)
    cos_d = p.din("cos_t", [128, S], F32)
    sin_d = p.din("sin_t", [128, S], F32)
    ccos_d = p.din("ccos_t", [128, NCC * 128], F32)
    csin_d = p.din("csin_t", [128, NCC * 128], F32)
    pmask_d = p.din("pmask", [2, 16, 128, 128], BF16)
    r0mask_d = p.din("r0mask", [128, 256], BF16)
    cmask_d = p.din("cmask", [2, 128, 256], BF16)
    emat_d = p.din("emat", [64, 128, 128], BF16)
    wfull_d = p.din("wfull", [NCC * 128, 257], BF16)
    fix_d = p.din("fix3", [128, 6], F32)
    ident_d = p.din("ident", [128, 128], BF16)
    gscr = [nc.dram_tensor(f"gscr{i}" + p.sfx, [1, 12 * 512], F32).ap() for i in range(2)]
    if not fused:
        oT_out = p.dout("oT_out", [64, 4, S], BF16)

    ident = p.sb("ident", [128, 128], BF16)
    gT = p.sb("gT", [128, 8], F32)
    fst = [p.sb(f"fst{i}", [128, 1024], F32) for i in range(2)]
    WQ = p.sb("WQ", [128, 8, 2, 256], BF16)
    WKS = p.sb("WKS", [128, 8, 2, 128], BF16)
    WKW = p.sb("WKW", [128, 8, 2, 128], BF16)
    WKC = p.sb("WKC", [128, 8, 64], BF16)
    WVC = p.sb("WVC", [128, 8, 64], BF16)
    WV2 = p.sb("WV2", [128, 8, 128], BF16)
    WG = p.sb("WG", [128, 8, 12], BF16)
    W1c = [p.sb(f"W1c{i}", [64, 4, 256], BF16) for i in range(2)]
    W2K = p.sb("W2K", [128, 2, 2, 128], BF16)
    W2V = p.sb("W2V", [128, 2, 64], BF16)
    posT = p.sb("posT", [64, 2, 32], BF16)
    c1 = p.sb("c1", [128, 4], F32)
    ccos = p.sb("ccos", [128, NCC * 128], F32)
    csin = p.sb("csin", [128, NCC * 128], F32)
    pmask = p.sb("pmask", [128, 2, 16, 128], BF16)
    r0mask = p.sb("r0mask", [128, 256], BF16)
    cmask = p.sb("cmask", [128, 2, 256], BF16)
    emat = p.sb("emat", [128, 64, 128], BF16)
    wfull = p.sb("wfull", [128, NCC, 257], BF16)
    fix3 = p.sb("fix3", [128, 6], F32)
    hbuf = [p.sb("hbuf0", [128, 1024], F32)] * 2
    sq = p.sb("sq", [128, 1024], BF16)
    ss = p.sb("ss", [128, 1], F32)
    sd = p.sb("sd", [128, 1], F32)
    rs = p.sb("rs", [128, 1], F32)
    hn = p.sb("hn", [128, 1024], BF16)
    hnT = p.sb("hnT", [128, 8, 512], BF16)
    cos_sb = p.sb("cos_sb", [128, 512], F32)
    sin_sb = p.sb("sin_sb", [128, 512], F32)
    t1 = p.sb("t1", [128, 512], F32)
    t2 = p.sb("t2", [128, 512], F32)
    QT2 = p.sb("QT2", [128, 2, 512], BF16)
    KsT2 = p.sb("KsT2", [128, S], BF16)
    VsA = p.sb("VsA", [128, NQB, 65], BF16)
    KwT2 = p.sb("KwT2", [128, 1024], BF16)
    VwA = p.sb("VwA", [128, 8, 65], BF16)
    KcT2 = p.sb("KcT2", [128, NCC * 128], BF16)
    VcA = p.sb("VcA", [128, NCC, 65], BF16)
    xT = [p.sb(f"xT{i}", [64, 528], BF16) for i in range(2)]
    hidK = p.sb("hidK", [128, 2, 32], BF16)
    hidV = p.sb("hidV", [128, 2, 128], BF16)
    gx = [p.sb(f"gx{i}", [128, 32], F32) for i in range(3)]
    gsb = p.sb("gsb", [12, 512], F32)
    G64b = [p.sb(f"G64b{i}", [128, 12 * 128], F32) for i in range(2)]
    PT = [p.sb(f"PT{i}", [128, 512], BF16) for i in range(2)]
    PcT = p.sb("PcT", [128, NCC, 512], BF16)
    zr = p.sb("zr", [128, 512], F32)
    Rr = p.sb("Rr", [128, 512], F32)
    ones = p.sb("ones", [128, 64], F32)
    osb = p.sb("osb", [64, 512], F32)
    acc = p.sb("acc", [64, 512], F32)
    tmpo = p.sb("tmpo", [64, 512], F32)
    oacc = p.sb("oacc", [64, 4, 128], BF16)
    imp = p.sb("imp", [128, 256], F32)
    selbuf = p.sb("selbuf", [128, 256], F32)
    work = p.sb("work", [128, 256], F32)
    mx8 = p.sb("mx8", [128, 8], F32)
    thr = p.sb("thr", [128, 1], F32)
    zq = p.sb("zq", [128, 1], F32)
    Bq = p.sb("Bq", [128, 256], BF16)
    BT = p.sb("BT", [128, 2, 256], BF16)
    psum = p.ps("psum", [128, 7 * 512])
    psT = p.ps("psT", [128, 8, 128], BF16)
    zero_b = p.sb("zero_b", [128, 256], BF16)
    p.memset(zero_b[:], 0.0, w=["zero_b"])

    def bank(i, n=1):
        return psum[:, i * 512:(i + n) * 512]

    def bk(i):
        return ("bank", i)

    p.dma(ident[:], ident_d, w=["ident"])
    p.dma(gT[:], g_attn, w=["gT"])
    p.dma(ccos[:], ccos_d, w=["ccos"])
    p.dma(csin[:], csin_d, w=["csin"])
    for a_ in range(2):
        for r4 in range(0, 16, 4):
            p.dma(pmask[:, a_, r4:r4 + 4, :], pmask_d[a_, r4:r4 + 4].rearrange("r p c -> p r c"),
                  w=["pmask"])
    p.dma(r0mask[:], r0mask_d, w=["r0mask"])
    p.dma(cmask[:], cmask_d.rearrange("a p c -> p a c"), w=["cmask"])
    for e8 in range(0, 64, 8):
        p.dma(emat[:, e8:e8 + 8, :], emat_d[e8:e8 + 8].rearrange("e p c -> p e c"), w=["emat"])
    p.dma(wfull[:], wfull_d.rearrange("(c p) f -> p c f", p=128), w=["wfull"])
    p.dma(fix3[:], fix_d, w=["fix3"])
    p.memset(ones[:], 1.0, w=["ones"])
    p.memset(VsA[:], 1.0, w=["VsA"])
    p.memset(VwA[:], 1.0, w=["VwA"])
    p.memset(VcA[:], 1.0, w=["VcA"])
    p.memset(KwT2[:], 0.0, w=["KwT2"])
    p.memset(KcT2[:], 0.0, w=["KcT2"])
    p.memset(selbuf[:], -FORCE, w=["selbuf"])
    p.memset(hidV[:], 0.0, w=["hidV"])
    for i in range(2):
        p.memset(xT[i][:], 0.0, w=[("xT", i)])
    nst_ = [0]

    def stage_load(dst_fn, src_ap, ncols, parts=128):
        i = nst_[0] % 2
        nst_[0] += 1
        k = ("fst", i)
        p.dma(fst[i][0:parts, 0:ncols], src_ap, w=[k])
        return fst[i], k

    def swapcopy(dst, src, r, w):
        d4 = dst.rearrange("p (h d) -> p h d", d=64)
        s4 = src.rearrange("p (h d) -> p h d", d=64)
        p.cp(d4[:, :, 0:32], s4[:, :, 32:64], r=r, w=w, eng="pool")
        p.cp(d4[:, :, 32:64], s4[:, :, 0:32], r=r, w=w, eng="pool")

    for kc in range(8):
        gs = gT[:, kc:kc + 1]
        rows = slice(kc * 128, (kc + 1) * 128)
        st_, k = stage_load(None, wq_d[rows, :], 256)
        p.ts(WQ[:, kc, 0, :], st_[:, 0:256], gs, None, ALU.mult, r=[k, "gT"], w=["WQ"], eng="pool")
        swapcopy(WQ[:, kc, 1, :], WQ[:, kc, 0, :], ["WQ"], ["WQ"])
        st_, k = stage_load(None, wk3_d[rows, :], 192)
        p.ts(WKC[:, kc, :], st_[:, 0:64], gs, None, ALU.mult, r=[k, "gT"], w=["WKC"], eng="pool")
        for (W_, c0) in ((WKS, 64), (WKW, 128)):
            for dup in range(2):
                p.ts(W_[:, kc, 0, dup * 64:(dup + 1) * 64], st_[:, c0:c0 + 64], gs, None, ALU.mult,
                     r=[k, "gT"], w=["WK"], eng="pool")
            swapcopy(W_[:, kc, 1, :], W_[:, kc, 0, :], ["WK"], ["WK"])
        st_, k = stage_load(None, wv3_d[rows, :], 192)
        p.ts(WVC[:, kc, :], st_[:, 0:64], gs, None, ALU.mult, r=[k, "gT"], w=["WVC"], eng="pool")
        p.ts(WV2[:, kc, :], st_[:, 64:192], gs, None, ALU.mult, r=[k, "gT"], w=["WV2"], eng="pool")
        st_, k = stage_load(None, wg_d[rows, :], 12)
        p.ts(WG[:, kc, :], st_[:, 0:12], gs, None, ALU.mult, r=[k, "gT"], w=["WG"], eng="pool")
    nW1 = [0]

    def w1_piece(kv, l0):
        i = nW1[0] % 2
        nW1[0] += 1
        k = ("fst", i)
        p.dma(fst[i][0:64, :].rearrange("p (l m) -> p l m", l=4),
              w1_d[kv, l0 * 64:(l0 + 4) * 64, :].rearrange("(l d) m -> d l m", d=64), w=[k])
        p.cp(W1c[i][:], fst[i][0:64, :].rearrange("p (l m) -> p l m", l=4),
             r=[k], w=[("W1c", i)], eng="pool")
        return W1c[i], ("W1c", i)

    for kv in range(2):
        for mt in range(2):
            st_, k = stage_load(None, w2_d[kv, mt * 128:(mt + 1) * 128, :], 64)
            if kv == 0:
                for dup in range(2):
                    p.cp(W2K[:, mt, 0, dup * 64:(dup + 1) * 64], st_[:, 0:64], r=[k], w=["W2K"],
                         eng="pool")
                swapcopy(W2K[:, mt, 1, :], W2K[:, mt, 0, :], ["W2K"], ["W2K"])
            else:
                p.cp(W2V[:, mt, :], st_[:, 0:64], r=[k], w=["W2V"], eng="pool")
    st_, k = stage_load(None, posT_d.rearrange("d a l -> d (a l)"), 64, parts=64)
    p.cp(posT[:].rearrange("d a l -> d (a l)"), st_[0:64, 0:64], r=[k], w=["posT"], eng="pool")
    for kv in range(2):
        for l0 in range(0, 32, 4):
            wt, wk_ = w1_piece(kv, l0)
            for mt in range(2):
                col = kv * 2 + mt
                bb = 1 if mt == 0 else 6
                for li in range(4):
                    l = l0 + li
                    p.mm(bank(bb)[:, col:col + 1], wt[:, li, mt * 128:(mt + 1) * 128],
                         posT[:, kv, l:l + 1], start=(l == 0), stop=(l == 31),
                         r=[wk_, "posT"], w=[bk(bb)])
    p.cp(c1[:, 0:1], bank(1)[:, 0:1], r=[bk(1)], w=["c1"], eng="act")
    p.cp(c1[:, 2:3], bank(1)[:, 2:3], r=[bk(1)], w=["c1"], eng="act")
    p.cp(c1[:, 1:2], bank(6)[:, 1:2], r=[bk(6)], w=["c1"], eng="act")
    p.cp(c1[:, 3:4], bank(6)[:, 3:4], r=[bk(6)], w=["c1"], eng="act")

    if dbg == 0:
        return p.finish()
    if fused:
        for cb in range(4):
            p.dma(io["o_zero"][:, cb, :], zero_b[0:64, 0:128], r=["zero_b"], w=[io["dkey"]],
                  q="pool")
    scr = (sq[:], ss[:], sd[:], rs[:])

    def rope_out(dst, psn, pss, cs, sn, rk, wk, n):
        p.tt(t1[:, 0:n], psn, cs, ALU.mult, r=rk[0:1] + ["cos", "ccos"], w=["t1"])
        p.tt(t2[:, 0:n], pss, sn, ALU.mult, r=rk[1:2] + ["sin", "csin"], w=["t2"])
        p.tt(dst, t1[:, 0:n], t2[:, 0:n], ALU.add, r=["t1", "t2"], w=wk)

    nU = [0]
    nH = [0]

    def proj_pair(W_, dst, cs, sn, wkey, rkey):
        par = nU[0] % 2
        nU[0] += 1
        b0, b1 = 2 + 2 * par, 3 + 2 * par
        for v, b in ((0, b0), (1, b1)):
            for kc in range(8):
                p.mm(bank(b), W_(kc, v), hnT[:, kc, :], start=(kc == 0), stop=(kc == 7),
                     r=[rkey, "hnT"], w=[bk(b)])
        rope_out(dst, bank(b0), bank(b1), cs, sn, [bk(b0), bk(b1)], wkey, 512)

    def gelu_to(dst, ps_ap, bias_ap, rk, wk):
        x, a, b = gx[0][:], gx[1][:], gx[2][:]
        p.act(x, ps_ap, AF.Identity, r=rk + ["c1"], w=["gx0"], bias=bias_ap)
        p.tt(a, x, x, ALU.mult, r=["gx0"], w=["gx1"])
        p.ts(a, a, 0.044715, 1.0, ALU.mult, ALU.add, r=["gx1"], w=["gx1"])
        p.tt(a, a, x, ALU.mult, r=["gx1", "gx0"], w=["gx1"])
        p.act(b, a, AF.Sigmoid, r=["gx1"], w=["gx2"], scale=1.5957691216057308)
        p.tt(dst, x, b, ALU.mult, r=["gx0", "gx2"], w=wk)

    nS = [0]

    def attn_chunk(kT2, kcols, vaug, biases, first, last, n_extra_r):
        sp_ = nS[0] % 2
        nS[0] += 1
        sb_ = 2 + sp_
        if ZERO_BIAS and len(biases) == 0:
            biases = [(ident[:], zero_b[:], ["ident", "zero_b"])]
        for par in range(2):
            pr = slice(par * 64, (par + 1) * 64)
            out = bank(sb_)[:, par * 256:(par + 1) * 256]
            p.mm(out, kT2[pr, kcols], QT2[pr, :, qsl[0]], start=True, stop=(len(biases) == 0),
                 r=n_extra_r + ["QT2"], w=[bk(sb_)])
            for bi, bias in enumerate(biases):
                lh, rh, rk = bias[0:3]
                if len(bias) == 4:
                    for hh in range(2):
                        p.mm(out[:, hh * 128:(hh + 1) * 128], lh, rh, start=False,
                             stop=(bi == len(biases) - 1), r=rk, w=[bk(sb_)])
                else:
                    p.mm(out, lh, rh, start=False, stop=(bi == len(biases) - 1), r=rk, w=[bk(sb_)])
        return sp_, sb_

    qsl = [None]
    for st in range(NST):
        tok0 = st * 512
        for j in range(4):
            hb = hbuf[j % 2]
            hk = ("hbuf", 0)
            if fused:
                r0 = io["h_row"](tok0 + j * 128)
                p.dma(hb[:], io["h_ap"][r0:r0 + 128, :], r=list(io["rkeys"]), w=[hk])
            else:
                p.dma(hb[:], h_in[tok0 + j * 128:tok0 + (j + 1) * 128, :], w=[hk])
            rmsnorm_tile(p, hb[:], None, hn[:], scr, [hk], ["hn"], "a")
            for kc in range(8):
                p.tr(psT[:, kc, :], hn[:, kc * 128:(kc + 1) * 128], ident[:],
                     r=["hn", "ident"], w=["psT"])
            p.cp(hnT[:, :, j * 128:(j + 1) * 128], psT[:], r=["psT"], w=["hnT"], eng="act")
        p.dma(cos_sb[:], cos_d[:, tok0:tok0 + 512], w=["cos"])
        p.dma(sin_sb[:], sin_d[:, tok0:tok0 + 512], w=["sin"])
        for hp in range(2):
            proj_pair(lambda kc, v, hp=hp: WQ[:, kc, v, hp * 128:(hp + 1) * 128],
                      QT2[:, hp, :], cos_sb[:], sin_sb[:], ["QT2"], "WQ")
        proj_pair(lambda kc, v: WKS[:, kc, v, :], KsT2[:, tok0:tok0 + 512], cos_sb[:], sin_sb[:],
                  ["KsT2"], "WK")
        proj_pair(lambda kc, v: WKW[:, kc, v, :], KwT2[:, 512:1024], cos_sb[:], sin_sb[:],
                  ["KwT2"], "WK")
        for i, W_ in enumerate((WKC, WVC)):
            for kc in range(8):
                p.mm(bank(1)[0:64, :], W_[:, kc, :], hnT[:, kc, :], start=(kc == 0), stop=(kc == 7),
                     r=["WKC", "WVC", "hnT"], w=[bk(1)])
            p.cp(xT[i][:, 16:528], bank(1)[0:64, :], r=[bk(1)], w=[("xT", i)], eng="act")
        for j in range(4):
            for kc in range(8):
                p.mm(bank(0)[:, 0:128], hnT[:, kc, j * 128:(j + 1) * 128], WV2[:, kc, :],
                     start=(kc == 0), stop=(kc == 7), r=["hnT", "WV2"], w=[bk(0)])
            p.cp(VsA[:, st * 4 + j, 0:64], bank(0)[:, 0:64], r=[bk(0), "VsA"], w=["VsA"], eng="act")
            p.cp(VwA[:, 4 + j, 0:64], bank(0)[:, 64:128], r=[bk(0), "VwA"], w=["VwA"], eng="act")
        for kc in range(8):
            p.mm(bank(1)[0:12, :], WG[:, kc, :], hnT[:, kc, :], start=(kc == 0), stop=(kc == 7),
                 r=["WG", "hnT"], w=[bk(1)])
        p.act(gsb[:], bank(1)[0:12, :], AF.Sigmoid, r=[bk(1)], w=["gsb"])
        p.dma(gscr[st % 2].rearrange("o (a b) -> (o a) b", a=12), gsb[:], r=["gsb"],
              w=[("gscr", st % 2)])
        if dbg == 1:
            return p.finish()
        for kv in range(2):
            x3 = xT[kv][:].rearrange("p (i s) -> p i s", s=16)
            for l0 in range(0, 32, 4):
                wt, wk_ = w1_piece(kv, l0)
                for mt in range(2):
                    bb = 1 if mt == 0 else 6
                    for li in range(4):
                        l = l0 + li
                        rhs = x3[:, 0:32, l] if l < 16 else x3[:, 1:33, l - 16]
                        p.mm(bank(bb)[:, 0:32], wt[:, li, mt * 128:(mt + 1) * 128], rhs,
                             start=(l == 0), stop=(l == 31), r=[wk_, ("xT", kv)], w=[bk(bb)])
            for mt in range(2):
                bb = 1 if mt == 0 else 6
                if kv == 0:
                    gelu_to(hidK[:, mt, :], bank(bb)[:, 0:32], c1[:, mt:mt + 1], [bk(bb)], ["hidK"])
                else:
                    if st % 4 == 0 and mt == 0:
                        p.memset(hidV[:], 0.0, w=["hidV"])
                    gelu_to(hidV[:, mt, (st % 4) * 32:(st % 4) * 32 + 32], bank(bb)[:, 0:32],
                            c1[:, 2 + mt:3 + mt], [bk(bb)], ["hidV"])
            if kv == 0:
                par = nU[0] % 2
                nU[0] += 1
                b0, b1 = 2 + 2 * par, 3 + 2 * par
                for v, b in ((0, b0), (1, b1)):
                    for mt in range(2):
                        p.mm(bank(b)[:, 0:32], W2K[:, mt, v, :], hidK[:, mt, :],
                             start=(mt == 0), stop=(mt == 1), r=["W2K", "hidK"], w=[bk(b)])
                sl = slice(st * 32, st * 32 + 32)
                rope_out(KcT2[:, sl], bank(b0)[:, 0:32], bank(b1)[:, 0:32], ccos[:, sl], csin[:, sl],
                         [bk(b0), bk(b1)], ["KcT2"], 32)
            else:
                for mt in range(2):
                    p.mm(bank(1)[:, 0:64], hidV[:, mt, :], W2V[:, mt, :],
                         start=(mt == 0), stop=(mt == 1), r=["W2V", "hidV"], w=[bk(1)])
                p.cp(VcA[:, st // 4, 0:64], bank(1)[:, 0:64], r=[bk(1), "VcA"], w=["VcA"], eng="act")
            p.cp(xT[kv][:, 0:16], xT[kv][:, 512:528], r=[("xT", kv)], w=[("xT", kv)], eng="pool")
        if dbg == 2:
            return p.finish()
        for j in range(4):
            qb = st * 4 + j
            qsl[0] = slice(j * 128, (j + 1) * 128)
            tsl = qsl[0]
            p.dma(G64b[j % 2][64:65, :].rearrange("p (a b) -> p a b", a=12),
                  gscr[st % 2].rearrange("o (a b) -> o a b", a=12)[:, :, tsl],
                  r=[("gscr", st % 2)], w=[("G64", j % 2)])

            def finish_branch(br, first):
                p.ts(zr[64:65, :], bank(0)[64:65, :], TINY, None, ALU.max, r=[bk(0)], w=["zr"])
                p.recip(zr[64:65, :], zr[64:65, :], r=["zr"], w=["zr"])
                g3 = G64b[j % 2][64:65, :].rearrange("p (h b t) -> p h b t", h=4, b=3)
                for par in range(2):
                    for hpl in range(2):
                        hl = 2 * hpl + par
                        c0 = (par * 2 + hpl) * 128
                        p.tt(Rr[64:65, c0:c0 + 128], zr[64:65, c0:c0 + 128], g3[:, hl, br, :],
                             ALU.mult, r=["zr", ("G64", j % 2)], w=["Rr"])
                p.cp(osb[:], bank(0)[0:64, :], r=[bk(0)], w=["osb"], eng="act")
                p.mm(bank(1)[0:64, :], ones[64:65, :], Rr[64:65, :], start=True, stop=True,
                     r=["ones", "Rr"], w=[bk(1)])
                if first:
                    p.tt(acc[:], osb[:], bank(1)[0:64, :], ALU.mult, r=["osb", bk(1)], w=["acc"])
                else:
                    p.tt(tmpo[:], osb[:], bank(1)[0:64, :], ALU.mult, r=["osb", bk(1)], w=["tmpo"])
                    p.tt(acc[:], acc[:], tmpo[:], ALU.add, r=["tmpo", "acc"], w=["acc"])

            ncc = qb // 16 + 1
            r_ = qb % 16
            for cc in range(ncc):
                biases = []
                lastc = (cc == ncc - 1)
                if lastc:
                    biases.append((ident[:], pmask[:, 1 if cc == 0 else 0, r_, :], ["ident", "pmask"], 128))
                elif cc == 0:
                    biases.append((ident[:], r0mask[:], ["ident", "r0mask"]))
                sp_, sb_ = attn_chunk(KcT2, slice(cc * 128, (cc + 1) * 128), None, biases,
                                      cc == 0, lastc, ["KcT2"])
                p.act(PcT[:, cc, :], bank(sb_), AF.Exp, r=[bk(sb_)], w=[("PcT", cc)], scale=0.125)
                p.mm(bank(0)[0:65, :], VcA[:, cc, :], PcT[:, cc, :], start=(cc == 0), stop=lastc,
                     r=[("PcT", cc), "VcA"], w=[bk(0)])
            for par in range(2):
                for hpl in range(2):
                    hi = par * 2 + hpl
                    c0 = hi * 128
                    ib = 4 + (hi % 2)
                    for cc in range(ncc):
                        p.mm(bank(ib)[:, 0:257], PcT[:, cc, c0:c0 + 128], wfull[:, cc, :],
                             start=(cc == 0), stop=(cc == ncc - 1),
                             r=[("PcT", cc), "wfull"], w=[bk(ib)])
                    p.ts(zq[:], bank(ib)[:, 256:257], TINY, None, ALU.max, r=[bk(ib)], w=["zq"])
                    p.recip(zq[:], zq[:], r=["zq"], w=["zq"])
                    if hi == 0:
                        p.ts(imp[:], bank(ib)[:, 0:256], zq[:], None, ALU.mult, r=[bk(ib), "zq"],
                             w=["imp"])
                    else:
                        p.stt(imp[:], bank(ib)[:, 0:256], zq[:], imp[:], ALU.mult, ALU.add,
                              r=[bk(ib), "zq", "imp"], w=["imp"])
            finish_branch(0, True)
            if dbg == 3 or dbg == 100 + j * 10 + 3:
                return p.finish()
            nb = 2 * qb + 2
            p.cp(selbuf[:, 0:nb], imp[:, 0:nb], r=["imp"], w=["selbuf"])
            lo = 2 * qb - 1
            k0 = 0
            if lo < 0:
                lo, k0 = 0, 1
            nfx = 3 - k0
            p.tt(selbuf[:, lo:lo + nfx], selbuf[:, lo:lo + nfx], fix3[:, k0:3], ALU.mult,
                 r=["selbuf", "fix3"], w=["selbuf"])
            p.tt(selbuf[:, lo:lo + nfx], selbuf[:, lo:lo + nfx], fix3[:, 3 + k0:6], ALU.add,
                 r=["selbuf", "fix3"], w=["selbuf"])
            p.memset(selbuf[:, 0:1], 3.0 * FORCE, w=["selbuf"])
            p.s.op("dve", lambda e: e.max(out=mx8[:], in_=selbuf[:]), ["selbuf"], ["mx8"])
            p.s.op("dve", lambda e: e.match_replace(out=work[:], in_to_replace=mx8[:],
                                                    in_values=selbuf[:], imm_value=-2.0 * FORCE),
                   ["selbuf", "mx8"], ["work"])
            p.s.op("dve", lambda e: e.max(out=mx8[:], in_=work[:]), ["work"], ["mx8"])
            p.s.op("dve", lambda e: e.tensor_reduce(out=thr[:], in_=mx8[:], axis=AX.X, op=ALU.min),
                   ["mx8"], ["thr"])
            p.ts(Bq[:], selbuf[:], thr[:], 1.0, ALU.is_ge, ALU.subtract, r=["selbuf", "thr"], w=["Bq"])
            nhalf = 1 if nb <= 128 else 2
            for hf in range(nhalf):
                p.tr(psT[:, hf, :], Bq[:, hf * 128:(hf + 1) * 128], ident[:], r=["Bq", "ident"],
                     w=["psT"])
            for hf in range(nhalf):
                for rep in range(2):
                    p.cp(BT[:, hf, rep * 128:(rep + 1) * 128], psT[:, hf, :], r=["psT"], w=["BT"],
                         eng="act")
            if dbg == 4 or dbg == 100 + j * 10 + 4:
                return p.finish()
            for kc in range(qb + 1):
                biases = [(emat[:, kc % 64, :], BT[:, kc // 64, :], ["emat", "BT"])]
                if kc == qb:
                    biases.append((ident[:], cmask[:, 0, :], ["ident", "cmask"]))
                sp_, sb_ = attn_chunk(KsT2, slice(kc * 128, (kc + 1) * 128), None, biases,
                                      kc == 0, kc == qb, ["KsT2"])
                p.act(PT[sp_][:], bank(sb_), AF.Exp, r=[bk(sb_)], w=[("PT", sp_)], scale=0.125)
                p.mm(bank(0)[0:65, :], VsA[:, kc, :], PT[sp_][:], start=(kc == 0), stop=(kc == qb),
                     r=[("PT", sp_), "VsA"], w=[bk(0)])
            finish_branch(1, False)
            if dbg == 5 or dbg == 100 + j * 10 + 5:
                return p.finish()
            k_lo = max(0, qb - 4)
            for kc in range(k_lo, qb + 1):
                biases = []
                if kc == qb - 4:
                    biases.append((ident[:], cmask[:, 1, :], ["ident", "cmask"]))
                if kc == qb:
                    biases.append((ident[:], cmask[:, 0, :], ["ident", "cmask"]))
                slot = 4 + j - (qb - kc)
                sp_, sb_ = attn_chunk(KwT2, slice(slot * 128, (slot + 1) * 128), None, biases,
                                      kc == k_lo, kc == qb, ["KwT2"])
                p.act(PT[sp_][:], bank(sb_), AF.Exp, r=[bk(sb_)], w=[("PT", sp_)], scale=0.125)
                p.mm(bank(0)[0:65, :], VwA[:, slot, :], PT[sp_][:], start=(kc == k_lo),
                     stop=(kc == qb), r=[("PT", sp_), "VwA"], w=[bk(0)])
            finish_branch(2, False)
            if dbg == 6 or dbg == 100 + j * 10 + 6:
                return p.finish()
            p.cp(oacc[:], acc[:].rearrange("p (c q) -> p c q", c=4), r=["acc"], w=["oacc"])
            if fused:
                for dst in io["o_dst"](qb):
                    p.dma(dst, oacc[:], r=["oacc"], w=[io["dkey"]], q="pool")
            else:
                p.dma(oT_out[:, :, tok0 + j * 128:tok0 + (j + 1) * 128], oacc[:], r=["oacc"],
                      q="pool", is_output=True)
            if dbg == 100 + j * 10 + 7:
                return p.finish()
        p.cp(KwT2[:, 0:512], KwT2[:, 512:1024], r=["KwT2"], w=["KwT2"], eng="pool")
        p.cp(VwA[:, 0:4, :], VwA[:, 4:8, :], r=["VwA"], w=["VwA"], eng="pool")
        if dbg == 7 + st:
            return p.finish()
    if fused:
        return None
    return p.finish()


def nsa_consts(S):
    NCC = max(1, S // 2048)
    ml = np.arange(128)[:, None]
    q = np.arange(128)[None, :]
    pm = np.zeros((2, 16, 128, 128), np.float32)
    for a in range(2):
        for r in range(16):
            valid = (16 * ml + 15 <= 128 * r + q)
            if a == 1:
                valid = valid & (ml >= 1)
            pm[a, r] = np.where(valid, 0.0, MASKV)
    pmask = pm.astype(NPBF16)
    r0 = np.zeros((128, 256), np.float32)
    r0[0, :] = MASKV
    cur = np.where(ml <= q, 0.0, MASKV).astype(np.float32)
    upper = np.where(ml > q, 0.0, MASKV).astype(np.float32)
    cmask = np.stack([np.tile(cur, (1, 2)), np.tile(upper, (1, 2))], 0).astype(NPBF16)
    emat = np.zeros((64, 128, 128), np.float32)
    for e in range(64):
        emat[e, 2 * e, 0:64] = -MASKV
        emat[e, 2 * e + 1, 64:128] = -MASKV
    ws = [1, 2, 2, 2, 1]
    wfull = np.zeros((NCC * 128, 257), np.float32)
    for m in range(1, NCC * 128):
        n = m - 1
        for j in range(256):
            i = n - 4 * j + 1
            if 0 <= i <= 4:
                wfull[m, j] = ws[i]
    wfull[:, 256] = 1.0
    fix = np.zeros((128, 6), np.float32)
    lo = np.arange(128) < 64
    fix[:, 0] = np.where(lo, 0.0, 1.0)
    fix[:, 3] = np.where(lo, FORCE, 0.0)
    fix[:, 4] = 2.0 * FORCE
    fix[:, 5] = np.where(lo, -FORCE, FORCE)
    cpos = 16 * np.arange(NCC * 128) + 15
    ccos, csin = rope_tables(cpos)
    return dict(pmask=pmask, r0mask=r0.astype(NPBF16), cmask=cmask, emat=emat.astype(NPBF16),
                wfull=wfull.astype(NPBF16), fix3=fix, ccos_t=ccos, csin_t=csin, ident=ident_np())


def nsa_weights(a_w_in_l, cmp_pos_l, g):
    W = a_w_in_l
    q0 = g * 256
    def kcol(i):
        return W[:, 1024 + i * 256 + g * 64: 1024 + i * 256 + (g + 1) * 64]
    kc_, vc_, ks_, vs_, kw_, vw_ = [kcol(i) for i in range(6)]
    wg = W[:, 1024 + 6 * 256 + g * 12: 1024 + 6 * 256 + (g + 1) * 12]
    return dict(wq=np.ascontiguousarray(W[:, q0:q0 + 256]),
                wk3=np.ascontiguousarray(np.concatenate([kc_, ks_, kw_], 1)),
                wv3=np.ascontiguousarray(np.concatenate([vc_, vs_, vw_], 1)),
                wg=np.ascontiguousarray(wg),
                posT=np.ascontiguousarray(np.transpose(cmp_pos_l, (2, 0, 1))))


SEQ = 16384
NB = 2
CH = 4096
NPHASE = 99


def _run(nc, in_maps):
    res = run_bass_kernel_spmd(nc, in_maps, core_ids=list(range(8)))
    return res.results


def _cwb(conv_w, conv_b):
    return np.ascontiguousarray(np.concatenate([conv_w, conv_b[None]], 0).T.astype(np.float32))


def _chunk_with_halo(x_b, c, halo):
    lo = c * CH - halo
    if lo >= 0:
        return np.ascontiguousarray(x_b[lo:(c + 1) * CH])
    pad = np.zeros((-lo,) + x_b.shape[1:], x_b.dtype)
    return np.ascontiguousarray(np.concatenate([pad, x_b[0:(c + 1) * CH]], 0))


def kernel_unfused(x, norm_attn, norm_ffn, a_w_in, a_cmp_pos, a_cmp_w1, a_cmp_w2, a_w_out, kv_norm,
           b_w_kv, b_w_q, b_sinks, b_w_out, ffn_w_in, ffn_conv_w, ffn_conv_b, ffn_w_out,
           final_norm):
    f32 = lambda a: np.ascontiguousarray(np.asarray(a, dtype=np.float32))
    x = f32(x)
    norm_attn, norm_ffn = f32(norm_attn), f32(norm_ffn)
    a_w_in, a_cmp_pos, a_cmp_w1, a_cmp_w2, a_w_out = map(f32, (a_w_in, a_cmp_pos, a_cmp_w1,
                                                                a_cmp_w2, a_w_out))
    kv_norm, b_w_kv, b_w_q, b_sinks, b_w_out = map(f32, (kv_norm, b_w_kv, b_w_q, b_sinks, b_w_out))
    ffn_w_in, ffn_conv_w, ffn_conv_b, ffn_w_out, final_norm = map(
        f32, (ffn_w_in, ffn_conv_w, ffn_conv_b, ffn_w_out, final_norm))
    h = x
    ident = ident_np()
    gfin = np.ascontiguousarray(final_norm[None, :])
    cosA, sinA = rope_tables(np.arange(SEQ))
    constsA = nsa_consts(SEQ)

    for l in range(2):
        ncA = build_A(SEQ)
        maps = []
        for i in range(8):
            b, g = divmod(i, 4)
            m = dict(h_in=h[b], g_attn=gT_np(norm_attn[l]), w1=a_cmp_w1[l], w2=a_cmp_w2[l],
                     cos_t=cosA, sin_t=sinA)
            m.update(constsA)
            m.update(nsa_weights(a_w_in[l], a_cmp_pos[l], g))
            maps.append(m)
        resA = _run(ncA, maps)
        oT_full = np.zeros((NB, 16, 64, SEQ), NPBF16)
        for i in range(8):
            b, g = divmod(i, 4)
            o = resA[i]["oT_out"]
            for par in range(2):
                for hpl in range(2):
                    oT_full[b, g * 4 + 2 * hpl + par] = o[:, par * 2 + hpl, :]
        oT_full = oT_full.reshape(NB, 1024, SEQ)
        ncB = build_B([1] + [4] * 8, 1)
        maps = []
        for i in range(8):
            b, c = divmod(i, 4)
            maps.append(dict(
                h_in=_chunk_with_halo(h[b], c, 128),
                oT_in=np.ascontiguousarray(_chunk_with_halo(oT_full[b].T, c, 128).T),
                w_o=a_w_out[l], w_in=ffn_w_in[l], w_out=ffn_w_out[l],
                cwb=_cwb(ffn_conv_w[l], ffn_conv_b[l]), g_ffn=gT_np(norm_ffn[l]), g_fin=gfin,
                ident=ident))
        resB = _run(ncB, maps)
        h = np.stack([np.concatenate([resB[b * 4 + c]["h_out"] for c in range(4)], 0)
                      for b in range(NB)], 0)

    hkv = h
    for l in range(2, 4):
        j = l - 2
        ncC = build_C([2] + [4] * 8, 2, final_norm=(l == 3))
        maps = []
        for i in range(8):
            b, c = divmod(i, 4)
            pos = c * CH - 256 + np.arange(CH + 256)
            cos_t, sin_t = rope_tables(pos)
            maps.append(dict(
                h_in=_chunk_with_halo(h[b], c, 256), hkv_in=_chunk_with_halo(hkv[b], c, 256),
                w_q=b_w_q[j], w_kv=b_w_kv, sinks_b=sinks_row(b_sinks[j]),
                g_attn=gT_np(norm_attn[l]), g_kv=gT_np(kv_norm), cos_t=cos_t, sin_t=sin_t,
                masks=swa_masks(c > 0), w_o=b_w_out[j], w_in=ffn_w_in[l], w_out=ffn_w_out[l],
                cwb=_cwb(ffn_conv_w[l], ffn_conv_b[l]), g_ffn=gT_np(norm_ffn[l]), g_fin=gfin,
                ident=ident))
        resC = _run(ncC, maps)
        h = np.stack([np.concatenate([resC[b * 4 + c]["h_out"] for c in range(4)], 0)
                      for b in range(NB)], 0)
    return np.ascontiguousarray(h.astype(np.float32))


def build_fused(nphase=99):
    from concourse.bass import ds
    nph = [0]

    def stop():
        nph[0] += 1
        return nph[0] >= nphase

    nc = bass.Bass("TRN2", target_bir_lowering=False)
    p = Prog(nc)
    S = SEQ
    WB = 128 + CH
    WC = 256 + CH
    SUBW = 11 * 128
    xA = p.din("xA", [S, D], F32)
    xB = p.din("xB", [WB, D], F32)
    flag = p.din("flag", [128, 1], F32)
    out = p.dout("out", [CH, D], F32)
    oTloc = [nc.dram_tensor(f"oTloc{l}", [12 * 64, 4 * SUBW], BF16) for l in range(2)]
    OTb = [nc.dram_tensor(f"OTb{l}", [12 * 256, 4 * SUBW], BF16) for l in range(2)]
    oTwin = nc.dram_tensor("oTwin", [3 * 256, 4 * SUBW], BF16).ap()
    hloc = [nc.dram_tensor(f"hloc{k}", [CH, D], F32) for k in range(3)]
    Hb = [nc.dram_tensor(f"Hb{k}", [S, D], F32) for k in range(3)]
    hwin = nc.dram_tensor("hwin", [WC, D], F32).ap()
    hkvwin = nc.dram_tensor("hkvwin", [WC, D], F32).ap()
    rg = [[0, 1, 2, 3], [4, 5, 6, 7]]
    PID = p.s.pid

    def gather_group(src, dst, nchunk, rows, rk, wk):
        def fn(e, sem):
            for k in range(nchunk):
                e.collective_compute(
                    "AllGather", ALU.bypass, replica_groups=rg,
                    ins=[src.ap()[k * rows:(k + 1) * rows, :].opt()],
                    outs=[dst.ap()[k * 4 * rows:(k + 1) * 4 * rows, :].opt()]).then_inc(sem)
        p.s.cc(fn, [rk], [wk], n=nchunk)

    def h_row(tok):
        rank, rem = divmod(tok, CH)
        k, r = divmod(rem, 256)
        return (k * 4 + rank) * 256 + r

    def win_copy(dst, src, halo, q, rk, wk):
        s5 = src.rearrange("(k g r e) d -> k g r (e d)", k=16, g=4, e=8)
        dm = dst[halo:halo + CH, :].rearrange("(k g r e) d -> k g r (e d)", k=16, g=1, e=8)
        dh = dst[0:halo, :].rearrange("(k g r e) d -> k g r (e d)", k=1, g=1, e=8)
        h8 = halo // 8
        p.dmaf(lambda e: e.dma_start(
            out=dm, in_=s5[:, ds(PID(e, "c", lambda pid: pid % 4), 1), :, :]),
            r=[rk], w=[wk], q=q)
        p.dmaf(lambda e: e.dma_start(
            out=dh, in_=s5[15:16, ds(PID(e, "cm1", lambda pid: (pid + 3) % 4), 1), 32 - h8:32, :]),
            r=[rk], w=[wk], q=q)

    for l in range(2):
        p.sfx = f"_A{l}"
        rk = [] if l == 0 else [f"Hb{l - 1}"]
        O5 = oTloc[l].ap().rearrange("(c s d) (b t) -> c s d b t", c=4, s=3, b=4)

        def o_dst(qb, O5=O5):
            c, sl = divmod(qb, 32)
            sl += 1
            dsts = [O5[c, sl // 11, :, :, (sl % 11) * 128:(sl % 11) * 128 + 128]]
            if sl == 32 and c < 3:
                dsts.append(O5[c + 1, 0, :, :, 0:128])
            return dsts

        build_A(S, p=p, io=dict(h_ap=(xA if l == 0 else Hb[l - 1].ap()),
                                h_row=((lambda t: t) if l == 0 else h_row),
                                rkeys=rk, dkey=f"oTloc{l}", o_dst=o_dst,
                                o_zero=O5[0, 0, :, :, 0:128]))
        p.phase_end()
        gather_group(oTloc[l], OTb[l], 12, 64, f"oTloc{l}", f"OTb{l}")
        if stop():
            return p.finish(), dict(p.dins)
        p.sfx = f"_B{l}"
        O3 = OTb[l].ap().rearrange("(c r) f -> c r f", c=4)
        p.dmaf(lambda e, O3=O3: e.dma_start(
            out=oTwin.rearrange("(c r) f -> c r f", c=1),
            in_=O3[ds(PID(e, "c", lambda pid: pid % 4), 1), :, :]),
            r=[f"OTb{l}"], w=["oTwin"], q="act")
        if l == 0:
            h_ap = xB
            rkb = ["oTwin"]
        else:
            win_copy(hwin[0:WB, :], Hb[l - 1].ap(), 128, "act", f"Hb{l - 1}", "hwin")
            h_ap = hwin[0:WB, :]
            rkb = ["oTwin", "hwin"]
        W5 = oTwin.rearrange("(s g d) (b t) -> d s g b t", s=3, g=4, b=4)

        def oT_ap(wt, g, W5=W5):
            return W5[:, wt // 11, g, :, (wt % 11) * 128:(wt % 11) * 128 + 128]

        build_B([1] + [4] * 8, 1, p=p,
                io=dict(h_ap=h_ap, oT_ap=oT_ap, h_dst=hloc[l].ap(), flag=flag,
                        rkeys=rkb, dkey=f"hloc{l}"))
        p.phase_end()
        gather_group(hloc[l], Hb[l], 16, 256, f"hloc{l}", f"Hb{l}")
        if stop():
            return p.finish(), dict(p.dins)

    for l in range(2, 4):
        p.sfx = f"_C{l}"
        last = (l == 3)
        if l == 2:
            win_copy(hkvwin, Hb[1].ap(), 256, "sp", "Hb1", "hkvwin")
            h_ap, hkv_ap, rkc = hkvwin, hkvwin, ["hkvwin"]
        else:
            win_copy(hwin, Hb[2].ap(), 256, "sp", "Hb2", "hwin")
            h_ap, hkv_ap, rkc = hwin, hkvwin, ["hwin", "hkvwin"]
        build_C([2] + [4] * 8, 2, final_norm=last, p=p,
                io=dict(h_ap=h_ap, hkv_ap=hkv_ap, h_dst=(out if last else hloc[2].ap()), flag=flag,
                        rkeys=rkc, dkey=(None if last else "hloc2")))
        if not last:
            p.phase_end()
            gather_group(hloc[2], Hb[2], 16, 256, "hloc2", "Hb2")
            if stop():
                return p.finish(), dict(p.dins)
    return p.finish(), dict(p.dins)


def kernel(x, norm_attn, norm_ffn, a_w_in, a_cmp_pos, a_cmp_w1, a_cmp_w2, a_w_out, kv_norm,
           b_w_kv, b_w_q, b_sinks, b_w_out, ffn_w_in, ffn_conv_w, ffn_conv_b, ffn_w_out,
           final_norm):
    f32 = lambda a: np.ascontiguousarray(np.asarray(a, dtype=np.float32))
    x = f32(x)
    norm_attn, norm_ffn = f32(norm_attn), f32(norm_ffn)
    a_w_in, a_cmp_pos, a_cmp_w1, a_cmp_w2, a_w_out = map(f32, (a_w_in, a_cmp_pos, a_cmp_w1,
                                                                a_cmp_w2, a_w_out))
    kv_norm, b_w_kv, b_w_q, b_sinks, b_w_out = map(f32, (kv_norm, b_w_kv, b_w_q, b_sinks, b_w_out))
    ffn_w_in, ffn_conv_w, ffn_conv_b, ffn_w_out, final_norm = map(
        f32, (ffn_w_in, ffn_conv_w, ffn_conv_b, ffn_w_out, final_norm))
    nc, dins = build_fused(NPHASE)
    ident = ident_np()
    gfin = np.ascontiguousarray(final_norm[None, :])
    cosA, sinA = rope_tables(np.arange(SEQ))
    constsA = nsa_consts(SEQ)
    maps = []
    for i in range(8):
        b, c = divmod(i, 4)
        g = c
        m = dict(xA=x[b], xB=_chunk_with_halo(x[b], c, 128),
                 flag=np.full((128, 1), 0.0 if c == 0 else 1.0, np.float32))
        for l in range(2):
            a = dict(g_attn=gT_np(norm_attn[l]), w1=a_cmp_w1[l], w2=a_cmp_w2[l],
                     cos_t=cosA, sin_t=sinA)
            a.update(constsA)
            a.update(nsa_weights(a_w_in[l], a_cmp_pos[l], g))
            for k, v in a.items():
                m[f"{k}_A{l}"] = v
            bb = dict(w_o=a_w_out[l], w_in=ffn_w_in[l], w_out=ffn_w_out[l],
                      cwb=_cwb(ffn_conv_w[l], ffn_conv_b[l]), g_ffn=gT_np(norm_ffn[l]), g_fin=gfin,
                      ident=ident)
            for k, v in bb.items():
                m[f"{k}_B{l}"] = v
        pos = c * CH - 256 + np.arange(CH + 256)
        cos_t, sin_t = rope_tables(pos)
        for l in range(2, 4):
            j = l - 2
            cc = dict(w_q=b_w_q[j], w_kv=b_w_kv, sinks_b=sinks_row(b_sinks[j]),
                      g_attn=gT_np(norm_attn[l]), g_kv=gT_np(kv_norm), cos_t=cos_t, sin_t=sin_t,
                      masks=swa_masks(c > 0), w_o=b_w_out[j], w_in=ffn_w_in[l], w_out=ffn_w_out[l],
                      cwb=_cwb(ffn_conv_w[l], ffn_conv_b[l]), g_ffn=gT_np(norm_ffn[l]), g_fin=gfin,
                      ident=ident)
            for k, v in cc.items():
                m[f"{k}_C{l}"] = v
        m = {k: v for k, v in m.items() if k in dins}
        maps.append(m)
    res = _run(nc, maps)
    h = np.stack([np.concatenate([res[b * 4 + c]["out"] for c in range(4)], 0)
                  for b in range(NB)], 0)
    return np.ascontiguousarray(h.astype(np.float32))
```

```python
import numpy as np
import ml_dtypes
import concourse.bass as bass
import concourse.mybir as mybir
from concourse.bass_utils import run_bass_kernel_spmd

F32 = mybir.dt.float32
BF16 = mybir.dt.bfloat16
AF = mybir.ActivationFunctionType
ALU = mybir.AluOpType
AX = mybir.AxisListType

NPBF16 = ml_dtypes.bfloat16

D = 1024
DFF = 2816
EPS = 1e-6
MASKV = -240000.0

COMPUTE = ("pe", "act", "dve", "pool")
EPOCH = 30000
NSLOT = 12
SAME_ENG_SYNC = True
ZERO_BIAS = True


class Sched:
    def __init__(self, nc):
        self.nc = nc
        self.streams = {e: [] for e in COMPUTE + ("sp",)}
        self.cnt = {e: 0 for e in COMPUTE}
        self.known = {e: {} for e in self.streams}
        self.known_dma = {e: set() for e in self.streams}
        self.last_w = {}
        self.readers = {}
        self.ndma = {e: 0 for e in self.streams}
        self.sems = {}
        self.nsem = 0
        self.out_dmas = []
        self.ncc = 0

    def _sem(self, name):
        if name not in self.sems:
            self.sems[name] = self.nc.alloc_semaphore(name=name)
        return self.sems[name]

    def _ev_wait_args(self, ev):
        kind = ev[0]
        if kind == "c":
            _, eng, idx = ev
            ep, off = divmod(idx, EPOCH)
            return self._sem(f"s_{eng}_{ep}"), off + 1
        elif kind == "x":
            return self._sem(f"x_{ev[1]}"), ev[2]
        else:
            _, q, j = ev
            slot, use = j % NSLOT, j // NSLOT
            return self._sem(f"d_{q}_{slot}"), 16 * (use + 1)

    def _deps(self, eng, reads, writes):
        deps = set()
        for k in reads:
            w = self.last_w.get(k)
            if w is not None:
                deps.add(w)
        for k in writes:
            w = self.last_w.get(k)
            if w is not None:
                deps.add(w)
            for r in self.readers.get(k, ()):
                deps.add(r)
        waits = []
        best = {}
        for ev in deps:
            if ev[0] == "c":
                _, src, idx = ev
                if src == eng and eng == "pe":
                    continue
                if self.known[eng].get(src, -1) >= idx:
                    continue
                if best.get(src, -1) < idx:
                    best[src] = idx
            else:
                if ev in self.known_dma[eng]:
                    continue
                waits.append(ev)
                self.known_dma[eng].add(ev)
        for src, idx in best.items():
            self.known[eng][src] = idx
            waits.append(("c", src, idx))
        return waits

    def _mark(self, ev, reads, writes):
        for k in reads:
            self.readers.setdefault(k, []).append(ev)
        for k in writes:
            self.last_w[k] = ev
            self.readers[k] = []

    def op(self, eng, fn, reads=(), writes=()):
        assert eng in COMPUTE
        waits = self._deps(eng, reads, writes)
        idx = self.cnt[eng]
        self.cnt[eng] += 1
        ev = ("c", eng, idx)
        if not SAME_ENG_SYNC or eng == "pe":
            self.known[eng][eng] = idx
        self._mark(ev, reads, writes)
        self.streams[eng].append((waits, fn, ev))
        return ev

    def dma(self, q, fn, reads=(), writes=(), is_output=False):
        waits = self._deps(q, reads, writes)
        j = self.ndma[q]
        self.ndma[q] += 1
        if j >= NSLOT:
            prev = ("d", q, j - NSLOT)
            if prev not in self.known_dma[q]:
                waits.append(prev)
                self.known_dma[q].add(prev)
        ev = ("d", q, j)
        self._mark(ev, reads, writes)
        self.streams[q].append((waits, fn, ev))
        if is_output:
            self.out_dmas.append(ev)
        return ev

    def pid(self, e, key="pid", fn=None):
        k = (self.cur_eng, key)
        if k not in self.pid_cache:
            if key == "pid":
                self.pid_cache[k] = e.partition_id()
            else:
                self.pid_cache[k] = e.snap(fn(self.pid(e)))
        return self.pid_cache[k]

    def cc(self, fn, reads=(), writes=(), n=1):
        waits = self._deps("pool", reads, writes)
        ev = ("x", self.ncc, n)
        self.ncc += 1
        self._mark(ev, reads, writes)
        self.streams["pool"].append((waits, fn, ev))
        self.known_dma["pool"].add(ev)
        idx = self.cnt["pool"]
        self.cnt["pool"] += 1
        nev = ("c", "pool", idx)
        self._mark(nev, (), writes)
        self.streams["pool"].append(([ev], lambda e: e.nop(), nev))
        return nev

    def barrier(self):
        evs = []
        for eng in COMPUTE:
            if self.cnt[eng] > 0:
                evs.append(("c", eng, self.cnt[eng] - 1))
        for q, n in self.ndma.items():
            for j in range(max(0, n - NSLOT), n):
                evs.append(("d", q, j))
        for eng in self.streams:
            waits = []
            for ev in evs:
                if ev[0] == "c":
                    if ev[1] == eng:
                        continue
                    if self.known[eng].get(ev[1], -1) >= ev[2]:
                        continue
                    self.known[eng][ev[1]] = ev[2]
                elif ev[0] == "x":
                    continue
                elif ev in self.known_dma[eng]:
                    continue
                else:
                    self.known_dma[eng].add(ev)
                waits.append(ev)
            self.streams[eng].append((waits, None, None))

    def emit(self, final=True):
        nc = self.nc
        final_waits = list(self.out_dmas) if final else []
        self.pid_cache = {}
        with nc.Block() as block:
            def run(engname, e):
                self.cur_eng = engname
                for waits, fn, ev in self.streams[engname]:
                    for w in waits:
                        s, v = self._ev_wait_args(w)
                        e.wait_ge(s, v)
                    if fn is None:
                        continue
                    s, v = self._ev_wait_args(ev)
                    if ev[0] == "x":
                        fn(e, s)
                        continue
                    ins = fn(e)
                    if ev[0] == "c":
                        ins.then_inc(s, 1)
                    else:
                        ins.then_inc(s, 16)
                if engname == "sp":
                    for w in final_waits:
                        s, v = self._ev_wait_args(w)
                        e.wait_ge(s, v)

            @block.tensor
            def _(e):
                run("pe", e)

            @block.scalar
            def _(e):
                run("act", e)

            @block.vector
            def _(e):
                run("dve", e)

            @block.gpsimd
            def _(e):
                run("pool", e)

            @block.sync
            def _(e):
                run("sp", e)
        for k in self.streams:
            self.streams[k] = []


class Prog:
    def __init__(self, nc):
        from contextlib import ExitStack
        self.nc = nc
        self.s = Sched(nc)
        self.es = ExitStack()
        self.ndram = 0
        self.sfx = ""
        self.dins = {}
        self.ext = {}

    def sb(self, name, shape, dt):
        return self.es.enter_context(self.nc.sbuf_tensor("sb_" + name + self.sfx, list(shape), dt))

    def ps(self, name, shape, dt=F32):
        return self.es.enter_context(self.nc.psum_tensor("ps_" + name + self.sfx, list(shape), dt))

    def din(self, name, shape, dt):
        nm = name + self.sfx
        if nm in self.ext:
            return self.ext[nm]
        self.dins[nm] = (tuple(shape), dt)
        return self.nc.dram_tensor(nm, list(shape), dt, kind="ExternalInput").ap()

    def dint(self, name, shape, dt):
        return self.nc.dram_tensor(name, list(shape), dt)

    def phase_end(self):
        from contextlib import ExitStack
        self.s.barrier()
        self.s.emit(final=False)
        self.es.close()
        self.es = ExitStack()

    def dmaf(self, fn, r=(), w=(), q="sp", is_output=False):
        return self.s.dma(q, fn, r, w, is_output)

    def dout(self, name, shape, dt):
        return self.nc.dram_tensor(name, list(shape), dt, kind="ExternalOutput").ap()

    def dma(self, out, in_, r=(), w=(), q="sp", is_output=False):
        return self.s.dma(q, lambda e: e.dma_start(out=out, in_=in_), r, w, is_output)

    def mm(self, out, lhsT, rhs, start, stop, r=(), w=()):
        return self.s.op("pe", lambda e: e.matmul(out, lhsT, rhs, start=start, stop=stop), r, w)

    def tr(self, out, in_, ident, r=(), w=()):
        return self.s.op("pe", lambda e: e.transpose(out, in_, ident), r, w)

    def act(self, out, in_, func, r=(), w=(), bias=None, scale=None, accum_out=None):
        kw = {}
        if bias is not None:
            kw["bias"] = bias
        if scale is not None:
            kw["scale"] = scale
        if accum_out is not None:
            kw["accum_out"] = accum_out
        return self.s.op("act", lambda e: e.activation(out, in_, func, **kw), r, w)

    def tt(self, out, in0, in1, op, r=(), w=(), eng="dve"):
        return self.s.op(eng, lambda e: e.tensor_tensor(out, in0, in1, op), r, w)

    def ts(self, out, in0, s1, s2, op0, op1=None, r=(), w=(), eng="dve", accum_out=None):
        kw = {}
        if accum_out is not None:
            kw["accum_out"] = accum_out
        if op1 is None:
            return self.s.op(eng, lambda e: e.tensor_scalar(out, in0, s1, s2, op0, **kw), r, w)
        return self.s.op(eng, lambda e: e.tensor_scalar(out, in0, s1, s2, op0, op1, **kw), r, w)

    def stt(self, out, in0, scalar, in1, op0, op1, r=(), w=(), eng="dve"):
        return self.s.op(eng, lambda e: e.scalar_tensor_tensor(out, in0, scalar, in1, op0, op1), r, w)

    def cp(self, out, in_, r=(), w=(), eng="dve"):
        if eng == "act":
            return self.s.op("act", lambda e: e.copy(out, in_), r, w)
        return self.s.op(eng, lambda e: e.tensor_copy(out, in_), r, w)

    def recip(self, out, in_, r=(), w=()):
        return self.s.op("dve", lambda e: e.reciprocal(out, in_), r, w)

    def memset(self, ap, val, w=(), eng="dve"):
        return self.s.op(eng, lambda e: e.memset(ap, val), (), w)

    def finish(self):
        self.s.emit()
        self.es.close()
        return self.nc


def load_cast_weight(p, w_dram, dst, nk, ncols, stage, tag, chunk_cols=1024):
    i = 0
    for kc in range(nk):
        for c0 in range(0, ncols, chunk_cols):
            cw = min(chunk_cols, ncols - c0)
            stg = stage[i % 2]
            p.dma(stg[:, 0:cw], w_dram[kc * 128:(kc + 1) * 128, c0:c0 + cw],
                  w=[("stg", i % 2)])
            p.cp(dst[:, kc, c0:c0 + cw], stg[:, 0:cw], r=[("stg", i % 2)],
                 w=[(tag, kc)], eng="pool")
            i += 1


def rmsnorm_tile(p, x_ap, gain_bc, out_ap, scr, keys_r, keys_w, tagk):
    sq, ss, sd, rs = scr
    p.act(sq, x_ap, AF.Square, r=keys_r, w=[("sq", tagk), ("ss", tagk)], accum_out=ss)
    p.act(sd, ss, AF.Sqrt, r=[("ss", tagk)], w=[("sd", tagk)], bias=EPS, scale=1.0 / D)
    p.recip(rs, sd, r=[("sd", tagk)], w=[("rs", tagk)])
    if gain_bc is None:
        p.ts(out_ap, x_ap, rs, None, ALU.mult, r=list(keys_r) + [("rs", tagk)], w=keys_w)
    else:
        p.stt(out_ap, x_ap, rs, gain_bc, ALU.mult, ALU.mult,
              r=list(keys_r) + [("rs", tagk), "gains"], w=keys_w)


class FFNCtx:
    def __init__(self, p, pre, max_nt=4):
        self.p = p
        self.max_nt = max_nt
        nt = max_nt
        self.wo = p.sb(pre + "wo", [64, 16, 1024], BF16)
        self.woch = [p.sb(pre + f"woch{i}", [128, 512], BF16) for i in range(2)]
        self.fstage = [p.sb(pre + f"fstg{i}", [128, 2048], F32) for i in range(2)]
        self.stage = [self.fstage[i][:, 0:1024] for i in range(2)]
        self.gT = p.sb(pre + "gT", [128, 8], F32)
        self.wch = [p.sb(pre + f"wch{i}", [128, 8, 2, 128], BF16) for i in range(2)]
        self.h1 = p.sb(pre + "h1", [128, nt, 1024], F32)
        self.oT = p.sb(pre + "oT", [64, 16, nt * 128], BF16)
        self.hn = p.sb(pre + "hn", [128, 1024], BF16)
        self.hnT = p.sb(pre + "hnT", [128, 8, nt * 128], BF16)
        self.actT = p.sb(pre + "actT", [128, 22, nt * 128], BF16)
        self.usb = [[p.sb(pre + f"usb{i}{a}", [128, 2 + nt * 128], F32) for a in range(2)]
                    for i in range(2)]
        self.carry = p.sb(pre + "carry", [128, 44, 2], F32)
        self.cwb = p.sb(pre + "cwb", [128, 44, 4], F32)
        self.t1 = p.sb(pre + "t1", [128, nt * 128], F32)
        self.t2 = p.sb(pre + "t2", [128, nt * 128], F32)
        self.ca = p.sb(pre + "ca", [128, nt * 128], F32)
        self.cg = p.sb(pre + "cg", [128, nt * 128], F32)
        self.sa = p.sb(pre + "sa", [128, nt * 128], F32)
        self.sq = p.sb(pre + "sq", [128, 1024], BF16)
        self.ss = p.sb(pre + "ss", [128, 1], F32)
        self.sd = p.sb(pre + "sd", [128, 1], F32)
        self.rs = p.sb(pre + "rs", [128, 1], F32)
        self.gainf = p.sb(pre + "gainf", [128, 1024], F32)
        self.ident = p.sb(pre + "ident", [128, 128], BF16)
        self.hfin = p.sb(pre + "hfin", [128, 1024], F32)
        self.psum = p.ps(pre + "psum", [128, 7 * 512])
        self.psT = p.ps(pre + "psT", [128, 8, 128], BF16)
        self.psA = [self.bank(0), self.bank(1)]
        self.psU = [[self.bank(2), self.bank(3)], [self.bank(4), self.bank(5)]]
        self.nA = 0
        self.nfc = 0
        self.nwo = 0

    def bank(self, i, n=1):
        return self.psum[:, i * 512:(i + n) * 512]

    def load_weights(self, w_o, w_in, w_out, cwb, g_ffn, ident, g_final=None, head_order=None):
        p = self.p
        self.w_in = w_in
        p.dma(self.ident[:], ident, w=["ident"])
        p.dma(self.cwb[:], cwb.rearrange("(c p) f -> p c f", p=128), w=["cwb"])
        p.dma(self.gT[:], g_ffn, w=["gT"])
        self.w_out = w_out
        if g_final is not None:
            p.dma(self.gainf[:], g_final.to_broadcast([128, 1024]), w=["gainf"])
        p.memset(self.carry[:], 0.0, w=["carry"])
        if w_o is not None:
            ho = head_order if head_order is not None else list(range(16))
            for i, h in enumerate(ho):
                stg = self.stage[i % 2]
                sk = ("stg", "ffn", i % 2, 0)
                p.dma(stg[0:64, :], w_o[h * 64:(h + 1) * 64, :], w=[sk])
                p.cp(self.wo[:, i, :], stg[0:64, :], r=[sk], w=[("wo", i)], eng="pool")

    def run_supertile(self, nt, h_src, oT_src, h_dst, n_skip_out=0, final_norm=False,
                      h1_preloaded=False, h_fn=None, oT_fn=None, n_flag=0, flag=None,
                      rkeys=(), dkey=None):
        p = self.p
        ntok = nt * 128
        if not h1_preloaded:
            for j in range(nt):
                p.dma(self.h1[:, j, :], h_src[j * 128:(j + 1) * 128, :], r=list(rkeys),
                      w=[("h1", j)])
                if j < n_flag:
                    p.ts(self.h1[:, j, :], self.h1[:, j, :], flag, None, ALU.mult,
                         r=[("h1", j), "flag"], w=[("h1", j)])
        if oT_src is not None or oT_fn is not None:
            if oT_fn is not None:
                for j in range(nt):
                    for g in range(4):
                        p.dma(self.oT[:, g * 4:(g + 1) * 4, j * 128:(j + 1) * 128], oT_fn(j, g),
                              r=list(rkeys), w=["oT"])
            elif not isinstance(oT_src, str):
                p.dma(self.oT[:, :, 0:ntok], oT_src.rearrange("(c p) t -> p c t", p=64), w=["oT"])
            for j in range(nt):
                for half in range(2):
                    ps = self.psA[self.nA % 2]
                    pk = ("bank", self.nA % 2)
                    self.nA += 1
                    for kc in range(16):
                        p.mm(ps, self.oT[:, kc, j * 128:(j + 1) * 128],
                             self.wo[:, kc, half * 512:(half + 1) * 512],
                             start=(kc == 0), stop=(kc == 15),
                             r=["oT", ("wo", kc)], w=[pk])
                    hs = self.h1[:, j, half * 512:(half + 1) * 512]
                    p.tt(hs, hs, ps, ALU.add, r=[pk, ("h1", j)], w=[("h1", j)])
        for j in range(nt):
            rmsnorm_tile(p, self.h1[:, j, :], None, self.hn[:],
                         (self.sq[:], self.ss[:], self.sd[:], self.rs[:]),
                         [("h1", j)], ["hn"], "f")
            for kc in range(8):
                p.tr(self.psT[:, kc, :], self.hn[:, kc * 128:(kc + 1) * 128], self.ident[:],
                     r=["hn", "ident"], w=["psT"])
            p.cp(self.hnT[:, :, j * 128:(j + 1) * 128], self.psT[:], r=["psT"], w=[("hnT", j)],
                 eng="act")
        hnT_keys = [("hnT", j) for j in range(nt)]
        for fc in range(22):
            par = self.nfc % 2
            self.nfc += 1
            stg = self.fstage[par]
            wch = self.wch[par]
            for ag in range(2):
                c0 = ag * DFF + fc * 128
                p.dma(stg[:, ag * 1024:(ag + 1) * 1024].rearrange("p (c f) -> p c f", c=8),
                      self.w_in[:, c0:c0 + 128].rearrange("(c p) f -> p c f", p=128),
                      w=[("stg", "ffn", par, ag)])
                p.tt(wch[:, :, ag, :],
                     stg[:, ag * 1024:(ag + 1) * 1024].rearrange("p (c f) -> p c f", c=8),
                     self.gT[:].unsqueeze(2).to_broadcast([128, 8, 128]), ALU.mult,
                     r=[("stg", "ffn", par, ag), "gT"], w=[("wch", par, ag)], eng="pool")
            cs = []
            for ag in range(2):
                ps = self.psU[par][ag]
                pk = ("bank", 2 + 2 * par + ag)
                for kc in range(8):
                    p.mm(ps[:, 0:ntok], wch[:, kc, ag, :], self.hnT[:, kc, 0:ntok],
                         start=(kc == 0), stop=(kc == 7),
                         r=[("wch", par, ag)] + hnT_keys, w=[pk])
                usb = self.usb[par][ag]
                uk = ("usb", par, ag)
                ch = ag * 22 + fc
                p.cp(usb[:, 0:2], self.carry[:, ch, :], r=["carry%d" % ch, "carry"], w=[uk],
                     eng="pool")
                p.cp(usb[:, 2:2 + ntok], ps[:, 0:ntok], r=[pk], w=[uk], eng="act")
                p.cp(self.carry[:, ch, :], usb[:, ntok:ntok + 2], r=[uk], w=["carry%d" % ch],
                     eng="pool")
                cw = self.cwb
                dst = self.ca if ag == 0 else self.cg
                dk = "ca" if ag == 0 else "cg"
                p.ts(self.t1[:, 0:ntok], usb[:, 2:2 + ntok], cw[:, ch, 2:3], cw[:, ch, 3:4],
                     ALU.mult, ALU.add, r=[uk, "cwb"], w=["t1"])
                p.stt(self.t2[:, 0:ntok], usb[:, 1:1 + ntok], cw[:, ch, 1:2], self.t1[:, 0:ntok],
                      ALU.mult, ALU.add, r=[uk, "cwb", "t1"], w=["t2"])
                p.stt(dst[:, 0:ntok], usb[:, 0:ntok], cw[:, ch, 0:1], self.t2[:, 0:ntok],
                      ALU.mult, ALU.add, r=[uk, "cwb", "t2"], w=[dk])
            p.act(self.sa[:, 0:ntok], self.ca[:, 0:ntok], AF.Silu, r=["ca"], w=["sa"])
            p.tt(self.actT[:, fc, 0:ntok], self.sa[:, 0:ntok], self.cg[:, 0:ntok], ALU.mult,
                 r=["sa", "cg"], w=[("actT", fc)])
        for half in range(2):
            for fc in range(22):
                wp = self.nwo % 2
                self.nwo += 1
                stg = self.fstage[wp]
                sk = ("stg", "ffn", wp, 0)
                p.dma(stg[:, 0:512], self.w_out[fc * 128:(fc + 1) * 128, half * 512:(half + 1) * 512],
                      w=[sk])
                p.cp(self.woch[wp][:], stg[:, 0:512], r=[sk], w=[("woch", wp)], eng="pool")
                for j in range(nt):
                    p.mm(self.bank(j), self.actT[:, fc, j * 128:(j + 1) * 128], self.woch[wp][:],
                         start=(fc == 0), stop=(fc == 21),
                         r=[("actT", fc), ("woch", wp)], w=[("bank", j)])
            for j in range(nt):
                hs = self.h1[:, j, half * 512:(half + 1) * 512]
                p.tt(hs, hs, self.bank(j), ALU.add, r=[("bank", j), ("h1", j)], w=[("h1", j)])
        for j in range(nt):
            if h_dst is not None and j >= n_skip_out:
                jo = j - n_skip_out
                if final_norm:
                    rmsnorm_tile(p, self.h1[:, j, :], self.gainf[:], self.hfin[:],
                                 (self.sq[:], self.ss[:], self.sd[:], self.rs[:]),
                                 [("h1", j), "gainf"], ["hfin"], "f")
                    p.dma(h_dst[jo * 128:(jo + 1) * 128, :], self.hfin[:], r=["hfin"],
                          w=([dkey] if dkey else []), q="pool", is_output=True)
                else:
                    p.dma(h_dst[jo * 128:(jo + 1) * 128, :], self.h1[:, j, :], r=[("h1", j)],
                          w=([dkey] if dkey else []), q="pool", is_output=(dkey is None))


def ident_np():
    return np.eye(128, dtype=np.float32).astype(NPBF16)


def b_head_order():
    return [g * 4 + 2 * hpl + par for g in range(4) for par in range(2) for hpl in range(2)]


def build_B(st_sizes, n_skip_tiles, final_norm=False, p=None, io=None):
    fused = p is not None
    if not fused:
        nc = bass.Bass("TRN2", target_bir_lowering=False)
        p = Prog(nc)
    ntiles = sum(st_sizes)
    ntok = ntiles * 128
    if not fused:
        h_in = p.din("h_in", [ntok, D], F32)
        oT_in = p.din("oT_in", [D, ntok], BF16)
    w_o = p.din("w_o", [D, D], F32)
    w_in = p.din("w_in", [D, 2 * DFF], F32)
    w_out = p.din("w_out", [DFF, D], F32)
    cwb = p.din("cwb", [2 * DFF, 4], F32)
    g_ffn = p.din("g_ffn", [128, 8], F32)
    g_fin = p.din("g_fin", [1, D], F32)
    ident = p.din("ident", [128, 128], BF16)
    if not fused:
        h_out = p.dout("h_out", [(ntiles - n_skip_tiles) * 128, D], F32)
    else:
        h_out = io["h_dst"]
    f = FFNCtx(p, "f_", max_nt=max(st_sizes))
    f.load_weights(w_o, w_in, w_out, cwb, g_ffn, ident, g_fin,
                   head_order=(b_head_order() if fused else None))
    if fused:
        flag_sb = p.sb("flag", [128, 1], F32)
        p.dma(flag_sb[:], io["flag"], w=["flag"])
    t0 = 0
    for nt in st_sizes:
        skip = max(0, min(nt, n_skip_tiles - t0))
        o0 = max(0, t0 - n_skip_tiles)
        dst = h_out[o0 * 128:(o0 + nt - skip) * 128, :] if skip < nt else None
        if fused:
            f.run_supertile(nt, io["h_ap"][t0 * 128:(t0 + nt) * 128, :], None, dst,
                            n_skip_out=skip, final_norm=final_norm,
                            oT_fn=lambda j, g, t0=t0: io["oT_ap"](t0 + j, g),
                            n_flag=skip, flag=flag_sb[:], rkeys=io["rkeys"], dkey=io["dkey"])
        else:
            f.run_supertile(nt, h_in[t0 * 128:(t0 + nt) * 128, :],
                            oT_in[:, t0 * 128:(t0 + nt) * 128], dst, n_skip_out=skip,
                            final_norm=final_norm)
        t0 += nt
    if fused:
        return None
    return p.finish()


def c_head_order():
    return [8 * g + 2 * hpl + par for g in range(2) for par in range(2) for hpl in range(4)]


def build_C(st_sizes, n_skip_tiles, final_norm=False, p=None, io=None):
    fused = p is not None
    if not fused:
        nc = bass.Bass("TRN2", target_bir_lowering=False)
        p = Prog(nc)
    ntiles = sum(st_sizes)
    ntok = ntiles * 128
    mx = max(st_sizes)
    if not fused:
        h_in = p.din("h_in", [ntok, D], F32)
        hkv_in = p.din("hkv_in", [ntok, D], F32)
    w_q = p.din("w_q", [D, D], F32)
    w_kv = p.din("w_kv", [D, 256], F32)
    sinks_b = p.din("sinks_b", [1, 2048], F32)
    g_attn = p.din("g_attn", [128, 8], F32)
    g_kv = p.din("g_kv", [128, 8], F32)
    cos_t = p.din("cos_t", [128, ntok], F32)
    sin_t = p.din("sin_t", [128, ntok], F32)
    masks = p.din("masks", [3, 128, 512], BF16)
    w_o = p.din("w_o", [D, D], F32)
    w_in = p.din("w_in", [D, 2 * DFF], F32)
    w_out = p.din("w_out", [DFF, D], F32)
    cwb = p.din("cwb", [2 * DFF, 4], F32)
    g_ffn = p.din("g_ffn", [128, 8], F32)
    g_fin = p.din("g_fin", [1, D], F32)
    ident = p.din("ident", [128, 128], BF16)
    if not fused:
        h_out = p.dout("h_out", [(ntiles - n_skip_tiles) * 128, D], F32)
    else:
        h_out = io["h_dst"]

    f = FFNCtx(p, "f_", max_nt=mx)
    f.load_weights(w_o, w_in, w_out, cwb, g_ffn, ident, g_fin, head_order=c_head_order())
    if fused:
        flag_sb = p.sb("flag", [128, 1], F32)
        p.dma(flag_sb[:], io["flag"], w=["flag"])

    gTq = p.sb("gTq", [128, 8], F32)
    gTk = p.sb("gTk", [128, 8], F32)
    wk2 = p.sb("wk2", [128, 8, 2, 2, 128], BF16)
    wv = p.sb("wv", [128, 8, 128], BF16)
    hkv = p.sb("hkv", [128, 1024], F32)
    hnqT = f.hnT
    hkvT = p.sb("hkvT", [128, 8, mx * 128], BF16)
    QT2 = p.sb("QT2", [128, 8, mx * 128], BF16)
    KT2 = p.sb("KT2", [128, 2, (mx + 1) * 128], BF16)
    VA = p.sb("VA", [128, mx + 1, 2, 65], BF16)
    PT = [p.sb(f"PT{i}", [128, 1024], BF16) for i in range(2)]
    msk = p.sb("msk", [128, 3, 512], BF16)
    cos_sb = p.sb("cos_sb", [128, mx * 128], F32)
    sin_sb = p.sb("sin_sb", [128, mx * 128], F32)
    sexp = p.sb("sexp", [128, 2048], BF16)
    zr = p.sb("zr", [128, 1024], F32)
    rz = zr
    ones = p.sb("ones", [128, 64], F32)
    osb = f.hfin[0:64, :]
    psS = [f.bank(2, 2), f.bank(4, 2)]
    psSk = [[("bank", 2), ("bank", 3)], [("bank", 4), ("bank", 5)]]
    psO = f.bank(0, 2)
    psOk = [("bank", 0), ("bank", 1)]
    psB = f.bank(6)
    psBk = [("bank", 6)]

    p.dma(gTq[:], g_attn, w=["gTq"])
    p.dma(gTk[:], g_kv, w=["gTk"])
    p.dma(msk[:], masks.rearrange("m p c -> p m c"), w=["msk"])
    p.dma(zr[64:65, :], sinks_b[:, 0:1024], w=["zr"])
    p.act(sexp[64:65, 0:1024], zr[64:65, :], AF.Exp, r=["zr"], w=["sexp"])
    p.dma(zr[64:65, :], sinks_b[:, 1024:2048], r=["sexp"], w=["zr"])
    p.act(sexp[64:65, 1024:2048], zr[64:65, :], AF.Exp, r=["zr"], w=["sexp"])
    p.memset(ones[:], 1.0, w=["ones"])
    p.memset(VA[:], 1.0, w=["VA"] + [("VA", i) for i in range(mx + 1)])
    p.memset(KT2[:], 0.0, w=["KT2", ("KT2", 0), ("KT2", 1)])
    for kc in range(8):
        stg = f.stage[kc % 2]
        sk = ("stg", "ffn", kc % 2, 0)
        gs = gTk[:, kc:kc + 1]
        p.dma(stg[:, 0:256], w_kv[kc * 128:(kc + 1) * 128, :], w=[sk])
        for g in range(2):
            for dup in range(2):
                p.ts(wk2[:, kc, g, 0, dup * 64:(dup + 1) * 64], stg[:, g * 64:(g + 1) * 64],
                     gs, None, ALU.mult, r=[sk, "gTk"], w=["wk2"], eng="pool")
                p.ts(wk2[:, kc, g, 1, dup * 64:dup * 64 + 32], stg[:, g * 64 + 32:g * 64 + 64],
                     gs, None, ALU.mult, r=[sk, "gTk"], w=["wk2"], eng="pool")
                p.ts(wk2[:, kc, g, 1, dup * 64 + 32:dup * 64 + 64], stg[:, g * 64:g * 64 + 32],
                     gs, None, ALU.mult, r=[sk, "gTk"], w=["wk2"], eng="pool")
        p.ts(wv[:, kc, :], stg[:, 128:256], gs, None, ALU.mult, r=[sk, "gTk"], w=["wv"], eng="pool")

    scr = (f.sq[:], f.ss[:], f.sd[:], f.rs[:])
    t0 = 0
    first_real = n_skip_tiles
    for nt in st_sizes:
        n = nt * 128
        for j in range(nt):
            if fused:
                p.dma(f.h1[:, j, :], io["h_ap"][(t0 + j) * 128:(t0 + j + 1) * 128, :],
                      r=list(io["rkeys"]), w=[("h1", j)])
                if t0 + j < n_skip_tiles:
                    p.ts(f.h1[:, j, :], f.h1[:, j, :], flag_sb[:], None, ALU.mult,
                         r=[("h1", j), "flag"], w=[("h1", j)])
            else:
                p.dma(f.h1[:, j, :], h_in[(t0 + j) * 128:(t0 + j + 1) * 128, :], w=[("h1", j)])
        p.dma(cos_sb[:, 0:n], cos_t[:, t0 * 128:t0 * 128 + n], w=["cos"])
        p.dma(sin_sb[:, 0:n], sin_t[:, t0 * 128:t0 * 128 + n], w=["sin"])
        for j in range(nt):
            rmsnorm_tile(p, f.h1[:, j, :], None, f.hn[:], scr, [("h1", j)], ["hn"], "f")
            for kc in range(8):
                p.tr(f.psT[:, kc, :], f.hn[:, kc * 128:(kc + 1) * 128], f.ident[:],
                     r=["hn", "ident"], w=["psT"])
            p.cp(hnqT[:, :, j * 128:(j + 1) * 128], f.psT[:], r=["psT"], w=[("hnT", j)], eng="act")
            if fused:
                p.dma(hkv[:], io["hkv_ap"][(t0 + j) * 128:(t0 + j + 1) * 128, :],
                      r=list(io["rkeys"]), w=["hkv"])
                if t0 + j < n_skip_tiles:
                    p.ts(hkv[:], hkv[:], flag_sb[:], None, ALU.mult, r=["hkv", "flag"], w=["hkv"])
            else:
                p.dma(hkv[:], hkv_in[(t0 + j) * 128:(t0 + j + 1) * 128, :], w=["hkv"])
            rmsnorm_tile(p, hkv[:], None, f.hn[:], scr, ["hkv"], ["hn"], "f")
            for kc in range(8):
                p.tr(f.psT[:, kc, :], f.hn[:, kc * 128:(kc + 1) * 128], f.ident[:],
                     r=["hn", "ident"], w=["psT"])
            p.cp(hkvT[:, :, j * 128:(j + 1) * 128], f.psT[:], r=["psT"], w=[("hkvT", j)], eng="act")
        hq_keys = [("hnT", j) for j in range(nt)]
        hk_keys = [("hkvT", j) for j in range(nt)]

        def rope_out(dst, psn, pss, rk, wk):
            p.tt(f.t1[:, 0:n], psn, cos_sb[:, 0:n], ALU.mult, r=rk[0:1] + ["cos"], w=["t1"])
            p.tt(f.t2[:, 0:n], pss, sin_sb[:, 0:n], ALU.mult, r=rk[1:2] + ["sin"], w=["t2"])
            p.tt(dst, f.t1[:, 0:n], f.t2[:, 0:n], ALU.add, r=["t1", "t2"], w=wk)

        for g in range(2):
            par = f.nfc % 2
            f.nfc += 1
            bk = [("bank", 2 + 2 * par), ("bank", 3 + 2 * par)]
            for v in range(2):
                for kc in range(8):
                    p.mm(f.psU[par][v][:, 0:n], wk2[:, kc, g, v, :], hkvT[:, kc, 0:n],
                         start=(kc == 0), stop=(kc == 7), r=["wk2"] + hk_keys, w=[bk[v]])
            rope_out(KT2[:, g, 128:128 + n], f.psU[par][0][:, 0:n], f.psU[par][1][:, 0:n],
                     bk, [("KT2", g)])
        for j in range(nt):
            ps = f.psA[f.nA % 2]
            pk = ("bank", f.nA % 2)
            f.nA += 1
            for kc in range(8):
                p.mm(ps[:, 0:128], hkvT[:, kc, j * 128:(j + 1) * 128], wv[:, kc, :],
                     start=(kc == 0), stop=(kc == 7), r=[("hkvT", j), "wv"], w=[pk])
            p.cp(VA[:, j + 1, :, 0:64], ps[:, 0:128].rearrange("p (g d) -> p g d", g=2),
                 r=[pk], w=[("VA", j + 1)], eng="act")
        for hp in range(8):
            par = f.nfc % 2
            f.nfc += 1
            stg = f.fstage[par]
            wch = f.wch[par]
            p.dma(stg[:, 0:1024].rearrange("p (c f) -> p c f", c=8),
                  w_q[:, hp * 128:(hp + 1) * 128].rearrange("(c p) f -> p c f", p=128),
                  w=[("stg", "ffn", par, 0)])
            p.tt(wch[:, :, 0, :], stg[:, 0:1024].rearrange("p (c f) -> p c f", c=8),
                 gTq[:].unsqueeze(2).to_broadcast([128, 8, 128]), ALU.mult,
                 r=[("stg", "ffn", par, 0), "gTq"], w=[("wch", par, 0)], eng="pool")
            src = wch[:, :, 0, :].rearrange("p c (h d) -> p c h d", h=2)
            dsw = wch[:, :, 1, :].rearrange("p c (h d) -> p c h d", h=2)
            p.cp(dsw[:, :, :, 0:32], src[:, :, :, 32:64], r=[("wch", par, 0)],
                 w=[("wch", par, 1)], eng="pool")
            p.cp(dsw[:, :, :, 32:64], src[:, :, :, 0:32], r=[("wch", par, 0)],
                 w=[("wch", par, 1)], eng="pool")
            bk = [("bank", 2 + 2 * par), ("bank", 3 + 2 * par)]
            for v in range(2):
                for kc in range(8):
                    p.mm(f.psU[par][v][:, 0:n], wch[:, kc, v, :], hnqT[:, kc, 0:n],
                         start=(kc == 0), stop=(kc == 7),
                         r=[("wch", par, v)] + hq_keys, w=[bk[v]])
            rope_out(QT2[:, hp, 0:n], f.psU[par][0][:, 0:n], f.psU[par][1][:, 0:n],
                     bk, [("QT2", hp)])
        nS = 0
        for j in range(nt):
            gt = t0 + j
            for g in range(2):
                chunks = [(j, 0 if gt == first_real else 1), (j + 1, 2)]
                for ci, (slot, mi) in enumerate(chunks):
                    sp_ = nS % 2
                    nS += 1
                    for par in range(2):
                        pr = slice(par * 64, (par + 1) * 64)
                        p.mm(psS[sp_][:, par * 512:(par + 1) * 512],
                             KT2[pr, g, slot * 128:(slot + 1) * 128],
                             QT2[pr, 4 * g:4 * g + 4, j * 128:(j + 1) * 128],
                             start=True, stop=False,
                             r=[("KT2", g)] + [("QT2", 4 * g + i) for i in range(4)],
                             w=[psSk[sp_][par]])
                        p.mm(psS[sp_][:, par * 512:(par + 1) * 512], f.ident[:], msk[:, mi, :],
                             start=False, stop=True, r=["ident", "msk"], w=[psSk[sp_][par]])
                    p.act(PT[sp_][:], psS[sp_], AF.Exp, r=psSk[sp_], w=[("PT", sp_)], scale=0.125)
                    for par in range(2):
                        p.mm(psO[0:65, par * 512:(par + 1) * 512], VA[:, slot, g, :],
                             PT[sp_][:, par * 512:(par + 1) * 512],
                             start=(ci == 0), stop=(ci == 1),
                             r=[("PT", sp_), ("VA", slot), "VA"], w=[psOk[par]])
                p.tt(zr[64:65, :], psO[64:65, :], sexp[64:65, g * 1024:(g + 1) * 1024], ALU.add,
                     r=psOk + ["sexp"], w=["zr"])
                p.recip(rz[64:65, :], zr[64:65, :], r=["zr"], w=["rz"])
                p.cp(osb, psO[0:64, :], r=psOk, w=["hfin"], eng="act")
                for par in range(2):
                    p.mm(psB[0:64, :], ones[64:65, :], rz[64:65, par * 512:(par + 1) * 512],
                         start=True, stop=True, r=["ones", "rz"], w=psBk)
                    dst = f.oT[:, g * 8 + par * 4:g * 8 + par * 4 + 4, j * 128:(j + 1) * 128]
                    p.tt(dst, osb[:, par * 512:(par + 1) * 512].rearrange("p (h q) -> p h q", h=4),
                         psB[0:64, :].rearrange("p (h q) -> p h q", h=4), ALU.mult,
                         r=["hfin"] + psBk, w=["oT"])
        for g in range(2):
            p.cp(KT2[:, g, 0:128], KT2[:, g, n:n + 128], r=[("KT2", g)], w=[("KT2", g)], eng="pool")
        p.cp(VA[:, 0, :, :], VA[:, nt, :, :], r=[("VA", nt)], w=[("VA", 0)], eng="pool")
        skip = max(0, min(nt, n_skip_tiles - t0))
        o0 = max(0, t0 - n_skip_tiles)
        dst = h_out[o0 * 128:(o0 + nt - skip) * 128, :] if skip < nt else None
        f.run_supertile(nt, None, "resident", dst, n_skip_out=skip, final_norm=final_norm,
                        h1_preloaded=True, dkey=(io["dkey"] if fused else None))
        t0 += nt
    if fused:
        return None
    return p.finish()


def rope_tables(pos):
    half = 32
    inv = (np.float32(10000.0) ** (-np.arange(half, dtype=np.float32) / half)).astype(np.float32)
    ang = pos.astype(np.float32)[None, :] * inv[:, None]
    cos = np.cos(ang).astype(np.float32)
    sin = np.sin(ang).astype(np.float32)
    cos64 = np.concatenate([cos, cos], 0)
    sin64 = np.concatenate([-sin, sin], 0)
    return (np.ascontiguousarray(np.concatenate([cos64, cos64], 0)),
            np.ascontiguousarray(np.concatenate([sin64, sin64], 0)))


def swa_masks(first_exists):
    i = np.arange(128)[:, None]
    q = np.arange(128)[None, :]
    prev = np.where(i > q, 0.0, MASKV).astype(np.float32)
    cur = np.where(i <= q, 0.0, MASKV).astype(np.float32)
    pf = prev if first_exists else np.full((128, 128), MASKV, np.float32)
    m = np.stack([np.tile(pf, (1, 4)), np.tile(prev, (1, 4)), np.tile(cur, (1, 4))], 0)
    return m.astype(NPBF16)


def sinks_row(sinks16):
    ho = c_head_order()
    return np.ascontiguousarray(
        np.repeat(np.asarray(sinks16, np.float32)[ho], 128)[None, :])


def gT_np(g):
    return np.ascontiguousarray(np.asarray(g, np.float32).reshape(8, 128).T)


FORCE = 1.0e6
TINY = 1.0e-30


def build_A(S, dbg=99, p=None, io=None):
    fused = p is not None
    if not fused:
        nc = bass.Bass("TRN2", target_bir_lowering=False)
        p = Prog(nc)
    nc = p.nc
    NST = S // 512
    NQB = S // 128
    NCC = max(1, S // 2048)
    if not fused:
        h_in = p.din("h_in", [S, D], F32)
    g_attn = p.din("g_attn", [128, 8], F32)
    wq_d = p.din("wq", [D, 256], F32)
    wk3_d = p.din("wk3", [D, 192], F32)
    wv3_d = p.din("wv3", [D, 192], F32)
    wg_d = p.din("wg", [D, 12], F32)
    w1_d = p.din("w1", [2, 2048, 256], F32)
    w2_d = p.din("w2", [2, 256, 64], F32)
    posT_d = p.din("posT", [64, 2, 32], F32)
    cos_d = p.din("cos_t", [128, S], F32)
    sin_d = p.din("sin_t", [128, S], F32)
    ccos_d = p.din("ccos_t", [128, NCC * 128], F32)
    csin_d = p.din("csin_t", [128, NCC * 128], F32)
    pmask_d = p.din("pmask", [2, 16, 128, 128], BF16)
    r0mask_d = p.din("r0mask", [128, 256], BF16)
    cmask_d = p.din("cmask", [2, 128, 256], BF16)
    emat_d = p.din("emat", [64, 128, 128], BF16)
    wfull_d = p.din("wfull", [NCC * 128, 257], BF16)
    fix_d = p.din("fix3", [128, 6], F32)
    ident_d = p.din("ident", [128, 128], BF16)
    gscr = [nc.dram_tensor(f"gscr{i}" + p.sfx, [1, 12 * 512], F32).ap() for i in range(2)]
    if not fused:
        oT_out = p.dout("oT_out", [64, 4, S], BF16)

    ident = p.sb("ident", [128, 128], BF16)
    gT = p.sb("gT", [128, 8], F32)
    fst = [p.sb(f"fst{i}", [128, 1024], F32) for i in range(2)]
    WQ = p.sb("WQ", [128, 8, 2, 256], BF16)
    WKS = p.sb("WKS", [128, 8, 2, 128], BF16)
    WKW = p.sb("WKW", [128, 8, 2, 128], BF16)
    WKC = p.sb("WKC", [128, 8, 64], BF16)
    WVC = p.sb("WVC", [128, 8, 64], BF16)
    WV2 = p.sb("WV2", [128, 8, 128], BF16)
    WG = p.sb("WG", [128, 8, 12], BF16)
    W1c = [p.sb(f"W1c{i}", [64, 4, 256], BF16) for i in range(2)]
    W2K = p.sb("W2K", [128, 2, 2, 128], BF16)
    W2V = p.sb("W2V", [128, 2, 64], BF16)
    posT = p.sb("posT", [64, 2, 32], BF16)
    c1 = p.sb("c1", [128, 4], F32)
    ccos = p.sb("ccos", [128, NCC * 128], F32)
    csin = p.sb("csin", [128, NCC * 128], F32)
    pmask = p.sb("pmask", [128, 2, 16, 128], BF16)
    r0mask = p.sb("r0mask", [128, 256], BF16)
    cmask = p.sb("cmask", [128, 2, 256], BF16)
    emat = p.sb("emat", [128, 64, 128], BF16)
    wfull = p.sb("wfull", [128, NCC, 257], BF16)
    fix3 = p.sb("fix3", [128, 6], F32)
    hbuf = [p.sb("hbuf0", [128, 1024], F32)] * 2
    sq = p.sb("sq", [128, 1024], BF16)
    ss = p.sb("ss", [128, 1], F32)
    sd = p.sb("sd", [128, 1], F32)
    rs = p.sb("rs", [128, 1], F32)
    hn = p.sb("hn", [128, 1024], BF16)
    hnT = p.sb("hnT", [128, 8, 512], BF16)
    cos_sb = p.sb("cos_sb", [128, 512], F32)
    sin_sb = p.sb("sin_sb", [128, 512], F32)
    t1 = p.sb("t1", [128, 512], F32)
    t2 = p.sb("t2", [128, 512], F32)
    QT2 = p.sb("QT2", [128, 2, 512], BF16)
    KsT2 = p.sb("KsT2", [128, S], BF16)
    VsA = p.sb("VsA", [128, NQB, 65], BF16)
    KwT2 = p.sb("KwT2", [128, 1024], BF16)
    VwA = p.sb("VwA", [128, 8, 65], BF16)
    KcT2 = p.sb("KcT2", [128, NCC * 128], BF16)
    VcA = p.sb("VcA", [128, NCC, 65], BF16)
    xT = [p.sb(f"xT{i}", [64, 528], BF16) for i in range(2)]
    hidK = p.sb("hidK", [128, 2, 32], BF16)
    hidV = p.sb("hidV", [128, 2, 128], BF16)
    gx = [p.sb(f"gx{i}", [128, 32], F32) for i in range(3)]
    gsb = p.sb("gsb", [12, 512], F32)
    G64b = [p.sb(f"G64b{i}", [128, 12 * 128], F32) for i in range(2)]
    PT = [p.sb(f"PT{i}", [128, 512], BF16) for i in range(2)]
    PcT = p.sb("PcT", [128, NCC, 512], BF16)
    zr = p.sb("zr", [128, 512], F32)
    Rr = p.sb("Rr", [128, 512], F32)
    ones = p.sb("ones", [128, 64], F32)
    osb = p.sb("osb", [64, 512], F32)
    acc = p.sb("acc", [64, 512], F32)
    tmpo = p.sb("tmpo", [64, 512], F32)
    oacc = p.sb("oacc", [64, 4, 128], BF16)
    imp = p.sb("imp", [128, 256], F32)
    selbuf = p.sb("selbuf", [128, 256], F32)
    work = p.sb("work", [128, 256], F32)
    mx8 = p.sb("mx8", [128, 8], F32)
    thr = p.sb("thr", [128, 1], F32)
    zq = p.sb("zq", [128, 1], F32)
    Bq = p.sb("Bq", [128, 256], BF16)
    BT = p.sb("BT", [128, 2, 256], BF16)
    psum = p.ps("psum", [128, 7 * 512])
    psT = p.ps("psT", [128, 8, 128], BF16)
    zero_b = p.sb("zero_b", [128, 256], BF16)
    p.memset(zero_b[:], 0.0, w=["zero_b"])

    def bank(i, n=1):
        return psum[:, i * 512:(i + n) * 512]

    def bk(i):
        return ("bank", i)

    p.dma(ident[:], ident_d, w=["ident"])
    p.dma(gT[:], g_attn, w=["gT"])
    p.dma(ccos[:], ccos_d, w=["ccos"])
    p.dma(csin[:], csin_d, w=["csin"])
    for a_ in range(2):
        for r4 in range(0, 16, 4):
            p.dma(pmask[:, a_, r4:r4 + 4, :], pmask_d[a_, r4:r4 + 4].rearrange("r p c -> p r c"),
                  w=["pmask"])
    p.dma(r0mask[:], r0mask_d, w=["r0mask"])
    p.dma(cmask[:], cmask_d.rearrange("a p c -> p a c"), w=["cmask"])
    for e8 in range(0, 64, 8):
        p.dma(emat[:, e8:e8 + 8, :], emat_d[e8:e8 + 8].rearrange("e p c -> p e c"), w=["emat"])
    p.dma(wfull[:], wfull_d.rearrange("(c p) f -> p c f", p=128), w=["wfull"])
    p.dma(fix3[:], fix_d, w=["fix3"])
    p.memset(ones[:], 1.0, w=["ones"])
    p.memset(VsA[:], 1.0, w=["VsA"])
    p.memset(VwA[:], 1.0, w=["VwA"])
    p.memset(VcA[:], 1.0, w=["VcA"])
    p.memset(KwT2[:], 0.0, w=["KwT2"])
    p.memset(KcT2[:], 0.0, w=["KcT2"])
    p.memset(selbuf[:], -FORCE, w=["selbuf"])
    p.memset(hidV[:], 0.0, w=["hidV"])
    for i in range(2):
        p.memset(xT[i][:], 0.0, w=[("xT", i)])
    nst_ = [0]

    def stage_load(dst_fn, src_ap, ncols, parts=128):
        i = nst_[0] % 2
        nst_[0] += 1
        k = ("fst", i)
        p.dma(fst[i][0:parts, 0:ncols], src_ap, w=[k])
        return fst[i], k

    def swapcopy(dst, src, r, w):
        d4 = dst.rearrange("p (h d) -> p h d", d=64)
        s4 = src.rearrange("p (h d) -> p h d", d=64)
        p.cp(d4[:, :, 0:32], s4[:, :, 32:64], r=r, w=w, eng="pool")
        p.cp(d4[:, :, 32:64], s4[:, :, 0:32], r=r, w=w, eng="pool")

    for kc in range(8):
        gs = gT[:, kc:kc + 1]
        rows = slice(kc * 128, (kc + 1) * 128)
        st_, k = stage_load(None, wq_d[rows, :], 256)
        p.ts(WQ[:, kc, 0, :], st_[:, 0:256], gs, None, ALU.mult, r=[k, "gT"], w=["WQ"], eng="pool")
        swapcopy(WQ[:, kc, 1, :], WQ[:, kc, 0, :], ["WQ"], ["WQ"])
        st_, k = stage_load(None, wk3_d[rows, :], 192)
        p.ts(WKC[:, kc, :], st_[:, 0:64], gs, None, ALU.mult, r=[k, "gT"], w=["WKC"], eng="pool")
        for (W_, c0) in ((WKS, 64), (WKW, 128)):
            for dup in range(2):
                p.ts(W_[:, kc, 0, dup * 64:(dup + 1) * 64], st_[:, c0:c0 + 64], gs, None, ALU.mult,
                     r=[k, "gT"], w=["WK"], eng="pool")
            swapcopy(W_[:, kc, 1, :], W_[:, kc, 0, :], ["WK"], ["WK"])
        st_, k = stage_load(None, wv3_d[rows, :], 192)
        p.ts(WVC[:, kc, :], st_[:, 0:64], gs, None, ALU.mult, r=[k, "gT"], w=["WVC"], eng="pool")
        p.ts(WV2[:, kc, :], st_[:, 64:192], gs, None, ALU.mult, r=[k, "gT"], w=["WV2"], eng="pool")
        st_, k = stage_load(None, wg_d[rows, :], 12)
        p.ts(WG[:, kc, :], st_[:, 0:12], gs, None, ALU.mult, r=[k, "gT"], w=["WG"], eng="pool")
    nW1 = [0]

    def w1_piece(kv, l0):
        i = nW1[0] % 2
        nW1[0] += 1
        k = ("fst", i)
        p.dma(fst[i][0:64, :].rearrange("p (l m) -> p l m", l=4),
              w1_d[kv, l0 * 64:(l0 + 4) * 64, :].rearrange("(l d) m -> d l m", d=64), w=[k])
        p.cp(W1c[i][:], fst[i][0:64, :].rearrange("p (l m) -> p l m", l=4),
             r=[k], w=[("W1c", i)], eng="pool")
        return W1c[i], ("W1c", i)

    for kv in range(2):
        for mt in range(2):
            st_, k = stage_load(None, w2_d[kv, mt * 128:(mt + 1) * 128, :], 64)
            if kv == 0:
                for dup in range(2):
                    p.cp(W2K[:, mt, 0, dup * 64:(dup + 1) * 64], st_[:, 0:64], r=[k], w=["W2K"],
                         eng="pool")
                swapcopy(W2K[:, mt, 1, :], W2K[:, mt, 0, :], ["W2K"], ["W2K"])
            else:
                p.cp(W2V[:, mt, :], st_[:, 0:64], r=[k], w=["W2V"], eng="pool")
    st_, k = stage_load(None, posT_d.rearrange("d a l -> d (a l)"), 64, parts=64)
    p.cp(posT[:].rearrange("d a l -> d (a l)"), st_[0:64, 0:64], r=[k], w=["posT"], eng="pool")
    for kv in range(2):
        for l0 in range(0, 32, 4):
            wt, wk_ = w1_piece(kv, l0)
            for mt in range(2):
                col = kv * 2 + mt
                bb = 1 if mt == 0 else 6
                for li in range(4):
                    l = l0 + li
                    p.mm(bank(bb)[:, col:col + 1], wt[:, li, mt * 128:(mt + 1) * 128],
                         posT[:, kv, l:l + 1], start=(l == 0), stop=(l == 31),
                         r=[wk_, "posT"], w=[bk(bb)])
    p.cp(c1[:, 0:1], bank(1)[:, 0:1], r=[bk(1)], w=["c1"], eng="act")
    p.cp(c1[:, 2:3], bank(1)[:, 2:3], r=[bk(1)], w=["c1"], eng="act")
    p.cp(c1[:, 1:2], bank(6)[:, 1:2], r=[bk(6)], w=["c1"], eng="act")
    p.cp(c1[:, 3:4], bank(6)[:, 3:4], r=[bk(6)], w=["c1"], eng="act")

    if dbg == 0:
        return p.finish()
    if fused:
        for cb in range(4):
            p.dma(io["o_zero"][:, cb, :], zero_b[0:64, 0:128], r=["zero_b"], w=[io["dkey"]],
                  q="pool")
    scr = (sq[:], ss[:], sd[:], rs[:])

    def rope_out(dst, psn, pss, cs, sn, rk, wk, n):
        p.tt(t1[:, 0:n], psn, cs, ALU.mult, r=rk[0:1] + ["cos", "ccos"], w=["t1"])
        p.tt(t2[:, 0:n], pss, sn, ALU.mult, r=rk[1:2] + ["sin", "csin"], w=["t2"])
        p.tt(dst, t1[:, 0:n], t2[:, 0:n], ALU.add, r=["t1", "t2"], w=wk)

    nU = [0]
    nH = [0]

    def proj_pair(W_, dst, cs, sn, wkey, rkey):
        par = nU[0] % 2
        nU[0] += 1
        b0, b1 = 2 + 2 * par, 3 + 2 * par
        for v, b in ((0, b0), (1, b1)):
            for kc in range(8):
                p.mm(bank(b), W_(kc, v), hnT[:, kc, :], start=(kc == 0), stop=(kc == 7),
                     r=[rkey, "hnT"], w=[bk(b)])
        rope_out(dst, bank(b0), bank(b1), cs, sn, [bk(b0), bk(b1)], wkey, 512)

    def gelu_to(dst, ps_ap, bias_ap, rk, wk):
        x, a, b = gx[0][:], gx[1][:], gx[2][:]
        p.act(x, ps_ap, AF.Identity, r=rk + ["c1"], w=["gx0"], bias=bias_ap)
        p.tt(a, x, x, ALU.mult, r=["gx0"], w=["gx1"])
        p.ts(a, a, 0.044715, 1.0, ALU.mult, ALU.add, r=["gx1"], w=["gx1"])
        p.tt(a, a, x, ALU.mult, r=["gx1", "gx0"], w=["gx1"])
        p.act(b, a, AF.Sigmoid, r=["gx1"], w=["gx2"], scale=1.5957691216057308)
        p.tt(dst, x, b, ALU.mult, r=["gx0", "gx2"], w=wk)

    nS = [0]

    def attn_chunk(kT2, kcols, vaug, biases, first, last, n_extra_r):
        sp_ = nS[0] % 2
        nS[0] += 1
        sb_ = 2 + sp_
        if ZERO_BIAS and len(biases) == 0:
            biases = [(ident[:], zero_b[:], ["ident", "zero_b"])]
        for par in range(2):
            pr = slice(par * 64, (par + 1) * 64)
            out = bank(sb_)[:, par * 256:(par + 1) * 256]
            p.mm(out, kT2[pr, kcols], QT2[pr, :, qsl[0]], start=True, stop=(len(biases) == 0),
                 r=n_extra_r + ["QT2"], w=[bk(sb_)])
            for bi, bias in enumerate(biases):
                lh, rh, rk = bias[0:3]
                if len(bias) == 4:
                    for hh in range(2):
                        p.mm(out[:, hh * 128:(hh + 1) * 128], lh, rh, start=False,
                             stop=(bi == len(biases) - 1), r=rk, w=[bk(sb_)])
                else:
                    p.mm(out, lh, rh, start=False, stop=(bi == len(biases) - 1), r=rk, w=[bk(sb_)])
        return sp_, sb_

    qsl = [None]
    for st in range(NST):
        tok0 = st * 512
        for j in range(4):
            hb = hbuf[j % 2]
            hk = ("hbuf", 0)
            if fused:
                r0 = io["h_row"](tok0 + j * 128)
                p.dma(hb[:], io["h_ap"][r0:r0 + 128, :], r=list(io["rkeys"]), w=[hk])
            else:
                p.dma(hb[:], h_in[tok0 + j * 128:tok0 + (j + 1) * 128, :], w=[hk])
            rmsnorm_tile(p, hb[:], None, hn[:], scr, [hk], ["hn"], "a")
            for kc in range(8):
                p.tr(psT[:, kc, :], hn[:, kc * 128:(kc + 1) * 128], ident[:],
                     r=["hn", "ident"], w=["psT"])
            p.cp(hnT[:, :, j * 128:(j + 1) * 128], psT[:], r=["psT"], w=["hnT"], eng="act")
        p.dma(cos_sb[:], cos_d[:, tok0:tok0 + 512], w=["cos"])
        p.dma(sin_sb[:], sin_d[:, tok0:tok0 + 512], w=["sin"])
        for hp in range(2):
            proj_pair(lambda kc, v, hp=hp: WQ[:, kc, v, hp * 128:(hp + 1) * 128],
                      QT2[:, hp, :], cos_sb[:], sin_sb[:], ["QT2"], "WQ")
        proj_pair(lambda kc, v: WKS[:, kc, v, :], KsT2[:, tok0:tok0 + 512], cos_sb[:], sin_sb[:],
                  ["KsT2"], "WK")
        proj_pair(lambda kc, v: WKW[:, kc, v, :], KwT2[:, 512:1024], cos_sb[:], sin_sb[:],
                  ["KwT2"], "WK")
        for i, W_ in enumerate((WKC, WVC)):
            for kc in range(8):
                p.mm(bank(1)[0:64, :], W_[:, kc, :], hnT[:, kc, :], start=(kc == 0), stop=(kc == 7),
                     r=["WKC", "WVC", "hnT"], w=[bk(1)])
            p.cp(xT[i][:, 16:528], bank(1)[0:64, :], r=[bk(1)], w=[("xT", i)], eng="act")
        for j in range(4):
            for kc in range(8):
                p.mm(bank(0)[:, 0:128], hnT[:, kc, j * 128:(j + 1) * 128], WV2[:, kc, :],
                     start=(kc == 0), stop=(kc == 7), r=["hnT", "WV2"], w=[bk(0)])
            p.cp(VsA[:, st * 4 + j, 0:64], bank(0)[:, 0:64], r=[bk(0), "VsA"], w=["VsA"], eng="act")
            p.cp(VwA[:, 4 + j, 0:64], bank(0)[:, 64:128], r=[bk(0), "VwA"], w=["VwA"], eng="act")
        for kc in range(8):
            p.mm(bank(1)[0:12, :], WG[:, kc, :], hnT[:, kc, :], start=(kc == 0), stop=(kc == 7),
                 r=["WG", "hnT"], w=[bk(1)])
        p.act(gsb[:], bank(1)[0:12, :], AF.Sigmoid, r=[bk(1)], w=["gsb"])
        p.dma(gscr[st % 2].rearrange("o (a b) -> (o a) b", a=12), gsb[:], r=["gsb"],
              w=[("gscr", st % 2)])
        if dbg == 1:
            return p.finish()
        for kv in range(2):
            x3 = xT[kv][:].rearrange("p (i s) -> p i s", s=16)
            for l0 in range(0, 32, 4):
                wt, wk_ = w1_piece(kv, l0)
                for mt in range(2):
                    bb = 1 if mt == 0 else 6
                    for li in range(4):
                        l = l0 + li
                        rhs = x3[:, 0:32, l] if l < 16 else x3[:, 1:33, l - 16]
                        p.mm(bank(bb)[:, 0:32], wt[:, li, mt * 128:(mt + 1) * 128], rhs,
                             start=(l == 0), stop=(l == 31), r=[wk_, ("xT", kv)], w=[bk(bb)])
            for mt in range(2):
                bb = 1 if mt == 0 else 6
                if kv == 0:
                    gelu_to(hidK[:, mt, :], bank(bb)[:, 0:32], c1[:, mt:mt + 1], [bk(bb)], ["hidK"])
                else:
                    if st % 4 == 0 and mt == 0:
                        p.memset(hidV[:], 0.0, w=["hidV"])
                    gelu_to(hidV[:, mt, (st % 4) * 32:(st % 4) * 32 + 32], bank(bb)[:, 0:32],
                            c1[:, 2 + mt:3 + mt], [bk(bb)], ["hidV"])
            if kv == 0:
                par = nU[0] % 2
                nU[0] += 1
                b0, b1 = 2 + 2 * par, 3 + 2 * par
                for v, b in ((0, b0), (1, b1)):
                    for mt in range(2):
                        p.mm(bank(b)[:, 0:32], W2K[:, mt, v, :], hidK[:, mt, :],
                             start=(mt == 0), stop=(mt == 1), r=["W2K", "hidK"], w=[bk(b)])
                sl = slice(st * 32, st * 32 + 32)
                rope_out(KcT2[:, sl], bank(b0)[:, 0:32], bank(b1)[:, 0:32], ccos[:, sl], csin[:, sl],
                         [bk(b0), bk(b1)], ["KcT2"], 32)
            else:
                for mt in range(2):
                    p.mm(bank(1)[:, 0:64], hidV[:, mt, :], W2V[:, mt, :],
                         start=(mt == 0), stop=(mt == 1), r=["W2V", "hidV"], w=[bk(1)])
                p.cp(VcA[:, st // 4, 0:64], bank(1)[:, 0:64], r=[bk(1), "VcA"], w=["VcA"], eng="act")
            p.cp(xT[kv][:, 0:16], xT[kv][:, 512:528], r=[("xT", kv)], w=[("xT", kv)], eng="pool")
        if dbg == 2:
            return p.finish()
        for j in range(4):
            qb = st * 4 + j
            qsl[0] = slice(j * 128, (j + 1) * 128)
            tsl = qsl[0]
            p.dma(G64b[j % 2][64:65, :].rearrange("p (a b) -> p a b", a=12),
                  gscr[st % 2].rearrange("o (a b) -> o a b", a=12)[:, :, tsl],
                  r=[("gscr", st % 2)], w=[("G64", j % 2)])

            def finish_branch(br, first):
                p.ts(zr[64:65, :], bank(0)[64:65, :], TINY, None, ALU.max, r=[bk(0)], w=["zr"])
                p.recip(zr[64:65, :], zr[64:65, :], r=["zr"], w=["zr"])
                g3 = G64b[j % 2][64:65, :].rearrange("p (h b t) -> p h b t", h=4, b=3)
                for par in range(2):
                    for hpl in range(2):
                        hl = 2 * hpl + par
                        c0 = (par * 2 + hpl) * 128
                        p.tt(Rr[64:65, c0:c0 + 128], zr[64:65, c0:c0 + 128], g3[:, hl, br, :],
                             ALU.mult, r=["zr", ("G64", j % 2)], w=["Rr"])
                p.cp(osb[:], bank(0)[0:64, :], r=[bk(0)], w=["osb"], eng="act")
                p.mm(bank(1)[0:64, :], ones[64:65, :], Rr[64:65, :], start=True, stop=True,
                     r=["ones", "Rr"], w=[bk(1)])
                if first:
                    p.tt(acc[:], osb[:], bank(1)[0:64, :], ALU.mult, r=["osb", bk(1)], w=["acc"])
                else:
                    p.tt(tmpo[:], osb[:], bank(1)[0:64, :], ALU.mult, r=["osb", bk(1)], w=["tmpo"])
                    p.tt(acc[:], acc[:], tmpo[:], ALU.add, r=["tmpo", "acc"], w=["acc"])

            pend = [None]
            ncc = qb // 16 + 1
            r_ = qb % 16
            for cc in range(ncc):
                biases = []
                lastc = (cc == ncc - 1)
                if lastc:
                    biases.append((ident[:], pmask[:, 1 if cc == 0 else 0, r_, :], ["ident", "pmask"], 128))
                elif cc == 0:
                    biases.append((ident[:], r0mask[:], ["ident", "r0mask"]))
                sp_, sb_ = attn_chunk(KcT2, slice(cc * 128, (cc + 1) * 128), None, biases,
                                      cc == 0, lastc, ["KcT2"])
                p.act(PcT[:, cc, :], bank(sb_), AF.Exp, r=[bk(sb_)], w=[("PcT", cc)], scale=0.125)
                if pend[0] is not None:
                    pend[0]()
                pend[0] = (lambda cc=cc, lastc=lastc: p.mm(
                    bank(0)[0:65, :], VcA[:, cc, :], PcT[:, cc, :], start=(cc == 0), stop=lastc,
                    r=[("PcT", cc), "VcA"], w=[bk(0)]))
            pend[0]()
            pend[0] = None
            for par in range(2):
                for hpl in range(2):
                    hi = par * 2 + hpl
                    c0 = hi * 128
                    ib = 4 + (hi % 2)
                    for cc in range(ncc):
                        p.mm(bank(ib)[:, 0:257], PcT[:, cc, c0:c0 + 128], wfull[:, cc, :],
                             start=(cc == 0), stop=(cc == ncc - 1),
                             r=[("PcT", cc), "wfull"], w=[bk(ib)])
                    p.ts(zq[:], bank(ib)[:, 256:257], TINY, None, ALU.max, r=[bk(ib)], w=["zq"])
                    p.recip(zq[:], zq[:], r=["zq"], w=["zq"])
                    if hi == 0:
                        p.ts(imp[:], bank(ib)[:, 0:256], zq[:], None, ALU.mult, r=[bk(ib), "zq"],
                             w=["imp"])
                    else:
                        p.stt(imp[:], bank(ib)[:, 0:256], zq[:], imp[:], ALU.mult, ALU.add,
                              r=[bk(ib), "zq", "imp"], w=["imp"])
            finish_branch(0, True)
            if dbg == 3 or dbg == 100 + j * 10 + 3:
                return p.finish()
            nb = 2 * qb + 2
            p.cp(selbuf[:, 0:nb], imp[:, 0:nb], r=["imp"], w=["selbuf"])
            lo = 2 * qb - 1
            k0 = 0
            if lo < 0:
                lo, k0 = 0, 1
            nfx = 3 - k0
            p.tt(selbuf[:, lo:lo + nfx], selbuf[:, lo:lo + nfx], fix3[:, k0:3], ALU.mult,
                 r=["selbuf", "fix3"], w=["selbuf"])
            p.tt(selbuf[:, lo:lo + nfx], selbuf[:, lo:lo + nfx], fix3[:, 3 + k0:6], ALU.add,
                 r=["selbuf", "fix3"], w=["selbuf"])
            p.memset(selbuf[:, 0:1], 3.0 * FORCE, w=["selbuf"])
            p.s.op("dve", lambda e: e.max(out=mx8[:], in_=selbuf[:]), ["selbuf"], ["mx8"])
            p.s.op("dve", lambda e: e.match_replace(out=work[:], in_to_replace=mx8[:],
                                                    in_values=selbuf[:], imm_value=-2.0 * FORCE),
                   ["selbuf", "mx8"], ["work"])
            p.s.op("dve", lambda e: e.max(out=mx8[:], in_=work[:]), ["work"], ["mx8"])
            p.s.op("dve", lambda e: e.tensor_reduce(out=thr[:], in_=mx8[:], axis=AX.X, op=ALU.min),
                   ["mx8"], ["thr"])
            p.ts(Bq[:], selbuf[:], thr[:], 1.0, ALU.is_ge, ALU.subtract, r=["selbuf", "thr"], w=["Bq"])
            nhalf = 1 if nb <= 128 else 2
            for hf in range(nhalf):
                p.tr(psT[:, hf, :], Bq[:, hf * 128:(hf + 1) * 128], ident[:], r=["Bq", "ident"],
                     w=["psT"])
            for hf in range(nhalf):
                for rep in range(2):
                    p.cp(BT[:, hf, rep * 128:(rep + 1) * 128], psT[:, hf, :], r=["psT"], w=["BT"],
                         eng="act")
            if dbg == 4 or dbg == 100 + j * 10 + 4:
                return p.finish()
            for kc in range(qb + 1):
                biases = [(emat[:, kc % 64, :], BT[:, kc // 64, :], ["emat", "BT"])]
                if kc == qb:
                    biases.append((ident[:], cmask[:, 0, :], ["ident", "cmask"]))
                sp_, sb_ = attn_chunk(KsT2, slice(kc * 128, (kc + 1) * 128), None, biases,
                                      kc == 0, kc == qb, ["KsT2"])
                p.act(PT[sp_][:], bank(sb_), AF.Exp, r=[bk(sb_)], w=[("PT", sp_)], scale=0.125)
                if pend[0] is not None:
                    pend[0]()
                pend[0] = (lambda kc=kc, sp_=sp_: p.mm(
                    bank(0)[0:65, :], VsA[:, kc, :], PT[sp_][:], start=(kc == 0), stop=(kc == qb),
                    r=[("PT", sp_), "VsA"], w=[bk(0)]))
            pend[0]()
            pend[0] = None
            finish_branch(1, False)
            if dbg == 5 or dbg == 100 + j * 10 + 5:
                return p.finish()
            k_lo = max(0, qb - 4)
            for kc in range(k_lo, qb + 1):
                biases = []
                if kc == qb - 4:
                    biases.append((ident[:], cmask[:, 1, :], ["ident", "cmask"]))
                if kc == qb:
                    biases.append((ident[:], cmask[:, 0, :], ["ident", "cmask"]))
                slot = 4 + j - (qb - kc)
                sp_, sb_ = attn_chunk(KwT2, slice(slot * 128, (slot + 1) * 128), None, biases,
                                      kc == k_lo, kc == qb, ["KwT2"])
                p.act(PT[sp_][:], bank(sb_), AF.Exp, r=[bk(sb_)], w=[("PT", sp_)], scale=0.125)
                if pend[0] is not None:
                    pend[0]()
                pend[0] = (lambda kc=kc, sp_=sp_, slot=slot: p.mm(
                    bank(0)[0:65, :], VwA[:, slot, :], PT[sp_][:], start=(kc == k_lo),
                    stop=(kc == qb), r=[("PT", sp_), "VwA"], w=[bk(0)]))
            pend[0]()
            pend[0] = None
            finish_branch(2, False)
            if dbg == 6 or dbg == 100 + j * 10 + 6:
                return p.finish()
            p.cp(oacc[:], acc[:].rearrange("p (c q) -> p c q", c=4), r=["acc"], w=["oacc"])
            if fused:
                for dst in io["o_dst"](qb):
                    p.dma(dst, oacc[:], r=["oacc"], w=[io["dkey"]], q="pool")
            else:
                p.dma(oT_out[:, :, tok0 + j * 128:tok0 + (j + 1) * 128], oacc[:], r=["oacc"],
                      q="pool", is_output=True)
            if dbg == 100 + j * 10 + 7:
                return p.finish()
        p.cp(KwT2[:, 0:512], KwT2[:, 512:1024], r=["KwT2"], w=["KwT2"], eng="pool")
        p.cp(VwA[:, 0:4, :], VwA[:, 4:8, :], r=["VwA"], w=["VwA"], eng="pool")
        if dbg == 7 + st:
            return p.finish()
    if fused:
        return None
    return p.finish()


def nsa_consts(S):
    NCC = max(1, S // 2048)
    ml = np.arange(128)[:, None]
    q = np.arange(128)[None, :]
    pm = np.zeros((2, 16, 128, 128), np.float32)
    for a in range(2):
        for r in range(16):
            valid = (16 * ml + 15 <= 128 * r + q)
            if a == 1:
                valid = valid & (ml >= 1)
            pm[a, r] = np.where(valid, 0.0, MASKV)
    pmask = pm.astype(NPBF16)
    r0 = np.zeros((128, 256), np.float32)
    r0[0, :] = MASKV
    cur = np.where(ml <= q, 0.0, MASKV).astype(np.float32)
    upper = np.where(ml > q, 0.0, MASKV).astype(np.float32)
    cmask = np.stack([np.tile(cur, (1, 2)), np.tile(upper, (1, 2))], 0).astype(NPBF16)
    emat = np.zeros((64, 128, 128), np.float32)
    for e in range(64):
        emat[e, 2 * e, 0:64] = -MASKV
        emat[e, 2 * e + 1, 64:128] = -MASKV
    ws = [1, 2, 2, 2, 1]
    wfull = np.zeros((NCC * 128, 257), np.float32)
    for m in range(1, NCC * 128):
        n = m - 1
        for j in range(256):
            i = n - 4 * j + 1
            if 0 <= i <= 4:
                wfull[m, j] = ws[i]
    wfull[:, 256] = 1.0
    fix = np.zeros((128, 6), np.float32)
    lo = np.arange(128) < 64
    fix[:, 0] = np.where(lo, 0.0, 1.0)
    fix[:, 3] = np.where(lo, FORCE, 0.0)
    fix[:, 4] = 2.0 * FORCE
    fix[:, 5] = np.where(lo, -FORCE, FORCE)
    cpos = 16 * np.arange(NCC * 128) + 15
    ccos, csin = rope_tables(cpos)
    return dict(pmask=pmask, r0mask=r0.astype(NPBF16), cmask=cmask, emat=emat.astype(NPBF16),
                wfull=wfull.astype(NPBF16), fix3=fix, ccos_t=ccos, csin_t=csin, ident=ident_np())


def nsa_weights(a_w_in_l, cmp_pos_l, g):
    W = a_w_in_l
    q0 = g * 256
    def kcol(i):
        return W[:, 1024 + i * 256 + g * 64: 1024 + i * 256 + (g + 1) * 64]
    kc_, vc_, ks_, vs_, kw_, vw_ = [kcol(i) for i in range(6)]
    wg = W[:, 1024 + 6 * 256 + g * 12: 1024 + 6 * 256 + (g + 1) * 12]
    return dict(wq=np.ascontiguousarray(W[:, q0:q0 + 256]),
                wk3=np.ascontiguousarray(np.concatenate([kc_, ks_, kw_], 1)),
                wv3=np.ascontiguousarray(np.concatenate([vc_, vs_, vw_], 1)),
                wg=np.ascontiguousarray(wg),
                posT=np.ascontiguousarray(np.transpose(cmp_pos_l, (2, 0, 1))))


SEQ = 16384
NB = 2
CH = 4096
NPHASE = 99


def _run(nc, in_maps):
    res = run_bass_kernel_spmd(nc, in_maps, core_ids=list(range(8)))
    return res.results


def _cwb(conv_w, conv_b):
    return np.ascontiguousarray(np.concatenate([conv_w, conv_b[None]], 0).T.astype(np.float32))


def _chunk_with_halo(x_b, c, halo):
    lo = c * CH - halo
    if lo >= 0:
        return np.ascontiguousarray(x_b[lo:(c + 1) * CH])
    pad = np.zeros((-lo,) + x_b.shape[1:], x_b.dtype)
    return np.ascontiguousarray(np.concatenate([pad, x_b[0:(c + 1) * CH]], 0))


def kernel_unfused(x, norm_attn, norm_ffn, a_w_in, a_cmp_pos, a_cmp_w1, a_cmp_w2, a_w_out, kv_norm,
           b_w_kv, b_w_q, b_sinks, b_w_out, ffn_w_in, ffn_conv_w, ffn_conv_b, ffn_w_out,
           final_norm):
    f32 = lambda a: np.ascontiguousarray(np.asarray(a, dtype=np.float32))
    x = f32(x)
    norm_attn, norm_ffn = f32(norm_attn), f32(norm_ffn)
    a_w_in, a_cmp_pos, a_cmp_w1, a_cmp_w2, a_w_out = map(f32, (a_w_in, a_cmp_pos, a_cmp_w1,
                                                                a_cmp_w2, a_w_out))
    kv_norm, b_w_kv, b_w_q, b_sinks, b_w_out = map(f32, (kv_norm, b_w_kv, b_w_q, b_sinks, b_w_out))
    ffn_w_in, ffn_conv_w, ffn_conv_b, ffn_w_out, final_norm = map(
        f32, (ffn_w_in, ffn_conv_w, ffn_conv_b, ffn_w_out, final_norm))
    h = x
    ident = ident_np()
    gfin = np.ascontiguousarray(final_norm[None, :])
    cosA, sinA = rope_tables(np.arange(SEQ))
    constsA = nsa_consts(SEQ)

    for l in range(2):
        ncA = build_A(SEQ)
        maps = []
        for i in range(8):
            b, g = divmod(i, 4)
            m = dict(h_in=h[b], g_attn=gT_np(norm_attn[l]), w1=a_cmp_w1[l], w2=a_cmp_w2[l],
                     cos_t=cosA, sin_t=sinA)
            m.update(constsA)
            m.update(nsa_weights(a_w_in[l], a_cmp_pos[l], g))
            maps.append(m)
        resA = _run(ncA, maps)
        oT_full = np.zeros((NB, 16, 64, SEQ), NPBF16)
        for i in range(8):
            b, g = divmod(i, 4)
            o = resA[i]["oT_out"]
            for par in range(2):
                for hpl in range(2):
                    oT_full[b, g * 4 + 2 * hpl + par] = o[:, par * 2 + hpl, :]
        oT_full = oT_full.reshape(NB, 1024, SEQ)
        ncB = build_B([1] + [4] * 8, 1)
        maps = []
        for i in range(8):
            b, c = divmod(i, 4)
            maps.append(dict(
                h_in=_chunk_with_halo(h[b], c, 128),
                oT_in=np.ascontiguousarray(_chunk_with_halo(oT_full[b].T, c, 128).T),
                w_o=a_w_out[l], w_in=ffn_w_in[l], w_out=ffn_w_out[l],
                cwb=_cwb(ffn_conv_w[l], ffn_conv_b[l]), g_ffn=gT_np(norm_ffn[l]), g_fin=gfin,
                ident=ident))
        resB = _run(ncB, maps)
        h = np.stack([np.concatenate([resB[b * 4 + c]["h_out"] for c in range(4)], 0)
                      for b in range(NB)], 0)

    hkv = h
    for l in range(2, 4):
        j = l - 2
        ncC = build_C([2] + [4] * 8, 2, final_norm=(l == 3))
        maps = []
        for i in range(8):
            b, c = divmod(i, 4)
            pos = c * CH - 256 + np.arange(CH + 256)
            cos_t, sin_t = rope_tables(pos)
            maps.append(dict(
                h_in=_chunk_with_halo(h[b], c, 256), hkv_in=_chunk_with_halo(hkv[b], c, 256),
                w_q=b_w_q[j], w_kv=b_w_kv, sinks_b=sinks_row(b_sinks[j]),
                g_attn=gT_np(norm_attn[l]), g_kv=gT_np(kv_norm), cos_t=cos_t, sin_t=sin_t,
                masks=swa_masks(c > 0), w_o=b_w_out[j], w_in=ffn_w_in[l], w_out=ffn_w_out[l],
                cwb=_cwb(ffn_conv_w[l], ffn_conv_b[l]), g_ffn=gT_np(norm_ffn[l]), g_fin=gfin,
                ident=ident))
        resC = _run(ncC, maps)
        h = np.stack([np.concatenate([resC[b * 4 + c]["h_out"] for c in range(4)], 0)
                      for b in range(NB)], 0)
    return np.ascontiguousarray(h.astype(np.float32))


def build_fused(nphase=99):
    from concourse.bass import ds
    nph = [0]

    def stop():
        nph[0] += 1
        return nph[0] >= nphase

    nc = bass.Bass("TRN2", target_bir_lowering=False)
    p = Prog(nc)
    S = SEQ
    WB = 128 + CH
    WC = 256 + CH
    SUBW = 11 * 128
    xA = p.din("xA", [S, D], F32)
    xB = p.din("xB", [WB, D], F32)
    flag = p.din("flag", [128, 1], F32)
    out = p.dout("out", [CH, D], F32)
    oTloc = [nc.dram_tensor(f"oTloc{l}", [12 * 64, 4 * SUBW], BF16) for l in range(2)]
    OTb = [nc.dram_tensor(f"OTb{l}", [12 * 256, 4 * SUBW], BF16) for l in range(2)]
    oTwin = nc.dram_tensor("oTwin", [3 * 256, 4 * SUBW], BF16).ap()
    hloc = [nc.dram_tensor(f"hloc{k}", [CH, D], F32) for k in range(3)]
    Hb = [nc.dram_tensor(f"Hb{k}", [S, D], F32) for k in range(3)]
    hwin = nc.dram_tensor("hwin", [WC, D], F32).ap()
    hkvwin = nc.dram_tensor("hkvwin", [WC, D], F32).ap()
    rg = [[0, 1, 2, 3], [4, 5, 6, 7]]
    PID = p.s.pid

    def gather_group(src, dst, nchunk, rows, rk, wk):
        def fn(e, sem):
            for k in range(nchunk):
                e.collective_compute(
                    "AllGather", ALU.bypass, replica_groups=rg,
                    ins=[src.ap()[k * rows:(k + 1) * rows, :].opt()],
                    outs=[dst.ap()[k * 4 * rows:(k + 1) * 4 * rows, :].opt()]).then_inc(sem)
        p.s.cc(fn, [rk], [wk], n=nchunk)

    def h_row(tok):
        rank, rem = divmod(tok, CH)
        k, r = divmod(rem, 256)
        return (k * 4 + rank) * 256 + r

    def win_copy(dst, src, halo, q, rk, wk):
        s5 = src.rearrange("(k g r e) d -> k g r (e d)", k=16, g=4, e=8)
        dm = dst[halo:halo + CH, :].rearrange("(k g r e) d -> k g r (e d)", k=16, g=1, e=8)
        dh = dst[0:halo, :].rearrange("(k g r e) d -> k g r (e d)", k=1, g=1, e=8)
        h8 = halo // 8
        p.dmaf(lambda e: e.dma_start(
            out=dm, in_=s5[:, ds(PID(e, "c", lambda pid: pid % 4), 1), :, :]),
            r=[rk], w=[wk], q=q)
        p.dmaf(lambda e: e.dma_start(
            out=dh, in_=s5[15:16, ds(PID(e, "cm1", lambda pid: (pid + 3) % 4), 1), 32 - h8:32, :]),
            r=[rk], w=[wk], q=q)

    for l in range(2):
        p.sfx = f"_A{l}"
        rk = [] if l == 0 else [f"Hb{l - 1}"]
        O5 = oTloc[l].ap().rearrange("(c s d) (b t) -> c s d b t", c=4, s=3, b=4)

        def o_dst(qb, O5=O5):
            c, sl = divmod(qb, 32)
            sl += 1
            dsts = [O5[c, sl // 11, :, :, (sl % 11) * 128:(sl % 11) * 128 + 128]]
            if sl == 32 and c < 3:
                dsts.append(O5[c + 1, 0, :, :, 0:128])
            return dsts

        build_A(S, p=p, io=dict(h_ap=(xA if l == 0 else Hb[l - 1].ap()),
                                h_row=((lambda t: t) if l == 0 else h_row),
                                rkeys=rk, dkey=f"oTloc{l}", o_dst=o_dst,
                                o_zero=O5[0, 0, :, :, 0:128]))
        p.phase_end()
        gather_group(oTloc[l], OTb[l], 12, 64, f"oTloc{l}", f"OTb{l}")
        if stop():
            return p.finish(), dict(p.dins)
        p.sfx = f"_B{l}"
        O3 = OTb[l].ap().rearrange("(c r) f -> c r f", c=4)
        p.dmaf(lambda e, O3=O3: e.dma_start(
            out=oTwin.rearrange("(c r) f -> c r f", c=1),
            in_=O3[ds(PID(e, "c", lambda pid: pid % 4), 1), :, :]),
            r=[f"OTb{l}"], w=["oTwin"], q="act")
        if l == 0:
            h_ap = xB
            rkb = ["oTwin"]
        else:
            win_copy(hwin[0:WB, :], Hb[l - 1].ap(), 128, "act", f"Hb{l - 1}", "hwin")
            h_ap = hwin[0:WB, :]
            rkb = ["oTwin", "hwin"]
        W5 = oTwin.rearrange("(s g d) (b t) -> d s g b t", s=3, g=4, b=4)

        def oT_ap(wt, g, W5=W5):
            return W5[:, wt // 11, g, :, (wt % 11) * 128:(wt % 11) * 128 + 128]

        build_B([1] + [4] * 8, 1, p=p,
                io=dict(h_ap=h_ap, oT_ap=oT_ap, h_dst=hloc[l].ap(), flag=flag,
                        rkeys=rkb, dkey=f"hloc{l}"))
        p.phase_end()
        gather_group(hloc[l], Hb[l], 16, 256, f"hloc{l}", f"Hb{l}")
        if stop():
            return p.finish(), dict(p.dins)

    for l in range(2, 4):
        p.sfx = f"_C{l}"
        last = (l == 3)
        if l == 2:
            win_copy(hkvwin, Hb[1].ap(), 256, "sp", "Hb1", "hkvwin")
            h_ap, hkv_ap, rkc = hkvwin, hkvwin, ["hkvwin"]
        else:
            win_copy(hwin, Hb[2].ap(), 256, "sp", "Hb2", "hwin")
            h_ap, hkv_ap, rkc = hwin, hkvwin, ["hwin", "hkvwin"]
        build_C([2] + [4] * 8, 2, final_norm=last, p=p,
                io=dict(h_ap=h_ap, hkv_ap=hkv_ap, h_dst=(out if last else hloc[2].ap()), flag=flag,
                        rkeys=rkc, dkey=(None if last else "hloc2")))
        if not last:
            p.phase_end()
            gather_group(hloc[2], Hb[2], 16, 256, "hloc2", "Hb2")
            if stop():
                return p.finish(), dict(p.dins)
    return p.finish(), dict(p.dins)


def kernel(x, norm_attn, norm_ffn, a_w_in, a_cmp_pos, a_cmp_w1, a_cmp_w2, a_w_out, kv_norm,
           b_w_kv, b_w_q, b_sinks, b_w_out, ffn_w_in, ffn_conv_w, ffn_conv_b, ffn_w_out,
           final_norm):
    f32 = lambda a: np.ascontiguousarray(np.asarray(a, dtype=np.float32))
    x = f32(x)
    norm_attn, norm_ffn = f32(norm_attn), f32(norm_ffn)
    a_w_in, a_cmp_pos, a_cmp_w1, a_cmp_w2, a_w_out = map(f32, (a_w_in, a_cmp_pos, a_cmp_w1,
                                                                a_cmp_w2, a_w_out))
    kv_norm, b_w_kv, b_w_q, b_sinks, b_w_out = map(f32, (kv_norm, b_w_kv, b_w_q, b_sinks, b_w_out))
    ffn_w_in, ffn_conv_w, ffn_conv_b, ffn_w_out, final_norm = map(
        f32, (ffn_w_in, ffn_conv_w, ffn_conv_b, ffn_w_out, final_norm))
    nc, dins = build_fused(NPHASE)
    ident = ident_np()
    gfin = np.ascontiguousarray(final_norm[None, :])
    cosA, sinA = rope_tables(np.arange(SEQ))
    constsA = nsa_consts(SEQ)
    maps = []
    for i in range(8):
        b, c = divmod(i, 4)
        g = c
        m = dict(xA=x[b], xB=_chunk_with_halo(x[b], c, 128),
                 flag=np.full((128, 1), 0.0 if c == 0 else 1.0, np.float32))
        for l in range(2):
            a = dict(g_attn=gT_np(norm_attn[l]), w1=a_cmp_w1[l], w2=a_cmp_w2[l],
                     cos_t=cosA, sin_t=sinA)
            a.update(constsA)
            a.update(nsa_weights(a_w_in[l], a_cmp_pos[l], g))
            for k, v in a.items():
                m[f"{k}_A{l}"] = v
            bb = dict(w_o=a_w_out[l], w_in=ffn_w_in[l], w_out=ffn_w_out[l],
                      cwb=_cwb(ffn_conv_w[l], ffn_conv_b[l]), g_ffn=gT_np(norm_ffn[l]), g_fin=gfin,
                      ident=ident)
            for k, v in bb.items():
                m[f"{k}_B{l}"] = v
        pos = c * CH - 256 + np.arange(CH + 256)
        cos_t, sin_t = rope_tables(pos)
        for l in range(2, 4):
            j = l - 2
            cc = dict(w_q=b_w_q[j], w_kv=b_w_kv, sinks_b=sinks_row(b_sinks[j]),
                      g_attn=gT_np(norm_attn[l]), g_kv=gT_np(kv_norm), cos_t=cos_t, sin_t=sin_t,
                      masks=swa_masks(c > 0), w_o=b_w_out[j], w_in=ffn_w_in[l], w_out=ffn_w_out[l],
                      cwb=_cwb(ffn_conv_w[l], ffn_conv_b[l]), g_ffn=gT_np(norm_ffn[l]), g_fin=gfin,
                      ident=ident)
            for k, v in cc.items():
                m[f"{k}_C{l}"] = v
        m = {k: v for k, v in m.items() if k in dins}
        maps.append(m)
    res = _run(nc, maps)
    h = np.stack([np.concatenate([res[b * 4 + c]["out"] for c in range(4)], 0)
                  for b in range(NB)], 0)
    return np.ascontiguousarray(h.astype(np.float32))
```

```python
import numpy as np
import ml_dtypes
import concourse.bass as bass
import concourse.mybir as mybir
from concourse.bass_utils import run_bass_kernel_spmd

F32 = mybir.dt.float32
BF16 = mybir.dt.bfloat16
AF = mybir.ActivationFunctionType
ALU = mybir.AluOpType
AX = mybir.AxisListType

NPBF16 = ml_dtypes.bfloat16

D = 1024
DFF = 2816
EPS = 1e-6
MASKV = -240000.0

COMPUTE = ("pe", "act", "dve", "pool")
EPOCH = 30000
NSLOT = 12
SAME_ENG_SYNC = True
ZERO_BIAS = True


class Sched:
    def __init__(self, nc):
        self.nc = nc
        self.streams = {e: [] for e in COMPUTE + ("sp",)}
        self.cnt = {e: 0 for e in COMPUTE}
        self.known = {e: {} for e in self.streams}
        self.known_dma = {e: set() for e in self.streams}
        self.last_w = {}
        self.readers = {}
        self.ndma = {e: 0 for e in self.streams}
        self.sems = {}
        self.nsem = 0
        self.out_dmas = []
        self.ncc = 0
        self.snap = {e: [] for e in COMPUTE}
        self.snapd = {}

    def _sem(self, name):
        if name not in self.sems:
            self.sems[name] = self.nc.alloc_semaphore(name=name)
        return self.sems[name]

    def _ev_wait_args(self, ev):
        kind = ev[0]
        if kind == "c":
            _, eng, idx = ev
            ep, off = divmod(idx, EPOCH)
            return self._sem(f"s_{eng}_{ep}"), off + 1
        elif kind == "x":
            return self._sem(f"x_{ev[1]}"), ev[2]
        else:
            _, q, j = ev
            slot, use = j % NSLOT, j // NSLOT
            return self._sem(f"d_{q}_{slot}"), 16 * (use + 1)

    def _deps(self, eng, reads, writes):
        deps = set()
        for k in reads:
            w = self.last_w.get(k)
            if w is not None:
                deps.add(w)
        for k in writes:
            w = self.last_w.get(k)
            if w is not None:
                deps.add(w)
            for r in self.readers.get(k, ()):
                deps.add(r)
        waits = []
        best = {}
        for ev in deps:
            if ev[0] == "c":
                _, src, idx = ev
                if src == eng and eng == "pe":
                    continue
                if self.known[eng].get(src, -1) >= idx:
                    continue
                if best.get(src, -1) < idx:
                    best[src] = idx
            else:
                if ev in self.known_dma[eng]:
                    continue
                waits.append(ev)
                self.known_dma[eng].add(ev)
        for src, idx in best.items():
            self.known[eng][src] = idx
            waits.append(("c", src, idx))
        for ev in list(waits):
            sn = self.snap[ev[1]][ev[2]] if ev[0] == "c" else self.snapd.get(ev)
            if sn is None:
                continue
            kn = self.known[eng]
            for ci, ce in enumerate(COMPUTE):
                if sn[ci] > kn.get(ce, -1) and (ce != eng or True):
                    kn[ce] = sn[ci]
        return waits

    def _snapshot(self, eng):
        kn = self.known[eng]
        return tuple(kn.get(ce, -1) for ce in COMPUTE)

    def _mark(self, ev, reads, writes):
        for k in reads:
            self.readers.setdefault(k, []).append(ev)
        for k in writes:
            self.last_w[k] = ev
            self.readers[k] = []

    def op(self, eng, fn, reads=(), writes=()):
        assert eng in COMPUTE
        waits = self._deps(eng, reads, writes)
        idx = self.cnt[eng]
        self.cnt[eng] += 1
        ev = ("c", eng, idx)
        if not SAME_ENG_SYNC or eng == "pe":
            self.known[eng][eng] = idx
        sn = list(self._snapshot(eng))
        sn[COMPUTE.index(eng)] = max(sn[COMPUTE.index(eng)], idx - 1)
        self.snap[eng].append(tuple(sn))
        self._mark(ev, reads, writes)
        self.streams[eng].append((waits, fn, ev))
        return ev

    def dma(self, q, fn, reads=(), writes=(), is_output=False):
        waits = self._deps(q, reads, writes)
        j = self.ndma[q]
        self.ndma[q] += 1
        if j >= NSLOT:
            prev = ("d", q, j - NSLOT)
            if prev not in self.known_dma[q]:
                waits.append(prev)
                self.known_dma[q].add(prev)
        ev = ("d", q, j)
        self.snapd[ev] = self._snapshot(q)
        self._mark(ev, reads, writes)
        self.streams[q].append((waits, fn, ev))
        if is_output:
            self.out_dmas.append(ev)
        return ev

    def pid(self, e, key="pid", fn=None):
        k = (self.cur_eng, key)
        if k not in self.pid_cache:
            if key == "pid":
                self.pid_cache[k] = e.partition_id()
            else:
                self.pid_cache[k] = e.snap(fn(self.pid(e)))
        return self.pid_cache[k]

    def cc(self, fn, reads=(), writes=(), n=1):
        waits = self._deps("pool", reads, writes)
        ev = ("x", self.ncc, n)
        self.ncc += 1
        self._mark(ev, reads, writes)
        self.streams["pool"].append((waits, fn, ev))
        self.known_dma["pool"].add(ev)
        idx = self.cnt["pool"]
        self.cnt["pool"] += 1
        nev = ("c", "pool", idx)
        self.snap["pool"].append(self._snapshot("pool"))
        self._mark(nev, (), writes)
        self.streams["pool"].append(([ev], lambda e: e.nop(), nev))
        return nev

    def barrier(self):
        evs = []
        for eng in COMPUTE:
            if self.cnt[eng] > 0:
                evs.append(("c", eng, self.cnt[eng] - 1))
        for q, n in self.ndma.items():
            for j in range(max(0, n - NSLOT), n):
                evs.append(("d", q, j))
        for eng in self.streams:
            waits = []
            for ev in evs:
                if ev[0] == "c":
                    if ev[1] == eng:
                        continue
                    if self.known[eng].get(ev[1], -1) >= ev[2]:
                        continue
                    self.known[eng][ev[1]] = ev[2]
                elif ev[0] == "x":
                    continue
                elif ev in self.known_dma[eng]:
                    continue
                else:
                    self.known_dma[eng].add(ev)
                waits.append(ev)
            self.streams[eng].append((waits, None, None))

    def emit(self, final=True):
        nc = self.nc
        final_waits = list(self.out_dmas) if final else []
        self.pid_cache = {}
        with nc.Block() as block:
            def run(engname, e):
                self.cur_eng = engname
                for waits, fn, ev in self.streams[engname]:
                    for w in waits:
                        s, v = self._ev_wait_args(w)
                        e.wait_ge(s, v)
                    if fn is None:
                        continue
                    s, v = self._ev_wait_args(ev)
                    if ev[0] == "x":
                        fn(e, s)
                        continue
                    ins = fn(e)
                    if ev[0] == "c":
                        ins.then_inc(s, 1)
                    else:
                        ins.then_inc(s, 16)
                if engname == "sp":
                    for w in final_waits:
                        s, v = self._ev_wait_args(w)
                        e.wait_ge(s, v)

            @block.tensor
            def _(e):
                run("pe", e)

            @block.scalar
            def _(e):
                run("act", e)

            @block.vector
            def _(e):
                run("dve", e)

            @block.gpsimd
            def _(e):
                run("pool", e)

            @block.sync
            def _(e):
                run("sp", e)
        for k in self.streams:
            self.streams[k] = []


class Prog:
    def __init__(self, nc):
        from contextlib import ExitStack
        self.nc = nc
        self.s = Sched(nc)
        self.es = ExitStack()
        self.ndram = 0
        self.sfx = ""
        self.dins = {}
        self.ext = {}

    def sb(self, name, shape, dt):
        return self.es.enter_context(self.nc.sbuf_tensor("sb_" + name + self.sfx, list(shape), dt))

    def ps(self, name, shape, dt=F32):
        return self.es.enter_context(self.nc.psum_tensor("ps_" + name + self.sfx, list(shape), dt))

    def din(self, name, shape, dt):
        nm = name + self.sfx
        if nm in self.ext:
            return self.ext[nm]
        self.dins[nm] = (tuple(shape), dt)
        return self.nc.dram_tensor(nm, list(shape), dt, kind="ExternalInput").ap()

    def dint(self, name, shape, dt):
        return self.nc.dram_tensor(name, list(shape), dt)

    def phase_end(self):
        from contextlib import ExitStack
        self.s.barrier()
        self.s.emit(final=False)
        self.es.close()
        self.es = ExitStack()

    def dmaf(self, fn, r=(), w=(), q="sp", is_output=False):
        return self.s.dma(q, fn, r, w, is_output)

    def dout(self, name, shape, dt):
        return self.nc.dram_tensor(name, list(shape), dt, kind="ExternalOutput").ap()

    def dma(self, out, in_, r=(), w=(), q="sp", is_output=False):
        return self.s.dma(q, lambda e: e.dma_start(out=out, in_=in_), r, w, is_output)

    def mm(self, out, lhsT, rhs, start, stop, r=(), w=()):
        return self.s.op("pe", lambda e: e.matmul(out, lhsT, rhs, start=start, stop=stop), r, w)

    def tr(self, out, in_, ident, r=(), w=()):
        return self.s.op("pe", lambda e: e.transpose(out, in_, ident), r, w)

    def act(self, out, in_, func, r=(), w=(), bias=None, scale=None, accum_out=None):
        kw = {}
        if bias is not None:
            kw["bias"] = bias
        if scale is not None:
            kw["scale"] = scale
        if accum_out is not None:
            kw["accum_out"] = accum_out
        return self.s.op("act", lambda e: e.activation(out, in_, func, **kw), r, w)

    def tt(self, out, in0, in1, op, r=(), w=(), eng="dve"):
        return self.s.op(eng, lambda e: e.tensor_tensor(out, in0, in1, op), r, w)

    def ts(self, out, in0, s1, s2, op0, op1=None, r=(), w=(), eng="dve", accum_out=None):
        kw = {}
        if accum_out is not None:
            kw["accum_out"] = accum_out
        if op1 is None:
            return self.s.op(eng, lambda e: e.tensor_scalar(out, in0, s1, s2, op0, **kw), r, w)
        return self.s.op(eng, lambda e: e.tensor_scalar(out, in0, s1, s2, op0, op1, **kw), r, w)

    def stt(self, out, in0, scalar, in1, op0, op1, r=(), w=(), eng="dve"):
        return self.s.op(eng, lambda e: e.scalar_tensor_tensor(out, in0, scalar, in1, op0, op1), r, w)

    def cp(self, out, in_, r=(), w=(), eng="dve"):
        if eng == "act":
            return self.s.op("act", lambda e: e.copy(out, in_), r, w)
        return self.s.op(eng, lambda e: e.tensor_copy(out, in_), r, w)

    def recip(self, out, in_, r=(), w=()):
        return self.s.op("dve", lambda e: e.reciprocal(out, in_), r, w)

    def memset(self, ap, val, w=(), eng="dve"):
        return self.s.op(eng, lambda e: e.memset(ap, val), (), w)

    def finish(self):
        self.s.emit()
        self.es.close()
        return self.nc


def load_cast_weight(p, w_dram, dst, nk, ncols, stage, tag, chunk_cols=1024):
    i = 0
    for kc in range(nk):
        for c0 in range(0, ncols, chunk_cols):
            cw = min(chunk_cols, ncols - c0)
            stg = stage[i % 2]
            p.dma(stg[:, 0:cw], w_dram[kc * 128:(kc + 1) * 128, c0:c0 + cw],
                  w=[("stg", i % 2)])
            p.cp(dst[:, kc, c0:c0 + cw], stg[:, 0:cw], r=[("stg", i % 2)],
                 w=[(tag, kc)], eng="pool")
            i += 1


def rmsnorm_tile(p, x_ap, gain_bc, out_ap, scr, keys_r, keys_w, tagk):
    sq, ss, sd, rs = scr
    p.act(sq, x_ap, AF.Square, r=keys_r, w=[("sq", tagk), ("ss", tagk)], accum_out=ss)
    p.act(sd, ss, AF.Sqrt, r=[("ss", tagk)], w=[("sd", tagk)], bias=EPS, scale=1.0 / D)
    p.recip(rs, sd, r=[("sd", tagk)], w=[("rs", tagk)])
    if gain_bc is None:
        p.ts(out_ap, x_ap, rs, None, ALU.mult, r=list(keys_r) + [("rs", tagk)], w=keys_w)
    else:
        p.stt(out_ap, x_ap, rs, gain_bc, ALU.mult, ALU.mult,
              r=list(keys_r) + [("rs", tagk), "gains"], w=keys_w)


class FFNCtx:
    def __init__(self, p, pre, max_nt=4):
        self.p = p
        self.max_nt = max_nt
        nt = max_nt
        self.wo = p.sb(pre + "wo", [64, 16, 1024], BF16)
        self.woch = [p.sb(pre + f"woch{i}", [128, 512], BF16) for i in range(2)]
        self.fstage = [p.sb(pre + f"fstg{i}", [128, 2048], F32) for i in range(2)]
        self.stage = [self.fstage[i][:, 0:1024] for i in range(2)]
        self.gT = p.sb(pre + "gT", [128, 8], F32)
        self.wch = [p.sb(pre + f"wch{i}", [128, 8, 2, 128], BF16) for i in range(2)]
        self.h1 = p.sb(pre + "h1", [128, nt, 1024], F32)
        self.oT = p.sb(pre + "oT", [64, 16, nt * 128], BF16)
        self.hn = p.sb(pre + "hn", [128, 1024], BF16)
        self.hnT = p.sb(pre + "hnT", [128, 8, nt * 128], BF16)
        self.actT = p.sb(pre + "actT", [128, 22, nt * 128], BF16)
        self.usb = [[p.sb(pre + f"usb{i}{a}", [128, 2 + nt * 128], F32) for a in range(2)]
                    for i in range(2)]
        self.carry = p.sb(pre + "carry", [128, 44, 2], F32)
        self.cwb = p.sb(pre + "cwb", [128, 44, 4], F32)
        self.t1 = p.sb(pre + "t1", [128, nt * 128], F32)
        self.t2 = p.sb(pre + "t2", [128, nt * 128], F32)
        self.ca = p.sb(pre + "ca", [128, nt * 128], F32)
        self.cg = p.sb(pre + "cg", [128, nt * 128], F32)
        self.sa = p.sb(pre + "sa", [128, nt * 128], F32)
        self.sq = p.sb(pre + "sq", [128, 1024], BF16)
        self.ss = p.sb(pre + "ss", [128, 1], F32)
        self.sd = p.sb(pre + "sd", [128, 1], F32)
        self.rs = p.sb(pre + "rs", [128, 1], F32)
        self.gainf = p.sb(pre + "gainf", [128, 1024], F32)
        self.ident = p.sb(pre + "ident", [128, 128], BF16)
        self.hfin = p.sb(pre + "hfin", [128, 1024], F32)
        self.psum = p.ps(pre + "psum", [128, 7 * 512])
        self.psT = p.ps(pre + "psT", [128, 8, 128], BF16)
        self.psA = [self.bank(0), self.bank(1)]
        self.psU = [[self.bank(2), self.bank(3)], [self.bank(4), self.bank(5)]]
        self.nA = 0
        self.nfc = 0
        self.nwo = 0

    def bank(self, i, n=1):
        return self.psum[:, i * 512:(i + n) * 512]

    def load_weights(self, w_o, w_in, w_out, cwb, g_ffn, ident, g_final=None, head_order=None):
        p = self.p
        self.w_in = w_in
        p.dma(self.ident[:], ident, w=["ident"])
        p.dma(self.cwb[:], cwb.rearrange("(c p) f -> p c f", p=128), w=["cwb"])
        p.dma(self.gT[:], g_ffn, w=["gT"])
        self.w_out = w_out
        if g_final is not None:
            p.dma(self.gainf[:], g_final.to_broadcast([128, 1024]), w=["gainf"])
        p.memset(self.carry[:], 0.0, w=["carry"])
        if w_o is not None:
            ho = head_order if head_order is not None else list(range(16))
            for i, h in enumerate(ho):
                stg = self.stage[i % 2]
                sk = ("stg", "ffn", i % 2, 0)
                p.dma(stg[0:64, :], w_o[h * 64:(h + 1) * 64, :], w=[sk])
                p.cp(self.wo[:, i, :], stg[0:64, :], r=[sk], w=[("wo", i)], eng="pool")

    def run_supertile(self, nt, h_src, oT_src, h_dst, n_skip_out=0, final_norm=False,
                      h1_preloaded=False, h_fn=None, oT_fn=None, n_flag=0, flag=None,
                      rkeys=(), dkey=None):
        p = self.p
        ntok = nt * 128
        if not h1_preloaded:
            for j in range(nt):
                p.dma(self.h1[:, j, :], h_src[j * 128:(j + 1) * 128, :], r=list(rkeys),
                      w=[("h1", j)])
                if j < n_flag:
                    p.ts(self.h1[:, j, :], self.h1[:, j, :], flag, None, ALU.mult,
                         r=[("h1", j), "flag"], w=[("h1", j)])
        if oT_src is not None or oT_fn is not None:
            if oT_fn is not None:
                for j in range(nt):
                    for g in range(4):
                        p.dma(self.oT[:, g * 4:(g + 1) * 4, j * 128:(j + 1) * 128], oT_fn(j, g),
                              r=list(rkeys), w=["oT"])
            elif not isinstance(oT_src, str):
                p.dma(self.oT[:, :, 0:ntok], oT_src.rearrange("(c p) t -> p c t", p=64), w=["oT"])
            for j in range(nt):
                for half in range(2):
                    ps = self.psA[self.nA % 2]
                    pk = ("bank", self.nA % 2)
                    self.nA += 1
                    for kc in range(16):
                        p.mm(ps, self.oT[:, kc, j * 128:(j + 1) * 128],
                             self.wo[:, kc, half * 512:(half + 1) * 512],
                             start=(kc == 0), stop=(kc == 15),
                             r=["oT", ("wo", kc)], w=[pk])
                    hs = self.h1[:, j, half * 512:(half + 1) * 512]
                    p.tt(hs, hs, ps, ALU.add, r=[pk, ("h1", j)], w=[("h1", j)])
        for j in range(nt):
            rmsnorm_tile(p, self.h1[:, j, :], None, self.hn[:],
                         (self.sq[:], self.ss[:], self.sd[:], self.rs[:]),
                         [("h1", j)], ["hn"], "f")
            for kc in range(8):
                p.tr(self.psT[:, kc, :], self.hn[:, kc * 128:(kc + 1) * 128], self.ident[:],
                     r=["hn", "ident"], w=["psT"])
            p.cp(self.hnT[:, :, j * 128:(j + 1) * 128], self.psT[:], r=["psT"], w=[("hnT", j)],
                 eng="act")
        hnT_keys = [("hnT", j) for j in range(nt)]
        for fc in range(22):
            par = self.nfc % 2
            self.nfc += 1
            stg = self.fstage[par]
            wch = self.wch[par]
            for ag in range(2):
                c0 = ag * DFF + fc * 128
                p.dma(stg[:, ag * 1024:(ag + 1) * 1024].rearrange("p (c f) -> p c f", c=8),
                      self.w_in[:, c0:c0 + 128].rearrange("(c p) f -> p c f", p=128),
                      w=[("stg", "ffn", par, ag)])
                p.tt(wch[:, :, ag, :],
                     stg[:, ag * 1024:(ag + 1) * 1024].rearrange("p (c f) -> p c f", c=8),
                     self.gT[:].unsqueeze(2).to_broadcast([128, 8, 128]), ALU.mult,
                     r=[("stg", "ffn", par, ag), "gT"], w=[("wch", par, ag)], eng="pool")
            cs = []
            for ag in range(2):
                ps = self.psU[par][ag]
                pk = ("bank", 2 + 2 * par + ag)
                for kc in range(8):
                    p.mm(ps[:, 0:ntok], wch[:, kc, ag, :], self.hnT[:, kc, 0:ntok],
                         start=(kc == 0), stop=(kc == 7),
                         r=[("wch", par, ag)] + hnT_keys, w=[pk])
                usb = self.usb[par][ag]
                uk = ("usb", par, ag)
                ch = ag * 22 + fc
                p.cp(usb[:, 0:2], self.carry[:, ch, :], r=["carry%d" % ch, "carry"], w=[uk],
                     eng="pool")
                p.cp(usb[:, 2:2 + ntok], ps[:, 0:ntok], r=[pk], w=[uk], eng="act")
                p.cp(self.carry[:, ch, :], usb[:, ntok:ntok + 2], r=[uk], w=["carry%d" % ch],
                     eng="pool")
                cw = self.cwb
                dst = self.ca if ag == 0 else self.cg
                dk = "ca" if ag == 0 else "cg"
                p.ts(self.t1[:, 0:ntok], usb[:, 2:2 + ntok], cw[:, ch, 2:3], cw[:, ch, 3:4],
                     ALU.mult, ALU.add, r=[uk, "cwb"], w=["t1"])
                p.stt(self.t2[:, 0:ntok], usb[:, 1:1 + ntok], cw[:, ch, 1:2], self.t1[:, 0:ntok],
                      ALU.mult, ALU.add, r=[uk, "cwb", "t1"], w=["t2"])
                p.stt(dst[:, 0:ntok], usb[:, 0:ntok], cw[:, ch, 0:1], self.t2[:, 0:ntok],
                      ALU.mult, ALU.add, r=[uk, "cwb", "t2"], w=[dk])
            p.act(self.sa[:, 0:ntok], self.ca[:, 0:ntok], AF.Silu, r=["ca"], w=["sa"])
            p.tt(self.actT[:, fc, 0:ntok], self.sa[:, 0:ntok], self.cg[:, 0:ntok], ALU.mult,
                 r=["sa", "cg"], w=[("actT", fc)])
        for half in range(2):
            for fc in range(22):
                wp = self.nwo % 2
                self.nwo += 1
                stg = self.fstage[wp]
                sk = ("stg", "ffn", wp, 0)
                p.dma(stg[:, 0:512], self.w_out[fc * 128:(fc + 1) * 128, half * 512:(half + 1) * 512],
                      w=[sk])
                p.cp(self.woch[wp][:], stg[:, 0:512], r=[sk], w=[("woch", wp)], eng="pool")
                for j in range(nt):
                    p.mm(self.bank(j), self.actT[:, fc, j * 128:(j + 1) * 128], self.woch[wp][:],
                         start=(fc == 0), stop=(fc == 21),
                         r=[("actT", fc), ("woch", wp)], w=[("bank", j)])
            for j in range(nt):
                hs = self.h1[:, j, half * 512:(half + 1) * 512]
                p.tt(hs, hs, self.bank(j), ALU.add, r=[("bank", j), ("h1", j)], w=[("h1", j)])
        for j in range(nt):
            if h_dst is not None and j >= n_skip_out:
                jo = j - n_skip_out
                if final_norm:
                    rmsnorm_tile(p, self.h1[:, j, :], self.gainf[:], self.hfin[:],
                                 (self.sq[:], self.ss[:], self.sd[:], self.rs[:]),
                                 [("h1", j), "gainf"], ["hfin"], "f")
                    p.dma(h_dst[jo * 128:(jo + 1) * 128, :], self.hfin[:], r=["hfin"],
                          w=([dkey] if dkey else []), q="pool", is_output=True)
                else:
                    p.dma(h_dst[jo * 128:(jo + 1) * 128, :], self.h1[:, j, :], r=[("h1", j)],
                          w=([dkey] if dkey else []), q="pool", is_output=(dkey is None))


def ident_np():
    return np.eye(128, dtype=np.float32).astype(NPBF16)


def b_head_order():
    return [g * 4 + 2 * hpl + par for g in range(4) for par in range(2) for hpl in range(2)]


def build_B(st_sizes, n_skip_tiles, final_norm=False, p=None, io=None):
    fused = p is not None
    if not fused:
        nc = bass.Bass("TRN2", target_bir_lowering=False)
        p = Prog(nc)
    ntiles = sum(st_sizes)
    ntok = ntiles * 128
    if not fused:
        h_in = p.din("h_in", [ntok, D], F32)
        oT_in = p.din("oT_in", [D, ntok], BF16)
    w_o = p.din("w_o", [D, D], F32)
    w_in = p.din("w_in", [D, 2 * DFF], F32)
    w_out = p.din("w_out", [DFF, D], F32)
    cwb = p.din("cwb", [2 * DFF, 4], F32)
    g_ffn = p.din("g_ffn", [128, 8], F32)
    g_fin = p.din("g_fin", [1, D], F32)
    ident = p.din("ident", [128, 128], BF16)
    if not fused:
        h_out = p.dout("h_out", [(ntiles - n_skip_tiles) * 128, D], F32)
    else:
        h_out = io["h_dst"]
    f = FFNCtx(p, "f_", max_nt=max(st_sizes))
    f.load_weights(w_o, w_in, w_out, cwb, g_ffn, ident, g_fin,
                   head_order=(b_head_order() if fused else None))
    if fused:
        flag_sb = p.sb("flag", [128, 1], F32)
        p.dma(flag_sb[:], io["flag"], w=["flag"])
    t0 = 0
    for nt in st_sizes:
        skip = max(0, min(nt, n_skip_tiles - t0))
        o0 = max(0, t0 - n_skip_tiles)
        dst = h_out[o0 * 128:(o0 + nt - skip) * 128, :] if skip < nt else None
        if fused:
            f.run_supertile(nt, io["h_ap"][t0 * 128:(t0 + nt) * 128, :], None, dst,
                            n_skip_out=skip, final_norm=final_norm,
                            oT_fn=lambda j, g, t0=t0: io["oT_ap"](t0 + j, g),
                            n_flag=skip, flag=flag_sb[:], rkeys=io["rkeys"], dkey=io["dkey"])
        else:
            f.run_supertile(nt, h_in[t0 * 128:(t0 + nt) * 128, :],
                            oT_in[:, t0 * 128:(t0 + nt) * 128], dst, n_skip_out=skip,
                            final_norm=final_norm)
        t0 += nt
    if fused:
        return None
    return p.finish()


def c_head_order():
    return [8 * g + 2 * hpl + par for g in range(2) for par in range(2) for hpl in range(4)]


def build_C(st_sizes, n_skip_tiles, final_norm=False, p=None, io=None):
    fused = p is not None
    if not fused:
        nc = bass.Bass("TRN2", target_bir_lowering=False)
        p = Prog(nc)
    ntiles = sum(st_sizes)
    ntok = ntiles * 128
    mx = max(st_sizes)
    if not fused:
        h_in = p.din("h_in", [ntok, D], F32)
        hkv_in = p.din("hkv_in", [ntok, D], F32)
    w_q = p.din("w_q", [D, D], F32)
    w_kv = p.din("w_kv", [D, 256], F32)
    sinks_b = p.din("sinks_b", [1, 2048], F32)
    g_attn = p.din("g_attn", [128, 8], F32)
    g_kv = p.din("g_kv", [128, 8], F32)
    cos_t = p.din("cos_t", [128, ntok], F32)
    sin_t = p.din("sin_t", [128, ntok], F32)
    masks = p.din("masks", [3, 128, 512], BF16)
    w_o = p.din("w_o", [D, D], F32)
    w_in = p.din("w_in", [D, 2 * DFF], F32)
    w_out = p.din("w_out", [DFF, D], F32)
    cwb = p.din("cwb", [2 * DFF, 4], F32)
    g_ffn = p.din("g_ffn", [128, 8], F32)
    g_fin = p.din("g_fin", [1, D], F32)
    ident = p.din("ident", [128, 128], BF16)
    if not fused:
        h_out = p.dout("h_out", [(ntiles - n_skip_tiles) * 128, D], F32)
    else:
        h_out = io["h_dst"]

    f = FFNCtx(p, "f_", max_nt=mx)
    f.load_weights(w_o, w_in, w_out, cwb, g_ffn, ident, g_fin, head_order=c_head_order())
    if fused:
        flag_sb = p.sb("flag", [128, 1], F32)
        p.dma(flag_sb[:], io["flag"], w=["flag"])

    gTq = p.sb("gTq", [128, 8], F32)
    gTk = p.sb("gTk", [128, 8], F32)
    wk2 = p.sb("wk2", [128, 8, 2, 2, 128], BF16)
    wv = p.sb("wv", [128, 8, 128], BF16)
    hkv = p.sb("hkv", [128, 1024], F32)
    hnqT = f.hnT
    hkvT = p.sb("hkvT", [128, 8, mx * 128], BF16)
    QT2 = p.sb("QT2", [128, 8, mx * 128], BF16)
    KT2 = p.sb("KT2", [128, 2, (mx + 1) * 128], BF16)
    VA = p.sb("VA", [128, mx + 1, 2, 65], BF16)
    PT = [p.sb(f"PT{i}", [128, 1024], BF16) for i in range(2)]
    msk = p.sb("msk", [128, 3, 512], BF16)
    cos_sb = p.sb("cos_sb", [128, mx * 128], F32)
    sin_sb = p.sb("sin_sb", [128, mx * 128], F32)
    sexp = p.sb("sexp", [128, 2048], BF16)
    zr = p.sb("zr", [128, 1024], F32)
    rz = zr
    ones = p.sb("ones", [128, 64], F32)
    osb = f.hfin[0:64, :]
    psS = [f.bank(2, 2), f.bank(4, 2)]
    psSk = [[("bank", 2), ("bank", 3)], [("bank", 4), ("bank", 5)]]
    psO = f.bank(0, 2)
    psOk = [("bank", 0), ("bank", 1)]
    psB = f.bank(6)
    psBk = [("bank", 6)]

    p.dma(gTq[:], g_attn, w=["gTq"])
    p.dma(gTk[:], g_kv, w=["gTk"])
    p.dma(msk[:], masks.rearrange("m p c -> p m c"), w=["msk"])
    p.dma(zr[64:65, :], sinks_b[:, 0:1024], w=["zr"])
    p.act(sexp[64:65, 0:1024], zr[64:65, :], AF.Exp, r=["zr"], w=["sexp"])
    p.dma(zr[64:65, :], sinks_b[:, 1024:2048], r=["sexp"], w=["zr"])
    p.act(sexp[64:65, 1024:2048], zr[64:65, :], AF.Exp, r=["zr"], w=["sexp"])
    p.memset(ones[:], 1.0, w=["ones"])
    p.memset(VA[:], 1.0, w=["VA"] + [("VA", i) for i in range(mx + 1)])
    p.memset(KT2[:], 0.0, w=["KT2", ("KT2", 0), ("KT2", 1)])
    for kc in range(8):
        stg = f.stage[kc % 2]
        sk = ("stg", "ffn", kc % 2, 0)
        gs = gTk[:, kc:kc + 1]
        p.dma(stg[:, 0:256], w_kv[kc * 128:(kc + 1) * 128, :], w=[sk])
        for g in range(2):
            for dup in range(2):
                p.ts(wk2[:, kc, g, 0, dup * 64:(dup + 1) * 64], stg[:, g * 64:(g + 1) * 64],
                     gs, None, ALU.mult, r=[sk, "gTk"], w=["wk2"], eng="pool")
                p.ts(wk2[:, kc, g, 1, dup * 64:dup * 64 + 32], stg[:, g * 64 + 32:g * 64 + 64],
                     gs, None, ALU.mult, r=[sk, "gTk"], w=["wk2"], eng="pool")
                p.ts(wk2[:, kc, g, 1, dup * 64 + 32:dup * 64 + 64], stg[:, g * 64:g * 64 + 32],
                     gs, None, ALU.mult, r=[sk, "gTk"], w=["wk2"], eng="pool")
        p.ts(wv[:, kc, :], stg[:, 128:256], gs, None, ALU.mult, r=[sk, "gTk"], w=["wv"], eng="pool")

    scr = (f.sq[:], f.ss[:], f.sd[:], f.rs[:])
    t0 = 0
    first_real = n_skip_tiles
    for nt in st_sizes:
        n = nt * 128
        for j in range(nt):
            if fused:
                p.dma(f.h1[:, j, :], io["h_ap"][(t0 + j) * 128:(t0 + j + 1) * 128, :],
                      r=list(io["rkeys"]), w=[("h1", j)])
                if t0 + j < n_skip_tiles:
                    p.ts(f.h1[:, j, :], f.h1[:, j, :], flag_sb[:], None, ALU.mult,
                         r=[("h1", j), "flag"], w=[("h1", j)])
            else:
                p.dma(f.h1[:, j, :], h_in[(t0 + j) * 128:(t0 + j + 1) * 128, :], w=[("h1", j)])
        p.dma(cos_sb[:, 0:n], cos_t[:, t0 * 128:t0 * 128 + n], w=["cos"])
        p.dma(sin_sb[:, 0:n], sin_t[:, t0 * 128:t0 * 128 + n], w=["sin"])
        for j in range(nt):
            rmsnorm_tile(p, f.h1[:, j, :], None, f.hn[:], scr, [("h1", j)], ["hn"], "f")
            for kc in range(8):
                p.tr(f.psT[:, kc, :], f.hn[:, kc * 128:(kc + 1) * 128], f.ident[:],
                     r=["hn", "ident"], w=["psT"])
            p.cp(hnqT[:, :, j * 128:(j + 1) * 128], f.psT[:], r=["psT"], w=[("hnT", j)], eng="act")
            if fused:
                p.dma(hkv[:], io["hkv_ap"][(t0 + j) * 128:(t0 + j + 1) * 128, :],
                      r=list(io["rkeys"]), w=["hkv"])
                if t0 + j < n_skip_tiles:
                    p.ts(hkv[:], hkv[:], flag_sb[:], None, ALU.mult, r=["hkv", "flag"], w=["hkv"])
            else:
                p.dma(hkv[:], hkv_in[(t0 + j) * 128:(t0 + j + 1) * 128, :], w=["hkv"])
            rmsnorm_tile(p, hkv[:], None, f.hn[:], scr, ["hkv"], ["hn"], "f")
            for kc in range(8):
                p.tr(f.psT[:, kc, :], f.hn[:, kc * 128:(kc + 1) * 128], f.ident[:],
                     r=["hn", "ident"], w=["psT"])
            p.cp(hkvT[:, :, j * 128:(j + 1) * 128], f.psT[:], r=["psT"], w=[("hkvT", j)], eng="act")
        hq_keys = [("hnT", j) for j in range(nt)]
        hk_keys = [("hkvT", j) for j in range(nt)]

        def rope_out(dst, psn, pss, rk, wk):
            p.tt(f.t1[:, 0:n], psn, cos_sb[:, 0:n], ALU.mult, r=rk[0:1] + ["cos"], w=["t1"])
            p.tt(f.t2[:, 0:n], pss, sin_sb[:, 0:n], ALU.mult, r=rk[1:2] + ["sin"], w=["t2"])
            p.tt(dst, f.t1[:, 0:n], f.t2[:, 0:n], ALU.add, r=["t1", "t2"], w=wk)

        for g in range(2):
            par = f.nfc % 2
            f.nfc += 1
            bk = [("bank", 2 + 2 * par), ("bank", 3 + 2 * par)]
            for v in range(2):
                for kc in range(8):
                    p.mm(f.psU[par][v][:, 0:n], wk2[:, kc, g, v, :], hkvT[:, kc, 0:n],
                         start=(kc == 0), stop=(kc == 7), r=["wk2"] + hk_keys, w=[bk[v]])
            rope_out(KT2[:, g, 128:128 + n], f.psU[par][0][:, 0:n], f.psU[par][1][:, 0:n],
                     bk, [("KT2", g)])
        for j in range(nt):
            ps = f.psA[f.nA % 2]
            pk = ("bank", f.nA % 2)
            f.nA += 1
            for kc in range(8):
                p.mm(ps[:, 0:128], hkvT[:, kc, j * 128:(j + 1) * 128], wv[:, kc, :],
                     start=(kc == 0), stop=(kc == 7), r=[("hkvT", j), "wv"], w=[pk])
            p.cp(VA[:, j + 1, :, 0:64], ps[:, 0:128].rearrange("p (g d) -> p g d", g=2),
                 r=[pk], w=[("VA", j + 1)], eng="act")
        for hp in range(8):
            par = f.nfc % 2
            f.nfc += 1
            stg = f.fstage[par]
            wch = f.wch[par]
            p.dma(stg[:, 0:1024].rearrange("p (c f) -> p c f", c=8),
                  w_q[:, hp * 128:(hp + 1) * 128].rearrange("(c p) f -> p c f", p=128),
                  w=[("stg", "ffn", par, 0)])
            p.tt(wch[:, :, 0, :], stg[:, 0:1024].rearrange("p (c f) -> p c f", c=8),
                 gTq[:].unsqueeze(2).to_broadcast([128, 8, 128]), ALU.mult,
                 r=[("stg", "ffn", par, 0), "gTq"], w=[("wch", par, 0)], eng="pool")
            src = wch[:, :, 0, :].rearrange("p c (h d) -> p c h d", h=2)
            dsw = wch[:, :, 1, :].rearrange("p c (h d) -> p c h d", h=2)
            p.cp(dsw[:, :, :, 0:32], src[:, :, :, 32:64], r=[("wch", par, 0)],
                 w=[("wch", par, 1)], eng="pool")
            p.cp(dsw[:, :, :, 32:64], src[:, :, :, 0:32], r=[("wch", par, 0)],
                 w=[("wch", par, 1)], eng="pool")
            bk = [("bank", 2 + 2 * par), ("bank", 3 + 2 * par)]
            for v in range(2):
                for kc in range(8):
                    p.mm(f.psU[par][v][:, 0:n], wch[:, kc, v, :], hnqT[:, kc, 0:n],
                         start=(kc == 0), stop=(kc == 7),
                         r=[("wch", par, v)] + hq_keys, w=[bk[v]])
            rope_out(QT2[:, hp, 0:n], f.psU[par][0][:, 0:n], f.psU[par][1][:, 0:n],
                     bk, [("QT2", hp)])
        nS = 0
        for j in range(nt):
            gt = t0 + j
            for g in range(2):
                chunks = [(j, 0 if gt == first_real else 1), (j + 1, 2)]
                for ci, (slot, mi) in enumerate(chunks):
                    sp_ = nS % 2
                    nS += 1
                    for par in range(2):
                        pr = slice(par * 64, (par + 1) * 64)
                        p.mm(psS[sp_][:, par * 512:(par + 1) * 512],
                             KT2[pr, g, slot * 128:(slot + 1) * 128],
                             QT2[pr, 4 * g:4 * g + 4, j * 128:(j + 1) * 128],
                             start=True, stop=False,
                             r=[("KT2", g)] + [("QT2", 4 * g + i) for i in range(4)],
                             w=[psSk[sp_][par]])
                        p.mm(psS[sp_][:, par * 512:(par + 1) * 512], f.ident[:], msk[:, mi, :],
                             start=False, stop=True, r=["ident", "msk"], w=[psSk[sp_][par]])
                    p.act(PT[sp_][:], psS[sp_], AF.Exp, r=psSk[sp_], w=[("PT", sp_)], scale=0.125)
                    for par in range(2):
                        p.mm(psO[0:65, par * 512:(par + 1) * 512], VA[:, slot, g, :],
                             PT[sp_][:, par * 512:(par + 1) * 512],
                             start=(ci == 0), stop=(ci == 1),
                             r=[("PT", sp_), ("VA", slot), "VA"], w=[psOk[par]])
                p.tt(zr[64:65, :], psO[64:65, :], sexp[64:65, g * 1024:(g + 1) * 1024], ALU.add,
                     r=psOk + ["sexp"], w=["zr"])
                p.recip(rz[64:65, :], zr[64:65, :], r=["zr"], w=["rz"])
                p.cp(osb, psO[0:64, :], r=psOk, w=["hfin"], eng="act")
                for par in range(2):
                    p.mm(psB[0:64, :], ones[64:65, :], rz[64:65, par * 512:(par + 1) * 512],
                         start=True, stop=True, r=["ones", "rz"], w=psBk)
                    dst = f.oT[:, g * 8 + par * 4:g * 8 + par * 4 + 4, j * 128:(j + 1) * 128]
                    p.tt(dst, osb[:, par * 512:(par + 1) * 512].rearrange("p (h q) -> p h q", h=4),
                         psB[0:64, :].rearrange("p (h q) -> p h q", h=4), ALU.mult,
                         r=["hfin"] + psBk, w=["oT"])
        for g in range(2):
            p.cp(KT2[:, g, 0:128], KT2[:, g, n:n + 128], r=[("KT2", g)], w=[("KT2", g)], eng="pool")
        p.cp(VA[:, 0, :, :], VA[:, nt, :, :], r=[("VA", nt)], w=[("VA", 0)], eng="pool")
        skip = max(0, min(nt, n_skip_tiles - t0))
        o0 = max(0, t0 - n_skip_tiles)
        dst = h_out[o0 * 128:(o0 + nt - skip) * 128, :] if skip < nt else None
        f.run_supertile(nt, None, "resident", dst, n_skip_out=skip, final_norm=final_norm,
                        h1_preloaded=True, dkey=(io["dkey"] if fused else None))
        t0 += nt
    if fused:
        return None
    return p.finish()


def rope_tables(pos):
    half = 32
    inv = (np.float32(10000.0) ** (-np.arange(half, dtype=np.float32) / half)).astype(np.float32)
    ang = pos.astype(np.float32)[None, :] * inv[:, None]
    cos = np.cos(ang).astype(np.float32)
    sin = np.sin(ang).astype(np.float32)
    cos64 = np.concatenate([cos, cos], 0)
    sin64 = np.concatenate([-sin, sin], 0)
    return (np.ascontiguousarray(np.concatenate([cos64, cos64], 0)),
            np.ascontiguousarray(np.concatenate([sin64, sin64], 0)))


def swa_masks(first_exists):
    i = np.arange(128)[:, None]
    q = np.arange(128)[None, :]
    prev = np.where(i > q, 0.0, MASKV).astype(np.float32)
    cur = np.where(i <= q, 0.0, MASKV).astype(np.float32)
    pf = prev if first_exists else np.full((128, 128), MASKV, np.float32)
    m = np.stack([np.tile(pf, (1, 4)), np.tile(prev, (1, 4)), np.tile(cur, (1, 4))], 0)
    return m.astype(NPBF16)


def sinks_row(sinks16):
    ho = c_head_order()
    return np.ascontiguousarray(
        np.repeat(np.asarray(sinks16, np.float32)[ho], 128)[None, :])


def gT_np(g):
    return np.ascontiguousarray(np.asarray(g, np.float32).reshape(8, 128).T)


FORCE = 1.0e6
TINY = 1.0e-30


def build_A(S, dbg=99, p=None, io=None):
    fused = p is not None
    if not fused:
        nc = bass.Bass("TRN2", target_bir_lowering=False)
        p = Prog(nc)
    nc = p.nc
    NST = S // 512
    NQB = S // 128
    NCC = max(1, S // 2048)
    if not fused:
        h_in = p.din("h_in", [S, D], F32)
    g_attn = p.din("g_attn", [128, 8], F32)
    wq_d = p.din("wq", [D, 256], F32)
    wk3_d = p.din("wk3", [D, 192], F32)
    wv3_d = p.din("wv3", [D, 192], F32)
    wg_d = p.din("wg", [D, 12], F32)
    w1_d = p.din("w1", [2, 2048, 256], F32)
    w2_d = p.din("w2", [2, 256, 64], F32)
    posT_d = p.din("posT", [64, 2, 32], F32)
    cos_d = p.din("cos_t", [128, S], F32)
    sin_d = p.din("sin_t", [128, S], F32)
    ccos_d = p.din("ccos_t", [128, NCC * 128], F32)
    csin_d = p.din("csin_t", [128, NCC * 128], F32)
    pmask_d = p.din("pmask", [2, 16, 128, 128], BF16)
    r0mask_d = p.din("r0mask", [128, 512], BF16)
    cmask_d = p.din("cmask", [2, 128, 512], BF16)
    emat_d = p.din("emat", [64, 128, 128], BF16)
    wfull_d = p.din("wfull", [NCC * 128, 257], BF16)
    fix_d = p.din("fix3", [128, 6], F32)
    ident_d = p.din("ident", [128, 128], BF16)
    gscr = [nc.dram_tensor(f"gscr{i}" + p.sfx, [1, 12 * 512], F32).ap() for i in range(2)]
    if not fused:
        oT_out = p.dout("oT_out", [64, 4, S], BF16)

    ident = p.sb("ident", [128, 128], BF16)
    gT = p.sb("gT", [128, 8], F32)
    fst = [p.sb(f"fst{i}", [128, 1024], F32) for i in range(2)]
    WQ = p.sb("WQ", [128, 8, 2, 256], BF16)
    WKS = p.sb("WKS", [128, 8, 2, 128], BF16)
    WKW = p.sb("WKW", [128, 8, 2, 128], BF16)
    WKC = p.sb("WKC", [128, 8, 64], BF16)
    WVC = p.sb("WVC", [128, 8, 64], BF16)
    WV2 = p.sb("WV2", [128, 8, 128], BF16)
    WG = p.sb("WG", [128, 8, 12], BF16)
    W1c = [p.sb(f"W1c{i}", [64, 4, 256], BF16) for i in range(2)]
    W2K = p.sb("W2K", [128, 2, 2, 128], BF16)
    W2V = p.sb("W2V", [128, 2, 64], BF16)
    posT = p.sb("posT", [64, 2, 32], BF16)
    c1 = p.sb("c1", [128, 4], F32)
    ccos = p.sb("ccos", [128, NCC * 128], F32)
    csin = p.sb("csin", [128, NCC * 128], F32)
    pmask = p.sb("pmask", [128, 2, 16, 128], BF16)
    r0mask = p.sb("r0mask", [128, 512], BF16)
    cmask = p.sb("cmask", [128, 2, 512], BF16)
    emat = p.sb("emat", [128, 64, 128], BF16)
    wfull = p.sb("wfull", [128, NCC, 257], BF16)
    fix3 = p.sb("fix3", [128, 6], F32)
    hbuf = [p.sb("hbuf0", [128, 1024], F32)] * 2
    sq = p.sb("sq", [128, 1024], BF16)
    ss = p.sb("ss", [128, 1], F32)
    sd = p.sb("sd", [128, 1], F32)
    rs = p.sb("rs", [128, 1], F32)
    hn = p.sb("hn", [128, 1024], BF16)
    hnT = p.sb("hnT", [128, 8, 512], BF16)
    cos_sb = p.sb("cos_sb", [128, 512], F32)
    sin_sb = p.sb("sin_sb", [128, 512], F32)
    t1 = p.sb("t1", [128, 512], F32)
    t2 = p.sb("t2", [128, 512], F32)
    Qblk = p.sb("Qblk", [128, 4, 512], BF16)
    KsT2 = p.sb("KsT2", [128, S], BF16)
    VsA = p.sb("VsA", [128, NQB, 65], BF16)
    KwT2 = p.sb("KwT2", [128, 1024], BF16)
    VwA = p.sb("VwA", [128, 8, 65], BF16)
    KcT2 = p.sb("KcT2", [128, NCC * 128], BF16)
    VcA = p.sb("VcA", [128, NCC, 65], BF16)
    xT = [p.sb(f"xT{i}", [64, 528], BF16) for i in range(2)]
    hidK = p.sb("hidK", [128, 2, 32], BF16)
    hidV = p.sb("hidV", [128, 2, 128], BF16)
    gx = [p.sb(f"gx{i}", [128, 32], F32) for i in range(3)]
    gsb = p.sb("gsb", [12, 512], F32)
    G64b = [p.sb(f"G64b{i}", [128, 12 * 128], F32) for i in range(2)]
    PT = [p.sb(f"PT{i}", [128, 512], BF16) for i in range(4)]
    PcT = p.sb("PcT", [128, NCC, 512], BF16)
    zr = p.sb("zr", [128, 512], F32)
    Rr = p.sb("Rr", [128, 512], F32)
    ones = p.sb("ones", [128, 64], F32)
    osb = p.sb("osb", [64, 512], F32)
    acc = p.sb("acc", [64, 512], F32)
    tmpo = p.sb("tmpo", [64, 512], F32)
    oacc = p.sb("oacc", [64, 4, 128], BF16)
    imp = p.sb("imp", [128, 256], F32)
    selbuf = p.sb("selbuf", [128, 256], F32)
    work = p.sb("work", [128, 256], F32)
    mx8 = p.sb("mx8", [128, 8], F32)
    thr = p.sb("thr", [128, 1], F32)
    zq = p.sb("zq", [128, 1], F32)
    Bq = p.sb("Bq", [128, 256], BF16)
    BT = p.sb("BT", [128, 2, 512], BF16)
    psum = p.ps("psum", [128, 7 * 512])
    psT = p.ps("psT", [128, 8, 128], BF16)
    zero_b = p.sb("zero_b", [128, 512], BF16)
    p.memset(zero_b[:], 0.0, w=["zero_b"])
    p.memset(Qblk[:], 0.0, w=["QT2"])

    def bank(i, n=1):
        return psum[:, i * 512:(i + n) * 512]

    def bk(i):
        return ("bank", i)

    p.dma(ident[:], ident_d, w=["ident"])
    p.dma(gT[:], g_attn, w=["gT"])
    p.dma(ccos[:], ccos_d, w=["ccos"])
    p.dma(csin[:], csin_d, w=["csin"])
    for a_ in range(2):
        for r4 in range(0, 16, 4):
            p.dma(pmask[:, a_, r4:r4 + 4, :], pmask_d[a_, r4:r4 + 4].rearrange("r p c -> p r c"),
                  w=["pmask"])
    p.dma(r0mask[:], r0mask_d, w=["r0mask"])
    p.dma(cmask[:], cmask_d.rearrange("a p c -> p a c"), w=["cmask"])
    for e8 in range(0, 64, 8):
        p.dma(emat[:, e8:e8 + 8, :], emat_d[e8:e8 + 8].rearrange("e p c -> p e c"), w=["emat"])
    p.dma(wfull[:], wfull_d.rearrange("(c p) f -> p c f", p=128), w=["wfull"])
    p.dma(fix3[:], fix_d, w=["fix3"])
    p.memset(ones[:], 1.0, w=["ones"])
    p.memset(VsA[:], 1.0, w=["VsA"])
    p.memset(VwA[:], 1.0, w=["VwA"])
    p.memset(VcA[:], 1.0, w=["VcA"])
    p.memset(KwT2[:], 0.0, w=["KwT2"])
    p.memset(KcT2[:], 0.0, w=["KcT2"])
    p.memset(selbuf[:], -FORCE, w=["selbuf"])
    p.memset(hidV[:], 0.0, w=["hidV"])
    for i in range(2):
        p.memset(xT[i][:], 0.0, w=[("xT", i)])
    nst_ = [0]

    def stage_load(dst_fn, src_ap, ncols, parts=128):
        i = nst_[0] % 2
        nst_[0] += 1
        k = ("fst", i)
        p.dma(fst[i][0:parts, 0:ncols], src_ap, w=[k])
        return fst[i], k

    def swapcopy(dst, src, r, w):
        d4 = dst.rearrange("p (h d) -> p h d", d=64)
        s4 = src.rearrange("p (h d) -> p h d", d=64)
        p.cp(d4[:, :, 0:32], s4[:, :, 32:64], r=r, w=w, eng="pool")
        p.cp(d4[:, :, 32:64], s4[:, :, 0:32], r=r, w=w, eng="pool")

    for kc in range(8):
        gs = gT[:, kc:kc + 1]
        rows = slice(kc * 128, (kc + 1) * 128)
        st_, k = stage_load(None, wq_d[rows, :], 256)
        p.ts(WQ[:, kc, 0, :], st_[:, 0:256], gs, None, ALU.mult, r=[k, "gT"], w=["WQ"], eng="pool")
        swapcopy(WQ[:, kc, 1, :], WQ[:, kc, 0, :], ["WQ"], ["WQ"])
        st_, k = stage_load(None, wk3_d[rows, :], 192)
        p.ts(WKC[:, kc, :], st_[:, 0:64], gs, None, ALU.mult, r=[k, "gT"], w=["WKC"], eng="pool")
        for (W_, c0) in ((WKS, 64), (WKW, 128)):
            for dup in range(2):
                p.ts(W_[:, kc, 0, dup * 64:(dup + 1) * 64], st_[:, c0:c0 + 64], gs, None, ALU.mult,
                     r=[k, "gT"], w=["WK"], eng="pool")
            swapcopy(W_[:, kc, 1, :], W_[:, kc, 0, :], ["WK"], ["WK"])
        st_, k = stage_load(None, wv3_d[rows, :], 192)
        p.ts(WVC[:, kc, :], st_[:, 0:64], gs, None, ALU.mult, r=[k, "gT"], w=["WVC"], eng="pool")
        p.ts(WV2[:, kc, :], st_[:, 64:192], gs, None, ALU.mult, r=[k, "gT"], w=["WV2"], eng="pool")
        st_, k = stage_load(None, wg_d[rows, :], 12)
        p.ts(WG[:, kc, :], st_[:, 0:12], gs, None, ALU.mult, r=[k, "gT"], w=["WG"], eng="pool")
    nW1 = [0]

    def w1_piece(kv, l0):
        i = nW1[0] % 2
        nW1[0] += 1
        k = ("fst", i)
        p.dma(fst[i][0:64, :].rearrange("p (l m) -> p l m", l=4),
              w1_d[kv, l0 * 64:(l0 + 4) * 64, :].rearrange("(l d) m -> d l m", d=64), w=[k])
        p.cp(W1c[i][:], fst[i][0:64, :].rearrange("p (l m) -> p l m", l=4),
             r=[k], w=[("W1c", i)], eng="pool")
        return W1c[i], ("W1c", i)

    for kv in range(2):
        for mt in range(2):
            st_, k = stage_load(None, w2_d[kv, mt * 128:(mt + 1) * 128, :], 64)
            if kv == 0:
                for dup in range(2):
                    p.cp(W2K[:, mt, 0, dup * 64:(dup + 1) * 64], st_[:, 0:64], r=[k], w=["W2K"],
                         eng="pool")
                swapcopy(W2K[:, mt, 1, :], W2K[:, mt, 0, :], ["W2K"], ["W2K"])
            else:
                p.cp(W2V[:, mt, :], st_[:, 0:64], r=[k], w=["W2V"], eng="pool")
    st_, k = stage_load(None, posT_d.rearrange("d a l -> d (a l)"), 64, parts=64)
    p.cp(posT[:].rearrange("d a l -> d (a l)"), st_[0:64, 0:64], r=[k], w=["posT"], eng="pool")
    for kv in range(2):
        for l0 in range(0, 32, 4):
            wt, wk_ = w1_piece(kv, l0)
            for mt in range(2):
                col = kv * 2 + mt
                bb = 1 if mt == 0 else 6
                for li in range(4):
                    l = l0 + li
                    p.mm(bank(bb)[:, col:col + 1], wt[:, li, mt * 128:(mt + 1) * 128],
                         posT[:, kv, l:l + 1], start=(l == 0), stop=(l == 31),
                         r=[wk_, "posT"], w=[bk(bb)])
    p.cp(c1[:, 0:1], bank(1)[:, 0:1], r=[bk(1)], w=["c1"], eng="act")
    p.cp(c1[:, 2:3], bank(1)[:, 2:3], r=[bk(1)], w=["c1"], eng="act")
    p.cp(c1[:, 1:2], bank(6)[:, 1:2], r=[bk(6)], w=["c1"], eng="act")
    p.cp(c1[:, 3:4], bank(6)[:, 3:4], r=[bk(6)], w=["c1"], eng="act")

    if dbg == 0:
        return p.finish()
    if fused:
        for cb in range(4):
            p.dma(io["o_zero"][:, cb, :], zero_b[0:64, 0:128], r=["zero_b"], w=[io["dkey"]],
                  q="pool")
    scr = (sq[:], ss[:], sd[:], rs[:])

    def rope_out(dst, psn, pss, cs, sn, rk, wk, n):
        p.tt(t1[:, 0:n], psn, cs, ALU.mult, r=rk[0:1] + ["cos", "ccos"], w=["t1"])
        p.tt(t2[:, 0:n], pss, sn, ALU.mult, r=rk[1:2] + ["sin", "csin"], w=["t2"])
        if isinstance(dst, tuple):
            p.tt(dst[0], t1[0:64, 0:n], t2[0:64, 0:n], ALU.add, r=["t1", "t2"], w=wk)
            p.tt(dst[1], t1[64:128, 0:n], t2[64:128, 0:n], ALU.add, r=["t1", "t2"], w=wk)
        else:
            p.tt(dst, t1[:, 0:n], t2[:, 0:n], ALU.add, r=["t1", "t2"], w=wk)

    nU = [0]
    nH = [0]

    def proj_pair(W_, dst, cs, sn, wkey, rkey):
        par = nU[0] % 2
        nU[0] += 1
        b0, b1 = 2 + 2 * par, 3 + 2 * par
        for v, b in ((0, b0), (1, b1)):
            for kc in range(8):
                p.mm(bank(b), W_(kc, v), hnT[:, kc, :], start=(kc == 0), stop=(kc == 7),
                     r=[rkey, "hnT"], w=[bk(b)])
        rope_out(dst, bank(b0), bank(b1), cs, sn, [bk(b0), bk(b1)], wkey, 512)

    def gelu_to(dst, ps_ap, bias_ap, rk, wk):
        x, a, b = gx[0][:], gx[1][:], gx[2][:]
        p.act(x, ps_ap, AF.Identity, r=rk + ["c1"], w=["gx0"], bias=bias_ap)
        p.tt(a, x, x, ALU.mult, r=["gx0"], w=["gx1"])
        p.ts(a, a, 0.044715, 1.0, ALU.mult, ALU.add, r=["gx1"], w=["gx1"])
        p.tt(a, a, x, ALU.mult, r=["gx1", "gx0"], w=["gx1"])
        p.act(b, a, AF.Sigmoid, r=["gx1"], w=["gx2"], scale=1.5957691216057308)
        p.tt(dst, x, b, ALU.mult, r=["gx0", "gx2"], w=wk)

    nS = [0]
    sdepth = [2]

    def attn_chunk(kT2, kcols, vaug, biases, first, last, n_extra_r):
        sp_ = nS[0] % sdepth[0]
        nS[0] += 1
        sb_ = 2 + sp_
        if ZERO_BIAS and len(biases) == 0:
            biases = [(ident[:], zero_b[:], ["ident", "zero_b"])]
        out = bank(sb_)
        p.mm(out, kT2[:, kcols], Qblk[:, :, qsl[0]], start=True, stop=(len(biases) == 0),
             r=n_extra_r + ["QT2"], w=[bk(sb_)])
        for bi, bias in enumerate(biases):
            lh, rh, rk = bias[0:3]
            if len(bias) == 4:
                for hh in range(4):
                    p.mm(out[:, hh * 128:(hh + 1) * 128], lh, rh, start=False,
                         stop=(bi == len(biases) - 1), r=rk, w=[bk(sb_)])
            else:
                p.mm(out, lh, rh, start=False, stop=(bi == len(biases) - 1), r=rk, w=[bk(sb_)])
        return sp_, sb_

    qsl = [None]
    for st in range(NST):
        tok0 = st * 512
        for j in range(4):
            hb = hbuf[j % 2]
            hk = ("hbuf", 0)
            if fused:
                r0 = io["h_row"](tok0 + j * 128)
                p.dma(hb[:], io["h_ap"][r0:r0 + 128, :], r=list(io["rkeys"]), w=[hk])
            else:
                p.dma(hb[:], h_in[tok0 + j * 128:tok0 + (j + 1) * 128, :], w=[hk])
            rmsnorm_tile(p, hb[:], None, hn[:], scr, [hk], ["hn"], "a")
            for kc in range(8):
                p.tr(psT[:, kc, :], hn[:, kc * 128:(kc + 1) * 128], ident[:],
                     r=["hn", "ident"], w=["psT"])
            p.cp(hnT[:, :, j * 128:(j + 1) * 128], psT[:], r=["psT"], w=["hnT"], eng="act")
        p.dma(cos_sb[:], cos_d[:, tok0:tok0 + 512], w=["cos"])
        p.dma(sin_sb[:], sin_d[:, tok0:tok0 + 512], w=["sin"])
        for hp in range(2):
            proj_pair(lambda kc, v, hp=hp: WQ[:, kc, v, hp * 128:(hp + 1) * 128],
                      (Qblk[0:64, hp, :], Qblk[64:128, 2 + hp, :]), cos_sb[:], sin_sb[:],
                      ["QT2"], "WQ")
        proj_pair(lambda kc, v: WKS[:, kc, v, :], KsT2[:, tok0:tok0 + 512], cos_sb[:], sin_sb[:],
                  ["KsT2"], "WK")
        proj_pair(lambda kc, v: WKW[:, kc, v, :], KwT2[:, 512:1024], cos_sb[:], sin_sb[:],
                  ["KwT2"], "WK")
        for i, W_ in enumerate((WKC, WVC)):
            for kc in range(8):
                p.mm(bank(1)[0:64, :], W_[:, kc, :], hnT[:, kc, :], start=(kc == 0), stop=(kc == 7),
                     r=["WKC", "WVC", "hnT"], w=[bk(1)])
            p.cp(xT[i][:, 16:528], bank(1)[0:64, :], r=[bk(1)], w=[("xT", i)], eng="act")
        for j in range(4):
            for kc in range(8):
                p.mm(bank(0)[:, 0:128], hnT[:, kc, j * 128:(j + 1) * 128], WV2[:, kc, :],
                     start=(kc == 0), stop=(kc == 7), r=["hnT", "WV2"], w=[bk(0)])
            p.cp(VsA[:, st * 4 + j, 0:64], bank(0)[:, 0:64], r=[bk(0), "VsA"], w=["VsA"], eng="act")
            p.cp(VwA[:, 4 + j, 0:64], bank(0)[:, 64:128], r=[bk(0), "VwA"], w=["VwA"], eng="act")
        for kc in range(8):
            p.mm(bank(1)[0:12, :], WG[:, kc, :], hnT[:, kc, :], start=(kc == 0), stop=(kc == 7),
                 r=["WG", "hnT"], w=[bk(1)])
        p.act(gsb[:], bank(1)[0:12, :], AF.Sigmoid, r=[bk(1)], w=["gsb"])
        p.dma(gscr[st % 2].rearrange("o (a b) -> (o a) b", a=12), gsb[:], r=["gsb"],
              w=[("gscr", st % 2)])
        if dbg == 1:
            return p.finish()
        for kv in range(2):
            x3 = xT[kv][:].rearrange("p (i s) -> p i s", s=16)
            for l0 in range(0, 32, 4):
                wt, wk_ = w1_piece(kv, l0)
                for mt in range(2):
                    bb = 1 if mt == 0 else 6
                    for li in range(4):
                        l = l0 + li
                        rhs = x3[:, 0:32, l] if l < 16 else x3[:, 1:33, l - 16]
                        p.mm(bank(bb)[:, 0:32], wt[:, li, mt * 128:(mt + 1) * 128], rhs,
                             start=(l == 0), stop=(l == 31), r=[wk_, ("xT", kv)], w=[bk(bb)])
            for mt in range(2):
                bb = 1 if mt == 0 else 6
                if kv == 0:
                    gelu_to(hidK[:, mt, :], bank(bb)[:, 0:32], c1[:, mt:mt + 1], [bk(bb)], ["hidK"])
                else:
                    if st % 4 == 0 and mt == 0:
                        p.memset(hidV[:], 0.0, w=["hidV"])
                    gelu_to(hidV[:, mt, (st % 4) * 32:(st % 4) * 32 + 32], bank(bb)[:, 0:32],
                            c1[:, 2 + mt:3 + mt], [bk(bb)], ["hidV"])
            if kv == 0:
                par = nU[0] % 2
                nU[0] += 1
                b0, b1 = 2 + 2 * par, 3 + 2 * par
                for v, b in ((0, b0), (1, b1)):
                    for mt in range(2):
                        p.mm(bank(b)[:, 0:32], W2K[:, mt, v, :], hidK[:, mt, :],
                             start=(mt == 0), stop=(mt == 1), r=["W2K", "hidK"], w=[bk(b)])
                sl = slice(st * 32, st * 32 + 32)
                rope_out(KcT2[:, sl], bank(b0)[:, 0:32], bank(b1)[:, 0:32], ccos[:, sl], csin[:, sl],
                         [bk(b0), bk(b1)], ["KcT2"], 32)
            else:
                for mt in range(2):
                    p.mm(bank(1)[:, 0:64], hidV[:, mt, :], W2V[:, mt, :],
                         start=(mt == 0), stop=(mt == 1), r=["W2V", "hidV"], w=[bk(1)])
                p.cp(VcA[:, st // 4, 0:64], bank(1)[:, 0:64], r=[bk(1), "VcA"], w=["VcA"], eng="act")
            p.cp(xT[kv][:, 0:16], xT[kv][:, 512:528], r=[("xT", kv)], w=[("xT", kv)], eng="pool")
        if dbg == 2:
            return p.finish()
        for j in range(4):
            qb = st * 4 + j
            qsl[0] = slice(j * 128, (j + 1) * 128)
            tsl = qsl[0]
            p.dma(G64b[j % 2][64:65, :].rearrange("p (a b) -> p a b", a=12),
                  gscr[st % 2].rearrange("o (a b) -> o a b", a=12)[:, :, tsl],
                  r=[("gscr", st % 2)], w=[("G64", j % 2)])

            def finish_branch(br, first):
                p.ts(zr[64:65, :], bank(0)[64:65, :], TINY, None, ALU.max, r=[bk(0)], w=["zr"])
                p.recip(zr[64:65, :], zr[64:65, :], r=["zr"], w=["zr"])
                g3 = G64b[j % 2][64:65, :].rearrange("p (h b t) -> p h b t", h=4, b=3)
                for par in range(2):
                    for hpl in range(2):
                        hl = 2 * hpl + par
                        c0 = (par * 2 + hpl) * 128
                        p.tt(Rr[64:65, c0:c0 + 128], zr[64:65, c0:c0 + 128], g3[:, hl, br, :],
                             ALU.mult, r=["zr", ("G64", j % 2)], w=["Rr"])
                p.cp(osb[:], bank(0)[0:64, :], r=[bk(0)], w=["osb"], eng="act")

                def part2(first=first):
                    p.mm(bank(1)[0:64, :], ones[64:65, :], Rr[64:65, :], start=True, stop=True,
                         r=["ones", "Rr"], w=[bk(1)])
                    if first:
                        p.tt(acc[:], osb[:], bank(1)[0:64, :], ALU.mult, r=["osb", bk(1)], w=["acc"])
                    else:
                        p.tt(tmpo[:], osb[:], bank(1)[0:64, :], ALU.mult, r=["osb", bk(1)],
                             w=["tmpo"])
                        p.tt(acc[:], acc[:], tmpo[:], ALU.add, r=["tmpo", "acc"], w=["acc"])
                return part2

            pend = [None]
            ncc = qb // 16 + 1
            r_ = qb % 16
            for cc in range(ncc):
                biases = []
                lastc = (cc == ncc - 1)
                if lastc:
                    biases.append((ident[:], pmask[:, 1 if cc == 0 else 0, r_, :], ["ident", "pmask"], 128))
                elif cc == 0:
                    biases.append((ident[:], r0mask[:], ["ident", "r0mask"]))
                sp_, sb_ = attn_chunk(KcT2, slice(cc * 128, (cc + 1) * 128), None, biases,
                                      cc == 0, lastc, ["KcT2"])
                p.act(PcT[:, cc, :], bank(sb_), AF.Exp, r=[bk(sb_)], w=[("PcT", cc)], scale=0.125)
                if pend[0] is not None:
                    pend[0]()
                pend[0] = (lambda cc=cc, lastc=lastc: p.mm(
                    bank(0)[0:65, :], VcA[:, cc, :], PcT[:, cc, :], start=(cc == 0), stop=lastc,
                    r=[("PcT", cc), "VcA"], w=[bk(0)]))
            pend[0]()
            pend[0] = None
            for par in range(2):
                for hpl in range(2):
                    hi = par * 2 + hpl
                    c0 = hi * 128
                    ib = 4 + (hi % 2)
                    for cc in range(ncc):
                        p.mm(bank(ib)[:, 0:257], PcT[:, cc, c0:c0 + 128], wfull[:, cc, :],
                             start=(cc == 0), stop=(cc == ncc - 1),
                             r=[("PcT", cc), "wfull"], w=[bk(ib)])
                    p.ts(zq[:], bank(ib)[:, 256:257], TINY, None, ALU.max, r=[bk(ib)], w=["zq"])
                    p.recip(zq[:], zq[:], r=["zq"], w=["zq"])
                    if hi == 0:
                        p.ts(imp[:], bank(ib)[:, 0:256], zq[:], None, ALU.mult, r=[bk(ib), "zq"],
                             w=["imp"])
                    else:
                        p.stt(imp[:], bank(ib)[:, 0:256], zq[:], imp[:], ALU.mult, ALU.add,
                              r=[bk(ib), "zq", "imp"], w=["imp"])
            fin0 = finish_branch(0, True)
            if dbg == 3 or dbg == 100 + j * 10 + 3:
                return p.finish()
            nb = 2 * qb + 2
            p.cp(selbuf[:, 0:nb], imp[:, 0:nb], r=["imp"], w=["selbuf"])
            lo = 2 * qb - 1
            k0 = 0
            if lo < 0:
                lo, k0 = 0, 1
            nfx = 3 - k0
            p.tt(selbuf[:, lo:lo + nfx], selbuf[:, lo:lo + nfx], fix3[:, k0:3], ALU.mult,
                 r=["selbuf", "fix3"], w=["selbuf"])
            p.tt(selbuf[:, lo:lo + nfx], selbuf[:, lo:lo + nfx], fix3[:, 3 + k0:6], ALU.add,
                 r=["selbuf", "fix3"], w=["selbuf"])
            p.memset(selbuf[:, 0:1], 3.0 * FORCE, w=["selbuf"])
            p.s.op("dve", lambda e: e.max(out=mx8[:], in_=selbuf[:]), ["selbuf"], ["mx8"])
            p.s.op("dve", lambda e: e.match_replace(out=work[:], in_to_replace=mx8[:],
                                                    in_values=selbuf[:], imm_value=-2.0 * FORCE),
                   ["selbuf", "mx8"], ["work"])
            p.s.op("dve", lambda e: e.max(out=mx8[:], in_=work[:]), ["work"], ["mx8"])
            p.s.op("dve", lambda e: e.tensor_reduce(out=thr[:], in_=mx8[:], axis=AX.X, op=ALU.min),
                   ["mx8"], ["thr"])
            p.ts(Bq[:], selbuf[:], thr[:], 1.0, ALU.is_ge, ALU.subtract, r=["selbuf", "thr"], w=["Bq"])
            nhalf = 1 if nb <= 128 else 2
            for hf in range(nhalf):
                p.tr(psT[:, hf, :], Bq[:, hf * 128:(hf + 1) * 128], ident[:], r=["Bq", "ident"],
                     w=["psT"])
            for hf in range(nhalf):
                for rep in range(4):
                    p.cp(BT[:, hf, rep * 128:(rep + 1) * 128], psT[:, hf, :], r=["psT"], w=["BT"],
                         eng=("act" if rep % 2 == 0 else "dve"))
            if dbg == 5 or dbg == 100 + j * 10 + 5:
                return p.finish()
            k_lo = max(0, qb - 4)
            sdepth[0] = 4
            for kc in range(k_lo, qb + 1):
                biases = []
                if kc == qb - 4:
                    biases.append((ident[:], cmask[:, 1, :], ["ident", "cmask"]))
                if kc == qb:
                    biases.append((ident[:], cmask[:, 0, :], ["ident", "cmask"]))
                slot = 4 + j - (qb - kc)
                sp_, sb_ = attn_chunk(KwT2, slice(slot * 128, (slot + 1) * 128), None, biases,
                                      kc == k_lo, kc == qb, ["KwT2"])
                p.act(PT[sp_][:], bank(sb_), AF.Exp, r=[bk(sb_)], w=[("PT", sp_)], scale=0.125)
                if pend[0] is not None:
                    pend[0]()
                if kc == min(k_lo + 1, qb) and fin0 is not None:
                    fin0()
                    fin0 = None
                pend[0] = (lambda kc=kc, sp_=sp_, slot=slot: p.mm(
                    bank(0)[0:65, :], VwA[:, slot, :], PT[sp_][:], start=(kc == k_lo),
                    stop=(kc == qb), r=[("PT", sp_), "VwA"], w=[bk(0)]))
            pend[0]()
            pend[0] = None
            fin2 = finish_branch(2, False)
            if fin0 is not None:
                fin0()
                fin0 = None
            if dbg == 4 or dbg == 100 + j * 10 + 4:
                return p.finish()
            sdepth[0] = 4
            for kc in range(qb + 1):
                biases = [(emat[:, kc % 64, :], BT[:, kc // 64, :], ["emat", "BT"])]
                if kc == qb:
                    biases.append((ident[:], cmask[:, 0, :], ["ident", "cmask"]))
                sp_, sb_ = attn_chunk(KsT2, slice(kc * 128, (kc + 1) * 128), None, biases,
                                      kc == 0, kc == qb, ["KsT2"])
                p.act(PT[sp_][:], bank(sb_), AF.Exp, r=[bk(sb_)], w=[("PT", sp_)], scale=0.125)
                if pend[0] is not None:
                    pend[0]()
                if kc == min(1, qb) and fin2 is not None:
                    fin2()
                    fin2 = None
                pend[0] = (lambda kc=kc, sp_=sp_: p.mm(
                    bank(0)[0:65, :], VsA[:, kc, :], PT[sp_][:], start=(kc == 0), stop=(kc == qb),
                    r=[("PT", sp_), "VsA"], w=[bk(0)]))
            pend[0]()
            pend[0] = None
            fin1 = finish_branch(1, False)
            fin1()
            sdepth[0] = 2
            if dbg == 6 or dbg == 100 + j * 10 + 6:
                return p.finish()
            p.cp(oacc[:], acc[:].rearrange("p (c q) -> p c q", c=4), r=["acc"], w=["oacc"])
            if fused:
                for dst in io["o_dst"](qb):
                    p.dma(dst, oacc[:], r=["oacc"], w=[io["dkey"]], q="pool")
            else:
                p.dma(oT_out[:, :, tok0 + j * 128:tok0 + (j + 1) * 128], oacc[:], r=["oacc"],
                      q="pool", is_output=True)
            if dbg == 100 + j * 10 + 7:
                return p.finish()
        p.cp(KwT2[:, 0:512], KwT2[:, 512:1024], r=["KwT2"], w=["KwT2"], eng="pool")
        p.cp(VwA[:, 0:4, :], VwA[:, 4:8, :], r=["VwA"], w=["VwA"], eng="pool")
        if dbg == 7 + st:
            return p.finish()
    if fused:
        return None
    return p.finish()


def nsa_consts(S):
    NCC = max(1, S // 2048)
    ml = np.arange(128)[:, None]
    q = np.arange(128)[None, :]
    pm = np.zeros((2, 16, 128, 128), np.float32)
    for a in range(2):
        for r in range(16):
            valid = (16 * ml + 15 <= 128 * r + q)
            if a == 1:
                valid = valid & (ml >= 1)
            pm[a, r] = np.where(valid, 0.0, MASKV)
    pmask = pm.astype(NPBF16)
    r0 = np.zeros((128, 512), np.float32)
    r0[0, :] = MASKV
    cur = np.where(ml <= q, 0.0, MASKV).astype(np.float32)
    upper = np.where(ml > q, 0.0, MASKV).astype(np.float32)
    cmask = np.stack([np.tile(cur, (1, 4)), np.tile(upper, (1, 4))], 0).astype(NPBF16)
    emat = np.zeros((64, 128, 128), np.float32)
    for e in range(64):
        emat[e, 2 * e, 0:64] = -MASKV
        emat[e, 2 * e + 1, 64:128] = -MASKV
    ws = [1, 2, 2, 2, 1]
    wfull = np.zeros((NCC * 128, 257), np.float32)
    for m in range(1, NCC * 128):
        n = m - 1
        for j in range(256):
            i = n - 4 * j + 1
            if 0 <= i <= 4:
                wfull[m, j] = ws[i]
    wfull[:, 256] = 1.0
    fix = np.zeros((128, 6), np.float32)
    lo = np.arange(128) < 64
    fix[:, 0] = np.where(lo, 0.0, 1.0)
    fix[:, 3] = np.where(lo, FORCE, 0.0)
    fix[:, 4] = 2.0 * FORCE
    fix[:, 5] = np.where(lo, -FORCE, FORCE)
    cpos = 16 * np.arange(NCC * 128) + 15
    ccos, csin = rope_tables(cpos)
    return dict(pmask=pmask, r0mask=r0.astype(NPBF16), cmask=cmask, emat=emat.astype(NPBF16),
                wfull=wfull.astype(NPBF16), fix3=fix, ccos_t=ccos, csin_t=csin, ident=ident_np())


def nsa_weights(a_w_in_l, cmp_pos_l, g):
    W = a_w_in_l
    q0 = g * 256
    def kcol(i):
        return W[:, 1024 + i * 256 + g * 64: 1024 + i * 256 + (g + 1) * 64]
    kc_, vc_, ks_, vs_, kw_, vw_ = [kcol(i) for i in range(6)]
    wg = W[:, 1024 + 6 * 256 + g * 12: 1024 + 6 * 256 + (g + 1) * 12]
    return dict(wq=np.ascontiguousarray(W[:, q0:q0 + 256]),
                wk3=np.ascontiguousarray(np.concatenate([kc_, ks_, kw_], 1)),
                wv3=np.ascontiguousarray(np.concatenate([vc_, vs_, vw_], 1)),
                wg=np.ascontiguousarray(wg),
                posT=np.ascontiguousarray(np.transpose(cmp_pos_l, (2, 0, 1))))


SEQ = 16384
NB = 2
CH = 4096
NPHASE = 99


def _run(nc, in_maps):
    res = run_bass_kernel_spmd(nc, in_maps, core_ids=list(range(8)))
    return res.results


def _cwb(conv_w, conv_b):
    return np.ascontiguousarray(np.concatenate([conv_w, conv_b[None]], 0).T.astype(np.float32))


def _chunk_with_halo(x_b, c, halo):
    lo = c * CH - halo
    if lo >= 0:
        return np.ascontiguousarray(x_b[lo:(c + 1) * CH])
    pad = np.zeros((-lo,) + x_b.shape[1:], x_b.dtype)
    return np.ascontiguousarray(np.concatenate([pad, x_b[0:(c + 1) * CH]], 0))


def kernel_unfused(x, norm_attn, norm_ffn, a_w_in, a_cmp_pos, a_cmp_w1, a_cmp_w2, a_w_out, kv_norm,
           b_w_kv, b_w_q, b_sinks, b_w_out, ffn_w_in, ffn_conv_w, ffn_conv_b, ffn_w_out,
           final_norm):
    f32 = lambda a: np.ascontiguousarray(np.asarray(a, dtype=np.float32))
    x = f32(x)
    norm_attn, norm_ffn = f32(norm_attn), f32(norm_ffn)
    a_w_in, a_cmp_pos, a_cmp_w1, a_cmp_w2, a_w_out = map(f32, (a_w_in, a_cmp_pos, a_cmp_w1,
                                                                a_cmp_w2, a_w_out))
    kv_norm, b_w_kv, b_w_q, b_sinks, b_w_out = map(f32, (kv_norm, b_w_kv, b_w_q, b_sinks, b_w_out))
    ffn_w_in, ffn_conv_w, ffn_conv_b, ffn_w_out, final_norm = map(
        f32, (ffn_w_in, ffn_conv_w, ffn_conv_b, ffn_w_out, final_norm))
    h = x
    ident = ident_np()
    gfin = np.ascontiguousarray(final_norm[None, :])
    cosA, sinA = rope_tables(np.arange(SEQ))
    constsA = nsa_consts(SEQ)

    for l in range(2):
        ncA = build_A(SEQ)
        maps = []
        for i in range(8):
            b, g = divmod(i, 4)
            m = dict(h_in=h[b], g_attn=gT_np(norm_attn[l]), w1=a_cmp_w1[l], w2=a_cmp_w2[l],
                     cos_t=cosA, sin_t=sinA)
            m.update(constsA)
            m.update(nsa_weights(a_w_in[l], a_cmp_pos[l], g))
            maps.append(m)
        resA = _run(ncA, maps)
        oT_full = np.zeros((NB, 16, 64, SEQ), NPBF16)
        for i in range(8):
            b, g = divmod(i, 4)
            o = resA[i]["oT_out"]
            for par in range(2):
                for hpl in range(2):
                    oT_full[b, g * 4 + 2 * hpl + par] = o[:, par * 2 + hpl, :]
        oT_full = oT_full.reshape(NB, 1024, SEQ)
        ncB = build_B([1] + [4] * 8, 1)
        maps = []
        for i in range(8):
            b, c = divmod(i, 4)
            maps.append(dict(
                h_in=_chunk_with_halo(h[b], c, 128),
                oT_in=np.ascontiguousarray(_chunk_with_halo(oT_full[b].T, c, 128).T),
                w_o=a_w_out[l], w_in=ffn_w_in[l], w_out=ffn_w_out[l],
                cwb=_cwb(ffn_conv_w[l], ffn_conv_b[l]), g_ffn=gT_np(norm_ffn[l]), g_fin=gfin,
                ident=ident))
        resB = _run(ncB, maps)
        h = np.stack([np.concatenate([resB[b * 4 + c]["h_out"] for c in range(4)], 0)
                      for b in range(NB)], 0)

    hkv = h
    for l in range(2, 4):
        j = l - 2
        ncC = build_C([2] + [4] * 8, 2, final_norm=(l == 3))
        maps = []
        for i in range(8):
            b, c = divmod(i, 4)
            pos = c * CH - 256 + np.arange(CH + 256)
            cos_t, sin_t = rope_tables(pos)
            maps.append(dict(
                h_in=_chunk_with_halo(h[b], c, 256), hkv_in=_chunk_with_halo(hkv[b], c, 256),
                w_q=b_w_q[j], w_kv=b_w_kv, sinks_b=sinks_row(b_sinks[j]),
                g_attn=gT_np(norm_attn[l]), g_kv=gT_np(kv_norm), cos_t=cos_t, sin_t=sin_t,
                masks=swa_masks(c > 0), w_o=b_w_out[j], w_in=ffn_w_in[l], w_out=ffn_w_out[l],
                cwb=_cwb(ffn_conv_w[l], ffn_conv_b[l]), g_ffn=gT_np(norm_ffn[l]), g_fin=gfin,
                ident=ident))
        resC = _run(ncC, maps)
        h = np.stack([np.concatenate([resC[b * 4 + c]["h_out"] for c in range(4)], 0)
                      for b in range(NB)], 0)
    return np.ascontiguousarray(h.astype(np.float32))


def build_fused(nphase=99):
    from concourse.bass import ds
    nph = [0]

    def stop():
        nph[0] += 1
        return nph[0] >= nphase

    nc = bass.Bass("TRN2", target_bir_lowering=False)
    p = Prog(nc)
    S = SEQ
    WB = 128 + CH
    WC = 256 + CH
    SUBW = 11 * 128
    xA = p.din("xA", [S, D], F32)
    xB = p.din("xB", [WB, D], F32)
    flag = p.din("flag", [128, 1], F32)
    out = p.dout("out", [CH, D], F32)
    oTloc = [nc.dram_tensor(f"oTloc{l}", [12 * 64, 4 * SUBW], BF16) for l in range(2)]
    OTb = [nc.dram_tensor(f"OTb{l}", [12 * 256, 4 * SUBW], BF16) for l in range(2)]
    oTwin = nc.dram_tensor("oTwin", [3 * 256, 4 * SUBW], BF16).ap()
    hloc = [nc.dram_tensor(f"hloc{k}", [CH, D], F32) for k in range(3)]
    Hb = [nc.dram_tensor(f"Hb{k}", [S, D], F32) for k in range(3)]
    hwin = nc.dram_tensor("hwin", [WC, D], F32).ap()
    hkvwin = nc.dram_tensor("hkvwin", [WC, D], F32).ap()
    rg = [[0, 1, 2, 3], [4, 5, 6, 7]]
    PID = p.s.pid

    def gather_group(src, dst, nchunk, rows, rk, wk):
        def fn(e, sem):
            for k in range(nchunk):
                e.collective_compute(
                    "AllGather", ALU.bypass, replica_groups=rg,
                    ins=[src.ap()[k * rows:(k + 1) * rows, :].opt()],
                    outs=[dst.ap()[k * 4 * rows:(k + 1) * 4 * rows, :].opt()]).then_inc(sem)
        p.s.cc(fn, [rk], [wk], n=nchunk)

    def h_row(tok):
        rank, rem = divmod(tok, CH)
        k, r = divmod(rem, 256)
        return (k * 4 + rank) * 256 + r

    def win_copy(dst, src, halo, q, rk, wk):
        s5 = src.rearrange("(k g r e) d -> k g r (e d)", k=16, g=4, e=8)
        dm = dst[halo:halo + CH, :].rearrange("(k g r e) d -> k g r (e d)", k=16, g=1, e=8)
        dh = dst[0:halo, :].rearrange("(k g r e) d -> k g r (e d)", k=1, g=1, e=8)
        h8 = halo // 8
        p.dmaf(lambda e: e.dma_start(
            out=dm, in_=s5[:, ds(PID(e, "c", lambda pid: pid % 4), 1), :, :]),
            r=[rk], w=[wk], q=q)
        p.dmaf(lambda e: e.dma_start(
            out=dh, in_=s5[15:16, ds(PID(e, "cm1", lambda pid: (pid + 3) % 4), 1), 32 - h8:32, :]),
            r=[rk], w=[wk], q=q)

    for l in range(2):
        p.sfx = f"_A{l}"
        rk = [] if l == 0 else [f"Hb{l - 1}"]
        O5 = oTloc[l].ap().rearrange("(c s d) (b t) -> c s d b t", c=4, s=3, b=4)

        def o_dst(qb, O5=O5):
            c, sl = divmod(qb, 32)
            sl += 1
            dsts = [O5[c, sl // 11, :, :, (sl % 11) * 128:(sl % 11) * 128 + 128]]
            if sl == 32 and c < 3:
                dsts.append(O5[c + 1, 0, :, :, 0:128])
            return dsts

        build_A(S, p=p, io=dict(h_ap=(xA if l == 0 else Hb[l - 1].ap()),
                                h_row=((lambda t: t) if l == 0 else h_row),
                                rkeys=rk, dkey=f"oTloc{l}", o_dst=o_dst,
                                o_zero=O5[0, 0, :, :, 0:128]))
        p.phase_end()
        gather_group(oTloc[l], OTb[l], 12, 64, f"oTloc{l}", f"OTb{l}")
        if stop():
            return p.finish(), dict(p.dins)
        p.sfx = f"_B{l}"
        O3 = OTb[l].ap().rearrange("(c r) f -> c r f", c=4)
        p.dmaf(lambda e, O3=O3: e.dma_start(
            out=oTwin.rearrange("(c r) f -> c r f", c=1),
            in_=O3[ds(PID(e, "c", lambda pid: pid % 4), 1), :, :]),
            r=[f"OTb{l}"], w=["oTwin"], q="act")
        if l == 0:
            h_ap = xB
            rkb = ["oTwin"]
        else:
            win_copy(hwin[0:WB, :], Hb[l - 1].ap(), 128, "act", f"Hb{l - 1}", "hwin")
            h_ap = hwin[0:WB, :]
            rkb = ["oTwin", "hwin"]
        W5 = oTwin.rearrange("(s g d) (b t) -> d s g b t", s=3, g=4, b=4)

        def oT_ap(wt, g, W5=W5):
            return W5[:, wt // 11, g, :, (wt % 11) * 128:(wt % 11) * 128 + 128]

        build_B([1] + [4] * 8, 1, p=p,
                io=dict(h_ap=h_ap, oT_ap=oT_ap, h_dst=hloc[l].ap(), flag=flag,
                        rkeys=rkb, dkey=f"hloc{l}"))
        p.phase_end()
        gather_group(hloc[l], Hb[l], 16, 256, f"hloc{l}", f"Hb{l}")
        if stop():
            return p.finish(), dict(p.dins)

    for l in range(2, 4):
        p.sfx = f"_C{l}"
        last = (l == 3)
        if l == 2:
            win_copy(hkvwin, Hb[1].ap(), 256, "sp", "Hb1", "hkvwin")
            h_ap, hkv_ap, rkc = hkvwin, hkvwin, ["hkvwin"]
        else:
            win_copy(hwin, Hb[2].ap(), 256, "sp", "Hb2", "hwin")
            h_ap, hkv_ap, rkc = hwin, hkvwin, ["hwin", "hkvwin"]
        build_C([2] + [4] * 8, 2, final_norm=last, p=p,
                io=dict(h_ap=h_ap, hkv_ap=hkv_ap, h_dst=(out if last else hloc[2].ap()), flag=flag,
                        rkeys=rkc, dkey=(None if last else "hloc2")))
        if not last:
            p.phase_end()
            gather_group(hloc[2], Hb[2], 16, 256, "hloc2", "Hb2")
            if stop():
                return p.finish(), dict(p.dins)
    return p.finish(), dict(p.dins)


def kernel(x, norm_attn, norm_ffn, a_w_in, a_cmp_pos, a_cmp_w1, a_cmp_w2, a_w_out, kv_norm,
           b_w_kv, b_w_q, b_sinks, b_w_out, ffn_w_in, ffn_conv_w, ffn_conv_b, ffn_w_out,
           final_norm):
    f32 = lambda a: np.ascontiguousarray(np.asarray(a, dtype=np.float32))
    x = f32(x)
    norm_attn, norm_ffn = f32(norm_attn), f32(norm_ffn)
    a_w_in, a_cmp_pos, a_cmp_w1, a_cmp_w2, a_w_out = map(f32, (a_w_in, a_cmp_pos, a_cmp_w1,
                                                                a_cmp_w2, a_w_out))
    kv_norm, b_w_kv, b_w_q, b_sinks, b_w_out = map(f32, (kv_norm, b_w_kv, b_w_q, b_sinks, b_w_out))
    ffn_w_in, ffn_conv_w, ffn_conv_b, ffn_w_out, final_norm = map(
        f32, (ffn_w_in, ffn_conv_w, ffn_conv_b, ffn_w_out, final_norm))
    nc, dins = build_fused(NPHASE)
    ident = ident_np()
    gfin = np.ascontiguousarray(final_norm[None, :])
    cosA, sinA = rope_tables(np.arange(SEQ))
    constsA = nsa_consts(SEQ)
    maps = []
    for i in range(8):
        b, c = divmod(i, 4)
        g = c
        m = dict(xA=x[b], xB=_chunk_with_halo(x[b], c, 128),
                 flag=np.full((128, 1), 0.0 if c == 0 else 1.0, np.float32))
        for l in range(2):
            a = dict(g_attn=gT_np(norm_attn[l]), w1=a_cmp_w1[l], w2=a_cmp_w2[l],
                     cos_t=cosA, sin_t=sinA)
            a.update(constsA)
            a.update(nsa_weights(a_w_in[l], a_cmp_pos[l], g))
            for k, v in a.items():
                m[f"{k}_A{l}"] = v
            bb = dict(w_o=a_w_out[l], w_in=ffn_w_in[l], w_out=ffn_w_out[l],
                      cwb=_cwb(ffn_conv_w[l], ffn_conv_b[l]), g_ffn=gT_np(norm_ffn[l]), g_fin=gfin,
                      ident=ident)
            for k, v in bb.items():
                m[f"{k}_B{l}"] = v
        pos = c * CH - 256 + np.arange(CH + 256)
        cos_t, sin_t = rope_tables(pos)
        for l in range(2, 4):
            j = l - 2
            cc = dict(w_q=b_w_q[j], w_kv=b_w_kv, sinks_b=sinks_row(b_sinks[j]),
                      g_attn=gT_np(norm_attn[l]), g_kv=gT_np(kv_norm), cos_t=cos_t, sin_t=sin_t,
                      masks=swa_masks(c > 0), w_o=b_w_out[j], w_in=ffn_w_in[l], w_out=ffn_w_out[l],
                      cwb=_cwb(ffn_conv_w[l], ffn_conv_b[l]), g_ffn=gT_np(norm_ffn[l]), g_fin=gfin,
                      ident=ident)
            for k, v in cc.items():
                m[f"{k}_C{l}"] = v
        m = {k: v for k, v in m.items() if k in dins}
        maps.append(m)
    res = _run(nc, maps)
    h = np.stack([np.concatenate([res[b * 4 + c]["out"] for c in range(4)], 0)
                  for b in range(NB)], 0)
    return np.ascontiguousarray(h.astype(np.float32))
```

```python
import numpy as np
import ml_dtypes
import concourse.bass as bass
import concourse.mybir as mybir
from concourse.bass_utils import run_bass_kernel_spmd

F32 = mybir.dt.float32
BF16 = mybir.dt.bfloat16
AF = mybir.ActivationFunctionType
ALU = mybir.AluOpType
AX = mybir.AxisListType

NPBF16 = ml_dtypes.bfloat16

D = 1024
DFF = 2816
EPS = 1e-6
MASKV = -240000.0

COMPUTE = ("pe", "act", "dve", "pool")
EPOCH = 30000
NSLOT = 12
SAME_ENG_SYNC = True
ZERO_BIAS = True


class Sched:
    def __init__(self, nc):
        self.nc = nc
        self.streams = {e: [] for e in COMPUTE + ("sp",)}
        self.cnt = {e: 0 for e in COMPUTE}
        self.known = {e: {} for e in self.streams}
        self.known_dma = {e: set() for e in self.streams}
        self.last_w = {}
        self.readers = {}
        self.ndma = {e: 0 for e in self.streams}
        self.sems = {}
        self.nsem = 0
        self.out_dmas = []
        self.ncc = 0
        self.snap = {e: [] for e in COMPUTE}
        self.snapd = {}

    def _sem(self, name):
        if name not in self.sems:
            self.sems[name] = self.nc.alloc_semaphore(name=name)
        return self.sems[name]

    def _ev_wait_args(self, ev):
        kind = ev[0]
        if kind == "c":
            _, eng, idx = ev
            ep, off = divmod(idx, EPOCH)
            return self._sem(f"s_{eng}_{ep}"), off + 1
        elif kind == "x":
            return self._sem(f"x_{ev[1]}"), ev[2]
        else:
            _, q, j = ev
            slot, use = j % NSLOT, j // NSLOT
            return self._sem(f"d_{q}_{slot}"), 16 * (use + 1)

    def _deps(self, eng, reads, writes):
        deps = set()
        for k in reads:
            w = self.last_w.get(k)
            if w is not None:
                deps.add(w)
        for k in writes:
            w = self.last_w.get(k)
            if w is not None:
                deps.add(w)
            for r in self.readers.get(k, ()):
                deps.add(r)
        waits = []
        best = {}
        for ev in deps:
            if ev[0] == "c":
                _, src, idx = ev
                if src == eng and eng == "pe":
                    continue
                if self.known[eng].get(src, -1) >= idx:
                    continue
                if best.get(src, -1) < idx:
                    best[src] = idx
            else:
                if ev in self.known_dma[eng]:
                    continue
                waits.append(ev)
                self.known_dma[eng].add(ev)
        for src, idx in best.items():
            self.known[eng][src] = idx
            waits.append(("c", src, idx))
        for ev in list(waits):
            sn = self.snap[ev[1]][ev[2]] if ev[0] == "c" else self.snapd.get(ev)
            if sn is None:
                continue
            kn = self.known[eng]
            for ci, ce in enumerate(COMPUTE):
                if sn[ci] > kn.get(ce, -1) and (ce != eng or True):
                    kn[ce] = sn[ci]
        return waits

    def _snapshot(self, eng):
        kn = self.known[eng]
        return tuple(kn.get(ce, -1) for ce in COMPUTE)

    def _mark(self, ev, reads, writes):
        for k in reads:
            self.readers.setdefault(k, []).append(ev)
        for k in writes:
            self.last_w[k] = ev
            self.readers[k] = []

    def op(self, eng, fn, reads=(), writes=()):
        assert eng in COMPUTE
        waits = self._deps(eng, reads, writes)
        idx = self.cnt[eng]
        self.cnt[eng] += 1
        ev = ("c", eng, idx)
        if not SAME_ENG_SYNC or eng == "pe":
            self.known[eng][eng] = idx
        sn = list(self._snapshot(eng))
        sn[COMPUTE.index(eng)] = max(sn[COMPUTE.index(eng)], idx - 1)
        self.snap[eng].append(tuple(sn))
        self._mark(ev, reads, writes)
        self.streams[eng].append((waits, fn, ev))
        return ev

    def dma(self, q, fn, reads=(), writes=(), is_output=False):
        waits = self._deps(q, reads, writes)
        j = self.ndma[q]
        self.ndma[q] += 1
        if j >= NSLOT:
            prev = ("d", q, j - NSLOT)
            if prev not in self.known_dma[q]:
                waits.append(prev)
                self.known_dma[q].add(prev)
        ev = ("d", q, j)
        self.snapd[ev] = self._snapshot(q)
        self._mark(ev, reads, writes)
        self.streams[q].append((waits, fn, ev))
        if is_output:
            self.out_dmas.append(ev)
        return ev

    def pid(self, e, key="pid", fn=None):
        k = (self.cur_eng, key)
        if k not in self.pid_cache:
            if key == "pid":
                self.pid_cache[k] = e.partition_id()
            else:
                self.pid_cache[k] = e.snap(fn(self.pid(e)))
        return self.pid_cache[k]

    def cc(self, fn, reads=(), writes=(), n=1):
        waits = self._deps("pool", reads, writes)
        ev = ("x", self.ncc, n)
        self.ncc += 1
        self._mark(ev, reads, writes)
        self.streams["pool"].append((waits, fn, ev))
        self.known_dma["pool"].add(ev)
        idx = self.cnt["pool"]
        self.cnt["pool"] += 1
        nev = ("c", "pool", idx)
        self.snap["pool"].append(self._snapshot("pool"))
        self._mark(nev, (), writes)
        self.streams["pool"].append(([ev], lambda e: e.nop(), nev))
        return nev

    def barrier(self):
        evs = []
        for eng in COMPUTE:
            if self.cnt[eng] > 0:
                evs.append(("c", eng, self.cnt[eng] - 1))
        for q, n in self.ndma.items():
            for j in range(max(0, n - NSLOT), n):
                evs.append(("d", q, j))
        for eng in self.streams:
            waits = []
            for ev in evs:
                if ev[0] == "c":
                    if ev[1] == eng:
                        continue
                    if self.known[eng].get(ev[1], -1) >= ev[2]:
                        continue
                    self.known[eng][ev[1]] = ev[2]
                elif ev[0] == "x":
                    continue
                elif ev in self.known_dma[eng]:
                    continue
                else:
                    self.known_dma[eng].add(ev)
                waits.append(ev)
            self.streams[eng].append((waits, None, None))

    def emit(self, final=True):
        nc = self.nc
        final_waits = list(self.out_dmas) if final else []
        self.pid_cache = {}
        with nc.Block() as block:
            def run(engname, e):
                self.cur_eng = engname
                for waits, fn, ev in self.streams[engname]:
                    for w in waits:
                        s, v = self._ev_wait_args(w)
                        e.wait_ge(s, v)
                    if fn is None:
                        continue
                    s, v = self._ev_wait_args(ev)
                    if ev[0] == "x":
                        fn(e, s)
                        continue
                    ins = fn(e)
                    if ev[0] == "c":
                        ins.then_inc(s, 1)
                    else:
                        ins.then_inc(s, 16)
                if engname == "sp":
                    for w in final_waits:
                        s, v = self._ev_wait_args(w)
                        e.wait_ge(s, v)

            @block.tensor
            def _(e):
                run("pe", e)

            @block.scalar
            def _(e):
                run("act", e)

            @block.vector
            def _(e):
                run("dve", e)

            @block.gpsimd
            def _(e):
                run("pool", e)

            @block.sync
            def _(e):
                run("sp", e)
        for k in self.streams:
            self.streams[k] = []


class Prog:
    def __init__(self, nc):
        from contextlib import ExitStack
        self.nc = nc
        self.s = Sched(nc)
        self.es = ExitStack()
        self.ndram = 0
        self.sfx = ""
        self.dins = {}
        self.ext = {}

    def sb(self, name, shape, dt):
        return self.es.enter_context(self.nc.sbuf_tensor("sb_" + name + self.sfx, list(shape), dt))

    def ps(self, name, shape, dt=F32):
        return self.es.enter_context(self.nc.psum_tensor("ps_" + name + self.sfx, list(shape), dt))

    def din(self, name, shape, dt):
        nm = name + self.sfx
        if nm in self.ext:
            return self.ext[nm]
        self.dins[nm] = (tuple(shape), dt)
        return self.nc.dram_tensor(nm, list(shape), dt, kind="ExternalInput").ap()

    def dint(self, name, shape, dt):
        return self.nc.dram_tensor(name, list(shape), dt)

    def phase_end(self):
        from contextlib import ExitStack
        self.s.barrier()
        self.s.emit(final=False)
        self.es.close()
        self.es = ExitStack()

    def dmaf(self, fn, r=(), w=(), q="sp", is_output=False):
        return self.s.dma(q, fn, r, w, is_output)

    def dout(self, name, shape, dt):
        return self.nc.dram_tensor(name, list(shape), dt, kind="ExternalOutput").ap()

    def dma(self, out, in_, r=(), w=(), q="sp", is_output=False):
        return self.s.dma(q, lambda e: e.dma_start(out=out, in_=in_), r, w, is_output)

    def mm(self, out, lhsT, rhs, start, stop, r=(), w=()):
        return self.s.op("pe", lambda e: e.matmul(out, lhsT, rhs, start=start, stop=stop), r, w)

    def tr(self, out, in_, ident, r=(), w=()):
        return self.s.op("pe", lambda e: e.transpose(out, in_, ident), r, w)

    def act(self, out, in_, func, r=(), w=(), bias=None, scale=None, accum_out=None):
        kw = {}
        if bias is not None:
            kw["bias"] = bias
        if scale is not None:
            kw["scale"] = scale
        if accum_out is not None:
            kw["accum_out"] = accum_out
        return self.s.op("act", lambda e: e.activation(out, in_, func, **kw), r, w)

    def tt(self, out, in0, in1, op, r=(), w=(), eng="dve"):
        return self.s.op(eng, lambda e: e.tensor_tensor(out, in0, in1, op), r, w)

    def ts(self, out, in0, s1, s2, op0, op1=None, r=(), w=(), eng="dve", accum_out=None):
        kw = {}
        if accum_out is not None:
            kw["accum_out"] = accum_out
        if op1 is None:
            return self.s.op(eng, lambda e: e.tensor_scalar(out, in0, s1, s2, op0, **kw), r, w)
        return self.s.op(eng, lambda e: e.tensor_scalar(out, in0, s1, s2, op0, op1, **kw), r, w)

    def stt(self, out, in0, scalar, in1, op0, op1, r=(), w=(), eng="dve"):
        return self.s.op(eng, lambda e: e.scalar_tensor_tensor(out, in0, scalar, in1, op0, op1), r, w)

    def cp(self, out, in_, r=(), w=(), eng="dve"):
        if eng == "act":
            return self.s.op("act", lambda e: e.copy(out, in_), r, w)
        return self.s.op(eng, lambda e: e.tensor_copy(out, in_), r, w)

    def recip(self, out, in_, r=(), w=()):
        return self.s.op("dve", lambda e: e.reciprocal(out, in_), r, w)

    def memset(self, ap, val, w=(), eng="dve"):
        return self.s.op(eng, lambda e: e.memset(ap, val), (), w)

    def finish(self):
        self.s.emit()
        self.es.close()
        return self.nc


def load_cast_weight(p, w_dram, dst, nk, ncols, stage, tag, chunk_cols=1024):
    i = 0
    for kc in range(nk):
        for c0 in range(0, ncols, chunk_cols):
            cw = min(chunk_cols, ncols - c0)
            stg = stage[i % 2]
            p.dma(stg[:, 0:cw], w_dram[kc * 128:(kc + 1) * 128, c0:c0 + cw],
                  w=[("stg", i % 2)])
            p.cp(dst[:, kc, c0:c0 + cw], stg[:, 0:cw], r=[("stg", i % 2)],
                 w=[(tag, kc)], eng="pool")
            i += 1


def rmsnorm_tile(p, x_ap, gain_bc, out_ap, scr, keys_r, keys_w, tagk):
    sq, ss, sd, rs = scr
    p.act(sq, x_ap, AF.Square, r=keys_r, w=[("sq", tagk), ("ss", tagk)], accum_out=ss)
    p.act(sd, ss, AF.Sqrt, r=[("ss", tagk)], w=[("sd", tagk)], bias=EPS, scale=1.0 / D)
    p.recip(rs, sd, r=[("sd", tagk)], w=[("rs", tagk)])
    if gain_bc is None:
        p.ts(out_ap, x_ap, rs, None, ALU.mult, r=list(keys_r) + [("rs", tagk)], w=keys_w)
    else:
        p.stt(out_ap, x_ap, rs, gain_bc, ALU.mult, ALU.mult,
              r=list(keys_r) + [("rs", tagk), "gains"], w=keys_w)


class FFNCtx:
    def __init__(self, p, pre, max_nt=4):
        self.p = p
        self.max_nt = max_nt
        nt = max_nt
        self.wo = p.sb(pre + "wo", [64, 16, 1024], BF16)
        self.woch = [p.sb(pre + f"woch{i}", [128, 512], BF16) for i in range(2)]
        self.fstage = [p.sb(pre + f"fstg{i}", [128, 2048], F32) for i in range(2)]
        self.stage = [self.fstage[i][:, 0:1024] for i in range(2)]
        self.gT = p.sb(pre + "gT", [128, 8], F32)
        self.wch = [p.sb(pre + f"wch{i}", [128, 8, 2, 128], BF16) for i in range(2)]
        self.h1 = p.sb(pre + "h1", [128, nt, 1024], F32)
        self.oT = p.sb(pre + "oT", [64, 16, nt * 128], BF16)
        self.hn = p.sb(pre + "hn", [128, 1024], BF16)
        self.hnT = p.sb(pre + "hnT", [128, 8, nt * 128], BF16)
        self.actT = p.sb(pre + "actT", [128, 22, nt * 128], BF16)
        self.usb = [[p.sb(pre + f"usb{i}{a}", [128, 2 + nt * 128], F32) for a in range(2)]
                    for i in range(2)]
        self.carry = p.sb(pre + "carry", [128, 44, 2], F32)
        self.cwb = p.sb(pre + "cwb", [128, 44, 4], F32)
        self.t1 = p.sb(pre + "t1", [128, nt * 128], F32)
        self.t2 = p.sb(pre + "t2", [128, nt * 128], F32)
        self.ca = p.sb(pre + "ca", [128, nt * 128], F32)
        self.cg = p.sb(pre + "cg", [128, nt * 128], F32)
        self.sa = p.sb(pre + "sa", [128, nt * 128], F32)
        self.sq = p.sb(pre + "sq", [128, 1024], BF16)
        self.ss = p.sb(pre + "ss", [128, 1], F32)
        self.sd = p.sb(pre + "sd", [128, 1], F32)
        self.rs = p.sb(pre + "rs", [128, 1], F32)
        self.gainf = p.sb(pre + "gainf", [128, 1024], F32)
        self.ident = p.sb(pre + "ident", [128, 128], BF16)
        self.hfin = p.sb(pre + "hfin", [128, 1024], F32)
        self.psum = p.ps(pre + "psum", [128, 7 * 512])
        self.psT = p.ps(pre + "psT", [128, 8, 128], BF16)
        self.psA = [self.bank(0), self.bank(1)]
        self.psU = [[self.bank(2), self.bank(3)], [self.bank(4), self.bank(5)]]
        self.nA = 0
        self.nfc = 0
        self.nwo = 0

    def bank(self, i, n=1):
        return self.psum[:, i * 512:(i + n) * 512]

    def load_weights(self, w_o, w_in, w_out, cwb, g_ffn, ident, g_final=None, head_order=None):
        p = self.p
        self.w_in = w_in
        p.dma(self.ident[:], ident, w=["ident"])
        p.dma(self.cwb[:], cwb.rearrange("(c p) f -> p c f", p=128), w=["cwb"])
        p.dma(self.gT[:], g_ffn, w=["gT"])
        self.w_out = w_out
        nc = p.nc
        self.winb = nc.dram_tensor("winb" + p.sfx, [44 * 128, 1024], BF16).ap()
        self.woutb = nc.dram_tensor("woutb" + p.sfx, [44 * 128, 512], BF16).ap()
        i = 0
        for fc in range(22):
            for ag in range(2):
                par = i % 2
                i += 1
                stg = self.fstage[par]
                c0 = ag * DFF + fc * 128
                sk = ("stg", "ffn", par, 0)
                p.dma(stg[:, 0:1024].rearrange("p (c f) -> p c f", c=8),
                      w_in[:, c0:c0 + 128].rearrange("(c p) f -> p c f", p=128), w=[sk])
                p.tt(self.wch[par][:, :, 0, :],
                     stg[:, 0:1024].rearrange("p (c f) -> p c f", c=8),
                     self.gT[:].unsqueeze(2).to_broadcast([128, 8, 128]), ALU.mult,
                     r=[sk, "gT"], w=[("wch", par, 0)], eng="pool")
                p.dma(self.winb[(fc * 2 + ag) * 128:(fc * 2 + ag + 1) * 128, :],
                      self.wch[par][:, :, 0, :], r=[("wch", par, 0)], w=["winb"], q="pool")
        for half in range(2):
            for fc in range(22):
                par = i % 2
                i += 1
                stg = self.fstage[par]
                sk = ("stg", "ffn", par, 0)
                p.dma(stg[:, 0:512], w_out[fc * 128:(fc + 1) * 128, half * 512:(half + 1) * 512],
                      w=[sk])
                p.cp(self.woch[par][:], stg[:, 0:512], r=[sk], w=[("woch", par)], eng="pool")
                p.dma(self.woutb[(half * 22 + fc) * 128:(half * 22 + fc + 1) * 128, :],
                      self.woch[par][:], r=[("woch", par)], w=["woutb"], q="pool")
        if g_final is not None:
            p.dma(self.gainf[:], g_final.to_broadcast([128, 1024]), w=["gainf"])
        p.memset(self.carry[:], 0.0, w=["carry"])
        if w_o is not None:
            ho = head_order if head_order is not None else list(range(16))
            for i, h in enumerate(ho):
                stg = self.stage[i % 2]
                sk = ("stg", "ffn", i % 2, 0)
                p.dma(stg[0:64, :], w_o[h * 64:(h + 1) * 64, :], w=[sk])
                p.cp(self.wo[:, i, :], stg[0:64, :], r=[sk], w=[("wo", i)], eng="pool")

    def run_supertile(self, nt, h_src, oT_src, h_dst, n_skip_out=0, final_norm=False,
                      h1_preloaded=False, h_fn=None, oT_fn=None, n_flag=0, flag=None,
                      rkeys=(), dkey=None):
        p = self.p
        ntok = nt * 128
        if not h1_preloaded:
            for j in range(nt):
                p.dma(self.h1[:, j, :], h_src[j * 128:(j + 1) * 128, :], r=list(rkeys),
                      w=[("h1", j)])
                if j < n_flag:
                    p.ts(self.h1[:, j, :], self.h1[:, j, :], flag, None, ALU.mult,
                         r=[("h1", j), "flag"], w=[("h1", j)])
        if oT_src is not None or oT_fn is not None:
            if oT_fn is not None:
                for j in range(nt):
                    for g in range(4):
                        p.dma(self.oT[:, g * 4:(g + 1) * 4, j * 128:(j + 1) * 128], oT_fn(j, g),
                              r=list(rkeys), w=["oT"])
            elif not isinstance(oT_src, str):
                p.dma(self.oT[:, :, 0:ntok], oT_src.rearrange("(c p) t -> p c t", p=64), w=["oT"])
            for j in range(nt):
                for half in range(2):
                    ps = self.psA[self.nA % 2]
                    pk = ("bank", self.nA % 2)
                    self.nA += 1
                    for kc in range(16):
                        p.mm(ps, self.oT[:, kc, j * 128:(j + 1) * 128],
                             self.wo[:, kc, half * 512:(half + 1) * 512],
                             start=(kc == 0), stop=(kc == 15),
                             r=["oT", ("wo", kc)], w=[pk])
                    hs = self.h1[:, j, half * 512:(half + 1) * 512]
                    p.tt(hs, hs, ps, ALU.add, r=[pk, ("h1", j)], w=[("h1", j)])
        for j in range(nt):
            rmsnorm_tile(p, self.h1[:, j, :], None, self.hn[:],
                         (self.sq[:], self.ss[:], self.sd[:], self.rs[:]),
                         [("h1", j)], ["hn"], "f")
            for kc in range(8):
                p.tr(self.psT[:, kc, :], self.hn[:, kc * 128:(kc + 1) * 128], self.ident[:],
                     r=["hn", "ident"], w=["psT"])
            p.cp(self.hnT[:, :, j * 128:(j + 1) * 128], self.psT[:], r=["psT"], w=[("hnT", j)],
                 eng="act")
        hnT_keys = [("hnT", j) for j in range(nt)]
        for fc in range(22):
            par = self.nfc % 2
            self.nfc += 1
            wch = self.wch[par]
            for ag in range(2):
                p.dma(wch[:, :, ag, :],
                      self.winb[(fc * 2 + ag) * 128:(fc * 2 + ag + 1) * 128, :].rearrange(
                          "p (c f) -> p c f", c=8),
                      r=["winb"], w=[("wch", par, ag)])
            cs = []
            for ag in range(2):
                ps = self.psU[par][ag]
                pk = ("bank", 2 + 2 * par + ag)
                for kc in range(8):
                    p.mm(ps[:, 0:ntok], wch[:, kc, ag, :], self.hnT[:, kc, 0:ntok],
                         start=(kc == 0), stop=(kc == 7),
                         r=[("wch", par, ag)] + hnT_keys, w=[pk])
                usb = self.usb[par][ag]
                uk = ("usb", par, ag)
                ch = ag * 22 + fc
                p.cp(usb[:, 0:2], self.carry[:, ch, :], r=["carry%d" % ch, "carry"], w=[uk],
                     eng="pool")
                p.cp(usb[:, 2:2 + ntok], ps[:, 0:ntok], r=[pk], w=[uk], eng="act")
                p.cp(self.carry[:, ch, :], usb[:, ntok:ntok + 2], r=[uk], w=["carry%d" % ch],
                     eng="pool")
                cw = self.cwb
                dst = self.ca if ag == 0 else self.cg
                dk = "ca" if ag == 0 else "cg"
                p.ts(self.t1[:, 0:ntok], usb[:, 2:2 + ntok], cw[:, ch, 2:3], cw[:, ch, 3:4],
                     ALU.mult, ALU.add, r=[uk, "cwb"], w=["t1"])
                p.stt(self.t2[:, 0:ntok], usb[:, 1:1 + ntok], cw[:, ch, 1:2], self.t1[:, 0:ntok],
                      ALU.mult, ALU.add, r=[uk, "cwb", "t1"], w=["t2"])
                p.stt(dst[:, 0:ntok], usb[:, 0:ntok], cw[:, ch, 0:1], self.t2[:, 0:ntok],
                      ALU.mult, ALU.add, r=[uk, "cwb", "t2"], w=[dk])
            p.act(self.sa[:, 0:ntok], self.ca[:, 0:ntok], AF.Silu, r=["ca"], w=["sa"])
            p.tt(self.actT[:, fc, 0:ntok], self.sa[:, 0:ntok], self.cg[:, 0:ntok], ALU.mult,
                 r=["sa", "cg"], w=[("actT", fc)])
        for half in range(2):
            for fc in range(22):
                wp = self.nwo % 2
                self.nwo += 1
                p.dma(self.woch[wp][:],
                      self.woutb[(half * 22 + fc) * 128:(half * 22 + fc + 1) * 128, :],
                      r=["woutb"], w=[("woch", wp)])
                for j in range(nt):
                    p.mm(self.bank(j), self.actT[:, fc, j * 128:(j + 1) * 128], self.woch[wp][:],
                         start=(fc == 0), stop=(fc == 21),
                         r=[("actT", fc), ("woch", wp)], w=[("bank", j)])
            for j in range(nt):
                hs = self.h1[:, j, half * 512:(half + 1) * 512]
                p.tt(hs, hs, self.bank(j), ALU.add, r=[("bank", j), ("h1", j)], w=[("h1", j)])
        for j in range(nt):
            if h_dst is not None and j >= n_skip_out:
                jo = j - n_skip_out
                if final_norm:
                    rmsnorm_tile(p, self.h1[:, j, :], self.gainf[:], self.hfin[:],
                                 (self.sq[:], self.ss[:], self.sd[:], self.rs[:]),
                                 [("h1", j), "gainf"], ["hfin"], "f")
                    p.dma(h_dst[jo * 128:(jo + 1) * 128, :], self.hfin[:], r=["hfin"],
                          w=([dkey] if dkey else []), q="pool", is_output=True)
                else:
                    p.dma(h_dst[jo * 128:(jo + 1) * 128, :], self.h1[:, j, :], r=[("h1", j)],
                          w=([dkey] if dkey else []), q="pool", is_output=(dkey is None))


def ident_np():
    return np.eye(128, dtype=np.float32).astype(NPBF16)


def b_head_order():
    return [g * 4 + 2 * hpl + par for g in range(4) for par in range(2) for hpl in range(2)]


def build_B(st_sizes, n_skip_tiles, final_norm=False, p=None, io=None):
    fused = p is not None
    if not fused:
        nc = bass.Bass("TRN2", target_bir_lowering=False)
        p = Prog(nc)
    ntiles = sum(st_sizes)
    ntok = ntiles * 128
    if not fused:
        h_in = p.din("h_in", [ntok, D], F32)
        oT_in = p.din("oT_in", [D, ntok], BF16)
    w_o = p.din("w_o", [D, D], F32)
    w_in = p.din("w_in", [D, 2 * DFF], F32)
    w_out = p.din("w_out", [DFF, D], F32)
    cwb = p.din("cwb", [2 * DFF, 4], F32)
    g_ffn = p.din("g_ffn", [128, 8], F32)
    g_fin = p.din("g_fin", [1, D], F32)
    ident = p.din("ident", [128, 128], BF16)
    if not fused:
        h_out = p.dout("h_out", [(ntiles - n_skip_tiles) * 128, D], F32)
    else:
        h_out = io["h_dst"]
    f = FFNCtx(p, "f_", max_nt=max(st_sizes))
    f.load_weights(w_o, w_in, w_out, cwb, g_ffn, ident, g_fin,
                   head_order=(b_head_order() if fused else None))
    if fused:
        flag_sb = p.sb("flag", [128, 1], F32)
        p.dma(flag_sb[:], io["flag"], w=["flag"])
    t0 = 0
    for nt in st_sizes:
        skip = max(0, min(nt, n_skip_tiles - t0))
        o0 = max(0, t0 - n_skip_tiles)
        dst = h_out[o0 * 128:(o0 + nt - skip) * 128, :] if skip < nt else None
        if fused:
            f.run_supertile(nt, io["h_ap"][t0 * 128:(t0 + nt) * 128, :], None, dst,
                            n_skip_out=skip, final_norm=final_norm,
                            oT_fn=lambda j, g, t0=t0: io["oT_ap"](t0 + j, g),
                            n_flag=skip, flag=flag_sb[:], rkeys=io["rkeys"], dkey=io["dkey"])
        else:
            f.run_supertile(nt, h_in[t0 * 128:(t0 + nt) * 128, :],
                            oT_in[:, t0 * 128:(t0 + nt) * 128], dst, n_skip_out=skip,
                            final_norm=final_norm)
        t0 += nt
    if fused:
        return None
    return p.finish()


def c_head_order():
    return [8 * g + 2 * hpl + par for g in range(2) for par in range(2) for hpl in range(4)]


def build_C(st_sizes, n_skip_tiles, final_norm=False, p=None, io=None):
    fused = p is not None
    if not fused:
        nc = bass.Bass("TRN2", target_bir_lowering=False)
        p = Prog(nc)
    ntiles = sum(st_sizes)
    ntok = ntiles * 128
    mx = max(st_sizes)
    if not fused:
        h_in = p.din("h_in", [ntok, D], F32)
        hkv_in = p.din("hkv_in", [ntok, D], F32)
    w_q = p.din("w_q", [D, D], F32)
    w_kv = p.din("w_kv", [D, 256], F32)
    sinks_b = p.din("sinks_b", [1, 2048], F32)
    g_attn = p.din("g_attn", [128, 8], F32)
    g_kv = p.din("g_kv", [128, 8], F32)
    cos_t = p.din("cos_t", [128, ntok], F32)
    sin_t = p.din("sin_t", [128, ntok], F32)
    masks = p.din("masks", [3, 128, 512], BF16)
    w_o = p.din("w_o", [D, D], F32)
    w_in = p.din("w_in", [D, 2 * DFF], F32)
    w_out = p.din("w_out", [DFF, D], F32)
    cwb = p.din("cwb", [2 * DFF, 4], F32)
    g_ffn = p.din("g_ffn", [128, 8], F32)
    g_fin = p.din("g_fin", [1, D], F32)
    ident = p.din("ident", [128, 128], BF16)
    if not fused:
        h_out = p.dout("h_out", [(ntiles - n_skip_tiles) * 128, D], F32)
    else:
        h_out = io["h_dst"]

    f = FFNCtx(p, "f_", max_nt=mx)
    f.load_weights(w_o, w_in, w_out, cwb, g_ffn, ident, g_fin, head_order=c_head_order())
    if fused:
        flag_sb = p.sb("flag", [128, 1], F32)
        p.dma(flag_sb[:], io["flag"], w=["flag"])

    gTq = p.sb("gTq", [128, 8], F32)
    gTk = p.sb("gTk", [128, 8], F32)
    wk2 = p.sb("wk2", [128, 8, 2, 2, 128], BF16)
    wv = p.sb("wv", [128, 8, 128], BF16)
    hkv = p.sb("hkv", [128, 1024], F32)
    hnqT = f.hnT
    hkvT = p.sb("hkvT", [128, 8, mx * 128], BF16)
    QT2 = p.sb("QT2", [128, 8, mx * 128], BF16)
    KT2 = p.sb("KT2", [128, 2, (mx + 1) * 128], BF16)
    VA = p.sb("VA", [128, mx + 1, 2, 65], BF16)
    PT = [p.sb(f"PT{i}", [128, 1024], BF16) for i in range(2)]
    msk = p.sb("msk", [128, 3, 512], BF16)
    cos_sb = p.sb("cos_sb", [128, mx * 128], F32)
    sin_sb = p.sb("sin_sb", [128, mx * 128], F32)
    sexp = p.sb("sexp", [128, 2048], BF16)
    zr = p.sb("zr", [128, 1024], F32)
    rz = zr
    ones = p.sb("ones", [128, 64], F32)
    osb = f.hfin[0:64, :]
    psS = [f.bank(2, 2), f.bank(4, 2)]
    psSk = [[("bank", 2), ("bank", 3)], [("bank", 4), ("bank", 5)]]
    psO = f.bank(0, 2)
    psOk = [("bank", 0), ("bank", 1)]
    psB = f.bank(6)
    psBk = [("bank", 6)]

    p.dma(gTq[:], g_attn, w=["gTq"])
    p.dma(gTk[:], g_kv, w=["gTk"])
    p.dma(msk[:], masks.rearrange("m p c -> p m c"), w=["msk"])
    p.dma(zr[64:65, :], sinks_b[:, 0:1024], w=["zr"])
    p.act(sexp[64:65, 0:1024], zr[64:65, :], AF.Exp, r=["zr"], w=["sexp"])
    p.dma(zr[64:65, :], sinks_b[:, 1024:2048], r=["sexp"], w=["zr"])
    p.act(sexp[64:65, 1024:2048], zr[64:65, :], AF.Exp, r=["zr"], w=["sexp"])
    p.memset(ones[:], 1.0, w=["ones"])
    p.memset(VA[:], 1.0, w=["VA"] + [("VA", i) for i in range(mx + 1)])
    p.memset(KT2[:], 0.0, w=["KT2", ("KT2", 0), ("KT2", 1)])
    for kc in range(8):
        stg = f.stage[kc % 2]
        sk = ("stg", "ffn", kc % 2, 0)
        gs = gTk[:, kc:kc + 1]
        p.dma(stg[:, 0:256], w_kv[kc * 128:(kc + 1) * 128, :], w=[sk])
        for g in range(2):
            for dup in range(2):
                p.ts(wk2[:, kc, g, 0, dup * 64:(dup + 1) * 64], stg[:, g * 64:(g + 1) * 64],
                     gs, None, ALU.mult, r=[sk, "gTk"], w=["wk2"], eng="pool")
                p.ts(wk2[:, kc, g, 1, dup * 64:dup * 64 + 32], stg[:, g * 64 + 32:g * 64 + 64],
                     gs, None, ALU.mult, r=[sk, "gTk"], w=["wk2"], eng="pool")
                p.ts(wk2[:, kc, g, 1, dup * 64 + 32:dup * 64 + 64], stg[:, g * 64:g * 64 + 32],
                     gs, None, ALU.mult, r=[sk, "gTk"], w=["wk2"], eng="pool")
        p.ts(wv[:, kc, :], stg[:, 128:256], gs, None, ALU.mult, r=[sk, "gTk"], w=["wv"], eng="pool")

    scr = (f.sq[:], f.ss[:], f.sd[:], f.rs[:])
    t0 = 0
    first_real = n_skip_tiles
    for nt in st_sizes:
        n = nt * 128
        for j in range(nt):
            if fused:
                p.dma(f.h1[:, j, :], io["h_ap"][(t0 + j) * 128:(t0 + j + 1) * 128, :],
                      r=list(io["rkeys"]), w=[("h1", j)])
                if t0 + j < n_skip_tiles:
                    p.ts(f.h1[:, j, :], f.h1[:, j, :], flag_sb[:], None, ALU.mult,
                         r=[("h1", j), "flag"], w=[("h1", j)])
            else:
                p.dma(f.h1[:, j, :], h_in[(t0 + j) * 128:(t0 + j + 1) * 128, :], w=[("h1", j)])
        p.dma(cos_sb[:, 0:n], cos_t[:, t0 * 128:t0 * 128 + n], w=["cos"])
        p.dma(sin_sb[:, 0:n], sin_t[:, t0 * 128:t0 * 128 + n], w=["sin"])
        for j in range(nt):
            rmsnorm_tile(p, f.h1[:, j, :], None, f.hn[:], scr, [("h1", j)], ["hn"], "f")
            for kc in range(8):
                p.tr(f.psT[:, kc, :], f.hn[:, kc * 128:(kc + 1) * 128], f.ident[:],
                     r=["hn", "ident"], w=["psT"])
            p.cp(hnqT[:, :, j * 128:(j + 1) * 128], f.psT[:], r=["psT"], w=[("hnT", j)], eng="act")
            if fused:
                p.dma(hkv[:], io["hkv_ap"][(t0 + j) * 128:(t0 + j + 1) * 128, :],
                      r=list(io["rkeys"]), w=["hkv"])
                if t0 + j < n_skip_tiles:
                    p.ts(hkv[:], hkv[:], flag_sb[:], None, ALU.mult, r=["hkv", "flag"], w=["hkv"])
            else:
                p.dma(hkv[:], hkv_in[(t0 + j) * 128:(t0 + j + 1) * 128, :], w=["hkv"])
            rmsnorm_tile(p, hkv[:], None, f.hn[:], scr, ["hkv"], ["hn"], "f")
            for kc in range(8):
                p.tr(f.psT[:, kc, :], f.hn[:, kc * 128:(kc + 1) * 128], f.ident[:],
                     r=["hn", "ident"], w=["psT"])
            p.cp(hkvT[:, :, j * 128:(j + 1) * 128], f.psT[:], r=["psT"], w=[("hkvT", j)], eng="act")
        hq_keys = [("hnT", j) for j in range(nt)]
        hk_keys = [("hkvT", j) for j in range(nt)]

        def rope_out(dst, psn, pss, rk, wk):
            p.tt(f.t1[:, 0:n], psn, cos_sb[:, 0:n], ALU.mult, r=rk[0:1] + ["cos"], w=["t1"])
            p.tt(f.t2[:, 0:n], pss, sin_sb[:, 0:n], ALU.mult, r=rk[1:2] + ["sin"], w=["t2"])
            p.tt(dst, f.t1[:, 0:n], f.t2[:, 0:n], ALU.add, r=["t1", "t2"], w=wk)

        for g in range(2):
            par = f.nfc % 2
            f.nfc += 1
            bk = [("bank", 2 + 2 * par), ("bank", 3 + 2 * par)]
            for v in range(2):
                for kc in range(8):
                    p.mm(f.psU[par][v][:, 0:n], wk2[:, kc, g, v, :], hkvT[:, kc, 0:n],
                         start=(kc == 0), stop=(kc == 7), r=["wk2"] + hk_keys, w=[bk[v]])
            rope_out(KT2[:, g, 128:128 + n], f.psU[par][0][:, 0:n], f.psU[par][1][:, 0:n],
                     bk, [("KT2", g)])
        for j in range(nt):
            ps = f.psA[f.nA % 2]
            pk = ("bank", f.nA % 2)
            f.nA += 1
            for kc in range(8):
                p.mm(ps[:, 0:128], hkvT[:, kc, j * 128:(j + 1) * 128], wv[:, kc, :],
                     start=(kc == 0), stop=(kc == 7), r=[("hkvT", j), "wv"], w=[pk])
            p.cp(VA[:, j + 1, :, 0:64], ps[:, 0:128].rearrange("p (g d) -> p g d", g=2),
                 r=[pk], w=[("VA", j + 1)], eng="act")
        for hp in range(8):
            par = f.nfc % 2
            f.nfc += 1
            stg = f.fstage[par]
            wch = f.wch[par]
            p.dma(stg[:, 0:1024].rearrange("p (c f) -> p c f", c=8),
                  w_q[:, hp * 128:(hp + 1) * 128].rearrange("(c p) f -> p c f", p=128),
                  w=[("stg", "ffn", par, 0)])
            p.tt(wch[:, :, 0, :], stg[:, 0:1024].rearrange("p (c f) -> p c f", c=8),
                 gTq[:].unsqueeze(2).to_broadcast([128, 8, 128]), ALU.mult,
                 r=[("stg", "ffn", par, 0), "gTq"], w=[("wch", par, 0)], eng="pool")
            src = wch[:, :, 0, :].rearrange("p c (h d) -> p c h d", h=2)
            dsw = wch[:, :, 1, :].rearrange("p c (h d) -> p c h d", h=2)
            p.cp(dsw[:, :, :, 0:32], src[:, :, :, 32:64], r=[("wch", par, 0)],
                 w=[("wch", par, 1)], eng="pool")
            p.cp(dsw[:, :, :, 32:64], src[:, :, :, 0:32], r=[("wch", par, 0)],
                 w=[("wch", par, 1)], eng="pool")
            bk = [("bank", 2 + 2 * par), ("bank", 3 + 2 * par)]
            for v in range(2):
                for kc in range(8):
                    p.mm(f.psU[par][v][:, 0:n], wch[:, kc, v, :], hnqT[:, kc, 0:n],
                         start=(kc == 0), stop=(kc == 7),
                         r=[("wch", par, v)] + hq_keys, w=[bk[v]])
            rope_out(QT2[:, hp, 0:n], f.psU[par][0][:, 0:n], f.psU[par][1][:, 0:n],
                     bk, [("QT2", hp)])
        nS = 0
        for j in range(nt):
            gt = t0 + j
            for g in range(2):
                chunks = [(j, 0 if gt == first_real else 1), (j + 1, 2)]
                for ci, (slot, mi) in enumerate(chunks):
                    sp_ = nS % 2
                    nS += 1
                    for par in range(2):
                        pr = slice(par * 64, (par + 1) * 64)
                        p.mm(psS[sp_][:, par * 512:(par + 1) * 512],
                             KT2[pr, g, slot * 128:(slot + 1) * 128],
                             QT2[pr, 4 * g:4 * g + 4, j * 128:(j + 1) * 128],
                             start=True, stop=False,
                             r=[("KT2", g)] + [("QT2", 4 * g + i) for i in range(4)],
                             w=[psSk[sp_][par]])
                        p.mm(psS[sp_][:, par * 512:(par + 1) * 512], f.ident[:], msk[:, mi, :],
                             start=False, stop=True, r=["ident", "msk"], w=[psSk[sp_][par]])
                    p.act(PT[sp_][:], psS[sp_], AF.Exp, r=psSk[sp_], w=[("PT", sp_)], scale=0.125)
                    for par in range(2):
                        p.mm(psO[0:65, par * 512:(par + 1) * 512], VA[:, slot, g, :],
                             PT[sp_][:, par * 512:(par + 1) * 512],
                             start=(ci == 0), stop=(ci == 1),
                             r=[("PT", sp_), ("VA", slot), "VA"], w=[psOk[par]])
                p.tt(zr[64:65, :], psO[64:65, :], sexp[64:65, g * 1024:(g + 1) * 1024], ALU.add,
                     r=psOk + ["sexp"], w=["zr"])
                p.recip(rz[64:65, :], zr[64:65, :], r=["zr"], w=["rz"])
                p.cp(osb, psO[0:64, :], r=psOk, w=["hfin"], eng="act")
                for par in range(2):
                    p.mm(psB[0:64, :], ones[64:65, :], rz[64:65, par * 512:(par + 1) * 512],
                         start=True, stop=True, r=["ones", "rz"], w=psBk)
                    dst = f.oT[:, g * 8 + par * 4:g * 8 + par * 4 + 4, j * 128:(j + 1) * 128]
                    p.tt(dst, osb[:, par * 512:(par + 1) * 512].rearrange("p (h q) -> p h q", h=4),
                         psB[0:64, :].rearrange("p (h q) -> p h q", h=4), ALU.mult,
                         r=["hfin"] + psBk, w=["oT"])
        for g in range(2):
            p.cp(KT2[:, g, 0:128], KT2[:, g, n:n + 128], r=[("KT2", g)], w=[("KT2", g)], eng="pool")
        p.cp(VA[:, 0, :, :], VA[:, nt, :, :], r=[("VA", nt)], w=[("VA", 0)], eng="pool")
        skip = max(0, min(nt, n_skip_tiles - t0))
        o0 = max(0, t0 - n_skip_tiles)
        dst = h_out[o0 * 128:(o0 + nt - skip) * 128, :] if skip < nt else None
        f.run_supertile(nt, None, "resident", dst, n_skip_out=skip, final_norm=final_norm,
                        h1_preloaded=True, dkey=(io["dkey"] if fused else None))
        t0 += nt
    if fused:
        return None
    return p.finish()


def rope_tables(pos):
    half = 32
    inv = (np.float32(10000.0) ** (-np.arange(half, dtype=np.float32) / half)).astype(np.float32)
    ang = pos.astype(np.float32)[None, :] * inv[:, None]
    cos = np.cos(ang).astype(np.float32)
    sin = np.sin(ang).astype(np.float32)
    cos64 = np.concatenate([cos, cos], 0)
    sin64 = np.concatenate([-sin, sin], 0)
    return (np.ascontiguousarray(np.concatenate([cos64, cos64], 0)),
            np.ascontiguousarray(np.concatenate([sin64, sin64], 0)))


def swa_masks(first_exists):
    i = np.arange(128)[:, None]
    q = np.arange(128)[None, :]
    prev = np.where(i > q, 0.0, MASKV).astype(np.float32)
    cur = np.where(i <= q, 0.0, MASKV).astype(np.float32)
    pf = prev if first_exists else np.full((128, 128), MASKV, np.float32)
    m = np.stack([np.tile(pf, (1, 4)), np.tile(prev, (1, 4)), np.tile(cur, (1, 4))], 0)
    return m.astype(NPBF16)


def sinks_row(sinks16):
    ho = c_head_order()
    return np.ascontiguousarray(
        np.repeat(np.asarray(sinks16, np.float32)[ho], 128)[None, :])


def gT_np(g):
    return np.ascontiguousarray(np.asarray(g, np.float32).reshape(8, 128).T)


FORCE = 1.0e6
TINY = 1.0e-30


def build_A(S, dbg=99, p=None, io=None):
    fused = p is not None
    if not fused:
        nc = bass.Bass("TRN2", target_bir_lowering=False)
        p = Prog(nc)
    nc = p.nc
    NST = S // 512
    NQB = S // 128
    NCC = max(1, S // 2048)
    if not fused:
        h_in = p.din("h_in", [S, D], F32)
    g_attn = p.din("g_attn", [128, 8], F32)
    wq_d = p.din("wq", [D, 256], F32)
    wk3_d = p.din("wk3", [D, 192], F32)
    wv3_d = p.din("wv3", [D, 192], F32)
    wg_d = p.din("wg", [D, 12], F32)
    w1_d = p.din("w1", [2, 2048, 256], F32)
    w2_d = p.din("w2", [2, 256, 64], F32)
    posT_d = p.din("posT", [64, 2, 32], F32)
    cos_d = p.din("cos_t", [128, S], F32)
    sin_d = p.din("sin_t", [128, S], F32)
    ccos_d = p.din("ccos_t", [128, NCC * 128], F32)
    csin_d = p.din("csin_t", [128, NCC * 128], F32)
    pmask_d = p.din("pmask", [2, 16, 128, 128], BF16)
    r0mask_d = p.din("r0mask", [128, 512], BF16)
    cmask_d = p.din("cmask", [2, 128, 512], BF16)
    emat_d = p.din("emat", [64, 128, 128], BF16)
    wfull_d = p.din("wfull", [NCC * 128, 257], BF16)
    fix_d = p.din("fix3", [128, 6], F32)
    ident_d = p.din("ident", [128, 128], BF16)
    gscr = [nc.dram_tensor(f"gscr{i}" + p.sfx, [1, 12 * 512], F32).ap() for i in range(2)]
    if not fused:
        oT_out = p.dout("oT_out", [64, 4, S], BF16)

    ident = p.sb("ident", [128, 128], BF16)
    gT = p.sb("gT", [128, 8], F32)
    fst = [p.sb(f"fst{i}", [128, 1024], F32) for i in range(2)]
    WQ = p.sb("WQ", [128, 8, 2, 256], BF16)
    WKS = p.sb("WKS", [128, 8, 2, 128], BF16)
    WKW = p.sb("WKW", [128, 8, 2, 128], BF16)
    WKC = p.sb("WKC", [128, 8, 64], BF16)
    WVC = p.sb("WVC", [128, 8, 64], BF16)
    WV2 = p.sb("WV2", [128, 8, 128], BF16)
    WG = p.sb("WG", [128, 8, 12], BF16)
    W1c = [p.sb(f"W1c{i}", [64, 4, 256], BF16) for i in range(2)]
    W2K = p.sb("W2K", [128, 2, 2, 128], BF16)
    W2V = p.sb("W2V", [128, 2, 64], BF16)
    posT = p.sb("posT", [64, 2, 32], BF16)
    c1 = p.sb("c1", [128, 4], F32)
    ccos = p.sb("ccos", [128, NCC * 128], F32)
    csin = p.sb("csin", [128, NCC * 128], F32)
    pmask = p.sb("pmask", [128, 2, 16, 128], BF16)
    r0mask = p.sb("r0mask", [128, 512], BF16)
    cmask = p.sb("cmask", [128, 2, 512], BF16)
    emat = p.sb("emat", [128, 64, 128], BF16)
    wfull = p.sb("wfull", [128, NCC, 257], BF16)
    fix3 = p.sb("fix3", [128, 6], F32)
    hbuf = [p.sb("hbuf0", [128, 1024], F32)] * 2
    sq = p.sb("sq", [128, 1024], BF16)
    ss = p.sb("ss", [128, 1], F32)
    sd = p.sb("sd", [128, 1], F32)
    rs = p.sb("rs", [128, 1], F32)
    hn = p.sb("hn", [128, 1024], BF16)
    hnT = p.sb("hnT", [128, 8, 512], BF16)
    cos_sb = p.sb("cos_sb", [128, 512], F32)
    sin_sb = p.sb("sin_sb", [128, 512], F32)
    t1 = p.sb("t1", [128, 512], F32)
    t2 = p.sb("t2", [128, 512], F32)
    Qblk = p.sb("Qblk", [128, 4, 512], BF16)
    KsT2 = p.sb("KsT2", [128, S], BF16)
    VsA = p.sb("VsA", [128, NQB, 65], BF16)
    KwT2 = p.sb("KwT2", [128, 1024], BF16)
    VwA = p.sb("VwA", [128, 8, 65], BF16)
    KcT2 = p.sb("KcT2", [128, NCC * 128], BF16)
    VcA = p.sb("VcA", [128, NCC, 65], BF16)
    xT = [p.sb(f"xT{i}", [64, 528], BF16) for i in range(2)]
    hidK = p.sb("hidK", [128, 2, 32], BF16)
    hidV = p.sb("hidV", [128, 2, 128], BF16)
    gx = [p.sb(f"gx{i}", [128, 32], F32) for i in range(3)]
    gsb = p.sb("gsb", [12, 512], F32)
    G64b = [p.sb(f"G64b{i}", [128, 12 * 128], F32) for i in range(2)]
    PT = [p.sb(f"PT{i}", [128, 512], BF16) for i in range(4)]
    PcT = p.sb("PcT", [128, NCC, 512], BF16)
    zr = p.sb("zr", [128, 512], F32)
    Rr = p.sb("Rr", [128, 512], F32)
    ones = p.sb("ones", [128, 64], F32)
    osb = p.sb("osb", [64, 512], F32)
    acc = p.sb("acc", [64, 512], F32)
    tmpo = p.sb("tmpo", [64, 512], F32)
    oacc = p.sb("oacc", [64, 4, 128], BF16)
    imp = p.sb("imp", [128, 256], F32)
    selbuf = p.sb("selbuf", [128, 256], F32)
    work = p.sb("work", [128, 256], F32)
    mx8 = p.sb("mx8", [128, 8], F32)
    thr = p.sb("thr", [128, 1], F32)
    zq = p.sb("zq", [128, 1], F32)
    Bq = p.sb("Bq", [128, 256], BF16)
    BT = p.sb("BT", [128, 2, 512], BF16)
    psum = p.ps("psum", [128, 7 * 512])
    psT = p.ps("psT", [128, 8, 128], BF16)
    zero_b = p.sb("zero_b", [128, 512], BF16)
    p.memset(zero_b[:], 0.0, w=["zero_b"])
    p.memset(Qblk[:], 0.0, w=["QT2"])

    def bank(i, n=1):
        return psum[:, i * 512:(i + n) * 512]

    def bk(i):
        return ("bank", i)

    p.dma(ident[:], ident_d, w=["ident"])
    p.dma(gT[:], g_attn, w=["gT"])
    p.dma(ccos[:], ccos_d, w=["ccos"])
    p.dma(csin[:], csin_d, w=["csin"])
    for a_ in range(2):
        for r4 in range(0, 16, 4):
            p.dma(pmask[:, a_, r4:r4 + 4, :], pmask_d[a_, r4:r4 + 4].rearrange("r p c -> p r c"),
                  w=["pmask"])
    p.dma(r0mask[:], r0mask_d, w=["r0mask"])
    p.dma(cmask[:], cmask_d.rearrange("a p c -> p a c"), w=["cmask"])
    for e8 in range(0, 64, 8):
        p.dma(emat[:, e8:e8 + 8, :], emat_d[e8:e8 + 8].rearrange("e p c -> p e c"), w=["emat"])
    p.dma(wfull[:], wfull_d.rearrange("(c p) f -> p c f", p=128), w=["wfull"])
    p.dma(fix3[:], fix_d, w=["fix3"])
    p.memset(ones[:], 1.0, w=["ones"])
    p.memset(VsA[:], 1.0, w=["VsA"])
    p.memset(VwA[:], 1.0, w=["VwA"])
    p.memset(VcA[:], 1.0, w=["VcA"])
    p.memset(KwT2[:], 0.0, w=["KwT2"])
    p.memset(KcT2[:], 0.0, w=["KcT2"])
    p.memset(selbuf[:], -FORCE, w=["selbuf"])
    p.memset(hidV[:], 0.0, w=["hidV"])
    for i in range(2):
        p.memset(xT[i][:], 0.0, w=[("xT", i)])
    nst_ = [0]

    def stage_load(dst_fn, src_ap, ncols, parts=128):
        i = nst_[0] % 2
        nst_[0] += 1
        k = ("fst", i)
        p.dma(fst[i][0:parts, 0:ncols], src_ap, w=[k])
        return fst[i], k

    def swapcopy(dst, src, r, w):
        d4 = dst.rearrange("p (h d) -> p h d", d=64)
        s4 = src.rearrange("p (h d) -> p h d", d=64)
        p.cp(d4[:, :, 0:32], s4[:, :, 32:64], r=r, w=w, eng="pool")
        p.cp(d4[:, :, 32:64], s4[:, :, 0:32], r=r, w=w, eng="pool")

    for kc in range(8):
        gs = gT[:, kc:kc + 1]
        rows = slice(kc * 128, (kc + 1) * 128)
        st_, k = stage_load(None, wq_d[rows, :], 256)
        p.ts(WQ[:, kc, 0, :], st_[:, 0:256], gs, None, ALU.mult, r=[k, "gT"], w=["WQ"], eng="pool")
        swapcopy(WQ[:, kc, 1, :], WQ[:, kc, 0, :], ["WQ"], ["WQ"])
        st_, k = stage_load(None, wk3_d[rows, :], 192)
        p.ts(WKC[:, kc, :], st_[:, 0:64], gs, None, ALU.mult, r=[k, "gT"], w=["WKC"], eng="pool")
        for (W_, c0) in ((WKS, 64), (WKW, 128)):
            for dup in range(2):
                p.ts(W_[:, kc, 0, dup * 64:(dup + 1) * 64], st_[:, c0:c0 + 64], gs, None, ALU.mult,
                     r=[k, "gT"], w=["WK"], eng="pool")
            swapcopy(W_[:, kc, 1, :], W_[:, kc, 0, :], ["WK"], ["WK"])
        st_, k = stage_load(None, wv3_d[rows, :], 192)
        p.ts(WVC[:, kc, :], st_[:, 0:64], gs, None, ALU.mult, r=[k, "gT"], w=["WVC"], eng="pool")
        p.ts(WV2[:, kc, :], st_[:, 64:192], gs, None, ALU.mult, r=[k, "gT"], w=["WV2"], eng="pool")
        st_, k = stage_load(None, wg_d[rows, :], 12)
        p.ts(WG[:, kc, :], st_[:, 0:12], gs, None, ALU.mult, r=[k, "gT"], w=["WG"], eng="pool")
    nW1 = [0]

    def w1_piece(kv, l0):
        i = nW1[0] % 2
        nW1[0] += 1
        k = ("fst", i)
        p.dma(fst[i][0:64, :].rearrange("p (l m) -> p l m", l=4),
              w1_d[kv, l0 * 64:(l0 + 4) * 64, :].rearrange("(l d) m -> d l m", d=64), w=[k])
        p.cp(W1c[i][:], fst[i][0:64, :].rearrange("p (l m) -> p l m", l=4),
             r=[k], w=[("W1c", i)], eng="pool")
        return W1c[i], ("W1c", i)

    for kv in range(2):
        for mt in range(2):
            st_, k = stage_load(None, w2_d[kv, mt * 128:(mt + 1) * 128, :], 64)
            if kv == 0:
                for dup in range(2):
                    p.cp(W2K[:, mt, 0, dup * 64:(dup + 1) * 64], st_[:, 0:64], r=[k], w=["W2K"],
                         eng="pool")
                swapcopy(W2K[:, mt, 1, :], W2K[:, mt, 0, :], ["W2K"], ["W2K"])
            else:
                p.cp(W2V[:, mt, :], st_[:, 0:64], r=[k], w=["W2V"], eng="pool")
    st_, k = stage_load(None, posT_d.rearrange("d a l -> d (a l)"), 64, parts=64)
    p.cp(posT[:].rearrange("d a l -> d (a l)"), st_[0:64, 0:64], r=[k], w=["posT"], eng="pool")
    for kv in range(2):
        for l0 in range(0, 32, 4):
            wt, wk_ = w1_piece(kv, l0)
            for mt in range(2):
                col = kv * 2 + mt
                bb = 1 if mt == 0 else 6
                for li in range(4):
                    l = l0 + li
                    p.mm(bank(bb)[:, col:col + 1], wt[:, li, mt * 128:(mt + 1) * 128],
                         posT[:, kv, l:l + 1], start=(l == 0), stop=(l == 31),
                         r=[wk_, "posT"], w=[bk(bb)])
    p.cp(c1[:, 0:1], bank(1)[:, 0:1], r=[bk(1)], w=["c1"], eng="act")
    p.cp(c1[:, 2:3], bank(1)[:, 2:3], r=[bk(1)], w=["c1"], eng="act")
    p.cp(c1[:, 1:2], bank(6)[:, 1:2], r=[bk(6)], w=["c1"], eng="act")
    p.cp(c1[:, 3:4], bank(6)[:, 3:4], r=[bk(6)], w=["c1"], eng="act")

    if dbg == 0:
        return p.finish()
    if fused:
        for cb in range(4):
            p.dma(io["o_zero"][:, cb, :], zero_b[0:64, 0:128], r=["zero_b"], w=[io["dkey"]],
                  q="pool")
    scr = (sq[:], ss[:], sd[:], rs[:])

    def rope_out(dst, psn, pss, cs, sn, rk, wk, n):
        p.tt(t1[:, 0:n], psn, cs, ALU.mult, r=rk[0:1] + ["cos", "ccos"], w=["t1"])
        p.tt(t2[:, 0:n], pss, sn, ALU.mult, r=rk[1:2] + ["sin", "csin"], w=["t2"])
        if isinstance(dst, tuple):
            p.tt(dst[0], t1[0:64, 0:n], t2[0:64, 0:n], ALU.add, r=["t1", "t2"], w=wk)
            p.tt(dst[1], t1[64:128, 0:n], t2[64:128, 0:n], ALU.add, r=["t1", "t2"], w=wk)
        else:
            p.tt(dst, t1[:, 0:n], t2[:, 0:n], ALU.add, r=["t1", "t2"], w=wk)

    nU = [0]
    nH = [0]

    def proj_pair(W_, dst, cs, sn, wkey, rkey):
        par = nU[0] % 2
        nU[0] += 1
        b0, b1 = 2 + 2 * par, 3 + 2 * par
        for v, b in ((0, b0), (1, b1)):
            for kc in range(8):
                p.mm(bank(b), W_(kc, v), hnT[:, kc, :], start=(kc == 0), stop=(kc == 7),
                     r=[rkey, "hnT"], w=[bk(b)])
        rope_out(dst, bank(b0), bank(b1), cs, sn, [bk(b0), bk(b1)], wkey, 512)

    def gelu_to(dst, ps_ap, bias_ap, rk, wk):
        x, a, b = gx[0][:], gx[1][:], gx[2][:]
        p.act(x, ps_ap, AF.Identity, r=rk + ["c1"], w=["gx0"], bias=bias_ap)
        p.tt(a, x, x, ALU.mult, r=["gx0"], w=["gx1"])
        p.ts(a, a, 0.044715, 1.0, ALU.mult, ALU.add, r=["gx1"], w=["gx1"])
        p.tt(a, a, x, ALU.mult, r=["gx1", "gx0"], w=["gx1"])
        p.act(b, a, AF.Sigmoid, r=["gx1"], w=["gx2"], scale=1.5957691216057308)
        p.tt(dst, x, b, ALU.mult, r=["gx0", "gx2"], w=wk)

    nS = [0]
    sdepth = [2]

    def attn_chunk(kT2, kcols, vaug, biases, first, last, n_extra_r):
        sp_ = nS[0] % sdepth[0]
        nS[0] += 1
        sb_ = 2 + sp_
        if ZERO_BIAS and len(biases) == 0:
            biases = [(ident[:], zero_b[:], ["ident", "zero_b"])]
        out = bank(sb_)
        p.mm(out, kT2[:, kcols], Qblk[:, :, qsl[0]], start=True, stop=(len(biases) == 0),
             r=n_extra_r + ["QT2"], w=[bk(sb_)])
        for bi, bias in enumerate(biases):
            lh, rh, rk = bias[0:3]
            if len(bias) == 4:
                for hh in range(4):
                    p.mm(out[:, hh * 128:(hh + 1) * 128], lh, rh, start=False,
                         stop=(bi == len(biases) - 1), r=rk, w=[bk(sb_)])
            else:
                p.mm(out, lh, rh, start=False, stop=(bi == len(biases) - 1), r=rk, w=[bk(sb_)])
        return sp_, sb_

    qsl = [None]
    for st in range(NST):
        tok0 = st * 512
        for j in range(4):
            hb = hbuf[j % 2]
            hk = ("hbuf", 0)
            if fused:
                r0 = io["h_row"](tok0 + j * 128)
                p.dma(hb[:], io["h_ap"][r0:r0 + 128, :], r=list(io["rkeys"]), w=[hk])
            else:
                p.dma(hb[:], h_in[tok0 + j * 128:tok0 + (j + 1) * 128, :], w=[hk])
            rmsnorm_tile(p, hb[:], None, hn[:], scr, [hk], ["hn"], "a")
            for kc in range(8):
                p.tr(psT[:, kc, :], hn[:, kc * 128:(kc + 1) * 128], ident[:],
                     r=["hn", "ident"], w=["psT"])
            p.cp(hnT[:, :, j * 128:(j + 1) * 128], psT[:], r=["psT"], w=["hnT"], eng="act")
        p.dma(cos_sb[:], cos_d[:, tok0:tok0 + 512], w=["cos"])
        p.dma(sin_sb[:], sin_d[:, tok0:tok0 + 512], w=["sin"])
        for hp in range(2):
            proj_pair(lambda kc, v, hp=hp: WQ[:, kc, v, hp * 128:(hp + 1) * 128],
                      (Qblk[0:64, hp, :], Qblk[64:128, 2 + hp, :]), cos_sb[:], sin_sb[:],
                      ["QT2"], "WQ")
        proj_pair(lambda kc, v: WKS[:, kc, v, :], KsT2[:, tok0:tok0 + 512], cos_sb[:], sin_sb[:],
                  ["KsT2"], "WK")
        proj_pair(lambda kc, v: WKW[:, kc, v, :], KwT2[:, 512:1024], cos_sb[:], sin_sb[:],
                  ["KwT2"], "WK")
        for i, W_ in enumerate((WKC, WVC)):
            for kc in range(8):
                p.mm(bank(1)[0:64, :], W_[:, kc, :], hnT[:, kc, :], start=(kc == 0), stop=(kc == 7),
                     r=["WKC", "WVC", "hnT"], w=[bk(1)])
            p.cp(xT[i][:, 16:528], bank(1)[0:64, :], r=[bk(1)], w=[("xT", i)], eng="act")
        for j in range(4):
            for kc in range(8):
                p.mm(bank(0)[:, 0:128], hnT[:, kc, j * 128:(j + 1) * 128], WV2[:, kc, :],
                     start=(kc == 0), stop=(kc == 7), r=["hnT", "WV2"], w=[bk(0)])
            p.cp(VsA[:, st * 4 + j, 0:64], bank(0)[:, 0:64], r=[bk(0), "VsA"], w=["VsA"], eng="act")
            p.cp(VwA[:, 4 + j, 0:64], bank(0)[:, 64:128], r=[bk(0), "VwA"], w=["VwA"], eng="act")
        for kc in range(8):
            p.mm(bank(1)[0:12, :], WG[:, kc, :], hnT[:, kc, :], start=(kc == 0), stop=(kc == 7),
                 r=["WG", "hnT"], w=[bk(1)])
        p.act(gsb[:], bank(1)[0:12, :], AF.Sigmoid, r=[bk(1)], w=["gsb"])
        p.dma(gscr[st % 2].rearrange("o (a b) -> (o a) b", a=12), gsb[:], r=["gsb"],
              w=[("gscr", st % 2)])
        if dbg == 1:
            return p.finish()
        for kv in range(2):
            x3 = xT[kv][:].rearrange("p (i s) -> p i s", s=16)
            for l0 in range(0, 32, 4):
                wt, wk_ = w1_piece(kv, l0)
                for mt in range(2):
                    bb = 1 if mt == 0 else 6
                    for li in range(4):
                        l = l0 + li
                        rhs = x3[:, 0:32, l] if l < 16 else x3[:, 1:33, l - 16]
                        p.mm(bank(bb)[:, 0:32], wt[:, li, mt * 128:(mt + 1) * 128], rhs,
                             start=(l == 0), stop=(l == 31), r=[wk_, ("xT", kv)], w=[bk(bb)])
            for mt in range(2):
                bb = 1 if mt == 0 else 6
                if kv == 0:
                    gelu_to(hidK[:, mt, :], bank(bb)[:, 0:32], c1[:, mt:mt + 1], [bk(bb)], ["hidK"])
                else:
                    if st % 4 == 0 and mt == 0:
                        p.memset(hidV[:], 0.0, w=["hidV"])
                    gelu_to(hidV[:, mt, (st % 4) * 32:(st % 4) * 32 + 32], bank(bb)[:, 0:32],
                            c1[:, 2 + mt:3 + mt], [bk(bb)], ["hidV"])
            if kv == 0:
                par = nU[0] % 2
                nU[0] += 1
                b0, b1 = 2 + 2 * par, 3 + 2 * par
                for v, b in ((0, b0), (1, b1)):
                    for mt in range(2):
                        p.mm(bank(b)[:, 0:32], W2K[:, mt, v, :], hidK[:, mt, :],
                             start=(mt == 0), stop=(mt == 1), r=["W2K", "hidK"], w=[bk(b)])
                sl = slice(st * 32, st * 32 + 32)
                rope_out(KcT2[:, sl], bank(b0)[:, 0:32], bank(b1)[:, 0:32], ccos[:, sl], csin[:, sl],
                         [bk(b0), bk(b1)], ["KcT2"], 32)
            else:
                for mt in range(2):
                    p.mm(bank(1)[:, 0:64], hidV[:, mt, :], W2V[:, mt, :],
                         start=(mt == 0), stop=(mt == 1), r=["W2V", "hidV"], w=[bk(1)])
                p.cp(VcA[:, st // 4, 0:64], bank(1)[:, 0:64], r=[bk(1), "VcA"], w=["VcA"], eng="act")
            p.cp(xT[kv][:, 0:16], xT[kv][:, 512:528], r=[("xT", kv)], w=[("xT", kv)], eng="pool")
        if dbg == 2:
            return p.finish()
        for j in range(4):
            qb = st * 4 + j
            qsl[0] = slice(j * 128, (j + 1) * 128)
            tsl = qsl[0]
            p.dma(G64b[j % 2][64:65, :].rearrange("p (a b) -> p a b", a=12),
                  gscr[st % 2].rearrange("o (a b) -> o a b", a=12)[:, :, tsl],
                  r=[("gscr", st % 2)], w=[("G64", j % 2)])

            def finish_branch(br, first):
                p.ts(zr[64:65, :], bank(0)[64:65, :], TINY, None, ALU.max, r=[bk(0)], w=["zr"])
                p.recip(zr[64:65, :], zr[64:65, :], r=["zr"], w=["zr"])
                g3 = G64b[j % 2][64:65, :].rearrange("p (h b t) -> p h b t", h=4, b=3)
                for par in range(2):
                    for hpl in range(2):
                        hl = 2 * hpl + par
                        c0 = (par * 2 + hpl) * 128
                        p.tt(Rr[64:65, c0:c0 + 128], zr[64:65, c0:c0 + 128], g3[:, hl, br, :],
                             ALU.mult, r=["zr", ("G64", j % 2)], w=["Rr"])
                p.cp(osb[:], bank(0)[0:64, :], r=[bk(0)], w=["osb"], eng="act")

                def part2(first=first):
                    p.mm(bank(1)[0:64, :], ones[64:65, :], Rr[64:65, :], start=True, stop=True,
                         r=["ones", "Rr"], w=[bk(1)])
                    if first:
                        p.tt(acc[:], osb[:], bank(1)[0:64, :], ALU.mult, r=["osb", bk(1)], w=["acc"])
                    else:
                        p.tt(tmpo[:], osb[:], bank(1)[0:64, :], ALU.mult, r=["osb", bk(1)],
                             w=["tmpo"])
                        p.tt(acc[:], acc[:], tmpo[:], ALU.add, r=["tmpo", "acc"], w=["acc"])
                return part2

            pend = []
            pdepth = [1]

            def pend_push(fn):
                pend.append(fn)
                while len(pend) > pdepth[0]:
                    pend.pop(0)()

            def pend_flush():
                while pend:
                    pend.pop(0)()

            ncc = qb // 16 + 1
            r_ = qb % 16
            for cc in range(ncc):
                biases = []
                lastc = (cc == ncc - 1)
                if lastc:
                    biases.append((ident[:], pmask[:, 1 if cc == 0 else 0, r_, :], ["ident", "pmask"], 128))
                elif cc == 0:
                    biases.append((ident[:], r0mask[:], ["ident", "r0mask"]))
                sp_, sb_ = attn_chunk(KcT2, slice(cc * 128, (cc + 1) * 128), None, biases,
                                      cc == 0, lastc, ["KcT2"])
                p.act(PcT[:, cc, :], bank(sb_), AF.Exp, r=[bk(sb_)], w=[("PcT", cc)], scale=0.125)
                pend_push(lambda cc=cc, lastc=lastc: p.mm(
                    bank(0)[0:65, :], VcA[:, cc, :], PcT[:, cc, :], start=(cc == 0), stop=lastc,
                    r=[("PcT", cc), "VcA"], w=[bk(0)]))
            pend_flush()
            for par in range(2):
                for hpl in range(2):
                    hi = par * 2 + hpl
                    c0 = hi * 128
                    ib = 4 + (hi % 2)
                    for cc in range(ncc):
                        p.mm(bank(ib)[:, 0:257], PcT[:, cc, c0:c0 + 128], wfull[:, cc, :],
                             start=(cc == 0), stop=(cc == ncc - 1),
                             r=[("PcT", cc), "wfull"], w=[bk(ib)])
                    p.ts(zq[:], bank(ib)[:, 256:257], TINY, None, ALU.max, r=[bk(ib)], w=["zq"])
                    p.recip(zq[:], zq[:], r=["zq"], w=["zq"])
                    if hi == 0:
                        p.ts(imp[:], bank(ib)[:, 0:256], zq[:], None, ALU.mult, r=[bk(ib), "zq"],
                             w=["imp"])
                    else:
                        p.stt(imp[:], bank(ib)[:, 0:256], zq[:], imp[:], ALU.mult, ALU.add,
                              r=[bk(ib), "zq", "imp"], w=["imp"])
            fin0 = finish_branch(0, True)
            if dbg == 3 or dbg == 100 + j * 10 + 3:
                return p.finish()
            nb = 2 * qb + 2
            p.cp(selbuf[:, 0:nb], imp[:, 0:nb], r=["imp"], w=["selbuf"])
            lo = 2 * qb - 1
            k0 = 0
            if lo < 0:
                lo, k0 = 0, 1
            nfx = 3 - k0
            p.tt(selbuf[:, lo:lo + nfx], selbuf[:, lo:lo + nfx], fix3[:, k0:3], ALU.mult,
                 r=["selbuf", "fix3"], w=["selbuf"])
            p.tt(selbuf[:, lo:lo + nfx], selbuf[:, lo:lo + nfx], fix3[:, 3 + k0:6], ALU.add,
                 r=["selbuf", "fix3"], w=["selbuf"])
            p.memset(selbuf[:, 0:1], 3.0 * FORCE, w=["selbuf"])
            p.s.op("dve", lambda e: e.max(out=mx8[:], in_=selbuf[:]), ["selbuf"], ["mx8"])
            p.s.op("dve", lambda e: e.match_replace(out=work[:], in_to_replace=mx8[:],
                                                    in_values=selbuf[:], imm_value=-2.0 * FORCE),
                   ["selbuf", "mx8"], ["work"])
            p.s.op("dve", lambda e: e.max(out=mx8[:], in_=work[:]), ["work"], ["mx8"])
            p.s.op("dve", lambda e: e.tensor_reduce(out=thr[:], in_=mx8[:], axis=AX.X, op=ALU.min),
                   ["mx8"], ["thr"])
            p.ts(Bq[:], selbuf[:], thr[:], 1.0, ALU.is_ge, ALU.subtract, r=["selbuf", "thr"], w=["Bq"])
            nhalf = 1 if nb <= 128 else 2
            for hf in range(nhalf):
                p.tr(psT[:, hf, :], Bq[:, hf * 128:(hf + 1) * 128], ident[:], r=["Bq", "ident"],
                     w=["psT"])
            for hf in range(nhalf):
                for rep in range(4):
                    p.cp(BT[:, hf, rep * 128:(rep + 1) * 128], psT[:, hf, :], r=["psT"], w=["BT"],
                         eng=("act" if rep % 2 == 0 else "dve"))
            if dbg == 5 or dbg == 100 + j * 10 + 5:
                return p.finish()
            k_lo = max(0, qb - 4)
            sdepth[0] = 4
            pdepth[0] = 2
            for kc in range(k_lo, qb + 1):
                biases = []
                if kc == qb - 4:
                    biases.append((ident[:], cmask[:, 1, :], ["ident", "cmask"]))
                if kc == qb:
                    biases.append((ident[:], cmask[:, 0, :], ["ident", "cmask"]))
                slot = 4 + j - (qb - kc)
                sp_, sb_ = attn_chunk(KwT2, slice(slot * 128, (slot + 1) * 128), None, biases,
                                      kc == k_lo, kc == qb, ["KwT2"])
                p.act(PT[sp_][:], bank(sb_), AF.Exp, r=[bk(sb_)], w=[("PT", sp_)], scale=0.125)
                if kc == min(k_lo + 1, qb) and fin0 is not None:
                    fin0()
                    fin0 = None
                pend_push(lambda kc=kc, sp_=sp_, slot=slot: p.mm(
                    bank(0)[0:65, :], VwA[:, slot, :], PT[sp_][:], start=(kc == k_lo),
                    stop=(kc == qb), r=[("PT", sp_), "VwA"], w=[bk(0)]))
            pend_flush()
            fin2 = finish_branch(2, False)
            if fin0 is not None:
                fin0()
                fin0 = None
            if dbg == 4 or dbg == 100 + j * 10 + 4:
                return p.finish()
            sdepth[0] = 4
            pdepth[0] = 2
            for kc in range(qb + 1):
                biases = [(emat[:, kc % 64, :], BT[:, kc // 64, :], ["emat", "BT"])]
                if kc == qb:
                    biases.append((ident[:], cmask[:, 0, :], ["ident", "cmask"]))
                sp_, sb_ = attn_chunk(KsT2, slice(kc * 128, (kc + 1) * 128), None, biases,
                                      kc == 0, kc == qb, ["KsT2"])
                p.act(PT[sp_][:], bank(sb_), AF.Exp, r=[bk(sb_)], w=[("PT", sp_)], scale=0.125)
                if kc == min(1, qb) and fin2 is not None:
                    fin2()
                    fin2 = None
                pend_push(lambda kc=kc, sp_=sp_: p.mm(
                    bank(0)[0:65, :], VsA[:, kc, :], PT[sp_][:], start=(kc == 0), stop=(kc == qb),
                    r=[("PT", sp_), "VsA"], w=[bk(0)]))
            pend_flush()
            fin1 = finish_branch(1, False)
            fin1()
            sdepth[0] = 2
            if dbg == 6 or dbg == 100 + j * 10 + 6:
                return p.finish()
            p.cp(oacc[:], acc[:].rearrange("p (c q) -> p c q", c=4), r=["acc"], w=["oacc"])
            if fused:
                for dst in io["o_dst"](qb):
                    p.dma(dst, oacc[:], r=["oacc"], w=[io["dkey"]], q="pool")
            else:
                p.dma(oT_out[:, :, tok0 + j * 128:tok0 + (j + 1) * 128], oacc[:], r=["oacc"],
                      q="pool", is_output=True)
            if dbg == 100 + j * 10 + 7:
                return p.finish()
        p.cp(KwT2[:, 0:512], KwT2[:, 512:1024], r=["KwT2"], w=["KwT2"], eng="pool")
        p.cp(VwA[:, 0:4, :], VwA[:, 4:8, :], r=["VwA"], w=["VwA"], eng="pool")
        if dbg == 7 + st:
            return p.finish()
    if fused:
        return None
    return p.finish()


def nsa_consts(S):
    NCC = max(1, S // 2048)
    ml = np.arange(128)[:, None]
    q = np.arange(128)[None, :]
    pm = np.zeros((2, 16, 128, 128), np.float32)
    for a in range(2):
        for r in range(16):
            valid = (16 * ml + 15 <= 128 * r + q)
            if a == 1:
                valid = valid & (ml >= 1)
            pm[a, r] = np.where(valid, 0.0, MASKV)
    pmask = pm.astype(NPBF16)
    r0 = np.zeros((128, 512), np.float32)
    r0[0, :] = MASKV
    cur = np.where(ml <= q, 0.0, MASKV).astype(np.float32)
    upper = np.where(ml > q, 0.0, MASKV).astype(np.float32)
    cmask = np.stack([np.tile(cur, (1, 4)), np.tile(upper, (1, 4))], 0).astype(NPBF16)
    emat = np.zeros((64, 128, 128), np.float32)
    for e in range(64):
        emat[e, 2 * e, 0:64] = -MASKV
        emat[e, 2 * e + 1, 64:128] = -MASKV
    ws = [1, 2, 2, 2, 1]
    wfull = np.zeros((NCC * 128, 257), np.float32)
    for m in range(1, NCC * 128):
        n = m - 1
        for j in range(256):
            i = n - 4 * j + 1
            if 0 <= i <= 4:
                wfull[m, j] = ws[i]
    wfull[:, 256] = 1.0
    fix = np.zeros((128, 6), np.float32)
    lo = np.arange(128) < 64
    fix[:, 0] = np.where(lo, 0.0, 1.0)
    fix[:, 3] = np.where(lo, FORCE, 0.0)
    fix[:, 4] = 2.0 * FORCE
    fix[:, 5] = np.where(lo, -FORCE, FORCE)
    cpos = 16 * np.arange(NCC * 128) + 15
    ccos, csin = rope_tables(cpos)
    return dict(pmask=pmask, r0mask=r0.astype(NPBF16), cmask=cmask, emat=emat.astype(NPBF16),
                wfull=wfull.astype(NPBF16), fix3=fix, ccos_t=ccos, csin_t=csin, ident=ident_np())


def nsa_weights(a_w_in_l, cmp_pos_l, g):
    W = a_w_in_l
    q0 = g * 256
    def kcol(i):
        return W[:, 1024 + i * 256 + g * 64: 1024 + i * 256 + (g + 1) * 64]
    kc_, vc_, ks_, vs_, kw_, vw_ = [kcol(i) for i in range(6)]
    wg = W[:, 1024 + 6 * 256 + g * 12: 1024 + 6 * 256 + (g + 1) * 12]
    return dict(wq=np.ascontiguousarray(W[:, q0:q0 + 256]),
                wk3=np.ascontiguousarray(np.concatenate([kc_, ks_, kw_], 1)),
                wv3=np.ascontiguousarray(np.concatenate([vc_, vs_, vw_], 1)),
                wg=np.ascontiguousarray(wg),
                posT=np.ascontiguousarray(np.transpose(cmp_pos_l, (2, 0, 1))))


SEQ = 16384
NB = 2
CH = 4096
NPHASE = 99


def _run(nc, in_maps):
    res = run_bass_kernel_spmd(nc, in_maps, core_ids=list(range(8)))
    return res.results


def _cwb(conv_w, conv_b):
    return np.ascontiguousarray(np.concatenate([conv_w, conv_b[None]], 0).T.astype(np.float32))


def _chunk_with_halo(x_b, c, halo):
    lo = c * CH - halo
    if lo >= 0:
        return np.ascontiguousarray(x_b[lo:(c + 1) * CH])
    pad = np.zeros((-lo,) + x_b.shape[1:], x_b.dtype)
    return np.ascontiguousarray(np.concatenate([pad, x_b[0:(c + 1) * CH]], 0))


def kernel_unfused(x, norm_attn, norm_ffn, a_w_in, a_cmp_pos, a_cmp_w1, a_cmp_w2, a_w_out, kv_norm,
           b_w_kv, b_w_q, b_sinks, b_w_out, ffn_w_in, ffn_conv_w, ffn_conv_b, ffn_w_out,
           final_norm):
    f32 = lambda a: np.ascontiguousarray(np.asarray(a, dtype=np.float32))
    x = f32(x)
    norm_attn, norm_ffn = f32(norm_attn), f32(norm_ffn)
    a_w_in, a_cmp_pos, a_cmp_w1, a_cmp_w2, a_w_out = map(f32, (a_w_in, a_cmp_pos, a_cmp_w1,
                                                                a_cmp_w2, a_w_out))
    kv_norm, b_w_kv, b_w_q, b_sinks, b_w_out = map(f32, (kv_norm, b_w_kv, b_w_q, b_sinks, b_w_out))
    ffn_w_in, ffn_conv_w, ffn_conv_b, ffn_w_out, final_norm = map(
        f32, (ffn_w_in, ffn_conv_w, ffn_conv_b, ffn_w_out, final_norm))
    h = x
    ident = ident_np()
    gfin = np.ascontiguousarray(final_norm[None, :])
    cosA, sinA = rope_tables(np.arange(SEQ))
    constsA = nsa_consts(SEQ)

    for l in range(2):
        ncA = build_A(SEQ)
        maps = []
        for i in range(8):
            b, g = divmod(i, 4)
            m = dict(h_in=h[b], g_attn=gT_np(norm_attn[l]), w1=a_cmp_w1[l], w2=a_cmp_w2[l],
                     cos_t=cosA, sin_t=sinA)
            m.update(constsA)
            m.update(nsa_weights(a_w_in[l], a_cmp_pos[l], g))
            maps.append(m)
        resA = _run(ncA, maps)
        oT_full = np.zeros((NB, 16, 64, SEQ), NPBF16)
        for i in range(8):
            b, g = divmod(i, 4)
            o = resA[i]["oT_out"]
            for par in range(2):
                for hpl in range(2):
                    oT_full[b, g * 4 + 2 * hpl + par] = o[:, par * 2 + hpl, :]
        oT_full = oT_full.reshape(NB, 1024, SEQ)
        ncB = build_B([1] + [4] * 8, 1)
        maps = []
        for i in range(8):
            b, c = divmod(i, 4)
            maps.append(dict(
                h_in=_chunk_with_halo(h[b], c, 128),
                oT_in=np.ascontiguousarray(_chunk_with_halo(oT_full[b].T, c, 128).T),
                w_o=a_w_out[l], w_in=ffn_w_in[l], w_out=ffn_w_out[l],
                cwb=_cwb(ffn_conv_w[l], ffn_conv_b[l]), g_ffn=gT_np(norm_ffn[l]), g_fin=gfin,
                ident=ident))
        resB = _run(ncB, maps)
        h = np.stack([np.concatenate([resB[b * 4 + c]["h_out"] for c in range(4)], 0)
                      for b in range(NB)], 0)

    hkv = h
    for l in range(2, 4):
        j = l - 2
        ncC = build_C([2] + [4] * 8, 2, final_norm=(l == 3))
        maps = []
        for i in range(8):
            b, c = divmod(i, 4)
            pos = c * CH - 256 + np.arange(CH + 256)
            cos_t, sin_t = rope_tables(pos)
            maps.append(dict(
                h_in=_chunk_with_halo(h[b], c, 256), hkv_in=_chunk_with_halo(hkv[b], c, 256),
                w_q=b_w_q[j], w_kv=b_w_kv, sinks_b=sinks_row(b_sinks[j]),
                g_attn=gT_np(norm_attn[l]), g_kv=gT_np(kv_norm), cos_t=cos_t, sin_t=sin_t,
                masks=swa_masks(c > 0), w_o=b_w_out[j], w_in=ffn_w_in[l], w_out=ffn_w_out[l],
                cwb=_cwb(ffn_conv_w[l], ffn_conv_b[l]), g_ffn=gT_np(norm_ffn[l]), g_fin=gfin,
                ident=ident))
        resC = _run(ncC, maps)
        h = np.stack([np.concatenate([resC[b * 4 + c]["h_out"] for c in range(4)], 0)
                      for b in range(NB)], 0)
    return np.ascontiguousarray(h.astype(np.float32))


def build_fused(nphase=99):
    from concourse.bass import ds
    nph = [0]

    def stop():
        nph[0] += 1
        return nph[0] >= nphase

    nc = bass.Bass("TRN2", target_bir_lowering=False)
    p = Prog(nc)
    S = SEQ
    WB = 128 + CH
    WC = 256 + CH
    SUBW = 11 * 128
    xA = p.din("xA", [S, D], F32)
    xB = p.din("xB", [WB, D], F32)
    flag = p.din("flag", [128, 1], F32)
    out = p.dout("out", [CH, D], F32)
    oTloc = [nc.dram_tensor(f"oTloc{l}", [12 * 64, 4 * SUBW], BF16) for l in range(2)]
    OTb = [nc.dram_tensor(f"OTb{l}", [12 * 256, 4 * SUBW], BF16) for l in range(2)]
    oTwin = nc.dram_tensor("oTwin", [3 * 256, 4 * SUBW], BF16).ap()
    hloc = [nc.dram_tensor(f"hloc{k}", [CH, D], F32) for k in range(3)]
    Hb = [nc.dram_tensor(f"Hb{k}", [S, D], F32) for k in range(3)]
    hwin = nc.dram_tensor("hwin", [WC, D], F32).ap()
    hkvwin = nc.dram_tensor("hkvwin", [WC, D], F32).ap()
    rg = [[0, 1, 2, 3], [4, 5, 6, 7]]
    PID = p.s.pid

    def gather_group(src, dst, nchunk, rows, rk, wk):
        def fn(e, sem):
            for k in range(nchunk):
                e.collective_compute(
                    "AllGather", ALU.bypass, replica_groups=rg,
                    ins=[src.ap()[k * rows:(k + 1) * rows, :].opt()],
                    outs=[dst.ap()[k * 4 * rows:(k + 1) * 4 * rows, :].opt()]).then_inc(sem)
        p.s.cc(fn, [rk], [wk], n=nchunk)

    def h_row(tok):
        rank, rem = divmod(tok, CH)
        k, r = divmod(rem, 256)
        return (k * 4 + rank) * 256 + r

    def win_copy(dst, src, halo, q, rk, wk):
        s5 = src.rearrange("(k g r e) d -> k g r (e d)", k=16, g=4, e=8)
        dm = dst[halo:halo + CH, :].rearrange("(k g r e) d -> k g r (e d)", k=16, g=1, e=8)
        dh = dst[0:halo, :].rearrange("(k g r e) d -> k g r (e d)", k=1, g=1, e=8)
        h8 = halo // 8
        p.dmaf(lambda e: e.dma_start(
            out=dm, in_=s5[:, ds(PID(e, "c", lambda pid: pid % 4), 1), :, :]),
            r=[rk], w=[wk], q=q)
        p.dmaf(lambda e: e.dma_start(
            out=dh, in_=s5[15:16, ds(PID(e, "cm1", lambda pid: (pid + 3) % 4), 1), 32 - h8:32, :]),
            r=[rk], w=[wk], q=q)

    for l in range(2):
        p.sfx = f"_A{l}"
        rk = [] if l == 0 else [f"Hb{l - 1}"]
        O5 = oTloc[l].ap().rearrange("(c s d) (b t) -> c s d b t", c=4, s=3, b=4)

        def o_dst(qb, O5=O5):
            c, sl = divmod(qb, 32)
            sl += 1
            dsts = [O5[c, sl // 11, :, :, (sl % 11) * 128:(sl % 11) * 128 + 128]]
            if sl == 32 and c < 3:
                dsts.append(O5[c + 1, 0, :, :, 0:128])
            return dsts

        build_A(S, p=p, io=dict(h_ap=(xA if l == 0 else Hb[l - 1].ap()),
                                h_row=((lambda t: t) if l == 0 else h_row),
                                rkeys=rk, dkey=f"oTloc{l}", o_dst=o_dst,
                                o_zero=O5[0, 0, :, :, 0:128]))
        p.phase_end()
        gather_group(oTloc[l], OTb[l], 12, 64, f"oTloc{l}", f"OTb{l}")
        if stop():
            return p.finish(), dict(p.dins)
        p.sfx = f"_B{l}"
        O3 = OTb[l].ap().rearrange("(c r) f -> c r f", c=4)
        p.dmaf(lambda e, O3=O3: e.dma_start(
            out=oTwin.rearrange("(c r) f -> c r f", c=1),
            in_=O3[ds(PID(e, "c", lambda pid: pid % 4), 1), :, :]),
            r=[f"OTb{l}"], w=["oTwin"], q="act")
        if l == 0:
            h_ap = xB
            rkb = ["oTwin"]
        else:
            win_copy(hwin[0:WB, :], Hb[l - 1].ap(), 128, "act", f"Hb{l - 1}", "hwin")
            h_ap = hwin[0:WB, :]
            rkb = ["oTwin", "hwin"]
        W5 = oTwin.rearrange("(s g d) (b t) -> d s g b t", s=3, g=4, b=4)

        def oT_ap(wt, g, W5=W5):
            return W5[:, wt // 11, g, :, (wt % 11) * 128:(wt % 11) * 128 + 128]

        build_B([1] + [4] * 8, 1, p=p,
                io=dict(h_ap=h_ap, oT_ap=oT_ap, h_dst=hloc[l].ap(), flag=flag,
                        rkeys=rkb, dkey=f"hloc{l}"))
        p.phase_end()
        gather_group(hloc[l], Hb[l], 16, 256, f"hloc{l}", f"Hb{l}")
        if stop():
            return p.finish(), dict(p.dins)

    for l in range(2, 4):
        p.sfx = f"_C{l}"
        last = (l == 3)
        if l == 2:
            win_copy(hkvwin, Hb[1].ap(), 256, "sp", "Hb1", "hkvwin")
            h_ap, hkv_ap, rkc = hkvwin, hkvwin, ["hkvwin"]
        else:
            win_copy(hwin, Hb[2].ap(), 256, "sp", "Hb2", "hwin")
            h_ap, hkv_ap, rkc = hwin, hkvwin, ["hwin", "hkvwin"]
        build_C([2] + [4] * 8, 2, final_norm=last, p=p,
                io=dict(h_ap=h_ap, hkv_ap=hkv_ap, h_dst=(out if last else hloc[2].ap()), flag=flag,
                        rkeys=rkc, dkey=(None if last else "hloc2")))
        if not last:
            p.phase_end()
            gather_group(hloc[2], Hb[2], 16, 256, "hloc2", "Hb2")
            if stop():
                return p.finish(), dict(p.dins)
    return p.finish(), dict(p.dins)


def kernel(x, norm_attn, norm_ffn, a_w_in, a_cmp_pos, a_cmp_w1, a_cmp_w2, a_w_out, kv_norm,
           b_w_kv, b_w_q, b_sinks, b_w_out, ffn_w_in, ffn_conv_w, ffn_conv_b, ffn_w_out,
           final_norm):
    f32 = lambda a: np.ascontiguousarray(np.asarray(a, dtype=np.float32))
    x = f32(x)
    norm_attn, norm_ffn = f32(norm_attn), f32(norm_ffn)
    a_w_in, a_cmp_pos, a_cmp_w1, a_cmp_w2, a_w_out = map(f32, (a_w_in, a_cmp_pos, a_cmp_w1,
                                                                a_cmp_w2, a_w_out))
    kv_norm, b_w_kv, b_w_q, b_sinks, b_w_out = map(f32, (kv_norm, b_w_kv, b_w_q, b_sinks, b_w_out))
    ffn_w_in, ffn_conv_w, ffn_conv_b, ffn_w_out, final_norm = map(
        f32, (ffn_w_in, ffn_conv_w, ffn_conv_b, ffn_w_out, final_norm))
    nc, dins = build_fused(NPHASE)
    ident = ident_np()
    gfin = np.ascontiguousarray(final_norm[None, :])
    cosA, sinA = rope_tables(np.arange(SEQ))
    constsA = nsa_consts(SEQ)
    maps = []
    for i in range(8):
        b, c = divmod(i, 4)
        g = c
        m = dict(xA=x[b], xB=_chunk_with_halo(x[b], c, 128),
                 flag=np.full((128, 1), 0.0 if c == 0 else 1.0, np.float32))
        for l in range(2):
            a = dict(g_attn=gT_np(norm_attn[l]), w1=a_cmp_w1[l], w2=a_cmp_w2[l],
                     cos_t=cosA, sin_t=sinA)
            a.update(constsA)
            a.update(nsa_weights(a_w_in[l], a_cmp_pos[l], g))
            for k, v in a.items():
                m[f"{k}_A{l}"] = v
            bb = dict(w_o=a_w_out[l], w_in=ffn_w_in[l], w_out=ffn_w_out[l],
                      cwb=_cwb(ffn_conv_w[l], ffn_conv_b[l]), g_ffn=gT_np(norm_ffn[l]), g_fin=gfin,
                      ident=ident)
            for k, v in bb.items():
                m[f"{k}_B{l}"] = v
        pos = c * CH - 256 + np.arange(CH + 256)
        cos_t, sin_t = rope_tables(pos)
        for l in range(2, 4):
            j = l - 2
            cc = dict(w_q=b_w_q[j], w_kv=b_w_kv, sinks_b=sinks_row(b_sinks[j]),
                      g_attn=gT_np(norm_attn[l]), g_kv=gT_np(kv_norm), cos_t=cos_t, sin_t=sin_t,
                      masks=swa_masks(c > 0), w_o=b_w_out[j], w_in=ffn_w_in[l], w_out=ffn_w_out[l],
                      cwb=_cwb(ffn_conv_w[l], ffn_conv_b[l]), g_ffn=gT_np(norm_ffn[l]), g_fin=gfin,
                      ident=ident)
            for k, v in cc.items():
                m[f"{k}_C{l}"] = v
        m = {k: v for k, v in m.items() if k in dins}
        maps.append(m)
    res = _run(nc, maps)
    h = np.stack([np.concatenate([res[b * 4 + c]["out"] for c in range(4)], 0)
                  for b in range(NB)], 0)
    return np.ascontiguousarray(h.astype(np.float32))
```

```python
import numpy as np
import ml_dtypes
import concourse.bass as bass
import concourse.mybir as mybir
from concourse.bass_utils import run_bass_kernel_spmd

F32 = mybir.dt.float32
BF16 = mybir.dt.bfloat16
AF = mybir.ActivationFunctionType
ALU = mybir.AluOpType
AX = mybir.AxisListType

NPBF16 = ml_dtypes.bfloat16

D = 1024
DFF = 2816
EPS = 1e-6
MASKV = -240000.0

COMPUTE = ("pe", "act", "dve", "pool")
EPOCH = 30000
NSLOT = 12
SAME_ENG_SYNC = True
ZERO_BIAS = True


class Sched:
    def __init__(self, nc):
        self.nc = nc
        self.streams = {e: [] for e in COMPUTE + ("sp",)}
        self.cnt = {e: 0 for e in COMPUTE}
        self.known = {e: {} for e in self.streams}
        self.known_dma = {e: set() for e in self.streams}
        self.last_w = {}
        self.readers = {}
        self.ndma = {e: 0 for e in self.streams}
        self.sems = {}
        self.nsem = 0
        self.out_dmas = []
        self.ncc = 0
        self.cc_pending = []
        self.snap = {e: [] for e in COMPUTE}
        self.snapd = {}

    def _sem(self, name):
        if name not in self.sems:
            self.sems[name] = self.nc.alloc_semaphore(name=name)
        return self.sems[name]

    def _ev_wait_args(self, ev):
        kind = ev[0]
        if kind == "c":
            _, eng, idx = ev
            ep, off = divmod(idx, EPOCH)
            return self._sem(f"s_{eng}_{ep}"), off + 1
        elif kind == "x":
            return self._sem(f"x_{ev[1]}"), ev[2]
        else:
            _, q, j = ev
            slot, use = j % NSLOT, j // NSLOT
            return self._sem(f"d_{q}_{slot}"), 16 * (use + 1)

    def _deps(self, eng, reads, writes):
        deps = set()
        for k in reads:
            w = self.last_w.get(k)
            if w is not None:
                deps.add(w)
        for k in writes:
            w = self.last_w.get(k)
            if w is not None:
                deps.add(w)
            for r in self.readers.get(k, ()):
                deps.add(r)
        waits = []
        best = {}
        for ev in deps:
            if ev[0] == "c":
                _, src, idx = ev
                if src == eng and eng == "pe":
                    continue
                if self.known[eng].get(src, -1) >= idx:
                    continue
                if best.get(src, -1) < idx:
                    best[src] = idx
            else:
                if ev in self.known_dma[eng]:
                    continue
                waits.append(ev)
                self.known_dma[eng].add(ev)
        for src, idx in best.items():
            self.known[eng][src] = idx
            waits.append(("c", src, idx))
        for ev in list(waits):
            sn = self.snap[ev[1]][ev[2]] if ev[0] == "c" else self.snapd.get(ev)
            if sn is None:
                continue
            kn = self.known[eng]
            for ci, ce in enumerate(COMPUTE):
                if sn[ci] > kn.get(ce, -1) and (ce != eng or True):
                    kn[ce] = sn[ci]
        return waits

    def _snapshot(self, eng):
        kn = self.known[eng]
        return tuple(kn.get(ce, -1) for ce in COMPUTE)

    def _mark(self, ev, reads, writes):
        for k in reads:
            self.readers.setdefault(k, []).append(ev)
        for k in writes:
            self.last_w[k] = ev
            self.readers[k] = []

    def op(self, eng, fn, reads=(), writes=()):
        assert eng in COMPUTE
        waits = self._deps(eng, reads, writes)
        idx = self.cnt[eng]
        self.cnt[eng] += 1
        ev = ("c", eng, idx)
        if not SAME_ENG_SYNC or eng == "pe":
            self.known[eng][eng] = idx
        sn = list(self._snapshot(eng))
        sn[COMPUTE.index(eng)] = max(sn[COMPUTE.index(eng)], idx - 1)
        self.snap[eng].append(tuple(sn))
        self._mark(ev, reads, writes)
        self.streams[eng].append((waits, fn, ev))
        return ev

    def dma(self, q, fn, reads=(), writes=(), is_output=False):
        waits = self._deps(q, reads, writes)
        j = self.ndma[q]
        self.ndma[q] += 1
        if j >= NSLOT:
            prev = ("d", q, j - NSLOT)
            if prev not in self.known_dma[q]:
                waits.append(prev)
                self.known_dma[q].add(prev)
        ev = ("d", q, j)
        self.snapd[ev] = self._snapshot(q)
        self._mark(ev, reads, writes)
        self.streams[q].append((waits, fn, ev))
        if is_output:
            self.out_dmas.append(ev)
        return ev

    def pid(self, e, key="pid", fn=None):
        k = (self.cur_eng, key)
        if k not in self.pid_cache:
            if key == "pid":
                self.pid_cache[k] = e.partition_id()
            else:
                self.pid_cache[k] = e.snap(fn(self.pid(e)))
        return self.pid_cache[k]

    def cc(self, fn, reads=(), writes=(), n=1):
        waits = self._deps("pool", reads, writes)
        ev = ("x", self.ncc, n)
        self.ncc += 1
        self._mark(ev, reads, writes)
        self.streams["pool"].append((waits, fn, ev))
        self.known_dma["pool"].add(ev)
        self.cc_pending.append((ev, tuple(writes)))
        return ev

    def cc_wait(self):
        for ev, writes in self.cc_pending:
            idx = self.cnt["pool"]
            self.cnt["pool"] += 1
            nev = ("c", "pool", idx)
            self.snap["pool"].append(self._snapshot("pool"))
            self._mark(nev, (), writes)
            self.streams["pool"].append(([ev], lambda e: e.nop(), nev))
        self.cc_pending = []

    def barrier(self):
        self.cc_wait()
        evs = []
        for eng in COMPUTE:
            if self.cnt[eng] > 0:
                evs.append(("c", eng, self.cnt[eng] - 1))
        for q, n in self.ndma.items():
            for j in range(max(0, n - NSLOT), n):
                evs.append(("d", q, j))
        for eng in self.streams:
            waits = []
            for ev in evs:
                if ev[0] == "c":
                    if ev[1] == eng:
                        continue
                    if self.known[eng].get(ev[1], -1) >= ev[2]:
                        continue
                    self.known[eng][ev[1]] = ev[2]
                elif ev[0] == "x":
                    continue
                elif ev in self.known_dma[eng]:
                    continue
                else:
                    self.known_dma[eng].add(ev)
                waits.append(ev)
            self.streams[eng].append((waits, None, None))

    def emit(self, final=True):
        nc = self.nc
        if final:
            self.cc_wait()
        final_waits = list(self.out_dmas) if final else []
        self.pid_cache = {}
        with nc.Block() as block:
            def run(engname, e):
                self.cur_eng = engname
                for waits, fn, ev in self.streams[engname]:
                    for w in waits:
                        s, v = self._ev_wait_args(w)
                        e.wait_ge(s, v)
                    if fn is None:
                        continue
                    s, v = self._ev_wait_args(ev)
                    if ev[0] == "x":
                        fn(e, s)
                        continue
                    ins = fn(e)
                    if ev[0] == "c":
                        ins.then_inc(s, 1)
                    else:
                        ins.then_inc(s, 16)
                if engname == "sp":
                    for w in final_waits:
                        s, v = self._ev_wait_args(w)
                        e.wait_ge(s, v)

            @block.tensor
            def _(e):
                run("pe", e)

            @block.scalar
            def _(e):
                run("act", e)

            @block.vector
            def _(e):
                run("dve", e)

            @block.gpsimd
            def _(e):
                run("pool", e)

            @block.sync
            def _(e):
                run("sp", e)
        for k in self.streams:
            self.streams[k] = []


class Prog:
    def __init__(self, nc):
        from contextlib import ExitStack
        self.nc = nc
        self.s = Sched(nc)
        self.es = ExitStack()
        self.ndram = 0
        self.sfx = ""
        self.dins = {}
        self.ext = {}

    def sb(self, name, shape, dt):
        return self.es.enter_context(self.nc.sbuf_tensor("sb_" + name + self.sfx, list(shape), dt))

    def ps(self, name, shape, dt=F32):
        return self.es.enter_context(self.nc.psum_tensor("ps_" + name + self.sfx, list(shape), dt))

    def din(self, name, shape, dt):
        nm = name + self.sfx
        if nm in self.ext:
            return self.ext[nm]
        self.dins[nm] = (tuple(shape), dt)
        return self.nc.dram_tensor(nm, list(shape), dt, kind="ExternalInput").ap()

    def dint(self, name, shape, dt):
        return self.nc.dram_tensor(name, list(shape), dt)

    def phase_end(self):
        from contextlib import ExitStack
        self.s.barrier()
        self.s.emit(final=False)
        self.es.close()
        self.es = ExitStack()

    def dmaf(self, fn, r=(), w=(), q="sp", is_output=False):
        return self.s.dma(q, fn, r, w, is_output)

    def dout(self, name, shape, dt):
        return self.nc.dram_tensor(name, list(shape), dt, kind="ExternalOutput").ap()

    def dma(self, out, in_, r=(), w=(), q="sp", is_output=False):
        return self.s.dma(q, lambda e: e.dma_start(out=out, in_=in_), r, w, is_output)

    def mm(self, out, lhsT, rhs, start, stop, r=(), w=()):
        return self.s.op("pe", lambda e: e.matmul(out, lhsT, rhs, start=start, stop=stop), r, w)

    def tr(self, out, in_, ident, r=(), w=()):
        return self.s.op("pe", lambda e: e.transpose(out, in_, ident), r, w)

    def act(self, out, in_, func, r=(), w=(), bias=None, scale=None, accum_out=None):
        kw = {}
        if bias is not None:
            kw["bias"] = bias
        if scale is not None:
            kw["scale"] = scale
        if accum_out is not None:
            kw["accum_out"] = accum_out
        return self.s.op("act", lambda e: e.activation(out, in_, func, **kw), r, w)

    def tt(self, out, in0, in1, op, r=(), w=(), eng="dve"):
        return self.s.op(eng, lambda e: e.tensor_tensor(out, in0, in1, op), r, w)

    def ts(self, out, in0, s1, s2, op0, op1=None, r=(), w=(), eng="dve", accum_out=None):
        kw = {}
        if accum_out is not None:
            kw["accum_out"] = accum_out
        if op1 is None:
            return self.s.op(eng, lambda e: e.tensor_scalar(out, in0, s1, s2, op0, **kw), r, w)
        return self.s.op(eng, lambda e: e.tensor_scalar(out, in0, s1, s2, op0, op1, **kw), r, w)

    def stt(self, out, in0, scalar, in1, op0, op1, r=(), w=(), eng="dve"):
        return self.s.op(eng, lambda e: e.scalar_tensor_tensor(out, in0, scalar, in1, op0, op1), r, w)

    def cp(self, out, in_, r=(), w=(), eng="dve"):
        if eng == "act":
            return self.s.op("act", lambda e: e.copy(out, in_), r, w)
        return self.s.op(eng, lambda e: e.tensor_copy(out, in_), r, w)

    def recip(self, out, in_, r=(), w=()):
        return self.s.op("dve", lambda e: e.reciprocal(out, in_), r, w)

    def memset(self, ap, val, w=(), eng="dve"):
        return self.s.op(eng, lambda e: e.memset(ap, val), (), w)

    def finish(self):
        self.s.emit()
        self.es.close()
        return self.nc


def load_cast_weight(p, w_dram, dst, nk, ncols, stage, tag, chunk_cols=1024):
    i = 0
    for kc in range(nk):
        for c0 in range(0, ncols, chunk_cols):
            cw = min(chunk_cols, ncols - c0)
            stg = stage[i % 2]
            p.dma(stg[:, 0:cw], w_dram[kc * 128:(kc + 1) * 128, c0:c0 + cw],
                  w=[("stg", i % 2)])
            p.cp(dst[:, kc, c0:c0 + cw], stg[:, 0:cw], r=[("stg", i % 2)],
                 w=[(tag, kc)], eng="pool")
            i += 1


def rmsnorm_tile(p, x_ap, gain_bc, out_ap, scr, keys_r, keys_w, tagk):
    sq, ss, sd, rs = scr
    p.act(sq, x_ap, AF.Square, r=keys_r, w=[("sq", tagk), ("ss", tagk)], accum_out=ss)
    p.act(sd, ss, AF.Sqrt, r=[("ss", tagk)], w=[("sd", tagk)], bias=EPS, scale=1.0 / D)
    p.recip(rs, sd, r=[("sd", tagk)], w=[("rs", tagk)])
    if gain_bc is None:
        p.ts(out_ap, x_ap, rs, None, ALU.mult, r=list(keys_r) + [("rs", tagk)], w=keys_w)
    else:
        p.stt(out_ap, x_ap, rs, gain_bc, ALU.mult, ALU.mult,
              r=list(keys_r) + [("rs", tagk), "gains"], w=keys_w)


class FFNCtx:
    def __init__(self, p, pre, max_nt=4):
        self.p = p
        self.max_nt = max_nt
        nt = max_nt
        self.wo = p.sb(pre + "wo", [64, 16, 1024], BF16)
        self.woch = [p.sb(pre + f"woch{i}", [128, 512], BF16) for i in range(2)]
        self.fstage = [p.sb(pre + f"fstg{i}", [128, 2048], F32) for i in range(2)]
        self.stage = [self.fstage[i][:, 0:1024] for i in range(2)]
        self.gT = p.sb(pre + "gT", [128, 8], F32)
        self.wch = [p.sb(pre + f"wch{i}", [128, 8, 2, 128], BF16) for i in range(2)]
        self.h1 = p.sb(pre + "h1", [128, nt, 1024], F32)
        self.oT = p.sb(pre + "oT", [64, 16, nt * 128], BF16)
        self.hn = p.sb(pre + "hn", [128, 1024], BF16)
        self.hnT = p.sb(pre + "hnT", [128, 8, nt * 128], BF16)
        self.actT = p.sb(pre + "actT", [128, 22, nt * 128], BF16)
        self.usb = [[p.sb(pre + f"usb{i}{a}", [128, 2 + nt * 128], F32) for a in range(2)]
                    for i in range(2)]
        self.carry = p.sb(pre + "carry", [128, 44, 2], F32)
        self.cwb = p.sb(pre + "cwb", [128, 44, 4], F32)
        self.t1 = p.sb(pre + "t1", [128, nt * 128], F32)
        self.t2 = p.sb(pre + "t2", [128, nt * 128], F32)
        self.ca = p.sb(pre + "ca", [128, nt * 128], F32)
        self.cg = p.sb(pre + "cg", [128, nt * 128], F32)
        self.sa = p.sb(pre + "sa", [128, nt * 128], F32)
        self.sq = p.sb(pre + "sq", [128, 1024], BF16)
        self.ss = p.sb(pre + "ss", [128, 1], F32)
        self.sd = p.sb(pre + "sd", [128, 1], F32)
        self.rs = p.sb(pre + "rs", [128, 1], F32)
        self.gainf = p.sb(pre + "gainf", [128, 1024], F32)
        self.ident = p.sb(pre + "ident", [128, 128], BF16)
        self.hfin = p.sb(pre + "hfin", [128, 1024], F32)
        self.psum = p.ps(pre + "psum", [128, 7 * 512])
        self.psT = p.ps(pre + "psT", [128, 8, 128], BF16)
        self.psA = [self.bank(0), self.bank(1)]
        self.psU = [[self.bank(2), self.bank(3)], [self.bank(4), self.bank(5)]]
        self.nA = 0
        self.nfc = 0
        self.nwo = 0

    def bank(self, i, n=1):
        return self.psum[:, i * 512:(i + n) * 512]

    def load_weights(self, w_o, w_in, w_out, cwb, g_ffn, ident, g_final=None, head_order=None):
        p = self.p
        self.w_in = w_in
        p.dma(self.ident[:], ident, w=["ident"])
        p.dma(self.cwb[:], cwb.rearrange("(c p) f -> p c f", p=128), w=["cwb"])
        p.dma(self.gT[:], g_ffn, w=["gT"])
        self.w_out = w_out
        nc = p.nc
        self.winb = nc.dram_tensor("winb" + p.sfx, [44 * 128, 1024], BF16).ap()
        self.woutb = nc.dram_tensor("woutb" + p.sfx, [44 * 128, 512], BF16).ap()
        i = 0
        for fc in range(22):
            for ag in range(2):
                par = i % 2
                i += 1
                stg = self.fstage[par]
                c0 = ag * DFF + fc * 128
                sk = ("stg", "ffn", par, 0)
                p.dma(stg[:, 0:1024].rearrange("p (c f) -> p c f", c=8),
                      w_in[:, c0:c0 + 128].rearrange("(c p) f -> p c f", p=128), w=[sk])
                p.tt(self.wch[par][:, :, 0, :],
                     stg[:, 0:1024].rearrange("p (c f) -> p c f", c=8),
                     self.gT[:].unsqueeze(2).to_broadcast([128, 8, 128]), ALU.mult,
                     r=[sk, "gT"], w=[("wch", par, 0)], eng="pool")
                p.dma(self.winb[(fc * 2 + ag) * 128:(fc * 2 + ag + 1) * 128, :],
                      self.wch[par][:, :, 0, :], r=[("wch", par, 0)], w=["winb"], q="pool")
        for half in range(2):
            for fc in range(22):
                par = i % 2
                i += 1
                stg = self.fstage[par]
                sk = ("stg", "ffn", par, 0)
                p.dma(stg[:, 0:512], w_out[fc * 128:(fc + 1) * 128, half * 512:(half + 1) * 512],
                      w=[sk])
                p.cp(self.woch[par][:], stg[:, 0:512], r=[sk], w=[("woch", par)], eng="pool")
                p.dma(self.woutb[(half * 22 + fc) * 128:(half * 22 + fc + 1) * 128, :],
                      self.woch[par][:], r=[("woch", par)], w=["woutb"], q="pool")
        if g_final is not None:
            p.dma(self.gainf[:], g_final.to_broadcast([128, 1024]), w=["gainf"])
        p.memset(self.carry[:], 0.0, w=["carry"])
        if w_o is not None:
            ho = head_order if head_order is not None else list(range(16))
            for i, h in enumerate(ho):
                stg = self.stage[i % 2]
                sk = ("stg", "ffn", i % 2, 0)
                p.dma(stg[0:64, :], w_o[h * 64:(h + 1) * 64, :], w=[sk])
                p.cp(self.wo[:, i, :], stg[0:64, :], r=[sk], w=[("wo", i)], eng="pool")

    def run_supertile(self, nt, h_src, oT_src, h_dst, n_skip_out=0, final_norm=False,
                      h1_preloaded=False, h_fn=None, oT_fn=None, n_flag=0, flag=None,
                      rkeys=(), dkey=None):
        p = self.p
        ntok = nt * 128
        if not h1_preloaded:
            for j in range(nt):
                p.dma(self.h1[:, j, :], h_src[j * 128:(j + 1) * 128, :], r=list(rkeys),
                      w=[("h1", j)])
                if j < n_flag:
                    p.ts(self.h1[:, j, :], self.h1[:, j, :], flag, None, ALU.mult,
                         r=[("h1", j), "flag"], w=[("h1", j)])
        if oT_src is not None or oT_fn is not None:
            if oT_fn is not None:
                for j in range(nt):
                    for g in range(4):
                        p.dma(self.oT[:, g * 4:(g + 1) * 4, j * 128:(j + 1) * 128], oT_fn(j, g),
                              r=list(rkeys), w=["oT"])
            elif not isinstance(oT_src, str):
                p.dma(self.oT[:, :, 0:ntok], oT_src.rearrange("(c p) t -> p c t", p=64), w=["oT"])
            for j in range(nt):
                for half in range(2):
                    ps = self.psA[self.nA % 2]
                    pk = ("bank", self.nA % 2)
                    self.nA += 1
                    for kc in range(16):
                        p.mm(ps, self.oT[:, kc, j * 128:(j + 1) * 128],
                             self.wo[:, kc, half * 512:(half + 1) * 512],
                             start=(kc == 0), stop=(kc == 15),
                             r=["oT", ("wo", kc)], w=[pk])
                    hs = self.h1[:, j, half * 512:(half + 1) * 512]
                    p.tt(hs, hs, ps, ALU.add, r=[pk, ("h1", j)], w=[("h1", j)])
        for j in range(nt):
            rmsnorm_tile(p, self.h1[:, j, :], None, self.hn[:],
                         (self.sq[:], self.ss[:], self.sd[:], self.rs[:]),
                         [("h1", j)], ["hn"], "f")
            for kc in range(8):
                p.tr(self.psT[:, kc, :], self.hn[:, kc * 128:(kc + 1) * 128], self.ident[:],
                     r=["hn", "ident"], w=["psT"])
            p.cp(self.hnT[:, :, j * 128:(j + 1) * 128], self.psT[:], r=["psT"], w=[("hnT", j)],
                 eng="act")
        hnT_keys = [("hnT", j) for j in range(nt)]
        for fc in range(22):
            par = self.nfc % 2
            self.nfc += 1
            wch = self.wch[par]
            for ag in range(2):
                p.dma(wch[:, :, ag, :],
                      self.winb[(fc * 2 + ag) * 128:(fc * 2 + ag + 1) * 128, :].rearrange(
                          "p (c f) -> p c f", c=8),
                      r=["winb"], w=[("wch", par, ag)])
            cs = []
            for ag in range(2):
                ps = self.psU[par][ag]
                pk = ("bank", 2 + 2 * par + ag)
                for kc in range(8):
                    p.mm(ps[:, 0:ntok], wch[:, kc, ag, :], self.hnT[:, kc, 0:ntok],
                         start=(kc == 0), stop=(kc == 7),
                         r=[("wch", par, ag)] + hnT_keys, w=[pk])
                usb = self.usb[par][ag]
                uk = ("usb", par, ag)
                ch = ag * 22 + fc
                p.cp(usb[:, 0:2], self.carry[:, ch, :], r=["carry%d" % ch, "carry"], w=[uk],
                     eng="pool")
                p.cp(usb[:, 2:2 + ntok], ps[:, 0:ntok], r=[pk], w=[uk], eng="act")
                p.cp(self.carry[:, ch, :], usb[:, ntok:ntok + 2], r=[uk], w=["carry%d" % ch],
                     eng="pool")
                cw = self.cwb
                dst = self.ca if ag == 0 else self.cg
                dk = "ca" if ag == 0 else "cg"
                p.ts(self.t1[:, 0:ntok], usb[:, 2:2 + ntok], cw[:, ch, 2:3], cw[:, ch, 3:4],
                     ALU.mult, ALU.add, r=[uk, "cwb"], w=["t1"])
                p.stt(self.t2[:, 0:ntok], usb[:, 1:1 + ntok], cw[:, ch, 1:2], self.t1[:, 0:ntok],
                      ALU.mult, ALU.add, r=[uk, "cwb", "t1"], w=["t2"])
                p.stt(dst[:, 0:ntok], usb[:, 0:ntok], cw[:, ch, 0:1], self.t2[:, 0:ntok],
                      ALU.mult, ALU.add, r=[uk, "cwb", "t2"], w=[dk])
            p.act(self.sa[:, 0:ntok], self.ca[:, 0:ntok], AF.Silu, r=["ca"], w=["sa"])
            p.tt(self.actT[:, fc, 0:ntok], self.sa[:, 0:ntok], self.cg[:, 0:ntok], ALU.mult,
                 r=["sa", "cg"], w=[("actT", fc)])
        for half in range(2):
            for fc in range(22):
                wp = self.nwo % 2
                self.nwo += 1
                p.dma(self.woch[wp][:],
                      self.woutb[(half * 22 + fc) * 128:(half * 22 + fc + 1) * 128, :],
                      r=["woutb"], w=[("woch", wp)])
                for j in range(nt):
                    p.mm(self.bank(j), self.actT[:, fc, j * 128:(j + 1) * 128], self.woch[wp][:],
                         start=(fc == 0), stop=(fc == 21),
                         r=[("actT", fc), ("woch", wp)], w=[("bank", j)])
            for j in range(nt):
                hs = self.h1[:, j, half * 512:(half + 1) * 512]
                p.tt(hs, hs, self.bank(j), ALU.add, r=[("bank", j), ("h1", j)], w=[("h1", j)])
        for j in range(nt):
            if h_dst is not None and j >= n_skip_out:
                jo = j - n_skip_out
                if final_norm:
                    rmsnorm_tile(p, self.h1[:, j, :], self.gainf[:], self.hfin[:],
                                 (self.sq[:], self.ss[:], self.sd[:], self.rs[:]),
                                 [("h1", j), "gainf"], ["hfin"], "f")
                    p.dma(h_dst[jo * 128:(jo + 1) * 128, :], self.hfin[:], r=["hfin"],
                          w=([dkey] if dkey else []), q="pool", is_output=True)
                else:
                    p.dma(h_dst[jo * 128:(jo + 1) * 128, :], self.h1[:, j, :], r=[("h1", j)],
                          w=([dkey] if dkey else []), q="pool", is_output=(dkey is None))


def ident_np():
    return np.eye(128, dtype=np.float32).astype(NPBF16)


def b_head_order():
    return [g * 4 + 2 * hpl + par for g in range(4) for par in range(2) for hpl in range(2)]


def build_B(st_sizes, n_skip_tiles, final_norm=False, p=None, io=None):
    fused = p is not None
    if not fused:
        nc = bass.Bass("TRN2", target_bir_lowering=False)
        p = Prog(nc)
    ntiles = sum(st_sizes)
    ntok = ntiles * 128
    if not fused:
        h_in = p.din("h_in", [ntok, D], F32)
        oT_in = p.din("oT_in", [D, ntok], BF16)
    w_o = p.din("w_o", [D, D], F32)
    w_in = p.din("w_in", [D, 2 * DFF], F32)
    w_out = p.din("w_out", [DFF, D], F32)
    cwb = p.din("cwb", [2 * DFF, 4], F32)
    g_ffn = p.din("g_ffn", [128, 8], F32)
    g_fin = p.din("g_fin", [1, D], F32)
    ident = p.din("ident", [128, 128], BF16)
    if not fused:
        h_out = p.dout("h_out", [(ntiles - n_skip_tiles) * 128, D], F32)
    else:
        h_out = io["h_dst"]
    f = FFNCtx(p, "f_", max_nt=max(st_sizes))
    f.load_weights(w_o, w_in, w_out, cwb, g_ffn, ident, g_fin,
                   head_order=(b_head_order() if fused else None))
    if fused:
        flag_sb = p.sb("flag", [128, 1], F32)
        p.dma(flag_sb[:], io["flag"], w=["flag"])
        if io.get("after_setup"):
            io["after_setup"]()
    t0 = 0
    for nt in st_sizes:
        skip = max(0, min(nt, n_skip_tiles - t0))
        o0 = max(0, t0 - n_skip_tiles)
        dst = h_out[o0 * 128:(o0 + nt - skip) * 128, :] if skip < nt else None
        if fused:
            f.run_supertile(nt, io["h_ap"][t0 * 128:(t0 + nt) * 128, :], None, dst,
                            n_skip_out=skip, final_norm=final_norm,
                            oT_fn=lambda j, g, t0=t0: io["oT_ap"](t0 + j, g),
                            n_flag=skip, flag=flag_sb[:], rkeys=io["rkeys"], dkey=io["dkey"])
        else:
            f.run_supertile(nt, h_in[t0 * 128:(t0 + nt) * 128, :],
                            oT_in[:, t0 * 128:(t0 + nt) * 128], dst, n_skip_out=skip,
                            final_norm=final_norm)
        t0 += nt
    if fused:
        return None
    return p.finish()


def c_head_order():
    return [8 * g + 2 * hpl + par for g in range(2) for par in range(2) for hpl in range(4)]


def build_C(st_sizes, n_skip_tiles, final_norm=False, p=None, io=None):
    fused = p is not None
    if not fused:
        nc = bass.Bass("TRN2", target_bir_lowering=False)
        p = Prog(nc)
    ntiles = sum(st_sizes)
    ntok = ntiles * 128
    mx = max(st_sizes)
    if not fused:
        h_in = p.din("h_in", [ntok, D], F32)
        hkv_in = p.din("hkv_in", [ntok, D], F32)
    w_q = p.din("w_q", [D, D], F32)
    w_kv = p.din("w_kv", [D, 256], F32)
    sinks_b = p.din("sinks_b", [1, 2048], F32)
    g_attn = p.din("g_attn", [128, 8], F32)
    g_kv = p.din("g_kv", [128, 8], F32)
    cos_t = p.din("cos_t", [128, ntok], F32)
    sin_t = p.din("sin_t", [128, ntok], F32)
    masks = p.din("masks", [3, 128, 512], BF16)
    w_o = p.din("w_o", [D, D], F32)
    w_in = p.din("w_in", [D, 2 * DFF], F32)
    w_out = p.din("w_out", [DFF, D], F32)
    cwb = p.din("cwb", [2 * DFF, 4], F32)
    g_ffn = p.din("g_ffn", [128, 8], F32)
    g_fin = p.din("g_fin", [1, D], F32)
    ident = p.din("ident", [128, 128], BF16)
    if not fused:
        h_out = p.dout("h_out", [(ntiles - n_skip_tiles) * 128, D], F32)
    else:
        h_out = io["h_dst"]

    f = FFNCtx(p, "f_", max_nt=mx)
    f.load_weights(w_o, w_in, w_out, cwb, g_ffn, ident, g_fin, head_order=c_head_order())
    if fused:
        flag_sb = p.sb("flag", [128, 1], F32)
        p.dma(flag_sb[:], io["flag"], w=["flag"])

    gTq = p.sb("gTq", [128, 8], F32)
    gTk = p.sb("gTk", [128, 8], F32)
    wk2 = p.sb("wk2", [128, 8, 2, 2, 128], BF16)
    wv = p.sb("wv", [128, 8, 128], BF16)
    hkv = p.sb("hkv", [128, 1024], F32)
    hnqT = f.hnT
    hkvT = p.sb("hkvT", [128, 8, mx * 128], BF16)
    QT2 = p.sb("QT2", [128, 8, mx * 128], BF16)
    KT2 = p.sb("KT2", [128, 2, (mx + 1) * 128], BF16)
    VA = p.sb("VA", [128, mx + 1, 2, 65], BF16)
    PT = [p.sb(f"PT{i}", [128, 1024], BF16) for i in range(2)]
    msk = p.sb("msk", [128, 3, 512], BF16)
    cos_sb = p.sb("cos_sb", [128, mx * 128], F32)
    sin_sb = p.sb("sin_sb", [128, mx * 128], F32)
    sexp = p.sb("sexp", [128, 2048], BF16)
    zr = p.sb("zr", [128, 1024], F32)
    rz = zr
    ones = p.sb("ones", [128, 64], F32)
    osb = f.hfin[0:64, :]
    psS = [f.bank(2, 2), f.bank(4, 2)]
    psSk = [[("bank", 2), ("bank", 3)], [("bank", 4), ("bank", 5)]]
    psO = f.bank(0, 2)
    psOk = [("bank", 0), ("bank", 1)]
    psB = f.bank(6)
    psBk = [("bank", 6)]

    p.dma(gTq[:], g_attn, w=["gTq"])
    p.dma(gTk[:], g_kv, w=["gTk"])
    p.dma(msk[:], masks.rearrange("m p c -> p m c"), w=["msk"])
    p.dma(zr[64:65, :], sinks_b[:, 0:1024], w=["zr"])
    p.act(sexp[64:65, 0:1024], zr[64:65, :], AF.Exp, r=["zr"], w=["sexp"])
    p.dma(zr[64:65, :], sinks_b[:, 1024:2048], r=["sexp"], w=["zr"])
    p.act(sexp[64:65, 1024:2048], zr[64:65, :], AF.Exp, r=["zr"], w=["sexp"])
    p.memset(ones[:], 1.0, w=["ones"])
    p.memset(VA[:], 1.0, w=["VA"] + [("VA", i) for i in range(mx + 1)])
    p.memset(KT2[:], 0.0, w=["KT2", ("KT2", 0), ("KT2", 1)])
    for kc in range(8):
        stg = f.stage[kc % 2]
        sk = ("stg", "ffn", kc % 2, 0)
        gs = gTk[:, kc:kc + 1]
        p.dma(stg[:, 0:256], w_kv[kc * 128:(kc + 1) * 128, :], w=[sk])
        for g in range(2):
            for dup in range(2):
                p.ts(wk2[:, kc, g, 0, dup * 64:(dup + 1) * 64], stg[:, g * 64:(g + 1) * 64],
                     gs, None, ALU.mult, r=[sk, "gTk"], w=["wk2"], eng="pool")
                p.ts(wk2[:, kc, g, 1, dup * 64:dup * 64 + 32], stg[:, g * 64 + 32:g * 64 + 64],
                     gs, None, ALU.mult, r=[sk, "gTk"], w=["wk2"], eng="pool")
                p.ts(wk2[:, kc, g, 1, dup * 64 + 32:dup * 64 + 64], stg[:, g * 64:g * 64 + 32],
                     gs, None, ALU.mult, r=[sk, "gTk"], w=["wk2"], eng="pool")
        p.ts(wv[:, kc, :], stg[:, 128:256], gs, None, ALU.mult, r=[sk, "gTk"], w=["wv"], eng="pool")

    if fused and io.get("after_setup"):
        io["after_setup"]()
    scr = (f.sq[:], f.ss[:], f.sd[:], f.rs[:])
    t0 = 0
    first_real = n_skip_tiles
    for nt in st_sizes:
        n = nt * 128
        for j in range(nt):
            if fused:
                p.dma(f.h1[:, j, :], io["h_ap"][(t0 + j) * 128:(t0 + j + 1) * 128, :],
                      r=list(io["rkeys"]), w=[("h1", j)])
                if t0 + j < n_skip_tiles:
                    p.ts(f.h1[:, j, :], f.h1[:, j, :], flag_sb[:], None, ALU.mult,
                         r=[("h1", j), "flag"], w=[("h1", j)])
            else:
                p.dma(f.h1[:, j, :], h_in[(t0 + j) * 128:(t0 + j + 1) * 128, :], w=[("h1", j)])
        p.dma(cos_sb[:, 0:n], cos_t[:, t0 * 128:t0 * 128 + n], w=["cos"])
        p.dma(sin_sb[:, 0:n], sin_t[:, t0 * 128:t0 * 128 + n], w=["sin"])
        for j in range(nt):
            rmsnorm_tile(p, f.h1[:, j, :], None, f.hn[:], scr, [("h1", j)], ["hn"], "f")
            for kc in range(8):
                p.tr(f.psT[:, kc, :], f.hn[:, kc * 128:(kc + 1) * 128], f.ident[:],
                     r=["hn", "ident"], w=["psT"])
            p.cp(hnqT[:, :, j * 128:(j + 1) * 128], f.psT[:], r=["psT"], w=[("hnT", j)], eng="act")
            if fused:
                p.dma(hkv[:], io["hkv_ap"][(t0 + j) * 128:(t0 + j + 1) * 128, :],
                      r=list(io["rkeys"]), w=["hkv"])
                if t0 + j < n_skip_tiles:
                    p.ts(hkv[:], hkv[:], flag_sb[:], None, ALU.mult, r=["hkv", "flag"], w=["hkv"])
            else:
                p.dma(hkv[:], hkv_in[(t0 + j) * 128:(t0 + j + 1) * 128, :], w=["hkv"])
            rmsnorm_tile(p, hkv[:], None, f.hn[:], scr, ["hkv"], ["hn"], "f")
            for kc in range(8):
                p.tr(f.psT[:, kc, :], f.hn[:, kc * 128:(kc + 1) * 128], f.ident[:],
                     r=["hn", "ident"], w=["psT"])
            p.cp(hkvT[:, :, j * 128:(j + 1) * 128], f.psT[:], r=["psT"], w=[("hkvT", j)], eng="act")
        hq_keys = [("hnT", j) for j in range(nt)]
        hk_keys = [("hkvT", j) for j in range(nt)]

        def rope_out(dst, psn, pss, rk, wk):
            p.tt(f.t1[:, 0:n], psn, cos_sb[:, 0:n], ALU.mult, r=rk[0:1] + ["cos"], w=["t1"])
            p.tt(f.t2[:, 0:n], pss, sin_sb[:, 0:n], ALU.mult, r=rk[1:2] + ["sin"], w=["t2"])
            p.tt(dst, f.t1[:, 0:n], f.t2[:, 0:n], ALU.add, r=["t1", "t2"], w=wk)

        for g in range(2):
            par = f.nfc % 2
            f.nfc += 1
            bk = [("bank", 2 + 2 * par), ("bank", 3 + 2 * par)]
            for v in range(2):
                for kc in range(8):
                    p.mm(f.psU[par][v][:, 0:n], wk2[:, kc, g, v, :], hkvT[:, kc, 0:n],
                         start=(kc == 0), stop=(kc == 7), r=["wk2"] + hk_keys, w=[bk[v]])
            rope_out(KT2[:, g, 128:128 + n], f.psU[par][0][:, 0:n], f.psU[par][1][:, 0:n],
                     bk, [("KT2", g)])
        for j in range(nt):
            ps = f.psA[f.nA % 2]
            pk = ("bank", f.nA % 2)
            f.nA += 1
            for kc in range(8):
                p.mm(ps[:, 0:128], hkvT[:, kc, j * 128:(j + 1) * 128], wv[:, kc, :],
                     start=(kc == 0), stop=(kc == 7), r=[("hkvT", j), "wv"], w=[pk])
            p.cp(VA[:, j + 1, :, 0:64], ps[:, 0:128].rearrange("p (g d) -> p g d", g=2),
                 r=[pk], w=[("VA", j + 1)], eng="act")
        for hp in range(8):
            par = f.nfc % 2
            f.nfc += 1
            stg = f.fstage[par]
            wch = f.wch[par]
            p.dma(stg[:, 0:1024].rearrange("p (c f) -> p c f", c=8),
                  w_q[:, hp * 128:(hp + 1) * 128].rearrange("(c p) f -> p c f", p=128),
                  w=[("stg", "ffn", par, 0)])
            p.tt(wch[:, :, 0, :], stg[:, 0:1024].rearrange("p (c f) -> p c f", c=8),
                 gTq[:].unsqueeze(2).to_broadcast([128, 8, 128]), ALU.mult,
                 r=[("stg", "ffn", par, 0), "gTq"], w=[("wch", par, 0)], eng="pool")
            src = wch[:, :, 0, :].rearrange("p c (h d) -> p c h d", h=2)
            dsw = wch[:, :, 1, :].rearrange("p c (h d) -> p c h d", h=2)
            p.cp(dsw[:, :, :, 0:32], src[:, :, :, 32:64], r=[("wch", par, 0)],
                 w=[("wch", par, 1)], eng="pool")
            p.cp(dsw[:, :, :, 32:64], src[:, :, :, 0:32], r=[("wch", par, 0)],
                 w=[("wch", par, 1)], eng="pool")
            bk = [("bank", 2 + 2 * par), ("bank", 3 + 2 * par)]
            for v in range(2):
                for kc in range(8):
                    p.mm(f.psU[par][v][:, 0:n], wch[:, kc, v, :], hnqT[:, kc, 0:n],
                         start=(kc == 0), stop=(kc == 7),
                         r=[("wch", par, v)] + hq_keys, w=[bk[v]])
            rope_out(QT2[:, hp, 0:n], f.psU[par][0][:, 0:n], f.psU[par][1][:, 0:n],
                     bk, [("QT2", hp)])
        nS = 0
        for j in range(nt):
            gt = t0 + j
            for g in range(2):
                chunks = [(j, 0 if gt == first_real else 1), (j + 1, 2)]
                for ci, (slot, mi) in enumerate(chunks):
                    sp_ = nS % 2
                    nS += 1
                    for par in range(2):
                        pr = slice(par * 64, (par + 1) * 64)
                        p.mm(psS[sp_][:, par * 512:(par + 1) * 512],
                             KT2[pr, g, slot * 128:(slot + 1) * 128],
                             QT2[pr, 4 * g:4 * g + 4, j * 128:(j + 1) * 128],
                             start=True, stop=False,
                             r=[("KT2", g)] + [("QT2", 4 * g + i) for i in range(4)],
                             w=[psSk[sp_][par]])
                        p.mm(psS[sp_][:, par * 512:(par + 1) * 512], f.ident[:], msk[:, mi, :],
                             start=False, stop=True, r=["ident", "msk"], w=[psSk[sp_][par]])
                    p.act(PT[sp_][:], psS[sp_], AF.Exp, r=psSk[sp_], w=[("PT", sp_)], scale=0.125)
                    for par in range(2):
                        p.mm(psO[0:65, par * 512:(par + 1) * 512], VA[:, slot, g, :],
                             PT[sp_][:, par * 512:(par + 1) * 512],
                             start=(ci == 0), stop=(ci == 1),
                             r=[("PT", sp_), ("VA", slot), "VA"], w=[psOk[par]])
                p.tt(zr[64:65, :], psO[64:65, :], sexp[64:65, g * 1024:(g + 1) * 1024], ALU.add,
                     r=psOk + ["sexp"], w=["zr"])
                p.recip(rz[64:65, :], zr[64:65, :], r=["zr"], w=["rz"])
                p.cp(osb, psO[0:64, :], r=psOk, w=["hfin"], eng="act")
                for par in range(2):
                    p.mm(psB[0:64, :], ones[64:65, :], rz[64:65, par * 512:(par + 1) * 512],
                         start=True, stop=True, r=["ones", "rz"], w=psBk)
                    dst = f.oT[:, g * 8 + par * 4:g * 8 + par * 4 + 4, j * 128:(j + 1) * 128]
                    p.tt(dst, osb[:, par * 512:(par + 1) * 512].rearrange("p (h q) -> p h q", h=4),
                         psB[0:64, :].rearrange("p (h q) -> p h q", h=4), ALU.mult,
                         r=["hfin"] + psBk, w=["oT"])
        for g in range(2):
            p.cp(KT2[:, g, 0:128], KT2[:, g, n:n + 128], r=[("KT2", g)], w=[("KT2", g)], eng="pool")
        p.cp(VA[:, 0, :, :], VA[:, nt, :, :], r=[("VA", nt)], w=[("VA", 0)], eng="pool")
        skip = max(0, min(nt, n_skip_tiles - t0))
        o0 = max(0, t0 - n_skip_tiles)
        dst = h_out[o0 * 128:(o0 + nt - skip) * 128, :] if skip < nt else None
        f.run_supertile(nt, None, "resident", dst, n_skip_out=skip, final_norm=final_norm,
                        h1_preloaded=True, dkey=(io["dkey"] if fused else None))
        t0 += nt
    if fused:
        return None
    return p.finish()


def rope_tables(pos):
    half = 32
    inv = (np.float32(10000.0) ** (-np.arange(half, dtype=np.float32) / half)).astype(np.float32)
    ang = pos.astype(np.float32)[None, :] * inv[:, None]
    cos = np.cos(ang).astype(np.float32)
    sin = np.sin(ang).astype(np.float32)
    cos64 = np.concatenate([cos, cos], 0)
    sin64 = np.concatenate([-sin, sin], 0)
    return (np.ascontiguousarray(np.concatenate([cos64, cos64], 0)),
            np.ascontiguousarray(np.concatenate([sin64, sin64], 0)))


def swa_masks(first_exists):
    i = np.arange(128)[:, None]
    q = np.arange(128)[None, :]
    prev = np.where(i > q, 0.0, MASKV).astype(np.float32)
    cur = np.where(i <= q, 0.0, MASKV).astype(np.float32)
    pf = prev if first_exists else np.full((128, 128), MASKV, np.float32)
    m = np.stack([np.tile(pf, (1, 4)), np.tile(prev, (1, 4)), np.tile(cur, (1, 4))], 0)
    return m.astype(NPBF16)


def sinks_row(sinks16):
    ho = c_head_order()
    return np.ascontiguousarray(
        np.repeat(np.asarray(sinks16, np.float32)[ho], 128)[None, :])


def gT_np(g):
    return np.ascontiguousarray(np.asarray(g, np.float32).reshape(8, 128).T)


FORCE = 1.0e6
TINY = 1.0e-30
C_BF = float(np.float32(NPBF16(-MASKV)))
LN_C = float(np.log(np.float64(C_BF)))
MUL_MASK = False
KEEP_ZB = True


def build_A(S, dbg=99, p=None, io=None):
    fused = p is not None
    if not fused:
        nc = bass.Bass("TRN2", target_bir_lowering=False)
        p = Prog(nc)
    nc = p.nc
    NST = S // 512
    NQB = S // 128
    NCC = max(1, S // 2048)
    if not fused:
        h_in = p.din("h_in", [S, D], F32)
    g_attn = p.din("g_attn", [128, 8], F32)
    wq_d = p.din("wq", [D, 256], F32)
    wk3_d = p.din("wk3", [D, 192], F32)
    wv3_d = p.din("wv3", [D, 192], F32)
    wg_d = p.din("wg", [D, 12], F32)
    w1_d = p.din("w1", [2, 2048, 256], F32)
    w2_d = p.din("w2", [2, 256, 64], F32)
    posT_d = p.din("posT", [64, 2, 32], F32)
    cos_d = p.din("cos_t", [128, S], F32)
    sin_d = p.din("sin_t", [128, S], F32)
    ccos_d = p.din("ccos_t", [128, NCC * 128], F32)
    csin_d = p.din("csin_t", [128, NCC * 128], F32)
    pmask_d = p.din("pmask", [2, 16, 128, 128], BF16)
    r0mask_d = p.din("r0mask", [128, 512], BF16)
    cmask_d = p.din("cmask", [2, 128, 512], BF16)
    emat_d = p.din("emat", [64, 128, 128], BF16)
    wfull_d = p.din("wfull", [NCC * 128, 257], BF16)
    fix_d = p.din("fix3", [128, 6], F32)
    ident_d = p.din("ident", [128, 128], BF16)
    gscr = [nc.dram_tensor(f"gscr{i}" + p.sfx, [1, 12 * 512], F32).ap() for i in range(2)]
    if not fused:
        oT_out = p.dout("oT_out", [64, 4, S], BF16)

    ident = p.sb("ident", [128, 128], BF16)
    gT = p.sb("gT", [128, 8], F32)
    fst = [p.sb(f"fst{i}", [128, 1024], F32) for i in range(2)]
    WQ = p.sb("WQ", [128, 8, 2, 256], BF16)
    WKS = p.sb("WKS", [128, 8, 2, 128], BF16)
    WKW = p.sb("WKW", [128, 8, 2, 128], BF16)
    WKC = p.sb("WKC", [128, 8, 64], BF16)
    WVC = p.sb("WVC", [128, 8, 64], BF16)
    WV2 = p.sb("WV2", [128, 8, 128], BF16)
    WG = p.sb("WG", [128, 8, 12], BF16)
    W1c = [p.sb(f"W1c{i}", [64, 4, 256], BF16) for i in range(2)]
    W2K = p.sb("W2K", [128, 2, 2, 128], BF16)
    W2V = p.sb("W2V", [128, 2, 64], BF16)
    posT = p.sb("posT", [64, 2, 32], BF16)
    c1 = p.sb("c1", [128, 4], F32)
    ccos = p.sb("ccos", [128, NCC * 128], F32)
    csin = p.sb("csin", [128, NCC * 128], F32)
    pmask = p.sb("pmask", [128, 2, 16, 128], BF16)
    r0mask = p.sb("r0mask", [128, 512], BF16)
    cmask = p.sb("cmask", [128, 2, 512], BF16)
    emat = p.sb("emat", [128, 64, 128], BF16)
    wfull = p.sb("wfull", [128, NCC, 257], BF16)
    fix3 = p.sb("fix3", [128, 6], F32)
    hbuf = [p.sb("hbuf0", [128, 1024], F32)] * 2
    sq = p.sb("sq", [128, 1024], BF16)
    ss = p.sb("ss", [128, 1], F32)
    sd = p.sb("sd", [128, 1], F32)
    rs = p.sb("rs", [128, 1], F32)
    hn = p.sb("hn", [128, 1024], BF16)
    hnT = p.sb("hnT", [128, 8, 512], BF16)
    cos_sb = p.sb("cos_sb", [128, 512], F32)
    sin_sb = p.sb("sin_sb", [128, 512], F32)
    t1 = p.sb("t1", [128, 512], F32)
    t2 = p.sb("t2", [128, 512], F32)
    Qblk = p.sb("Qblk", [128, 4, 512], BF16)
    KsT2 = p.sb("KsT2", [128, S], BF16)
    VsA = p.sb("VsA", [128, NQB, 65], BF16)
    KwT2 = p.sb("KwT2", [128, 1024], BF16)
    VwA = p.sb("VwA", [128, 8, 65], BF16)
    KcT2 = p.sb("KcT2", [128, NCC * 128], BF16)
    VcA = p.sb("VcA", [128, NCC, 65], BF16)
    xT = [p.sb(f"xT{i}", [64, 528], BF16) for i in range(2)]
    hidK = p.sb("hidK", [128, 2, 32], BF16)
    hidV = p.sb("hidV", [128, 2, 128], BF16)
    gx = [p.sb(f"gx{i}", [128, 32], F32) for i in range(3)]
    gsb = p.sb("gsb", [12, 512], F32)
    G64b = [p.sb(f"G64b{i}", [128, 12 * 128], F32) for i in range(2)]
    PT = [p.sb(f"PT{i}", [128, 512], BF16) for i in range(4)]
    EX = [p.sb(f"EX{i}", [128, 512], BF16) for i in range(4)]
    PcT = p.sb("PcT", [128, NCC, 512], BF16)
    zr = p.sb("zr", [128, 512], F32)
    Rr = p.sb("Rr", [128, 512], F32)
    ones = p.sb("ones", [128, 64], F32)
    osb = p.sb("osb", [64, 512], F32)
    acc = p.sb("acc", [64, 512], F32)
    tmpo = p.sb("tmpo", [64, 512], F32)
    oacc = p.sb("oacc", [64, 4, 128], BF16)
    imp = p.sb("imp", [128, 256], F32)
    selbuf = p.sb("selbuf", [128, 256], F32)
    work = p.sb("work", [128, 256], F32)
    mx8 = p.sb("mx8", [128, 8], F32)
    thr = p.sb("thr", [128, 1], F32)
    zq = p.sb("zq", [128, 1], F32)
    Bq = p.sb("Bq", [128, 256], BF16)
    BT = p.sb("BT", [128, 2, 512], BF16)
    psum = p.ps("psum", [128, 7 * 512])
    psT = p.ps("psT", [128, 8, 128], BF16)
    zero_b = p.sb("zero_b", [128, 512], BF16)
    p.memset(zero_b[:], 0.0, w=["zero_b"])
    p.memset(Qblk[:], 0.0, w=["QT2"])

    def bank(i, n=1):
        return psum[:, i * 512:(i + n) * 512]

    def bk(i):
        return ("bank", i)

    p.dma(ident[:], ident_d, w=["ident"])
    p.dma(gT[:], g_attn, w=["gT"])
    p.dma(ccos[:], ccos_d, w=["ccos"])
    p.dma(csin[:], csin_d, w=["csin"])
    for a_ in range(2):
        for r4 in range(0, 16, 4):
            p.dma(pmask[:, a_, r4:r4 + 4, :], pmask_d[a_, r4:r4 + 4].rearrange("r p c -> p r c"),
                  w=["pmask"])
    p.dma(r0mask[:], r0mask_d, w=["r0mask"])
    p.dma(cmask[:], cmask_d.rearrange("a p c -> p a c"), w=["cmask"])
    for e8 in range(0, 64, 8):
        p.dma(emat[:, e8:e8 + 8, :], emat_d[e8:e8 + 8].rearrange("e p c -> p e c"), w=["emat"])
    p.dma(wfull[:], wfull_d.rearrange("(c p) f -> p c f", p=128), w=["wfull"])
    p.dma(fix3[:], fix_d, w=["fix3"])
    p.memset(ones[:], 1.0, w=["ones"])
    p.memset(VsA[:], 1.0, w=["VsA"])
    p.memset(VwA[:], 1.0, w=["VwA"])
    p.memset(VcA[:], 1.0, w=["VcA"])
    p.memset(KwT2[:], 0.0, w=["KwT2"])
    p.memset(KcT2[:], 0.0, w=["KcT2"])
    p.memset(selbuf[:], -FORCE, w=["selbuf"])
    p.memset(hidV[:], 0.0, w=["hidV"])
    for i in range(2):
        p.memset(xT[i][:], 0.0, w=[("xT", i)])
    nst_ = [0]

    def stage_load(dst_fn, src_ap, ncols, parts=128):
        i = nst_[0] % 2
        nst_[0] += 1
        k = ("fst", i)
        p.dma(fst[i][0:parts, 0:ncols], src_ap, w=[k])
        return fst[i], k

    def swapcopy(dst, src, r, w):
        d4 = dst.rearrange("p (h d) -> p h d", d=64)
        s4 = src.rearrange("p (h d) -> p h d", d=64)
        p.cp(d4[:, :, 0:32], s4[:, :, 32:64], r=r, w=w, eng="pool")
        p.cp(d4[:, :, 32:64], s4[:, :, 0:32], r=r, w=w, eng="pool")

    for kc in range(8):
        gs = gT[:, kc:kc + 1]
        rows = slice(kc * 128, (kc + 1) * 128)
        st_, k = stage_load(None, wq_d[rows, :], 256)
        p.ts(WQ[:, kc, 0, :], st_[:, 0:256], gs, None, ALU.mult, r=[k, "gT"], w=["WQ"], eng="pool")
        swapcopy(WQ[:, kc, 1, :], WQ[:, kc, 0, :], ["WQ"], ["WQ"])
        st_, k = stage_load(None, wk3_d[rows, :], 192)
        p.ts(WKC[:, kc, :], st_[:, 0:64], gs, None, ALU.mult, r=[k, "gT"], w=["WKC"], eng="pool")
        for (W_, c0) in ((WKS, 64), (WKW, 128)):
            for dup in range(2):
                p.ts(W_[:, kc, 0, dup * 64:(dup + 1) * 64], st_[:, c0:c0 + 64], gs, None, ALU.mult,
                     r=[k, "gT"], w=["WK"], eng="pool")
            swapcopy(W_[:, kc, 1, :], W_[:, kc, 0, :], ["WK"], ["WK"])
        st_, k = stage_load(None, wv3_d[rows, :], 192)
        p.ts(WVC[:, kc, :], st_[:, 0:64], gs, None, ALU.mult, r=[k, "gT"], w=["WVC"], eng="pool")
        p.ts(WV2[:, kc, :], st_[:, 64:192], gs, None, ALU.mult, r=[k, "gT"], w=["WV2"], eng="pool")
        st_, k = stage_load(None, wg_d[rows, :], 12)
        p.ts(WG[:, kc, :], st_[:, 0:12], gs, None, ALU.mult, r=[k, "gT"], w=["WG"], eng="pool")
    nW1 = [0]

    def w1_piece(kv, l0):
        i = nW1[0] % 2
        nW1[0] += 1
        k = ("fst", i)
        p.dma(fst[i][0:64, :].rearrange("p (l m) -> p l m", l=4),
              w1_d[kv, l0 * 64:(l0 + 4) * 64, :].rearrange("(l d) m -> d l m", d=64), w=[k])
        p.cp(W1c[i][:], fst[i][0:64, :].rearrange("p (l m) -> p l m", l=4),
             r=[k], w=[("W1c", i)], eng="pool")
        return W1c[i], ("W1c", i)

    for kv in range(2):
        for mt in range(2):
            st_, k = stage_load(None, w2_d[kv, mt * 128:(mt + 1) * 128, :], 64)
            if kv == 0:
                for dup in range(2):
                    p.cp(W2K[:, mt, 0, dup * 64:(dup + 1) * 64], st_[:, 0:64], r=[k], w=["W2K"],
                         eng="pool")
                swapcopy(W2K[:, mt, 1, :], W2K[:, mt, 0, :], ["W2K"], ["W2K"])
            else:
                p.cp(W2V[:, mt, :], st_[:, 0:64], r=[k], w=["W2V"], eng="pool")
    st_, k = stage_load(None, posT_d.rearrange("d a l -> d (a l)"), 64, parts=64)
    p.cp(posT[:].rearrange("d a l -> d (a l)"), st_[0:64, 0:64], r=[k], w=["posT"], eng="pool")
    for kv in range(2):
        for l0 in range(0, 32, 4):
            wt, wk_ = w1_piece(kv, l0)
            for mt in range(2):
                col = kv * 2 + mt
                bb = 1 if mt == 0 else 5
                for li in range(4):
                    l = l0 + li
                    p.mm(bank(bb)[:, col:col + 1], wt[:, li, mt * 128:(mt + 1) * 128],
                         posT[:, kv, l:l + 1], start=(l == 0), stop=(l == 31),
                         r=[wk_, "posT"], w=[bk(bb)])
    p.cp(c1[:, 0:1], bank(1)[:, 0:1], r=[bk(1)], w=["c1"], eng="act")
    p.cp(c1[:, 2:3], bank(1)[:, 2:3], r=[bk(1)], w=["c1"], eng="act")
    p.cp(c1[:, 1:2], bank(5)[:, 1:2], r=[bk(5)], w=["c1"], eng="act")
    p.cp(c1[:, 3:4], bank(5)[:, 3:4], r=[bk(5)], w=["c1"], eng="act")

    if dbg == 0:
        return p.finish()
    if fused:
        for cb in range(4):
            p.dma(io["o_zero"][:, cb, :], zero_b[0:64, 0:128], r=["zero_b"], w=[io["dkey"]],
                  q="pool")
    if fused and io.get("after_setup"):
        io["after_setup"]()
    scr = (sq[:], ss[:], sd[:], rs[:])

    def rope_out(dst, psn, pss, cs, sn, rk, wk, n):
        p.tt(t1[:, 0:n], psn, cs, ALU.mult, r=rk[0:1] + ["cos", "ccos"], w=["t1"])
        p.tt(t2[:, 0:n], pss, sn, ALU.mult, r=rk[1:2] + ["sin", "csin"], w=["t2"])
        if isinstance(dst, tuple):
            p.tt(dst[0], t1[0:64, 0:n], t2[0:64, 0:n], ALU.add, r=["t1", "t2"], w=wk)
            p.tt(dst[1], t1[64:128, 0:n], t2[64:128, 0:n], ALU.add, r=["t1", "t2"], w=wk)
        else:
            p.tt(dst, t1[:, 0:n], t2[:, 0:n], ALU.add, r=["t1", "t2"], w=wk)

    nU = [0]
    nH = [0]

    def proj_pair(W_, dst, cs, sn, wkey, rkey):
        par = nU[0] % 2
        nU[0] += 1
        b0, b1 = 2 + 2 * par, 3 + 2 * par
        for v, b in ((0, b0), (1, b1)):
            for kc in range(8):
                p.mm(bank(b), W_(kc, v), hnT[:, kc, :], start=(kc == 0), stop=(kc == 7),
                     r=[rkey, "hnT"], w=[bk(b)])
        rope_out(dst, bank(b0), bank(b1), cs, sn, [bk(b0), bk(b1)], wkey, 512)

    def gelu_to(dst, ps_ap, bias_ap, rk, wk):
        x, a, b = gx[0][:], gx[1][:], gx[2][:]
        p.act(x, ps_ap, AF.Identity, r=rk + ["c1"], w=["gx0"], bias=bias_ap)
        p.tt(a, x, x, ALU.mult, r=["gx0"], w=["gx1"])
        p.ts(a, a, 0.044715, 1.0, ALU.mult, ALU.add, r=["gx1"], w=["gx1"])
        p.tt(a, a, x, ALU.mult, r=["gx1", "gx0"], w=["gx1"])
        p.act(b, a, AF.Sigmoid, r=["gx1"], w=["gx2"], scale=1.5957691216057308)
        p.tt(dst, x, b, ALU.mult, r=["gx0", "gx2"], w=wk)

    nS = [0]
    sdepth = [2]
    ZB = [True]

    def attn_chunk(kT2, kcols, vaug, biases, first, last, n_extra_r):
        sp_ = nS[0] % sdepth[0]
        nS[0] += 1
        sb_ = 2 + sp_
        if ZERO_BIAS and ZB[0] and len(biases) == 0:
            biases = [(ident[:], zero_b[:], ["ident", "zero_b"])]
        out = bank(sb_)
        p.mm(out, kT2[:, kcols], Qblk[:, :, qsl[0]], start=True, stop=(len(biases) == 0),
             r=n_extra_r + ["QT2"], w=[bk(sb_)])
        for bi, bias in enumerate(biases):
            lh, rh, rk = bias[0:3]
            if len(bias) == 4:
                for hh in range(4):
                    p.mm(out[:, hh * 128:(hh + 1) * 128], lh, rh, start=False,
                         stop=(bi == len(biases) - 1), r=rk, w=[bk(sb_)])
            else:
                p.mm(out, lh, rh, start=False, stop=(bi == len(biases) - 1), r=rk, w=[bk(sb_)])
        return sp_, sb_

    qsl = [None]
    for st in range(NST):
        tok0 = st * 512
        for j in range(4):
            hb = hbuf[j % 2]
            hk = ("hbuf", 0)
            if fused:
                r0 = io["h_row"](tok0 + j * 128)
                p.dma(hb[:], io["h_ap"][r0:r0 + 128, :], r=list(io["rkeys"]), w=[hk])
            else:
                p.dma(hb[:], h_in[tok0 + j * 128:tok0 + (j + 1) * 128, :], w=[hk])
            rmsnorm_tile(p, hb[:], None, hn[:], scr, [hk], ["hn"], "a")
            for kc in range(8):
                p.tr(psT[:, kc, :], hn[:, kc * 128:(kc + 1) * 128], ident[:],
                     r=["hn", "ident"], w=["psT"])
            p.cp(hnT[:, :, j * 128:(j + 1) * 128], psT[:], r=["psT"], w=["hnT"], eng="act")
        p.dma(cos_sb[:], cos_d[:, tok0:tok0 + 512], w=["cos"])
        p.dma(sin_sb[:], sin_d[:, tok0:tok0 + 512], w=["sin"])
        for hp in range(2):
            proj_pair(lambda kc, v, hp=hp: WQ[:, kc, v, hp * 128:(hp + 1) * 128],
                      (Qblk[0:64, hp, :], Qblk[64:128, 2 + hp, :]), cos_sb[:], sin_sb[:],
                      ["QT2"], "WQ")
        proj_pair(lambda kc, v: WKS[:, kc, v, :], KsT2[:, tok0:tok0 + 512], cos_sb[:], sin_sb[:],
                  ["KsT2"], "WK")
        proj_pair(lambda kc, v: WKW[:, kc, v, :], KwT2[:, 512:1024], cos_sb[:], sin_sb[:],
                  ["KwT2"], "WK")
        for i, W_ in enumerate((WKC, WVC)):
            for kc in range(8):
                p.mm(bank(1)[0:64, :], W_[:, kc, :], hnT[:, kc, :], start=(kc == 0), stop=(kc == 7),
                     r=["WKC", "WVC", "hnT"], w=[bk(1)])
            p.cp(xT[i][:, 16:528], bank(1)[0:64, :], r=[bk(1)], w=[("xT", i)], eng="act")
        for j in range(4):
            for kc in range(8):
                p.mm(bank(0)[:, 0:128], hnT[:, kc, j * 128:(j + 1) * 128], WV2[:, kc, :],
                     start=(kc == 0), stop=(kc == 7), r=["hnT", "WV2"], w=[bk(0)])
            p.cp(VsA[:, st * 4 + j, 0:64], bank(0)[:, 0:64], r=[bk(0), "VsA"], w=["VsA"], eng="act")
            p.cp(VwA[:, 4 + j, 0:64], bank(0)[:, 64:128], r=[bk(0), "VwA"], w=["VwA"], eng="act")
        for kc in range(8):
            p.mm(bank(1)[0:12, :], WG[:, kc, :], hnT[:, kc, :], start=(kc == 0), stop=(kc == 7),
                 r=["WG", "hnT"], w=[bk(1)])
        p.act(gsb[:], bank(1)[0:12, :], AF.Sigmoid, r=[bk(1)], w=["gsb"])
        p.dma(gscr[st % 2].rearrange("o (a b) -> (o a) b", a=12), gsb[:], r=["gsb"],
              w=[("gscr", st % 2)])
        if dbg == 1:
            return p.finish()
        for kv in range(2):
            x3 = xT[kv][:].rearrange("p (i s) -> p i s", s=16)
            for l0 in range(0, 32, 4):
                wt, wk_ = w1_piece(kv, l0)
                for mt in range(2):
                    bb = 1 if mt == 0 else 5
                    for li in range(4):
                        l = l0 + li
                        rhs = x3[:, 0:32, l] if l < 16 else x3[:, 1:33, l - 16]
                        p.mm(bank(bb)[:, 0:32], wt[:, li, mt * 128:(mt + 1) * 128], rhs,
                             start=(l == 0), stop=(l == 31), r=[wk_, ("xT", kv)], w=[bk(bb)])
            for mt in range(2):
                bb = 1 if mt == 0 else 5
                if kv == 0:
                    gelu_to(hidK[:, mt, :], bank(bb)[:, 0:32], c1[:, mt:mt + 1], [bk(bb)], ["hidK"])
                else:
                    if st % 4 == 0 and mt == 0:
                        p.memset(hidV[:], 0.0, w=["hidV"])
                    gelu_to(hidV[:, mt, (st % 4) * 32:(st % 4) * 32 + 32], bank(bb)[:, 0:32],
                            c1[:, 2 + mt:3 + mt], [bk(bb)], ["hidV"])
            if kv == 0:
                par = nU[0] % 2
                nU[0] += 1
                b0, b1 = 2 + 2 * par, 3 + 2 * par
                for v, b in ((0, b0), (1, b1)):
                    for mt in range(2):
                        p.mm(bank(b)[:, 0:32], W2K[:, mt, v, :], hidK[:, mt, :],
                             start=(mt == 0), stop=(mt == 1), r=["W2K", "hidK"], w=[bk(b)])
                sl = slice(st * 32, st * 32 + 32)
                rope_out(KcT2[:, sl], bank(b0)[:, 0:32], bank(b1)[:, 0:32], ccos[:, sl], csin[:, sl],
                         [bk(b0), bk(b1)], ["KcT2"], 32)
            else:
                for mt in range(2):
                    p.mm(bank(1)[:, 0:64], hidV[:, mt, :], W2V[:, mt, :],
                         start=(mt == 0), stop=(mt == 1), r=["W2V", "hidV"], w=[bk(1)])
                p.cp(VcA[:, st // 4, 0:64], bank(1)[:, 0:64], r=[bk(1), "VcA"], w=["VcA"], eng="act")
            p.cp(xT[kv][:, 0:16], xT[kv][:, 512:528], r=[("xT", kv)], w=[("xT", kv)], eng="pool")
        if dbg == 2:
            return p.finish()
        for j in range(4):
            qb = st * 4 + j
            qsl[0] = slice(j * 128, (j + 1) * 128)
            tsl = qsl[0]
            p.dma(G64b[j % 2][64:65, :].rearrange("p (a b) -> p a b", a=12),
                  gscr[st % 2].rearrange("o (a b) -> o a b", a=12)[:, :, tsl],
                  r=[("gscr", st % 2)], w=[("G64", j % 2)])

            def finish_branch(br, first):
                p.ts(zr[64:65, :], bank(0)[64:65, :], TINY, None, ALU.max, r=[bk(0)], w=["zr"])
                p.recip(zr[64:65, :], zr[64:65, :], r=["zr"], w=["zr"])
                g3 = G64b[j % 2][64:65, :].rearrange("p (h b t) -> p h b t", h=4, b=3)
                for par in range(2):
                    for hpl in range(2):
                        hl = 2 * hpl + par
                        c0 = (par * 2 + hpl) * 128
                        p.tt(Rr[64:65, c0:c0 + 128], zr[64:65, c0:c0 + 128], g3[:, hl, br, :],
                             ALU.mult, r=["zr", ("G64", j % 2)], w=["Rr"])
                p.cp(osb[:], bank(0)[0:64, :], r=[bk(0)], w=["osb"], eng="act")

                def part2(first=first):
                    p.mm(bank(1)[0:64, :], ones[64:65, :], Rr[64:65, :], start=True, stop=True,
                         r=["ones", "Rr"], w=[bk(1)])
                    if first:
                        p.tt(acc[:], osb[:], bank(1)[0:64, :], ALU.mult, r=["osb", bk(1)], w=["acc"])
                    else:
                        p.tt(tmpo[:], osb[:], bank(1)[0:64, :], ALU.mult, r=["osb", bk(1)],
                             w=["tmpo"])
                        p.tt(acc[:], acc[:], tmpo[:], ALU.add, r=["tmpo", "acc"], w=["acc"])
                return part2

            pend = []
            pdepth = [1]

            def pend_push(fn):
                pend.append(fn)
                while len(pend) > pdepth[0]:
                    pend.pop(0)()

            def pend_flush():
                while pend:
                    pend.pop(0)()

            ncc = qb // 16 + 1
            r_ = qb % 16
            for cc in range(ncc):
                biases = []
                lastc = (cc == ncc - 1)
                if lastc:
                    biases.append((ident[:], pmask[:, 1 if cc == 0 else 0, r_, :], ["ident", "pmask"], 128))
                elif cc == 0:
                    biases.append((ident[:], r0mask[:], ["ident", "r0mask"]))
                sp_, sb_ = attn_chunk(KcT2, slice(cc * 128, (cc + 1) * 128), None, biases,
                                      cc == 0, lastc, ["KcT2"])
                p.act(PcT[:, cc, :], bank(sb_), AF.Exp, r=[bk(sb_)], w=[("PcT", cc)], scale=0.125)
                pend_push(lambda cc=cc, lastc=lastc: p.mm(
                    bank(0)[0:65, :], VcA[:, cc, :], PcT[:, cc, :], start=(cc == 0), stop=lastc,
                    r=[("PcT", cc), "VcA"], w=[bk(0)]))
            pend_flush()
            for par in range(2):
                for hpl in range(2):
                    hi = par * 2 + hpl
                    c0 = hi * 128
                    ib = 4 + (hi % 2)
                    for cc in range(ncc):
                        p.mm(bank(ib)[:, 0:257], PcT[:, cc, c0:c0 + 128], wfull[:, cc, :],
                             start=(cc == 0), stop=(cc == ncc - 1),
                             r=[("PcT", cc), "wfull"], w=[bk(ib)])
                    p.ts(zq[:], bank(ib)[:, 256:257], TINY, None, ALU.max, r=[bk(ib)], w=["zq"])
                    p.recip(zq[:], zq[:], r=["zq"], w=["zq"])
                    if hi == 0:
                        p.ts(imp[:], bank(ib)[:, 0:256], zq[:], None, ALU.mult, r=[bk(ib), "zq"],
                             w=["imp"])
                    else:
                        p.stt(imp[:], bank(ib)[:, 0:256], zq[:], imp[:], ALU.mult, ALU.add,
                              r=[bk(ib), "zq", "imp"], w=["imp"])
            fin0 = finish_branch(0, True)
            if dbg == 3 or dbg == 100 + j * 10 + 3:
                return p.finish()
            nb = 2 * qb + 2
            p.cp(selbuf[:, 0:nb], imp[:, 0:nb], r=["imp"], w=["selbuf"])
            lo = 2 * qb - 1
            k0 = 0
            if lo < 0:
                lo, k0 = 0, 1
            nfx = 3 - k0
            p.tt(selbuf[:, lo:lo + nfx], selbuf[:, lo:lo + nfx], fix3[:, k0:3], ALU.mult,
                 r=["selbuf", "fix3"], w=["selbuf"])
            p.tt(selbuf[:, lo:lo + nfx], selbuf[:, lo:lo + nfx], fix3[:, 3 + k0:6], ALU.add,
                 r=["selbuf", "fix3"], w=["selbuf"])
            p.memset(selbuf[:, 0:1], 3.0 * FORCE, w=["selbuf"])
            p.s.op("dve", lambda e: e.max(out=mx8[:], in_=selbuf[:]), ["selbuf"], ["mx8"])
            p.s.op("dve", lambda e: e.match_replace(out=work[:], in_to_replace=mx8[:],
                                                    in_values=selbuf[:], imm_value=-2.0 * FORCE),
                   ["selbuf", "mx8"], ["work"])
            p.s.op("dve", lambda e: e.max(out=mx8[:], in_=work[:]), ["work"], ["mx8"])
            p.s.op("dve", lambda e: e.tensor_reduce(out=thr[:], in_=mx8[:], axis=AX.X, op=ALU.min),
                   ["mx8"], ["thr"])
            p.ts(Bq[:], selbuf[:], thr[:], 1.0, ALU.is_ge, ALU.subtract, r=["selbuf", "thr"], w=["Bq"])
            nhalf = 1 if nb <= 128 else 2
            for hf in range(nhalf):
                p.tr(psT[:, hf, :], Bq[:, hf * 128:(hf + 1) * 128], ident[:], r=["Bq", "ident"],
                     w=["psT"])
            for hf in range(nhalf):
                for rep in range(4):
                    p.cp(BT[:, hf, rep * 128:(rep + 1) * 128], psT[:, hf, :], r=["psT"], w=["BT"],
                         eng=("act" if rep % 2 == 0 else "dve"))
            if dbg == 5 or dbg == 100 + j * 10 + 5:
                return p.finish()
            k_lo = max(0, qb - 4)
            sdepth[0] = 4
            pdepth[0] = 2
            for kc in range(k_lo, qb + 1):
                biases = []
                if kc == qb - 4:
                    biases.append((ident[:], cmask[:, 1, :], ["ident", "cmask"]))
                if kc == qb:
                    biases.append((ident[:], cmask[:, 0, :], ["ident", "cmask"]))
                slot = 4 + j - (qb - kc)
                sp_, sb_ = attn_chunk(KwT2, slice(slot * 128, (slot + 1) * 128), None, biases,
                                      kc == k_lo, kc == qb, ["KwT2"])
                p.act(PT[sp_][:], bank(sb_), AF.Exp, r=[bk(sb_)], w=[("PT", sp_)], scale=0.125)
                if kc == min(k_lo + 1, qb) and fin0 is not None:
                    fin0()
                    fin0 = None
                pend_push(lambda kc=kc, sp_=sp_, slot=slot: p.mm(
                    bank(0)[0:65, :], VwA[:, slot, :], PT[sp_][:], start=(kc == k_lo),
                    stop=(kc == qb), r=[("PT", sp_), "VwA"], w=[bk(0)]))
            pend_flush()
            fin2 = finish_branch(2, False)
            if fin0 is not None:
                fin0()
                fin0 = None
            if dbg == 4 or dbg == 100 + j * 10 + 4:
                return p.finish()
            sdepth[0] = 4
            pdepth[0] = 2
            for kc in range(qb + 1):
                mulmask = MUL_MASK and kc != qb
                if mulmask:
                    zb = ZB[0]
                    ZB[0] = KEEP_ZB
                    sp_, sb_ = attn_chunk(KsT2, slice(kc * 128, (kc + 1) * 128), None, [],
                                          kc == 0, kc == qb, ["KsT2"])
                    ZB[0] = zb
                    ms = bank(6)[:, sp_ * 128:(sp_ + 1) * 128]
                    p.mm(ms, emat[:, kc % 64, :], BT[:, kc // 64, 0:128], start=True, stop=True,
                         r=["emat", "BT"], w=[("mslot", sp_)])
                    p.act(EX[sp_][:], bank(sb_), AF.Exp, r=[bk(sb_)], w=[("EX", sp_)], scale=0.125,
                          bias=-LN_C)
                    p.stt(PT[sp_][:].rearrange("p (h q) -> p h q", h=4),
                          ms.unsqueeze(1).to_broadcast([128, 4, 128]), C_BF,
                          EX[sp_][:].rearrange("p (h q) -> p h q", h=4), ALU.add, ALU.mult,
                          r=[("mslot", sp_), ("EX", sp_)], w=[("PT", sp_)])
                else:
                    biases = [(emat[:, kc % 64, :], BT[:, kc // 64, :], ["emat", "BT"])]
                    if kc == qb:
                        biases.append((ident[:], cmask[:, 0, :], ["ident", "cmask"]))
                    sp_, sb_ = attn_chunk(KsT2, slice(kc * 128, (kc + 1) * 128), None, biases,
                                          kc == 0, kc == qb, ["KsT2"])
                    p.act(PT[sp_][:], bank(sb_), AF.Exp, r=[bk(sb_)], w=[("PT", sp_)], scale=0.125)
                if kc == min(1, qb) and fin2 is not None:
                    fin2()
                    fin2 = None
                pend_push(lambda kc=kc, sp_=sp_: p.mm(
                    bank(0)[0:65, :], VsA[:, kc, :], PT[sp_][:], start=(kc == 0), stop=(kc == qb),
                    r=[("PT", sp_), "VsA"], w=[bk(0)]))
            pend_flush()
            fin1 = finish_branch(1, False)
            fin1()
            sdepth[0] = 2
            if dbg == 6 or dbg == 100 + j * 10 + 6:
                return p.finish()
            p.cp(oacc[:], acc[:].rearrange("p (c q) -> p c q", c=4), r=["acc"], w=["oacc"])
            if fused:
                for dst in io["o_dst"](qb):
                    p.dma(dst, oacc[:], r=["oacc"], w=[io["dkey"]], q="pool")
            else:
                p.dma(oT_out[:, :, tok0 + j * 128:tok0 + (j + 1) * 128], oacc[:], r=["oacc"],
                      q="pool", is_output=True)
            if dbg == 100 + j * 10 + 7:
                return p.finish()
        p.cp(KwT2[:, 0:512], KwT2[:, 512:1024], r=["KwT2"], w=["KwT2"], eng="pool")
        p.cp(VwA[:, 0:4, :], VwA[:, 4:8, :], r=["VwA"], w=["VwA"], eng="pool")
        if dbg == 7 + st:
            return p.finish()
    if fused:
        return None
    return p.finish()


def nsa_consts(S):
    NCC = max(1, S // 2048)
    ml = np.arange(128)[:, None]
    q = np.arange(128)[None, :]
    pm = np.zeros((2, 16, 128, 128), np.float32)
    for a in range(2):
        for r in range(16):
            valid = (16 * ml + 15 <= 128 * r + q)
            if a == 1:
                valid = valid & (ml >= 1)
            pm[a, r] = np.where(valid, 0.0, MASKV)
    pmask = pm.astype(NPBF16)
    r0 = np.zeros((128, 512), np.float32)
    r0[0, :] = MASKV
    cur = np.where(ml <= q, 0.0, MASKV).astype(np.float32)
    upper = np.where(ml > q, 0.0, MASKV).astype(np.float32)
    cmask = np.stack([np.tile(cur, (1, 4)), np.tile(upper, (1, 4))], 0).astype(NPBF16)
    emat = np.zeros((64, 128, 128), np.float32)
    for e in range(64):
        emat[e, 2 * e, 0:64] = -MASKV
        emat[e, 2 * e + 1, 64:128] = -MASKV
    ws = [1, 2, 2, 2, 1]
    wfull = np.zeros((NCC * 128, 257), np.float32)
    for m in range(1, NCC * 128):
        n = m - 1
        for j in range(256):
            i = n - 4 * j + 1
            if 0 <= i <= 4:
                wfull[m, j] = ws[i]
    wfull[:, 256] = 1.0
    fix = np.zeros((128, 6), np.float32)
    lo = np.arange(128) < 64
    fix[:, 0] = np.where(lo, 0.0, 1.0)
    fix[:, 3] = np.where(lo, FORCE, 0.0)
    fix[:, 4] = 2.0 * FORCE
    fix[:, 5] = np.where(lo, -FORCE, FORCE)
    cpos = 16 * np.arange(NCC * 128) + 15
    ccos, csin = rope_tables(cpos)
    return dict(pmask=pmask, r0mask=r0.astype(NPBF16), cmask=cmask, emat=emat.astype(NPBF16),
                wfull=wfull.astype(NPBF16), fix3=fix, ccos_t=ccos, csin_t=csin, ident=ident_np())


def nsa_weights(a_w_in_l, cmp_pos_l, g):
    W = a_w_in_l
    q0 = g * 256
    def kcol(i):
        return W[:, 1024 + i * 256 + g * 64: 1024 + i * 256 + (g + 1) * 64]
    kc_, vc_, ks_, vs_, kw_, vw_ = [kcol(i) for i in range(6)]
    wg = W[:, 1024 + 6 * 256 + g * 12: 1024 + 6 * 256 + (g + 1) * 12]
    return dict(wq=np.ascontiguousarray(W[:, q0:q0 + 256]),
                wk3=np.ascontiguousarray(np.concatenate([kc_, ks_, kw_], 1)),
                wv3=np.ascontiguousarray(np.concatenate([vc_, vs_, vw_], 1)),
                wg=np.ascontiguousarray(wg),
                posT=np.ascontiguousarray(np.transpose(cmp_pos_l, (2, 0, 1))))


SEQ = 16384
NB = 2
CH = 4096
NPHASE = 99


def _run(nc, in_maps):
    res = run_bass_kernel_spmd(nc, in_maps, core_ids=list(range(8)))
    return res.results


def _cwb(conv_w, conv_b):
    return np.ascontiguousarray(np.concatenate([conv_w, conv_b[None]], 0).T.astype(np.float32))


def _chunk_with_halo(x_b, c, halo):
    lo = c * CH - halo
    if lo >= 0:
        return np.ascontiguousarray(x_b[lo:(c + 1) * CH])
    pad = np.zeros((-lo,) + x_b.shape[1:], x_b.dtype)
    return np.ascontiguousarray(np.concatenate([pad, x_b[0:(c + 1) * CH]], 0))


def kernel_unfused(x, norm_attn, norm_ffn, a_w_in, a_cmp_pos, a_cmp_w1, a_cmp_w2, a_w_out, kv_norm,
           b_w_kv, b_w_q, b_sinks, b_w_out, ffn_w_in, ffn_conv_w, ffn_conv_b, ffn_w_out,
           final_norm):
    f32 = lambda a: np.ascontiguousarray(np.asarray(a, dtype=np.float32))
    x = f32(x)
    norm_attn, norm_ffn = f32(norm_attn), f32(norm_ffn)
    a_w_in, a_cmp_pos, a_cmp_w1, a_cmp_w2, a_w_out = map(f32, (a_w_in, a_cmp_pos, a_cmp_w1,
                                                                a_cmp_w2, a_w_out))
    kv_norm, b_w_kv, b_w_q, b_sinks, b_w_out = map(f32, (kv_norm, b_w_kv, b_w_q, b_sinks, b_w_out))
    ffn_w_in, ffn_conv_w, ffn_conv_b, ffn_w_out, final_norm = map(
        f32, (ffn_w_in, ffn_conv_w, ffn_conv_b, ffn_w_out, final_norm))
    h = x
    ident = ident_np()
    gfin = np.ascontiguousarray(final_norm[None, :])
    cosA, sinA = rope_tables(np.arange(SEQ))
    constsA = nsa_consts(SEQ)

    for l in range(2):
        ncA = build_A(SEQ)
        maps = []
        for i in range(8):
            b, g = divmod(i, 4)
            m = dict(h_in=h[b], g_attn=gT_np(norm_attn[l]), w1=a_cmp_w1[l], w2=a_cmp_w2[l],
                     cos_t=cosA, sin_t=sinA)
            m.update(constsA)
            m.update(nsa_weights(a_w_in[l], a_cmp_pos[l], g))
            maps.append(m)
        resA = _run(ncA, maps)
        oT_full = np.zeros((NB, 16, 64, SEQ), NPBF16)
        for i in range(8):
            b, g = divmod(i, 4)
            o = resA[i]["oT_out"]
            for par in range(2):
                for hpl in range(2):
                    oT_full[b, g * 4 + 2 * hpl + par] = o[:, par * 2 + hpl, :]
        oT_full = oT_full.reshape(NB, 1024, SEQ)
        ncB = build_B([1] + [4] * 8, 1)
        maps = []
        for i in range(8):
            b, c = divmod(i, 4)
            maps.append(dict(
                h_in=_chunk_with_halo(h[b], c, 128),
                oT_in=np.ascontiguousarray(_chunk_with_halo(oT_full[b].T, c, 128).T),
                w_o=a_w_out[l], w_in=ffn_w_in[l], w_out=ffn_w_out[l],
                cwb=_cwb(ffn_conv_w[l], ffn_conv_b[l]), g_ffn=gT_np(norm_ffn[l]), g_fin=gfin,
                ident=ident))
        resB = _run(ncB, maps)
        h = np.stack([np.concatenate([resB[b * 4 + c]["h_out"] for c in range(4)], 0)
                      for b in range(NB)], 0)

    hkv = h
    for l in range(2, 4):
        j = l - 2
        ncC = build_C([2] + [4] * 8, 2, final_norm=(l == 3))
        maps = []
        for i in range(8):
            b, c = divmod(i, 4)
            pos = c * CH - 256 + np.arange(CH + 256)
            cos_t, sin_t = rope_tables(pos)
            maps.append(dict(
                h_in=_chunk_with_halo(h[b], c, 256), hkv_in=_chunk_with_halo(hkv[b], c, 256),
                w_q=b_w_q[j], w_kv=b_w_kv, sinks_b=sinks_row(b_sinks[j]),
                g_attn=gT_np(norm_attn[l]), g_kv=gT_np(kv_norm), cos_t=cos_t, sin_t=sin_t,
                masks=swa_masks(c > 0), w_o=b_w_out[j], w_in=ffn_w_in[l], w_out=ffn_w_out[l],
                cwb=_cwb(ffn_conv_w[l], ffn_conv_b[l]), g_ffn=gT_np(norm_ffn[l]), g_fin=gfin,
                ident=ident))
        resC = _run(ncC, maps)
        h = np.stack([np.concatenate([resC[b * 4 + c]["h_out"] for c in range(4)], 0)
                      for b in range(NB)], 0)
    return np.ascontiguousarray(h.astype(np.float32))


def build_fused(nphase=99):
    from concourse.bass import ds
    nph = [0]

    def stop():
        nph[0] += 1
        return nph[0] >= nphase

    nc = bass.Bass("TRN2", target_bir_lowering=False)
    p = Prog(nc)
    S = SEQ
    WB = 128 + CH
    WC = 256 + CH
    SUBW = 11 * 128
    xA = p.din("xA", [S, D], F32)
    xB = p.din("xB", [WB, D], F32)
    flag = p.din("flag", [128, 1], F32)
    out = p.dout("out", [CH, D], F32)
    oTloc = [nc.dram_tensor(f"oTloc{l}", [12 * 64, 4 * SUBW], BF16) for l in range(2)]
    OTb = [nc.dram_tensor(f"OTb{l}", [12 * 256, 4 * SUBW], BF16) for l in range(2)]
    oTwin = nc.dram_tensor("oTwin", [3 * 256, 4 * SUBW], BF16).ap()
    hloc = [nc.dram_tensor(f"hloc{k}", [CH, D], F32) for k in range(3)]
    Hb = [nc.dram_tensor(f"Hb{k}", [S, D], F32) for k in range(3)]
    hwin = nc.dram_tensor("hwin", [WC, D], F32).ap()
    hkvwin = nc.dram_tensor("hkvwin", [WC, D], F32).ap()
    rg = [[0, 1, 2, 3], [4, 5, 6, 7]]
    PID = p.s.pid

    def gather_group(src, dst, nchunk, rows, rk, wk):
        def fn(e, sem):
            for k in range(nchunk):
                e.collective_compute(
                    "AllGather", ALU.bypass, replica_groups=rg,
                    ins=[src.ap()[k * rows:(k + 1) * rows, :].opt()],
                    outs=[dst.ap()[k * 4 * rows:(k + 1) * 4 * rows, :].opt()]).then_inc(sem)
        p.s.cc(fn, [rk], [wk], n=nchunk)

    def h_row(tok):
        rank, rem = divmod(tok, CH)
        k, r = divmod(rem, 256)
        return (k * 4 + rank) * 256 + r

    def win_copy(dst, src, halo, q, rk, wk):
        s5 = src.rearrange("(k g r e) d -> k g r (e d)", k=16, g=4, e=8)
        dm = dst[halo:halo + CH, :].rearrange("(k g r e) d -> k g r (e d)", k=16, g=1, e=8)
        dh = dst[0:halo, :].rearrange("(k g r e) d -> k g r (e d)", k=1, g=1, e=8)
        h8 = halo // 8
        p.dmaf(lambda e: e.dma_start(
            out=dm, in_=s5[:, ds(PID(e, "c", lambda pid: pid % 4), 1), :, :]),
            r=[rk], w=[wk], q=q)
        p.dmaf(lambda e: e.dma_start(
            out=dh, in_=s5[15:16, ds(PID(e, "cm1", lambda pid: (pid + 3) % 4), 1), 32 - h8:32, :]),
            r=[rk], w=[wk], q=q)

    for l in range(2):
        p.sfx = f"_A{l}"
        rk = [] if l == 0 else [f"Hb{l - 1}"]
        O5 = oTloc[l].ap().rearrange("(c s d) (b t) -> c s d b t", c=4, s=3, b=4)

        def o_dst(qb, O5=O5):
            c, sl = divmod(qb, 32)
            sl += 1
            dsts = [O5[c, sl // 11, :, :, (sl % 11) * 128:(sl % 11) * 128 + 128]]
            if sl == 32 and c < 3:
                dsts.append(O5[c + 1, 0, :, :, 0:128])
            return dsts

        build_A(S, p=p, io=dict(h_ap=(xA if l == 0 else Hb[l - 1].ap()),
                                h_row=((lambda t: t) if l == 0 else h_row),
                                rkeys=rk, dkey=f"oTloc{l}", o_dst=o_dst,
                                o_zero=O5[0, 0, :, :, 0:128], after_setup=p.s.cc_wait))
        p.phase_end()
        gather_group(oTloc[l], OTb[l], 12, 64, f"oTloc{l}", f"OTb{l}")
        if stop():
            return p.finish(), dict(p.dins)
        p.sfx = f"_B{l}"
        O3 = OTb[l].ap().rearrange("(c r) f -> c r f", c=4)

        def after_b(l=l, O3=O3):
            p.s.cc_wait()
            p.dmaf(lambda e: e.dma_start(
                out=oTwin.rearrange("(c r) f -> c r f", c=1),
                in_=O3[ds(PID(e, "c", lambda pid: pid % 4), 1), :, :]),
                r=[f"OTb{l}"], w=["oTwin"], q="act")
            if l > 0:
                win_copy(hwin[0:WB, :], Hb[l - 1].ap(), 128, "act", f"Hb{l - 1}", "hwin")

        if l == 0:
            h_ap = xB
            rkb = ["oTwin"]
        else:
            h_ap = hwin[0:WB, :]
            rkb = ["oTwin", "hwin"]
        W5 = oTwin.rearrange("(s g d) (b t) -> d s g b t", s=3, g=4, b=4)

        def oT_ap(wt, g, W5=W5):
            return W5[:, wt // 11, g, :, (wt % 11) * 128:(wt % 11) * 128 + 128]

        build_B([1] + [4] * 8, 1, p=p,
                io=dict(h_ap=h_ap, oT_ap=oT_ap, h_dst=hloc[l].ap(), flag=flag,
                        rkeys=rkb, dkey=f"hloc{l}", after_setup=after_b))
        p.phase_end()
        gather_group(hloc[l], Hb[l], 16, 256, f"hloc{l}", f"Hb{l}")
        if stop():
            return p.finish(), dict(p.dins)

    for l in range(2, 4):
        p.sfx = f"_C{l}"
        last = (l == 3)
        if l == 2:
            def after_c():
                p.s.cc_wait()
                win_copy(hkvwin, Hb[1].ap(), 256, "sp", "Hb1", "hkvwin")
            h_ap, hkv_ap, rkc = hkvwin, hkvwin, ["hkvwin"]
        else:
            def after_c():
                p.s.cc_wait()
                win_copy(hwin, Hb[2].ap(), 256, "sp", "Hb2", "hwin")
            h_ap, hkv_ap, rkc = hwin, hkvwin, ["hwin", "hkvwin"]
        build_C([2] + [4] * 8, 2, final_norm=last, p=p,
                io=dict(h_ap=h_ap, hkv_ap=hkv_ap, h_dst=(out if last else hloc[2].ap()), flag=flag,
                        rkeys=rkc, dkey=(None if last else "hloc2"), after_setup=after_c))
        if not last:
            p.phase_end()
            gather_group(hloc[2], Hb[2], 16, 256, "hloc2", "Hb2")
            if stop():
                return p.finish(), dict(p.dins)
    return p.finish(), dict(p.dins)


def kernel(x, norm_attn, norm_ffn, a_w_in, a_cmp_pos, a_cmp_w1, a_cmp_w2, a_w_out, kv_norm,
           b_w_kv, b_w_q, b_sinks, b_w_out, ffn_w_in, ffn_conv_w, ffn_conv_b, ffn_w_out,
           final_norm):
    f32 = lambda a: np.ascontiguousarray(np.asarray(a, dtype=np.float32))
    x = f32(x)
    norm_attn, norm_ffn = f32(norm_attn), f32(norm_ffn)
    a_w_in, a_cmp_pos, a_cmp_w1, a_cmp_w2, a_w_out = map(f32, (a_w_in, a_cmp_pos, a_cmp_w1,
                                                                a_cmp_w2, a_w_out))
    kv_norm, b_w_kv, b_w_q, b_sinks, b_w_out = map(f32, (kv_norm, b_w_kv, b_w_q, b_sinks, b_w_out))
    ffn_w_in, ffn_conv_w, ffn_conv_b, ffn_w_out, final_norm = map(
        f32, (ffn_w_in, ffn_conv_w, ffn_conv_b, ffn_w_out, final_norm))
    nc, dins = build_fused(NPHASE)
    ident = ident_np()
    gfin = np.ascontiguousarray(final_norm[None, :])
    cosA, sinA = rope_tables(np.arange(SEQ))
    constsA = nsa_consts(SEQ)
    maps = []
    for i in range(8):
        b, c = divmod(i, 4)
        g = c
        m = dict(xA=x[b], xB=_chunk_with_halo(x[b], c, 128),
                 flag=np.full((128, 1), 0.0 if c == 0 else 1.0, np.float32))
        for l in range(2):
            a = dict(g_attn=gT_np(norm_attn[l]), w1=a_cmp_w1[l], w2=a_cmp_w2[l],
                     cos_t=cosA, sin_t=sinA)
            a.update(constsA)
            a.update(nsa_weights(a_w_in[l], a_cmp_pos[l], g))
            for k, v in a.items():
                m[f"{k}_A{l}"] = v
            bb = dict(w_o=a_w_out[l], w_in=ffn_w_in[l], w_out=ffn_w_out[l],
                      cwb=_cwb(ffn_conv_w[l], ffn_conv_b[l]), g_ffn=gT_np(norm_ffn[l]), g_fin=gfin,
                      ident=ident)
            for k, v in bb.items():
                m[f"{k}_B{l}"] = v
        pos = c * CH - 256 + np.arange(CH + 256)
        cos_t, sin_t = rope_tables(pos)
        for l in range(2, 4):
            j = l - 2
            cc = dict(w_q=b_w_q[j], w_kv=b_w_kv, sinks_b=sinks_row(b_sinks[j]),
                      g_attn=gT_np(norm_attn[l]), g_kv=gT_np(kv_norm), cos_t=cos_t, sin_t=sin_t,
                      masks=swa_masks(c > 0), w_o=b_w_out[j], w_in=ffn_w_in[l], w_out=ffn_w_out[l],
                      cwb=_cwb(ffn_conv_w[l], ffn_conv_b[l]), g_ffn=gT_np(norm_ffn[l]), g_fin=gfin,
                      ident=ident)
            for k, v in cc.items():
                m[f"{k}_C{l}"] = v
        m = {k: v for k, v in m.items() if k in dins}
        maps.append(m)
    res = _run(nc, maps)
    h = np.stack([np.concatenate([res[b * 4 + c]["out"] for c in range(4)], 0)
                  for b in range(NB)], 0)
    return np.ascontiguousarray(h.astype(np.float32))
```

```python
import numpy as np
import ml_dtypes
import concourse.bass as bass
import concourse.mybir as mybir
from concourse.bass_utils import run_bass_kernel_spmd

F32 = mybir.dt.float32
BF16 = mybir.dt.bfloat16
AF = mybir.ActivationFunctionType
ALU = mybir.AluOpType
AX = mybir.AxisListType

NPBF16 = ml_dtypes.bfloat16

D = 1024
DFF = 2816
EPS = 1e-6
MASKV = -240000.0

COMPUTE = ("pe", "act", "dve", "pool")
EPOCH = 30000
NSLOT = 12
SAME_ENG_SYNC = True
ZERO_BIAS = True


class Sched:
    def __init__(self, nc):
        self.nc = nc
        self.streams = {e: [] for e in COMPUTE + ("sp",)}
        self.cnt = {e: 0 for e in COMPUTE}
        self.known = {e: {} for e in self.streams}
        self.known_dma = {e: set() for e in self.streams}
        self.last_w = {}
        self.readers = {}
        self.ndma = {e: 0 for e in self.streams}
        self.sems = {}
        self.nsem = 0
        self.out_dmas = []
        self.ncc = 0
        self.cc_pending = []
        self.snap = {e: [] for e in COMPUTE}
        self.snapd = {}

    def _sem(self, name):
        if name not in self.sems:
            self.sems[name] = self.nc.alloc_semaphore(name=name)
        return self.sems[name]

    def _ev_wait_args(self, ev):
        kind = ev[0]
        if kind == "c":
            _, eng, idx = ev
            ep, off = divmod(idx, EPOCH)
            return self._sem(f"s_{eng}_{ep}"), off + 1
        elif kind == "x":
            return self._sem(f"x_{ev[1]}"), ev[2]
        else:
            _, q, j = ev
            slot, use = j % NSLOT, j // NSLOT
            return self._sem(f"d_{q}_{slot}"), 16 * (use + 1)

    def _deps(self, eng, reads, writes):
        deps = set()
        for k in reads:
            w = self.last_w.get(k)
            if w is not None:
                deps.add(w)
        for k in writes:
            w = self.last_w.get(k)
            if w is not None:
                deps.add(w)
            for r in self.readers.get(k, ()):
                deps.add(r)
        waits = []
        best = {}
        for ev in deps:
            if ev[0] == "c":
                _, src, idx = ev
                if src == eng and eng == "pe":
                    continue
                if self.known[eng].get(src, -1) >= idx:
                    continue
                if best.get(src, -1) < idx:
                    best[src] = idx
            else:
                if ev in self.known_dma[eng]:
                    continue
                waits.append(ev)
                self.known_dma[eng].add(ev)
        for src, idx in best.items():
            self.known[eng][src] = idx
            waits.append(("c", src, idx))
        for ev in list(waits):
            sn = self.snap[ev[1]][ev[2]] if ev[0] == "c" else self.snapd.get(ev)
            if sn is None:
                continue
            kn = self.known[eng]
            for ci, ce in enumerate(COMPUTE):
                if sn[ci] > kn.get(ce, -1) and (ce != eng or True):
                    kn[ce] = sn[ci]
        return waits

    def _snapshot(self, eng):
        kn = self.known[eng]
        return tuple(kn.get(ce, -1) for ce in COMPUTE)

    def _mark(self, ev, reads, writes):
        for k in reads:
            self.readers.setdefault(k, []).append(ev)
        for k in writes:
            self.last_w[k] = ev
            self.readers[k] = []

    def op(self, eng, fn, reads=(), writes=()):
        assert eng in COMPUTE
        waits = self._deps(eng, reads, writes)
        idx = self.cnt[eng]
        self.cnt[eng] += 1
        ev = ("c", eng, idx)
        if not SAME_ENG_SYNC or eng == "pe":
            self.known[eng][eng] = idx
        sn = list(self._snapshot(eng))
        sn[COMPUTE.index(eng)] = max(sn[COMPUTE.index(eng)], idx - 1)
        self.snap[eng].append(tuple(sn))
        self._mark(ev, reads, writes)
        self.streams[eng].append((waits, fn, ev))
        return ev

    def dma(self, q, fn, reads=(), writes=(), is_output=False):
        waits = self._deps(q, reads, writes)
        j = self.ndma[q]
        self.ndma[q] += 1
        if j >= NSLOT:
            prev = ("d", q, j - NSLOT)
            if prev not in self.known_dma[q]:
                waits.append(prev)
                self.known_dma[q].add(prev)
        ev = ("d", q, j)
        self.snapd[ev] = self._snapshot(q)
        self._mark(ev, reads, writes)
        self.streams[q].append((waits, fn, ev))
        if is_output:
            self.out_dmas.append(ev)
        return ev

    def pid(self, e, key="pid", fn=None):
        k = (self.cur_eng, key)
        if k not in self.pid_cache:
            if key == "pid":
                self.pid_cache[k] = e.partition_id()
            else:
                self.pid_cache[k] = e.snap(fn(self.pid(e)))
        return self.pid_cache[k]

    def cc(self, fn, reads=(), writes=(), n=1):
        waits = self._deps("pool", reads, writes)
        ev = ("x", self.ncc, n)
        self.ncc += 1
        self._mark(ev, reads, writes)
        self.streams["pool"].append((waits, fn, ev))
        self.known_dma["pool"].add(ev)
        self.cc_pending.append((ev, tuple(writes)))
        return ev

    def cc_wait(self):
        for ev, writes in self.cc_pending:
            idx = self.cnt["pool"]
            self.cnt["pool"] += 1
            nev = ("c", "pool", idx)
            self.snap["pool"].append(self._snapshot("pool"))
            self._mark(nev, (), writes)
            self.streams["pool"].append(([ev], lambda e: e.nop(), nev))
        self.cc_pending = []

    def barrier(self):
        self.cc_wait()
        evs = []
        for eng in COMPUTE:
            if self.cnt[eng] > 0:
                evs.append(("c", eng, self.cnt[eng] - 1))
        for q, n in self.ndma.items():
            for j in range(max(0, n - NSLOT), n):
                evs.append(("d", q, j))
        for eng in self.streams:
            waits = []
            for ev in evs:
                if ev[0] == "c":
                    if ev[1] == eng:
                        continue
                    if self.known[eng].get(ev[1], -1) >= ev[2]:
                        continue
                    self.known[eng][ev[1]] = ev[2]
                elif ev[0] == "x":
                    continue
                elif ev in self.known_dma[eng]:
                    continue
                else:
                    self.known_dma[eng].add(ev)
                waits.append(ev)
            self.streams[eng].append((waits, None, None))

    def emit(self, final=True):
        nc = self.nc
        if final:
            self.cc_wait()
        final_waits = list(self.out_dmas) if final else []
        self.pid_cache = {}
        with nc.Block() as block:
            def run(engname, e):
                self.cur_eng = engname
                for waits, fn, ev in self.streams[engname]:
                    for w in waits:
                        s, v = self._ev_wait_args(w)
                        e.wait_ge(s, v)
                    if fn is None:
                        continue
                    s, v = self._ev_wait_args(ev)
                    if ev[0] == "x":
                        fn(e, s)
                        continue
                    ins = fn(e)
                    if ev[0] == "c":
                        ins.then_inc(s, 1)
                    else:
                        ins.then_inc(s, 16)
                if engname == "sp":
                    for w in final_waits:
                        s, v = self._ev_wait_args(w)
                        e.wait_ge(s, v)

            @block.tensor
            def _(e):
                run("pe", e)

            @block.scalar
            def _(e):
                run("act", e)

            @block.vector
            def _(e):
                run("dve", e)

            @block.gpsimd
            def _(e):
                run("pool", e)

            @block.sync
            def _(e):
                run("sp", e)
        for k in self.streams:
            self.streams[k] = []


class Prog:
    def __init__(self, nc):
        from contextlib import ExitStack
        self.nc = nc
        self.s = Sched(nc)
        self.es = ExitStack()
        self.ndram = 0
        self.sfx = ""
        self.dins = {}
        self.ext = {}

    def sb(self, name, shape, dt):
        return self.es.enter_context(self.nc.sbuf_tensor("sb_" + name + self.sfx, list(shape), dt))

    def ps(self, name, shape, dt=F32):
        return self.es.enter_context(self.nc.psum_tensor("ps_" + name + self.sfx, list(shape), dt))

    def din(self, name, shape, dt):
        nm = name + self.sfx
        if nm in self.ext:
            return self.ext[nm]
        self.dins[nm] = (tuple(shape), dt)
        return self.nc.dram_tensor(nm, list(shape), dt, kind="ExternalInput").ap()

    def dint(self, name, shape, dt):
        return self.nc.dram_tensor(name, list(shape), dt)

    def phase_end(self):
        from contextlib import ExitStack
        self.s.barrier()
        self.s.emit(final=False)
        self.es.close()
        self.es = ExitStack()

    def dmaf(self, fn, r=(), w=(), q="sp", is_output=False):
        return self.s.dma(q, fn, r, w, is_output)

    def dout(self, name, shape, dt):
        return self.nc.dram_tensor(name, list(shape), dt, kind="ExternalOutput").ap()

    def dma(self, out, in_, r=(), w=(), q="sp", is_output=False):
        return self.s.dma(q, lambda e: e.dma_start(out=out, in_=in_), r, w, is_output)

    def mm(self, out, lhsT, rhs, start, stop, r=(), w=()):
        return self.s.op("pe", lambda e: e.matmul(out, lhsT, rhs, start=start, stop=stop), r, w)

    def tr(self, out, in_, ident, r=(), w=()):
        return self.s.op("pe", lambda e: e.transpose(out, in_, ident), r, w)

    def act(self, out, in_, func, r=(), w=(), bias=None, scale=None, accum_out=None):
        kw = {}
        if bias is not None:
            kw["bias"] = bias
        if scale is not None:
            kw["scale"] = scale
        if accum_out is not None:
            kw["accum_out"] = accum_out
        return self.s.op("act", lambda e: e.activation(out, in_, func, **kw), r, w)

    def tt(self, out, in0, in1, op, r=(), w=(), eng="dve"):
        return self.s.op(eng, lambda e: e.tensor_tensor(out, in0, in1, op), r, w)

    def ts(self, out, in0, s1, s2, op0, op1=None, r=(), w=(), eng="dve", accum_out=None):
        kw = {}
        if accum_out is not None:
            kw["accum_out"] = accum_out
        if op1 is None:
            return self.s.op(eng, lambda e: e.tensor_scalar(out, in0, s1, s2, op0, **kw), r, w)
        return self.s.op(eng, lambda e: e.tensor_scalar(out, in0, s1, s2, op0, op1, **kw), r, w)

    def stt(self, out, in0, scalar, in1, op0, op1, r=(), w=(), eng="dve"):
        return self.s.op(eng, lambda e: e.scalar_tensor_tensor(out, in0, scalar, in1, op0, op1), r, w)

    def cp(self, out, in_, r=(), w=(), eng="dve"):
        if eng == "act":
            return self.s.op("act", lambda e: e.copy(out, in_), r, w)
        return self.s.op(eng, lambda e: e.tensor_copy(out, in_), r, w)

    def recip(self, out, in_, r=(), w=()):
        return self.s.op("dve", lambda e: e.reciprocal(out, in_), r, w)

    def memset(self, ap, val, w=(), eng="dve"):
        return self.s.op(eng, lambda e: e.memset(ap, val), (), w)

    def finish(self):
        self.s.emit()
        self.es.close()
        return self.nc


def load_cast_weight(p, w_dram, dst, nk, ncols, stage, tag, chunk_cols=1024):
    i = 0
    for kc in range(nk):
        for c0 in range(0, ncols, chunk_cols):
            cw = min(chunk_cols, ncols - c0)
            stg = stage[i % 2]
            p.dma(stg[:, 0:cw], w_dram[kc * 128:(kc + 1) * 128, c0:c0 + cw],
                  w=[("stg", i % 2)])
            p.cp(dst[:, kc, c0:c0 + cw], stg[:, 0:cw], r=[("stg", i % 2)],
                 w=[(tag, kc)], eng="pool")
            i += 1


def rmsnorm_tile(p, x_ap, gain_bc, out_ap, scr, keys_r, keys_w, tagk):
    sq, ss, sd, rs = scr
    p.act(sq, x_ap, AF.Square, r=keys_r, w=[("sq", tagk), ("ss", tagk)], accum_out=ss)
    p.act(sd, ss, AF.Sqrt, r=[("ss", tagk)], w=[("sd", tagk)], bias=EPS, scale=1.0 / D)
    p.recip(rs, sd, r=[("sd", tagk)], w=[("rs", tagk)])
    if gain_bc is None:
        p.ts(out_ap, x_ap, rs, None, ALU.mult, r=list(keys_r) + [("rs", tagk)], w=keys_w)
    else:
        p.stt(out_ap, x_ap, rs, gain_bc, ALU.mult, ALU.mult,
              r=list(keys_r) + [("rs", tagk), "gains"], w=keys_w)


class FFNCtx:
    def __init__(self, p, pre, max_nt=4):
        self.p = p
        self.max_nt = max_nt
        nt = max_nt
        self.wo = p.sb(pre + "wo", [64, 16, 1024], BF16)
        self.woch = [p.sb(pre + f"woch{i}", [128, 512], BF16) for i in range(2)]
        self.fstage = [p.sb(pre + f"fstg{i}", [128, 2048], F32) for i in range(2)]
        self.stage = [self.fstage[i][:, 0:1024] for i in range(2)]
        self.gT = p.sb(pre + "gT", [128, 8], F32)
        self.wch = [p.sb(pre + f"wch{i}", [128, 8, 2, 128], BF16) for i in range(2)]
        self.h1 = p.sb(pre + "h1", [128, nt, 1024], F32)
        self.oT = p.sb(pre + "oT", [64, 16, nt * 128], BF16)
        self.hn = p.sb(pre + "hn", [128, 1024], BF16)
        self.hnT = p.sb(pre + "hnT", [128, 8, nt * 128], BF16)
        self.actT = p.sb(pre + "actT", [128, 22, nt * 128], BF16)
        self.usb = [[p.sb(pre + f"usb{i}{a}", [128, 2 + nt * 128], F32) for a in range(2)]
                    for i in range(2)]
        self.carry = p.sb(pre + "carry", [128, 44, 2], F32)
        self.cwb = p.sb(pre + "cwb", [128, 44, 4], F32)
        self.t1 = p.sb(pre + "t1", [128, nt * 128], F32)
        self.t2 = p.sb(pre + "t2", [128, nt * 128], F32)
        self.ca = p.sb(pre + "ca", [128, nt * 128], F32)
        self.cg = p.sb(pre + "cg", [128, nt * 128], F32)
        self.sa = p.sb(pre + "sa", [128, nt * 128], F32)
        self.sq = p.sb(pre + "sq", [128, 1024], BF16)
        self.ss = p.sb(pre + "ss", [128, 1], F32)
        self.sd = p.sb(pre + "sd", [128, 1], F32)
        self.rs = p.sb(pre + "rs", [128, 1], F32)
        self.gainf = p.sb(pre + "gainf", [128, 1024], F32)
        self.ident = p.sb(pre + "ident", [128, 128], BF16)
        self.hfin = p.sb(pre + "hfin", [128, 1024], F32)
        self.psum = p.ps(pre + "psum", [128, 7 * 512])
        self.psT = p.ps(pre + "psT", [128, 8, 128], BF16)
        self.psA = [self.bank(0), self.bank(1)]
        self.psU = [[self.bank(2), self.bank(3)], [self.bank(4), self.bank(5)]]
        self.nA = 0
        self.nfc = 0
        self.nwo = 0

    def bank(self, i, n=1):
        return self.psum[:, i * 512:(i + n) * 512]

    def load_weights(self, w_o, w_in, w_out, cwb, g_ffn, ident, g_final=None, head_order=None):
        p = self.p
        self.w_in = w_in
        p.dma(self.ident[:], ident, w=["ident"])
        p.dma(self.cwb[:], cwb.rearrange("(c p) f -> p c f", p=128), w=["cwb"])
        p.dma(self.gT[:], g_ffn, w=["gT"])
        self.w_out = w_out
        nc = p.nc
        self.winb = nc.dram_tensor("winb" + p.sfx, [44 * 128, 1024], BF16).ap()
        self.woutb = nc.dram_tensor("woutb" + p.sfx, [44 * 128, 512], BF16).ap()
        i = 0
        for fc in range(22):
            for ag in range(2):
                par = i % 2
                i += 1
                stg = self.fstage[par]
                c0 = ag * DFF + fc * 128
                sk = ("stg", "ffn", par, 0)
                p.dma(stg[:, 0:1024].rearrange("p (c f) -> p c f", c=8),
                      w_in[:, c0:c0 + 128].rearrange("(c p) f -> p c f", p=128), w=[sk])
                p.tt(self.wch[par][:, :, 0, :],
                     stg[:, 0:1024].rearrange("p (c f) -> p c f", c=8),
                     self.gT[:].unsqueeze(2).to_broadcast([128, 8, 128]), ALU.mult,
                     r=[sk, "gT"], w=[("wch", par, 0)], eng="pool")
                p.dma(self.winb[(fc * 2 + ag) * 128:(fc * 2 + ag + 1) * 128, :],
                      self.wch[par][:, :, 0, :], r=[("wch", par, 0)], w=["winb"], q="pool")
        for half in range(2):
            for fc in range(22):
                par = i % 2
                i += 1
                stg = self.fstage[par]
                sk = ("stg", "ffn", par, 0)
                p.dma(stg[:, 0:512], w_out[fc * 128:(fc + 1) * 128, half * 512:(half + 1) * 512],
                      w=[sk])
                p.cp(self.woch[par][:], stg[:, 0:512], r=[sk], w=[("woch", par)], eng="pool")
                p.dma(self.woutb[(half * 22 + fc) * 128:(half * 22 + fc + 1) * 128, :],
                      self.woch[par][:], r=[("woch", par)], w=["woutb"], q="pool")
        if g_final is not None:
            p.dma(self.gainf[:], g_final.to_broadcast([128, 1024]), w=["gainf"])
        p.memset(self.carry[:], 0.0, w=["carry"])
        if w_o is not None:
            ho = head_order if head_order is not None else list(range(16))
            for i, h in enumerate(ho):
                stg = self.stage[i % 2]
                sk = ("stg", "ffn", i % 2, 0)
                p.dma(stg[0:64, :], w_o[h * 64:(h + 1) * 64, :], w=[sk])
                p.cp(self.wo[:, i, :], stg[0:64, :], r=[sk], w=[("wo", i)], eng="pool")

    def run_supertile(self, nt, h_src, oT_src, h_dst, n_skip_out=0, final_norm=False,
                      h1_preloaded=False, h_fn=None, oT_fn=None, n_flag=0, flag=None,
                      rkeys=(), dkey=None):
        p = self.p
        ntok = nt * 128
        if not h1_preloaded:
            for j in range(nt):
                p.dma(self.h1[:, j, :], h_src[j * 128:(j + 1) * 128, :], r=list(rkeys),
                      w=[("h1", j)])
                if j < n_flag:
                    p.ts(self.h1[:, j, :], self.h1[:, j, :], flag, None, ALU.mult,
                         r=[("h1", j), "flag"], w=[("h1", j)])
        if oT_src is not None or oT_fn is not None:
            if oT_fn is not None:
                for j in range(nt):
                    for g in range(4):
                        p.dma(self.oT[:, g * 4:(g + 1) * 4, j * 128:(j + 1) * 128], oT_fn(j, g),
                              r=list(rkeys), w=["oT"])
            elif not isinstance(oT_src, str):
                p.dma(self.oT[:, :, 0:ntok], oT_src.rearrange("(c p) t -> p c t", p=64), w=["oT"])
            for j in range(nt):
                for half in range(2):
                    ps = self.psA[self.nA % 2]
                    pk = ("bank", self.nA % 2)
                    self.nA += 1
                    for kc in range(16):
                        p.mm(ps, self.oT[:, kc, j * 128:(j + 1) * 128],
                             self.wo[:, kc, half * 512:(half + 1) * 512],
                             start=(kc == 0), stop=(kc == 15),
                             r=["oT", ("wo", kc)], w=[pk])
                    hs = self.h1[:, j, half * 512:(half + 1) * 512]
                    p.tt(hs, hs, ps, ALU.add, r=[pk, ("h1", j)], w=[("h1", j)])
        for j in range(nt):
            rmsnorm_tile(p, self.h1[:, j, :], None, self.hn[:],
                         (self.sq[:], self.ss[:], self.sd[:], self.rs[:]),
                         [("h1", j)], ["hn"], "f")
            for kc in range(8):
                p.tr(self.psT[:, kc, :], self.hn[:, kc * 128:(kc + 1) * 128], self.ident[:],
                     r=["hn", "ident"], w=["psT"])
            p.cp(self.hnT[:, :, j * 128:(j + 1) * 128], self.psT[:], r=["psT"], w=[("hnT", j)],
                 eng="act")
        hnT_keys = [("hnT", j) for j in range(nt)]
        for fc in range(22):
            par = self.nfc % 2
            self.nfc += 1
            wch = self.wch[par]
            for ag in range(2):
                p.dma(wch[:, :, ag, :],
                      self.winb[(fc * 2 + ag) * 128:(fc * 2 + ag + 1) * 128, :].rearrange(
                          "p (c f) -> p c f", c=8),
                      r=["winb"], w=[("wch", par, ag)])
            cs = []
            for ag in range(2):
                ps = self.psU[par][ag]
                pk = ("bank", 2 + 2 * par + ag)
                for kc in range(8):
                    p.mm(ps[:, 0:ntok], wch[:, kc, ag, :], self.hnT[:, kc, 0:ntok],
                         start=(kc == 0), stop=(kc == 7),
                         r=[("wch", par, ag)] + hnT_keys, w=[pk])
                usb = self.usb[par][ag]
                uk = ("usb", par, ag)
                ch = ag * 22 + fc
                p.cp(usb[:, 0:2], self.carry[:, ch, :], r=["carry%d" % ch, "carry"], w=[uk],
                     eng="pool")
                p.cp(usb[:, 2:2 + ntok], ps[:, 0:ntok], r=[pk], w=[uk], eng="act")
                p.cp(self.carry[:, ch, :], usb[:, ntok:ntok + 2], r=[uk], w=["carry%d" % ch],
                     eng="pool")
                cw = self.cwb
                dst = self.ca if ag == 0 else self.cg
                dk = "ca" if ag == 0 else "cg"
                p.ts(self.t1[:, 0:ntok], usb[:, 2:2 + ntok], cw[:, ch, 2:3], cw[:, ch, 3:4],
                     ALU.mult, ALU.add, r=[uk, "cwb"], w=["t1"])
                p.stt(self.t2[:, 0:ntok], usb[:, 1:1 + ntok], cw[:, ch, 1:2], self.t1[:, 0:ntok],
                      ALU.mult, ALU.add, r=[uk, "cwb", "t1"], w=["t2"])
                p.stt(dst[:, 0:ntok], usb[:, 0:ntok], cw[:, ch, 0:1], self.t2[:, 0:ntok],
                      ALU.mult, ALU.add, r=[uk, "cwb", "t2"], w=[dk])
            p.act(self.sa[:, 0:ntok], self.ca[:, 0:ntok], AF.Silu, r=["ca"], w=["sa"])
            p.tt(self.actT[:, fc, 0:ntok], self.sa[:, 0:ntok], self.cg[:, 0:ntok], ALU.mult,
                 r=["sa", "cg"], w=[("actT", fc)])
        for half in range(2):
            for fc in range(22):
                wp = self.nwo % 2
                self.nwo += 1
                p.dma(self.woch[wp][:],
                      self.woutb[(half * 22 + fc) * 128:(half * 22 + fc + 1) * 128, :],
                      r=["woutb"], w=[("woch", wp)])
                for j in range(nt):
                    p.mm(self.bank(j), self.actT[:, fc, j * 128:(j + 1) * 128], self.woch[wp][:],
                         start=(fc == 0), stop=(fc == 21),
                         r=[("actT", fc), ("woch", wp)], w=[("bank", j)])
            for j in range(nt):
                hs = self.h1[:, j, half * 512:(half + 1) * 512]
                p.tt(hs, hs, self.bank(j), ALU.add, r=[("bank", j), ("h1", j)], w=[("h1", j)])
        for j in range(nt):
            if h_dst is not None and j >= n_skip_out:
                jo = j - n_skip_out
                if final_norm:
                    rmsnorm_tile(p, self.h1[:, j, :], self.gainf[:], self.hfin[:],
                                 (self.sq[:], self.ss[:], self.sd[:], self.rs[:]),
                                 [("h1", j), "gainf"], ["hfin"], "f")
                    p.dma(h_dst[jo * 128:(jo + 1) * 128, :], self.hfin[:], r=["hfin"],
                          w=([dkey] if dkey else []), q="pool", is_output=True)
                else:
                    p.dma(h_dst[jo * 128:(jo + 1) * 128, :], self.h1[:, j, :], r=[("h1", j)],
                          w=([dkey] if dkey else []), q="pool", is_output=(dkey is None))


def ident_np():
    return np.eye(128, dtype=np.float32).astype(NPBF16)


def b_head_order():
    return [g * 4 + 2 * hpl + par for g in range(4) for par in range(2) for hpl in range(2)]


def build_B(st_sizes, n_skip_tiles, final_norm=False, p=None, io=None):
    fused = p is not None
    if not fused:
        nc = bass.Bass("TRN2", target_bir_lowering=False)
        p = Prog(nc)
    ntiles = sum(st_sizes)
    ntok = ntiles * 128
    if not fused:
        h_in = p.din("h_in", [ntok, D], F32)
        oT_in = p.din("oT_in", [D, ntok], BF16)
    w_o = p.din("w_o", [D, D], F32)
    w_in = p.din("w_in", [D, 2 * DFF], F32)
    w_out = p.din("w_out", [DFF, D], F32)
    cwb = p.din("cwb", [2 * DFF, 4], F32)
    g_ffn = p.din("g_ffn", [128, 8], F32)
    g_fin = p.din("g_fin", [1, D], F32)
    ident = p.din("ident", [128, 128], BF16)
    if not fused:
        h_out = p.dout("h_out", [(ntiles - n_skip_tiles) * 128, D], F32)
    else:
        h_out = io["h_dst"]
    f = FFNCtx(p, "f_", max_nt=max(st_sizes))
    f.load_weights(w_o, w_in, w_out, cwb, g_ffn, ident, g_fin,
                   head_order=(b_head_order() if fused else None))
    if fused:
        flag_sb = p.sb("flag", [128, 1], F32)
        p.dma(flag_sb[:], io["flag"], w=["flag"])
        if io.get("after_setup"):
            io["after_setup"]()
    t0 = 0
    for nt in st_sizes:
        skip = max(0, min(nt, n_skip_tiles - t0))
        o0 = max(0, t0 - n_skip_tiles)
        dst = h_out[o0 * 128:(o0 + nt - skip) * 128, :] if skip < nt else None
        if fused:
            f.run_supertile(nt, io["h_ap"][t0 * 128:(t0 + nt) * 128, :], None, dst,
                            n_skip_out=skip, final_norm=final_norm,
                            oT_fn=lambda j, g, t0=t0: io["oT_ap"](t0 + j, g),
                            n_flag=skip, flag=flag_sb[:], rkeys=io["rkeys"], dkey=io["dkey"])
        else:
            f.run_supertile(nt, h_in[t0 * 128:(t0 + nt) * 128, :],
                            oT_in[:, t0 * 128:(t0 + nt) * 128], dst, n_skip_out=skip,
                            final_norm=final_norm)
        t0 += nt
    if fused:
        return None
    return p.finish()


def c_head_order():
    return [8 * g + 2 * hpl + par for g in range(2) for par in range(2) for hpl in range(4)]


def build_C(st_sizes, n_skip_tiles, final_norm=False, p=None, io=None):
    fused = p is not None
    if not fused:
        nc = bass.Bass("TRN2", target_bir_lowering=False)
        p = Prog(nc)
    ntiles = sum(st_sizes)
    ntok = ntiles * 128
    mx = max(st_sizes)
    if not fused:
        h_in = p.din("h_in", [ntok, D], F32)
        hkv_in = p.din("hkv_in", [ntok, D], F32)
    w_q = p.din("w_q", [D, D], F32)
    w_kv = p.din("w_kv", [D, 256], F32)
    sinks_b = p.din("sinks_b", [1, 2048], F32)
    g_attn = p.din("g_attn", [128, 8], F32)
    g_kv = p.din("g_kv", [128, 8], F32)
    cos_t = p.din("cos_t", [128, ntok], F32)
    sin_t = p.din("sin_t", [128, ntok], F32)
    masks = p.din("masks", [3, 128, 512], BF16)
    w_o = p.din("w_o", [D, D], F32)
    w_in = p.din("w_in", [D, 2 * DFF], F32)
    w_out = p.din("w_out", [DFF, D], F32)
    cwb = p.din("cwb", [2 * DFF, 4], F32)
    g_ffn = p.din("g_ffn", [128, 8], F32)
    g_fin = p.din("g_fin", [1, D], F32)
    ident = p.din("ident", [128, 128], BF16)
    if not fused:
        h_out = p.dout("h_out", [(ntiles - n_skip_tiles) * 128, D], F32)
    else:
        h_out = io["h_dst"]

    f = FFNCtx(p, "f_", max_nt=mx)
    f.load_weights(w_o, w_in, w_out, cwb, g_ffn, ident, g_fin, head_order=c_head_order())
    if fused:
        flag_sb = p.sb("flag", [128, 1], F32)
        p.dma(flag_sb[:], io["flag"], w=["flag"])

    gTq = p.sb("gTq", [128, 8], F32)
    gTk = p.sb("gTk", [128, 8], F32)
    wk2 = p.sb("wk2", [128, 8, 2, 2, 128], BF16)
    wv = p.sb("wv", [128, 8, 128], BF16)
    hkv = p.sb("hkv", [128, 1024], F32)
    hnqT = f.hnT
    hkvT = p.sb("hkvT", [128, 8, mx * 128], BF16)
    QT2 = p.sb("QT2", [128, 8, mx * 128], BF16)
    KT2 = p.sb("KT2", [128, 2, (mx + 1) * 128], BF16)
    VA = p.sb("VA", [128, mx + 1, 2, 65], BF16)
    PT = [p.sb(f"PT{i}", [128, 1024], BF16) for i in range(2)]
    msk = p.sb("msk", [128, 3, 512], BF16)
    cos_sb = p.sb("cos_sb", [128, mx * 128], F32)
    sin_sb = p.sb("sin_sb", [128, mx * 128], F32)
    sexp = p.sb("sexp", [128, 2048], BF16)
    zr = p.sb("zr", [128, 1024], F32)
    rz = zr
    ones = p.sb("ones", [128, 64], F32)
    osb = f.hfin[0:64, :]
    psS = [f.bank(2, 2), f.bank(4, 2)]
    psSk = [[("bank", 2), ("bank", 3)], [("bank", 4), ("bank", 5)]]
    psO = f.bank(0, 2)
    psOk = [("bank", 0), ("bank", 1)]
    psB = f.bank(6)
    psBk = [("bank", 6)]

    p.dma(gTq[:], g_attn, w=["gTq"])
    p.dma(gTk[:], g_kv, w=["gTk"])
    p.dma(msk[:], masks.rearrange("m p c -> p m c"), w=["msk"])
    p.dma(zr[64:65, :], sinks_b[:, 0:1024], w=["zr"])
    p.act(sexp[64:65, 0:1024], zr[64:65, :], AF.Exp, r=["zr"], w=["sexp"])
    p.dma(zr[64:65, :], sinks_b[:, 1024:2048], r=["sexp"], w=["zr"])
    p.act(sexp[64:65, 1024:2048], zr[64:65, :], AF.Exp, r=["zr"], w=["sexp"])
    p.memset(ones[:], 1.0, w=["ones"])
    p.memset(VA[:], 1.0, w=["VA"] + [("VA", i) for i in range(mx + 1)])
    p.memset(KT2[:], 0.0, w=["KT2", ("KT2", 0), ("KT2", 1)])
    for kc in range(8):
        stg = f.stage[kc % 2]
        sk = ("stg", "ffn", kc % 2, 0)
        gs = gTk[:, kc:kc + 1]
        p.dma(stg[:, 0:256], w_kv[kc * 128:(kc + 1) * 128, :], w=[sk])
        for g in range(2):
            for dup in range(2):
                p.ts(wk2[:, kc, g, 0, dup * 64:(dup + 1) * 64], stg[:, g * 64:(g + 1) * 64],
                     gs, None, ALU.mult, r=[sk, "gTk"], w=["wk2"], eng="pool")
                p.ts(wk2[:, kc, g, 1, dup * 64:dup * 64 + 32], stg[:, g * 64 + 32:g * 64 + 64],
                     gs, None, ALU.mult, r=[sk, "gTk"], w=["wk2"], eng="pool")
                p.ts(wk2[:, kc, g, 1, dup * 64 + 32:dup * 64 + 64], stg[:, g * 64:g * 64 + 32],
                     gs, None, ALU.mult, r=[sk, "gTk"], w=["wk2"], eng="pool")
        p.ts(wv[:, kc, :], stg[:, 128:256], gs, None, ALU.mult, r=[sk, "gTk"], w=["wv"], eng="pool")

    if fused and io.get("after_setup"):
        io["after_setup"]()
    scr = (f.sq[:], f.ss[:], f.sd[:], f.rs[:])
    t0 = 0
    first_real = n_skip_tiles
    for nt in st_sizes:
        n = nt * 128
        for j in range(nt):
            if fused:
                p.dma(f.h1[:, j, :], io["h_tile"](t0 + j), r=list(io["rkeys"]), w=[("h1", j)])
                if t0 + j < n_skip_tiles:
                    p.ts(f.h1[:, j, :], f.h1[:, j, :], flag_sb[:], None, ALU.mult,
                         r=[("h1", j), "flag"], w=[("h1", j)])
            else:
                p.dma(f.h1[:, j, :], h_in[(t0 + j) * 128:(t0 + j + 1) * 128, :], w=[("h1", j)])
        p.dma(cos_sb[:, 0:n], cos_t[:, t0 * 128:t0 * 128 + n], w=["cos"])
        p.dma(sin_sb[:, 0:n], sin_t[:, t0 * 128:t0 * 128 + n], w=["sin"])
        for j in range(nt):
            rmsnorm_tile(p, f.h1[:, j, :], None, f.hn[:], scr, [("h1", j)], ["hn"], "f")
            for kc in range(8):
                p.tr(f.psT[:, kc, :], f.hn[:, kc * 128:(kc + 1) * 128], f.ident[:],
                     r=["hn", "ident"], w=["psT"])
            p.cp(hnqT[:, :, j * 128:(j + 1) * 128], f.psT[:], r=["psT"], w=[("hnT", j)], eng="act")
            if fused:
                p.dma(hkv[:], io["hkv_tile"](t0 + j), r=list(io["rkeys"]), w=["hkv"])
                if t0 + j < n_skip_tiles:
                    p.ts(hkv[:], hkv[:], flag_sb[:], None, ALU.mult, r=["hkv", "flag"], w=["hkv"])
            else:
                p.dma(hkv[:], hkv_in[(t0 + j) * 128:(t0 + j + 1) * 128, :], w=["hkv"])
            rmsnorm_tile(p, hkv[:], None, f.hn[:], scr, ["hkv"], ["hn"], "f")
            for kc in range(8):
                p.tr(f.psT[:, kc, :], f.hn[:, kc * 128:(kc + 1) * 128], f.ident[:],
                     r=["hn", "ident"], w=["psT"])
            p.cp(hkvT[:, :, j * 128:(j + 1) * 128], f.psT[:], r=["psT"], w=[("hkvT", j)], eng="act")
        hq_keys = [("hnT", j) for j in range(nt)]
        hk_keys = [("hkvT", j) for j in range(nt)]

        def rope_out(dst, psn, pss, rk, wk):
            p.tt(f.t1[:, 0:n], psn, cos_sb[:, 0:n], ALU.mult, r=rk[0:1] + ["cos"], w=["t1"])
            p.tt(f.t2[:, 0:n], pss, sin_sb[:, 0:n], ALU.mult, r=rk[1:2] + ["sin"], w=["t2"])
            p.tt(dst, f.t1[:, 0:n], f.t2[:, 0:n], ALU.add, r=["t1", "t2"], w=wk)

        for g in range(2):
            par = f.nfc % 2
            f.nfc += 1
            bk = [("bank", 2 + 2 * par), ("bank", 3 + 2 * par)]
            for v in range(2):
                for kc in range(8):
                    p.mm(f.psU[par][v][:, 0:n], wk2[:, kc, g, v, :], hkvT[:, kc, 0:n],
                         start=(kc == 0), stop=(kc == 7), r=["wk2"] + hk_keys, w=[bk[v]])
            rope_out(KT2[:, g, 128:128 + n], f.psU[par][0][:, 0:n], f.psU[par][1][:, 0:n],
                     bk, [("KT2", g)])
        for j in range(nt):
            ps = f.psA[f.nA % 2]
            pk = ("bank", f.nA % 2)
            f.nA += 1
            for kc in range(8):
                p.mm(ps[:, 0:128], hkvT[:, kc, j * 128:(j + 1) * 128], wv[:, kc, :],
                     start=(kc == 0), stop=(kc == 7), r=[("hkvT", j), "wv"], w=[pk])
            p.cp(VA[:, j + 1, :, 0:64], ps[:, 0:128].rearrange("p (g d) -> p g d", g=2),
                 r=[pk], w=[("VA", j + 1)], eng="act")
        for hp in range(8):
            par = f.nfc % 2
            f.nfc += 1
            stg = f.fstage[par]
            wch = f.wch[par]
            p.dma(stg[:, 0:1024].rearrange("p (c f) -> p c f", c=8),
                  w_q[:, hp * 128:(hp + 1) * 128].rearrange("(c p) f -> p c f", p=128),
                  w=[("stg", "ffn", par, 0)])
            p.tt(wch[:, :, 0, :], stg[:, 0:1024].rearrange("p (c f) -> p c f", c=8),
                 gTq[:].unsqueeze(2).to_broadcast([128, 8, 128]), ALU.mult,
                 r=[("stg", "ffn", par, 0), "gTq"], w=[("wch", par, 0)], eng="pool")
            src = wch[:, :, 0, :].rearrange("p c (h d) -> p c h d", h=2)
            dsw = wch[:, :, 1, :].rearrange("p c (h d) -> p c h d", h=2)
            p.cp(dsw[:, :, :, 0:32], src[:, :, :, 32:64], r=[("wch", par, 0)],
                 w=[("wch", par, 1)], eng="pool")
            p.cp(dsw[:, :, :, 32:64], src[:, :, :, 0:32], r=[("wch", par, 0)],
                 w=[("wch", par, 1)], eng="pool")
            bk = [("bank", 2 + 2 * par), ("bank", 3 + 2 * par)]
            for v in range(2):
                for kc in range(8):
                    p.mm(f.psU[par][v][:, 0:n], wch[:, kc, v, :], hnqT[:, kc, 0:n],
                         start=(kc == 0), stop=(kc == 7),
                         r=[("wch", par, v)] + hq_keys, w=[bk[v]])
            rope_out(QT2[:, hp, 0:n], f.psU[par][0][:, 0:n], f.psU[par][1][:, 0:n],
                     bk, [("QT2", hp)])
        nS = 0
        for j in range(nt):
            gt = t0 + j
            for g in range(2):
                chunks = [(j, 0 if gt == first_real else 1), (j + 1, 2)]
                for ci, (slot, mi) in enumerate(chunks):
                    sp_ = nS % 2
                    nS += 1
                    for par in range(2):
                        pr = slice(par * 64, (par + 1) * 64)
                        p.mm(psS[sp_][:, par * 512:(par + 1) * 512],
                             KT2[pr, g, slot * 128:(slot + 1) * 128],
                             QT2[pr, 4 * g:4 * g + 4, j * 128:(j + 1) * 128],
                             start=True, stop=False,
                             r=[("KT2", g)] + [("QT2", 4 * g + i) for i in range(4)],
                             w=[psSk[sp_][par]])
                        p.mm(psS[sp_][:, par * 512:(par + 1) * 512], f.ident[:], msk[:, mi, :],
                             start=False, stop=True, r=["ident", "msk"], w=[psSk[sp_][par]])
                    p.act(PT[sp_][:], psS[sp_], AF.Exp, r=psSk[sp_], w=[("PT", sp_)], scale=0.125)
                    for par in range(2):
                        p.mm(psO[0:65, par * 512:(par + 1) * 512], VA[:, slot, g, :],
                             PT[sp_][:, par * 512:(par + 1) * 512],
                             start=(ci == 0), stop=(ci == 1),
                             r=[("PT", sp_), ("VA", slot), "VA"], w=[psOk[par]])
                p.tt(zr[64:65, :], psO[64:65, :], sexp[64:65, g * 1024:(g + 1) * 1024], ALU.add,
                     r=psOk + ["sexp"], w=["zr"])
                p.recip(rz[64:65, :], zr[64:65, :], r=["zr"], w=["rz"])
                p.cp(osb, psO[0:64, :], r=psOk, w=["hfin"], eng="act")
                for par in range(2):
                    p.mm(psB[0:64, :], ones[64:65, :], rz[64:65, par * 512:(par + 1) * 512],
                         start=True, stop=True, r=["ones", "rz"], w=psBk)
                    dst = f.oT[:, g * 8 + par * 4:g * 8 + par * 4 + 4, j * 128:(j + 1) * 128]
                    p.tt(dst, osb[:, par * 512:(par + 1) * 512].rearrange("p (h q) -> p h q", h=4),
                         psB[0:64, :].rearrange("p (h q) -> p h q", h=4), ALU.mult,
                         r=["hfin"] + psBk, w=["oT"])
        for g in range(2):
            p.cp(KT2[:, g, 0:128], KT2[:, g, n:n + 128], r=[("KT2", g)], w=[("KT2", g)], eng="pool")
        p.cp(VA[:, 0, :, :], VA[:, nt, :, :], r=[("VA", nt)], w=[("VA", 0)], eng="pool")
        skip = max(0, min(nt, n_skip_tiles - t0))
        o0 = max(0, t0 - n_skip_tiles)
        dst = h_out[o0 * 128:(o0 + nt - skip) * 128, :] if skip < nt else None
        f.run_supertile(nt, None, "resident", dst, n_skip_out=skip, final_norm=final_norm,
                        h1_preloaded=True, dkey=(io["dkey"] if fused else None))
        t0 += nt
    if fused:
        return None
    return p.finish()


def rope_tables(pos):
    half = 32
    inv = (np.float32(10000.0) ** (-np.arange(half, dtype=np.float32) / half)).astype(np.float32)
    ang = pos.astype(np.float32)[None, :] * inv[:, None]
    cos = np.cos(ang).astype(np.float32)
    sin = np.sin(ang).astype(np.float32)
    cos64 = np.concatenate([cos, cos], 0)
    sin64 = np.concatenate([-sin, sin], 0)
    return (np.ascontiguousarray(np.concatenate([cos64, cos64], 0)),
            np.ascontiguousarray(np.concatenate([sin64, sin64], 0)))


def swa_masks(first_exists):
    i = np.arange(128)[:, None]
    q = np.arange(128)[None, :]
    prev = np.where(i > q, 0.0, MASKV).astype(np.float32)
    cur = np.where(i <= q, 0.0, MASKV).astype(np.float32)
    pf = prev if first_exists else np.full((128, 128), MASKV, np.float32)
    m = np.stack([np.tile(pf, (1, 4)), np.tile(prev, (1, 4)), np.tile(cur, (1, 4))], 0)
    return m.astype(NPBF16)


def sinks_row(sinks16):
    ho = c_head_order()
    return np.ascontiguousarray(
        np.repeat(np.asarray(sinks16, np.float32)[ho], 128)[None, :])


def gT_np(g):
    return np.ascontiguousarray(np.asarray(g, np.float32).reshape(8, 128).T)


FORCE = 1.0e6
TINY = 1.0e-30
C_BF = float(np.float32(NPBF16(-MASKV)))
LN_C = float(np.log(np.float64(C_BF)))
MUL_MASK = False
KEEP_ZB = False


def build_A(S, dbg=99, p=None, io=None):
    fused = p is not None
    if not fused:
        nc = bass.Bass("TRN2", target_bir_lowering=False)
        p = Prog(nc)
    nc = p.nc
    NST = S // 512
    NQB = S // 128
    NCC = max(1, S // 2048)
    if not fused:
        h_in = p.din("h_in", [S, D], F32)
    g_attn = p.din("g_attn", [128, 8], F32)
    wq_d = p.din("wq", [D, 256], F32)
    wk3_d = p.din("wk3", [D, 192], F32)
    wv3_d = p.din("wv3", [D, 192], F32)
    wg_d = p.din("wg", [D, 12], F32)
    w1_d = p.din("w1", [2, 2048, 256], F32)
    w2_d = p.din("w2", [2, 256, 64], F32)
    posT_d = p.din("posT", [64, 2, 32], F32)
    cos_d = p.din("cos_t", [128, S], F32)
    sin_d = p.din("sin_t", [128, S], F32)
    ccos_d = p.din("ccos_t", [128, NCC * 128], F32)
    csin_d = p.din("csin_t", [128, NCC * 128], F32)
    pmask_d = p.din("pmask", [2, 16, 128, 128], BF16)
    r0mask_d = p.din("r0mask", [128, 512], BF16)
    cmask_d = p.din("cmask", [2, 128, 512], BF16)
    emat_d = p.din("emat", [64, 128, 128], BF16)
    wfull_d = p.din("wfull", [NCC * 128, 257], BF16)
    fix_d = p.din("fix3", [128, 6], F32)
    ident_d = p.din("ident", [128, 128], BF16)
    gscr = [nc.dram_tensor(f"gscr{i}" + p.sfx, [1, 12 * 512], F32).ap() for i in range(2)]
    if not fused:
        oT_out = p.dout("oT_out", [64, 4, S], BF16)

    ident = p.sb("ident", [128, 128], BF16)
    gT = p.sb("gT", [128, 8], F32)
    fst = [p.sb(f"fst{i}", [128, 1024], F32) for i in range(2)]
    WQ = p.sb("WQ", [128, 8, 2, 256], BF16)
    WKS = p.sb("WKS", [128, 8, 2, 128], BF16)
    WKW = p.sb("WKW", [128, 8, 2, 128], BF16)
    WKC = p.sb("WKC", [128, 8, 64], BF16)
    WVC = p.sb("WVC", [128, 8, 64], BF16)
    WV2 = p.sb("WV2", [128, 8, 128], BF16)
    WG = p.sb("WG", [128, 8, 12], BF16)
    W1c = [p.sb(f"W1c{i}", [64, 4, 256], BF16) for i in range(2)]
    W2K = p.sb("W2K", [128, 2, 2, 128], BF16)
    W2V = p.sb("W2V", [128, 2, 64], BF16)
    posT = p.sb("posT", [64, 2, 32], BF16)
    c1 = p.sb("c1", [128, 4], F32)
    ccos = p.sb("ccos", [128, NCC * 128], F32)
    csin = p.sb("csin", [128, NCC * 128], F32)
    pmask = p.sb("pmask", [128, 2, 16, 128], BF16)
    r0mask = p.sb("r0mask", [128, 512], BF16)
    cmask = p.sb("cmask", [128, 2, 512], BF16)
    emat = p.sb("emat", [128, 64, 128], BF16)
    wfull = p.sb("wfull", [128, NCC, 257], BF16)
    fix3 = p.sb("fix3", [128, 6], F32)
    hbuf = [p.sb("hbuf0", [128, 1024], F32)] * 2
    sq = p.sb("sq", [128, 1024], BF16)
    ss = p.sb("ss", [128, 1], F32)
    sd = p.sb("sd", [128, 1], F32)
    rs = p.sb("rs", [128, 1], F32)
    hn = p.sb("hn", [128, 1024], BF16)
    hnT = p.sb("hnT", [128, 8, 512], BF16)
    cos_sb = p.sb("cos_sb", [128, 512], F32)
    sin_sb = p.sb("sin_sb", [128, 512], F32)
    t1 = p.sb("t1", [128, 512], F32)
    t2 = p.sb("t2", [128, 512], F32)
    Qblk = p.sb("Qblk", [128, 4, 512], BF16)
    KsT2 = p.sb("KsT2", [128, S], BF16)
    VsA = p.sb("VsA", [128, NQB, 65], BF16)
    KwT2 = p.sb("KwT2", [128, 1024], BF16)
    VwA = p.sb("VwA", [128, 8, 65], BF16)
    KcT2 = p.sb("KcT2", [128, NCC * 128], BF16)
    VcA = p.sb("VcA", [128, NCC, 65], BF16)
    xT = [p.sb(f"xT{i}", [64, 528], BF16) for i in range(2)]
    hidK = p.sb("hidK", [128, 2, 32], BF16)
    hidV = p.sb("hidV", [128, 2, 128], BF16)
    gx = [p.sb(f"gx{i}", [128, 32], F32) for i in range(3)]
    gsb = p.sb("gsb", [12, 512], F32)
    G64b = [p.sb(f"G64b{i}", [128, 12 * 128], F32) for i in range(2)]
    PT = [p.sb(f"PT{i}", [128, 512], BF16) for i in range(4)]
    EX = [p.sb(f"EX{i}", [128, 512], BF16) for i in range(4)]
    Msb = [p.sb(f"Msb{i}", [128, 128], BF16) for i in range(2)]
    nM = [0]
    PcT = p.sb("PcT", [128, NCC, 512], BF16)
    zr = p.sb("zr", [128, 512], F32)
    Rr = p.sb("Rr", [128, 512], F32)
    ones = p.sb("ones", [128, 64], F32)
    osb = p.sb("osb", [64, 512], F32)
    acc = p.sb("acc", [64, 512], F32)
    tmpo = p.sb("tmpo", [64, 512], F32)
    oacc = p.sb("oacc", [64, 4, 128], BF16)
    imp = p.sb("imp", [128, 256], F32)
    selbuf = p.sb("selbuf", [128, 256], F32)
    work = p.sb("work", [128, 256], F32)
    mx8 = p.sb("mx8", [128, 8], F32)
    thr = p.sb("thr", [128, 1], F32)
    zq = p.sb("zq", [128, 1], F32)
    Bq = p.sb("Bq", [128, 256], BF16)
    BT = p.sb("BT", [128, 2, 512], BF16)
    psum = p.ps("psum", [128, 7 * 512])
    psT = p.ps("psT", [128, 8, 128], BF16)
    zero_b = p.sb("zero_b", [128, 512], BF16)
    p.memset(zero_b[:], 0.0, w=["zero_b"])
    p.memset(Qblk[:], 0.0, w=["QT2"])

    def bank(i, n=1):
        return psum[:, i * 512:(i + n) * 512]

    def bk(i):
        return ("bank", i)

    p.dma(ident[:], ident_d, w=["ident"])
    p.dma(gT[:], g_attn, w=["gT"])
    p.dma(ccos[:], ccos_d, w=["ccos"])
    p.dma(csin[:], csin_d, w=["csin"])
    for a_ in range(2):
        for r4 in range(0, 16, 4):
            p.dma(pmask[:, a_, r4:r4 + 4, :], pmask_d[a_, r4:r4 + 4].rearrange("r p c -> p r c"),
                  w=["pmask"])
    p.dma(r0mask[:], r0mask_d, w=["r0mask"])
    p.dma(cmask[:], cmask_d.rearrange("a p c -> p a c"), w=["cmask"])
    for e8 in range(0, 64, 8):
        p.dma(emat[:, e8:e8 + 8, :], emat_d[e8:e8 + 8].rearrange("e p c -> p e c"), w=["emat"])
    p.dma(wfull[:], wfull_d.rearrange("(c p) f -> p c f", p=128), w=["wfull"])
    p.dma(fix3[:], fix_d, w=["fix3"])
    p.memset(ones[:], 1.0, w=["ones"])
    p.memset(VsA[:], 1.0, w=["VsA"])
    p.memset(VwA[:], 1.0, w=["VwA"])
    p.memset(VcA[:], 1.0, w=["VcA"])
    p.memset(KwT2[:], 0.0, w=["KwT2"])
    p.memset(KcT2[:], 0.0, w=["KcT2"])
    p.memset(selbuf[:], -FORCE, w=["selbuf"])
    p.memset(hidV[:], 0.0, w=["hidV"])
    for i in range(2):
        p.memset(xT[i][:], 0.0, w=[("xT", i)])
    nst_ = [0]

    def stage_load(dst_fn, src_ap, ncols, parts=128):
        i = nst_[0] % 2
        nst_[0] += 1
        k = ("fst", i)
        p.dma(fst[i][0:parts, 0:ncols], src_ap, w=[k])
        return fst[i], k

    def swapcopy(dst, src, r, w):
        d4 = dst.rearrange("p (h d) -> p h d", d=64)
        s4 = src.rearrange("p (h d) -> p h d", d=64)
        p.cp(d4[:, :, 0:32], s4[:, :, 32:64], r=r, w=w, eng="pool")
        p.cp(d4[:, :, 32:64], s4[:, :, 0:32], r=r, w=w, eng="pool")

    for kc in range(8):
        gs = gT[:, kc:kc + 1]
        rows = slice(kc * 128, (kc + 1) * 128)
        st_, k = stage_load(None, wq_d[rows, :], 256)
        p.ts(WQ[:, kc, 0, :], st_[:, 0:256], gs, None, ALU.mult, r=[k, "gT"], w=["WQ"], eng="pool")
        swapcopy(WQ[:, kc, 1, :], WQ[:, kc, 0, :], ["WQ"], ["WQ"])
        st_, k = stage_load(None, wk3_d[rows, :], 192)
        p.ts(WKC[:, kc, :], st_[:, 0:64], gs, None, ALU.mult, r=[k, "gT"], w=["WKC"], eng="pool")
        for (W_, c0) in ((WKS, 64), (WKW, 128)):
            for dup in range(2):
                p.ts(W_[:, kc, 0, dup * 64:(dup + 1) * 64], st_[:, c0:c0 + 64], gs, None, ALU.mult,
                     r=[k, "gT"], w=["WK"], eng="pool")
            swapcopy(W_[:, kc, 1, :], W_[:, kc, 0, :], ["WK"], ["WK"])
        st_, k = stage_load(None, wv3_d[rows, :], 192)
        p.ts(WVC[:, kc, :], st_[:, 0:64], gs, None, ALU.mult, r=[k, "gT"], w=["WVC"], eng="pool")
        p.ts(WV2[:, kc, :], st_[:, 64:192], gs, None, ALU.mult, r=[k, "gT"], w=["WV2"], eng="pool")
        st_, k = stage_load(None, wg_d[rows, :], 12)
        p.ts(WG[:, kc, :], st_[:, 0:12], gs, None, ALU.mult, r=[k, "gT"], w=["WG"], eng="pool")
    nW1 = [0]

    def w1_piece(kv, l0):
        i = nW1[0] % 2
        nW1[0] += 1
        k = ("fst", i)
        p.dma(fst[i][0:64, :].rearrange("p (l m) -> p l m", l=4),
              w1_d[kv, l0 * 64:(l0 + 4) * 64, :].rearrange("(l d) m -> d l m", d=64), w=[k])
        p.cp(W1c[i][:], fst[i][0:64, :].rearrange("p (l m) -> p l m", l=4),
             r=[k], w=[("W1c", i)], eng="pool")
        return W1c[i], ("W1c", i)

    for kv in range(2):
        for mt in range(2):
            st_, k = stage_load(None, w2_d[kv, mt * 128:(mt + 1) * 128, :], 64)
            if kv == 0:
                for dup in range(2):
                    p.cp(W2K[:, mt, 0, dup * 64:(dup + 1) * 64], st_[:, 0:64], r=[k], w=["W2K"],
                         eng="pool")
                swapcopy(W2K[:, mt, 1, :], W2K[:, mt, 0, :], ["W2K"], ["W2K"])
            else:
                p.cp(W2V[:, mt, :], st_[:, 0:64], r=[k], w=["W2V"], eng="pool")
    st_, k = stage_load(None, posT_d.rearrange("d a l -> d (a l)"), 64, parts=64)
    p.cp(posT[:].rearrange("d a l -> d (a l)"), st_[0:64, 0:64], r=[k], w=["posT"], eng="pool")
    for kv in range(2):
        for l0 in range(0, 32, 4):
            wt, wk_ = w1_piece(kv, l0)
            for mt in range(2):
                col = kv * 2 + mt
                bb = 1 if mt == 0 else 5
                for li in range(4):
                    l = l0 + li
                    p.mm(bank(bb)[:, col:col + 1], wt[:, li, mt * 128:(mt + 1) * 128],
                         posT[:, kv, l:l + 1], start=(l == 0), stop=(l == 31),
                         r=[wk_, "posT"], w=[bk(bb)])
    p.cp(c1[:, 0:1], bank(1)[:, 0:1], r=[bk(1)], w=["c1"], eng="act")
    p.cp(c1[:, 2:3], bank(1)[:, 2:3], r=[bk(1)], w=["c1"], eng="act")
    p.cp(c1[:, 1:2], bank(5)[:, 1:2], r=[bk(5)], w=["c1"], eng="act")
    p.cp(c1[:, 3:4], bank(5)[:, 3:4], r=[bk(5)], w=["c1"], eng="act")

    if dbg == 0:
        return p.finish()
    if fused:
        for cb in range(4):
            p.dma(io["o_zero"][:, cb, :], zero_b[0:64, 0:128], r=["zero_b"], w=[io["dkey"]],
                  q="pool")
    if fused and io.get("after_setup"):
        io["after_setup"]()
    scr = (sq[:], ss[:], sd[:], rs[:])

    def rope_out(dst, psn, pss, cs, sn, rk, wk, n):
        p.tt(t1[:, 0:n], psn, cs, ALU.mult, r=rk[0:1] + ["cos", "ccos"], w=["t1"])
        p.tt(t2[:, 0:n], pss, sn, ALU.mult, r=rk[1:2] + ["sin", "csin"], w=["t2"])
        if isinstance(dst, tuple):
            p.tt(dst[0], t1[0:64, 0:n], t2[0:64, 0:n], ALU.add, r=["t1", "t2"], w=wk)
            p.tt(dst[1], t1[64:128, 0:n], t2[64:128, 0:n], ALU.add, r=["t1", "t2"], w=wk)
        else:
            p.tt(dst, t1[:, 0:n], t2[:, 0:n], ALU.add, r=["t1", "t2"], w=wk)

    nU = [0]
    nH = [0]

    def proj_pair(W_, dst, cs, sn, wkey, rkey):
        par = nU[0] % 2
        nU[0] += 1
        b0, b1 = 2 + 2 * par, 3 + 2 * par
        for v, b in ((0, b0), (1, b1)):
            for kc in range(8):
                p.mm(bank(b), W_(kc, v), hnT[:, kc, :], start=(kc == 0), stop=(kc == 7),
                     r=[rkey, "hnT"], w=[bk(b)])
        rope_out(dst, bank(b0), bank(b1), cs, sn, [bk(b0), bk(b1)], wkey, 512)

    def gelu_to(dst, ps_ap, bias_ap, rk, wk):
        x, a, b = gx[0][:], gx[1][:], gx[2][:]
        p.act(x, ps_ap, AF.Identity, r=rk + ["c1"], w=["gx0"], bias=bias_ap)
        p.tt(a, x, x, ALU.mult, r=["gx0"], w=["gx1"])
        p.ts(a, a, 0.044715, 1.0, ALU.mult, ALU.add, r=["gx1"], w=["gx1"])
        p.tt(a, a, x, ALU.mult, r=["gx1", "gx0"], w=["gx1"])
        p.act(b, a, AF.Sigmoid, r=["gx1"], w=["gx2"], scale=1.5957691216057308)
        p.tt(dst, x, b, ALU.mult, r=["gx0", "gx2"], w=wk)

    nS = [0]
    sdepth = [2]
    ZB = [False]

    def attn_chunk(kT2, kcols, vaug, biases, first, last, n_extra_r):
        sp_ = nS[0] % sdepth[0]
        nS[0] += 1
        sb_ = 2 + sp_
        if ZERO_BIAS and len(biases) == 0:
            if ZB[0]:
                biases = [(ident[:], zero_b[:], ["ident", "zero_b"])]
            else:
                biases = [(ident[:], zero_b[:, 0:32], ["ident", "zero_b"], "small")]
        out = bank(sb_)
        p.mm(out, kT2[:, kcols], Qblk[:, :, qsl[0]], start=True, stop=(len(biases) == 0),
             r=n_extra_r + ["QT2"], w=[bk(sb_)])
        for bi, bias in enumerate(biases):
            lh, rh, rk = bias[0:3]
            if len(bias) == 4 and bias[3] == "small":
                p.mm(out[:, 0:32], lh, rh, start=False, stop=(bi == len(biases) - 1), r=rk,
                     w=[bk(sb_)])
            elif len(bias) == 4:
                for hh in range(4):
                    p.mm(out[:, hh * 128:(hh + 1) * 128], lh, rh, start=False,
                         stop=(bi == len(biases) - 1), r=rk, w=[bk(sb_)])
            else:
                p.mm(out, lh, rh, start=False, stop=(bi == len(biases) - 1), r=rk, w=[bk(sb_)])
        return sp_, sb_

    qsl = [None]
    for st in range(NST):
        tok0 = st * 512
        for j in range(4):
            hb = hbuf[j % 2]
            hk = ("hbuf", 0)
            if fused:
                r0 = io["h_row"](tok0 + j * 128)
                p.dma(hb[:], io["h_ap"][r0:r0 + 128, :], r=list(io["rkeys"]), w=[hk])
            else:
                p.dma(hb[:], h_in[tok0 + j * 128:tok0 + (j + 1) * 128, :], w=[hk])
            rmsnorm_tile(p, hb[:], None, hn[:], scr, [hk], ["hn"], "a")
            for kc in range(8):
                p.tr(psT[:, kc, :], hn[:, kc * 128:(kc + 1) * 128], ident[:],
                     r=["hn", "ident"], w=["psT"])
            p.cp(hnT[:, :, j * 128:(j + 1) * 128], psT[:], r=["psT"], w=["hnT"], eng="act")
        p.dma(cos_sb[:], cos_d[:, tok0:tok0 + 512], w=["cos"])
        p.dma(sin_sb[:], sin_d[:, tok0:tok0 + 512], w=["sin"])
        for hp in range(2):
            proj_pair(lambda kc, v, hp=hp: WQ[:, kc, v, hp * 128:(hp + 1) * 128],
                      (Qblk[0:64, hp, :], Qblk[64:128, 2 + hp, :]), cos_sb[:], sin_sb[:],
                      ["QT2"], "WQ")
        proj_pair(lambda kc, v: WKS[:, kc, v, :], KsT2[:, tok0:tok0 + 512], cos_sb[:], sin_sb[:],
                  ["KsT2"], "WK")
        proj_pair(lambda kc, v: WKW[:, kc, v, :], KwT2[:, 512:1024], cos_sb[:], sin_sb[:],
                  ["KwT2"], "WK")
        for i, W_ in enumerate((WKC, WVC)):
            for kc in range(8):
                p.mm(bank(1)[0:64, :], W_[:, kc, :], hnT[:, kc, :], start=(kc == 0), stop=(kc == 7),
                     r=["WKC", "WVC", "hnT"], w=[bk(1)])
            p.cp(xT[i][:, 16:528], bank(1)[0:64, :], r=[bk(1)], w=[("xT", i)], eng="act")
        for j in range(4):
            for kc in range(8):
                p.mm(bank(0)[:, 0:128], hnT[:, kc, j * 128:(j + 1) * 128], WV2[:, kc, :],
                     start=(kc == 0), stop=(kc == 7), r=["hnT", "WV2"], w=[bk(0)])
            p.cp(VsA[:, st * 4 + j, 0:64], bank(0)[:, 0:64], r=[bk(0), "VsA"], w=["VsA"], eng="act")
            p.cp(VwA[:, 4 + j, 0:64], bank(0)[:, 64:128], r=[bk(0), "VwA"], w=["VwA"], eng="act")
        for kc in range(8):
            p.mm(bank(1)[0:12, :], WG[:, kc, :], hnT[:, kc, :], start=(kc == 0), stop=(kc == 7),
                 r=["WG", "hnT"], w=[bk(1)])
        p.act(gsb[:], bank(1)[0:12, :], AF.Sigmoid, r=[bk(1)], w=["gsb"])
        p.dma(gscr[st % 2].rearrange("o (a b) -> (o a) b", a=12), gsb[:], r=["gsb"],
              w=[("gscr", st % 2)])
        if dbg == 1:
            return p.finish()
        for kv in range(2):
            x3 = xT[kv][:].rearrange("p (i s) -> p i s", s=16)
            for l0 in range(0, 32, 4):
                wt, wk_ = w1_piece(kv, l0)
                for mt in range(2):
                    bb = 1 if mt == 0 else 5
                    for li in range(4):
                        l = l0 + li
                        rhs = x3[:, 0:32, l] if l < 16 else x3[:, 1:33, l - 16]
                        p.mm(bank(bb)[:, 0:32], wt[:, li, mt * 128:(mt + 1) * 128], rhs,
                             start=(l == 0), stop=(l == 31), r=[wk_, ("xT", kv)], w=[bk(bb)])
            for mt in range(2):
                bb = 1 if mt == 0 else 5
                if kv == 0:
                    gelu_to(hidK[:, mt, :], bank(bb)[:, 0:32], c1[:, mt:mt + 1], [bk(bb)], ["hidK"])
                else:
                    if st % 4 == 0 and mt == 0:
                        p.memset(hidV[:], 0.0, w=["hidV"])
                    gelu_to(hidV[:, mt, (st % 4) * 32:(st % 4) * 32 + 32], bank(bb)[:, 0:32],
                            c1[:, 2 + mt:3 + mt], [bk(bb)], ["hidV"])
            if kv == 0:
                par = nU[0] % 2
                nU[0] += 1
                b0, b1 = 2 + 2 * par, 3 + 2 * par
                for v, b in ((0, b0), (1, b1)):
                    for mt in range(2):
                        p.mm(bank(b)[:, 0:32], W2K[:, mt, v, :], hidK[:, mt, :],
                             start=(mt == 0), stop=(mt == 1), r=["W2K", "hidK"], w=[bk(b)])
                sl = slice(st * 32, st * 32 + 32)
                rope_out(KcT2[:, sl], bank(b0)[:, 0:32], bank(b1)[:, 0:32], ccos[:, sl], csin[:, sl],
                         [bk(b0), bk(b1)], ["KcT2"], 32)
            else:
                for mt in range(2):
                    p.mm(bank(1)[:, 0:64], hidV[:, mt, :], W2V[:, mt, :],
                         start=(mt == 0), stop=(mt == 1), r=["W2V", "hidV"], w=[bk(1)])
                p.cp(VcA[:, st // 4, 0:64], bank(1)[:, 0:64], r=[bk(1), "VcA"], w=["VcA"], eng="act")
            p.cp(xT[kv][:, 0:16], xT[kv][:, 512:528], r=[("xT", kv)], w=[("xT", kv)], eng="pool")
        if dbg == 2:
            return p.finish()
        for j in range(4):
            qb = st * 4 + j
            qsl[0] = slice(j * 128, (j + 1) * 128)
            tsl = qsl[0]
            p.dma(G64b[j % 2][64:65, :].rearrange("p (a b) -> p a b", a=12),
                  gscr[st % 2].rearrange("o (a b) -> o a b", a=12)[:, :, tsl],
                  r=[("gscr", st % 2)], w=[("G64", j % 2)])

            def finish_branch(br, first):
                p.ts(zr[64:65, :], bank(0)[64:65, :], TINY, None, ALU.max, r=[bk(0)], w=["zr"])
                p.recip(zr[64:65, :], zr[64:65, :], r=["zr"], w=["zr"])
                g3 = G64b[j % 2][64:65, :].rearrange("p (h b t) -> p h b t", h=4, b=3)
                for par in range(2):
                    for hpl in range(2):
                        hl = 2 * hpl + par
                        c0 = (par * 2 + hpl) * 128
                        p.tt(Rr[64:65, c0:c0 + 128], zr[64:65, c0:c0 + 128], g3[:, hl, br, :],
                             ALU.mult, r=["zr", ("G64", j % 2)], w=["Rr"])
                p.cp(osb[:], bank(0)[0:64, :], r=[bk(0)], w=["osb"], eng="act")

                def part2(first=first):
                    p.mm(bank(1)[0:64, :], ones[64:65, :], Rr[64:65, :], start=True, stop=True,
                         r=["ones", "Rr"], w=[bk(1)])
                    if first:
                        p.tt(acc[:], osb[:], bank(1)[0:64, :], ALU.mult, r=["osb", bk(1)], w=["acc"])
                    else:
                        p.tt(tmpo[:], osb[:], bank(1)[0:64, :], ALU.mult, r=["osb", bk(1)],
                             w=["tmpo"])
                        p.tt(acc[:], acc[:], tmpo[:], ALU.add, r=["tmpo", "acc"], w=["acc"])
                return part2

            pend = []
            pdepth = [1]

            def pend_push(fn):
                pend.append(fn)
                while len(pend) > pdepth[0]:
                    pend.pop(0)()

            def pend_flush():
                while pend:
                    pend.pop(0)()

            ncc = qb // 16 + 1
            r_ = qb % 16
            for cc in range(ncc):
                biases = []
                lastc = (cc == ncc - 1)
                if lastc:
                    biases.append((ident[:], pmask[:, 1 if cc == 0 else 0, r_, :], ["ident", "pmask"], 128))
                elif cc == 0:
                    biases.append((ident[:], r0mask[:], ["ident", "r0mask"]))
                sp_, sb_ = attn_chunk(KcT2, slice(cc * 128, (cc + 1) * 128), None, biases,
                                      cc == 0, lastc, ["KcT2"])
                p.act(PcT[:, cc, :], bank(sb_), AF.Exp, r=[bk(sb_)], w=[("PcT", cc)], scale=0.125)
                pend_push(lambda cc=cc, lastc=lastc: p.mm(
                    bank(0)[0:65, :], VcA[:, cc, :], PcT[:, cc, :], start=(cc == 0), stop=lastc,
                    r=[("PcT", cc), "VcA"], w=[bk(0)]))
            pend_flush()
            for par in range(2):
                for hpl in range(2):
                    hi = par * 2 + hpl
                    c0 = hi * 128
                    ib = 4 + (hi % 2)
                    for cc in range(ncc):
                        p.mm(bank(ib)[:, 0:257], PcT[:, cc, c0:c0 + 128], wfull[:, cc, :],
                             start=(cc == 0), stop=(cc == ncc - 1),
                             r=[("PcT", cc), "wfull"], w=[bk(ib)])
                    p.ts(zq[:], bank(ib)[:, 256:257], TINY, None, ALU.max, r=[bk(ib)], w=["zq"])
                    p.recip(zq[:], zq[:], r=["zq"], w=["zq"])
                    if hi == 0:
                        p.ts(imp[:], bank(ib)[:, 0:256], zq[:], None, ALU.mult, r=[bk(ib), "zq"],
                             w=["imp"])
                    else:
                        p.stt(imp[:], bank(ib)[:, 0:256], zq[:], imp[:], ALU.mult, ALU.add,
                              r=[bk(ib), "zq", "imp"], w=["imp"])
            fin0 = finish_branch(0, True)
            if dbg == 3 or dbg == 100 + j * 10 + 3:
                return p.finish()
            nb = 2 * qb + 2
            p.cp(selbuf[:, 0:nb], imp[:, 0:nb], r=["imp"], w=["selbuf"])
            lo = 2 * qb - 1
            k0 = 0
            if lo < 0:
                lo, k0 = 0, 1
            nfx = 3 - k0
            p.tt(selbuf[:, lo:lo + nfx], selbuf[:, lo:lo + nfx], fix3[:, k0:3], ALU.mult,
                 r=["selbuf", "fix3"], w=["selbuf"])
            p.tt(selbuf[:, lo:lo + nfx], selbuf[:, lo:lo + nfx], fix3[:, 3 + k0:6], ALU.add,
                 r=["selbuf", "fix3"], w=["selbuf"])
            p.memset(selbuf[:, 0:1], 3.0 * FORCE, w=["selbuf"])
            p.s.op("dve", lambda e: e.max(out=mx8[:], in_=selbuf[:]), ["selbuf"], ["mx8"])
            p.s.op("dve", lambda e: e.match_replace(out=work[:], in_to_replace=mx8[:],
                                                    in_values=selbuf[:], imm_value=-2.0 * FORCE),
                   ["selbuf", "mx8"], ["work"])
            p.s.op("dve", lambda e: e.max(out=mx8[:], in_=work[:]), ["work"], ["mx8"])
            p.s.op("dve", lambda e: e.tensor_reduce(out=thr[:], in_=mx8[:], axis=AX.X, op=ALU.min),
                   ["mx8"], ["thr"])
            p.ts(Bq[:], selbuf[:], thr[:], 1.0, ALU.is_ge, ALU.subtract, r=["selbuf", "thr"], w=["Bq"])
            nhalf = 1 if nb <= 128 else 2
            for hf in range(nhalf):
                p.tr(psT[:, hf, :], Bq[:, hf * 128:(hf + 1) * 128], ident[:], r=["Bq", "ident"],
                     w=["psT"])
            for hf in range(nhalf):
                for rep in range(4):
                    p.cp(BT[:, hf, rep * 128:(rep + 1) * 128], psT[:, hf, :], r=["psT"], w=["BT"],
                         eng=("act" if rep % 2 == 0 else "dve"))
            if dbg == 5 or dbg == 100 + j * 10 + 5:
                return p.finish()
            k_lo = max(0, qb - 4)
            sdepth[0] = 4
            pdepth[0] = 2
            for kc in range(k_lo, qb + 1):
                biases = []
                if kc == qb - 4:
                    biases.append((ident[:], cmask[:, 1, :], ["ident", "cmask"]))
                if kc == qb:
                    biases.append((ident[:], cmask[:, 0, :], ["ident", "cmask"]))
                slot = 4 + j - (qb - kc)
                sp_, sb_ = attn_chunk(KwT2, slice(slot * 128, (slot + 1) * 128), None, biases,
                                      kc == k_lo, kc == qb, ["KwT2"])
                p.act(PT[sp_][:], bank(sb_), AF.Exp, r=[bk(sb_)], w=[("PT", sp_)], scale=0.125)
                if kc == min(k_lo + 1, qb) and fin0 is not None:
                    fin0()
                    fin0 = None
                pend_push(lambda kc=kc, sp_=sp_, slot=slot: p.mm(
                    bank(0)[0:65, :], VwA[:, slot, :], PT[sp_][:], start=(kc == k_lo),
                    stop=(kc == qb), r=[("PT", sp_), "VwA"], w=[bk(0)]))
            pend_flush()
            fin2 = finish_branch(2, False)
            if fin0 is not None:
                fin0()
                fin0 = None
            if dbg == 4 or dbg == 100 + j * 10 + 4:
                return p.finish()
            sdepth[0] = 3 if MUL_MASK else 4
            pdepth[0] = 2
            for kc in range(qb + 1):
                mulmask = MUL_MASK and kc != qb
                if mulmask:
                    zb = ZB[0]
                    ZB[0] = KEEP_ZB
                    sp_, sb_ = attn_chunk(KsT2, slice(kc * 128, (kc + 1) * 128), None, [],
                                          kc == 0, kc == qb, ["KsT2"])
                    ZB[0] = zb
                    mpar = nM[0] % 2
                    nM[0] += 1
                    mb = 5 + mpar
                    p.mm(bank(mb)[:, 0:128], emat[:, kc % 64, :], BT[:, kc // 64, 0:128],
                         start=True, stop=True, r=["emat", "BT"], w=[bk(mb)])
                    p.act(EX[sp_][:], bank(sb_), AF.Exp, r=[bk(sb_)], w=[("EX", sp_)], scale=0.125,
                          bias=-LN_C)
                    p.act(Msb[mpar][:], bank(mb)[:, 0:128], AF.Identity, r=[bk(mb)],
                          w=[("Msb", mpar)], bias=C_BF)
                    p.tt(PT[sp_][:].rearrange("p (h q) -> p h q", h=4),
                         EX[sp_][:].rearrange("p (h q) -> p h q", h=4),
                         Msb[mpar][:].unsqueeze(1).to_broadcast([128, 4, 128]), ALU.mult,
                         r=[("Msb", mpar), ("EX", sp_)], w=[("PT", sp_)])
                else:
                    biases = [(emat[:, kc % 64, :], BT[:, kc // 64, :], ["emat", "BT"])]
                    if kc == qb:
                        biases.append((ident[:], cmask[:, 0, :], ["ident", "cmask"]))
                    sp_, sb_ = attn_chunk(KsT2, slice(kc * 128, (kc + 1) * 128), None, biases,
                                          kc == 0, kc == qb, ["KsT2"])
                    p.act(PT[sp_][:], bank(sb_), AF.Exp, r=[bk(sb_)], w=[("PT", sp_)], scale=0.125)
                if kc == min(1, qb) and fin2 is not None:
                    fin2()
                    fin2 = None
                pend_push(lambda kc=kc, sp_=sp_: p.mm(
                    bank(0)[0:65, :], VsA[:, kc, :], PT[sp_][:], start=(kc == 0), stop=(kc == qb),
                    r=[("PT", sp_), "VsA"], w=[bk(0)]))
            pend_flush()
            fin1 = finish_branch(1, False)
            fin1()
            sdepth[0] = 2
            if dbg == 6 or dbg == 100 + j * 10 + 6:
                return p.finish()
            p.cp(oacc[:], acc[:].rearrange("p (c q) -> p c q", c=4), r=["acc"], w=["oacc"])
            if fused:
                for dst in io["o_dst"](qb):
                    p.dma(dst, oacc[:], r=["oacc"], w=[io["dkey"]], q="pool")
            else:
                p.dma(oT_out[:, :, tok0 + j * 128:tok0 + (j + 1) * 128], oacc[:], r=["oacc"],
                      q="pool", is_output=True)
            if dbg == 100 + j * 10 + 7:
                return p.finish()
        p.cp(KwT2[:, 0:512], KwT2[:, 512:1024], r=["KwT2"], w=["KwT2"], eng="pool")
        p.cp(VwA[:, 0:4, :], VwA[:, 4:8, :], r=["VwA"], w=["VwA"], eng="pool")
        if dbg == 7 + st:
            return p.finish()
    if fused:
        return None
    return p.finish()


def nsa_consts(S):
    NCC = max(1, S // 2048)
    ml = np.arange(128)[:, None]
    q = np.arange(128)[None, :]
    pm = np.zeros((2, 16, 128, 128), np.float32)
    for a in range(2):
        for r in range(16):
            valid = (16 * ml + 15 <= 128 * r + q)
            if a == 1:
                valid = valid & (ml >= 1)
            pm[a, r] = np.where(valid, 0.0, MASKV)
    pmask = pm.astype(NPBF16)
    r0 = np.zeros((128, 512), np.float32)
    r0[0, :] = MASKV
    cur = np.where(ml <= q, 0.0, MASKV).astype(np.float32)
    upper = np.where(ml > q, 0.0, MASKV).astype(np.float32)
    cmask = np.stack([np.tile(cur, (1, 4)), np.tile(upper, (1, 4))], 0).astype(NPBF16)
    emat = np.zeros((64, 128, 128), np.float32)
    for e in range(64):
        emat[e, 2 * e, 0:64] = -MASKV
        emat[e, 2 * e + 1, 64:128] = -MASKV
    ws = [1, 2, 2, 2, 1]
    wfull = np.zeros((NCC * 128, 257), np.float32)
    for m in range(1, NCC * 128):
        n = m - 1
        for j in range(256):
            i = n - 4 * j + 1
            if 0 <= i <= 4:
                wfull[m, j] = ws[i]
    wfull[:, 256] = 1.0
    fix = np.zeros((128, 6), np.float32)
    lo = np.arange(128) < 64
    fix[:, 0] = np.where(lo, 0.0, 1.0)
    fix[:, 3] = np.where(lo, FORCE, 0.0)
    fix[:, 4] = 2.0 * FORCE
    fix[:, 5] = np.where(lo, -FORCE, FORCE)
    cpos = 16 * np.arange(NCC * 128) + 15
    ccos, csin = rope_tables(cpos)
    return dict(pmask=pmask, r0mask=r0.astype(NPBF16), cmask=cmask, emat=emat.astype(NPBF16),
                wfull=wfull.astype(NPBF16), fix3=fix, ccos_t=ccos, csin_t=csin, ident=ident_np())


def nsa_weights(a_w_in_l, cmp_pos_l, g):
    W = a_w_in_l
    q0 = g * 256
    def kcol(i):
        return W[:, 1024 + i * 256 + g * 64: 1024 + i * 256 + (g + 1) * 64]
    kc_, vc_, ks_, vs_, kw_, vw_ = [kcol(i) for i in range(6)]
    wg = W[:, 1024 + 6 * 256 + g * 12: 1024 + 6 * 256 + (g + 1) * 12]
    return dict(wq=np.ascontiguousarray(W[:, q0:q0 + 256]),
                wk3=np.ascontiguousarray(np.concatenate([kc_, ks_, kw_], 1)),
                wv3=np.ascontiguousarray(np.concatenate([vc_, vs_, vw_], 1)),
                wg=np.ascontiguousarray(wg),
                posT=np.ascontiguousarray(np.transpose(cmp_pos_l, (2, 0, 1))))


SEQ = 16384
NB = 2
CH = 4096
NPHASE = 99


def _run(nc, in_maps):
    res = run_bass_kernel_spmd(nc, in_maps, core_ids=list(range(8)))
    return res.results


def _cwb(conv_w, conv_b):
    return np.ascontiguousarray(np.concatenate([conv_w, conv_b[None]], 0).T.astype(np.float32))


def _chunk_with_halo(x_b, c, halo):
    lo = c * CH - halo
    if lo >= 0:
        return np.ascontiguousarray(x_b[lo:(c + 1) * CH])
    pad = np.zeros((-lo,) + x_b.shape[1:], x_b.dtype)
    return np.ascontiguousarray(np.concatenate([pad, x_b[0:(c + 1) * CH]], 0))


def kernel_unfused(x, norm_attn, norm_ffn, a_w_in, a_cmp_pos, a_cmp_w1, a_cmp_w2, a_w_out, kv_norm,
           b_w_kv, b_w_q, b_sinks, b_w_out, ffn_w_in, ffn_conv_w, ffn_conv_b, ffn_w_out,
           final_norm):
    f32 = lambda a: np.ascontiguousarray(np.asarray(a, dtype=np.float32))
    x = f32(x)
    norm_attn, norm_ffn = f32(norm_attn), f32(norm_ffn)
    a_w_in, a_cmp_pos, a_cmp_w1, a_cmp_w2, a_w_out = map(f32, (a_w_in, a_cmp_pos, a_cmp_w1,
                                                                a_cmp_w2, a_w_out))
    kv_norm, b_w_kv, b_w_q, b_sinks, b_w_out = map(f32, (kv_norm, b_w_kv, b_w_q, b_sinks, b_w_out))
    ffn_w_in, ffn_conv_w, ffn_conv_b, ffn_w_out, final_norm = map(
        f32, (ffn_w_in, ffn_conv_w, ffn_conv_b, ffn_w_out, final_norm))
    h = x
    ident = ident_np()
    gfin = np.ascontiguousarray(final_norm[None, :])
    cosA, sinA = rope_tables(np.arange(SEQ))
    constsA = nsa_consts(SEQ)

    for l in range(2):
        ncA = build_A(SEQ)
        maps = []
        for i in range(8):
            b, g = divmod(i, 4)
            m = dict(h_in=h[b], g_attn=gT_np(norm_attn[l]), w1=a_cmp_w1[l], w2=a_cmp_w2[l],
                     cos_t=cosA, sin_t=sinA)
            m.update(constsA)
            m.update(nsa_weights(a_w_in[l], a_cmp_pos[l], g))
            maps.append(m)
        resA = _run(ncA, maps)
        oT_full = np.zeros((NB, 16, 64, SEQ), NPBF16)
        for i in range(8):
            b, g = divmod(i, 4)
            o = resA[i]["oT_out"]
            for par in range(2):
                for hpl in range(2):
                    oT_full[b, g * 4 + 2 * hpl + par] = o[:, par * 2 + hpl, :]
        oT_full = oT_full.reshape(NB, 1024, SEQ)
        ncB = build_B([1] + [4] * 8, 1)
        maps = []
        for i in range(8):
            b, c = divmod(i, 4)
            maps.append(dict(
                h_in=_chunk_with_halo(h[b], c, 128),
                oT_in=np.ascontiguousarray(_chunk_with_halo(oT_full[b].T, c, 128).T),
                w_o=a_w_out[l], w_in=ffn_w_in[l], w_out=ffn_w_out[l],
                cwb=_cwb(ffn_conv_w[l], ffn_conv_b[l]), g_ffn=gT_np(norm_ffn[l]), g_fin=gfin,
                ident=ident))
        resB = _run(ncB, maps)
        h = np.stack([np.concatenate([resB[b * 4 + c]["h_out"] for c in range(4)], 0)
                      for b in range(NB)], 0)

    hkv = h
    for l in range(2, 4):
        j = l - 2
        ncC = build_C([2] + [4] * 8, 2, final_norm=(l == 3))
        maps = []
        for i in range(8):
            b, c = divmod(i, 4)
            pos = c * CH - 256 + np.arange(CH + 256)
            cos_t, sin_t = rope_tables(pos)
            maps.append(dict(
                h_in=_chunk_with_halo(h[b], c, 256), hkv_in=_chunk_with_halo(hkv[b], c, 256),
                w_q=b_w_q[j], w_kv=b_w_kv, sinks_b=sinks_row(b_sinks[j]),
                g_attn=gT_np(norm_attn[l]), g_kv=gT_np(kv_norm), cos_t=cos_t, sin_t=sin_t,
                masks=swa_masks(c > 0), w_o=b_w_out[j], w_in=ffn_w_in[l], w_out=ffn_w_out[l],
                cwb=_cwb(ffn_conv_w[l], ffn_conv_b[l]), g_ffn=gT_np(norm_ffn[l]), g_fin=gfin,
                ident=ident))
        resC = _run(ncC, maps)
        h = np.stack([np.concatenate([resC[b * 4 + c]["h_out"] for c in range(4)], 0)
                      for b in range(NB)], 0)
    return np.ascontiguousarray(h.astype(np.float32))


def build_fused(nphase=99):
    from concourse.bass import ds
    nph = [0]

    def stop():
        nph[0] += 1
        return nph[0] >= nphase

    nc = bass.Bass("TRN2", target_bir_lowering=False)
    p = Prog(nc)
    S = SEQ
    WB = 128 + CH
    WC = 256 + CH
    SUBW = 11 * 128
    xA = p.din("xA", [S, D], F32)
    xB = p.din("xB", [WB, D], F32)
    flag = p.din("flag", [128, 1], F32)
    out = p.dout("out", [CH, D], F32)
    oTloc = [nc.dram_tensor(f"oTloc{l}", [12 * 64, 4 * SUBW], BF16) for l in range(2)]
    OTb = [nc.dram_tensor(f"OTb{l}", [12 * 256, 4 * SUBW], BF16) for l in range(2)]
    oTwin = nc.dram_tensor("oTwin", [3 * 256, 4 * SUBW], BF16).ap()
    hloc = [nc.dram_tensor(f"hloc{k}", [CH, D], F32) for k in range(3)]
    Hb = [nc.dram_tensor(f"Hb{k}", [S, D], F32) for k in range(3)]
    hwin = nc.dram_tensor("hwin", [WC, D], F32).ap()
    hkvwin = nc.dram_tensor("hkvwin", [WC, D], F32).ap()
    rg = [[0, 1, 2, 3], [4, 5, 6, 7]]
    PID = p.s.pid
    Hh = [nc.dram_tensor(f"Hh{k}", [4 * 256, D], F32) for k in (1, 2)]
    halowin = [nc.dram_tensor(f"halowin{k}", [256, D], F32).ap() for k in (1, 2)]

    def gather_group(src, dst, nchunk, rows, rk, wk):
        def fn(e, sem):
            for k in range(nchunk):
                e.collective_compute(
                    "AllGather", ALU.bypass, replica_groups=rg,
                    ins=[src.ap()[k * rows:(k + 1) * rows, :].opt()],
                    outs=[dst.ap()[k * 4 * rows:(k + 1) * 4 * rows, :].opt()]).then_inc(sem)
        p.s.cc(fn, [rk], [wk], n=nchunk)

    def h_row(tok):
        rank, rem = divmod(tok, CH)
        k, r = divmod(rem, 256)
        return (k * 4 + rank) * 256 + r

    def win_copy(dst, src, halo, q, rk, wk):
        s5 = src.rearrange("(k g r e) d -> k g r (e d)", k=16, g=4, e=8)
        dm = dst[halo:halo + CH, :].rearrange("(k g r e) d -> k g r (e d)", k=16, g=1, e=8)
        dh = dst[0:halo, :].rearrange("(k g r e) d -> k g r (e d)", k=1, g=1, e=8)
        h8 = halo // 8
        p.dmaf(lambda e: e.dma_start(
            out=dm, in_=s5[:, ds(PID(e, "c", lambda pid: pid % 4), 1), :, :]),
            r=[rk], w=[wk], q=q)
        p.dmaf(lambda e: e.dma_start(
            out=dh, in_=s5[15:16, ds(PID(e, "cm1", lambda pid: (pid + 3) % 4), 1), 32 - h8:32, :]),
            r=[rk], w=[wk], q=q)

    for l in range(2):
        p.sfx = f"_A{l}"
        rk = [] if l == 0 else [f"Hb{l - 1}"]
        O5 = oTloc[l].ap().rearrange("(c s d) (b t) -> c s d b t", c=4, s=3, b=4)

        def o_dst(qb, O5=O5):
            c, sl = divmod(qb, 32)
            sl += 1
            dsts = [O5[c, sl // 11, :, :, (sl % 11) * 128:(sl % 11) * 128 + 128]]
            if sl == 32 and c < 3:
                dsts.append(O5[c + 1, 0, :, :, 0:128])
            return dsts

        build_A(S, p=p, io=dict(h_ap=(xA if l == 0 else Hb[l - 1].ap()),
                                h_row=((lambda t: t) if l == 0 else h_row),
                                rkeys=rk, dkey=f"oTloc{l}", o_dst=o_dst,
                                o_zero=O5[0, 0, :, :, 0:128], after_setup=p.s.cc_wait))
        p.phase_end()
        gather_group(oTloc[l], OTb[l], 12, 64, f"oTloc{l}", f"OTb{l}")
        if stop():
            return p.finish(), dict(p.dins)
        p.sfx = f"_B{l}"
        O3 = OTb[l].ap().rearrange("(c r) f -> c r f", c=4)

        def after_b(l=l, O3=O3):
            p.s.cc_wait()
            p.dmaf(lambda e: e.dma_start(
                out=oTwin.rearrange("(c r) f -> c r f", c=1),
                in_=O3[ds(PID(e, "c", lambda pid: pid % 4), 1), :, :]),
                r=[f"OTb{l}"], w=["oTwin"], q="act")
            if l > 0:
                win_copy(hwin[0:WB, :], Hb[l - 1].ap(), 128, "act", f"Hb{l - 1}", "hwin")

        if l == 0:
            h_ap = xB
            rkb = ["oTwin"]
        else:
            h_ap = hwin[0:WB, :]
            rkb = ["oTwin", "hwin"]
        W5 = oTwin.rearrange("(s g d) (b t) -> d s g b t", s=3, g=4, b=4)

        def oT_ap(wt, g, W5=W5):
            return W5[:, wt // 11, g, :, (wt % 11) * 128:(wt % 11) * 128 + 128]

        build_B([1] + [4] * 8, 1, p=p,
                io=dict(h_ap=h_ap, oT_ap=oT_ap, h_dst=hloc[l].ap(), flag=flag,
                        rkeys=rkb, dkey=f"hloc{l}", after_setup=after_b))
        p.phase_end()
        if l == 0:
            gather_group(hloc[l], Hb[l], 16, 256, f"hloc{l}", f"Hb{l}")
        else:
            p.s.cc(lambda e, sem: e.collective_compute(
                "AllGather", ALU.bypass, replica_groups=rg,
                ins=[hloc[1].ap()[CH - 256:CH, :].opt()], outs=[Hh[0].ap().opt()]).then_inc(sem),
                ["hloc1"], ["Hh0"], n=1)
        if stop():
            return p.finish(), dict(p.dins)


    def halo_copy(k):
        src = Hh[k].ap().rearrange("(g r) d -> g r d", g=4)
        p.dmaf(lambda e: e.dma_start(
            out=halowin[k].rearrange("(g r) d -> g r d", g=1),
            in_=src[ds(PID(e, "cm1", lambda pid: (pid + 3) % 4), 1), :, :]),
            r=[f"Hh{k}"], w=[f"halowin{k}"], q="sp")

    def tile_src(halo_ap, main_ap):
        def f(t):
            if t < 2:
                return halo_ap[t * 128:(t + 1) * 128, :]
            return main_ap[(t - 2) * 128:(t - 1) * 128, :]
        return f

    for l in range(2, 4):
        p.sfx = f"_C{l}"
        last = (l == 3)
        if l == 2:
            def after_c():
                p.s.cc_wait()
                halo_copy(0)
            h_tile = hkv_tile = tile_src(halowin[0], hloc[1].ap())
            rkc = ["halowin0", "hloc1"]
        else:
            def after_c():
                p.s.cc_wait()
                halo_copy(1)
            h_tile = tile_src(halowin[1], hloc[2].ap())
            hkv_tile = tile_src(halowin[0], hloc[1].ap())
            rkc = ["halowin0", "hloc1", "halowin1", "hloc2"]
        build_C([2] + [4] * 8, 2, final_norm=last, p=p,
                io=dict(h_tile=h_tile, hkv_tile=hkv_tile, h_dst=(out if last else hloc[2].ap()),
                        flag=flag, rkeys=rkc, dkey=(None if last else "hloc2"),
                        after_setup=after_c))
        if not last:
            p.phase_end()
            p.s.cc(lambda e, sem: e.collective_compute(
                "AllGather", ALU.bypass, replica_groups=rg,
                ins=[hloc[2].ap()[CH - 256:CH, :].opt()], outs=[Hh[1].ap().opt()]).then_inc(sem),
                ["hloc2"], ["Hh1"], n=1)
            if stop():
                return p.finish(), dict(p.dins)
    return p.finish(), dict(p.dins)


def kernel(x, norm_attn, norm_ffn, a_w_in, a_cmp_pos, a_cmp_w1, a_cmp_w2, a_w_out, kv_norm,
           b_w_kv, b_w_q, b_sinks, b_w_out, ffn_w_in, ffn_conv_w, ffn_conv_b, ffn_w_out,
           final_norm):
    f32 = lambda a: np.ascontiguousarray(np.asarray(a, dtype=np.float32))
    x = f32(x)
    norm_attn, norm_ffn = f32(norm_attn), f32(norm_ffn)
    a_w_in, a_cmp_pos, a_cmp_w1, a_cmp_w2, a_w_out = map(f32, (a_w_in, a_cmp_pos, a_cmp_w1,
                                                                a_cmp_w2, a_w_out))
    kv_norm, b_w_kv, b_w_q, b_sinks, b_w_out = map(f32, (kv_norm, b_w_kv, b_w_q, b_sinks, b_w_out))
    ffn_w_in, ffn_conv_w, ffn_conv_b, ffn_w_out, final_norm = map(
        f32, (ffn_w_in, ffn_conv_w, ffn_conv_b, ffn_w_out, final_norm))
    nc, dins = build_fused(NPHASE)
    ident = ident_np()
    gfin = np.ascontiguousarray(final_norm[None, :])
    cosA, sinA = rope_tables(np.arange(SEQ))
    constsA = nsa_consts(SEQ)
    maps = []
    for i in range(8):
        b, c = divmod(i, 4)
        g = c
        m = dict(xA=x[b], xB=_chunk_with_halo(x[b], c, 128),
                 flag=np.full((128, 1), 0.0 if c == 0 else 1.0, np.float32))
        for l in range(2):
            a = dict(g_attn=gT_np(norm_attn[l]), w1=a_cmp_w1[l], w2=a_cmp_w2[l],
                     cos_t=cosA, sin_t=sinA)
            a.update(constsA)
            a.update(nsa_weights(a_w_in[l], a_cmp_pos[l], g))
            for k, v in a.items():
                m[f"{k}_A{l}"] = v
            bb = dict(w_o=a_w_out[l], w_in=ffn_w_in[l], w_out=ffn_w_out[l],
                      cwb=_cwb(ffn_conv_w[l], ffn_conv_b[l]), g_ffn=gT_np(norm_ffn[l]), g_fin=gfin,
                      ident=ident)
            for k, v in bb.items():
                m[f"{k}_B{l}"] = v
        pos = c * CH - 256 + np.arange(CH + 256)
        cos_t, sin_t = rope_tables(pos)
        for l in range(2, 4):
            j = l - 2
            cc = dict(w_q=b_w_q[j], w_kv=b_w_kv, sinks_b=sinks_row(b_sinks[j]),
                      g_attn=gT_np(norm_attn[l]), g_kv=gT_np(kv_norm), cos_t=cos_t, sin_t=sin_t,
                      masks=swa_masks(c > 0), w_o=b_w_out[j], w_in=ffn_w_in[l], w_out=ffn_w_out[l],
                      cwb=_cwb(ffn_conv_w[l], ffn_conv_b[l]), g_ffn=gT_np(norm_ffn[l]), g_fin=gfin,
                      ident=ident)
            for k, v in cc.items():
                m[f"{k}_C{l}"] = v
        m = {k: v for k, v in m.items() if k in dins}
        maps.append(m)
    res = _run(nc, maps)
    h = np.stack([np.concatenate([res[b * 4 + c]["out"] for c in range(4)], 0)
                  for b in range(NB)], 0)
    return np.ascontiguousarray(h.astype(np.float32))
```

```python
import numpy as np
import ml_dtypes
import concourse.bass as bass
import concourse.mybir as mybir
from concourse.bass_utils import run_bass_kernel_spmd

F32 = mybir.dt.float32
BF16 = mybir.dt.bfloat16
AF = mybir.ActivationFunctionType
ALU = mybir.AluOpType
AX = mybir.AxisListType

NPBF16 = ml_dtypes.bfloat16

D = 1024
DFF = 2816
EPS = 1e-6
MASKV = -240000.0

COMPUTE = ("pe", "act", "dve", "pool")
EPOCH = 30000
NSLOT = 12
SAME_ENG_SYNC = True
ZERO_BIAS = True


class Sched:
    def __init__(self, nc):
        self.nc = nc
        self.streams = {e: [] for e in COMPUTE + ("sp",)}
        self.cnt = {e: 0 for e in COMPUTE}
        self.known = {e: {} for e in self.streams}
        self.known_dma = {e: set() for e in self.streams}
        self.last_w = {}
        self.readers = {}
        self.ndma = {e: 0 for e in self.streams}
        self.sems = {}
        self.nsem = 0
        self.out_dmas = []
        self.ncc = 0
        self.cc_pending = []
        self.snap = {e: [] for e in COMPUTE}
        self.snapd = {}

    def _sem(self, name):
        if name not in self.sems:
            self.sems[name] = self.nc.alloc_semaphore(name=name)
        return self.sems[name]

    def _ev_wait_args(self, ev):
        kind = ev[0]
        if kind == "c":
            _, eng, idx = ev
            ep, off = divmod(idx, EPOCH)
            return self._sem(f"s_{eng}_{ep}"), off + 1
        elif kind == "x":
            return self._sem(f"x_{ev[1]}"), ev[2]
        else:
            _, q, j = ev
            slot, use = j % NSLOT, j // NSLOT
            return self._sem(f"d_{q}_{slot}"), 16 * (use + 1)

    def _deps(self, eng, reads, writes):
        deps = set()
        for k in reads:
            w = self.last_w.get(k)
            if w is not None:
                deps.add(w)
        for k in writes:
            w = self.last_w.get(k)
            if w is not None:
                deps.add(w)
            for r in self.readers.get(k, ()):
                deps.add(r)
        waits = []
        best = {}
        for ev in deps:
            if ev[0] == "c":
                _, src, idx = ev
                if src == eng and eng == "pe":
                    continue
                if self.known[eng].get(src, -1) >= idx:
                    continue
                if best.get(src, -1) < idx:
                    best[src] = idx
            else:
                if ev in self.known_dma[eng]:
                    continue
                waits.append(ev)
                self.known_dma[eng].add(ev)
        for src, idx in best.items():
            self.known[eng][src] = idx
            waits.append(("c", src, idx))
        for ev in list(waits):
            sn = self.snap[ev[1]][ev[2]] if ev[0] == "c" else self.snapd.get(ev)
            if sn is None:
                continue
            kn = self.known[eng]
            for ci, ce in enumerate(COMPUTE):
                if sn[ci] > kn.get(ce, -1) and (ce != eng or True):
                    kn[ce] = sn[ci]
        return waits

    def _snapshot(self, eng):
        kn = self.known[eng]
        return tuple(kn.get(ce, -1) for ce in COMPUTE)

    def _mark(self, ev, reads, writes):
        for k in reads:
            self.readers.setdefault(k, []).append(ev)
        for k in writes:
            self.last_w[k] = ev
            self.readers[k] = []

    def op(self, eng, fn, reads=(), writes=()):
        assert eng in COMPUTE
        waits = self._deps(eng, reads, writes)
        idx = self.cnt[eng]
        self.cnt[eng] += 1
        ev = ("c", eng, idx)
        if not SAME_ENG_SYNC or eng == "pe":
            self.known[eng][eng] = idx
        sn = list(self._snapshot(eng))
        sn[COMPUTE.index(eng)] = max(sn[COMPUTE.index(eng)], idx - 1)
        self.snap[eng].append(tuple(sn))
        self._mark(ev, reads, writes)
        self.streams[eng].append((waits, fn, ev))
        return ev

    def dma(self, q, fn, reads=(), writes=(), is_output=False):
        waits = self._deps(q, reads, writes)
        j = self.ndma[q]
        self.ndma[q] += 1
        if j >= NSLOT:
            prev = ("d", q, j - NSLOT)
            if prev not in self.known_dma[q]:
                waits.append(prev)
                self.known_dma[q].add(prev)
        ev = ("d", q, j)
        self.snapd[ev] = self._snapshot(q)
        self._mark(ev, reads, writes)
        self.streams[q].append((waits, fn, ev))
        if is_output:
            self.out_dmas.append(ev)
        return ev

    def pid(self, e, key="pid", fn=None):
        k = (self.cur_eng, key)
        if k not in self.pid_cache:
            if key == "pid":
                self.pid_cache[k] = e.partition_id()
            else:
                self.pid_cache[k] = e.snap(fn(self.pid(e)))
        return self.pid_cache[k]

    def cc(self, fn, reads=(), writes=(), n=1):
        waits = self._deps("pool", reads, writes)
        ev = ("x", self.ncc, n)
        self.ncc += 1
        self._mark(ev, reads, writes)
        self.streams["pool"].append((waits, fn, ev))
        self.known_dma["pool"].add(ev)
        self.cc_pending.append((ev, tuple(writes)))
        return ev

    def cc_wait(self):
        for ev, writes in self.cc_pending:
            idx = self.cnt["pool"]
            self.cnt["pool"] += 1
            nev = ("c", "pool", idx)
            self.snap["pool"].append(self._snapshot("pool"))
            self._mark(nev, (), writes)
            self.streams["pool"].append(([ev], lambda e: e.nop(), nev))
        self.cc_pending = []

    def barrier(self):
        self.cc_wait()
        evs = []
        for eng in COMPUTE:
            if self.cnt[eng] > 0:
                evs.append(("c", eng, self.cnt[eng] - 1))
        for q, n in self.ndma.items():
            for j in range(max(0, n - NSLOT), n):
                evs.append(("d", q, j))
        for eng in self.streams:
            waits = []
            for ev in evs:
                if ev[0] == "c":
                    if ev[1] == eng:
                        continue
                    if self.known[eng].get(ev[1], -1) >= ev[2]:
                        continue
                    self.known[eng][ev[1]] = ev[2]
                elif ev[0] == "x":
                    continue
                elif ev in self.known_dma[eng]:
                    continue
                else:
                    self.known_dma[eng].add(ev)
                waits.append(ev)
            self.streams[eng].append((waits, None, None))

    def emit(self, final=True):
        nc = self.nc
        if final:
            self.cc_wait()
        final_waits = list(self.out_dmas) if final else []
        self.pid_cache = {}
        with nc.Block() as block:
            def run(engname, e):
                self.cur_eng = engname
                for waits, fn, ev in self.streams[engname]:
                    for w in waits:
                        s, v = self._ev_wait_args(w)
                        e.wait_ge(s, v)
                    if fn is None:
                        continue
                    s, v = self._ev_wait_args(ev)
                    if ev[0] == "x":
                        fn(e, s)
                        continue
                    ins = fn(e)
                    if ev[0] == "c":
                        ins.then_inc(s, 1)
                    else:
                        ins.then_inc(s, 16)
                if engname == "sp":
                    for w in final_waits:
                        s, v = self._ev_wait_args(w)
                        e.wait_ge(s, v)

            @block.tensor
            def _(e):
                run("pe", e)

            @block.scalar
            def _(e):
                run("act", e)

            @block.vector
            def _(e):
                run("dve", e)

            @block.gpsimd
            def _(e):
                run("pool", e)

            @block.sync
            def _(e):
                run("sp", e)
        for k in self.streams:
            self.streams[k] = []


class Prog:
    def __init__(self, nc):
        from contextlib import ExitStack
        self.nc = nc
        self.s = Sched(nc)
        self.es = ExitStack()
        self.ndram = 0
        self.sfx = ""
        self.dins = {}
        self.ext = {}

    def sb(self, name, shape, dt):
        return self.es.enter_context(self.nc.sbuf_tensor("sb_" + name + self.sfx, list(shape), dt))

    def ps(self, name, shape, dt=F32):
        return self.es.enter_context(self.nc.psum_tensor("ps_" + name + self.sfx, list(shape), dt))

    def din(self, name, shape, dt):
        nm = name + self.sfx
        if nm in self.ext:
            return self.ext[nm]
        self.dins[nm] = (tuple(shape), dt)
        return self.nc.dram_tensor(nm, list(shape), dt, kind="ExternalInput").ap()

    def dint(self, name, shape, dt):
        return self.nc.dram_tensor(name, list(shape), dt)

    def phase_end(self):
        from contextlib import ExitStack
        self.s.barrier()
        self.s.emit(final=False)
        self.es.close()
        self.es = ExitStack()

    def dmaf(self, fn, r=(), w=(), q="sp", is_output=False):
        return self.s.dma(q, fn, r, w, is_output)

    def dout(self, name, shape, dt):
        return self.nc.dram_tensor(name, list(shape), dt, kind="ExternalOutput").ap()

    def dma(self, out, in_, r=(), w=(), q="sp", is_output=False):
        return self.s.dma(q, lambda e: e.dma_start(out=out, in_=in_), r, w, is_output)

    def mm(self, out, lhsT, rhs, start, stop, r=(), w=()):
        return self.s.op("pe", lambda e: e.matmul(out, lhsT, rhs, start=start, stop=stop), r, w)

    def tr(self, out, in_, ident, r=(), w=()):
        return self.s.op("pe", lambda e: e.transpose(out, in_, ident), r, w)

    def act(self, out, in_, func, r=(), w=(), bias=None, scale=None, accum_out=None):
        kw = {}
        if bias is not None:
            kw["bias"] = bias
        if scale is not None:
            kw["scale"] = scale
        if accum_out is not None:
            kw["accum_out"] = accum_out
        return self.s.op("act", lambda e: e.activation(out, in_, func, **kw), r, w)

    def tt(self, out, in0, in1, op, r=(), w=(), eng="dve"):
        return self.s.op(eng, lambda e: e.tensor_tensor(out, in0, in1, op), r, w)

    def ts(self, out, in0, s1, s2, op0, op1=None, r=(), w=(), eng="dve", accum_out=None):
        kw = {}
        if accum_out is not None:
            kw["accum_out"] = accum_out
        if op1 is None:
            return self.s.op(eng, lambda e: e.tensor_scalar(out, in0, s1, s2, op0, **kw), r, w)
        return self.s.op(eng, lambda e: e.tensor_scalar(out, in0, s1, s2, op0, op1, **kw), r, w)

    def stt(self, out, in0, scalar, in1, op0, op1, r=(), w=(), eng="dve"):
        return self.s.op(eng, lambda e: e.scalar_tensor_tensor(out, in0, scalar, in1, op0, op1), r, w)

    def cp(self, out, in_, r=(), w=(), eng="dve"):
        if eng == "act":
            return self.s.op("act", lambda e: e.copy(out, in_), r, w)
        return self.s.op(eng, lambda e: e.tensor_copy(out, in_), r, w)

    def recip(self, out, in_, r=(), w=()):
        return self.s.op("dve", lambda e: e.reciprocal(out, in_), r, w)

    def memset(self, ap, val, w=(), eng="dve"):
        return self.s.op(eng, lambda e: e.memset(ap, val), (), w)

    def finish(self):
        self.s.emit()
        self.es.close()
        return self.nc


def load_cast_weight(p, w_dram, dst, nk, ncols, stage, tag, chunk_cols=1024):
    i = 0
    for kc in range(nk):
        for c0 in range(0, ncols, chunk_cols):
            cw = min(chunk_cols, ncols - c0)
            stg = stage[i % 2]
            p.dma(stg[:, 0:cw], w_dram[kc * 128:(kc + 1) * 128, c0:c0 + cw],
                  w=[("stg", i % 2)])
            p.cp(dst[:, kc, c0:c0 + cw], stg[:, 0:cw], r=[("stg", i % 2)],
                 w=[(tag, kc)], eng="pool")
            i += 1


def rmsnorm_tile(p, x_ap, gain_bc, out_ap, scr, keys_r, keys_w, tagk):
    sq, ss, sd, rs = scr
    p.act(sq, x_ap, AF.Square, r=keys_r, w=[("sq", tagk), ("ss", tagk)], accum_out=ss)
    p.act(sd, ss, AF.Sqrt, r=[("ss", tagk)], w=[("sd", tagk)], bias=EPS, scale=1.0 / D)
    p.recip(rs, sd, r=[("sd", tagk)], w=[("rs", tagk)])
    if gain_bc is None:
        p.ts(out_ap, x_ap, rs, None, ALU.mult, r=list(keys_r) + [("rs", tagk)], w=keys_w)
    else:
        p.stt(out_ap, x_ap, rs, gain_bc, ALU.mult, ALU.mult,
              r=list(keys_r) + [("rs", tagk), "gains"], w=keys_w)


class FFNCtx:
    def __init__(self, p, pre, max_nt=4):
        self.p = p
        self.max_nt = max_nt
        nt = max_nt
        self.wo = p.sb(pre + "wo", [64, 16, 1024], BF16)
        self.woch = [p.sb(pre + f"woch{i}", [128, 512], BF16) for i in range(2)]
        self.fstage = [p.sb(pre + f"fstg{i}", [128, 2048], F32) for i in range(2)]
        self.stage = [self.fstage[i][:, 0:1024] for i in range(2)]
        self.gT = p.sb(pre + "gT", [128, 8], F32)
        self.wch = [p.sb(pre + f"wch{i}", [128, 8, 2, 128], BF16) for i in range(2)]
        self.h1 = p.sb(pre + "h1", [128, nt, 1024], F32)
        self.oT = p.sb(pre + "oT", [64, 16, nt * 128], BF16)
        self.hn = p.sb(pre + "hn", [128, 1024], BF16)
        self.hnT = p.sb(pre + "hnT", [128, 8, nt * 128], BF16)
        self.actT = p.sb(pre + "actT", [128, 22, nt * 128], BF16)
        self.usb = [[p.sb(pre + f"usb{i}{a}", [128, 2 + nt * 128], F32) for a in range(2)]
                    for i in range(2)]
        self.carry = p.sb(pre + "carry", [128, 44, 2], F32)
        self.cwb = p.sb(pre + "cwb", [128, 44, 4], F32)
        self.t1 = p.sb(pre + "t1", [128, nt * 128], F32)
        self.t2 = p.sb(pre + "t2", [128, nt * 128], F32)
        self.ca = p.sb(pre + "ca", [128, nt * 128], F32)
        self.cg = p.sb(pre + "cg", [128, nt * 128], F32)
        self.sa = p.sb(pre + "sa", [128, nt * 128], F32)
        self.sq = p.sb(pre + "sq", [128, 1024], BF16)
        self.ss = p.sb(pre + "ss", [128, 1], F32)
        self.sd = p.sb(pre + "sd", [128, 1], F32)
        self.rs = p.sb(pre + "rs", [128, 1], F32)
        self.gainf = p.sb(pre + "gainf", [128, 1024], F32)
        self.ident = p.sb(pre + "ident", [128, 128], BF16)
        self.hfin = p.sb(pre + "hfin", [128, 1024], F32)
        self.psum = p.ps(pre + "psum", [128, 7 * 512])
        self.psT = p.ps(pre + "psT", [128, 8, 128], BF16)
        self.psA = [self.bank(0), self.bank(1)]
        self.psU = [[self.bank(2), self.bank(3)], [self.bank(4), self.bank(5)]]
        self.nA = 0
        self.nfc = 0
        self.nwo = 0

    def bank(self, i, n=1):
        return self.psum[:, i * 512:(i + n) * 512]

    def load_weights(self, w_o, w_in, w_out, cwb, g_ffn, ident, g_final=None, head_order=None):
        p = self.p
        self.w_in = w_in
        p.dma(self.ident[:], ident, w=["ident"])
        p.dma(self.cwb[:], cwb.rearrange("(c p) f -> p c f", p=128), w=["cwb"])
        p.dma(self.gT[:], g_ffn, w=["gT"])
        self.w_out = w_out
        nc = p.nc
        self.winb = nc.dram_tensor("winb" + p.sfx, [44 * 128, 1024], BF16).ap()
        self.woutb = nc.dram_tensor("woutb" + p.sfx, [44 * 128, 512], BF16).ap()
        i = 0
        for fc in range(22):
            for ag in range(2):
                par = i % 2
                i += 1
                stg = self.fstage[par]
                c0 = ag * DFF + fc * 128
                sk = ("stg", "ffn", par, 0)
                p.dma(stg[:, 0:1024].rearrange("p (c f) -> p c f", c=8),
                      w_in[:, c0:c0 + 128].rearrange("(c p) f -> p c f", p=128), w=[sk])
                p.tt(self.wch[par][:, :, 0, :],
                     stg[:, 0:1024].rearrange("p (c f) -> p c f", c=8),
                     self.gT[:].unsqueeze(2).to_broadcast([128, 8, 128]), ALU.mult,
                     r=[sk, "gT"], w=[("wch", par, 0)], eng="pool")
                p.dma(self.winb[(fc * 2 + ag) * 128:(fc * 2 + ag + 1) * 128, :],
                      self.wch[par][:, :, 0, :], r=[("wch", par, 0)], w=["winb"], q="pool")
        for half in range(2):
            for fc in range(22):
                par = i % 2
                i += 1
                stg = self.fstage[par]
                sk = ("stg", "ffn", par, 0)
                p.dma(stg[:, 0:512], w_out[fc * 128:(fc + 1) * 128, half * 512:(half + 1) * 512],
                      w=[sk])
                p.cp(self.woch[par][:], stg[:, 0:512], r=[sk], w=[("woch", par)], eng="pool")
                p.dma(self.woutb[(half * 22 + fc) * 128:(half * 22 + fc + 1) * 128, :],
                      self.woch[par][:], r=[("woch", par)], w=["woutb"], q="pool")
        if g_final is not None:
            p.dma(self.gainf[:], g_final.to_broadcast([128, 1024]), w=["gainf"])
        p.memset(self.carry[:], 0.0, w=["carry"])
        if w_o is not None:
            ho = head_order if head_order is not None else list(range(16))
            for i, h in enumerate(ho):
                stg = self.stage[i % 2]
                sk = ("stg", "ffn", i % 2, 0)
                p.dma(stg[0:64, :], w_o[h * 64:(h + 1) * 64, :], w=[sk])
                p.cp(self.wo[:, i, :], stg[0:64, :], r=[sk], w=[("wo", i)], eng="pool")

    def run_supertile(self, nt, h_src, oT_src, h_dst, n_skip_out=0, final_norm=False,
                      h1_preloaded=False, h_fn=None, oT_fn=None, n_flag=0, flag=None,
                      rkeys=(), dkey=None):
        p = self.p
        ntok = nt * 128
        if not h1_preloaded:
            for j in range(nt):
                p.dma(self.h1[:, j, :], h_src[j * 128:(j + 1) * 128, :], r=list(rkeys),
                      w=[("h1", j)])
                if j < n_flag:
                    p.ts(self.h1[:, j, :], self.h1[:, j, :], flag, None, ALU.mult,
                         r=[("h1", j), "flag"], w=[("h1", j)])
        if oT_src is not None or oT_fn is not None:
            if oT_fn is not None:
                for j in range(nt):
                    for g in range(4):
                        p.dma(self.oT[:, g * 4:(g + 1) * 4, j * 128:(j + 1) * 128], oT_fn(j, g),
                              r=list(rkeys), w=["oT"])
            elif not isinstance(oT_src, str):
                p.dma(self.oT[:, :, 0:ntok], oT_src.rearrange("(c p) t -> p c t", p=64), w=["oT"])
            for j in range(nt):
                for half in range(2):
                    ps = self.psA[self.nA % 2]
                    pk = ("bank", self.nA % 2)
                    self.nA += 1
                    for kc in range(16):
                        p.mm(ps, self.oT[:, kc, j * 128:(j + 1) * 128],
                             self.wo[:, kc, half * 512:(half + 1) * 512],
                             start=(kc == 0), stop=(kc == 15),
                             r=["oT", ("wo", kc)], w=[pk])
                    hs = self.h1[:, j, half * 512:(half + 1) * 512]
                    p.tt(hs, hs, ps, ALU.add, r=[pk, ("h1", j)], w=[("h1", j)])
        for j in range(nt):
            rmsnorm_tile(p, self.h1[:, j, :], None, self.hn[:],
                         (self.sq[:], self.ss[:], self.sd[:], self.rs[:]),
                         [("h1", j)], ["hn"], "f")
            for kc in range(8):
                p.tr(self.psT[:, kc, :], self.hn[:, kc * 128:(kc + 1) * 128], self.ident[:],
                     r=["hn", "ident"], w=["psT"])
            p.cp(self.hnT[:, :, j * 128:(j + 1) * 128], self.psT[:], r=["psT"], w=[("hnT", j)],
                 eng="act")
        hnT_keys = [("hnT", j) for j in range(nt)]
        for fc in range(22):
            par = self.nfc % 2
            self.nfc += 1
            wch = self.wch[par]
            for ag in range(2):
                p.dma(wch[:, :, ag, :],
                      self.winb[(fc * 2 + ag) * 128:(fc * 2 + ag + 1) * 128, :].rearrange(
                          "p (c f) -> p c f", c=8),
                      r=["winb"], w=[("wch", par, ag)])
            cs = []
            for ag in range(2):
                ps = self.psU[par][ag]
                pk = ("bank", 2 + 2 * par + ag)
                for kc in range(8):
                    p.mm(ps[:, 0:ntok], wch[:, kc, ag, :], self.hnT[:, kc, 0:ntok],
                         start=(kc == 0), stop=(kc == 7),
                         r=[("wch", par, ag)] + hnT_keys, w=[pk])
                usb = self.usb[par][ag]
                uk = ("usb", par, ag)
                ch = ag * 22 + fc
                p.cp(usb[:, 0:2], self.carry[:, ch, :], r=["carry%d" % ch, "carry"], w=[uk],
                     eng="pool")
                p.cp(usb[:, 2:2 + ntok], ps[:, 0:ntok], r=[pk], w=[uk], eng="act")
                p.cp(self.carry[:, ch, :], usb[:, ntok:ntok + 2], r=[uk], w=["carry%d" % ch],
                     eng="pool")
                cw = self.cwb
                dst = self.ca if ag == 0 else self.cg
                dk = "ca" if ag == 0 else "cg"
                p.ts(self.t1[:, 0:ntok], usb[:, 2:2 + ntok], cw[:, ch, 2:3], cw[:, ch, 3:4],
                     ALU.mult, ALU.add, r=[uk, "cwb"], w=["t1"])
                p.stt(self.t2[:, 0:ntok], usb[:, 1:1 + ntok], cw[:, ch, 1:2], self.t1[:, 0:ntok],
                      ALU.mult, ALU.add, r=[uk, "cwb", "t1"], w=["t2"])
                p.stt(dst[:, 0:ntok], usb[:, 0:ntok], cw[:, ch, 0:1], self.t2[:, 0:ntok],
                      ALU.mult, ALU.add, r=[uk, "cwb", "t2"], w=[dk])
            p.act(self.sa[:, 0:ntok], self.ca[:, 0:ntok], AF.Silu, r=["ca"], w=["sa"])
            p.tt(self.actT[:, fc, 0:ntok], self.sa[:, 0:ntok], self.cg[:, 0:ntok], ALU.mult,
                 r=["sa", "cg"], w=[("actT", fc)])
        for half in range(2):
            for fc in range(22):
                wp = self.nwo % 2
                self.nwo += 1
                p.dma(self.woch[wp][:],
                      self.woutb[(half * 22 + fc) * 128:(half * 22 + fc + 1) * 128, :],
                      r=["woutb"], w=[("woch", wp)])
                for j in range(nt):
                    p.mm(self.bank(j), self.actT[:, fc, j * 128:(j + 1) * 128], self.woch[wp][:],
                         start=(fc == 0), stop=(fc == 21),
                         r=[("actT", fc), ("woch", wp)], w=[("bank", j)])
            for j in range(nt):
                hs = self.h1[:, j, half * 512:(half + 1) * 512]
                p.tt(hs, hs, self.bank(j), ALU.add, r=[("bank", j), ("h1", j)], w=[("h1", j)])
        for j in range(nt):
            if h_dst is not None and j >= n_skip_out:
                jo = j - n_skip_out
                if final_norm:
                    rmsnorm_tile(p, self.h1[:, j, :], self.gainf[:], self.hfin[:],
                                 (self.sq[:], self.ss[:], self.sd[:], self.rs[:]),
                                 [("h1", j), "gainf"], ["hfin"], "f")
                    p.dma(h_dst[jo * 128:(jo + 1) * 128, :], self.hfin[:], r=["hfin"],
                          w=([dkey] if dkey else []), q="pool", is_output=True)
                else:
                    p.dma(h_dst[jo * 128:(jo + 1) * 128, :], self.h1[:, j, :], r=[("h1", j)],
                          w=([dkey] if dkey else []), q="pool", is_output=(dkey is None))


def ident_np():
    return np.eye(128, dtype=np.float32).astype(NPBF16)


def b_head_order():
    return [g * 4 + 2 * hpl + par for g in range(4) for par in range(2) for hpl in range(2)]


def build_B(st_sizes, n_skip_tiles, final_norm=False, p=None, io=None):
    fused = p is not None
    if not fused:
        nc = bass.Bass("TRN2", target_bir_lowering=False)
        p = Prog(nc)
    ntiles = sum(st_sizes)
    ntok = ntiles * 128
    if not fused:
        h_in = p.din("h_in", [ntok, D], F32)
        oT_in = p.din("oT_in", [D, ntok], BF16)
    w_o = p.din("w_o", [D, D], F32)
    w_in = p.din("w_in", [D, 2 * DFF], F32)
    w_out = p.din("w_out", [DFF, D], F32)
    cwb = p.din("cwb", [2 * DFF, 4], F32)
    g_ffn = p.din("g_ffn", [128, 8], F32)
    g_fin = p.din("g_fin", [1, D], F32)
    ident = p.din("ident", [128, 128], BF16)
    if not fused:
        h_out = p.dout("h_out", [(ntiles - n_skip_tiles) * 128, D], F32)
    else:
        h_out = io["h_dst"]
    f = FFNCtx(p, "f_", max_nt=max(st_sizes))
    f.load_weights(w_o, w_in, w_out, cwb, g_ffn, ident, g_fin,
                   head_order=(b_head_order() if fused else None))
    if fused:
        flag_sb = p.sb("flag", [128, 1], F32)
        p.dma(flag_sb[:], io["flag"], w=["flag"])
        if io.get("after_setup"):
            io["after_setup"]()
    t0 = 0
    for nt in st_sizes:
        skip = max(0, min(nt, n_skip_tiles - t0))
        o0 = max(0, t0 - n_skip_tiles)
        dst = h_out[o0 * 128:(o0 + nt - skip) * 128, :] if skip < nt else None
        if fused:
            f.run_supertile(nt, io["h_ap"][t0 * 128:(t0 + nt) * 128, :], None, dst,
                            n_skip_out=skip, final_norm=final_norm,
                            oT_fn=lambda j, g, t0=t0: io["oT_ap"](t0 + j, g),
                            n_flag=skip, flag=flag_sb[:], rkeys=io["rkeys"], dkey=io["dkey"])
        else:
            f.run_supertile(nt, h_in[t0 * 128:(t0 + nt) * 128, :],
                            oT_in[:, t0 * 128:(t0 + nt) * 128], dst, n_skip_out=skip,
                            final_norm=final_norm)
        t0 += nt
    if fused:
        return None
    return p.finish()


def c_head_order():
    return [8 * g + 2 * hpl + par for g in range(2) for par in range(2) for hpl in range(4)]


def build_C(st_sizes, n_skip_tiles, final_norm=False, p=None, io=None):
    fused = p is not None
    if not fused:
        nc = bass.Bass("TRN2", target_bir_lowering=False)
        p = Prog(nc)
    ntiles = sum(st_sizes)
    ntok = ntiles * 128
    mx = max(st_sizes)
    if not fused:
        h_in = p.din("h_in", [ntok, D], F32)
        hkv_in = p.din("hkv_in", [ntok, D], F32)
    w_q = p.din("w_q", [D, D], F32)
    w_kv = p.din("w_kv", [D, 256], F32)
    sinks_b = p.din("sinks_b", [1, 16], F32)
    g_attn = p.din("g_attn", [128, 8], F32)
    g_kv = p.din("g_kv", [128, 8], F32)
    cos_t = p.din("cos_t", [128, ntok], F32)
    sin_t = p.din("sin_t", [128, ntok], F32)
    masks = p.din("masks", [3, 128, 512], BF16)
    w_o = p.din("w_o", [D, D], F32)
    w_in = p.din("w_in", [D, 2 * DFF], F32)
    w_out = p.din("w_out", [DFF, D], F32)
    cwb = p.din("cwb", [2 * DFF, 4], F32)
    g_ffn = p.din("g_ffn", [128, 8], F32)
    g_fin = p.din("g_fin", [1, D], F32)
    ident = p.din("ident", [128, 128], BF16)
    if not fused:
        h_out = p.dout("h_out", [(ntiles - n_skip_tiles) * 128, D], F32)
    else:
        h_out = io["h_dst"]

    f = FFNCtx(p, "f_", max_nt=mx)
    f.load_weights(w_o, w_in, w_out, cwb, g_ffn, ident, g_fin, head_order=c_head_order())
    if fused:
        flag_sb = p.sb("flag", [128, 1], F32)
        p.dma(flag_sb[:], io["flag"], w=["flag"])

    gTq = p.sb("gTq", [128, 8], F32)
    gTk = p.sb("gTk", [128, 8], F32)
    wk2 = p.sb("wk2", [128, 8, 2, 2, 128], BF16)
    wv = p.sb("wv", [128, 8, 128], BF16)
    hkv = p.sb("hkv", [128, 1024], F32)
    hnqT = f.hnT
    hkvT = p.sb("hkvT", [128, 8, mx * 128], BF16)
    QT2 = p.sb("QT2", [128, 8, mx * 128], BF16)
    KT2 = p.sb("KT2", [128, 2, (mx + 1) * 128], BF16)
    VA = p.sb("VA", [128, mx + 1, 2, 65], BF16)
    PT = [p.sb(f"PT{i}", [128, 1024], BF16) for i in range(4)]
    msk = p.sb("msk", [128, 3, 512], BF16)
    cos_sb = p.sb("cos_sb", [128, mx * 128], F32)
    sin_sb = p.sb("sin_sb", [128, mx * 128], F32)
    sexp = p.sb("sexp", [128, 16], F32)
    zr = p.sb("zr", [128, 1024], F32)
    rz = zr
    ones = p.sb("ones", [128, 64], F32)
    osb = f.hfin[0:64, :]
    psS = [f.bank(2, 2), f.bank(4, 2)]
    psSk = [[("bank", 2), ("bank", 3)], [("bank", 4), ("bank", 5)]]
    psO = f.bank(0, 2)
    psOk = [("bank", 0), ("bank", 1)]
    psB = f.bank(6)
    psBk = [("bank", 6)]

    p.dma(gTq[:], g_attn, w=["gTq"])
    p.dma(gTk[:], g_kv, w=["gTk"])
    p.dma(msk[:], masks.rearrange("m p c -> p m c"), w=["msk"])
    p.dma(zr[64:65, 0:16], sinks_b, w=["zr"])
    p.act(sexp[64:65, :], zr[64:65, 0:16], AF.Exp, r=["zr"], w=["sexp"])
    p.memset(ones[:], 1.0, w=["ones"])
    p.memset(VA[:], 1.0, w=["VA"] + [("VA", i) for i in range(mx + 1)])
    p.memset(KT2[:], 0.0, w=["KT2", ("KT2", 0), ("KT2", 1)])
    for kc in range(8):
        stg = f.stage[kc % 2]
        sk = ("stg", "ffn", kc % 2, 0)
        gs = gTk[:, kc:kc + 1]
        p.dma(stg[:, 0:256], w_kv[kc * 128:(kc + 1) * 128, :], w=[sk])
        for g in range(2):
            for dup in range(2):
                p.ts(wk2[:, kc, g, 0, dup * 64:(dup + 1) * 64], stg[:, g * 64:(g + 1) * 64],
                     gs, None, ALU.mult, r=[sk, "gTk"], w=["wk2"], eng="pool")
                p.ts(wk2[:, kc, g, 1, dup * 64:dup * 64 + 32], stg[:, g * 64 + 32:g * 64 + 64],
                     gs, None, ALU.mult, r=[sk, "gTk"], w=["wk2"], eng="pool")
                p.ts(wk2[:, kc, g, 1, dup * 64 + 32:dup * 64 + 64], stg[:, g * 64:g * 64 + 32],
                     gs, None, ALU.mult, r=[sk, "gTk"], w=["wk2"], eng="pool")
        p.ts(wv[:, kc, :], stg[:, 128:256], gs, None, ALU.mult, r=[sk, "gTk"], w=["wv"], eng="pool")

    if fused and io.get("after_setup"):
        io["after_setup"]()
    scr = (f.sq[:], f.ss[:], f.sd[:], f.rs[:])
    t0 = 0
    first_real = n_skip_tiles
    for nt in st_sizes:
        n = nt * 128
        for j in range(nt):
            if fused:
                p.dma(f.h1[:, j, :], io["h_tile"](t0 + j), r=list(io["rkeys"]), w=[("h1", j)])
                if t0 + j < n_skip_tiles:
                    p.ts(f.h1[:, j, :], f.h1[:, j, :], flag_sb[:], None, ALU.mult,
                         r=[("h1", j), "flag"], w=[("h1", j)])
            else:
                p.dma(f.h1[:, j, :], h_in[(t0 + j) * 128:(t0 + j + 1) * 128, :], w=[("h1", j)])
        p.dma(cos_sb[:, 0:n], cos_t[:, t0 * 128:t0 * 128 + n], w=["cos"])
        p.dma(sin_sb[:, 0:n], sin_t[:, t0 * 128:t0 * 128 + n], w=["sin"])
        for j in range(nt):
            rmsnorm_tile(p, f.h1[:, j, :], None, f.hn[:], scr, [("h1", j)], ["hn"], "f")
            for kc in range(8):
                p.tr(f.psT[:, kc, :], f.hn[:, kc * 128:(kc + 1) * 128], f.ident[:],
                     r=["hn", "ident"], w=["psT"])
            p.cp(hnqT[:, :, j * 128:(j + 1) * 128], f.psT[:], r=["psT"], w=[("hnT", j)], eng="act")
            if fused:
                p.dma(hkv[:], io["hkv_tile"](t0 + j), r=list(io["rkeys"]), w=["hkv"])
                if t0 + j < n_skip_tiles:
                    p.ts(hkv[:], hkv[:], flag_sb[:], None, ALU.mult, r=["hkv", "flag"], w=["hkv"])
            else:
                p.dma(hkv[:], hkv_in[(t0 + j) * 128:(t0 + j + 1) * 128, :], w=["hkv"])
            rmsnorm_tile(p, hkv[:], None, f.hn[:], scr, ["hkv"], ["hn"], "f")
            for kc in range(8):
                p.tr(f.psT[:, kc, :], f.hn[:, kc * 128:(kc + 1) * 128], f.ident[:],
                     r=["hn", "ident"], w=["psT"])
            p.cp(hkvT[:, :, j * 128:(j + 1) * 128], f.psT[:], r=["psT"], w=[("hkvT", j)], eng="act")
        hq_keys = [("hnT", j) for j in range(nt)]
        hk_keys = [("hkvT", j) for j in range(nt)]

        def rope_out(dst, psn, pss, rk, wk):
            p.tt(f.t1[:, 0:n], psn, cos_sb[:, 0:n], ALU.mult, r=rk[0:1] + ["cos"], w=["t1"])
            p.tt(f.t2[:, 0:n], pss, sin_sb[:, 0:n], ALU.mult, r=rk[1:2] + ["sin"], w=["t2"])
            p.tt(dst, f.t1[:, 0:n], f.t2[:, 0:n], ALU.add, r=["t1", "t2"], w=wk)

        for g in range(2):
            par = f.nfc % 2
            f.nfc += 1
            bk = [("bank", 2 + 2 * par), ("bank", 3 + 2 * par)]
            for v in range(2):
                for kc in range(8):
                    p.mm(f.psU[par][v][:, 0:n], wk2[:, kc, g, v, :], hkvT[:, kc, 0:n],
                         start=(kc == 0), stop=(kc == 7), r=["wk2"] + hk_keys, w=[bk[v]])
            rope_out(KT2[:, g, 128:128 + n], f.psU[par][0][:, 0:n], f.psU[par][1][:, 0:n],
                     bk, [("KT2", g)])
        for j in range(nt):
            ps = f.psA[f.nA % 2]
            pk = ("bank", f.nA % 2)
            f.nA += 1
            for kc in range(8):
                p.mm(ps[:, 0:128], hkvT[:, kc, j * 128:(j + 1) * 128], wv[:, kc, :],
                     start=(kc == 0), stop=(kc == 7), r=[("hkvT", j), "wv"], w=[pk])
            p.cp(VA[:, j + 1, :, 0:64], ps[:, 0:128].rearrange("p (g d) -> p g d", g=2),
                 r=[pk], w=[("VA", j + 1)], eng="act")
        for hp in range(8):
            par = f.nfc % 2
            f.nfc += 1
            stg = f.fstage[par]
            wch = f.wch[par]
            p.dma(stg[:, 0:1024].rearrange("p (c f) -> p c f", c=8),
                  w_q[:, hp * 128:(hp + 1) * 128].rearrange("(c p) f -> p c f", p=128),
                  w=[("stg", "ffn", par, 0)])
            p.tt(wch[:, :, 0, :], stg[:, 0:1024].rearrange("p (c f) -> p c f", c=8),
                 gTq[:].unsqueeze(2).to_broadcast([128, 8, 128]), ALU.mult,
                 r=[("stg", "ffn", par, 0), "gTq"], w=[("wch", par, 0)], eng="pool")
            src = wch[:, :, 0, :].rearrange("p c (h d) -> p c h d", h=2)
            dsw = wch[:, :, 1, :].rearrange("p c (h d) -> p c h d", h=2)
            p.cp(dsw[:, :, :, 0:32], src[:, :, :, 32:64], r=[("wch", par, 0)],
                 w=[("wch", par, 1)], eng="pool")
            p.cp(dsw[:, :, :, 32:64], src[:, :, :, 0:32], r=[("wch", par, 0)],
                 w=[("wch", par, 1)], eng="pool")
            bk = [("bank", 2 + 2 * par), ("bank", 3 + 2 * par)]
            for v in range(2):
                for kc in range(8):
                    p.mm(f.psU[par][v][:, 0:n], wch[:, kc, v, :], hnqT[:, kc, 0:n],
                         start=(kc == 0), stop=(kc == 7),
                         r=[("wch", par, v)] + hq_keys, w=[bk[v]])
            rope_out(QT2[:, hp, 0:n], f.psU[par][0][:, 0:n], f.psU[par][1][:, 0:n],
                     bk, [("QT2", hp)])
        nS = 0
        pending = None
        for j in range(nt):
            gt = t0 + j
            for g in range(2):
                chunks = [(j, 0 if gt == first_real else 1), (j + 1, 2)]
                pts = []
                for ci, (slot, mi) in enumerate(chunks):
                    sp_ = nS % 2
                    ptb = nS % 4
                    nS += 1
                    for par in range(2):
                        pr = slice(par * 64, (par + 1) * 64)
                        p.mm(psS[sp_][:, par * 512:(par + 1) * 512],
                             KT2[pr, g, slot * 128:(slot + 1) * 128],
                             QT2[pr, 4 * g:4 * g + 4, j * 128:(j + 1) * 128],
                             start=True, stop=False,
                             r=[("KT2", g)] + [("QT2", 4 * g + i) for i in range(4)],
                             w=[psSk[sp_][par]])
                        p.mm(psS[sp_][:, par * 512:(par + 1) * 512], f.ident[:], msk[:, mi, :],
                             start=False, stop=True, r=["ident", "msk"], w=[psSk[sp_][par]])
                    p.act(PT[ptb][:], psS[sp_], AF.Exp, r=psSk[sp_], w=[("PT", ptb)], scale=0.125)
                    pts.append((ci, slot, ptb))

                def fin(j=j, g=g, pts=pts):
                    for ci, slot, ptb in pts:
                        for par in range(2):
                            p.mm(psO[0:65, par * 512:(par + 1) * 512], VA[:, slot, g, :],
                                 PT[ptb][:, par * 512:(par + 1) * 512],
                                 start=(ci == 0), stop=(ci == 1),
                                 r=[("PT", ptb), ("VA", slot), "VA"], w=[psOk[par]])
                    p.tt(zr[64:65, :].rearrange("p (h q) -> p h q", h=8),
                         psO[64:65, :].rearrange("p (h q) -> p h q", h=8),
                         sexp[64:65, g * 8:(g + 1) * 8].unsqueeze(2).to_broadcast([1, 8, 128]),
                         ALU.add, r=psOk + ["sexp"], w=["zr"])
                    p.recip(rz[64:65, :], zr[64:65, :], r=["zr"], w=["rz"])
                    p.cp(osb, psO[0:64, :], r=psOk, w=["hfin"], eng="act")
                    for par in range(2):
                        p.mm(psB[0:64, :], ones[64:65, :], rz[64:65, par * 512:(par + 1) * 512],
                             start=True, stop=True, r=["ones", "rz"], w=psBk)
                        dst = f.oT[:, g * 8 + par * 4:g * 8 + par * 4 + 4, j * 128:(j + 1) * 128]
                        p.tt(dst,
                             osb[:, par * 512:(par + 1) * 512].rearrange("p (h q) -> p h q", h=4),
                             psB[0:64, :].rearrange("p (h q) -> p h q", h=4), ALU.mult,
                             r=["hfin"] + psBk, w=["oT"])

                if pending is not None:
                    pending()
                pending = fin
        pending()
        for g in range(2):
            p.cp(KT2[:, g, 0:128], KT2[:, g, n:n + 128], r=[("KT2", g)], w=[("KT2", g)], eng="pool")
        p.cp(VA[:, 0, :, :], VA[:, nt, :, :], r=[("VA", nt)], w=[("VA", 0)], eng="pool")
        skip = max(0, min(nt, n_skip_tiles - t0))
        o0 = max(0, t0 - n_skip_tiles)
        dst = h_out[o0 * 128:(o0 + nt - skip) * 128, :] if skip < nt else None
        f.run_supertile(nt, None, "resident", dst, n_skip_out=skip, final_norm=final_norm,
                        h1_preloaded=True, dkey=(io["dkey"] if fused else None))
        t0 += nt
    if fused:
        return None
    return p.finish()


def rope_tables(pos):
    half = 32
    inv = (np.float32(10000.0) ** (-np.arange(half, dtype=np.float32) / half)).astype(np.float32)
    ang = pos.astype(np.float32)[None, :] * inv[:, None]
    cos = np.cos(ang).astype(np.float32)
    sin = np.sin(ang).astype(np.float32)
    cos64 = np.concatenate([cos, cos], 0)
    sin64 = np.concatenate([-sin, sin], 0)
    return (np.ascontiguousarray(np.concatenate([cos64, cos64], 0)),
            np.ascontiguousarray(np.concatenate([sin64, sin64], 0)))


def swa_masks(first_exists):
    i = np.arange(128)[:, None]
    q = np.arange(128)[None, :]
    prev = np.where(i > q, 0.0, MASKV).astype(np.float32)
    cur = np.where(i <= q, 0.0, MASKV).astype(np.float32)
    pf = prev if first_exists else np.full((128, 128), MASKV, np.float32)
    m = np.stack([np.tile(pf, (1, 4)), np.tile(prev, (1, 4)), np.tile(cur, (1, 4))], 0)
    return m.astype(NPBF16)


def sinks_row(sinks16):
    ho = c_head_order()
    return np.ascontiguousarray(np.asarray(sinks16, np.float32)[ho][None, :])


def gT_np(g):
    return np.ascontiguousarray(np.asarray(g, np.float32).reshape(8, 128).T)


FORCE = 1.0e6
TINY = 1.0e-30
C_BF = float(np.float32(NPBF16(-MASKV)))
LN_C = float(np.log(np.float64(C_BF)))
MUL_MASK = False
KEEP_ZB = False


def build_A(S, dbg=99, p=None, io=None):
    fused = p is not None
    if not fused:
        nc = bass.Bass("TRN2", target_bir_lowering=False)
        p = Prog(nc)
    nc = p.nc
    NST = S // 512
    NQB = S // 128
    NCC = max(1, S // 2048)
    if not fused:
        h_in = p.din("h_in", [S, D], F32)
    g_attn = p.din("g_attn", [128, 8], F32)
    wq_d = p.din("wq", [D, 256], F32)
    wk3_d = p.din("wk3", [D, 192], F32)
    wv3_d = p.din("wv3", [D, 192], F32)
    wg_d = p.din("wg", [D, 12], F32)
    w1_d = p.din("w1", [2, 2048, 256], F32)
    w2_d = p.din("w2", [2, 256, 64], F32)
    posT_d = p.din("posT", [64, 2, 32], F32)
    cos_d = p.din("cos_t", [128, S], F32)
    sin_d = p.din("sin_t", [128, S], F32)
    ccos_d = p.din("ccos_t", [128, NCC * 128], F32)
    csin_d = p.din("csin_t", [128, NCC * 128], F32)
    pmask_d = p.din("pmask", [2, 16, 128, 128], BF16)
    r0mask_d = p.din("r0mask", [128, 512], BF16)
    cmask_d = p.din("cmask", [2, 128, 512], BF16)
    emat_d = p.din("emat", [64, 128, 128], BF16)
    wfull_d = p.din("wfull", [NCC * 128, 257], BF16)
    fix_d = p.din("fix3", [128, 6], F32)
    ident_d = p.din("ident", [128, 128], BF16)
    gscr = [nc.dram_tensor(f"gscr{i}" + p.sfx, [1, 12 * 512], F32).ap() for i in range(2)]
    if not fused:
        oT_out = p.dout("oT_out", [64, 4, S], BF16)

    ident = p.sb("ident", [128, 128], BF16)
    gT = p.sb("gT", [128, 8], F32)
    fst = [p.sb(f"fst{i}", [128, 1024], F32) for i in range(2)]
    WQ = p.sb("WQ", [128, 8, 2, 256], BF16)
    WKS = p.sb("WKS", [128, 8, 2, 128], BF16)
    WKW = p.sb("WKW", [128, 8, 2, 128], BF16)
    WKC = p.sb("WKC", [128, 8, 64], BF16)
    WVC = p.sb("WVC", [128, 8, 64], BF16)
    WV2 = p.sb("WV2", [128, 8, 128], BF16)
    WG = p.sb("WG", [128, 8, 12], BF16)
    W1c = [p.sb(f"W1c{i}", [64, 4, 256], BF16) for i in range(2)]
    W2K = p.sb("W2K", [128, 2, 2, 128], BF16)
    W2V = p.sb("W2V", [128, 2, 64], BF16)
    posT = p.sb("posT", [64, 2, 32], BF16)
    c1 = p.sb("c1", [128, 4], F32)
    ccos = p.sb("ccos", [128, NCC * 128], F32)
    csin = p.sb("csin", [128, NCC * 128], F32)
    pmask = p.sb("pmask", [128, 2, 16, 128], BF16)
    r0mask = p.sb("r0mask", [128, 512], BF16)
    cmask = p.sb("cmask", [128, 2, 512], BF16)
    emat = p.sb("emat", [128, 64, 128], BF16)
    wfull = p.sb("wfull", [128, NCC, 257], BF16)
    fix3 = p.sb("fix3", [128, 6], F32)
    hbuf = [p.sb("hbuf0", [128, 1024], F32)] * 2
    sq = p.sb("sq", [128, 1024], BF16)
    ss = p.sb("ss", [128, 1], F32)
    sd = p.sb("sd", [128, 1], F32)
    rs = p.sb("rs", [128, 1], F32)
    hn = p.sb("hn", [128, 1024], BF16)
    hnT = p.sb("hnT", [128, 8, 512], BF16)
    cos_sb = p.sb("cos_sb", [128, 512], F32)
    sin_sb = p.sb("sin_sb", [128, 512], F32)
    t1 = p.sb("t1", [128, 512], F32)
    t2 = p.sb("t2", [128, 512], F32)
    Qblk = p.sb("Qblk", [128, 4, 512], BF16)
    KsT2 = p.sb("KsT2", [128, S], BF16)
    VsA = p.sb("VsA", [128, NQB, 65], BF16)
    KwT2 = p.sb("KwT2", [128, 1024], BF16)
    VwA = p.sb("VwA", [128, 8, 65], BF16)
    KcT2 = p.sb("KcT2", [128, NCC * 128], BF16)
    VcA = p.sb("VcA", [128, NCC, 65], BF16)
    xT = [p.sb(f"xT{i}", [64, 528], BF16) for i in range(2)]
    hidK = p.sb("hidK", [128, 2, 32], BF16)
    hidV = p.sb("hidV", [128, 2, 128], BF16)
    gx = [p.sb(f"gx{i}", [128, 32], F32) for i in range(3)]
    gsb = p.sb("gsb", [12, 512], F32)
    G64b = [p.sb(f"G64b{i}", [128, 12 * 128], F32) for i in range(2)]
    PT = [p.sb(f"PT{i}", [128, 512], BF16) for i in range(4)]
    EX = [p.sb(f"EX{i}", [128, 512], BF16) for i in range(4)]
    Msb = [p.sb(f"Msb{i}", [128, 128], BF16) for i in range(2)]
    nM = [0]
    PcT = p.sb("PcT", [128, NCC, 512], BF16)
    zr = p.sb("zr", [128, 512], F32)
    Rr = p.sb("Rr", [128, 512], F32)
    ones = p.sb("ones", [128, 64], F32)
    osb = p.sb("osb", [64, 512], F32)
    acc = p.sb("acc", [64, 512], F32)
    tmpo = p.sb("tmpo", [64, 512], F32)
    oacc = p.sb("oacc", [64, 4, 128], BF16)
    imp = p.sb("imp", [128, 256], F32)
    selbuf = p.sb("selbuf", [128, 256], F32)
    work = p.sb("work", [128, 256], F32)
    mx8 = p.sb("mx8", [128, 8], F32)
    thr = p.sb("thr", [128, 1], F32)
    zq = p.sb("zq", [128, 1], F32)
    Bq = p.sb("Bq", [128, 256], BF16)
    BT = p.sb("BT", [128, 2, 512], BF16)
    psum = p.ps("psum", [128, 7 * 512])
    psT = p.ps("psT", [128, 8, 128], BF16)
    zero_b = p.sb("zero_b", [128, 512], BF16)
    p.memset(zero_b[:], 0.0, w=["zero_b"])
    p.memset(Qblk[:], 0.0, w=["QT2"])

    def bank(i, n=1):
        return psum[:, i * 512:(i + n) * 512]

    def bk(i):
        return ("bank", i)

    p.dma(ident[:], ident_d, w=["ident"])
    p.dma(gT[:], g_attn, w=["gT"])
    p.dma(ccos[:], ccos_d, w=["ccos"])
    p.dma(csin[:], csin_d, w=["csin"])
    for a_ in range(2):
        for r4 in range(0, 16, 4):
            p.dma(pmask[:, a_, r4:r4 + 4, :], pmask_d[a_, r4:r4 + 4].rearrange("r p c -> p r c"),
                  w=["pmask"])
    p.dma(r0mask[:], r0mask_d, w=["r0mask"])
    p.dma(cmask[:], cmask_d.rearrange("a p c -> p a c"), w=["cmask"])
    for e8 in range(0, 64, 8):
        p.dma(emat[:, e8:e8 + 8, :], emat_d[e8:e8 + 8].rearrange("e p c -> p e c"), w=["emat"])
    p.dma(wfull[:], wfull_d.rearrange("(c p) f -> p c f", p=128), w=["wfull"])
    p.dma(fix3[:], fix_d, w=["fix3"])
    p.memset(ones[:], 1.0, w=["ones"])
    p.memset(VsA[:], 1.0, w=["VsA"])
    p.memset(VwA[:], 1.0, w=["VwA"])
    p.memset(VcA[:], 1.0, w=["VcA"])
    p.memset(KwT2[:], 0.0, w=["KwT2"])
    p.memset(KcT2[:], 0.0, w=["KcT2"])
    p.memset(selbuf[:], -FORCE, w=["selbuf"])
    p.memset(hidV[:], 0.0, w=["hidV"])
    for i in range(2):
        p.memset(xT[i][:], 0.0, w=[("xT", i)])
    nst_ = [0]

    def stage_load(dst_fn, src_ap, ncols, parts=128):
        i = nst_[0] % 2
        nst_[0] += 1
        k = ("fst", i)
        p.dma(fst[i][0:parts, 0:ncols], src_ap, w=[k])
        return fst[i], k

    def swapcopy(dst, src, r, w):
        d4 = dst.rearrange("p (h d) -> p h d", d=64)
        s4 = src.rearrange("p (h d) -> p h d", d=64)
        p.cp(d4[:, :, 0:32], s4[:, :, 32:64], r=r, w=w, eng="pool")
        p.cp(d4[:, :, 32:64], s4[:, :, 0:32], r=r, w=w, eng="pool")

    for kc in range(8):
        gs = gT[:, kc:kc + 1]
        rows = slice(kc * 128, (kc + 1) * 128)
        st_, k = stage_load(None, wq_d[rows, :], 256)
        p.ts(WQ[:, kc, 0, :], st_[:, 0:256], gs, None, ALU.mult, r=[k, "gT"], w=["WQ"], eng="pool")
        swapcopy(WQ[:, kc, 1, :], WQ[:, kc, 0, :], ["WQ"], ["WQ"])
        st_, k = stage_load(None, wk3_d[rows, :], 192)
        p.ts(WKC[:, kc, :], st_[:, 0:64], gs, None, ALU.mult, r=[k, "gT"], w=["WKC"], eng="pool")
        for (W_, c0) in ((WKS, 64), (WKW, 128)):
            for dup in range(2):
                p.ts(W_[:, kc, 0, dup * 64:(dup + 1) * 64], st_[:, c0:c0 + 64], gs, None, ALU.mult,
                     r=[k, "gT"], w=["WK"], eng="pool")
            swapcopy(W_[:, kc, 1, :], W_[:, kc, 0, :], ["WK"], ["WK"])
        st_, k = stage_load(None, wv3_d[rows, :], 192)
        p.ts(WVC[:, kc, :], st_[:, 0:64], gs, None, ALU.mult, r=[k, "gT"], w=["WVC"], eng="pool")
        p.ts(WV2[:, kc, :], st_[:, 64:192], gs, None, ALU.mult, r=[k, "gT"], w=["WV2"], eng="pool")
        st_, k = stage_load(None, wg_d[rows, :], 12)
        p.ts(WG[:, kc, :], st_[:, 0:12], gs, None, ALU.mult, r=[k, "gT"], w=["WG"], eng="pool")
    nW1 = [0]

    def w1_piece(kv, l0):
        i = nW1[0] % 2
        nW1[0] += 1
        k = ("fst", i)
        p.dma(fst[i][0:64, :].rearrange("p (l m) -> p l m", l=4),
              w1_d[kv, l0 * 64:(l0 + 4) * 64, :].rearrange("(l d) m -> d l m", d=64), w=[k])
        p.cp(W1c[i][:], fst[i][0:64, :].rearrange("p (l m) -> p l m", l=4),
             r=[k], w=[("W1c", i)], eng="pool")
        return W1c[i], ("W1c", i)

    for kv in range(2):
        for mt in range(2):
            st_, k = stage_load(None, w2_d[kv, mt * 128:(mt + 1) * 128, :], 64)
            if kv == 0:
                for dup in range(2):
                    p.cp(W2K[:, mt, 0, dup * 64:(dup + 1) * 64], st_[:, 0:64], r=[k], w=["W2K"],
                         eng="pool")
                swapcopy(W2K[:, mt, 1, :], W2K[:, mt, 0, :], ["W2K"], ["W2K"])
            else:
                p.cp(W2V[:, mt, :], st_[:, 0:64], r=[k], w=["W2V"], eng="pool")
    st_, k = stage_load(None, posT_d.rearrange("d a l -> d (a l)"), 64, parts=64)
    p.cp(posT[:].rearrange("d a l -> d (a l)"), st_[0:64, 0:64], r=[k], w=["posT"], eng="pool")
    for kv in range(2):
        for l0 in range(0, 32, 4):
            wt, wk_ = w1_piece(kv, l0)
            for mt in range(2):
                col = kv * 2 + mt
                bb = 1 if mt == 0 else 5
                for li in range(4):
                    l = l0 + li
                    p.mm(bank(bb)[:, col:col + 1], wt[:, li, mt * 128:(mt + 1) * 128],
                         posT[:, kv, l:l + 1], start=(l == 0), stop=(l == 31),
                         r=[wk_, "posT"], w=[bk(bb)])
    p.cp(c1[:, 0:1], bank(1)[:, 0:1], r=[bk(1)], w=["c1"], eng="act")
    p.cp(c1[:, 2:3], bank(1)[:, 2:3], r=[bk(1)], w=["c1"], eng="act")
    p.cp(c1[:, 1:2], bank(5)[:, 1:2], r=[bk(5)], w=["c1"], eng="act")
    p.cp(c1[:, 3:4], bank(5)[:, 3:4], r=[bk(5)], w=["c1"], eng="act")

    if dbg == 0:
        return p.finish()
    if fused:
        for cb in range(4):
            p.dma(io["o_zero"][:, cb, :], zero_b[0:64, 0:128], r=["zero_b"], w=[io["dkey"]],
                  q="pool")
    if fused and io.get("after_setup"):
        io["after_setup"]()
    scr = (sq[:], ss[:], sd[:], rs[:])

    def rope_out(dst, psn, pss, cs, sn, rk, wk, n):
        p.tt(t1[:, 0:n], psn, cs, ALU.mult, r=rk[0:1] + ["cos", "ccos"], w=["t1"])
        p.tt(t2[:, 0:n], pss, sn, ALU.mult, r=rk[1:2] + ["sin", "csin"], w=["t2"])
        if isinstance(dst, tuple):
            p.tt(dst[0], t1[0:64, 0:n], t2[0:64, 0:n], ALU.add, r=["t1", "t2"], w=wk)
            p.tt(dst[1], t1[64:128, 0:n], t2[64:128, 0:n], ALU.add, r=["t1", "t2"], w=wk)
        else:
            p.tt(dst, t1[:, 0:n], t2[:, 0:n], ALU.add, r=["t1", "t2"], w=wk)

    nU = [0]
    nH = [0]

    def proj_pair(W_, dst, cs, sn, wkey, rkey):
        par = nU[0] % 2
        nU[0] += 1
        b0, b1 = 2 + 2 * par, 3 + 2 * par
        for v, b in ((0, b0), (1, b1)):
            for kc in range(8):
                p.mm(bank(b), W_(kc, v), hnT[:, kc, :], start=(kc == 0), stop=(kc == 7),
                     r=[rkey, "hnT"], w=[bk(b)])
        rope_out(dst, bank(b0), bank(b1), cs, sn, [bk(b0), bk(b1)], wkey, 512)

    def gelu_to(dst, ps_ap, bias_ap, rk, wk):
        x, a, b = gx[0][:], gx[1][:], gx[2][:]
        p.act(x, ps_ap, AF.Identity, r=rk + ["c1"], w=["gx0"], bias=bias_ap)
        p.tt(a, x, x, ALU.mult, r=["gx0"], w=["gx1"])
        p.ts(a, a, 0.044715, 1.0, ALU.mult, ALU.add, r=["gx1"], w=["gx1"])
        p.tt(a, a, x, ALU.mult, r=["gx1", "gx0"], w=["gx1"])
        p.act(b, a, AF.Sigmoid, r=["gx1"], w=["gx2"], scale=1.5957691216057308)
        p.tt(dst, x, b, ALU.mult, r=["gx0", "gx2"], w=wk)

    nS = [0]
    sdepth = [2]
    ZB = [False]

    def attn_chunk(kT2, kcols, vaug, biases, first, last, n_extra_r):
        sp_ = nS[0] % sdepth[0]
        nS[0] += 1
        sb_ = 2 + sp_
        if ZERO_BIAS and len(biases) == 0:
            if ZB[0]:
                biases = [(ident[:], zero_b[:], ["ident", "zero_b"])]
            else:
                biases = [(ident[:], zero_b[:, 0:32], ["ident", "zero_b"], "small")]
        out = bank(sb_)
        p.mm(out, kT2[:, kcols], Qblk[:, :, qsl[0]], start=True, stop=(len(biases) == 0),
             r=n_extra_r + ["QT2"], w=[bk(sb_)])
        for bi, bias in enumerate(biases):
            lh, rh, rk = bias[0:3]
            if len(bias) == 4 and bias[3] == "small":
                p.mm(out[:, 0:32], lh, rh, start=False, stop=(bi == len(biases) - 1), r=rk,
                     w=[bk(sb_)])
            elif len(bias) == 4:
                for hh in range(4):
                    p.mm(out[:, hh * 128:(hh + 1) * 128], lh, rh, start=False,
                         stop=(bi == len(biases) - 1), r=rk, w=[bk(sb_)])
            else:
                p.mm(out, lh, rh, start=False, stop=(bi == len(biases) - 1), r=rk, w=[bk(sb_)])
        return sp_, sb_

    qsl = [None]
    for st in range(NST):
        tok0 = st * 512
        for j in range(4):
            hb = hbuf[j % 2]
            hk = ("hbuf", 0)
            if fused:
                r0 = io["h_row"](tok0 + j * 128)
                p.dma(hb[:], io["h_ap"][r0:r0 + 128, :], r=list(io["rkeys"]), w=[hk])
            else:
                p.dma(hb[:], h_in[tok0 + j * 128:tok0 + (j + 1) * 128, :], w=[hk])
            rmsnorm_tile(p, hb[:], None, hn[:], scr, [hk], ["hn"], "a")
            for kc in range(8):
                p.tr(psT[:, kc, :], hn[:, kc * 128:(kc + 1) * 128], ident[:],
                     r=["hn", "ident"], w=["psT"])
            p.cp(hnT[:, :, j * 128:(j + 1) * 128], psT[:], r=["psT"], w=["hnT"], eng="act")
        p.dma(cos_sb[:], cos_d[:, tok0:tok0 + 512], w=["cos"])
        p.dma(sin_sb[:], sin_d[:, tok0:tok0 + 512], w=["sin"])
        for hp in range(2):
            proj_pair(lambda kc, v, hp=hp: WQ[:, kc, v, hp * 128:(hp + 1) * 128],
                      (Qblk[0:64, hp, :], Qblk[64:128, 2 + hp, :]), cos_sb[:], sin_sb[:],
                      ["QT2"], "WQ")
        proj_pair(lambda kc, v: WKS[:, kc, v, :], KsT2[:, tok0:tok0 + 512], cos_sb[:], sin_sb[:],
                  ["KsT2"], "WK")
        proj_pair(lambda kc, v: WKW[:, kc, v, :], KwT2[:, 512:1024], cos_sb[:], sin_sb[:],
                  ["KwT2"], "WK")
        for i, W_ in enumerate((WKC, WVC)):
            for kc in range(8):
                p.mm(bank(1)[0:64, :], W_[:, kc, :], hnT[:, kc, :], start=(kc == 0), stop=(kc == 7),
                     r=["WKC", "WVC", "hnT"], w=[bk(1)])
            p.cp(xT[i][:, 16:528], bank(1)[0:64, :], r=[bk(1)], w=[("xT", i)], eng="act")
        for j in range(4):
            for kc in range(8):
                p.mm(bank(0)[:, 0:128], hnT[:, kc, j * 128:(j + 1) * 128], WV2[:, kc, :],
                     start=(kc == 0), stop=(kc == 7), r=["hnT", "WV2"], w=[bk(0)])
            p.cp(VsA[:, st * 4 + j, 0:64], bank(0)[:, 0:64], r=[bk(0), "VsA"], w=["VsA"], eng="act")
            p.cp(VwA[:, 4 + j, 0:64], bank(0)[:, 64:128], r=[bk(0), "VwA"], w=["VwA"], eng="act")
        for kc in range(8):
            p.mm(bank(1)[0:12, :], WG[:, kc, :], hnT[:, kc, :], start=(kc == 0), stop=(kc == 7),
                 r=["WG", "hnT"], w=[bk(1)])
        p.act(gsb[:], bank(1)[0:12, :], AF.Sigmoid, r=[bk(1)], w=["gsb"])
        p.dma(gscr[st % 2].rearrange("o (a b) -> (o a) b", a=12), gsb[:], r=["gsb"],
              w=[("gscr", st % 2)])
        if dbg == 1:
            return p.finish()
        for kv in range(2):
            x3 = xT[kv][:].rearrange("p (i s) -> p i s", s=16)
            for l0 in range(0, 32, 4):
                wt, wk_ = w1_piece(kv, l0)
                for mt in range(2):
                    bb = 1 if mt == 0 else 5
                    for li in range(4):
                        l = l0 + li
                        rhs = x3[:, 0:32, l] if l < 16 else x3[:, 1:33, l - 16]
                        p.mm(bank(bb)[:, 0:32], wt[:, li, mt * 128:(mt + 1) * 128], rhs,
                             start=(l == 0), stop=(l == 31), r=[wk_, ("xT", kv)], w=[bk(bb)])
            for mt in range(2):
                bb = 1 if mt == 0 else 5
                if kv == 0:
                    gelu_to(hidK[:, mt, :], bank(bb)[:, 0:32], c1[:, mt:mt + 1], [bk(bb)], ["hidK"])
                else:
                    if st % 4 == 0 and mt == 0:
                        p.memset(hidV[:], 0.0, w=["hidV"])
                    gelu_to(hidV[:, mt, (st % 4) * 32:(st % 4) * 32 + 32], bank(bb)[:, 0:32],
                            c1[:, 2 + mt:3 + mt], [bk(bb)], ["hidV"])
            if kv == 0:
                par = nU[0] % 2
                nU[0] += 1
                b0, b1 = 2 + 2 * par, 3 + 2 * par
                for v, b in ((0, b0), (1, b1)):
                    for mt in range(2):
                        p.mm(bank(b)[:, 0:32], W2K[:, mt, v, :], hidK[:, mt, :],
                             start=(mt == 0), stop=(mt == 1), r=["W2K", "hidK"], w=[bk(b)])
                sl = slice(st * 32, st * 32 + 32)
                rope_out(KcT2[:, sl], bank(b0)[:, 0:32], bank(b1)[:, 0:32], ccos[:, sl], csin[:, sl],
                         [bk(b0), bk(b1)], ["KcT2"], 32)
            else:
                for mt in range(2):
                    p.mm(bank(1)[:, 0:64], hidV[:, mt, :], W2V[:, mt, :],
                         start=(mt == 0), stop=(mt == 1), r=["W2V", "hidV"], w=[bk(1)])
                p.cp(VcA[:, st // 4, 0:64], bank(1)[:, 0:64], r=[bk(1), "VcA"], w=["VcA"], eng="act")
            p.cp(xT[kv][:, 0:16], xT[kv][:, 512:528], r=[("xT", kv)], w=[("xT", kv)], eng="pool")
        if dbg == 2:
            return p.finish()
        for j in range(4):
            qb = st * 4 + j
            qsl[0] = slice(j * 128, (j + 1) * 128)
            tsl = qsl[0]
            p.dma(G64b[j % 2][64:65, :].rearrange("p (a b) -> p a b", a=12),
                  gscr[st % 2].rearrange("o (a b) -> o a b", a=12)[:, :, tsl],
                  r=[("gscr", st % 2)], w=[("G64", j % 2)])

            def finish_branch(br, first):
                p.ts(zr[64:65, :], bank(0)[64:65, :], TINY, None, ALU.max, r=[bk(0)], w=["zr"])
                p.recip(zr[64:65, :], zr[64:65, :], r=["zr"], w=["zr"])
                g3 = G64b[j % 2][64:65, :].rearrange("p (h b t) -> p h b t", h=4, b=3)
                for par in range(2):
                    for hpl in range(2):
                        hl = 2 * hpl + par
                        c0 = (par * 2 + hpl) * 128
                        p.tt(Rr[64:65, c0:c0 + 128], zr[64:65, c0:c0 + 128], g3[:, hl, br, :],
                             ALU.mult, r=["zr", ("G64", j % 2)], w=["Rr"])
                p.cp(osb[:], bank(0)[0:64, :], r=[bk(0)], w=["osb"], eng="act")

                def part2(first=first):
                    p.mm(bank(1)[0:64, :], ones[64:65, :], Rr[64:65, :], start=True, stop=True,
                         r=["ones", "Rr"], w=[bk(1)])
                    if first:
                        p.tt(acc[:], osb[:], bank(1)[0:64, :], ALU.mult, r=["osb", bk(1)], w=["acc"])
                    else:
                        p.tt(tmpo[:], osb[:], bank(1)[0:64, :], ALU.mult, r=["osb", bk(1)],
                             w=["tmpo"])
                        p.tt(acc[:], acc[:], tmpo[:], ALU.add, r=["tmpo", "acc"], w=["acc"])
                return part2

            pend = []
            pdepth = [1]

            def pend_push(fn):
                pend.append(fn)
                while len(pend) > pdepth[0]:
                    pend.pop(0)()

            def pend_flush():
                while pend:
                    pend.pop(0)()

            ncc = qb // 16 + 1
            r_ = qb % 16
            for cc in range(ncc):
                biases = []
                lastc = (cc == ncc - 1)
                if lastc:
                    biases.append((ident[:], pmask[:, 1 if cc == 0 else 0, r_, :], ["ident", "pmask"], 128))
                elif cc == 0:
                    biases.append((ident[:], r0mask[:], ["ident", "r0mask"]))
                sp_, sb_ = attn_chunk(KcT2, slice(cc * 128, (cc + 1) * 128), None, biases,
                                      cc == 0, lastc, ["KcT2"])
                p.act(PcT[:, cc, :], bank(sb_), AF.Exp, r=[bk(sb_)], w=[("PcT", cc)], scale=0.125)
                pend_push(lambda cc=cc, lastc=lastc: p.mm(
                    bank(0)[0:65, :], VcA[:, cc, :], PcT[:, cc, :], start=(cc == 0), stop=lastc,
                    r=[("PcT", cc), "VcA"], w=[bk(0)]))
            pend_flush()
            for par in range(2):
                for hpl in range(2):
                    hi = par * 2 + hpl
                    c0 = hi * 128
                    ib = 4 + (hi % 2)
                    for cc in range(ncc):
                        p.mm(bank(ib)[:, 0:257], PcT[:, cc, c0:c0 + 128], wfull[:, cc, :],
                             start=(cc == 0), stop=(cc == ncc - 1),
                             r=[("PcT", cc), "wfull"], w=[bk(ib)])
                    p.ts(zq[:], bank(ib)[:, 256:257], TINY, None, ALU.max, r=[bk(ib)], w=["zq"])
                    p.recip(zq[:], zq[:], r=["zq"], w=["zq"])
                    if hi == 0:
                        p.ts(imp[:], bank(ib)[:, 0:256], zq[:], None, ALU.mult, r=[bk(ib), "zq"],
                             w=["imp"])
                    else:
                        p.stt(imp[:], bank(ib)[:, 0:256], zq[:], imp[:], ALU.mult, ALU.add,
                              r=[bk(ib), "zq", "imp"], w=["imp"])
            fin0 = finish_branch(0, True)
            if dbg == 3 or dbg == 100 + j * 10 + 3:
                return p.finish()
            nb = 2 * qb + 2
            p.cp(selbuf[:, 0:nb], imp[:, 0:nb], r=["imp"], w=["selbuf"])
            lo = 2 * qb - 1
            k0 = 0
            if lo < 0:
                lo, k0 = 0, 1
            nfx = 3 - k0
            p.tt(selbuf[:, lo:lo + nfx], selbuf[:, lo:lo + nfx], fix3[:, k0:3], ALU.mult,
                 r=["selbuf", "fix3"], w=["selbuf"])
            p.tt(selbuf[:, lo:lo + nfx], selbuf[:, lo:lo + nfx], fix3[:, 3 + k0:6], ALU.add,
                 r=["selbuf", "fix3"], w=["selbuf"])
            p.memset(selbuf[:, 0:1], 3.0 * FORCE, w=["selbuf"])
            p.s.op("dve", lambda e: e.max(out=mx8[:], in_=selbuf[:]), ["selbuf"], ["mx8"])
            p.s.op("dve", lambda e: e.match_replace(out=work[:], in_to_replace=mx8[:],
                                                    in_values=selbuf[:], imm_value=-2.0 * FORCE),
                   ["selbuf", "mx8"], ["work"])
            p.s.op("dve", lambda e: e.max(out=mx8[:], in_=work[:]), ["work"], ["mx8"])
            p.s.op("dve", lambda e: e.tensor_reduce(out=thr[:], in_=mx8[:], axis=AX.X, op=ALU.min),
                   ["mx8"], ["thr"])
            p.ts(Bq[:], selbuf[:], thr[:], 1.0, ALU.is_ge, ALU.subtract, r=["selbuf", "thr"], w=["Bq"])
            nhalf = 1 if nb <= 128 else 2
            for hf in range(nhalf):
                p.tr(psT[:, hf, :], Bq[:, hf * 128:(hf + 1) * 128], ident[:], r=["Bq", "ident"],
                     w=["psT"])
            for hf in range(nhalf):
                for rep in range(4):
                    p.cp(BT[:, hf, rep * 128:(rep + 1) * 128], psT[:, hf, :], r=["psT"], w=["BT"],
                         eng=("act" if rep % 2 == 0 else "dve"))
            if dbg == 5 or dbg == 100 + j * 10 + 5:
                return p.finish()
            k_lo = max(0, qb - 4)
            sdepth[0] = 4
            pdepth[0] = 2
            for kc in range(k_lo, qb + 1):
                biases = []
                if kc == qb - 4:
                    biases.append((ident[:], cmask[:, 1, :], ["ident", "cmask"]))
                if kc == qb:
                    biases.append((ident[:], cmask[:, 0, :], ["ident", "cmask"]))
                slot = 4 + j - (qb - kc)
                sp_, sb_ = attn_chunk(KwT2, slice(slot * 128, (slot + 1) * 128), None, biases,
                                      kc == k_lo, kc == qb, ["KwT2"])
                p.act(PT[sp_][:], bank(sb_), AF.Exp, r=[bk(sb_)], w=[("PT", sp_)], scale=0.125)
                if kc == min(k_lo + 1, qb) and fin0 is not None:
                    fin0()
                    fin0 = None
                pend_push(lambda kc=kc, sp_=sp_, slot=slot: p.mm(
                    bank(0)[0:65, :], VwA[:, slot, :], PT[sp_][:], start=(kc == k_lo),
                    stop=(kc == qb), r=[("PT", sp_), "VwA"], w=[bk(0)]))
            pend_flush()
            fin2 = finish_branch(2, False)
            if fin0 is not None:
                fin0()
                fin0 = None
            if dbg == 4 or dbg == 100 + j * 10 + 4:
                return p.finish()
            sdepth[0] = 3 if MUL_MASK else 4
            pdepth[0] = 2
            for kc in range(qb + 1):
                mulmask = MUL_MASK and kc != qb
                if mulmask:
                    zb = ZB[0]
                    ZB[0] = KEEP_ZB
                    sp_, sb_ = attn_chunk(KsT2, slice(kc * 128, (kc + 1) * 128), None, [],
                                          kc == 0, kc == qb, ["KsT2"])
                    ZB[0] = zb
                    mpar = nM[0] % 2
                    nM[0] += 1
                    mb = 5 + mpar
                    p.mm(bank(mb)[:, 0:128], emat[:, kc % 64, :], BT[:, kc // 64, 0:128],
                         start=True, stop=True, r=["emat", "BT"], w=[bk(mb)])
                    p.act(EX[sp_][:], bank(sb_), AF.Exp, r=[bk(sb_)], w=[("EX", sp_)], scale=0.125,
                          bias=-LN_C)
                    p.act(Msb[mpar][:], bank(mb)[:, 0:128], AF.Identity, r=[bk(mb)],
                          w=[("Msb", mpar)], bias=C_BF)
                    p.tt(PT[sp_][:].rearrange("p (h q) -> p h q", h=4),
                         EX[sp_][:].rearrange("p (h q) -> p h q", h=4),
                         Msb[mpar][:].unsqueeze(1).to_broadcast([128, 4, 128]), ALU.mult,
                         r=[("Msb", mpar), ("EX", sp_)], w=[("PT", sp_)])
                else:
                    biases = [(emat[:, kc % 64, :], BT[:, kc // 64, :], ["emat", "BT"])]
                    if kc == qb:
                        biases.append((ident[:], cmask[:, 0, :], ["ident", "cmask"]))
                    sp_, sb_ = attn_chunk(KsT2, slice(kc * 128, (kc + 1) * 128), None, biases,
                                          kc == 0, kc == qb, ["KsT2"])
                    p.act(PT[sp_][:], bank(sb_), AF.Exp, r=[bk(sb_)], w=[("PT", sp_)], scale=0.125)
                if kc == min(1, qb) and fin2 is not None:
                    fin2()
                    fin2 = None
                pend_push(lambda kc=kc, sp_=sp_: p.mm(
                    bank(0)[0:65, :], VsA[:, kc, :], PT[sp_][:], start=(kc == 0), stop=(kc == qb),
                    r=[("PT", sp_), "VsA"], w=[bk(0)]))
            pend_flush()
            fin1 = finish_branch(1, False)
            fin1()
            sdepth[0] = 2
            if dbg == 6 or dbg == 100 + j * 10 + 6:
                return p.finish()
            p.cp(oacc[:], acc[:].rearrange("p (c q) -> p c q", c=4), r=["acc"], w=["oacc"])
            if fused:
                for dst in io["o_dst"](qb):
                    p.dma(dst, oacc[:], r=["oacc"], w=[io["dkey"]], q="pool")
            else:
                p.dma(oT_out[:, :, tok0 + j * 128:tok0 + (j + 1) * 128], oacc[:], r=["oacc"],
                      q="pool", is_output=True)
            if dbg == 100 + j * 10 + 7:
                return p.finish()
        p.cp(KwT2[:, 0:512], KwT2[:, 512:1024], r=["KwT2"], w=["KwT2"], eng="pool")
        p.cp(VwA[:, 0:4, :], VwA[:, 4:8, :], r=["VwA"], w=["VwA"], eng="pool")
        if dbg == 7 + st:
            return p.finish()
    if fused:
        return None
    return p.finish()


def nsa_consts(S):
    NCC = max(1, S // 2048)
    ml = np.arange(128)[:, None]
    q = np.arange(128)[None, :]
    pm = np.zeros((2, 16, 128, 128), np.float32)
    for a in range(2):
        for r in range(16):
            valid = (16 * ml + 15 <= 128 * r + q)
            if a == 1:
                valid = valid & (ml >= 1)
            pm[a, r] = np.where(valid, 0.0, MASKV)
    pmask = pm.astype(NPBF16)
    r0 = np.zeros((128, 512), np.float32)
    r0[0, :] = MASKV
    cur = np.where(ml <= q, 0.0, MASKV).astype(np.float32)
    upper = np.where(ml > q, 0.0, MASKV).astype(np.float32)
    cmask = np.stack([np.tile(cur, (1, 4)), np.tile(upper, (1, 4))], 0).astype(NPBF16)
    emat = np.zeros((64, 128, 128), np.float32)
    for e in range(64):
        emat[e, 2 * e, 0:64] = -MASKV
        emat[e, 2 * e + 1, 64:128] = -MASKV
    ws = [1, 2, 2, 2, 1]
    wfull = np.zeros((NCC * 128, 257), np.float32)
    for m in range(1, NCC * 128):
        n = m - 1
        for j in range(256):
            i = n - 4 * j + 1
            if 0 <= i <= 4:
                wfull[m, j] = ws[i]
    wfull[:, 256] = 1.0
    fix = np.zeros((128, 6), np.float32)
    lo = np.arange(128) < 64
    fix[:, 0] = np.where(lo, 0.0, 1.0)
    fix[:, 3] = np.where(lo, FORCE, 0.0)
    fix[:, 4] = 2.0 * FORCE
    fix[:, 5] = np.where(lo, -FORCE, FORCE)
    cpos = 16 * np.arange(NCC * 128) + 15
    ccos, csin = rope_tables(cpos)
    return dict(pmask=pmask, r0mask=r0.astype(NPBF16), cmask=cmask, emat=emat.astype(NPBF16),
                wfull=wfull.astype(NPBF16), fix3=fix, ccos_t=ccos, csin_t=csin, ident=ident_np())


def nsa_weights(a_w_in_l, cmp_pos_l, g):
    W = a_w_in_l
    q0 = g * 256
    def kcol(i):
        return W[:, 1024 + i * 256 + g * 64: 1024 + i * 256 + (g + 1) * 64]
    kc_, vc_, ks_, vs_, kw_, vw_ = [kcol(i) for i in range(6)]
    wg = W[:, 1024 + 6 * 256 + g * 12: 1024 + 6 * 256 + (g + 1) * 12]
    return dict(wq=np.ascontiguousarray(W[:, q0:q0 + 256]),
                wk3=np.ascontiguousarray(np.concatenate([kc_, ks_, kw_], 1)),
                wv3=np.ascontiguousarray(np.concatenate([vc_, vs_, vw_], 1)),
                wg=np.ascontiguousarray(wg),
                posT=np.ascontiguousarray(np.transpose(cmp_pos_l, (2, 0, 1))))


SEQ = 16384
NB = 2
CH = 4096
NPHASE = 99


def _run(nc, in_maps):
    res = run_bass_kernel_spmd(nc, in_maps, core_ids=list(range(8)))
    return res.results


def _cwb(conv_w, conv_b):
    return np.ascontiguousarray(np.concatenate([conv_w, conv_b[None]], 0).T.astype(np.float32))


def _chunk_with_halo(x_b, c, halo):
    lo = c * CH - halo
    if lo >= 0:
        return np.ascontiguousarray(x_b[lo:(c + 1) * CH])
    pad = np.zeros((-lo,) + x_b.shape[1:], x_b.dtype)
    return np.ascontiguousarray(np.concatenate([pad, x_b[0:(c + 1) * CH]], 0))


def kernel_unfused(x, norm_attn, norm_ffn, a_w_in, a_cmp_pos, a_cmp_w1, a_cmp_w2, a_w_out, kv_norm,
           b_w_kv, b_w_q, b_sinks, b_w_out, ffn_w_in, ffn_conv_w, ffn_conv_b, ffn_w_out,
           final_norm):
    f32 = lambda a: np.ascontiguousarray(np.asarray(a, dtype=np.float32))
    x = f32(x)
    norm_attn, norm_ffn = f32(norm_attn), f32(norm_ffn)
    a_w_in, a_cmp_pos, a_cmp_w1, a_cmp_w2, a_w_out = map(f32, (a_w_in, a_cmp_pos, a_cmp_w1,
                                                                a_cmp_w2, a_w_out))
    kv_norm, b_w_kv, b_w_q, b_sinks, b_w_out = map(f32, (kv_norm, b_w_kv, b_w_q, b_sinks, b_w_out))
    ffn_w_in, ffn_conv_w, ffn_conv_b, ffn_w_out, final_norm = map(
        f32, (ffn_w_in, ffn_conv_w, ffn_conv_b, ffn_w_out, final_norm))
    h = x
    ident = ident_np()
    gfin = np.ascontiguousarray(final_norm[None, :])
    cosA, sinA = rope_tables(np.arange(SEQ))
    constsA = nsa_consts(SEQ)

    for l in range(2):
        ncA = build_A(SEQ)
        maps = []
        for i in range(8):
            b, g = divmod(i, 4)
            m = dict(h_in=h[b], g_attn=gT_np(norm_attn[l]), w1=a_cmp_w1[l], w2=a_cmp_w2[l],
                     cos_t=cosA, sin_t=sinA)
            m.update(constsA)
            m.update(nsa_weights(a_w_in[l], a_cmp_pos[l], g))
            maps.append(m)
        resA = _run(ncA, maps)
        oT_full = np.zeros((NB, 16, 64, SEQ), NPBF16)
        for i in range(8):
            b, g = divmod(i, 4)
            o = resA[i]["oT_out"]
            for par in range(2):
                for hpl in range(2):
                    oT_full[b, g * 4 + 2 * hpl + par] = o[:, par * 2 + hpl, :]
        oT_full = oT_full.reshape(NB, 1024, SEQ)
        ncB = build_B([1] + [4] * 8, 1)
        maps = []
        for i in range(8):
            b, c = divmod(i, 4)
            maps.append(dict(
                h_in=_chunk_with_halo(h[b], c, 128),
                oT_in=np.ascontiguousarray(_chunk_with_halo(oT_full[b].T, c, 128).T),
                w_o=a_w_out[l], w_in=ffn_w_in[l], w_out=ffn_w_out[l],
                cwb=_cwb(ffn_conv_w[l], ffn_conv_b[l]), g_ffn=gT_np(norm_ffn[l]), g_fin=gfin,
                ident=ident))
        resB = _run(ncB, maps)
        h = np.stack([np.concatenate([resB[b * 4 + c]["h_out"] for c in range(4)], 0)
                      for b in range(NB)], 0)

    hkv = h
    for l in range(2, 4):
        j = l - 2
        ncC = build_C([2] + [4] * 8, 2, final_norm=(l == 3))
        maps = []
        for i in range(8):
            b, c = divmod(i, 4)
            pos = c * CH - 256 + np.arange(CH + 256)
            cos_t, sin_t = rope_tables(pos)
            maps.append(dict(
                h_in=_chunk_with_halo(h[b], c, 256), hkv_in=_chunk_with_halo(hkv[b], c, 256),
                w_q=b_w_q[j], w_kv=b_w_kv, sinks_b=sinks_row(b_sinks[j]),
                g_attn=gT_np(norm_attn[l]), g_kv=gT_np(kv_norm), cos_t=cos_t, sin_t=sin_t,
                masks=swa_masks(c > 0), w_o=b_w_out[j], w_in=ffn_w_in[l], w_out=ffn_w_out[l],
                cwb=_cwb(ffn_conv_w[l], ffn_conv_b[l]), g_ffn=gT_np(norm_ffn[l]), g_fin=gfin,
                ident=ident))
        resC = _run(ncC, maps)
        h = np.stack([np.concatenate([resC[b * 4 + c]["h_out"] for c in range(4)], 0)
                      for b in range(NB)], 0)
    return np.ascontiguousarray(h.astype(np.float32))


def build_fused(nphase=99):
    from concourse.bass import ds
    nph = [0]

    def stop():
        nph[0] += 1
        return nph[0] >= nphase

    nc = bass.Bass("TRN2", target_bir_lowering=False)
    p = Prog(nc)
    S = SEQ
    WB = 128 + CH
    WC = 256 + CH
    SUBW = 11 * 128
    xA = p.din("xA", [S, D], F32)
    xB = p.din("xB", [WB, D], F32)
    flag = p.din("flag", [128, 1], F32)
    out = p.dout("out", [CH, D], F32)
    oTloc = [nc.dram_tensor(f"oTloc{l}", [12 * 64, 4 * SUBW], BF16) for l in range(2)]
    OTb = [nc.dram_tensor(f"OTb{l}", [12 * 256, 4 * SUBW], BF16) for l in range(2)]
    oTwin = nc.dram_tensor("oTwin", [3 * 256, 4 * SUBW], BF16).ap()
    hloc = [nc.dram_tensor(f"hloc{k}", [CH, D], F32) for k in range(3)]
    Hb = [nc.dram_tensor(f"Hb{k}", [S, D], F32) for k in range(3)]
    hwin = nc.dram_tensor("hwin", [WC, D], F32).ap()
    hkvwin = nc.dram_tensor("hkvwin", [WC, D], F32).ap()
    rg = [[0, 1, 2, 3], [4, 5, 6, 7]]
    PID = p.s.pid
    Hh = [nc.dram_tensor(f"Hh{k}", [4 * 256, D], F32) for k in (1, 2)]
    halowin = [nc.dram_tensor(f"halowin{k}", [256, D], F32).ap() for k in (1, 2)]

    def gather_group(src, dst, nchunk, rows, rk, wk):
        def fn(e, sem):
            for k in range(nchunk):
                e.collective_compute(
                    "AllGather", ALU.bypass, replica_groups=rg,
                    ins=[src.ap()[k * rows:(k + 1) * rows, :].opt()],
                    outs=[dst.ap()[k * 4 * rows:(k + 1) * 4 * rows, :].opt()]).then_inc(sem)
        p.s.cc(fn, [rk], [wk], n=nchunk)

    def h_row(tok):
        rank, rem = divmod(tok, CH)
        k, r = divmod(rem, 256)
        return (k * 4 + rank) * 256 + r

    def win_copy(dst, src, halo, q, rk, wk):
        s5 = src.rearrange("(k g r e) d -> k g r (e d)", k=16, g=4, e=8)
        dm = dst[halo:halo + CH, :].rearrange("(k g r e) d -> k g r (e d)", k=16, g=1, e=8)
        dh = dst[0:halo, :].rearrange("(k g r e) d -> k g r (e d)", k=1, g=1, e=8)
        h8 = halo // 8
        p.dmaf(lambda e: e.dma_start(
            out=dm, in_=s5[:, ds(PID(e, "c", lambda pid: pid % 4), 1), :, :]),
            r=[rk], w=[wk], q=q)
        p.dmaf(lambda e: e.dma_start(
            out=dh, in_=s5[15:16, ds(PID(e, "cm1", lambda pid: (pid + 3) % 4), 1), 32 - h8:32, :]),
            r=[rk], w=[wk], q=q)

    for l in range(2):
        p.sfx = f"_A{l}"
        rk = [] if l == 0 else [f"Hb{l - 1}"]
        O5 = oTloc[l].ap().rearrange("(c s d) (b t) -> c s d b t", c=4, s=3, b=4)

        def o_dst(qb, O5=O5):
            c, sl = divmod(qb, 32)
            sl += 1
            dsts = [O5[c, sl // 11, :, :, (sl % 11) * 128:(sl % 11) * 128 + 128]]
            if sl == 32 and c < 3:
                dsts.append(O5[c + 1, 0, :, :, 0:128])
            return dsts

        build_A(S, p=p, io=dict(h_ap=(xA if l == 0 else Hb[l - 1].ap()),
                                h_row=((lambda t: t) if l == 0 else h_row),
                                rkeys=rk, dkey=f"oTloc{l}", o_dst=o_dst,
                                o_zero=O5[0, 0, :, :, 0:128], after_setup=p.s.cc_wait))
        p.phase_end()
        gather_group(oTloc[l], OTb[l], 12, 64, f"oTloc{l}", f"OTb{l}")
        if stop():
            return p.finish(), dict(p.dins)
        p.sfx = f"_B{l}"
        O3 = OTb[l].ap().rearrange("(c r) f -> c r f", c=4)

        def after_b(l=l, O3=O3):
            p.s.cc_wait()
            p.dmaf(lambda e: e.dma_start(
                out=oTwin.rearrange("(c r) f -> c r f", c=1),
                in_=O3[ds(PID(e, "c", lambda pid: pid % 4), 1), :, :]),
                r=[f"OTb{l}"], w=["oTwin"], q="act")
            if l > 0:
                win_copy(hwin[0:WB, :], Hb[l - 1].ap(), 128, "act", f"Hb{l - 1}", "hwin")

        if l == 0:
            h_ap = xB
            rkb = ["oTwin"]
        else:
            h_ap = hwin[0:WB, :]
            rkb = ["oTwin", "hwin"]
        W5 = oTwin.rearrange("(s g d) (b t) -> d s g b t", s=3, g=4, b=4)

        def oT_ap(wt, g, W5=W5):
            return W5[:, wt // 11, g, :, (wt % 11) * 128:(wt % 11) * 128 + 128]

        build_B([1] + [4] * 8, 1, p=p,
                io=dict(h_ap=h_ap, oT_ap=oT_ap, h_dst=hloc[l].ap(), flag=flag,
                        rkeys=rkb, dkey=f"hloc{l}", after_setup=after_b))
        p.phase_end()
        if l == 0:
            gather_group(hloc[l], Hb[l], 16, 256, f"hloc{l}", f"Hb{l}")
        else:
            p.s.cc(lambda e, sem: e.collective_compute(
                "AllGather", ALU.bypass, replica_groups=rg,
                ins=[hloc[1].ap()[CH - 256:CH, :].opt()], outs=[Hh[0].ap().opt()]).then_inc(sem),
                ["hloc1"], ["Hh0"], n=1)
        if stop():
            return p.finish(), dict(p.dins)


    def halo_copy(k):
        src = Hh[k].ap().rearrange("(g r) d -> g r d", g=4)
        p.dmaf(lambda e: e.dma_start(
            out=halowin[k].rearrange("(g r) d -> g r d", g=1),
            in_=src[ds(PID(e, "cm1", lambda pid: (pid + 3) % 4), 1), :, :]),
            r=[f"Hh{k}"], w=[f"halowin{k}"], q="sp")

    def tile_src(halo_ap, main_ap):
        def f(t):
            if t < 2:
                return halo_ap[t * 128:(t + 1) * 128, :]
            return main_ap[(t - 2) * 128:(t - 1) * 128, :]
        return f

    for l in range(2, 4):
        p.sfx = f"_C{l}"
        last = (l == 3)
        if l == 2:
            def after_c():
                p.s.cc_wait()
                halo_copy(0)
            h_tile = hkv_tile = tile_src(halowin[0], hloc[1].ap())
            rkc = ["halowin0", "hloc1"]
        else:
            def after_c():
                p.s.cc_wait()
                halo_copy(1)
            h_tile = tile_src(halowin[1], hloc[2].ap())
            hkv_tile = tile_src(halowin[0], hloc[1].ap())
            rkc = ["halowin0", "hloc1", "halowin1", "hloc2"]
        build_C([2] + [4] * 8, 2, final_norm=last, p=p,
                io=dict(h_tile=h_tile, hkv_tile=hkv_tile, h_dst=(out if last else hloc[2].ap()),
                        flag=flag, rkeys=rkc, dkey=(None if last else "hloc2"),
                        after_setup=after_c))
        if not last:
            p.phase_end()
            p.s.cc(lambda e, sem: e.collective_compute(
                "AllGather", ALU.bypass, replica_groups=rg,
                ins=[hloc[2].ap()[CH - 256:CH, :].opt()], outs=[Hh[1].ap().opt()]).then_inc(sem),
                ["hloc2"], ["Hh1"], n=1)
            if stop():
                return p.finish(), dict(p.dins)
    return p.finish(), dict(p.dins)


def kernel(x, norm_attn, norm_ffn, a_w_in, a_cmp_pos, a_cmp_w1, a_cmp_w2, a_w_out, kv_norm,
           b_w_kv, b_w_q, b_sinks, b_w_out, ffn_w_in, ffn_conv_w, ffn_conv_b, ffn_w_out,
           final_norm):
    f32 = lambda a: np.ascontiguousarray(np.asarray(a, dtype=np.float32))
    x = f32(x)
    norm_attn, norm_ffn = f32(norm_attn), f32(norm_ffn)
    a_w_in, a_cmp_pos, a_cmp_w1, a_cmp_w2, a_w_out = map(f32, (a_w_in, a_cmp_pos, a_cmp_w1,
                                                                a_cmp_w2, a_w_out))
    kv_norm, b_w_kv, b_w_q, b_sinks, b_w_out = map(f32, (kv_norm, b_w_kv, b_w_q, b_sinks, b_w_out))
    ffn_w_in, ffn_conv_w, ffn_conv_b, ffn_w_out, final_norm = map(
        f32, (ffn_w_in, ffn_conv_w, ffn_conv_b, ffn_w_out, final_norm))
    nc, dins = build_fused(NPHASE)
    ident = ident_np()
    gfin = np.ascontiguousarray(final_norm[None, :])
    cosA, sinA = rope_tables(np.arange(SEQ))
    constsA = nsa_consts(SEQ)
    maps = []
    for i in range(8):
        b, c = divmod(i, 4)
        g = c
        m = dict(xA=x[b], xB=_chunk_with_halo(x[b], c, 128),
                 flag=np.full((128, 1), 0.0 if c == 0 else 1.0, np.float32))
        for l in range(2):
            a = dict(g_attn=gT_np(norm_attn[l]), w1=a_cmp_w1[l], w2=a_cmp_w2[l],
                     cos_t=cosA, sin_t=sinA)
            a.update(constsA)
            a.update(nsa_weights(a_w_in[l], a_cmp_pos[l], g))
            for k, v in a.items():
                m[f"{k}_A{l}"] = v
            bb = dict(w_o=a_w_out[l], w_in=ffn_w_in[l], w_out=ffn_w_out[l],
                      cwb=_cwb(ffn_conv_w[l], ffn_conv_b[l]), g_ffn=gT_np(norm_ffn[l]), g_fin=gfin,
                      ident=ident)
            for k, v in bb.items():
                m[f"{k}_B{l}"] = v
        pos = c * CH - 256 + np.arange(CH + 256)
        cos_t, sin_t = rope_tables(pos)
        for l in range(2, 4):
            j = l - 2
            cc = dict(w_q=b_w_q[j], w_kv=b_w_kv, sinks_b=sinks_row(b_sinks[j]),
                      g_attn=gT_np(norm_attn[l]), g_kv=gT_np(kv_norm), cos_t=cos_t, sin_t=sin_t,
                      masks=swa_masks(c > 0), w_o=b_w_out[j], w_in=ffn_w_in[l], w_out=ffn_w_out[l],
                      cwb=_cwb(ffn_conv_w[l], ffn_conv_b[l]), g_ffn=gT_np(norm_ffn[l]), g_fin=gfin,
                      ident=ident)
            for k, v in cc.items():
                m[f"{k}_C{l}"] = v
        m = {k: v for k, v in m.items() if k in dins}
        maps.append(m)
    res = _run(nc, maps)
    h = np.stack([np.concatenate([res[b * 4 + c]["out"] for c in range(4)], 0)
                  for b in range(NB)], 0)
    return np.ascontiguousarray(h.astype(np.float32))
```

```python
import numpy as np
import ml_dtypes
import concourse.bass as bass
import concourse.mybir as mybir
from concourse.bass_utils import run_bass_kernel_spmd

F32 = mybir.dt.float32
BF16 = mybir.dt.bfloat16
AF = mybir.ActivationFunctionType
ALU = mybir.AluOpType
AX = mybir.AxisListType

NPBF16 = ml_dtypes.bfloat16

D = 1024
DFF = 2816
EPS = 1e-6
MASKV = -240000.0

COMPUTE = ("pe", "act", "dve", "pool")
EPOCH = 30000
NSLOT = 12
SAME_ENG_SYNC = True
ZERO_BIAS = True


class Sched:
    def __init__(self, nc):
        self.nc = nc
        self.streams = {e: [] for e in COMPUTE + ("sp",)}
        self.cnt = {e: 0 for e in COMPUTE}
        self.known = {e: {} for e in self.streams}
        self.known_dma = {e: set() for e in self.streams}
        self.last_w = {}
        self.readers = {}
        self.ndma = {e: 0 for e in self.streams}
        self.sems = {}
        self.nsem = 0
        self.out_dmas = []
        self.ncc = 0
        self.cc_pending = []
        self.snap = {e: [] for e in COMPUTE}
        self.snapd = {}

    def _sem(self, name):
        if name not in self.sems:
            self.sems[name] = self.nc.alloc_semaphore(name=name)
        return self.sems[name]

    def _ev_wait_args(self, ev):
        kind = ev[0]
        if kind == "c":
            _, eng, idx = ev
            ep, off = divmod(idx, EPOCH)
            return self._sem(f"s_{eng}_{ep}"), off + 1
        elif kind == "x":
            return self._sem(f"x_{ev[1]}"), ev[2]
        else:
            _, q, j = ev
            slot, use = j % NSLOT, j // NSLOT
            return self._sem(f"d_{q}_{slot}"), 16 * (use + 1)

    def _deps(self, eng, reads, writes):
        deps = set()
        for k in reads:
            w = self.last_w.get(k)
            if w is not None:
                deps.add(w)
        for k in writes:
            w = self.last_w.get(k)
            if w is not None:
                deps.add(w)
            for r in self.readers.get(k, ()):
                deps.add(r)
        waits = []
        best = {}
        for ev in deps:
            if ev[0] == "c":
                _, src, idx = ev
                if src == eng and eng == "pe":
                    continue
                if self.known[eng].get(src, -1) >= idx:
                    continue
                if best.get(src, -1) < idx:
                    best[src] = idx
            else:
                if ev in self.known_dma[eng]:
                    continue
                waits.append(ev)
                self.known_dma[eng].add(ev)
        for src, idx in best.items():
            self.known[eng][src] = idx
            waits.append(("c", src, idx))
        for ev in list(waits):
            sn = self.snap[ev[1]][ev[2]] if ev[0] == "c" else self.snapd.get(ev)
            if sn is None:
                continue
            kn = self.known[eng]
            for ci, ce in enumerate(COMPUTE):
                if sn[ci] > kn.get(ce, -1) and (ce != eng or True):
                    kn[ce] = sn[ci]
        return waits

    def _snapshot(self, eng):
        kn = self.known[eng]
        return tuple(kn.get(ce, -1) for ce in COMPUTE)

    def _mark(self, ev, reads, writes):
        for k in reads:
            self.readers.setdefault(k, []).append(ev)
        for k in writes:
            self.last_w[k] = ev
            self.readers[k] = []

    def op(self, eng, fn, reads=(), writes=()):
        assert eng in COMPUTE
        waits = self._deps(eng, reads, writes)
        idx = self.cnt[eng]
        self.cnt[eng] += 1
        ev = ("c", eng, idx)
        if not SAME_ENG_SYNC or eng == "pe":
            self.known[eng][eng] = idx
        sn = list(self._snapshot(eng))
        sn[COMPUTE.index(eng)] = max(sn[COMPUTE.index(eng)], idx - 1)
        self.snap[eng].append(tuple(sn))
        self._mark(ev, reads, writes)
        self.streams[eng].append((waits, fn, ev))
        return ev

    def dma(self, q, fn, reads=(), writes=(), is_output=False):
        waits = self._deps(q, reads, writes)
        j = self.ndma[q]
        self.ndma[q] += 1
        if j >= NSLOT:
            prev = ("d", q, j - NSLOT)
            if prev not in self.known_dma[q]:
                waits.append(prev)
                self.known_dma[q].add(prev)
        ev = ("d", q, j)
        self.snapd[ev] = self._snapshot(q)
        self._mark(ev, reads, writes)
        self.streams[q].append((waits, fn, ev))
        if is_output:
            self.out_dmas.append(ev)
        return ev

    def pid(self, e, key="pid", fn=None):
        k = (self.cur_eng, key)
        if k not in self.pid_cache:
            if key == "pid":
                self.pid_cache[k] = e.partition_id()
            else:
                self.pid_cache[k] = e.snap(fn(self.pid(e)))
        return self.pid_cache[k]

    def cc(self, fn, reads=(), writes=(), n=1):
        waits = self._deps("pool", reads, writes)
        ev = ("x", self.ncc, n)
        self.ncc += 1
        self._mark(ev, reads, writes)
        self.streams["pool"].append((waits, fn, ev))
        self.known_dma["pool"].add(ev)
        self.cc_pending.append((ev, tuple(writes)))
        return ev

    def cc_wait(self):
        for ev, writes in self.cc_pending:
            idx = self.cnt["pool"]
            self.cnt["pool"] += 1
            nev = ("c", "pool", idx)
            self.snap["pool"].append(self._snapshot("pool"))
            self._mark(nev, (), writes)
            self.streams["pool"].append(([ev], lambda e: e.nop(), nev))
        self.cc_pending = []

    def barrier(self):
        self.cc_wait()
        evs = []
        for eng in COMPUTE:
            if self.cnt[eng] > 0:
                evs.append(("c", eng, self.cnt[eng] - 1))
        for q, n in self.ndma.items():
            for j in range(max(0, n - NSLOT), n):
                evs.append(("d", q, j))
        for eng in self.streams:
            waits = []
            for ev in evs:
                if ev[0] == "c":
                    if ev[1] == eng:
                        continue
                    if self.known[eng].get(ev[1], -1) >= ev[2]:
                        continue
                    self.known[eng][ev[1]] = ev[2]
                elif ev[0] == "x":
                    continue
                elif ev in self.known_dma[eng]:
                    continue
                else:
                    self.known_dma[eng].add(ev)
                waits.append(ev)
            self.streams[eng].append((waits, None, None))

    def emit(self, final=True):
        nc = self.nc
        if final:
            self.cc_wait()
        final_waits = list(self.out_dmas) if final else []
        self.pid_cache = {}
        with nc.Block() as block:
            def run(engname, e):
                self.cur_eng = engname
                for waits, fn, ev in self.streams[engname]:
                    for w in waits:
                        s, v = self._ev_wait_args(w)
                        e.wait_ge(s, v)
                    if fn is None:
                        continue
                    s, v = self._ev_wait_args(ev)
                    if ev[0] == "x":
                        fn(e, s)
                        continue
                    ins = fn(e)
                    if ev[0] == "c":
                        ins.then_inc(s, 1)
                    else:
                        ins.then_inc(s, 16)
                if engname == "sp":
                    for w in final_waits:
                        s, v = self._ev_wait_args(w)
                        e.wait_ge(s, v)

            @block.tensor
            def _(e):
                run("pe", e)

            @block.scalar
            def _(e):
                run("act", e)

            @block.vector
            def _(e):
                run("dve", e)

            @block.gpsimd
            def _(e):
                run("pool", e)

            @block.sync
            def _(e):
                run("sp", e)
        for k in self.streams:
            self.streams[k] = []


class Prog:
    def __init__(self, nc):
        from contextlib import ExitStack
        self.nc = nc
        self.s = Sched(nc)
        self.es = ExitStack()
        self.ndram = 0
        self.sfx = ""
        self.dins = {}
        self.ext = {}

    def sb(self, name, shape, dt):
        return self.es.enter_context(self.nc.sbuf_tensor("sb_" + name + self.sfx, list(shape), dt))

    def ps(self, name, shape, dt=F32):
        return self.es.enter_context(self.nc.psum_tensor("ps_" + name + self.sfx, list(shape), dt))

    def din(self, name, shape, dt):
        nm = name + self.sfx
        if nm in self.ext:
            return self.ext[nm]
        self.dins[nm] = (tuple(shape), dt)
        return self.nc.dram_tensor(nm, list(shape), dt, kind="ExternalInput").ap()

    def dint(self, name, shape, dt):
        return self.nc.dram_tensor(name, list(shape), dt)

    def phase_end(self):
        from contextlib import ExitStack
        self.s.barrier()
        self.s.emit(final=False)
        self.es.close()
        self.es = ExitStack()

    def dmaf(self, fn, r=(), w=(), q="sp", is_output=False):
        return self.s.dma(q, fn, r, w, is_output)

    def dout(self, name, shape, dt):
        return self.nc.dram_tensor(name, list(shape), dt, kind="ExternalOutput").ap()

    def dma(self, out, in_, r=(), w=(), q="sp", is_output=False):
        return self.s.dma(q, lambda e: e.dma_start(out=out, in_=in_), r, w, is_output)

    def mm(self, out, lhsT, rhs, start, stop, r=(), w=()):
        return self.s.op("pe", lambda e: e.matmul(out, lhsT, rhs, start=start, stop=stop), r, w)

    def tr(self, out, in_, ident, r=(), w=()):
        return self.s.op("pe", lambda e: e.transpose(out, in_, ident), r, w)

    def act(self, out, in_, func, r=(), w=(), bias=None, scale=None, accum_out=None):
        kw = {}
        if bias is not None:
            kw["bias"] = bias
        if scale is not None:
            kw["scale"] = scale
        if accum_out is not None:
            kw["accum_out"] = accum_out
        return self.s.op("act", lambda e: e.activation(out, in_, func, **kw), r, w)

    def tt(self, out, in0, in1, op, r=(), w=(), eng="dve"):
        return self.s.op(eng, lambda e: e.tensor_tensor(out, in0, in1, op), r, w)

    def ts(self, out, in0, s1, s2, op0, op1=None, r=(), w=(), eng="dve", accum_out=None):
        kw = {}
        if accum_out is not None:
            kw["accum_out"] = accum_out
        if op1 is None:
            return self.s.op(eng, lambda e: e.tensor_scalar(out, in0, s1, s2, op0, **kw), r, w)
        return self.s.op(eng, lambda e: e.tensor_scalar(out, in0, s1, s2, op0, op1, **kw), r, w)

    def stt(self, out, in0, scalar, in1, op0, op1, r=(), w=(), eng="dve"):
        return self.s.op(eng, lambda e: e.scalar_tensor_tensor(out, in0, scalar, in1, op0, op1), r, w)

    def cp(self, out, in_, r=(), w=(), eng="dve"):
        if eng == "act":
            return self.s.op("act", lambda e: e.copy(out, in_), r, w)
        return self.s.op(eng, lambda e: e.tensor_copy(out, in_), r, w)

    def recip(self, out, in_, r=(), w=()):
        return self.s.op("dve", lambda e: e.reciprocal(out, in_), r, w)

    def memset(self, ap, val, w=(), eng="dve"):
        return self.s.op(eng, lambda e: e.memset(ap, val), (), w)

    def finish(self):
        self.s.emit()
        self.es.close()
        return self.nc


def load_cast_weight(p, w_dram, dst, nk, ncols, stage, tag, chunk_cols=1024):
    i = 0
    for kc in range(nk):
        for c0 in range(0, ncols, chunk_cols):
            cw = min(chunk_cols, ncols - c0)
            stg = stage[i % 2]
            p.dma(stg[:, 0:cw], w_dram[kc * 128:(kc + 1) * 128, c0:c0 + cw],
                  w=[("stg", i % 2)])
            p.cp(dst[:, kc, c0:c0 + cw], stg[:, 0:cw], r=[("stg", i % 2)],
                 w=[(tag, kc)], eng="pool")
            i += 1


def rmsnorm_tile(p, x_ap, gain_bc, out_ap, scr, keys_r, keys_w, tagk):
    sq, ss, sd, rs = scr
    p.act(sq, x_ap, AF.Square, r=keys_r, w=[("sq", tagk), ("ss", tagk)], accum_out=ss)
    p.act(sd, ss, AF.Sqrt, r=[("ss", tagk)], w=[("sd", tagk)], bias=EPS, scale=1.0 / D)
    p.recip(rs, sd, r=[("sd", tagk)], w=[("rs", tagk)])
    if gain_bc is None:
        p.ts(out_ap, x_ap, rs, None, ALU.mult, r=list(keys_r) + [("rs", tagk)], w=keys_w)
    else:
        p.stt(out_ap, x_ap, rs, gain_bc, ALU.mult, ALU.mult,
              r=list(keys_r) + [("rs", tagk), "gains"], w=keys_w)


class FFNCtx:
    def __init__(self, p, pre, max_nt=4):
        self.p = p
        self.max_nt = max_nt
        nt = max_nt
        self.wo = p.sb(pre + "wo", [64, 16, 1024], BF16)
        self.woch = [p.sb(pre + f"woch{i}", [128, 512], BF16) for i in range(2)]
        self.fstage = [p.sb(pre + f"fstg{i}", [128, 2048], F32) for i in range(2)]
        self.stage = [self.fstage[i][:, 0:1024] for i in range(2)]
        self.gT = p.sb(pre + "gT", [128, 8], F32)
        self.wch = [p.sb(pre + f"wch{i}", [128, 8, 2, 128], BF16) for i in range(2)]
        self.h1 = p.sb(pre + "h1", [128, nt, 1024], F32)
        self.oT = p.sb(pre + "oT", [64, 16, nt * 128], BF16)
        self.hn = p.sb(pre + "hn", [128, 1024], BF16)
        self.hnT = p.sb(pre + "hnT", [128, 8, nt * 128], BF16)
        self.actT = p.sb(pre + "actT", [128, 22, nt * 128], BF16)
        self.usb = [[p.sb(pre + f"usb{i}{a}", [128, 2 + nt * 128], F32) for a in range(2)]
                    for i in range(2)]
        self.carry = p.sb(pre + "carry", [128, 44, 2], F32)
        self.cwb = p.sb(pre + "cwb", [128, 44, 4], F32)
        self.t1 = p.sb(pre + "t1", [128, nt * 128], F32)
        self.t2 = p.sb(pre + "t2", [128, nt * 128], F32)
        self.ca = p.sb(pre + "ca", [128, nt * 128], F32)
        self.cg = p.sb(pre + "cg", [128, nt * 128], F32)
        self.sa = p.sb(pre + "sa", [128, nt * 128], F32)
        self.sq = p.sb(pre + "sq", [128, 1024], BF16)
        self.ss = p.sb(pre + "ss", [128, 1], F32)
        self.sd = p.sb(pre + "sd", [128, 1], F32)
        self.rs = p.sb(pre + "rs", [128, 1], F32)
        self.gainf = p.sb(pre + "gainf", [128, 1024], F32)
        self.ident = p.sb(pre + "ident", [128, 128], BF16)
        self.hfin = p.sb(pre + "hfin", [128, 1024], F32)
        self.psum = p.ps(pre + "psum", [128, 7 * 512])
        self.psT = p.ps(pre + "psT", [128, 8, 128], BF16)
        self.psA = [self.bank(0), self.bank(1)]
        self.psU = [[self.bank(2), self.bank(3)], [self.bank(4), self.bank(5)]]
        self.nA = 0
        self.nfc = 0
        self.nwo = 0

    def bank(self, i, n=1):
        return self.psum[:, i * 512:(i + n) * 512]

    def load_weights(self, w_o, w_in, w_out, cwb, g_ffn, ident, g_final=None, head_order=None):
        p = self.p
        self.w_in = w_in
        p.dma(self.ident[:], ident, w=["ident"])
        p.dma(self.cwb[:], cwb.rearrange("(c p) f -> p c f", p=128), w=["cwb"])
        p.dma(self.gT[:], g_ffn, w=["gT"])
        self.w_out = w_out
        nc = p.nc
        self.winb = nc.dram_tensor("winb" + p.sfx, [44 * 128, 1024], BF16).ap()
        self.woutb = nc.dram_tensor("woutb" + p.sfx, [44 * 128, 512], BF16).ap()
        i = 0
        for fc in range(22):
            for ag in range(2):
                par = i % 2
                i += 1
                stg = self.fstage[par]
                c0 = ag * DFF + fc * 128
                sk = ("stg", "ffn", par, 0)
                p.dma(stg[:, 0:1024].rearrange("p (c f) -> p c f", c=8),
                      w_in[:, c0:c0 + 128].rearrange("(c p) f -> p c f", p=128), w=[sk])
                p.tt(self.wch[par][:, :, 0, :],
                     stg[:, 0:1024].rearrange("p (c f) -> p c f", c=8),
                     self.gT[:].unsqueeze(2).to_broadcast([128, 8, 128]), ALU.mult,
                     r=[sk, "gT"], w=[("wch", par, 0)], eng="pool")
                p.dma(self.winb[(fc * 2 + ag) * 128:(fc * 2 + ag + 1) * 128, :],
                      self.wch[par][:, :, 0, :], r=[("wch", par, 0)], w=["winb"], q="pool")
        for half in range(2):
            for fc in range(22):
                par = i % 2
                i += 1
                stg = self.fstage[par]
                sk = ("stg", "ffn", par, 0)
                p.dma(stg[:, 0:512], w_out[fc * 128:(fc + 1) * 128, half * 512:(half + 1) * 512],
                      w=[sk])
                p.cp(self.woch[par][:], stg[:, 0:512], r=[sk], w=[("woch", par)], eng="pool")
                p.dma(self.woutb[(half * 22 + fc) * 128:(half * 22 + fc + 1) * 128, :],
                      self.woch[par][:], r=[("woch", par)], w=["woutb"], q="pool")
        if g_final is not None:
            p.dma(self.gainf[:], g_final.to_broadcast([128, 1024]), w=["gainf"])
        p.memset(self.carry[:], 0.0, w=["carry"])
        if w_o is not None:
            ho = head_order if head_order is not None else list(range(16))
            for i, h in enumerate(ho):
                stg = self.stage[i % 2]
                sk = ("stg", "ffn", i % 2, 0)
                p.dma(stg[0:64, :], w_o[h * 64:(h + 1) * 64, :], w=[sk])
                p.cp(self.wo[:, i, :], stg[0:64, :], r=[sk], w=[("wo", i)], eng="pool")

    def run_supertile(self, nt, h_src, oT_src, h_dst, n_skip_out=0, final_norm=False,
                      h1_preloaded=False, h_fn=None, oT_fn=None, n_flag=0, flag=None,
                      rkeys=(), dkey=None):
        p = self.p
        ntok = nt * 128
        if not h1_preloaded:
            for j in range(nt):
                p.dma(self.h1[:, j, :], h_src[j * 128:(j + 1) * 128, :], r=list(rkeys),
                      w=[("h1", j)])
                if j < n_flag:
                    p.ts(self.h1[:, j, :], self.h1[:, j, :], flag, None, ALU.mult,
                         r=[("h1", j), "flag"], w=[("h1", j)])
        if oT_src is not None or oT_fn is not None:
            if oT_fn is not None:
                for j in range(nt):
                    for g in range(4):
                        p.dma(self.oT[:, g * 4:(g + 1) * 4, j * 128:(j + 1) * 128], oT_fn(j, g),
                              r=list(rkeys), w=["oT"])
            elif not isinstance(oT_src, str):
                p.dma(self.oT[:, :, 0:ntok], oT_src.rearrange("(c p) t -> p c t", p=64), w=["oT"])
            for j in range(nt):
                for half in range(2):
                    ps = self.psA[self.nA % 2]
                    pk = ("bank", self.nA % 2)
                    self.nA += 1
                    for kc in range(16):
                        p.mm(ps, self.oT[:, kc, j * 128:(j + 1) * 128],
                             self.wo[:, kc, half * 512:(half + 1) * 512],
                             start=(kc == 0), stop=(kc == 15),
                             r=["oT", ("wo", kc)], w=[pk])
                    hs = self.h1[:, j, half * 512:(half + 1) * 512]
                    p.tt(hs, hs, ps, ALU.add, r=[pk, ("h1", j)], w=[("h1", j)])
        for j in range(nt):
            rmsnorm_tile(p, self.h1[:, j, :], None, self.hn[:],
                         (self.sq[:], self.ss[:], self.sd[:], self.rs[:]),
                         [("h1", j)], ["hn"], "f")
            for kc in range(8):
                p.tr(self.psT[:, kc, :], self.hn[:, kc * 128:(kc + 1) * 128], self.ident[:],
                     r=["hn", "ident"], w=["psT"])
            p.cp(self.hnT[:, :, j * 128:(j + 1) * 128], self.psT[:], r=["psT"], w=[("hnT", j)],
                 eng="act")
        hnT_keys = [("hnT", j) for j in range(nt)]
        for fc in range(22):
            par = self.nfc % 2
            self.nfc += 1
            wch = self.wch[par]
            for ag in range(2):
                p.dma(wch[:, :, ag, :],
                      self.winb[(fc * 2 + ag) * 128:(fc * 2 + ag + 1) * 128, :].rearrange(
                          "p (c f) -> p c f", c=8),
                      r=["winb"], w=[("wch", par, ag)])
            cs = []
            for ag in range(2):
                ps = self.psU[par][ag]
                pk = ("bank", 2 + 2 * par + ag)
                for kc in range(8):
                    p.mm(ps[:, 0:ntok], wch[:, kc, ag, :], self.hnT[:, kc, 0:ntok],
                         start=(kc == 0), stop=(kc == 7),
                         r=[("wch", par, ag)] + hnT_keys, w=[pk])
                usb = self.usb[par][ag]
                uk = ("usb", par, ag)
                ch = ag * 22 + fc
                p.cp(usb[:, 0:2], self.carry[:, ch, :], r=["carry%d" % ch, "carry"], w=[uk],
                     eng="pool")
                p.cp(usb[:, 2:2 + ntok], ps[:, 0:ntok], r=[pk], w=[uk], eng="act")
                p.cp(self.carry[:, ch, :], usb[:, ntok:ntok + 2], r=[uk], w=["carry%d" % ch],
                     eng="pool")
                cw = self.cwb
                dst = self.ca if ag == 0 else self.cg
                dk = "ca" if ag == 0 else "cg"
                p.ts(self.t1[:, 0:ntok], usb[:, 2:2 + ntok], cw[:, ch, 2:3], cw[:, ch, 3:4],
                     ALU.mult, ALU.add, r=[uk, "cwb"], w=["t1"])
                p.stt(self.t2[:, 0:ntok], usb[:, 1:1 + ntok], cw[:, ch, 1:2], self.t1[:, 0:ntok],
                      ALU.mult, ALU.add, r=[uk, "cwb", "t1"], w=["t2"])
                p.stt(dst[:, 0:ntok], usb[:, 0:ntok], cw[:, ch, 0:1], self.t2[:, 0:ntok],
                      ALU.mult, ALU.add, r=[uk, "cwb", "t2"], w=[dk])
            p.act(self.sa[:, 0:ntok], self.ca[:, 0:ntok], AF.Silu, r=["ca"], w=["sa"])
            p.tt(self.actT[:, fc, 0:ntok], self.sa[:, 0:ntok], self.cg[:, 0:ntok], ALU.mult,
                 r=["sa", "cg"], w=[("actT", fc)])
        for half in range(2):
            for fc in range(22):
                wp = self.nwo % 2
                self.nwo += 1
                p.dma(self.woch[wp][:],
                      self.woutb[(half * 22 + fc) * 128:(half * 22 + fc + 1) * 128, :],
                      r=["woutb"], w=[("woch", wp)])
                for j in range(nt):
                    p.mm(self.bank(j), self.actT[:, fc, j * 128:(j + 1) * 128], self.woch[wp][:],
                         start=(fc == 0), stop=(fc == 21),
                         r=[("actT", fc), ("woch", wp)], w=[("bank", j)])
            for j in range(nt):
                hs = self.h1[:, j, half * 512:(half + 1) * 512]
                p.tt(hs, hs, self.bank(j), ALU.add, r=[("bank", j), ("h1", j)], w=[("h1", j)])
        for j in range(nt):
            if h_dst is not None and j >= n_skip_out:
                jo = j - n_skip_out
                if final_norm:
                    rmsnorm_tile(p, self.h1[:, j, :], self.gainf[:], self.hfin[:],
                                 (self.sq[:], self.ss[:], self.sd[:], self.rs[:]),
                                 [("h1", j), "gainf"], ["hfin"], "f")
                    p.dma(h_dst[jo * 128:(jo + 1) * 128, :], self.hfin[:], r=["hfin"],
                          w=([dkey] if dkey else []), q="pool", is_output=True)
                else:
                    p.dma(h_dst[jo * 128:(jo + 1) * 128, :], self.h1[:, j, :], r=[("h1", j)],
                          w=([dkey] if dkey else []), q="pool", is_output=(dkey is None))


def ident_np():
    return np.eye(128, dtype=np.float32).astype(NPBF16)


def b_head_order():
    return [g * 4 + 2 * hpl + par for g in range(4) for par in range(2) for hpl in range(2)]


def build_B(st_sizes, n_skip_tiles, final_norm=False, p=None, io=None):
    fused = p is not None
    if not fused:
        nc = bass.Bass("TRN2", target_bir_lowering=False)
        p = Prog(nc)
    ntiles = sum(st_sizes)
    ntok = ntiles * 128
    if not fused:
        h_in = p.din("h_in", [ntok, D], F32)
        oT_in = p.din("oT_in", [D, ntok], BF16)
    w_o = p.din("w_o", [D, D], F32)
    w_in = p.din("w_in", [D, 2 * DFF], F32)
    w_out = p.din("w_out", [DFF, D], F32)
    cwb = p.din("cwb", [2 * DFF, 4], F32)
    g_ffn = p.din("g_ffn", [128, 8], F32)
    g_fin = p.din("g_fin", [1, D], F32)
    ident = p.din("ident", [128, 128], BF16)
    if not fused:
        h_out = p.dout("h_out", [(ntiles - n_skip_tiles) * 128, D], F32)
    else:
        h_out = io["h_dst"]
    f = FFNCtx(p, "f_", max_nt=max(st_sizes))
    f.load_weights(w_o, w_in, w_out, cwb, g_ffn, ident, g_fin,
                   head_order=(b_head_order() if fused else None))
    if fused:
        flag_sb = p.sb("flag", [128, 1], F32)
        p.dma(flag_sb[:], io["flag"], w=["flag"])
        if io.get("after_setup"):
            io["after_setup"]()
    t0 = 0
    for nt in st_sizes:
        skip = max(0, min(nt, n_skip_tiles - t0))
        o0 = max(0, t0 - n_skip_tiles)
        dst = h_out[o0 * 128:(o0 + nt - skip) * 128, :] if skip < nt else None
        if fused:
            f.run_supertile(nt, io["h_ap"][t0 * 128:(t0 + nt) * 128, :], None, dst,
                            n_skip_out=skip, final_norm=final_norm,
                            oT_fn=lambda j, g, t0=t0: io["oT_ap"](t0 + j, g),
                            n_flag=skip, flag=flag_sb[:], rkeys=io["rkeys"], dkey=io["dkey"])
        else:
            f.run_supertile(nt, h_in[t0 * 128:(t0 + nt) * 128, :],
                            oT_in[:, t0 * 128:(t0 + nt) * 128], dst, n_skip_out=skip,
                            final_norm=final_norm)
        t0 += nt
    if fused:
        return None
    return p.finish()


def c_head_order():
    return [8 * g + 2 * hpl + par for g in range(2) for par in range(2) for hpl in range(4)]


def build_C(st_sizes, n_skip_tiles, final_norm=False, p=None, io=None):
    fused = p is not None
    if not fused:
        nc = bass.Bass("TRN2", target_bir_lowering=False)
        p = Prog(nc)
    ntiles = sum(st_sizes)
    ntok = ntiles * 128
    mx = max(st_sizes)
    if not fused:
        h_in = p.din("h_in", [ntok, D], F32)
        hkv_in = p.din("hkv_in", [ntok, D], F32)
    w_q = p.din("w_q", [D, D], F32)
    w_kv = p.din("w_kv", [D, 256], F32)
    sinks_b = p.din("sinks_b", [1, 16], F32)
    g_attn = p.din("g_attn", [128, 8], F32)
    g_kv = p.din("g_kv", [128, 8], F32)
    cos_t = p.din("cos_t", [128, ntok], F32)
    sin_t = p.din("sin_t", [128, ntok], F32)
    masks = p.din("masks", [3, 128, 512], BF16)
    w_o = p.din("w_o", [D, D], F32)
    w_in = p.din("w_in", [D, 2 * DFF], F32)
    w_out = p.din("w_out", [DFF, D], F32)
    cwb = p.din("cwb", [2 * DFF, 4], F32)
    g_ffn = p.din("g_ffn", [128, 8], F32)
    g_fin = p.din("g_fin", [1, D], F32)
    ident = p.din("ident", [128, 128], BF16)
    if not fused:
        h_out = p.dout("h_out", [(ntiles - n_skip_tiles) * 128, D], F32)
    else:
        h_out = io["h_dst"]

    f = FFNCtx(p, "f_", max_nt=mx)
    f.load_weights(w_o, w_in, w_out, cwb, g_ffn, ident, g_fin, head_order=c_head_order())
    if fused:
        flag_sb = p.sb("flag", [128, 1], F32)
        p.dma(flag_sb[:], io["flag"], w=["flag"])

    gTq = p.sb("gTq", [128, 8], F32)
    gTk = p.sb("gTk", [128, 8], F32)
    wk2 = p.sb("wk2", [128, 8, 2, 2, 128], BF16)
    wv = p.sb("wv", [128, 8, 128], BF16)
    hkv = p.sb("hkv", [128, 1024], F32)
    hnqT = f.hnT
    hkvT = p.sb("hkvT", [128, 8, mx * 128], BF16)
    QT2 = p.sb("QT2", [128, 8, mx * 128], BF16)
    KT2 = p.sb("KT2", [128, 2, (mx + 1) * 128], BF16)
    VA = p.sb("VA", [128, mx + 1, 2, 65], BF16)
    PT = [p.sb(f"PT{i}", [128, 1024], BF16) for i in range(4)]
    msk = p.sb("msk", [128, 3, 512], BF16)
    cos_sb = p.sb("cos_sb", [128, mx * 128], F32)
    sin_sb = p.sb("sin_sb", [128, mx * 128], F32)
    sexp = p.sb("sexp", [128, 16], F32)
    zr = p.sb("zr", [128, 1024], F32)
    rz = zr
    ones = p.sb("ones", [128, 64], F32)
    osb = f.hfin[0:64, :]
    psS = [f.bank(2, 2), f.bank(4, 2)]
    psSk = [[("bank", 2), ("bank", 3)], [("bank", 4), ("bank", 5)]]
    psO = f.bank(0, 2)
    psOk = [("bank", 0), ("bank", 1)]
    psB = f.bank(6)
    psBk = [("bank", 6)]

    p.dma(gTq[:], g_attn, w=["gTq"])
    p.dma(gTk[:], g_kv, w=["gTk"])
    p.dma(msk[:], masks.rearrange("m p c -> p m c"), w=["msk"])
    p.dma(zr[64:65, 0:16], sinks_b, w=["zr"])
    p.act(sexp[64:65, :], zr[64:65, 0:16], AF.Exp, r=["zr"], w=["sexp"])
    p.memset(ones[:], 1.0, w=["ones"])
    p.memset(VA[:], 1.0, w=["VA"] + [("VA", i) for i in range(mx + 1)])
    p.memset(KT2[:], 0.0, w=["KT2", ("KT2", 0), ("KT2", 1)])
    for kc in range(8):
        stg = f.stage[kc % 2]
        sk = ("stg", "ffn", kc % 2, 0)
        gs = gTk[:, kc:kc + 1]
        p.dma(stg[:, 0:256], w_kv[kc * 128:(kc + 1) * 128, :], w=[sk])
        for g in range(2):
            for dup in range(2):
                p.ts(wk2[:, kc, g, 0, dup * 64:(dup + 1) * 64], stg[:, g * 64:(g + 1) * 64],
                     gs, None, ALU.mult, r=[sk, "gTk"], w=["wk2"], eng="pool")
                p.ts(wk2[:, kc, g, 1, dup * 64:dup * 64 + 32], stg[:, g * 64 + 32:g * 64 + 64],
                     gs, None, ALU.mult, r=[sk, "gTk"], w=["wk2"], eng="pool")
                p.ts(wk2[:, kc, g, 1, dup * 64 + 32:dup * 64 + 64], stg[:, g * 64:g * 64 + 32],
                     gs, None, ALU.mult, r=[sk, "gTk"], w=["wk2"], eng="pool")
        p.ts(wv[:, kc, :], stg[:, 128:256], gs, None, ALU.mult, r=[sk, "gTk"], w=["wv"], eng="pool")

    if fused and io.get("after_setup"):
        io["after_setup"]()
    scr = (f.sq[:], f.ss[:], f.sd[:], f.rs[:])
    t0 = 0
    first_real = n_skip_tiles
    for nt in st_sizes:
        n = nt * 128
        for j in range(nt):
            if fused:
                p.dma(f.h1[:, j, :], io["h_tile"](t0 + j), r=list(io["rkeys"]), w=[("h1", j)])
                if t0 + j < n_skip_tiles:
                    p.ts(f.h1[:, j, :], f.h1[:, j, :], flag_sb[:], None, ALU.mult,
                         r=[("h1", j), "flag"], w=[("h1", j)])
            else:
                p.dma(f.h1[:, j, :], h_in[(t0 + j) * 128:(t0 + j + 1) * 128, :], w=[("h1", j)])
        p.dma(cos_sb[:, 0:n], cos_t[:, t0 * 128:t0 * 128 + n], w=["cos"])
        p.dma(sin_sb[:, 0:n], sin_t[:, t0 * 128:t0 * 128 + n], w=["sin"])
        for j in range(nt):
            rmsnorm_tile(p, f.h1[:, j, :], None, f.hn[:], scr, [("h1", j)], ["hn"], "f")
            for kc in range(8):
                p.tr(f.psT[:, kc, :], f.hn[:, kc * 128:(kc + 1) * 128], f.ident[:],
                     r=["hn", "ident"], w=["psT"])
            p.cp(hnqT[:, :, j * 128:(j + 1) * 128], f.psT[:], r=["psT"], w=[("hnT", j)], eng="act")
            if fused:
                p.dma(hkv[:], io["hkv_tile"](t0 + j), r=list(io["rkeys"]), w=["hkv"])
                if t0 + j < n_skip_tiles:
                    p.ts(hkv[:], hkv[:], flag_sb[:], None, ALU.mult, r=["hkv", "flag"], w=["hkv"])
            else:
                p.dma(hkv[:], hkv_in[(t0 + j) * 128:(t0 + j + 1) * 128, :], w=["hkv"])
            rmsnorm_tile(p, hkv[:], None, f.hn[:], scr, ["hkv"], ["hn"], "f")
            for kc in range(8):
                p.tr(f.psT[:, kc, :], f.hn[:, kc * 128:(kc + 1) * 128], f.ident[:],
                     r=["hn", "ident"], w=["psT"])
            p.cp(hkvT[:, :, j * 128:(j + 1) * 128], f.psT[:], r=["psT"], w=[("hkvT", j)], eng="act")
        hq_keys = [("hnT", j) for j in range(nt)]
        hk_keys = [("hkvT", j) for j in range(nt)]

        def rope_out(dst, psn, pss, rk, wk):
            p.tt(f.t1[:, 0:n], psn, cos_sb[:, 0:n], ALU.mult, r=rk[0:1] + ["cos"], w=["t1"])
            p.tt(f.t2[:, 0:n], pss, sin_sb[:, 0:n], ALU.mult, r=rk[1:2] + ["sin"], w=["t2"])
            p.tt(dst, f.t1[:, 0:n], f.t2[:, 0:n], ALU.add, r=["t1", "t2"], w=wk)

        for g in range(2):
            par = f.nfc % 2
            f.nfc += 1
            bk = [("bank", 2 + 2 * par), ("bank", 3 + 2 * par)]
            for v in range(2):
                for kc in range(8):
                    p.mm(f.psU[par][v][:, 0:n], wk2[:, kc, g, v, :], hkvT[:, kc, 0:n],
                         start=(kc == 0), stop=(kc == 7), r=["wk2"] + hk_keys, w=[bk[v]])
            rope_out(KT2[:, g, 128:128 + n], f.psU[par][0][:, 0:n], f.psU[par][1][:, 0:n],
                     bk, [("KT2", g)])
        for j in range(nt):
            ps = f.psA[f.nA % 2]
            pk = ("bank", f.nA % 2)
            f.nA += 1
            for kc in range(8):
                p.mm(ps[:, 0:128], hkvT[:, kc, j * 128:(j + 1) * 128], wv[:, kc, :],
                     start=(kc == 0), stop=(kc == 7), r=[("hkvT", j), "wv"], w=[pk])
            p.cp(VA[:, j + 1, :, 0:64], ps[:, 0:128].rearrange("p (g d) -> p g d", g=2),
                 r=[pk], w=[("VA", j + 1)], eng="act")
        for hp in range(8):
            par = f.nfc % 2
            f.nfc += 1
            stg = f.fstage[par]
            wch = f.wch[par]
            p.dma(stg[:, 0:1024].rearrange("p (c f) -> p c f", c=8),
                  w_q[:, hp * 128:(hp + 1) * 128].rearrange("(c p) f -> p c f", p=128),
                  w=[("stg", "ffn", par, 0)])
            p.tt(wch[:, :, 0, :], stg[:, 0:1024].rearrange("p (c f) -> p c f", c=8),
                 gTq[:].unsqueeze(2).to_broadcast([128, 8, 128]), ALU.mult,
                 r=[("stg", "ffn", par, 0), "gTq"], w=[("wch", par, 0)], eng="pool")
            src = wch[:, :, 0, :].rearrange("p c (h d) -> p c h d", h=2)
            dsw = wch[:, :, 1, :].rearrange("p c (h d) -> p c h d", h=2)
            p.cp(dsw[:, :, :, 0:32], src[:, :, :, 32:64], r=[("wch", par, 0)],
                 w=[("wch", par, 1)], eng="pool")
            p.cp(dsw[:, :, :, 32:64], src[:, :, :, 0:32], r=[("wch", par, 0)],
                 w=[("wch", par, 1)], eng="pool")
            bk = [("bank", 2 + 2 * par), ("bank", 3 + 2 * par)]
            for v in range(2):
                for kc in range(8):
                    p.mm(f.psU[par][v][:, 0:n], wch[:, kc, v, :], hnqT[:, kc, 0:n],
                         start=(kc == 0), stop=(kc == 7),
                         r=[("wch", par, v)] + hq_keys, w=[bk[v]])
            rope_out(QT2[:, hp, 0:n], f.psU[par][0][:, 0:n], f.psU[par][1][:, 0:n],
                     bk, [("QT2", hp)])
        nS = 0
        pending = None
        for j in range(nt):
            gt = t0 + j
            for g in range(2):
                chunks = [(j, 0 if gt == first_real else 1), (j + 1, 2)]
                pts = []
                for ci, (slot, mi) in enumerate(chunks):
                    sp_ = nS % 2
                    ptb = nS % 4
                    nS += 1
                    for par in range(2):
                        pr = slice(par * 64, (par + 1) * 64)
                        p.mm(psS[sp_][:, par * 512:(par + 1) * 512],
                             KT2[pr, g, slot * 128:(slot + 1) * 128],
                             QT2[pr, 4 * g:4 * g + 4, j * 128:(j + 1) * 128],
                             start=True, stop=False,
                             r=[("KT2", g)] + [("QT2", 4 * g + i) for i in range(4)],
                             w=[psSk[sp_][par]])
                        p.mm(psS[sp_][:, par * 512:(par + 1) * 512], f.ident[:], msk[:, mi, :],
                             start=False, stop=True, r=["ident", "msk"], w=[psSk[sp_][par]])
                    p.act(PT[ptb][:], psS[sp_], AF.Exp, r=psSk[sp_], w=[("PT", ptb)], scale=0.125)
                    pts.append((ci, slot, ptb))

                def fin(j=j, g=g, pts=pts):
                    for ci, slot, ptb in pts:
                        for par in range(2):
                            p.mm(psO[0:65, par * 512:(par + 1) * 512], VA[:, slot, g, :],
                                 PT[ptb][:, par * 512:(par + 1) * 512],
                                 start=(ci == 0), stop=(ci == 1),
                                 r=[("PT", ptb), ("VA", slot), "VA"], w=[psOk[par]])
                    p.tt(zr[64:65, :].rearrange("p (h q) -> p h q", h=8),
                         psO[64:65, :].rearrange("p (h q) -> p h q", h=8),
                         sexp[64:65, g * 8:(g + 1) * 8].unsqueeze(2).to_broadcast([1, 8, 128]),
                         ALU.add, r=psOk + ["sexp"], w=["zr"])
                    p.recip(rz[64:65, :], zr[64:65, :], r=["zr"], w=["rz"])
                    p.cp(osb, psO[0:64, :], r=psOk, w=["hfin"], eng="act")
                    for par in range(2):
                        p.mm(psB[0:64, :], ones[64:65, :], rz[64:65, par * 512:(par + 1) * 512],
                             start=True, stop=True, r=["ones", "rz"], w=psBk)
                        dst = f.oT[:, g * 8 + par * 4:g * 8 + par * 4 + 4, j * 128:(j + 1) * 128]
                        p.tt(dst,
                             osb[:, par * 512:(par + 1) * 512].rearrange("p (h q) -> p h q", h=4),
                             psB[0:64, :].rearrange("p (h q) -> p h q", h=4), ALU.mult,
                             r=["hfin"] + psBk, w=["oT"])

                if pending is not None:
                    pending()
                pending = fin
        pending()
        for g in range(2):
            p.cp(KT2[:, g, 0:128], KT2[:, g, n:n + 128], r=[("KT2", g)], w=[("KT2", g)], eng="pool")
        p.cp(VA[:, 0, :, :], VA[:, nt, :, :], r=[("VA", nt)], w=[("VA", 0)], eng="pool")
        skip = max(0, min(nt, n_skip_tiles - t0))
        o0 = max(0, t0 - n_skip_tiles)
        dst = h_out[o0 * 128:(o0 + nt - skip) * 128, :] if skip < nt else None
        f.run_supertile(nt, None, "resident", dst, n_skip_out=skip, final_norm=final_norm,
                        h1_preloaded=True, dkey=(io["dkey"] if fused else None))
        t0 += nt
    if fused:
        return None
    return p.finish()


def rope_tables(pos):
    half = 32
    inv = (np.float32(10000.0) ** (-np.arange(half, dtype=np.float32) / half)).astype(np.float32)
    ang = pos.astype(np.float32)[None, :] * inv[:, None]
    cos = np.cos(ang).astype(np.float32)
    sin = np.sin(ang).astype(np.float32)
    cos64 = np.concatenate([cos, cos], 0)
    sin64 = np.concatenate([-sin, sin], 0)
    return (np.ascontiguousarray(np.concatenate([cos64, cos64], 0)),
            np.ascontiguousarray(np.concatenate([sin64, sin64], 0)))


def swa_masks(first_exists):
    i = np.arange(128)[:, None]
    q = np.arange(128)[None, :]
    prev = np.where(i > q, 0.0, MASKV).astype(np.float32)
    cur = np.where(i <= q, 0.0, MASKV).astype(np.float32)
    pf = prev if first_exists else np.full((128, 128), MASKV, np.float32)
    m = np.stack([np.tile(pf, (1, 4)), np.tile(prev, (1, 4)), np.tile(cur, (1, 4))], 0)
    return m.astype(NPBF16)


def sinks_row(sinks16):
    ho = c_head_order()
    return np.ascontiguousarray(np.asarray(sinks16, np.float32)[ho][None, :])


def gT_np(g):
    return np.ascontiguousarray(np.asarray(g, np.float32).reshape(8, 128).T)


FORCE = 1.0e6
TINY = 1.0e-30
C_BF = float(np.float32(NPBF16(-MASKV)))
LN_C = float(np.log(np.float64(C_BF)))
MUL_MASK = False
KEEP_ZB = False


def build_A(S, dbg=99, p=None, io=None):
    fused = p is not None
    if not fused:
        nc = bass.Bass("TRN2", target_bir_lowering=False)
        p = Prog(nc)
    nc = p.nc
    NST = S // 512
    NQB = S // 128
    NCC = max(1, S // 2048)
    if not fused:
        h_in = p.din("h_in", [S, D], F32)
    g_attn = p.din("g_attn", [128, 8], F32)
    wq_d = p.din("wq", [D, 256], F32)
    wk3_d = p.din("wk3", [D, 192], F32)
    wv3_d = p.din("wv3", [D, 192], F32)
    wg_d = p.din("wg", [D, 12], F32)
    w1_d = p.din("w1", [2, 2048, 256], F32)
    w2_d = p.din("w2", [2, 256, 64], F32)
    posT_d = p.din("posT", [64, 2, 32], F32)
    cos_d = p.din("cos_t", [128, S], F32)
    sin_d = p.din("sin_t", [128, S], F32)
    ccos_d = p.din("ccos_t", [128, NCC * 128], F32)
    csin_d = p.din("csin_t", [128, NCC * 128], F32)
    pmask_d = p.din("pmask", [2, 16, 128, 128], BF16)
    r0mask_d = p.din("r0mask", [128, 512], BF16)
    cmask_d = p.din("cmask", [2, 128, 512], BF16)
    emat_d = p.din("emat", [64, 128, 128], BF16)
    wfull_d = p.din("wfull", [NCC * 128, 257], BF16)
    fix_d = p.din("fix3", [128, 6], F32)
    ident_d = p.din("ident", [128, 128], BF16)
    gscr = [nc.dram_tensor(f"gscr{i}" + p.sfx, [1, 12 * 512], F32).ap() for i in range(2)]
    if not fused:
        oT_out = p.dout("oT_out", [64, 4, S], BF16)

    ident = p.sb("ident", [128, 128], BF16)
    gT = p.sb("gT", [128, 8], F32)
    fst = [p.sb(f"fst{i}", [128, 1024], F32) for i in range(2)]
    WQ = p.sb("WQ", [128, 8, 2, 256], BF16)
    WKS = p.sb("WKS", [128, 8, 2, 128], BF16)
    WKW = p.sb("WKW", [128, 8, 2, 128], BF16)
    WKC = p.sb("WKC", [128, 8, 64], BF16)
    WVC = p.sb("WVC", [128, 8, 64], BF16)
    WV2 = p.sb("WV2", [128, 8, 128], BF16)
    WG = p.sb("WG", [128, 8, 12], BF16)
    W1c = [p.sb(f"W1c{i}", [64, 4, 256], BF16) for i in range(3)]
    w1b = nc.dram_tensor("w1b" + p.sfx, [16 * 64, 1024], BF16).ap()
    W2K = p.sb("W2K", [128, 2, 2, 128], BF16)
    W2V = p.sb("W2V", [128, 2, 64], BF16)
    posT = p.sb("posT", [64, 2, 32], BF16)
    c1 = p.sb("c1", [128, 4], F32)
    ccos = p.sb("ccos", [128, NCC * 128], F32)
    csin = p.sb("csin", [128, NCC * 128], F32)
    pmask = p.sb("pmask", [128, 2, 16, 128], BF16)
    r0mask = p.sb("r0mask", [128, 512], BF16)
    cmask = p.sb("cmask", [128, 2, 512], BF16)
    emat = p.sb("emat", [128, 64, 128], BF16)
    wfull = p.sb("wfull", [128, NCC, 257], BF16)
    fix3 = p.sb("fix3", [128, 6], F32)
    hbuf = [p.sb("hbuf0", [128, 1024], F32)] * 2
    sq = p.sb("sq", [128, 1024], BF16)
    ss = p.sb("ss", [128, 1], F32)
    sd = p.sb("sd", [128, 1], F32)
    rs = p.sb("rs", [128, 1], F32)
    hn = p.sb("hn", [128, 1024], BF16)
    hnT = p.sb("hnT", [128, 8, 512], BF16)
    cos_sb = p.sb("cos_sb", [128, 512], F32)
    sin_sb = p.sb("sin_sb", [128, 512], F32)
    t1 = p.sb("t1", [128, 512], F32)
    t2 = p.sb("t2", [128, 512], F32)
    Qblk = p.sb("Qblk", [128, 4, 512], BF16)
    KsT2 = p.sb("KsT2", [128, S], BF16)
    VsA = p.sb("VsA", [128, NQB, 65], BF16)
    KwT2 = p.sb("KwT2", [128, 1024], BF16)
    VwA = p.sb("VwA", [128, 8, 65], BF16)
    KcT2 = p.sb("KcT2", [128, NCC * 128], BF16)
    VcA = p.sb("VcA", [128, NCC, 65], BF16)
    xT = [p.sb(f"xT{i}", [64, 528], BF16) for i in range(2)]
    hidK = p.sb("hidK", [128, 2, 32], BF16)
    hidV = p.sb("hidV", [128, 2, 128], BF16)
    gx = [p.sb(f"gx{i}", [128, 32], F32) for i in range(3)]
    gsb = p.sb("gsb", [12, 512], F32)
    G64b = [p.sb(f"G64b{i}", [128, 12 * 128], F32) for i in range(2)]
    PT = [p.sb(f"PT{i}", [128, 512], BF16) for i in range(4)]
    EX = [p.sb(f"EX{i}", [128, 512], BF16) for i in range(4)] if MUL_MASK else None
    Msb = [p.sb(f"Msb{i}", [128, 128], BF16) for i in range(2)] if MUL_MASK else None
    nM = [0]
    PcT = p.sb("PcT", [128, NCC, 512], BF16)
    zr = p.sb("zr", [128, 512], F32)
    Rr = p.sb("Rr", [128, 512], F32)
    ones = p.sb("ones", [128, 64], F32)
    osb = p.sb("osb", [64, 512], F32)
    acc = p.sb("acc", [64, 512], F32)
    tmpo = p.sb("tmpo", [64, 512], F32)
    oacc = p.sb("oacc", [64, 4, 128], BF16)
    imp = p.sb("imp", [128, 256], F32)
    selbuf = p.sb("selbuf", [128, 256], F32)
    work = p.sb("work", [128, 256], F32)
    mx8 = p.sb("mx8", [128, 8], F32)
    thr = p.sb("thr", [128, 1], F32)
    zq = p.sb("zq", [128, 1], F32)
    Bq = p.sb("Bq", [128, 256], BF16)
    BT = p.sb("BT", [128, 2, 512], BF16)
    psum = p.ps("psum", [128, 7 * 512])
    psT = p.ps("psT", [128, 8, 128], BF16)
    zero_b = p.sb("zero_b", [128, 512], BF16)
    p.memset(zero_b[:], 0.0, w=["zero_b"])
    p.memset(Qblk[:], 0.0, w=["QT2"])

    def bank(i, n=1):
        return psum[:, i * 512:(i + n) * 512]

    def bk(i):
        return ("bank", i)

    p.dma(ident[:], ident_d, w=["ident"])
    p.dma(gT[:], g_attn, w=["gT"])
    p.dma(ccos[:], ccos_d, w=["ccos"])
    p.dma(csin[:], csin_d, w=["csin"])
    for a_ in range(2):
        for r4 in range(0, 16, 4):
            p.dma(pmask[:, a_, r4:r4 + 4, :], pmask_d[a_, r4:r4 + 4].rearrange("r p c -> p r c"),
                  w=["pmask"])
    p.dma(r0mask[:], r0mask_d, w=["r0mask"])
    p.dma(cmask[:], cmask_d.rearrange("a p c -> p a c"), w=["cmask"])
    for e8 in range(0, 64, 8):
        p.dma(emat[:, e8:e8 + 8, :], emat_d[e8:e8 + 8].rearrange("e p c -> p e c"), w=["emat"])
    p.dma(wfull[:], wfull_d.rearrange("(c p) f -> p c f", p=128), w=["wfull"])
    p.dma(fix3[:], fix_d, w=["fix3"])
    p.memset(ones[:], 1.0, w=["ones"])
    p.memset(VsA[:], 1.0, w=["VsA"])
    p.memset(VwA[:], 1.0, w=["VwA"])
    p.memset(VcA[:], 1.0, w=["VcA"])
    p.memset(KwT2[:], 0.0, w=["KwT2"])
    p.memset(KcT2[:], 0.0, w=["KcT2"])
    p.memset(selbuf[:], -FORCE, w=["selbuf"])
    p.memset(hidV[:], 0.0, w=["hidV"])
    for i in range(2):
        p.memset(xT[i][:], 0.0, w=[("xT", i)])
    nst_ = [0]

    def stage_load(dst_fn, src_ap, ncols, parts=128):
        i = nst_[0] % 2
        nst_[0] += 1
        k = ("fst", i)
        p.dma(fst[i][0:parts, 0:ncols], src_ap, w=[k])
        return fst[i], k

    def swapcopy(dst, src, r, w):
        d4 = dst.rearrange("p (h d) -> p h d", d=64)
        s4 = src.rearrange("p (h d) -> p h d", d=64)
        p.cp(d4[:, :, 0:32], s4[:, :, 32:64], r=r, w=w, eng="pool")
        p.cp(d4[:, :, 32:64], s4[:, :, 0:32], r=r, w=w, eng="pool")

    for kc in range(8):
        gs = gT[:, kc:kc + 1]
        rows = slice(kc * 128, (kc + 1) * 128)
        st_, k = stage_load(None, wq_d[rows, :], 256)
        p.ts(WQ[:, kc, 0, :], st_[:, 0:256], gs, None, ALU.mult, r=[k, "gT"], w=["WQ"], eng="pool")
        swapcopy(WQ[:, kc, 1, :], WQ[:, kc, 0, :], ["WQ"], ["WQ"])
        st_, k = stage_load(None, wk3_d[rows, :], 192)
        p.ts(WKC[:, kc, :], st_[:, 0:64], gs, None, ALU.mult, r=[k, "gT"], w=["WKC"], eng="pool")
        for (W_, c0) in ((WKS, 64), (WKW, 128)):
            for dup in range(2):
                p.ts(W_[:, kc, 0, dup * 64:(dup + 1) * 64], st_[:, c0:c0 + 64], gs, None, ALU.mult,
                     r=[k, "gT"], w=["WK"], eng="pool")
            swapcopy(W_[:, kc, 1, :], W_[:, kc, 0, :], ["WK"], ["WK"])
        st_, k = stage_load(None, wv3_d[rows, :], 192)
        p.ts(WVC[:, kc, :], st_[:, 0:64], gs, None, ALU.mult, r=[k, "gT"], w=["WVC"], eng="pool")
        p.ts(WV2[:, kc, :], st_[:, 64:192], gs, None, ALU.mult, r=[k, "gT"], w=["WV2"], eng="pool")
        st_, k = stage_load(None, wg_d[rows, :], 12)
        p.ts(WG[:, kc, :], st_[:, 0:12], gs, None, ALU.mult, r=[k, "gT"], w=["WG"], eng="pool")
    nW1 = [0]

    for kv in range(2):
        for l0 in range(0, 32, 4):
            i = nst_[0] % 2
            nst_[0] += 1
            k = ("fst", i)
            pc = kv * 8 + l0 // 4
            p.dma(fst[i][0:64, :].rearrange("p (l m) -> p l m", l=4),
                  w1_d[kv, l0 * 64:(l0 + 4) * 64, :].rearrange("(l d) m -> d l m", d=64), w=[k])
            p.cp(W1c[i][:], fst[i][0:64, :].rearrange("p (l m) -> p l m", l=4),
                 r=[k], w=[("W1c", i)], eng="pool")
            p.dma(w1b[pc * 64:(pc + 1) * 64, :], W1c[i][:].rearrange("p l m -> p (l m)"),
                  r=[("W1c", i)], w=["w1b"], q="pool")

    def w1_piece(kv, l0):
        i = nW1[0] % 3
        nW1[0] += 1
        pc = kv * 8 + l0 // 4
        p.dma(W1c[i][:].rearrange("p l m -> p (l m)"), w1b[pc * 64:(pc + 1) * 64, :],
              r=["w1b"], w=[("W1c", i)])
        return W1c[i], ("W1c", i)

    for kv in range(2):
        for mt in range(2):
            st_, k = stage_load(None, w2_d[kv, mt * 128:(mt + 1) * 128, :], 64)
            if kv == 0:
                for dup in range(2):
                    p.cp(W2K[:, mt, 0, dup * 64:(dup + 1) * 64], st_[:, 0:64], r=[k], w=["W2K"],
                         eng="pool")
                swapcopy(W2K[:, mt, 1, :], W2K[:, mt, 0, :], ["W2K"], ["W2K"])
            else:
                p.cp(W2V[:, mt, :], st_[:, 0:64], r=[k], w=["W2V"], eng="pool")
    st_, k = stage_load(None, posT_d.rearrange("d a l -> d (a l)"), 64, parts=64)
    p.cp(posT[:].rearrange("d a l -> d (a l)"), st_[0:64, 0:64], r=[k], w=["posT"], eng="pool")
    for kv in range(2):
        for l0 in range(0, 32, 4):
            wt, wk_ = w1_piece(kv, l0)
            for mt in range(2):
                col = kv * 2 + mt
                bb = 1 if mt == 0 else 5
                for li in range(4):
                    l = l0 + li
                    p.mm(bank(bb)[:, col:col + 1], wt[:, li, mt * 128:(mt + 1) * 128],
                         posT[:, kv, l:l + 1], start=(l == 0), stop=(l == 31),
                         r=[wk_, "posT"], w=[bk(bb)])
    p.cp(c1[:, 0:1], bank(1)[:, 0:1], r=[bk(1)], w=["c1"], eng="act")
    p.cp(c1[:, 2:3], bank(1)[:, 2:3], r=[bk(1)], w=["c1"], eng="act")
    p.cp(c1[:, 1:2], bank(5)[:, 1:2], r=[bk(5)], w=["c1"], eng="act")
    p.cp(c1[:, 3:4], bank(5)[:, 3:4], r=[bk(5)], w=["c1"], eng="act")

    if dbg == 0:
        return p.finish()
    if fused:
        for cb in range(4):
            p.dma(io["o_zero"][:, cb, :], zero_b[0:64, 0:128], r=["zero_b"], w=[io["dkey"]],
                  q="pool")
    if fused and io.get("after_setup"):
        io["after_setup"]()
    scr = (sq[:], ss[:], sd[:], rs[:])

    def rope_out(dst, psn, pss, cs, sn, rk, wk, n):
        p.tt(t1[:, 0:n], psn, cs, ALU.mult, r=rk[0:1] + ["cos", "ccos"], w=["t1"])
        p.tt(t2[:, 0:n], pss, sn, ALU.mult, r=rk[1:2] + ["sin", "csin"], w=["t2"])
        if isinstance(dst, tuple):
            p.tt(dst[0], t1[0:64, 0:n], t2[0:64, 0:n], ALU.add, r=["t1", "t2"], w=wk)
            p.tt(dst[1], t1[64:128, 0:n], t2[64:128, 0:n], ALU.add, r=["t1", "t2"], w=wk)
        else:
            p.tt(dst, t1[:, 0:n], t2[:, 0:n], ALU.add, r=["t1", "t2"], w=wk)

    nU = [0]
    nH = [0]

    def proj_pair(W_, dst, cs, sn, wkey, rkey):
        par = nU[0] % 2
        nU[0] += 1
        b0, b1 = 2 + 2 * par, 3 + 2 * par
        for v, b in ((0, b0), (1, b1)):
            for kc in range(8):
                p.mm(bank(b), W_(kc, v), hnT[:, kc, :], start=(kc == 0), stop=(kc == 7),
                     r=[rkey, "hnT"], w=[bk(b)])
        rope_out(dst, bank(b0), bank(b1), cs, sn, [bk(b0), bk(b1)], wkey, 512)

    def gelu_to(dst, ps_ap, bias_ap, rk, wk):
        x, a, b = gx[0][:], gx[1][:], gx[2][:]
        p.act(x, ps_ap, AF.Identity, r=rk + ["c1"], w=["gx0"], bias=bias_ap)
        p.tt(a, x, x, ALU.mult, r=["gx0"], w=["gx1"])
        p.ts(a, a, 0.044715, 1.0, ALU.mult, ALU.add, r=["gx1"], w=["gx1"])
        p.tt(a, a, x, ALU.mult, r=["gx1", "gx0"], w=["gx1"])
        p.act(b, a, AF.Sigmoid, r=["gx1"], w=["gx2"], scale=1.5957691216057308)
        p.tt(dst, x, b, ALU.mult, r=["gx0", "gx2"], w=wk)

    nS = [0]
    sdepth = [2]
    ZB = [False]

    def attn_chunk(kT2, kcols, vaug, biases, first, last, n_extra_r):
        sp_ = nS[0] % sdepth[0]
        nS[0] += 1
        sb_ = 2 + sp_
        if ZERO_BIAS and len(biases) == 0:
            if ZB[0]:
                biases = [(ident[:], zero_b[:], ["ident", "zero_b"])]
            else:
                biases = [(ident[:], zero_b[:, 0:32], ["ident", "zero_b"], "small")]
        out = bank(sb_)
        p.mm(out, kT2[:, kcols], Qblk[:, :, qsl[0]], start=True, stop=(len(biases) == 0),
             r=n_extra_r + ["QT2"], w=[bk(sb_)])
        for bi, bias in enumerate(biases):
            lh, rh, rk = bias[0:3]
            if len(bias) == 4 and bias[3] == "small":
                p.mm(out[:, 0:32], lh, rh, start=False, stop=(bi == len(biases) - 1), r=rk,
                     w=[bk(sb_)])
            elif len(bias) == 4:
                for hh in range(4):
                    p.mm(out[:, hh * 128:(hh + 1) * 128], lh, rh, start=False,
                         stop=(bi == len(biases) - 1), r=rk, w=[bk(sb_)])
            else:
                p.mm(out, lh, rh, start=False, stop=(bi == len(biases) - 1), r=rk, w=[bk(sb_)])
        return sp_, sb_

    qsl = [None]
    for st in range(NST):
        tok0 = st * 512
        for j in range(4):
            hb = hbuf[j % 2]
            hk = ("hbuf", 0)
            if fused:
                r0 = io["h_row"](tok0 + j * 128)
                p.dma(hb[:], io["h_ap"][r0:r0 + 128, :], r=list(io["rkeys"]), w=[hk])
            else:
                p.dma(hb[:], h_in[tok0 + j * 128:tok0 + (j + 1) * 128, :], w=[hk])
            rmsnorm_tile(p, hb[:], None, hn[:], scr, [hk], ["hn"], "a")
            for kc in range(8):
                p.tr(psT[:, kc, :], hn[:, kc * 128:(kc + 1) * 128], ident[:],
                     r=["hn", "ident"], w=["psT"])
            p.cp(hnT[:, :, j * 128:(j + 1) * 128], psT[:], r=["psT"], w=["hnT"], eng="act")
        p.dma(cos_sb[:], cos_d[:, tok0:tok0 + 512], w=["cos"])
        p.dma(sin_sb[:], sin_d[:, tok0:tok0 + 512], w=["sin"])
        for hp in range(2):
            proj_pair(lambda kc, v, hp=hp: WQ[:, kc, v, hp * 128:(hp + 1) * 128],
                      (Qblk[0:64, hp, :], Qblk[64:128, 2 + hp, :]), cos_sb[:], sin_sb[:],
                      ["QT2"], "WQ")
        proj_pair(lambda kc, v: WKS[:, kc, v, :], KsT2[:, tok0:tok0 + 512], cos_sb[:], sin_sb[:],
                  ["KsT2"], "WK")
        proj_pair(lambda kc, v: WKW[:, kc, v, :], KwT2[:, 512:1024], cos_sb[:], sin_sb[:],
                  ["KwT2"], "WK")
        for i, W_ in enumerate((WKC, WVC)):
            for kc in range(8):
                p.mm(bank(1)[0:64, :], W_[:, kc, :], hnT[:, kc, :], start=(kc == 0), stop=(kc == 7),
                     r=["WKC", "WVC", "hnT"], w=[bk(1)])
            p.cp(xT[i][:, 16:528], bank(1)[0:64, :], r=[bk(1)], w=[("xT", i)], eng="act")
        for j in range(4):
            for kc in range(8):
                p.mm(bank(0)[:, 0:128], hnT[:, kc, j * 128:(j + 1) * 128], WV2[:, kc, :],
                     start=(kc == 0), stop=(kc == 7), r=["hnT", "WV2"], w=[bk(0)])
            p.cp(VsA[:, st * 4 + j, 0:64], bank(0)[:, 0:64], r=[bk(0), "VsA"], w=["VsA"], eng="act")
            p.cp(VwA[:, 4 + j, 0:64], bank(0)[:, 64:128], r=[bk(0), "VwA"], w=["VwA"], eng="act")
        for kc in range(8):
            p.mm(bank(1)[0:12, :], WG[:, kc, :], hnT[:, kc, :], start=(kc == 0), stop=(kc == 7),
                 r=["WG", "hnT"], w=[bk(1)])
        p.act(gsb[:], bank(1)[0:12, :], AF.Sigmoid, r=[bk(1)], w=["gsb"])
        p.dma(gscr[st % 2].rearrange("o (a b) -> (o a) b", a=12), gsb[:], r=["gsb"],
              w=[("gscr", st % 2)])
        if dbg == 1:
            return p.finish()
        for kv in range(2):
            x3 = xT[kv][:].rearrange("p (i s) -> p i s", s=16)
            for l0 in range(0, 32, 4):
                wt, wk_ = w1_piece(kv, l0)
                for mt in range(2):
                    bb = 1 if mt == 0 else 5
                    for li in range(4):
                        l = l0 + li
                        rhs = x3[:, 0:32, l] if l < 16 else x3[:, 1:33, l - 16]
                        p.mm(bank(bb)[:, 0:32], wt[:, li, mt * 128:(mt + 1) * 128], rhs,
                             start=(l == 0), stop=(l == 31), r=[wk_, ("xT", kv)], w=[bk(bb)])
            for mt in range(2):
                bb = 1 if mt == 0 else 5
                if kv == 0:
                    gelu_to(hidK[:, mt, :], bank(bb)[:, 0:32], c1[:, mt:mt + 1], [bk(bb)], ["hidK"])
                else:
                    if st % 4 == 0 and mt == 0:
                        p.memset(hidV[:], 0.0, w=["hidV"])
                    gelu_to(hidV[:, mt, (st % 4) * 32:(st % 4) * 32 + 32], bank(bb)[:, 0:32],
                            c1[:, 2 + mt:3 + mt], [bk(bb)], ["hidV"])
            if kv == 0:
                par = nU[0] % 2
                nU[0] += 1
                b0, b1 = 2 + 2 * par, 3 + 2 * par
                for v, b in ((0, b0), (1, b1)):
                    for mt in range(2):
                        p.mm(bank(b)[:, 0:32], W2K[:, mt, v, :], hidK[:, mt, :],
                             start=(mt == 0), stop=(mt == 1), r=["W2K", "hidK"], w=[bk(b)])
                sl = slice(st * 32, st * 32 + 32)
                rope_out(KcT2[:, sl], bank(b0)[:, 0:32], bank(b1)[:, 0:32], ccos[:, sl], csin[:, sl],
                         [bk(b0), bk(b1)], ["KcT2"], 32)
            else:
                for mt in range(2):
                    p.mm(bank(1)[:, 0:64], hidV[:, mt, :], W2V[:, mt, :],
                         start=(mt == 0), stop=(mt == 1), r=["W2V", "hidV"], w=[bk(1)])
                p.cp(VcA[:, st // 4, 0:64], bank(1)[:, 0:64], r=[bk(1), "VcA"], w=["VcA"], eng="act")
            p.cp(xT[kv][:, 0:16], xT[kv][:, 512:528], r=[("xT", kv)], w=[("xT", kv)], eng="pool")
        if dbg == 2:
            return p.finish()
        for j in range(4):
            qb = st * 4 + j
            qsl[0] = slice(j * 128, (j + 1) * 128)
            tsl = qsl[0]
            p.dma(G64b[j % 2][64:65, :].rearrange("p (a b) -> p a b", a=12),
                  gscr[st % 2].rearrange("o (a b) -> o a b", a=12)[:, :, tsl],
                  r=[("gscr", st % 2)], w=[("G64", j % 2)])

            def finish_branch(br, first):
                p.ts(zr[64:65, :], bank(0)[64:65, :], TINY, None, ALU.max, r=[bk(0)], w=["zr"])
                p.recip(zr[64:65, :], zr[64:65, :], r=["zr"], w=["zr"])
                g3 = G64b[j % 2][64:65, :].rearrange("p (h b t) -> p h b t", h=4, b=3)
                for par in range(2):
                    for hpl in range(2):
                        hl = 2 * hpl + par
                        c0 = (par * 2 + hpl) * 128
                        p.tt(Rr[64:65, c0:c0 + 128], zr[64:65, c0:c0 + 128], g3[:, hl, br, :],
                             ALU.mult, r=["zr", ("G64", j % 2)], w=["Rr"])
                p.cp(osb[:], bank(0)[0:64, :], r=[bk(0)], w=["osb"], eng="act")

                def part2(first=first):
                    p.mm(bank(1)[0:64, :], ones[64:65, :], Rr[64:65, :], start=True, stop=True,
                         r=["ones", "Rr"], w=[bk(1)])
                    if first:
                        p.tt(acc[:], osb[:], bank(1)[0:64, :], ALU.mult, r=["osb", bk(1)], w=["acc"])
                    else:
                        p.tt(tmpo[:], osb[:], bank(1)[0:64, :], ALU.mult, r=["osb", bk(1)],
                             w=["tmpo"])
                        p.tt(acc[:], acc[:], tmpo[:], ALU.add, r=["tmpo", "acc"], w=["acc"])
                return part2

            pend = []
            pdepth = [1]

            def pend_push(fn):
                pend.append(fn)
                while len(pend) > pdepth[0]:
                    pend.pop(0)()

            def pend_flush():
                while pend:
                    pend.pop(0)()

            ncc = qb // 16 + 1
            r_ = qb % 16
            for cc in range(ncc):
                biases = []
                lastc = (cc == ncc - 1)
                if lastc:
                    biases.append((ident[:], pmask[:, 1 if cc == 0 else 0, r_, :], ["ident", "pmask"], 128))
                elif cc == 0:
                    biases.append((ident[:], r0mask[:], ["ident", "r0mask"]))
                sp_, sb_ = attn_chunk(KcT2, slice(cc * 128, (cc + 1) * 128), None, biases,
                                      cc == 0, lastc, ["KcT2"])
                p.act(PcT[:, cc, :], bank(sb_), AF.Exp, r=[bk(sb_)], w=[("PcT", cc)], scale=0.125)
                pend_push(lambda cc=cc, lastc=lastc: p.mm(
                    bank(0)[0:65, :], VcA[:, cc, :], PcT[:, cc, :], start=(cc == 0), stop=lastc,
                    r=[("PcT", cc), "VcA"], w=[bk(0)]))
            pend_flush()
            for par in range(2):
                for hpl in range(2):
                    hi = par * 2 + hpl
                    c0 = hi * 128
                    ib = 4 + (hi % 2)
                    for cc in range(ncc):
                        p.mm(bank(ib)[:, 0:257], PcT[:, cc, c0:c0 + 128], wfull[:, cc, :],
                             start=(cc == 0), stop=(cc == ncc - 1),
                             r=[("PcT", cc), "wfull"], w=[bk(ib)])
                    p.ts(zq[:], bank(ib)[:, 256:257], TINY, None, ALU.max, r=[bk(ib)], w=["zq"])
                    p.recip(zq[:], zq[:], r=["zq"], w=["zq"])
                    if hi == 0:
                        p.ts(imp[:], bank(ib)[:, 0:256], zq[:], None, ALU.mult, r=[bk(ib), "zq"],
                             w=["imp"])
                    else:
                        p.stt(imp[:], bank(ib)[:, 0:256], zq[:], imp[:], ALU.mult, ALU.add,
                              r=[bk(ib), "zq", "imp"], w=["imp"])
            fin0 = finish_branch(0, True)
            if dbg == 3 or dbg == 100 + j * 10 + 3:
                return p.finish()
            nb = 2 * qb + 2
            p.cp(selbuf[:, 0:nb], imp[:, 0:nb], r=["imp"], w=["selbuf"])
            lo = 2 * qb - 1
            k0 = 0
            if lo < 0:
                lo, k0 = 0, 1
            nfx = 3 - k0
            p.tt(selbuf[:, lo:lo + nfx], selbuf[:, lo:lo + nfx], fix3[:, k0:3], ALU.mult,
                 r=["selbuf", "fix3"], w=["selbuf"])
            p.tt(selbuf[:, lo:lo + nfx], selbuf[:, lo:lo + nfx], fix3[:, 3 + k0:6], ALU.add,
                 r=["selbuf", "fix3"], w=["selbuf"])
            p.memset(selbuf[:, 0:1], 3.0 * FORCE, w=["selbuf"])
            p.s.op("dve", lambda e: e.max(out=mx8[:], in_=selbuf[:]), ["selbuf"], ["mx8"])
            p.s.op("dve", lambda e: e.match_replace(out=work[:], in_to_replace=mx8[:],
                                                    in_values=selbuf[:], imm_value=-2.0 * FORCE),
                   ["selbuf", "mx8"], ["work"])
            p.s.op("dve", lambda e: e.max(out=mx8[:], in_=work[:]), ["work"], ["mx8"])
            p.s.op("dve", lambda e: e.tensor_reduce(out=thr[:], in_=mx8[:], axis=AX.X, op=ALU.min),
                   ["mx8"], ["thr"])
            p.ts(Bq[:], selbuf[:], thr[:], 1.0, ALU.is_ge, ALU.subtract, r=["selbuf", "thr"], w=["Bq"])
            nhalf = 1 if nb <= 128 else 2
            for hf in range(nhalf):
                p.tr(psT[:, hf, :], Bq[:, hf * 128:(hf + 1) * 128], ident[:], r=["Bq", "ident"],
                     w=["psT"])
            for hf in range(nhalf):
                for rep in range(4):
                    p.cp(BT[:, hf, rep * 128:(rep + 1) * 128], psT[:, hf, :], r=["psT"], w=["BT"],
                         eng=("act" if rep % 2 == 0 else "dve"))
            if dbg == 5 or dbg == 100 + j * 10 + 5:
                return p.finish()
            k_lo = max(0, qb - 4)
            sdepth[0] = 4
            pdepth[0] = 2
            for kc in range(k_lo, qb + 1):
                biases = []
                if kc == qb - 4:
                    biases.append((ident[:], cmask[:, 1, :], ["ident", "cmask"]))
                if kc == qb:
                    biases.append((ident[:], cmask[:, 0, :], ["ident", "cmask"]))
                slot = 4 + j - (qb - kc)
                sp_, sb_ = attn_chunk(KwT2, slice(slot * 128, (slot + 1) * 128), None, biases,
                                      kc == k_lo, kc == qb, ["KwT2"])
                p.act(PT[sp_][:], bank(sb_), AF.Exp, r=[bk(sb_)], w=[("PT", sp_)], scale=0.125)
                if kc == min(k_lo + 1, qb) and fin0 is not None:
                    fin0()
                    fin0 = None
                pend_push(lambda kc=kc, sp_=sp_, slot=slot: p.mm(
                    bank(0)[0:65, :], VwA[:, slot, :], PT[sp_][:], start=(kc == k_lo),
                    stop=(kc == qb), r=[("PT", sp_), "VwA"], w=[bk(0)]))
            pend_flush()
            fin2 = finish_branch(2, False)
            if fin0 is not None:
                fin0()
                fin0 = None
            if dbg == 4 or dbg == 100 + j * 10 + 4:
                return p.finish()
            sdepth[0] = 3 if MUL_MASK else 4
            pdepth[0] = 2
            for kc in range(qb + 1):
                mulmask = MUL_MASK and kc != qb
                if mulmask:
                    zb = ZB[0]
                    ZB[0] = KEEP_ZB
                    sp_, sb_ = attn_chunk(KsT2, slice(kc * 128, (kc + 1) * 128), None, [],
                                          kc == 0, kc == qb, ["KsT2"])
                    ZB[0] = zb
                    mpar = nM[0] % 2
                    nM[0] += 1
                    mb = 5 + mpar
                    p.mm(bank(mb)[:, 0:128], emat[:, kc % 64, :], BT[:, kc // 64, 0:128],
                         start=True, stop=True, r=["emat", "BT"], w=[bk(mb)])
                    p.act(EX[sp_][:], bank(sb_), AF.Exp, r=[bk(sb_)], w=[("EX", sp_)], scale=0.125,
                          bias=-LN_C)
                    p.act(Msb[mpar][:], bank(mb)[:, 0:128], AF.Identity, r=[bk(mb)],
                          w=[("Msb", mpar)], bias=C_BF)
                    p.tt(PT[sp_][:].rearrange("p (h q) -> p h q", h=4),
                         EX[sp_][:].rearrange("p (h q) -> p h q", h=4),
                         Msb[mpar][:].unsqueeze(1).to_broadcast([128, 4, 128]), ALU.mult,
                         r=[("Msb", mpar), ("EX", sp_)], w=[("PT", sp_)])
                else:
                    biases = [(emat[:, kc % 64, :], BT[:, kc // 64, :], ["emat", "BT"])]
                    if kc == qb:
                        biases.append((ident[:], cmask[:, 0, :], ["ident", "cmask"]))
                    sp_, sb_ = attn_chunk(KsT2, slice(kc * 128, (kc + 1) * 128), None, biases,
                                          kc == 0, kc == qb, ["KsT2"])
                    p.act(PT[sp_][:], bank(sb_), AF.Exp, r=[bk(sb_)], w=[("PT", sp_)], scale=0.125)
                if kc == min(1, qb) and fin2 is not None:
                    fin2()
                    fin2 = None
                pend_push(lambda kc=kc, sp_=sp_: p.mm(
                    bank(0)[0:65, :], VsA[:, kc, :], PT[sp_][:], start=(kc == 0), stop=(kc == qb),
                    r=[("PT", sp_), "VsA"], w=[bk(0)]))
            pend_flush()
            fin1 = finish_branch(1, False)
            fin1()
            sdepth[0] = 2
            if dbg == 6 or dbg == 100 + j * 10 + 6:
                return p.finish()
            p.cp(oacc[:], acc[:].rearrange("p (c q) -> p c q", c=4), r=["acc"], w=["oacc"])
            if fused:
                for dst in io["o_dst"](qb):
                    p.dma(dst, oacc[:], r=["oacc"], w=[io["dkey"]], q="pool")
            else:
                p.dma(oT_out[:, :, tok0 + j * 128:tok0 + (j + 1) * 128], oacc[:], r=["oacc"],
                      q="pool", is_output=True)
            if dbg == 100 + j * 10 + 7:
                return p.finish()
        p.cp(KwT2[:, 0:512], KwT2[:, 512:1024], r=["KwT2"], w=["KwT2"], eng="pool")
        p.cp(VwA[:, 0:4, :], VwA[:, 4:8, :], r=["VwA"], w=["VwA"], eng="pool")
        if dbg == 7 + st:
            return p.finish()
    if fused:
        return None
    return p.finish()


def nsa_consts(S):
    NCC = max(1, S // 2048)
    ml = np.arange(128)[:, None]
    q = np.arange(128)[None, :]
    pm = np.zeros((2, 16, 128, 128), np.float32)
    for a in range(2):
        for r in range(16):
            valid = (16 * ml + 15 <= 128 * r + q)
            if a == 1:
                valid = valid & (ml >= 1)
            pm[a, r] = np.where(valid, 0.0, MASKV)
    pmask = pm.astype(NPBF16)
    r0 = np.zeros((128, 512), np.float32)
    r0[0, :] = MASKV
    cur = np.where(ml <= q, 0.0, MASKV).astype(np.float32)
    upper = np.where(ml > q, 0.0, MASKV).astype(np.float32)
    cmask = np.stack([np.tile(cur, (1, 4)), np.tile(upper, (1, 4))], 0).astype(NPBF16)
    emat = np.zeros((64, 128, 128), np.float32)
    for e in range(64):
        emat[e, 2 * e, 0:64] = -MASKV
        emat[e, 2 * e + 1, 64:128] = -MASKV
    ws = [1, 2, 2, 2, 1]
    wfull = np.zeros((NCC * 128, 257), np.float32)
    for m in range(1, NCC * 128):
        n = m - 1
        for j in range(256):
            i = n - 4 * j + 1
            if 0 <= i <= 4:
                wfull[m, j] = ws[i]
    wfull[:, 256] = 1.0
    fix = np.zeros((128, 6), np.float32)
    lo = np.arange(128) < 64
    fix[:, 0] = np.where(lo, 0.0, 1.0)
    fix[:, 3] = np.where(lo, FORCE, 0.0)
    fix[:, 4] = 2.0 * FORCE
    fix[:, 5] = np.where(lo, -FORCE, FORCE)
    cpos = 16 * np.arange(NCC * 128) + 15
    ccos, csin = rope_tables(cpos)
    return dict(pmask=pmask, r0mask=r0.astype(NPBF16), cmask=cmask, emat=emat.astype(NPBF16),
                wfull=wfull.astype(NPBF16), fix3=fix, ccos_t=ccos, csin_t=csin, ident=ident_np())


def nsa_weights(a_w_in_l, cmp_pos_l, g):
    W = a_w_in_l
    q0 = g * 256
    def kcol(i):
        return W[:, 1024 + i * 256 + g * 64: 1024 + i * 256 + (g + 1) * 64]
    kc_, vc_, ks_, vs_, kw_, vw_ = [kcol(i) for i in range(6)]
    wg = W[:, 1024 + 6 * 256 + g * 12: 1024 + 6 * 256 + (g + 1) * 12]
    return dict(wq=np.ascontiguousarray(W[:, q0:q0 + 256]),
                wk3=np.ascontiguousarray(np.concatenate([kc_, ks_, kw_], 1)),
                wv3=np.ascontiguousarray(np.concatenate([vc_, vs_, vw_], 1)),
                wg=np.ascontiguousarray(wg),
                posT=np.ascontiguousarray(np.transpose(cmp_pos_l, (2, 0, 1))))


SEQ = 16384
NB = 2
CH = 4096
NPHASE = 99


def _run(nc, in_maps):
    res = run_bass_kernel_spmd(nc, in_maps, core_ids=list(range(8)))
    return res.results


def _cwb(conv_w, conv_b):
    return np.ascontiguousarray(np.concatenate([conv_w, conv_b[None]], 0).T.astype(np.float32))


def _chunk_with_halo(x_b, c, halo):
    lo = c * CH - halo
    if lo >= 0:
        return np.ascontiguousarray(x_b[lo:(c + 1) * CH])
    pad = np.zeros((-lo,) + x_b.shape[1:], x_b.dtype)
    return np.ascontiguousarray(np.concatenate([pad, x_b[0:(c + 1) * CH]], 0))


def kernel_unfused(x, norm_attn, norm_ffn, a_w_in, a_cmp_pos, a_cmp_w1, a_cmp_w2, a_w_out, kv_norm,
           b_w_kv, b_w_q, b_sinks, b_w_out, ffn_w_in, ffn_conv_w, ffn_conv_b, ffn_w_out,
           final_norm):
    f32 = lambda a: np.ascontiguousarray(np.asarray(a, dtype=np.float32))
    x = f32(x)
    norm_attn, norm_ffn = f32(norm_attn), f32(norm_ffn)
    a_w_in, a_cmp_pos, a_cmp_w1, a_cmp_w2, a_w_out = map(f32, (a_w_in, a_cmp_pos, a_cmp_w1,
                                                                a_cmp_w2, a_w_out))
    kv_norm, b_w_kv, b_w_q, b_sinks, b_w_out = map(f32, (kv_norm, b_w_kv, b_w_q, b_sinks, b_w_out))
    ffn_w_in, ffn_conv_w, ffn_conv_b, ffn_w_out, final_norm = map(
        f32, (ffn_w_in, ffn_conv_w, ffn_conv_b, ffn_w_out, final_norm))
    h = x
    ident = ident_np()
    gfin = np.ascontiguousarray(final_norm[None, :])
    cosA, sinA = rope_tables(np.arange(SEQ))
    constsA = nsa_consts(SEQ)

    for l in range(2):
        ncA = build_A(SEQ)
        maps = []
        for i in range(8):
            b, g = divmod(i, 4)
            m = dict(h_in=h[b], g_attn=gT_np(norm_attn[l]), w1=a_cmp_w1[l], w2=a_cmp_w2[l],
                     cos_t=cosA, sin_t=sinA)
            m.update(constsA)
            m.update(nsa_weights(a_w_in[l], a_cmp_pos[l], g))
            maps.append(m)
        resA = _run(ncA, maps)
        oT_full = np.zeros((NB, 16, 64, SEQ), NPBF16)
        for i in range(8):
            b, g = divmod(i, 4)
            o = resA[i]["oT_out"]
            for par in range(2):
                for hpl in range(2):
                    oT_full[b, g * 4 + 2 * hpl + par] = o[:, par * 2 + hpl, :]
        oT_full = oT_full.reshape(NB, 1024, SEQ)
        ncB = build_B([1] + [4] * 8, 1)
        maps = []
        for i in range(8):
            b, c = divmod(i, 4)
            maps.append(dict(
                h_in=_chunk_with_halo(h[b], c, 128),
                oT_in=np.ascontiguousarray(_chunk_with_halo(oT_full[b].T, c, 128).T),
                w_o=a_w_out[l], w_in=ffn_w_in[l], w_out=ffn_w_out[l],
                cwb=_cwb(ffn_conv_w[l], ffn_conv_b[l]), g_ffn=gT_np(norm_ffn[l]), g_fin=gfin,
                ident=ident))
        resB = _run(ncB, maps)
        h = np.stack([np.concatenate([resB[b * 4 + c]["h_out"] for c in range(4)], 0)
                      for b in range(NB)], 0)

    hkv = h
    for l in range(2, 4):
        j = l - 2
        ncC = build_C([2] + [4] * 8, 2, final_norm=(l == 3))
        maps = []
        for i in range(8):
            b, c = divmod(i, 4)
            pos = c * CH - 256 + np.arange(CH + 256)
            cos_t, sin_t = rope_tables(pos)
            maps.append(dict(
                h_in=_chunk_with_halo(h[b], c, 256), hkv_in=_chunk_with_halo(hkv[b], c, 256),
                w_q=b_w_q[j], w_kv=b_w_kv, sinks_b=sinks_row(b_sinks[j]),
                g_attn=gT_np(norm_attn[l]), g_kv=gT_np(kv_norm), cos_t=cos_t, sin_t=sin_t,
                masks=swa_masks(c > 0), w_o=b_w_out[j], w_in=ffn_w_in[l], w_out=ffn_w_out[l],
                cwb=_cwb(ffn_conv_w[l], ffn_conv_b[l]), g_ffn=gT_np(norm_ffn[l]), g_fin=gfin,
                ident=ident))
        resC = _run(ncC, maps)
        h = np.stack([np.concatenate([resC[b * 4 + c]["h_out"] for c in range(4)], 0)
                      for b in range(NB)], 0)
    return np.ascontiguousarray(h.astype(np.float32))


def build_fused(nphase=99):
    from concourse.bass import ds
    nph = [0]

    def stop():
        nph[0] += 1
        return nph[0] >= nphase

    nc = bass.Bass("TRN2", target_bir_lowering=False)
    p = Prog(nc)
    S = SEQ
    WB = 128 + CH
    WC = 256 + CH
    SUBW = 11 * 128
    xA = p.din("xA", [S, D], F32)
    xB = p.din("xB", [WB, D], F32)
    flag = p.din("flag", [128, 1], F32)
    out = p.dout("out", [CH, D], F32)
    oTloc = [nc.dram_tensor(f"oTloc{l}", [12 * 64, 4 * SUBW], BF16) for l in range(2)]
    OTb = [nc.dram_tensor(f"OTb{l}", [12 * 256, 4 * SUBW], BF16) for l in range(2)]
    oTwin = nc.dram_tensor("oTwin", [3 * 256, 4 * SUBW], BF16).ap()
    hloc = [nc.dram_tensor(f"hloc{k}", [CH, D], F32) for k in range(3)]
    Hb = [nc.dram_tensor(f"Hb{k}", [S, D], F32) for k in range(3)]
    hwin = nc.dram_tensor("hwin", [WC, D], F32).ap()
    hkvwin = nc.dram_tensor("hkvwin", [WC, D], F32).ap()
    rg = [[0, 1, 2, 3], [4, 5, 6, 7]]
    PID = p.s.pid
    Hh = [nc.dram_tensor(f"Hh{k}", [4 * 256, D], F32) for k in (1, 2)]
    halowin = [nc.dram_tensor(f"halowin{k}", [256, D], F32).ap() for k in (1, 2)]

    def gather_group(src, dst, nchunk, rows, rk, wk):
        def fn(e, sem):
            for k in range(nchunk):
                e.collective_compute(
                    "AllGather", ALU.bypass, replica_groups=rg,
                    ins=[src.ap()[k * rows:(k + 1) * rows, :].opt()],
                    outs=[dst.ap()[k * 4 * rows:(k + 1) * 4 * rows, :].opt()]).then_inc(sem)
        p.s.cc(fn, [rk], [wk], n=nchunk)

    def h_row(tok):
        rank, rem = divmod(tok, CH)
        k, r = divmod(rem, 256)
        return (k * 4 + rank) * 256 + r

    def win_copy(dst, src, halo, q, rk, wk):
        s5 = src.rearrange("(k g r e) d -> k g r (e d)", k=16, g=4, e=8)
        dm = dst[halo:halo + CH, :].rearrange("(k g r e) d -> k g r (e d)", k=16, g=1, e=8)
        dh = dst[0:halo, :].rearrange("(k g r e) d -> k g r (e d)", k=1, g=1, e=8)
        h8 = halo // 8
        p.dmaf(lambda e: e.dma_start(
            out=dm, in_=s5[:, ds(PID(e, "c", lambda pid: pid % 4), 1), :, :]),
            r=[rk], w=[wk], q=q)
        p.dmaf(lambda e: e.dma_start(
            out=dh, in_=s5[15:16, ds(PID(e, "cm1", lambda pid: (pid + 3) % 4), 1), 32 - h8:32, :]),
            r=[rk], w=[wk], q=q)

    for l in range(2):
        p.sfx = f"_A{l}"
        rk = [] if l == 0 else [f"Hb{l - 1}"]
        O5 = oTloc[l].ap().rearrange("(c s d) (b t) -> c s d b t", c=4, s=3, b=4)

        def o_dst(qb, O5=O5):
            c, sl = divmod(qb, 32)
            sl += 1
            dsts = [O5[c, sl // 11, :, :, (sl % 11) * 128:(sl % 11) * 128 + 128]]
            if sl == 32 and c < 3:
                dsts.append(O5[c + 1, 0, :, :, 0:128])
            return dsts

        build_A(S, p=p, io=dict(h_ap=(xA if l == 0 else Hb[l - 1].ap()),
                                h_row=((lambda t: t) if l == 0 else h_row),
                                rkeys=rk, dkey=f"oTloc{l}", o_dst=o_dst,
                                o_zero=O5[0, 0, :, :, 0:128], after_setup=p.s.cc_wait))
        p.phase_end()
        gather_group(oTloc[l], OTb[l], 12, 64, f"oTloc{l}", f"OTb{l}")
        if stop():
            return p.finish(), dict(p.dins)
        p.sfx = f"_B{l}"
        O3 = OTb[l].ap().rearrange("(c r) f -> c r f", c=4)

        def after_b(l=l, O3=O3):
            p.s.cc_wait()
            p.dmaf(lambda e: e.dma_start(
                out=oTwin.rearrange("(c r) f -> c r f", c=1),
                in_=O3[ds(PID(e, "c", lambda pid: pid % 4), 1), :, :]),
                r=[f"OTb{l}"], w=["oTwin"], q="act")
            if l > 0:
                win_copy(hwin[0:WB, :], Hb[l - 1].ap(), 128, "act", f"Hb{l - 1}", "hwin")

        if l == 0:
            h_ap = xB
            rkb = ["oTwin"]
        else:
            h_ap = hwin[0:WB, :]
            rkb = ["oTwin", "hwin"]
        W5 = oTwin.rearrange("(s g d) (b t) -> d s g b t", s=3, g=4, b=4)

        def oT_ap(wt, g, W5=W5):
            return W5[:, wt // 11, g, :, (wt % 11) * 128:(wt % 11) * 128 + 128]

        build_B([1] + [4] * 8, 1, p=p,
                io=dict(h_ap=h_ap, oT_ap=oT_ap, h_dst=hloc[l].ap(), flag=flag,
                        rkeys=rkb, dkey=f"hloc{l}", after_setup=after_b))
        p.phase_end()
        if l == 0:
            gather_group(hloc[l], Hb[l], 16, 256, f"hloc{l}", f"Hb{l}")
        else:
            p.s.cc(lambda e, sem: e.collective_compute(
                "AllGather", ALU.bypass, replica_groups=rg,
                ins=[hloc[1].ap()[CH - 256:CH, :].opt()], outs=[Hh[0].ap().opt()]).then_inc(sem),
                ["hloc1"], ["Hh0"], n=1)
        if stop():
            return p.finish(), dict(p.dins)


    def halo_copy(k):
        src = Hh[k].ap().rearrange("(g r) d -> g r d", g=4)
        p.dmaf(lambda e: e.dma_start(
            out=halowin[k].rearrange("(g r) d -> g r d", g=1),
            in_=src[ds(PID(e, "cm1", lambda pid: (pid + 3) % 4), 1), :, :]),
            r=[f"Hh{k}"], w=[f"halowin{k}"], q="sp")

    def tile_src(halo_ap, main_ap):
        def f(t):
            if t < 2:
                return halo_ap[t * 128:(t + 1) * 128, :]
            return main_ap[(t - 2) * 128:(t - 1) * 128, :]
        return f

    for l in range(2, 4):
        p.sfx = f"_C{l}"
        last = (l == 3)
        if l == 2:
            def after_c():
                p.s.cc_wait()
                halo_copy(0)
            h_tile = hkv_tile = tile_src(halowin[0], hloc[1].ap())
            rkc = ["halowin0", "hloc1"]
        else:
            def after_c():
                p.s.cc_wait()
                halo_copy(1)
            h_tile = tile_src(halowin[1], hloc[2].ap())
            hkv_tile = tile_src(halowin[0], hloc[1].ap())
            rkc = ["halowin0", "hloc1", "halowin1", "hloc2"]
        build_C([2] + [4] * 8, 2, final_norm=last, p=p,
                io=dict(h_tile=h_tile, hkv_tile=hkv_tile, h_dst=(out if last else hloc[2].ap()),
                        flag=flag, rkeys=rkc, dkey=(None if last else "hloc2"),
                        after_setup=after_c))
        if not last:
            p.phase_end()
            p.s.cc(lambda e, sem: e.collective_compute(
                "AllGather", ALU.bypass, replica_groups=rg,
                ins=[hloc[2].ap()[CH - 256:CH, :].opt()], outs=[Hh[1].ap().opt()]).then_inc(sem),
                ["hloc2"], ["Hh1"], n=1)
            if stop():
                return p.finish(), dict(p.dins)
    return p.finish(), dict(p.dins)


def kernel(x, norm_attn, norm_ffn, a_w_in, a_cmp_pos, a_cmp_w1, a_cmp_w2, a_w_out, kv_norm,
           b_w_kv, b_w_q, b_sinks, b_w_out, ffn_w_in, ffn_conv_w, ffn_conv_b, ffn_w_out,
           final_norm):
    f32 = lambda a: np.ascontiguousarray(np.asarray(a, dtype=np.float32))
    x = f32(x)
    norm_attn, norm_ffn = f32(norm_attn), f32(norm_ffn)
    a_w_in, a_cmp_pos, a_cmp_w1, a_cmp_w2, a_w_out = map(f32, (a_w_in, a_cmp_pos, a_cmp_w1,
                                                                a_cmp_w2, a_w_out))
    kv_norm, b_w_kv, b_w_q, b_sinks, b_w_out = map(f32, (kv_norm, b_w_kv, b_w_q, b_sinks, b_w_out))
    ffn_w_in, ffn_conv_w, ffn_conv_b, ffn_w_out, final_norm = map(
        f32, (ffn_w_in, ffn_conv_w, ffn_conv_b, ffn_w_out, final_norm))
    nc, dins = build_fused(NPHASE)
    ident = ident_np()
    gfin = np.ascontiguousarray(final_norm[None, :])
    cosA, sinA = rope_tables(np.arange(SEQ))
    constsA = nsa_consts(SEQ)
    maps = []
    for i in range(8):
        b, c = divmod(i, 4)
        g = c
        m = dict(xA=x[b], xB=_chunk_with_halo(x[b], c, 128),
                 flag=np.full((128, 1), 0.0 if c == 0 else 1.0, np.float32))
        for l in range(2):
            a = dict(g_attn=gT_np(norm_attn[l]), w1=a_cmp_w1[l], w2=a_cmp_w2[l],
                     cos_t=cosA, sin_t=sinA)
            a.update(constsA)
            a.update(nsa_weights(a_w_in[l], a_cmp_pos[l], g))
            for k, v in a.items():
                m[f"{k}_A{l}"] = v
            bb = dict(w_o=a_w_out[l], w_in=ffn_w_in[l], w_out=ffn_w_out[l],
                      cwb=_cwb(ffn_conv_w[l], ffn_conv_b[l]), g_ffn=gT_np(norm_ffn[l]), g_fin=gfin,
                      ident=ident)
            for k, v in bb.items():
                m[f"{k}_B{l}"] = v
        pos = c * CH - 256 + np.arange(CH + 256)
        cos_t, sin_t = rope_tables(pos)
        for l in range(2, 4):
            j = l - 2
            cc = dict(w_q=b_w_q[j], w_kv=b_w_kv, sinks_b=sinks_row(b_sinks[j]),
                      g_attn=gT_np(norm_attn[l]), g_kv=gT_np(kv_norm), cos_t=cos_t, sin_t=sin_t,
                      masks=swa_masks(c > 0), w_o=b_w_out[j], w_in=ffn_w_in[l], w_out=ffn_w_out[l],
                      cwb=_cwb(ffn_conv_w[l], ffn_conv_b[l]), g_ffn=gT_np(norm_ffn[l]), g_fin=gfin,
                      ident=ident)
            for k, v in cc.items():
                m[f"{k}_C{l}"] = v
        m = {k: v for k, v in m.items() if k in dins}
        maps.append(m)
    res = _run(nc, maps)
    h = np.stack([np.concatenate([res[b * 4 + c]["out"] for c in range(4)], 0)
                  for b in range(NB)], 0)
    return np.ascontiguousarray(h.astype(np.float32))
```

```python
import numpy as np
import ml_dtypes
import concourse.bass as bass
import concourse.mybir as mybir
from concourse.bass_utils import run_bass_kernel_spmd

F32 = mybir.dt.float32
BF16 = mybir.dt.bfloat16
AF = mybir.ActivationFunctionType
ALU = mybir.AluOpType
AX = mybir.AxisListType

NPBF16 = ml_dtypes.bfloat16

D = 1024
DFF = 2816
EPS = 1e-6
MASKV = -240000.0

COMPUTE = ("pe", "act", "dve", "pool")
EPOCH = 30000
NSLOT = 12
SAME_ENG_SYNC = True
ZERO_BIAS = True


class Sched:
    def __init__(self, nc):
        self.nc = nc
        self.streams = {e: [] for e in COMPUTE + ("sp",)}
        self.cnt = {e: 0 for e in COMPUTE}
        self.known = {e: {} for e in self.streams}
        self.known_dma = {e: set() for e in self.streams}
        self.last_w = {}
        self.readers = {}
        self.ndma = {e: 0 for e in self.streams}
        self.sems = {}
        self.nsem = 0
        self.out_dmas = []
        self.ncc = 0
        self.cc_pending = []
        self.snap = {e: [] for e in COMPUTE}
        self.snapd = {}

    def _sem(self, name):
        if name not in self.sems:
            self.sems[name] = self.nc.alloc_semaphore(name=name)
        return self.sems[name]

    def _ev_wait_args(self, ev):
        kind = ev[0]
        if kind == "c":
            _, eng, idx = ev
            ep, off = divmod(idx, EPOCH)
            return self._sem(f"s_{eng}_{ep}"), off + 1
        elif kind == "x":
            return self._sem(f"x_{ev[1]}"), ev[2]
        else:
            _, q, j = ev
            slot, use = j % NSLOT, j // NSLOT
            return self._sem(f"d_{q}_{slot}"), 16 * (use + 1)

    def _deps(self, eng, reads, writes):
        deps = set()
        for k in reads:
            w = self.last_w.get(k)
            if w is not None:
                deps.add(w)
        for k in writes:
            w = self.last_w.get(k)
            if w is not None:
                deps.add(w)
            for r in self.readers.get(k, ()):
                deps.add(r)
        waits = []
        best = {}
        for ev in deps:
            if ev[0] == "c":
                _, src, idx = ev
                if src == eng and eng == "pe":
                    continue
                if self.known[eng].get(src, -1) >= idx:
                    continue
                if best.get(src, -1) < idx:
                    best[src] = idx
            else:
                if ev in self.known_dma[eng]:
                    continue
                waits.append(ev)
                self.known_dma[eng].add(ev)
        for src, idx in best.items():
            self.known[eng][src] = idx
            waits.append(("c", src, idx))
        for ev in list(waits):
            sn = self.snap[ev[1]][ev[2]] if ev[0] == "c" else self.snapd.get(ev)
            if sn is None:
                continue
            kn = self.known[eng]
            for ci, ce in enumerate(COMPUTE):
                if sn[ci] > kn.get(ce, -1) and (ce != eng or True):
                    kn[ce] = sn[ci]
        return waits

    def _snapshot(self, eng):
        kn = self.known[eng]
        return tuple(kn.get(ce, -1) for ce in COMPUTE)

    def _mark(self, ev, reads, writes):
        for k in reads:
            self.readers.setdefault(k, []).append(ev)
        for k in writes:
            self.last_w[k] = ev
            self.readers[k] = []

    def op(self, eng, fn, reads=(), writes=()):
        assert eng in COMPUTE
        waits = self._deps(eng, reads, writes)
        idx = self.cnt[eng]
        self.cnt[eng] += 1
        ev = ("c", eng, idx)
        if not SAME_ENG_SYNC or eng == "pe":
            self.known[eng][eng] = idx
        sn = list(self._snapshot(eng))
        sn[COMPUTE.index(eng)] = max(sn[COMPUTE.index(eng)], idx - 1)
        self.snap[eng].append(tuple(sn))
        self._mark(ev, reads, writes)
        self.streams[eng].append((waits, fn, ev))
        return ev

    def dma(self, q, fn, reads=(), writes=(), is_output=False):
        waits = self._deps(q, reads, writes)
        j = self.ndma[q]
        self.ndma[q] += 1
        if j >= NSLOT:
            prev = ("d", q, j - NSLOT)
            if prev not in self.known_dma[q]:
                waits.append(prev)
                self.known_dma[q].add(prev)
        ev = ("d", q, j)
        self.snapd[ev] = self._snapshot(q)
        self._mark(ev, reads, writes)
        self.streams[q].append((waits, fn, ev))
        if is_output:
            self.out_dmas.append(ev)
        return ev

    def pid(self, e, key="pid", fn=None):
        k = (self.cur_eng, key)
        if k not in self.pid_cache:
            if key == "pid":
                self.pid_cache[k] = e.partition_id()
            else:
                self.pid_cache[k] = e.snap(fn(self.pid(e)))
        return self.pid_cache[k]

    def cc(self, fn, reads=(), writes=(), n=1):
        waits = self._deps("pool", reads, writes)
        ev = ("x", self.ncc, n)
        self.ncc += 1
        self._mark(ev, reads, writes)
        self.streams["pool"].append((waits, fn, ev))
        self.known_dma["pool"].add(ev)
        self.cc_pending.append((ev, tuple(writes)))
        return ev

    def cc_wait(self):
        for ev, writes in self.cc_pending:
            idx = self.cnt["pool"]
            self.cnt["pool"] += 1
            nev = ("c", "pool", idx)
            self.snap["pool"].append(self._snapshot("pool"))
            self._mark(nev, (), writes)
            self.streams["pool"].append(([ev], lambda e: e.nop(), nev))
        self.cc_pending = []

    def barrier(self):
        self.cc_wait()
        evs = []
        for eng in COMPUTE:
            if self.cnt[eng] > 0:
                evs.append(("c", eng, self.cnt[eng] - 1))
        for q, n in self.ndma.items():
            for j in range(max(0, n - NSLOT), n):
                evs.append(("d", q, j))
        for eng in self.streams:
            waits = []
            for ev in evs:
                if ev[0] == "c":
                    if ev[1] == eng:
                        continue
                    if self.known[eng].get(ev[1], -1) >= ev[2]:
                        continue
                    self.known[eng][ev[1]] = ev[2]
                elif ev[0] == "x":
                    continue
                elif ev in self.known_dma[eng]:
                    continue
                else:
                    self.known_dma[eng].add(ev)
                waits.append(ev)
            self.streams[eng].append((waits, None, None))

    def emit(self, final=True):
        nc = self.nc
        if final:
            self.cc_wait()
        final_waits = list(self.out_dmas) if final else []
        self.pid_cache = {}
        with nc.Block() as block:
            def run(engname, e):
                self.cur_eng = engname
                for waits, fn, ev in self.streams[engname]:
                    for w in waits:
                        s, v = self._ev_wait_args(w)
                        e.wait_ge(s, v)
                    if fn is None:
                        continue
                    s, v = self._ev_wait_args(ev)
                    if ev[0] == "x":
                        fn(e, s)
                        continue
                    ins = fn(e)
                    if ev[0] == "c":
                        ins.then_inc(s, 1)
                    else:
                        ins.then_inc(s, 16)
                if engname == "sp":
                    for w in final_waits:
                        s, v = self._ev_wait_args(w)
                        e.wait_ge(s, v)

            @block.tensor
            def _(e):
                run("pe", e)

            @block.scalar
            def _(e):
                run("act", e)

            @block.vector
            def _(e):
                run("dve", e)

            @block.gpsimd
            def _(e):
                run("pool", e)

            @block.sync
            def _(e):
                run("sp", e)
        for k in self.streams:
            self.streams[k] = []


class Prog:
    def __init__(self, nc):
        from contextlib import ExitStack
        self.nc = nc
        self.s = Sched(nc)
        self.es = ExitStack()
        self.ndram = 0
        self.sfx = ""
        self.dins = {}
        self.ext = {}

    def sb(self, name, shape, dt):
        return self.es.enter_context(self.nc.sbuf_tensor("sb_" + name + self.sfx, list(shape), dt))

    def ps(self, name, shape, dt=F32):
        return self.es.enter_context(self.nc.psum_tensor("ps_" + name + self.sfx, list(shape), dt))

    def din(self, name, shape, dt):
        nm = name + self.sfx
        if nm in self.ext:
            return self.ext[nm]
        self.dins[nm] = (tuple(shape), dt)
        return self.nc.dram_tensor(nm, list(shape), dt, kind="ExternalInput").ap()

    def dint(self, name, shape, dt):
        return self.nc.dram_tensor(name, list(shape), dt)

    def phase_end(self):
        from contextlib import ExitStack
        self.s.barrier()
        self.s.emit(final=False)
        self.es.close()
        self.es = ExitStack()

    def dmaf(self, fn, r=(), w=(), q="sp", is_output=False):
        return self.s.dma(q, fn, r, w, is_output)

    def dout(self, name, shape, dt):
        return self.nc.dram_tensor(name, list(shape), dt, kind="ExternalOutput").ap()

    def dma(self, out, in_, r=(), w=(), q="sp", is_output=False):
        return self.s.dma(q, lambda e: e.dma_start(out=out, in_=in_), r, w, is_output)

    def mm(self, out, lhsT, rhs, start, stop, r=(), w=()):
        return self.s.op("pe", lambda e: e.matmul(out, lhsT, rhs, start=start, stop=stop), r, w)

    def tr(self, out, in_, ident, r=(), w=()):
        return self.s.op("pe", lambda e: e.transpose(out, in_, ident), r, w)

    def act(self, out, in_, func, r=(), w=(), bias=None, scale=None, accum_out=None):
        kw = {}
        if bias is not None:
            kw["bias"] = bias
        if scale is not None:
            kw["scale"] = scale
        if accum_out is not None:
            kw["accum_out"] = accum_out
        return self.s.op("act", lambda e: e.activation(out, in_, func, **kw), r, w)

    def tt(self, out, in0, in1, op, r=(), w=(), eng="dve"):
        return self.s.op(eng, lambda e: e.tensor_tensor(out, in0, in1, op), r, w)

    def ts(self, out, in0, s1, s2, op0, op1=None, r=(), w=(), eng="dve", accum_out=None):
        kw = {}
        if accum_out is not None:
            kw["accum_out"] = accum_out
        if op1 is None:
            return self.s.op(eng, lambda e: e.tensor_scalar(out, in0, s1, s2, op0, **kw), r, w)
        return self.s.op(eng, lambda e: e.tensor_scalar(out, in0, s1, s2, op0, op1, **kw), r, w)

    def stt(self, out, in0, scalar, in1, op0, op1, r=(), w=(), eng="dve"):
        return self.s.op(eng, lambda e: e.scalar_tensor_tensor(out, in0, scalar, in1, op0, op1), r, w)

    def cp(self, out, in_, r=(), w=(), eng="dve"):
        if eng == "act":
            return self.s.op("act", lambda e: e.copy(out, in_), r, w)
        return self.s.op(eng, lambda e: e.tensor_copy(out, in_), r, w)

    def recip(self, out, in_, r=(), w=()):
        return self.s.op("dve", lambda e: e.reciprocal(out, in_), r, w)

    def memset(self, ap, val, w=(), eng="dve"):
        return self.s.op(eng, lambda e: e.memset(ap, val), (), w)

    def finish(self):
        self.s.emit()
        self.es.close()
        return self.nc


def load_cast_weight(p, w_dram, dst, nk, ncols, stage, tag, chunk_cols=1024):
    i = 0
    for kc in range(nk):
        for c0 in range(0, ncols, chunk_cols):
            cw = min(chunk_cols, ncols - c0)
            stg = stage[i % 2]
            p.dma(stg[:, 0:cw], w_dram[kc * 128:(kc + 1) * 128, c0:c0 + cw],
                  w=[("stg", i % 2)])
            p.cp(dst[:, kc, c0:c0 + cw], stg[:, 0:cw], r=[("stg", i % 2)],
                 w=[(tag, kc)], eng="pool")
            i += 1


def rmsnorm_tile(p, x_ap, gain_bc, out_ap, scr, keys_r, keys_w, tagk):
    sq, ss, sd, rs = scr
    p.act(sq, x_ap, AF.Square, r=keys_r, w=[("sq", tagk), ("ss", tagk)], accum_out=ss)
    p.act(sd, ss, AF.Sqrt, r=[("ss", tagk)], w=[("sd", tagk)], bias=EPS, scale=1.0 / D)
    p.recip(rs, sd, r=[("sd", tagk)], w=[("rs", tagk)])
    if gain_bc is None:
        p.ts(out_ap, x_ap, rs, None, ALU.mult, r=list(keys_r) + [("rs", tagk)], w=keys_w)
    else:
        p.stt(out_ap, x_ap, rs, gain_bc, ALU.mult, ALU.mult,
              r=list(keys_r) + [("rs", tagk), "gains"], w=keys_w)


class FFNCtx:
    def __init__(self, p, pre, max_nt=4):
        self.p = p
        self.max_nt = max_nt
        nt = max_nt
        self.wo = p.sb(pre + "wo", [64, 16, 1024], BF16)
        self.woch = [p.sb(pre + f"woch{i}", [128, 512], BF16) for i in range(2)]
        self.fstage = [p.sb(pre + f"fstg{i}", [128, 2048], F32) for i in range(2)]
        self.stage = [self.fstage[i][:, 0:1024] for i in range(2)]
        self.gT = p.sb(pre + "gT", [128, 8], F32)
        self.wch = [p.sb(pre + f"wch{i}", [128, 8, 2, 128], BF16) for i in range(2)]
        self.h1 = p.sb(pre + "h1", [128, nt, 1024], F32)
        self.oT = p.sb(pre + "oT", [64, 16, nt * 128], BF16)
        self.hn = p.sb(pre + "hn", [128, 1024], BF16)
        self.hnT = p.sb(pre + "hnT", [128, 8, nt * 128], BF16)
        self.actT = p.sb(pre + "actT", [128, 22, nt * 128], BF16)
        self.usb = [[p.sb(pre + f"usb{i}{a}", [128, 2 + nt * 128], F32) for a in range(2)]
                    for i in range(2)]
        self.carry = p.sb(pre + "carry", [128, 44, 2], F32)
        self.cwb = p.sb(pre + "cwb", [128, 44, 4], F32)
        self.t1 = p.sb(pre + "t1", [128, nt * 128], F32)
        self.t2 = p.sb(pre + "t2", [128, nt * 128], F32)
        self.ca = p.sb(pre + "ca", [128, nt * 128], F32)
        self.cg = p.sb(pre + "cg", [128, nt * 128], F32)
        self.sa = p.sb(pre + "sa", [128, nt * 128], F32)
        self.sq = p.sb(pre + "sq", [128, 1024], BF16)
        self.ss = p.sb(pre + "ss", [128, 1], F32)
        self.sd = p.sb(pre + "sd", [128, 1], F32)
        self.rs = p.sb(pre + "rs", [128, 1], F32)
        self.gainf = p.sb(pre + "gainf", [128, 1024], F32)
        self.ident = p.sb(pre + "ident", [128, 128], BF16)
        self.hfin = p.sb(pre + "hfin", [128, 1024], F32)
        self.psum = p.ps(pre + "psum", [128, 7 * 512])
        self.psT = p.ps(pre + "psT", [128, 8, 128], BF16)
        self.psA = [self.bank(0), self.bank(1)]
        self.psU = [[self.bank(2), self.bank(3)], [self.bank(4), self.bank(5)]]
        self.nA = 0
        self.nfc = 0
        self.nwo = 0

    def bank(self, i, n=1):
        return self.psum[:, i * 512:(i + n) * 512]

    def load_weights(self, w_o, w_in, w_out, cwb, g_ffn, ident, g_final=None, head_order=None):
        p = self.p
        self.w_in = w_in
        p.dma(self.ident[:], ident, w=["ident"])
        p.dma(self.cwb[:], cwb.rearrange("(c p) f -> p c f", p=128), w=["cwb"])
        p.dma(self.gT[:], g_ffn, w=["gT"])
        self.w_out = w_out
        nc = p.nc
        self.winb = nc.dram_tensor("winb" + p.sfx, [44 * 128, 1024], BF16).ap()
        self.woutb = nc.dram_tensor("woutb" + p.sfx, [44 * 128, 512], BF16).ap()
        i = 0
        for fc in range(22):
            for ag in range(2):
                par = i % 2
                i += 1
                stg = self.fstage[par]
                c0 = ag * DFF + fc * 128
                sk = ("stg", "ffn", par, 0)
                p.dma(stg[:, 0:1024].rearrange("p (c f) -> p c f", c=8),
                      w_in[:, c0:c0 + 128].rearrange("(c p) f -> p c f", p=128), w=[sk])
                p.tt(self.wch[par][:, :, 0, :],
                     stg[:, 0:1024].rearrange("p (c f) -> p c f", c=8),
                     self.gT[:].unsqueeze(2).to_broadcast([128, 8, 128]), ALU.mult,
                     r=[sk, "gT"], w=[("wch", par, 0)], eng="pool")
                p.dma(self.winb[(fc * 2 + ag) * 128:(fc * 2 + ag + 1) * 128, :],
                      self.wch[par][:, :, 0, :], r=[("wch", par, 0)], w=["winb"], q="pool")
        for half in range(2):
            for fc in range(22):
                par = i % 2
                i += 1
                stg = self.fstage[par]
                sk = ("stg", "ffn", par, 0)
                p.dma(stg[:, 0:512], w_out[fc * 128:(fc + 1) * 128, half * 512:(half + 1) * 512],
                      w=[sk])
                p.cp(self.woch[par][:], stg[:, 0:512], r=[sk], w=[("woch", par)], eng="pool")
                p.dma(self.woutb[(half * 22 + fc) * 128:(half * 22 + fc + 1) * 128, :],
                      self.woch[par][:], r=[("woch", par)], w=["woutb"], q="pool")
        if g_final is not None:
            p.dma(self.gainf[:], g_final.to_broadcast([128, 1024]), w=["gainf"])
        p.memset(self.carry[:], 0.0, w=["carry"])
        if w_o is not None:
            ho = head_order if head_order is not None else list(range(16))
            for i, h in enumerate(ho):
                stg = self.stage[i % 2]
                sk = ("stg", "ffn", i % 2, 0)
                p.dma(stg[0:64, :], w_o[h * 64:(h + 1) * 64, :], w=[sk])
                p.cp(self.wo[:, i, :], stg[0:64, :], r=[sk], w=[("wo", i)], eng="pool")

    def run_supertile(self, nt, h_src, oT_src, h_dst, n_skip_out=0, final_norm=False,
                      h1_preloaded=False, h_fn=None, oT_fn=None, n_flag=0, flag=None,
                      rkeys=(), dkey=None):
        p = self.p
        ntok = nt * 128
        if not h1_preloaded:
            for j in range(nt):
                p.dma(self.h1[:, j, :], h_src[j * 128:(j + 1) * 128, :], r=list(rkeys),
                      w=[("h1", j)])
                if j < n_flag:
                    p.ts(self.h1[:, j, :], self.h1[:, j, :], flag, None, ALU.mult,
                         r=[("h1", j), "flag"], w=[("h1", j)])
        if oT_src is not None or oT_fn is not None:
            if oT_fn is not None:
                for j in range(nt):
                    for g in range(4):
                        p.dma(self.oT[:, g * 4:(g + 1) * 4, j * 128:(j + 1) * 128], oT_fn(j, g),
                              r=list(rkeys), w=["oT"])
            elif not isinstance(oT_src, str):
                p.dma(self.oT[:, :, 0:ntok], oT_src.rearrange("(c p) t -> p c t", p=64), w=["oT"])
            for j in range(nt):
                for half in range(2):
                    ps = self.psA[self.nA % 2]
                    pk = ("bank", self.nA % 2)
                    self.nA += 1
                    for kc in range(16):
                        p.mm(ps, self.oT[:, kc, j * 128:(j + 1) * 128],
                             self.wo[:, kc, half * 512:(half + 1) * 512],
                             start=(kc == 0), stop=(kc == 15),
                             r=["oT", ("wo", kc)], w=[pk])
                    hs = self.h1[:, j, half * 512:(half + 1) * 512]
                    p.tt(hs, hs, ps, ALU.add, r=[pk, ("h1", j)], w=[("h1", j)])
        for j in range(nt):
            rmsnorm_tile(p, self.h1[:, j, :], None, self.hn[:],
                         (self.sq[:], self.ss[:], self.sd[:], self.rs[:]),
                         [("h1", j)], ["hn"], "f")
            for kc in range(8):
                p.tr(self.psT[:, kc, :], self.hn[:, kc * 128:(kc + 1) * 128], self.ident[:],
                     r=["hn", "ident"], w=["psT"])
            p.cp(self.hnT[:, :, j * 128:(j + 1) * 128], self.psT[:], r=["psT"], w=[("hnT", j)],
                 eng="act")
        hnT_keys = [("hnT", j) for j in range(nt)]
        for fc in range(22):
            par = self.nfc % 2
            self.nfc += 1
            wch = self.wch[par]
            for ag in range(2):
                p.dma(wch[:, :, ag, :],
                      self.winb[(fc * 2 + ag) * 128:(fc * 2 + ag + 1) * 128, :].rearrange(
                          "p (c f) -> p c f", c=8),
                      r=["winb"], w=[("wch", par, ag)])
            cs = []
            for ag in range(2):
                ps = self.psU[par][ag]
                pk = ("bank", 2 + 2 * par + ag)
                for kc in range(8):
                    p.mm(ps[:, 0:ntok], wch[:, kc, ag, :], self.hnT[:, kc, 0:ntok],
                         start=(kc == 0), stop=(kc == 7),
                         r=[("wch", par, ag)] + hnT_keys, w=[pk])
                usb = self.usb[par][ag]
                uk = ("usb", par, ag)
                ch = ag * 22 + fc
                p.cp(usb[:, 0:2], self.carry[:, ch, :], r=["carry%d" % ch, "carry"], w=[uk],
                     eng="pool")
                p.cp(usb[:, 2:2 + ntok], ps[:, 0:ntok], r=[pk], w=[uk], eng="act")
                p.cp(self.carry[:, ch, :], usb[:, ntok:ntok + 2], r=[uk], w=["carry%d" % ch],
                     eng="pool")
                cw = self.cwb
                dst = self.ca if ag == 0 else self.cg
                dk = "ca" if ag == 0 else "cg"
                p.ts(self.t1[:, 0:ntok], usb[:, 2:2 + ntok], cw[:, ch, 2:3], cw[:, ch, 3:4],
                     ALU.mult, ALU.add, r=[uk, "cwb"], w=["t1"])
                p.stt(self.t2[:, 0:ntok], usb[:, 1:1 + ntok], cw[:, ch, 1:2], self.t1[:, 0:ntok],
                      ALU.mult, ALU.add, r=[uk, "cwb", "t1"], w=["t2"])
                p.stt(dst[:, 0:ntok], usb[:, 0:ntok], cw[:, ch, 0:1], self.t2[:, 0:ntok],
                      ALU.mult, ALU.add, r=[uk, "cwb", "t2"], w=[dk])
            p.act(self.sa[:, 0:ntok], self.ca[:, 0:ntok], AF.Silu, r=["ca"], w=["sa"])
            p.tt(self.actT[:, fc, 0:ntok], self.sa[:, 0:ntok], self.cg[:, 0:ntok], ALU.mult,
                 r=["sa", "cg"], w=[("actT", fc)])
        for half in range(2):
            for fc in range(22):
                wp = self.nwo % 2
                self.nwo += 1
                p.dma(self.woch[wp][:],
                      self.woutb[(half * 22 + fc) * 128:(half * 22 + fc + 1) * 128, :],
                      r=["woutb"], w=[("woch", wp)])
                for j in range(nt):
                    p.mm(self.bank(j), self.actT[:, fc, j * 128:(j + 1) * 128], self.woch[wp][:],
                         start=(fc == 0), stop=(fc == 21),
                         r=[("actT", fc), ("woch", wp)], w=[("bank", j)])
            for j in range(nt):
                hs = self.h1[:, j, half * 512:(half + 1) * 512]
                p.tt(hs, hs, self.bank(j), ALU.add, r=[("bank", j), ("h1", j)], w=[("h1", j)])
        for j in range(nt):
            if h_dst is not None and j >= n_skip_out:
                jo = j - n_skip_out
                if final_norm:
                    rmsnorm_tile(p, self.h1[:, j, :], self.gainf[:], self.hfin[:],
                                 (self.sq[:], self.ss[:], self.sd[:], self.rs[:]),
                                 [("h1", j), "gainf"], ["hfin"], "f")
                    p.dma(h_dst[jo * 128:(jo + 1) * 128, :], self.hfin[:], r=["hfin"],
                          w=([dkey] if dkey else []), q="pool", is_output=True)
                else:
                    p.dma(h_dst[jo * 128:(jo + 1) * 128, :], self.h1[:, j, :], r=[("h1", j)],
                          w=([dkey] if dkey else []), q="pool", is_output=(dkey is None))


def ident_np():
    return np.eye(128, dtype=np.float32).astype(NPBF16)


def b_head_order():
    return [g * 4 + 2 * hpl + par for g in range(4) for par in range(2) for hpl in range(2)]


def build_B(st_sizes, n_skip_tiles, final_norm=False, p=None, io=None):
    fused = p is not None
    if not fused:
        nc = bass.Bass("TRN2", target_bir_lowering=False)
        p = Prog(nc)
    ntiles = sum(st_sizes)
    ntok = ntiles * 128
    if not fused:
        h_in = p.din("h_in", [ntok, D], F32)
        oT_in = p.din("oT_in", [D, ntok], BF16)
    w_o = p.din("w_o", [D, D], F32)
    w_in = p.din("w_in", [D, 2 * DFF], F32)
    w_out = p.din("w_out", [DFF, D], F32)
    cwb = p.din("cwb", [2 * DFF, 4], F32)
    g_ffn = p.din("g_ffn", [128, 8], F32)
    g_fin = p.din("g_fin", [1, D], F32)
    ident = p.din("ident", [128, 128], BF16)
    if not fused:
        h_out = p.dout("h_out", [(ntiles - n_skip_tiles) * 128, D], F32)
    else:
        h_out = io["h_dst"]
    f = FFNCtx(p, "f_", max_nt=max(st_sizes))
    f.load_weights(w_o, w_in, w_out, cwb, g_ffn, ident, g_fin,
                   head_order=(b_head_order() if fused else None))
    if fused:
        flag_sb = p.sb("flag", [128, 1], F32)
        p.dma(flag_sb[:], io["flag"], w=["flag"])
        if io.get("after_setup"):
            io["after_setup"]()
    t0 = 0
    for nt in st_sizes:
        skip = max(0, min(nt, n_skip_tiles - t0))
        o0 = max(0, t0 - n_skip_tiles)
        dst = h_out[o0 * 128:(o0 + nt - skip) * 128, :] if skip < nt else None
        if fused:
            f.run_supertile(nt, io["h_ap"][t0 * 128:(t0 + nt) * 128, :], None, dst,
                            n_skip_out=skip, final_norm=final_norm,
                            oT_fn=lambda j, g, t0=t0: io["oT_ap"](t0 + j, g),
                            n_flag=skip, flag=flag_sb[:], rkeys=io["rkeys"], dkey=io["dkey"])
        else:
            f.run_supertile(nt, h_in[t0 * 128:(t0 + nt) * 128, :],
                            oT_in[:, t0 * 128:(t0 + nt) * 128], dst, n_skip_out=skip,
                            final_norm=final_norm)
        t0 += nt
    if fused:
        return None
    return p.finish()


def c_head_order():
    return [8 * g + 2 * hpl + par for g in range(2) for par in range(2) for hpl in range(4)]


def build_C(st_sizes, n_skip_tiles, final_norm=False, p=None, io=None):
    fused = p is not None
    if not fused:
        nc = bass.Bass("TRN2", target_bir_lowering=False)
        p = Prog(nc)
    ntiles = sum(st_sizes)
    ntok = ntiles * 128
    mx = max(st_sizes)
    if not fused:
        h_in = p.din("h_in", [ntok, D], F32)
        hkv_in = p.din("hkv_in", [ntok, D], F32)
    w_q = p.din("w_q", [D, D], F32)
    w_kv = p.din("w_kv", [D, 256], F32)
    sinks_b = p.din("sinks_b", [1, 16], F32)
    g_attn = p.din("g_attn", [128, 8], F32)
    g_kv = p.din("g_kv", [128, 8], F32)
    cos_t = p.din("cos_t", [128, ntok], F32)
    sin_t = p.din("sin_t", [128, ntok], F32)
    masks = p.din("masks", [3, 128, 512], BF16)
    w_o = p.din("w_o", [D, D], F32)
    w_in = p.din("w_in", [D, 2 * DFF], F32)
    w_out = p.din("w_out", [DFF, D], F32)
    cwb = p.din("cwb", [2 * DFF, 4], F32)
    g_ffn = p.din("g_ffn", [128, 8], F32)
    g_fin = p.din("g_fin", [1, D], F32)
    ident = p.din("ident", [128, 128], BF16)
    if not fused:
        h_out = p.dout("h_out", [(ntiles - n_skip_tiles) * 128, D], F32)
    else:
        h_out = io["h_dst"]

    f = FFNCtx(p, "f_", max_nt=mx)
    f.load_weights(w_o, w_in, w_out, cwb, g_ffn, ident, g_fin, head_order=c_head_order())
    if fused:
        flag_sb = p.sb("flag", [128, 1], F32)
        p.dma(flag_sb[:], io["flag"], w=["flag"])

    gTq = p.sb("gTq", [128, 8], F32)
    gTk = p.sb("gTk", [128, 8], F32)
    wk2 = p.sb("wk2", [128, 8, 2, 2, 128], BF16)
    wv = p.sb("wv", [128, 8, 128], BF16)
    hkv = p.sb("hkv", [128, 1024], F32)
    hnqT = f.hnT
    hkvT = p.sb("hkvT", [128, 8, mx * 128], BF16)
    QT2 = p.sb("QT2", [128, 8, mx * 128], BF16)
    KT2 = p.sb("KT2", [128, 2, (mx + 1) * 128], BF16)
    VA = p.sb("VA", [128, mx + 1, 2, 65], BF16)
    PT = [p.sb(f"PT{i}", [128, 1024], BF16) for i in range(4)]
    msk = p.sb("msk", [128, 3, 512], BF16)
    cos_sb = p.sb("cos_sb", [128, mx * 128], F32)
    sin_sb = p.sb("sin_sb", [128, mx * 128], F32)
    sexp = p.sb("sexp", [128, 16], F32)
    zr = p.sb("zr", [128, 1024], F32)
    rz = zr
    ones = p.sb("ones", [128, 64], F32)
    osb = f.hfin[0:64, :]
    psS = [f.bank(2, 2), f.bank(4, 2)]
    psSk = [[("bank", 2), ("bank", 3)], [("bank", 4), ("bank", 5)]]
    psO = f.bank(0, 2)
    psOk = [("bank", 0), ("bank", 1)]
    psB = f.bank(6)
    psBk = [("bank", 6)]

    p.dma(gTq[:], g_attn, w=["gTq"])
    p.dma(gTk[:], g_kv, w=["gTk"])
    p.dma(msk[:], masks.rearrange("m p c -> p m c"), w=["msk"])
    p.dma(zr[64:65, 0:16], sinks_b, w=["zr"])
    p.act(sexp[64:65, :], zr[64:65, 0:16], AF.Exp, r=["zr"], w=["sexp"])
    p.memset(ones[:], 1.0, w=["ones"])
    p.memset(VA[:], 1.0, w=["VA"] + [("VA", i) for i in range(mx + 1)])
    p.memset(KT2[:], 0.0, w=["KT2", ("KT2", 0), ("KT2", 1)])
    for kc in range(8):
        stg = f.stage[kc % 2]
        sk = ("stg", "ffn", kc % 2, 0)
        gs = gTk[:, kc:kc + 1]
        p.dma(stg[:, 0:256], w_kv[kc * 128:(kc + 1) * 128, :], w=[sk])
        for g in range(2):
            for dup in range(2):
                p.ts(wk2[:, kc, g, 0, dup * 64:(dup + 1) * 64], stg[:, g * 64:(g + 1) * 64],
                     gs, None, ALU.mult, r=[sk, "gTk"], w=["wk2"], eng="pool")
                p.ts(wk2[:, kc, g, 1, dup * 64:dup * 64 + 32], stg[:, g * 64 + 32:g * 64 + 64],
                     gs, None, ALU.mult, r=[sk, "gTk"], w=["wk2"], eng="pool")
                p.ts(wk2[:, kc, g, 1, dup * 64 + 32:dup * 64 + 64], stg[:, g * 64:g * 64 + 32],
                     gs, None, ALU.mult, r=[sk, "gTk"], w=["wk2"], eng="pool")
        p.ts(wv[:, kc, :], stg[:, 128:256], gs, None, ALU.mult, r=[sk, "gTk"], w=["wv"], eng="pool")

    wqb = p.nc.dram_tensor("wqb" + p.sfx, [16 * 128, 1024], BF16).ap()
    for hp in range(8):
        par = hp % 2
        stg = f.fstage[par]
        wch = f.wch[par]
        sk = ("stg", "ffn", par, 0)
        p.dma(stg[:, 0:1024].rearrange("p (c f) -> p c f", c=8),
              w_q[:, hp * 128:(hp + 1) * 128].rearrange("(c p) f -> p c f", p=128), w=[sk])
        p.tt(wch[:, :, 0, :], stg[:, 0:1024].rearrange("p (c f) -> p c f", c=8),
             gTq[:].unsqueeze(2).to_broadcast([128, 8, 128]), ALU.mult,
             r=[sk, "gTq"], w=[("wch", par, 0)], eng="pool")
        src = wch[:, :, 0, :].rearrange("p c (h d) -> p c h d", h=2)
        dsw = wch[:, :, 1, :].rearrange("p c (h d) -> p c h d", h=2)
        p.cp(dsw[:, :, :, 0:32], src[:, :, :, 32:64], r=[("wch", par, 0)], w=[("wch", par, 1)],
             eng="pool")
        p.cp(dsw[:, :, :, 32:64], src[:, :, :, 0:32], r=[("wch", par, 0)], w=[("wch", par, 1)],
             eng="pool")
        for v in range(2):
            p.dma(wqb[(hp * 2 + v) * 128:(hp * 2 + v + 1) * 128, :].rearrange(
                "p (c f) -> p c f", c=8), wch[:, :, v, :], r=[("wch", par, v)], w=["wqb"], q="pool")
    if fused and io.get("after_setup"):
        io["after_setup"]()
    scr = (f.sq[:], f.ss[:], f.sd[:], f.rs[:])
    t0 = 0
    first_real = n_skip_tiles
    for nt in st_sizes:
        n = nt * 128
        for j in range(nt):
            if fused:
                p.dma(f.h1[:, j, :], io["h_tile"](t0 + j), r=list(io["rkeys"]), w=[("h1", j)])
                if t0 + j < n_skip_tiles:
                    p.ts(f.h1[:, j, :], f.h1[:, j, :], flag_sb[:], None, ALU.mult,
                         r=[("h1", j), "flag"], w=[("h1", j)])
            else:
                p.dma(f.h1[:, j, :], h_in[(t0 + j) * 128:(t0 + j + 1) * 128, :], w=[("h1", j)])
        p.dma(cos_sb[:, 0:n], cos_t[:, t0 * 128:t0 * 128 + n], w=["cos"])
        p.dma(sin_sb[:, 0:n], sin_t[:, t0 * 128:t0 * 128 + n], w=["sin"])
        for j in range(nt):
            rmsnorm_tile(p, f.h1[:, j, :], None, f.hn[:], scr, [("h1", j)], ["hn"], "f")
            for kc in range(8):
                p.tr(f.psT[:, kc, :], f.hn[:, kc * 128:(kc + 1) * 128], f.ident[:],
                     r=["hn", "ident"], w=["psT"])
            p.cp(hnqT[:, :, j * 128:(j + 1) * 128], f.psT[:], r=["psT"], w=[("hnT", j)], eng="act")
            if fused:
                p.dma(hkv[:], io["hkv_tile"](t0 + j), r=list(io["rkeys"]), w=["hkv"])
                if t0 + j < n_skip_tiles:
                    p.ts(hkv[:], hkv[:], flag_sb[:], None, ALU.mult, r=["hkv", "flag"], w=["hkv"])
            else:
                p.dma(hkv[:], hkv_in[(t0 + j) * 128:(t0 + j + 1) * 128, :], w=["hkv"])
            rmsnorm_tile(p, hkv[:], None, f.hn[:], scr, ["hkv"], ["hn"], "f")
            for kc in range(8):
                p.tr(f.psT[:, kc, :], f.hn[:, kc * 128:(kc + 1) * 128], f.ident[:],
                     r=["hn", "ident"], w=["psT"])
            p.cp(hkvT[:, :, j * 128:(j + 1) * 128], f.psT[:], r=["psT"], w=[("hkvT", j)], eng="act")
        hq_keys = [("hnT", j) for j in range(nt)]
        hk_keys = [("hkvT", j) for j in range(nt)]

        def rope_out(dst, psn, pss, rk, wk):
            p.tt(f.t1[:, 0:n], psn, cos_sb[:, 0:n], ALU.mult, r=rk[0:1] + ["cos"], w=["t1"])
            p.tt(f.t2[:, 0:n], pss, sin_sb[:, 0:n], ALU.mult, r=rk[1:2] + ["sin"], w=["t2"])
            p.tt(dst, f.t1[:, 0:n], f.t2[:, 0:n], ALU.add, r=["t1", "t2"], w=wk)

        for g in range(2):
            par = f.nfc % 2
            f.nfc += 1
            bk = [("bank", 2 + 2 * par), ("bank", 3 + 2 * par)]
            for v in range(2):
                for kc in range(8):
                    p.mm(f.psU[par][v][:, 0:n], wk2[:, kc, g, v, :], hkvT[:, kc, 0:n],
                         start=(kc == 0), stop=(kc == 7), r=["wk2"] + hk_keys, w=[bk[v]])
            rope_out(KT2[:, g, 128:128 + n], f.psU[par][0][:, 0:n], f.psU[par][1][:, 0:n],
                     bk, [("KT2", g)])
        for j in range(nt):
            ps = f.psA[f.nA % 2]
            pk = ("bank", f.nA % 2)
            f.nA += 1
            for kc in range(8):
                p.mm(ps[:, 0:128], hkvT[:, kc, j * 128:(j + 1) * 128], wv[:, kc, :],
                     start=(kc == 0), stop=(kc == 7), r=[("hkvT", j), "wv"], w=[pk])
            p.cp(VA[:, j + 1, :, 0:64], ps[:, 0:128].rearrange("p (g d) -> p g d", g=2),
                 r=[pk], w=[("VA", j + 1)], eng="act")
        for hp in range(8):
            par = f.nfc % 2
            f.nfc += 1
            wch = f.wch[par]
            for v in range(2):
                p.dma(wch[:, :, v, :],
                      wqb[(hp * 2 + v) * 128:(hp * 2 + v + 1) * 128, :].rearrange(
                          "p (c f) -> p c f", c=8), r=["wqb"], w=[("wch", par, v)])
            bk = [("bank", 2 + 2 * par), ("bank", 3 + 2 * par)]
            for v in range(2):
                for kc in range(8):
                    p.mm(f.psU[par][v][:, 0:n], wch[:, kc, v, :], hnqT[:, kc, 0:n],
                         start=(kc == 0), stop=(kc == 7),
                         r=[("wch", par, v)] + hq_keys, w=[bk[v]])
            rope_out(QT2[:, hp, 0:n], f.psU[par][0][:, 0:n], f.psU[par][1][:, 0:n],
                     bk, [("QT2", hp)])
        nS = 0
        pending = None
        for j in range(nt):
            gt = t0 + j
            for g in range(2):
                chunks = [(j, 0 if gt == first_real else 1), (j + 1, 2)]
                pts = []
                for ci, (slot, mi) in enumerate(chunks):
                    sp_ = nS % 2
                    ptb = nS % 4
                    nS += 1
                    for par in range(2):
                        pr = slice(par * 64, (par + 1) * 64)
                        p.mm(psS[sp_][:, par * 512:(par + 1) * 512],
                             KT2[pr, g, slot * 128:(slot + 1) * 128],
                             QT2[pr, 4 * g:4 * g + 4, j * 128:(j + 1) * 128],
                             start=True, stop=False,
                             r=[("KT2", g)] + [("QT2", 4 * g + i) for i in range(4)],
                             w=[psSk[sp_][par]])
                        p.mm(psS[sp_][:, par * 512:(par + 1) * 512], f.ident[:], msk[:, mi, :],
                             start=False, stop=True, r=["ident", "msk"], w=[psSk[sp_][par]])
                    p.act(PT[ptb][:], psS[sp_], AF.Exp, r=psSk[sp_], w=[("PT", ptb)], scale=0.125)
                    pts.append((ci, slot, ptb))

                def fin(j=j, g=g, pts=pts):
                    for ci, slot, ptb in pts:
                        for par in range(2):
                            p.mm(psO[0:65, par * 512:(par + 1) * 512], VA[:, slot, g, :],
                                 PT[ptb][:, par * 512:(par + 1) * 512],
                                 start=(ci == 0), stop=(ci == 1),
                                 r=[("PT", ptb), ("VA", slot), "VA"], w=[psOk[par]])
                    p.tt(zr[64:65, :].rearrange("p (h q) -> p h q", h=8),
                         psO[64:65, :].rearrange("p (h q) -> p h q", h=8),
                         sexp[64:65, g * 8:(g + 1) * 8].unsqueeze(2).to_broadcast([1, 8, 128]),
                         ALU.add, r=psOk + ["sexp"], w=["zr"])
                    p.recip(rz[64:65, :], zr[64:65, :], r=["zr"], w=["rz"])
                    p.cp(osb, psO[0:64, :], r=psOk, w=["hfin"], eng="act")
                    for par in range(2):
                        p.mm(psB[0:64, :], ones[64:65, :], rz[64:65, par * 512:(par + 1) * 512],
                             start=True, stop=True, r=["ones", "rz"], w=psBk)
                        dst = f.oT[:, g * 8 + par * 4:g * 8 + par * 4 + 4, j * 128:(j + 1) * 128]
                        p.tt(dst,
                             osb[:, par * 512:(par + 1) * 512].rearrange("p (h q) -> p h q", h=4),
                             psB[0:64, :].rearrange("p (h q) -> p h q", h=4), ALU.mult,
                             r=["hfin"] + psBk, w=["oT"])

                if pending is not None:
                    pending()
                pending = fin
        pending()
        for g in range(2):
            p.cp(KT2[:, g, 0:128], KT2[:, g, n:n + 128], r=[("KT2", g)], w=[("KT2", g)], eng="pool")
        p.cp(VA[:, 0, :, :], VA[:, nt, :, :], r=[("VA", nt)], w=[("VA", 0)], eng="pool")
        skip = max(0, min(nt, n_skip_tiles - t0))
        o0 = max(0, t0 - n_skip_tiles)
        dst = h_out[o0 * 128:(o0 + nt - skip) * 128, :] if skip < nt else None
        f.run_supertile(nt, None, "resident", dst, n_skip_out=skip, final_norm=final_norm,
                        h1_preloaded=True, dkey=(io["dkey"] if fused else None))
        t0 += nt
    if fused:
        return None
    return p.finish()


def rope_tables(pos):
    half = 32
    inv = (np.float32(10000.0) ** (-np.arange(half, dtype=np.float32) / half)).astype(np.float32)
    ang = pos.astype(np.float32)[None, :] * inv[:, None]
    cos = np.cos(ang).astype(np.float32)
    sin = np.sin(ang).astype(np.float32)
    cos64 = np.concatenate([cos, cos], 0)
    sin64 = np.concatenate([-sin, sin], 0)
    return (np.ascontiguousarray(np.concatenate([cos64, cos64], 0)),
            np.ascontiguousarray(np.concatenate([sin64, sin64], 0)))


def swa_masks(first_exists):
    i = np.arange(128)[:, None]
    q = np.arange(128)[None, :]
    prev = np.where(i > q, 0.0, MASKV).astype(np.float32)
    cur = np.where(i <= q, 0.0, MASKV).astype(np.float32)
    pf = prev if first_exists else np.full((128, 128), MASKV, np.float32)
    m = np.stack([np.tile(pf, (1, 4)), np.tile(prev, (1, 4)), np.tile(cur, (1, 4))], 0)
    return m.astype(NPBF16)


def sinks_row(sinks16):
    ho = c_head_order()
    return np.ascontiguousarray(np.asarray(sinks16, np.float32)[ho][None, :])


def gT_np(g):
    return np.ascontiguousarray(np.asarray(g, np.float32).reshape(8, 128).T)


FORCE = 1.0e6
TINY = 1.0e-30
C_BF = float(np.float32(NPBF16(-MASKV)))
LN_C = float(np.log(np.float64(C_BF)))
MUL_MASK = False
KEEP_ZB = False


def build_A(S, dbg=99, p=None, io=None):
    fused = p is not None
    if not fused:
        nc = bass.Bass("TRN2", target_bir_lowering=False)
        p = Prog(nc)
    nc = p.nc
    NST = S // 512
    NQB = S // 128
    NCC = max(1, S // 2048)
    if not fused:
        h_in = p.din("h_in", [S, D], F32)
    g_attn = p.din("g_attn", [128, 8], F32)
    wq_d = p.din("wq", [D, 256], F32)
    wk3_d = p.din("wk3", [D, 192], F32)
    wv3_d = p.din("wv3", [D, 192], F32)
    wg_d = p.din("wg", [D, 12], F32)
    w1_d = p.din("w1", [2, 2048, 256], F32)
    w2_d = p.din("w2", [2, 256, 64], F32)
    posT_d = p.din("posT", [64, 2, 32], F32)
    cos_d = p.din("cos_t", [128, S], F32)
    sin_d = p.din("sin_t", [128, S], F32)
    ccos_d = p.din("ccos_t", [128, NCC * 128], F32)
    csin_d = p.din("csin_t", [128, NCC * 128], F32)
    pmask_d = p.din("pmask", [2, 16, 128, 128], BF16)
    r0mask_d = p.din("r0mask", [128, 512], BF16)
    cmask_d = p.din("cmask", [2, 128, 512], BF16)
    emat_d = p.din("emat", [64, 128, 128], BF16)
    wfull_d = p.din("wfull", [NCC * 128, 257], BF16)
    fix_d = p.din("fix3", [128, 6], F32)
    ident_d = p.din("ident", [128, 128], BF16)
    gscr = [nc.dram_tensor(f"gscr{i}" + p.sfx, [1, 12 * 512], F32).ap() for i in range(2)]
    if not fused:
        oT_out = p.dout("oT_out", [64, 4, S], BF16)

    ident = p.sb("ident", [128, 128], BF16)
    gT = p.sb("gT", [128, 8], F32)
    fst = [p.sb(f"fst{i}", [128, 1024], F32) for i in range(2)]
    WQ = p.sb("WQ", [128, 8, 2, 256], BF16)
    WKS = p.sb("WKS", [128, 8, 2, 128], BF16)
    WKW = p.sb("WKW", [128, 8, 2, 128], BF16)
    WKC = p.sb("WKC", [128, 8, 64], BF16)
    WVC = p.sb("WVC", [128, 8, 64], BF16)
    WV2 = p.sb("WV2", [128, 8, 128], BF16)
    WG = p.sb("WG", [128, 8, 12], BF16)
    W1c = [p.sb(f"W1c{i}", [64, 4, 256], BF16) for i in range(3)]
    w1b = nc.dram_tensor("w1b" + p.sfx, [16 * 64, 1024], BF16).ap()
    W2K = p.sb("W2K", [128, 2, 2, 128], BF16)
    W2V = p.sb("W2V", [128, 2, 64], BF16)
    posT = p.sb("posT", [64, 2, 32], BF16)
    c1 = p.sb("c1", [128, 4], F32)
    ccos = p.sb("ccos", [128, NCC * 128], F32)
    csin = p.sb("csin", [128, NCC * 128], F32)
    pmask = p.sb("pmask", [128, 2, 16, 128], BF16)
    r0mask = p.sb("r0mask", [128, 512], BF16)
    cmask = p.sb("cmask", [128, 2, 512], BF16)
    emat = p.sb("emat", [128, 64, 128], BF16)
    wfull = p.sb("wfull", [128, NCC, 257], BF16)
    fix3 = p.sb("fix3", [128, 6], F32)
    hbuf = [p.sb("hbuf0", [128, 1024], F32)] * 2
    sq = p.sb("sq", [128, 1024], BF16)
    ss = p.sb("ss", [128, 1], F32)
    sd = p.sb("sd", [128, 1], F32)
    rs = p.sb("rs", [128, 1], F32)
    hn = p.sb("hn", [128, 1024], BF16)
    hnT = p.sb("hnT", [128, 8, 512], BF16)
    cos_sb = p.sb("cos_sb", [128, 512], F32)
    sin_sb = p.sb("sin_sb", [128, 512], F32)
    t1 = p.sb("t1", [128, 512], F32)
    t2 = p.sb("t2", [128, 512], F32)
    Qblk = p.sb("Qblk", [128, 4, 512], BF16)
    KsT2 = p.sb("KsT2", [128, S], BF16)
    VsA = p.sb("VsA", [128, NQB, 65], BF16)
    KwT2 = p.sb("KwT2", [128, 1024], BF16)
    VwA = p.sb("VwA", [128, 8, 65], BF16)
    KcT2 = p.sb("KcT2", [128, NCC * 128], BF16)
    VcA = p.sb("VcA", [128, NCC, 65], BF16)
    xT = [p.sb(f"xT{i}", [64, 528], BF16) for i in range(2)]
    hidK = p.sb("hidK", [128, 2, 32], BF16)
    hidV = p.sb("hidV", [128, 2, 128], BF16)
    gx = [p.sb(f"gx{i}", [128, 32], F32) for i in range(3)]
    gsb = p.sb("gsb", [12, 512], F32)
    G64b = [p.sb(f"G64b{i}", [128, 12 * 128], F32) for i in range(2)]
    PT = [p.sb(f"PT{i}", [128, 512], BF16) for i in range(4)]
    EX = [p.sb(f"EX{i}", [128, 512], BF16) for i in range(4)] if MUL_MASK else None
    Msb = [p.sb(f"Msb{i}", [128, 128], BF16) for i in range(2)] if MUL_MASK else None
    nM = [0]
    PcT = p.sb("PcT", [128, NCC, 512], BF16)
    zr = p.sb("zr", [128, 512], F32)
    Rr = p.sb("Rr", [128, 512], F32)
    ones = p.sb("ones", [128, 64], F32)
    osb = p.sb("osb", [64, 512], F32)
    acc = p.sb("acc", [64, 512], F32)
    tmpo = p.sb("tmpo", [64, 512], F32)
    oacc = p.sb("oacc", [64, 4, 128], BF16)
    imp = p.sb("imp", [128, 256], F32)
    selbuf = p.sb("selbuf", [128, 256], F32)
    work = p.sb("work", [128, 256], F32)
    mx8 = p.sb("mx8", [128, 8], F32)
    thr = p.sb("thr", [128, 1], F32)
    zq = p.sb("zq", [128, 1], F32)
    Bq = p.sb("Bq", [128, 256], BF16)
    BT = p.sb("BT", [128, 2, 512], BF16)
    psum = p.ps("psum", [128, 7 * 512])
    psT = p.ps("psT", [128, 8, 128], BF16)
    zero_b = p.sb("zero_b", [128, 512], BF16)
    p.memset(zero_b[:], 0.0, w=["zero_b"])
    p.memset(Qblk[:], 0.0, w=["QT2"])

    def bank(i, n=1):
        return psum[:, i * 512:(i + n) * 512]

    def bk(i):
        return ("bank", i)

    p.dma(ident[:], ident_d, w=["ident"])
    p.dma(gT[:], g_attn, w=["gT"])
    p.dma(ccos[:], ccos_d, w=["ccos"])
    p.dma(csin[:], csin_d, w=["csin"])
    for a_ in range(2):
        for r4 in range(0, 16, 4):
            p.dma(pmask[:, a_, r4:r4 + 4, :], pmask_d[a_, r4:r4 + 4].rearrange("r p c -> p r c"),
                  w=["pmask"])
    p.dma(r0mask[:], r0mask_d, w=["r0mask"])
    p.dma(cmask[:], cmask_d.rearrange("a p c -> p a c"), w=["cmask"])
    for e8 in range(0, 64, 8):
        p.dma(emat[:, e8:e8 + 8, :], emat_d[e8:e8 + 8].rearrange("e p c -> p e c"), w=["emat"])
    p.dma(wfull[:], wfull_d.rearrange("(c p) f -> p c f", p=128), w=["wfull"])
    p.dma(fix3[:], fix_d, w=["fix3"])
    p.memset(ones[:], 1.0, w=["ones"])
    p.memset(VsA[:], 1.0, w=["VsA"])
    p.memset(VwA[:], 1.0, w=["VwA"])
    p.memset(VcA[:], 1.0, w=["VcA"])
    p.memset(KwT2[:], 0.0, w=["KwT2"])
    p.memset(KcT2[:], 0.0, w=["KcT2"])
    p.memset(selbuf[:], -FORCE, w=["selbuf"])
    p.memset(hidV[:], 0.0, w=["hidV"])
    for i in range(2):
        p.memset(xT[i][:], 0.0, w=[("xT", i)])
    nst_ = [0]

    def stage_load(dst_fn, src_ap, ncols, parts=128):
        i = nst_[0] % 2
        nst_[0] += 1
        k = ("fst", i)
        p.dma(fst[i][0:parts, 0:ncols], src_ap, w=[k])
        return fst[i], k

    def swapcopy(dst, src, r, w):
        d4 = dst.rearrange("p (h d) -> p h d", d=64)
        s4 = src.rearrange("p (h d) -> p h d", d=64)
        p.cp(d4[:, :, 0:32], s4[:, :, 32:64], r=r, w=w, eng="pool")
        p.cp(d4[:, :, 32:64], s4[:, :, 0:32], r=r, w=w, eng="pool")

    for kc in range(8):
        gs = gT[:, kc:kc + 1]
        rows = slice(kc * 128, (kc + 1) * 128)
        st_, k = stage_load(None, wq_d[rows, :], 256)
        p.ts(WQ[:, kc, 0, :], st_[:, 0:256], gs, None, ALU.mult, r=[k, "gT"], w=["WQ"], eng="pool")
        swapcopy(WQ[:, kc, 1, :], WQ[:, kc, 0, :], ["WQ"], ["WQ"])
        st_, k = stage_load(None, wk3_d[rows, :], 192)
        p.ts(WKC[:, kc, :], st_[:, 0:64], gs, None, ALU.mult, r=[k, "gT"], w=["WKC"], eng="pool")
        for (W_, c0) in ((WKS, 64), (WKW, 128)):
            for dup in range(2):
                p.ts(W_[:, kc, 0, dup * 64:(dup + 1) * 64], st_[:, c0:c0 + 64], gs, None, ALU.mult,
                     r=[k, "gT"], w=["WK"], eng="pool")
            swapcopy(W_[:, kc, 1, :], W_[:, kc, 0, :], ["WK"], ["WK"])
        st_, k = stage_load(None, wv3_d[rows, :], 192)
        p.ts(WVC[:, kc, :], st_[:, 0:64], gs, None, ALU.mult, r=[k, "gT"], w=["WVC"], eng="pool")
        p.ts(WV2[:, kc, :], st_[:, 64:192], gs, None, ALU.mult, r=[k, "gT"], w=["WV2"], eng="pool")
        st_, k = stage_load(None, wg_d[rows, :], 12)
        p.ts(WG[:, kc, :], st_[:, 0:12], gs, None, ALU.mult, r=[k, "gT"], w=["WG"], eng="pool")
    nW1 = [0]

    for kv in range(2):
        for l0 in range(0, 32, 4):
            i = nst_[0] % 2
            nst_[0] += 1
            k = ("fst", i)
            pc = kv * 8 + l0 // 4
            p.dma(fst[i][0:64, :].rearrange("p (l m) -> p l m", l=4),
                  w1_d[kv, l0 * 64:(l0 + 4) * 64, :].rearrange("(l d) m -> d l m", d=64), w=[k])
            p.cp(W1c[i][:], fst[i][0:64, :].rearrange("p (l m) -> p l m", l=4),
                 r=[k], w=[("W1c", i)], eng="pool")
            p.dma(w1b[pc * 64:(pc + 1) * 64, :], W1c[i][:].rearrange("p l m -> p (l m)"),
                  r=[("W1c", i)], w=["w1b"], q="pool")

    def w1_piece(kv, l0):
        i = nW1[0] % 3
        nW1[0] += 1
        pc = kv * 8 + l0 // 4
        p.dma(W1c[i][:].rearrange("p l m -> p (l m)"), w1b[pc * 64:(pc + 1) * 64, :],
              r=["w1b"], w=[("W1c", i)])
        return W1c[i], ("W1c", i)

    for kv in range(2):
        for mt in range(2):
            st_, k = stage_load(None, w2_d[kv, mt * 128:(mt + 1) * 128, :], 64)
            if kv == 0:
                for dup in range(2):
                    p.cp(W2K[:, mt, 0, dup * 64:(dup + 1) * 64], st_[:, 0:64], r=[k], w=["W2K"],
                         eng="pool")
                swapcopy(W2K[:, mt, 1, :], W2K[:, mt, 0, :], ["W2K"], ["W2K"])
            else:
                p.cp(W2V[:, mt, :], st_[:, 0:64], r=[k], w=["W2V"], eng="pool")
    st_, k = stage_load(None, posT_d.rearrange("d a l -> d (a l)"), 64, parts=64)
    p.cp(posT[:].rearrange("d a l -> d (a l)"), st_[0:64, 0:64], r=[k], w=["posT"], eng="pool")
    for kv in range(2):
        for l0 in range(0, 32, 4):
            wt, wk_ = w1_piece(kv, l0)
            for mt in range(2):
                col = kv * 2 + mt
                bb = 1 if mt == 0 else 5
                for li in range(4):
                    l = l0 + li
                    p.mm(bank(bb)[:, col:col + 1], wt[:, li, mt * 128:(mt + 1) * 128],
                         posT[:, kv, l:l + 1], start=(l == 0), stop=(l == 31),
                         r=[wk_, "posT"], w=[bk(bb)])
    p.cp(c1[:, 0:1], bank(1)[:, 0:1], r=[bk(1)], w=["c1"], eng="act")
    p.cp(c1[:, 2:3], bank(1)[:, 2:3], r=[bk(1)], w=["c1"], eng="act")
    p.cp(c1[:, 1:2], bank(5)[:, 1:2], r=[bk(5)], w=["c1"], eng="act")
    p.cp(c1[:, 3:4], bank(5)[:, 3:4], r=[bk(5)], w=["c1"], eng="act")

    if dbg == 0:
        return p.finish()
    if fused:
        for cb in range(4):
            p.dma(io["o_zero"][:, cb, :], zero_b[0:64, 0:128], r=["zero_b"], w=[io["dkey"]],
                  q="pool")
    if fused and io.get("after_setup"):
        io["after_setup"]()
    scr = (sq[:], ss[:], sd[:], rs[:])

    def rope_out(dst, psn, pss, cs, sn, rk, wk, n):
        p.tt(t1[:, 0:n], psn, cs, ALU.mult, r=rk[0:1] + ["cos", "ccos"], w=["t1"])
        p.tt(t2[:, 0:n], pss, sn, ALU.mult, r=rk[1:2] + ["sin", "csin"], w=["t2"])
        if isinstance(dst, tuple):
            p.tt(dst[0], t1[0:64, 0:n], t2[0:64, 0:n], ALU.add, r=["t1", "t2"], w=wk)
            p.tt(dst[1], t1[64:128, 0:n], t2[64:128, 0:n], ALU.add, r=["t1", "t2"], w=wk)
        else:
            p.tt(dst, t1[:, 0:n], t2[:, 0:n], ALU.add, r=["t1", "t2"], w=wk)

    nU = [0]
    nH = [0]

    def proj_pair(W_, dst, cs, sn, wkey, rkey):
        par = nU[0] % 2
        nU[0] += 1
        b0, b1 = 2 + 2 * par, 3 + 2 * par
        for v, b in ((0, b0), (1, b1)):
            for kc in range(8):
                p.mm(bank(b), W_(kc, v), hnT[:, kc, :], start=(kc == 0), stop=(kc == 7),
                     r=[rkey, "hnT"], w=[bk(b)])
        rope_out(dst, bank(b0), bank(b1), cs, sn, [bk(b0), bk(b1)], wkey, 512)

    def gelu_to(dst, ps_ap, bias_ap, rk, wk):
        x, a, b = gx[0][:], gx[1][:], gx[2][:]
        p.act(x, ps_ap, AF.Identity, r=rk + ["c1"], w=["gx0"], bias=bias_ap)
        p.tt(a, x, x, ALU.mult, r=["gx0"], w=["gx1"])
        p.ts(a, a, 0.044715, 1.0, ALU.mult, ALU.add, r=["gx1"], w=["gx1"])
        p.tt(a, a, x, ALU.mult, r=["gx1", "gx0"], w=["gx1"])
        p.act(b, a, AF.Sigmoid, r=["gx1"], w=["gx2"], scale=1.5957691216057308)
        p.tt(dst, x, b, ALU.mult, r=["gx0", "gx2"], w=wk)

    nS = [0]
    sdepth = [2]
    ZB = [False]

    def attn_chunk(kT2, kcols, vaug, biases, first, last, n_extra_r):
        sp_ = nS[0] % sdepth[0]
        nS[0] += 1
        sb_ = 2 + sp_
        if ZERO_BIAS and len(biases) == 0:
            if ZB[0]:
                biases = [(ident[:], zero_b[:], ["ident", "zero_b"])]
            else:
                biases = [(ident[:], zero_b[:, 0:32], ["ident", "zero_b"], "small")]
        out = bank(sb_)
        p.mm(out, kT2[:, kcols], Qblk[:, :, qsl[0]], start=True, stop=(len(biases) == 0),
             r=n_extra_r + ["QT2"], w=[bk(sb_)])
        for bi, bias in enumerate(biases):
            lh, rh, rk = bias[0:3]
            if len(bias) == 4 and bias[3] == "small":
                p.mm(out[:, 0:32], lh, rh, start=False, stop=(bi == len(biases) - 1), r=rk,
                     w=[bk(sb_)])
            elif len(bias) == 4:
                for hh in range(4):
                    p.mm(out[:, hh * 128:(hh + 1) * 128], lh, rh, start=False,
                         stop=(bi == len(biases) - 1), r=rk, w=[bk(sb_)])
            else:
                p.mm(out, lh, rh, start=False, stop=(bi == len(biases) - 1), r=rk, w=[bk(sb_)])
        return sp_, sb_

    qsl = [None]
    for st in range(NST):
        tok0 = st * 512
        for j in range(4):
            hb = hbuf[j % 2]
            hk = ("hbuf", 0)
            if fused:
                r0 = io["h_row"](tok0 + j * 128)
                p.dma(hb[:], io["h_ap"][r0:r0 + 128, :], r=list(io["rkeys"]), w=[hk])
            else:
                p.dma(hb[:], h_in[tok0 + j * 128:tok0 + (j + 1) * 128, :], w=[hk])
            rmsnorm_tile(p, hb[:], None, hn[:], scr, [hk], ["hn"], "a")
            for kc in range(8):
                p.tr(psT[:, kc, :], hn[:, kc * 128:(kc + 1) * 128], ident[:],
                     r=["hn", "ident"], w=["psT"])
            p.cp(hnT[:, :, j * 128:(j + 1) * 128], psT[:], r=["psT"], w=["hnT"], eng="act")
        p.dma(cos_sb[:], cos_d[:, tok0:tok0 + 512], w=["cos"])
        p.dma(sin_sb[:], sin_d[:, tok0:tok0 + 512], w=["sin"])
        for hp in range(2):
            proj_pair(lambda kc, v, hp=hp: WQ[:, kc, v, hp * 128:(hp + 1) * 128],
                      (Qblk[0:64, hp, :], Qblk[64:128, 2 + hp, :]), cos_sb[:], sin_sb[:],
                      ["QT2"], "WQ")
        proj_pair(lambda kc, v: WKS[:, kc, v, :], KsT2[:, tok0:tok0 + 512], cos_sb[:], sin_sb[:],
                  ["KsT2"], "WK")
        proj_pair(lambda kc, v: WKW[:, kc, v, :], KwT2[:, 512:1024], cos_sb[:], sin_sb[:],
                  ["KwT2"], "WK")
        for i, W_ in enumerate((WKC, WVC)):
            for kc in range(8):
                p.mm(bank(1)[0:64, :], W_[:, kc, :], hnT[:, kc, :], start=(kc == 0), stop=(kc == 7),
                     r=["WKC", "WVC", "hnT"], w=[bk(1)])
            p.cp(xT[i][:, 16:528], bank(1)[0:64, :], r=[bk(1)], w=[("xT", i)], eng="act")
        for j in range(4):
            for kc in range(8):
                p.mm(bank(0)[:, 0:128], hnT[:, kc, j * 128:(j + 1) * 128], WV2[:, kc, :],
                     start=(kc == 0), stop=(kc == 7), r=["hnT", "WV2"], w=[bk(0)])
            p.cp(VsA[:, st * 4 + j, 0:64], bank(0)[:, 0:64], r=[bk(0), "VsA"], w=["VsA"], eng="act")
            p.cp(VwA[:, 4 + j, 0:64], bank(0)[:, 64:128], r=[bk(0), "VwA"], w=["VwA"], eng="act")
        for kc in range(8):
            p.mm(bank(1)[0:12, :], WG[:, kc, :], hnT[:, kc, :], start=(kc == 0), stop=(kc == 7),
                 r=["WG", "hnT"], w=[bk(1)])
        p.act(gsb[:], bank(1)[0:12, :], AF.Sigmoid, r=[bk(1)], w=["gsb"])
        p.dma(gscr[st % 2].rearrange("o (a b) -> (o a) b", a=12), gsb[:], r=["gsb"],
              w=[("gscr", st % 2)])
        if dbg == 1:
            return p.finish()
        for kv in range(2):
            x3 = xT[kv][:].rearrange("p (i s) -> p i s", s=16)
            for l0 in range(0, 32, 4):
                wt, wk_ = w1_piece(kv, l0)
                for mt in range(2):
                    bb = 1 if mt == 0 else 5
                    for li in range(4):
                        l = l0 + li
                        rhs = x3[:, 0:32, l] if l < 16 else x3[:, 1:33, l - 16]
                        p.mm(bank(bb)[:, 0:32], wt[:, li, mt * 128:(mt + 1) * 128], rhs,
                             start=(l == 0), stop=(l == 31), r=[wk_, ("xT", kv)], w=[bk(bb)])
            for mt in range(2):
                bb = 1 if mt == 0 else 5
                if kv == 0:
                    gelu_to(hidK[:, mt, :], bank(bb)[:, 0:32], c1[:, mt:mt + 1], [bk(bb)], ["hidK"])
                else:
                    if st % 4 == 0 and mt == 0:
                        p.memset(hidV[:], 0.0, w=["hidV"])
                    gelu_to(hidV[:, mt, (st % 4) * 32:(st % 4) * 32 + 32], bank(bb)[:, 0:32],
                            c1[:, 2 + mt:3 + mt], [bk(bb)], ["hidV"])
            if kv == 0:
                par = nU[0] % 2
                nU[0] += 1
                b0, b1 = 2 + 2 * par, 3 + 2 * par
                for v, b in ((0, b0), (1, b1)):
                    for mt in range(2):
                        p.mm(bank(b)[:, 0:32], W2K[:, mt, v, :], hidK[:, mt, :],
                             start=(mt == 0), stop=(mt == 1), r=["W2K", "hidK"], w=[bk(b)])
                sl = slice(st * 32, st * 32 + 32)
                rope_out(KcT2[:, sl], bank(b0)[:, 0:32], bank(b1)[:, 0:32], ccos[:, sl], csin[:, sl],
                         [bk(b0), bk(b1)], ["KcT2"], 32)
            else:
                for mt in range(2):
                    p.mm(bank(1)[:, 0:64], hidV[:, mt, :], W2V[:, mt, :],
                         start=(mt == 0), stop=(mt == 1), r=["W2V", "hidV"], w=[bk(1)])
                p.cp(VcA[:, st // 4, 0:64], bank(1)[:, 0:64], r=[bk(1), "VcA"], w=["VcA"], eng="act")
            p.cp(xT[kv][:, 0:16], xT[kv][:, 512:528], r=[("xT", kv)], w=[("xT", kv)], eng="pool")
        if dbg == 2:
            return p.finish()
        for j in range(4):
            qb = st * 4 + j
            qsl[0] = slice(j * 128, (j + 1) * 128)
            tsl = qsl[0]
            p.dma(G64b[j % 2][64:65, :].rearrange("p (a b) -> p a b", a=12),
                  gscr[st % 2].rearrange("o (a b) -> o a b", a=12)[:, :, tsl],
                  r=[("gscr", st % 2)], w=[("G64", j % 2)])

            def finish_branch(br, first):
                p.ts(zr[64:65, :], bank(0)[64:65, :], TINY, None, ALU.max, r=[bk(0)], w=["zr"])
                p.recip(zr[64:65, :], zr[64:65, :], r=["zr"], w=["zr"])
                g3 = G64b[j % 2][64:65, :].rearrange("p (h b t) -> p h b t", h=4, b=3)
                for par in range(2):
                    for hpl in range(2):
                        hl = 2 * hpl + par
                        c0 = (par * 2 + hpl) * 128
                        p.tt(Rr[64:65, c0:c0 + 128], zr[64:65, c0:c0 + 128], g3[:, hl, br, :],
                             ALU.mult, r=["zr", ("G64", j % 2)], w=["Rr"])
                p.cp(osb[:], bank(0)[0:64, :], r=[bk(0)], w=["osb"], eng="act")

                def part2(first=first):
                    p.mm(bank(1)[0:64, :], ones[64:65, :], Rr[64:65, :], start=True, stop=True,
                         r=["ones", "Rr"], w=[bk(1)])
                    if first:
                        p.tt(acc[:], osb[:], bank(1)[0:64, :], ALU.mult, r=["osb", bk(1)], w=["acc"])
                    else:
                        p.tt(tmpo[:], osb[:], bank(1)[0:64, :], ALU.mult, r=["osb", bk(1)],
                             w=["tmpo"])
                        p.tt(acc[:], acc[:], tmpo[:], ALU.add, r=["tmpo", "acc"], w=["acc"])
                return part2

            pend = []
            pdepth = [1]

            def pend_push(fn):
                pend.append(fn)
                while len(pend) > pdepth[0]:
                    pend.pop(0)()

            def pend_flush():
                while pend:
                    pend.pop(0)()

            ncc = qb // 16 + 1
            r_ = qb % 16
            for cc in range(ncc):
                biases = []
                lastc = (cc == ncc - 1)
                if lastc:
                    biases.append((ident[:], pmask[:, 1 if cc == 0 else 0, r_, :], ["ident", "pmask"], 128))
                elif cc == 0:
                    biases.append((ident[:], r0mask[:], ["ident", "r0mask"]))
                sp_, sb_ = attn_chunk(KcT2, slice(cc * 128, (cc + 1) * 128), None, biases,
                                      cc == 0, lastc, ["KcT2"])
                p.act(PcT[:, cc, :], bank(sb_), AF.Exp, r=[bk(sb_)], w=[("PcT", cc)], scale=0.125)
                pend_push(lambda cc=cc, lastc=lastc: p.mm(
                    bank(0)[0:65, :], VcA[:, cc, :], PcT[:, cc, :], start=(cc == 0), stop=lastc,
                    r=[("PcT", cc), "VcA"], w=[bk(0)]))
            pend_flush()
            for par in range(2):
                for hpl in range(2):
                    hi = par * 2 + hpl
                    c0 = hi * 128
                    ib = 4 + (hi % 2)
                    for cc in range(ncc):
                        p.mm(bank(ib)[:, 0:257], PcT[:, cc, c0:c0 + 128], wfull[:, cc, :],
                             start=(cc == 0), stop=(cc == ncc - 1),
                             r=[("PcT", cc), "wfull"], w=[bk(ib)])
                    p.ts(zq[:], bank(ib)[:, 256:257], TINY, None, ALU.max, r=[bk(ib)], w=["zq"])
                    p.recip(zq[:], zq[:], r=["zq"], w=["zq"])
                    if hi == 0:
                        p.ts(imp[:], bank(ib)[:, 0:256], zq[:], None, ALU.mult, r=[bk(ib), "zq"],
                             w=["imp"])
                    else:
                        p.stt(imp[:], bank(ib)[:, 0:256], zq[:], imp[:], ALU.mult, ALU.add,
                              r=[bk(ib), "zq", "imp"], w=["imp"])
            fin0 = finish_branch(0, True)
            if dbg == 3 or dbg == 100 + j * 10 + 3:
                return p.finish()
            nb = 2 * qb + 2
            p.cp(selbuf[:, 0:nb], imp[:, 0:nb], r=["imp"], w=["selbuf"])
            lo = 2 * qb - 1
            k0 = 0
            if lo < 0:
                lo, k0 = 0, 1
            nfx = 3 - k0
            p.tt(selbuf[:, lo:lo + nfx], selbuf[:, lo:lo + nfx], fix3[:, k0:3], ALU.mult,
                 r=["selbuf", "fix3"], w=["selbuf"])
            p.tt(selbuf[:, lo:lo + nfx], selbuf[:, lo:lo + nfx], fix3[:, 3 + k0:6], ALU.add,
                 r=["selbuf", "fix3"], w=["selbuf"])
            p.memset(selbuf[:, 0:1], 3.0 * FORCE, w=["selbuf"])
            p.s.op("dve", lambda e: e.max(out=mx8[:], in_=selbuf[:]), ["selbuf"], ["mx8"])
            p.s.op("dve", lambda e: e.match_replace(out=work[:], in_to_replace=mx8[:],
                                                    in_values=selbuf[:], imm_value=-2.0 * FORCE),
                   ["selbuf", "mx8"], ["work"])
            p.s.op("dve", lambda e: e.max(out=mx8[:], in_=work[:]), ["work"], ["mx8"])
            p.s.op("dve", lambda e: e.tensor_reduce(out=thr[:], in_=mx8[:], axis=AX.X, op=ALU.min),
                   ["mx8"], ["thr"])
            p.ts(Bq[:], selbuf[:], thr[:], 1.0, ALU.is_ge, ALU.subtract, r=["selbuf", "thr"], w=["Bq"])
            nhalf = 1 if nb <= 128 else 2
            for hf in range(nhalf):
                p.tr(psT[:, hf, :], Bq[:, hf * 128:(hf + 1) * 128], ident[:], r=["Bq", "ident"],
                     w=["psT"])
            for hf in range(nhalf):
                for rep in range(4):
                    p.cp(BT[:, hf, rep * 128:(rep + 1) * 128], psT[:, hf, :], r=["psT"], w=["BT"],
                         eng=("act" if rep % 2 == 0 else "dve"))
            if dbg == 5 or dbg == 100 + j * 10 + 5:
                return p.finish()
            k_lo = max(0, qb - 4)
            sdepth[0] = 4
            pdepth[0] = 2
            for kc in range(k_lo, qb + 1):
                biases = []
                if kc == qb - 4:
                    biases.append((ident[:], cmask[:, 1, :], ["ident", "cmask"]))
                if kc == qb:
                    biases.append((ident[:], cmask[:, 0, :], ["ident", "cmask"]))
                slot = 4 + j - (qb - kc)
                sp_, sb_ = attn_chunk(KwT2, slice(slot * 128, (slot + 1) * 128), None, biases,
                                      kc == k_lo, kc == qb, ["KwT2"])
                p.act(PT[sp_][:], bank(sb_), AF.Exp, r=[bk(sb_)], w=[("PT", sp_)], scale=0.125)
                if kc == min(k_lo + 1, qb) and fin0 is not None:
                    fin0()
                    fin0 = None
                pend_push(lambda kc=kc, sp_=sp_, slot=slot: p.mm(
                    bank(0)[0:65, :], VwA[:, slot, :], PT[sp_][:], start=(kc == k_lo),
                    stop=(kc == qb), r=[("PT", sp_), "VwA"], w=[bk(0)]))
            pend_flush()
            fin2 = finish_branch(2, False)
            if fin0 is not None:
                fin0()
                fin0 = None
            if dbg == 4 or dbg == 100 + j * 10 + 4:
                return p.finish()
            sdepth[0] = 3 if MUL_MASK else 4
            pdepth[0] = 2
            for kc in range(qb + 1):
                mulmask = MUL_MASK and kc != qb
                if mulmask:
                    zb = ZB[0]
                    ZB[0] = KEEP_ZB
                    sp_, sb_ = attn_chunk(KsT2, slice(kc * 128, (kc + 1) * 128), None, [],
                                          kc == 0, kc == qb, ["KsT2"])
                    ZB[0] = zb
                    mpar = nM[0] % 2
                    nM[0] += 1
                    mb = 5 + mpar
                    p.mm(bank(mb)[:, 0:128], emat[:, kc % 64, :], BT[:, kc // 64, 0:128],
                         start=True, stop=True, r=["emat", "BT"], w=[bk(mb)])
                    p.act(EX[sp_][:], bank(sb_), AF.Exp, r=[bk(sb_)], w=[("EX", sp_)], scale=0.125,
                          bias=-LN_C)
                    p.act(Msb[mpar][:], bank(mb)[:, 0:128], AF.Identity, r=[bk(mb)],
                          w=[("Msb", mpar)], bias=C_BF)
                    p.tt(PT[sp_][:].rearrange("p (h q) -> p h q", h=4),
                         EX[sp_][:].rearrange("p (h q) -> p h q", h=4),
                         Msb[mpar][:].unsqueeze(1).to_broadcast([128, 4, 128]), ALU.mult,
                         r=[("Msb", mpar), ("EX", sp_)], w=[("PT", sp_)])
                else:
                    biases = [(emat[:, kc % 64, :], BT[:, kc // 64, :], ["emat", "BT"])]
                    if kc == qb:
                        biases.append((ident[:], cmask[:, 0, :], ["ident", "cmask"]))
                    sp_, sb_ = attn_chunk(KsT2, slice(kc * 128, (kc + 1) * 128), None, biases,
                                          kc == 0, kc == qb, ["KsT2"])
                    p.act(PT[sp_][:], bank(sb_), AF.Exp, r=[bk(sb_)], w=[("PT", sp_)], scale=0.125)
                if kc == min(1, qb) and fin2 is not None:
                    fin2()
                    fin2 = None
                pend_push(lambda kc=kc, sp_=sp_: p.mm(
                    bank(0)[0:65, :], VsA[:, kc, :], PT[sp_][:], start=(kc == 0), stop=(kc == qb),
                    r=[("PT", sp_), "VsA"], w=[bk(0)]))
            pend_flush()
            fin1 = finish_branch(1, False)
            fin1()
            sdepth[0] = 2
            if dbg == 6 or dbg == 100 + j * 10 + 6:
                return p.finish()
            p.cp(oacc[:], acc[:].rearrange("p (c q) -> p c q", c=4), r=["acc"], w=["oacc"])
            if fused:
                for dst in io["o_dst"](qb):
                    p.dma(dst, oacc[:], r=["oacc"], w=[io["dkey"]], q="pool")
            else:
                p.dma(oT_out[:, :, tok0 + j * 128:tok0 + (j + 1) * 128], oacc[:], r=["oacc"],
                      q="pool", is_output=True)
            if dbg == 100 + j * 10 + 7:
                return p.finish()
        p.cp(KwT2[:, 0:512], KwT2[:, 512:1024], r=["KwT2"], w=["KwT2"], eng="pool")
        p.cp(VwA[:, 0:4, :], VwA[:, 4:8, :], r=["VwA"], w=["VwA"], eng="pool")
        if dbg == 7 + st:
            return p.finish()
    if fused:
        return None
    return p.finish()


def nsa_consts(S):
    NCC = max(1, S // 2048)
    ml = np.arange(128)[:, None]
    q = np.arange(128)[None, :]
    pm = np.zeros((2, 16, 128, 128), np.float32)
    for a in range(2):
        for r in range(16):
            valid = (16 * ml + 15 <= 128 * r + q)
            if a == 1:
                valid = valid & (ml >= 1)
            pm[a, r] = np.where(valid, 0.0, MASKV)
    pmask = pm.astype(NPBF16)
    r0 = np.zeros((128, 512), np.float32)
    r0[0, :] = MASKV
    cur = np.where(ml <= q, 0.0, MASKV).astype(np.float32)
    upper = np.where(ml > q, 0.0, MASKV).astype(np.float32)
    cmask = np.stack([np.tile(cur, (1, 4)), np.tile(upper, (1, 4))], 0).astype(NPBF16)
    emat = np.zeros((64, 128, 128), np.float32)
    for e in range(64):
        emat[e, 2 * e, 0:64] = -MASKV
        emat[e, 2 * e + 1, 64:128] = -MASKV
    ws = [1, 2, 2, 2, 1]
    wfull = np.zeros((NCC * 128, 257), np.float32)
    for m in range(1, NCC * 128):
        n = m - 1
        for j in range(256):
            i = n - 4 * j + 1
            if 0 <= i <= 4:
                wfull[m, j] = ws[i]
    wfull[:, 256] = 1.0
    fix = np.zeros((128, 6), np.float32)
    lo = np.arange(128) < 64
    fix[:, 0] = np.where(lo, 0.0, 1.0)
    fix[:, 3] = np.where(lo, FORCE, 0.0)
    fix[:, 4] = 2.0 * FORCE
    fix[:, 5] = np.where(lo, -FORCE, FORCE)
    cpos = 16 * np.arange(NCC * 128) + 15
    ccos, csin = rope_tables(cpos)
    return dict(pmask=pmask, r0mask=r0.astype(NPBF16), cmask=cmask, emat=emat.astype(NPBF16),
                wfull=wfull.astype(NPBF16), fix3=fix, ccos_t=ccos, csin_t=csin, ident=ident_np())


def nsa_weights(a_w_in_l, cmp_pos_l, g):
    W = a_w_in_l
    q0 = g * 256
    def kcol(i):
        return W[:, 1024 + i * 256 + g * 64: 1024 + i * 256 + (g + 1) * 64]
    kc_, vc_, ks_, vs_, kw_, vw_ = [kcol(i) for i in range(6)]
    wg = W[:, 1024 + 6 * 256 + g * 12: 1024 + 6 * 256 + (g + 1) * 12]
    return dict(wq=np.ascontiguousarray(W[:, q0:q0 + 256]),
                wk3=np.ascontiguousarray(np.concatenate([kc_, ks_, kw_], 1)),
                wv3=np.ascontiguousarray(np.concatenate([vc_, vs_, vw_], 1)),
                wg=np.ascontiguousarray(wg),
                posT=np.ascontiguousarray(np.transpose(cmp_pos_l, (2, 0, 1))))


SEQ = 16384
NB = 2
CH = 4096
NPHASE = 99


def _run(nc, in_maps):
    res = run_bass_kernel_spmd(nc, in_maps, core_ids=list(range(8)))
    return res.results


def _cwb(conv_w, conv_b):
    return np.ascontiguousarray(np.concatenate([conv_w, conv_b[None]], 0).T.astype(np.float32))


def _chunk_with_halo(x_b, c, halo):
    lo = c * CH - halo
    if lo >= 0:
        return np.ascontiguousarray(x_b[lo:(c + 1) * CH])
    pad = np.zeros((-lo,) + x_b.shape[1:], x_b.dtype)
    return np.ascontiguousarray(np.concatenate([pad, x_b[0:(c + 1) * CH]], 0))


def kernel_unfused(x, norm_attn, norm_ffn, a_w_in, a_cmp_pos, a_cmp_w1, a_cmp_w2, a_w_out, kv_norm,
           b_w_kv, b_w_q, b_sinks, b_w_out, ffn_w_in, ffn_conv_w, ffn_conv_b, ffn_w_out,
           final_norm):
    f32 = lambda a: np.ascontiguousarray(np.asarray(a, dtype=np.float32))
    x = f32(x)
    norm_attn, norm_ffn = f32(norm_attn), f32(norm_ffn)
    a_w_in, a_cmp_pos, a_cmp_w1, a_cmp_w2, a_w_out = map(f32, (a_w_in, a_cmp_pos, a_cmp_w1,
                                                                a_cmp_w2, a_w_out))
    kv_norm, b_w_kv, b_w_q, b_sinks, b_w_out = map(f32, (kv_norm, b_w_kv, b_w_q, b_sinks, b_w_out))
    ffn_w_in, ffn_conv_w, ffn_conv_b, ffn_w_out, final_norm = map(
        f32, (ffn_w_in, ffn_conv_w, ffn_conv_b, ffn_w_out, final_norm))
    h = x
    ident = ident_np()
    gfin = np.ascontiguousarray(final_norm[None, :])
    cosA, sinA = rope_tables(np.arange(SEQ))
    constsA = nsa_consts(SEQ)

    for l in range(2):
        ncA = build_A(SEQ)
        maps = []
        for i in range(8):
            b, g = divmod(i, 4)
            m = dict(h_in=h[b], g_attn=gT_np(norm_attn[l]), w1=a_cmp_w1[l], w2=a_cmp_w2[l],
                     cos_t=cosA, sin_t=sinA)
            m.update(constsA)
            m.update(nsa_weights(a_w_in[l], a_cmp_pos[l], g))
            maps.append(m)
        resA = _run(ncA, maps)
        oT_full = np.zeros((NB, 16, 64, SEQ), NPBF16)
        for i in range(8):
            b, g = divmod(i, 4)
            o = resA[i]["oT_out"]
            for par in range(2):
                for hpl in range(2):
                    oT_full[b, g * 4 + 2 * hpl + par] = o[:, par * 2 + hpl, :]
        oT_full = oT_full.reshape(NB, 1024, SEQ)
        ncB = build_B([1] + [4] * 8, 1)
        maps = []
        for i in range(8):
            b, c = divmod(i, 4)
            maps.append(dict(
                h_in=_chunk_with_halo(h[b], c, 128),
                oT_in=np.ascontiguousarray(_chunk_with_halo(oT_full[b].T, c, 128).T),
                w_o=a_w_out[l], w_in=ffn_w_in[l], w_out=ffn_w_out[l],
                cwb=_cwb(ffn_conv_w[l], ffn_conv_b[l]), g_ffn=gT_np(norm_ffn[l]), g_fin=gfin,
                ident=ident))
        resB = _run(ncB, maps)
        h = np.stack([np.concatenate([resB[b * 4 + c]["h_out"] for c in range(4)], 0)
                      for b in range(NB)], 0)

    hkv = h
    for l in range(2, 4):
        j = l - 2
        ncC = build_C([2] + [4] * 8, 2, final_norm=(l == 3))
        maps = []
        for i in range(8):
            b, c = divmod(i, 4)
            pos = c * CH - 256 + np.arange(CH + 256)
            cos_t, sin_t = rope_tables(pos)
            maps.append(dict(
                h_in=_chunk_with_halo(h[b], c, 256), hkv_in=_chunk_with_halo(hkv[b], c, 256),
                w_q=b_w_q[j], w_kv=b_w_kv, sinks_b=sinks_row(b_sinks[j]),
                g_attn=gT_np(norm_attn[l]), g_kv=gT_np(kv_norm), cos_t=cos_t, sin_t=sin_t,
                masks=swa_masks(c > 0), w_o=b_w_out[j], w_in=ffn_w_in[l], w_out=ffn_w_out[l],
                cwb=_cwb(ffn_conv_w[l], ffn_conv_b[l]), g_ffn=gT_np(norm_ffn[l]), g_fin=gfin,
                ident=ident))
        resC = _run(ncC, maps)
        h = np.stack([np.concatenate([resC[b * 4 + c]["h_out"] for c in range(4)], 0)
                      for b in range(NB)], 0)
    return np.ascontiguousarray(h.astype(np.float32))


def build_fused(nphase=99):
    from concourse.bass import ds
    nph = [0]

    def stop():
        nph[0] += 1
        return nph[0] >= nphase

    nc = bass.Bass("TRN2", target_bir_lowering=False)
    p = Prog(nc)
    S = SEQ
    WB = 128 + CH
    WC = 256 + CH
    SUBW = 11 * 128
    xA = p.din("xA", [S, D], F32)
    xB = p.din("xB", [WB, D], F32)
    flag = p.din("flag", [128, 1], F32)
    out = p.dout("out", [CH, D], F32)
    oTloc = [nc.dram_tensor(f"oTloc{l}", [12 * 64, 4 * SUBW], BF16) for l in range(2)]
    OTb = [nc.dram_tensor(f"OTb{l}", [12 * 256, 4 * SUBW], BF16) for l in range(2)]
    oTwin = nc.dram_tensor("oTwin", [3 * 256, 4 * SUBW], BF16).ap()
    hloc = [nc.dram_tensor(f"hloc{k}", [CH, D], F32) for k in range(3)]
    Hb = [nc.dram_tensor(f"Hb{k}", [S, D], F32) for k in range(3)]
    hwin = nc.dram_tensor("hwin", [WC, D], F32).ap()
    hkvwin = nc.dram_tensor("hkvwin", [WC, D], F32).ap()
    rg = [[0, 1, 2, 3], [4, 5, 6, 7]]
    PID = p.s.pid
    Hh = [nc.dram_tensor(f"Hh{k}", [4 * 256, D], F32) for k in (1, 2)]
    halowin = [nc.dram_tensor(f"halowin{k}", [256, D], F32).ap() for k in (1, 2)]

    def gather_group(src, dst, nchunk, rows, rk, wk):
        def fn(e, sem):
            for k in range(nchunk):
                e.collective_compute(
                    "AllGather", ALU.bypass, replica_groups=rg,
                    ins=[src.ap()[k * rows:(k + 1) * rows, :].opt()],
                    outs=[dst.ap()[k * 4 * rows:(k + 1) * 4 * rows, :].opt()]).then_inc(sem)
        p.s.cc(fn, [rk], [wk], n=nchunk)

    def h_row(tok):
        rank, rem = divmod(tok, CH)
        k, r = divmod(rem, 256)
        return (k * 4 + rank) * 256 + r

    def win_copy(dst, src, halo, q, rk, wk):
        s5 = src.rearrange("(k g r e) d -> k g r (e d)", k=16, g=4, e=8)
        dm = dst[halo:halo + CH, :].rearrange("(k g r e) d -> k g r (e d)", k=16, g=1, e=8)
        dh = dst[0:halo, :].rearrange("(k g r e) d -> k g r (e d)", k=1, g=1, e=8)
        h8 = halo // 8
        p.dmaf(lambda e: e.dma_start(
            out=dm, in_=s5[:, ds(PID(e, "c", lambda pid: pid % 4), 1), :, :]),
            r=[rk], w=[wk], q=q)
        p.dmaf(lambda e: e.dma_start(
            out=dh, in_=s5[15:16, ds(PID(e, "cm1", lambda pid: (pid + 3) % 4), 1), 32 - h8:32, :]),
            r=[rk], w=[wk], q=q)

    for l in range(2):
        p.sfx = f"_A{l}"
        rk = [] if l == 0 else [f"Hb{l - 1}"]
        O5 = oTloc[l].ap().rearrange("(c s d) (b t) -> c s d b t", c=4, s=3, b=4)

        def o_dst(qb, O5=O5):
            c, sl = divmod(qb, 32)
            sl += 1
            dsts = [O5[c, sl // 11, :, :, (sl % 11) * 128:(sl % 11) * 128 + 128]]
            if sl == 32 and c < 3:
                dsts.append(O5[c + 1, 0, :, :, 0:128])
            return dsts

        build_A(S, p=p, io=dict(h_ap=(xA if l == 0 else Hb[l - 1].ap()),
                                h_row=((lambda t: t) if l == 0 else h_row),
                                rkeys=rk, dkey=f"oTloc{l}", o_dst=o_dst,
                                o_zero=O5[0, 0, :, :, 0:128], after_setup=p.s.cc_wait))
        p.phase_end()
        gather_group(oTloc[l], OTb[l], 12, 64, f"oTloc{l}", f"OTb{l}")
        if stop():
            return p.finish(), dict(p.dins)
        p.sfx = f"_B{l}"
        O3 = OTb[l].ap().rearrange("(c r) f -> c r f", c=4)

        def after_b(l=l, O3=O3):
            p.s.cc_wait()
            p.dmaf(lambda e: e.dma_start(
                out=oTwin.rearrange("(c r) f -> c r f", c=1),
                in_=O3[ds(PID(e, "c", lambda pid: pid % 4), 1), :, :]),
                r=[f"OTb{l}"], w=["oTwin"], q="act")
            if l > 0:
                win_copy(hwin[0:WB, :], Hb[l - 1].ap(), 128, "act", f"Hb{l - 1}", "hwin")

        if l == 0:
            h_ap = xB
            rkb = ["oTwin"]
        else:
            h_ap = hwin[0:WB, :]
            rkb = ["oTwin", "hwin"]
        W5 = oTwin.rearrange("(s g d) (b t) -> d s g b t", s=3, g=4, b=4)

        def oT_ap(wt, g, W5=W5):
            return W5[:, wt // 11, g, :, (wt % 11) * 128:(wt % 11) * 128 + 128]

        build_B([1] + [4] * 8, 1, p=p,
                io=dict(h_ap=h_ap, oT_ap=oT_ap, h_dst=hloc[l].ap(), flag=flag,
                        rkeys=rkb, dkey=f"hloc{l}", after_setup=after_b))
        p.phase_end()
        if l == 0:
            gather_group(hloc[l], Hb[l], 16, 256, f"hloc{l}", f"Hb{l}")
        else:
            p.s.cc(lambda e, sem: e.collective_compute(
                "AllGather", ALU.bypass, replica_groups=rg,
                ins=[hloc[1].ap()[CH - 256:CH, :].opt()], outs=[Hh[0].ap().opt()]).then_inc(sem),
                ["hloc1"], ["Hh0"], n=1)
        if stop():
            return p.finish(), dict(p.dins)


    def halo_copy(k):
        src = Hh[k].ap().rearrange("(g r) d -> g r d", g=4)
        p.dmaf(lambda e: e.dma_start(
            out=halowin[k].rearrange("(g r) d -> g r d", g=1),
            in_=src[ds(PID(e, "cm1", lambda pid: (pid + 3) % 4), 1), :, :]),
            r=[f"Hh{k}"], w=[f"halowin{k}"], q="sp")

    def tile_src(halo_ap, main_ap):
        def f(t):
            if t < 2:
                return halo_ap[t * 128:(t + 1) * 128, :]
            return main_ap[(t - 2) * 128:(t - 1) * 128, :]
        return f

    for l in range(2, 4):
        p.sfx = f"_C{l}"
        last = (l == 3)
        if l == 2:
            def after_c():
                p.s.cc_wait()
                halo_copy(0)
            h_tile = hkv_tile = tile_src(halowin[0], hloc[1].ap())
            rkc = ["halowin0", "hloc1"]
        else:
            def after_c():
                p.s.cc_wait()
                halo_copy(1)
            h_tile = tile_src(halowin[1], hloc[2].ap())
            hkv_tile = tile_src(halowin[0], hloc[1].ap())
            rkc = ["halowin0", "hloc1", "halowin1", "hloc2"]
        build_C([2] + [4] * 8, 2, final_norm=last, p=p,
                io=dict(h_tile=h_tile, hkv_tile=hkv_tile, h_dst=(out if last else hloc[2].ap()),
                        flag=flag, rkeys=rkc, dkey=(None if last else "hloc2"),
                        after_setup=after_c))
        if not last:
            p.phase_end()
            p.s.cc(lambda e, sem: e.collective_compute(
                "AllGather", ALU.bypass, replica_groups=rg,
                ins=[hloc[2].ap()[CH - 256:CH, :].opt()], outs=[Hh[1].ap().opt()]).then_inc(sem),
                ["hloc2"], ["Hh1"], n=1)
            if stop():
                return p.finish(), dict(p.dins)
    return p.finish(), dict(p.dins)


def kernel(x, norm_attn, norm_ffn, a_w_in, a_cmp_pos, a_cmp_w1, a_cmp_w2, a_w_out, kv_norm,
           b_w_kv, b_w_q, b_sinks, b_w_out, ffn_w_in, ffn_conv_w, ffn_conv_b, ffn_w_out,
           final_norm):
    f32 = lambda a: np.ascontiguousarray(np.asarray(a, dtype=np.float32))
    x = f32(x)
    norm_attn, norm_ffn = f32(norm_attn), f32(norm_ffn)
    a_w_in, a_cmp_pos, a_cmp_w1, a_cmp_w2, a_w_out = map(f32, (a_w_in, a_cmp_pos, a_cmp_w1,
                                                                a_cmp_w2, a_w_out))
    kv_norm, b_w_kv, b_w_q, b_sinks, b_w_out = map(f32, (kv_norm, b_w_kv, b_w_q, b_sinks, b_w_out))
    ffn_w_in, ffn_conv_w, ffn_conv_b, ffn_w_out, final_norm = map(
        f32, (ffn_w_in, ffn_conv_w, ffn_conv_b, ffn_w_out, final_norm))
    nc, dins = build_fused(NPHASE)
    ident = ident_np()
    gfin = np.ascontiguousarray(final_norm[None, :])
    cosA, sinA = rope_tables(np.arange(SEQ))
    constsA = nsa_consts(SEQ)
    maps = []
    for i in range(8):
        b, c = divmod(i, 4)
        g = c
        m = dict(xA=x[b], xB=_chunk_with_halo(x[b], c, 128),
                 flag=np.full((128, 1), 0.0 if c == 0 else 1.0, np.float32))
        for l in range(2):
            a = dict(g_attn=gT_np(norm_attn[l]), w1=a_cmp_w1[l], w2=a_cmp_w2[l],
                     cos_t=cosA, sin_t=sinA)
            a.update(constsA)
            a.update(nsa_weights(a_w_in[l], a_cmp_pos[l], g))
            for k, v in a.items():
                m[f"{k}_A{l}"] = v
            bb = dict(w_o=a_w_out[l], w_in=ffn_w_in[l], w_out=ffn_w_out[l],
                      cwb=_cwb(ffn_conv_w[l], ffn_conv_b[l]), g_ffn=gT_np(norm_ffn[l]), g_fin=gfin,
                      ident=ident)
            for k, v in bb.items():
                m[f"{k}_B{l}"] = v
        pos = c * CH - 256 + np.arange(CH + 256)
        cos_t, sin_t = rope_tables(pos)
        for l in range(2, 4):
            j = l - 2
            cc = dict(w_q=b_w_q[j], w_kv=b_w_kv, sinks_b=sinks_row(b_sinks[j]),
                      g_attn=gT_np(norm_attn[l]), g_kv=gT_np(kv_norm), cos_t=cos_t, sin_t=sin_t,
                      masks=swa_masks(c > 0), w_o=b_w_out[j], w_in=ffn_w_in[l], w_out=ffn_w_out[l],
                      cwb=_cwb(ffn_conv_w[l], ffn_conv_b[l]), g_ffn=gT_np(norm_ffn[l]), g_fin=gfin,
                      ident=ident)
            for k, v in cc.items():
                m[f"{k}_C{l}"] = v
        m = {k: v for k, v in m.items() if k in dins}
        maps.append(m)
    res = _run(nc, maps)
    h = np.stack([np.concatenate([res[b * 4 + c]["out"] for c in range(4)], 0)
                  for b in range(NB)], 0)
    return np.ascontiguousarray(h.astype(np.float32))
```
